# Optimizing a Trainium2 kernel written in Bass

```python
import math
import jax, jax.numpy as jnp
from jax import lax
import numpy as np

D_MODEL = 1024
BATCH = 8
SEQ = 2048
DEPTH = 1
DEC_BATCH = 128
DEC_SEQ = 4
PAST_LEN = 16384
PAGE_SIZE = 128

N_META = 16
CHUNK = 128
M_HEADS = 4
M_DK = D_MODEL // 8
M_DV = D_MODEL // 4
M_QK = M_HEADS * M_DK
M_V = M_HEADS * M_DV
M_CONV = 4
S_INNER = 2 * D_MODEL
S_HEADDIM = 64
S_HEADS = S_INNER // S_HEADDIM
S_GROUPS = 4
S_STATE = 128
S_CONV = 4
S_XBC = S_INNER + 2 * S_GROUPS * S_STATE
D_FF = 2816
F_CONV = 3
ALPHA = (2 * DEPTH) ** 0.25
BETA = (8 * DEPTH) ** -0.25
LN_EPS = 1e-5
RMS_EPS = 1e-5
IN_SPLITS = (2 * M_QK, M_V, M_V, M_HEADS, M_HEADS, S_INNER, S_XBC, S_HEADS, 2 * D_MODEL)
D_IN = 2 * M_QK + 2 * M_V + 2 * M_HEADS + S_INNER + S_XBC + S_HEADS + 2 * D_MODEL

kernel_name = 'mlstm_ssd_gated_hybrid_step'


def layer_norm(x, g, b):
    xf = x.astype(jnp.float32)
    mu = jnp.mean(xf, -1, keepdims=True)
    var = jnp.mean(jnp.square(xf - mu), -1, keepdims=True)
    y = (xf - mu) * lax.rsqrt(var + LN_EPS) * g.astype(jnp.float32) + b.astype(jnp.float32)
    return y.astype(x.dtype)


def split_cols(u, sizes):
    outs = []
    start = 0
    for s in sizes:
        outs.append(u[..., start:start + s])
        start += s
    return outs


def causal_dwconv(u, buf, w, b):
    width = w.shape[0]
    L = u.shape[1]
    full = jnp.concatenate([buf.astype(u.dtype), u], axis=1)
    y = b.astype(u.dtype) + sum(full[:, j:j + L] * w[j].astype(u.dtype) for j in range(width))
    return y, full[:, L:]


def chunk_len(L):
    return CHUNK if L % CHUNK == 0 else L


def to_chunks(a, cl):
    B, L = a.shape[:2]
    return jnp.moveaxis(a.reshape((B, L // cl, cl) + a.shape[2:]), 1, 0)


def from_chunks(a):
    nc, B, cl = a.shape[:3]
    return jnp.moveaxis(a, 0, 1).reshape((B, nc * cl) + a.shape[3:])


def mlstm_chunked(q, k, v, ig, lf, state):
    cl = chunk_len(q.shape[1])
    causal = jnp.tril(jnp.ones((cl, cl), dtype=bool))

    def step(carry, inp):
        C0, n0, m0 = carry
        qc, kc, vc, igc, lfc = inp
        qf = qc.astype(jnp.float32) * (M_DK ** -0.5)
        kf = kc.astype(jnp.float32)
        vf = vc.astype(jnp.float32)
        bt = jnp.moveaxis(jnp.cumsum(lfc, axis=1), 1, 2)
        it = jnp.moveaxis(igc, 1, 2)
        dmat = jnp.where(causal, bt[..., :, None] - bt[..., None, :] + it[..., None, :], -jnp.inf)
        inter = bt + m0[..., None]
        m_t = jnp.maximum(inter, jnp.max(dmat, -1))
        w_intra = jnp.exp(dmat - m_t[..., None])
        w_inter = jnp.exp(inter - m_t)
        s = jnp.einsum('bthd,bshd->bhts', qf, kf) * w_intra
        num = (jnp.einsum('bhts,bshe->bthe', s, vf)
               + jnp.einsum('bthd,bhde->bthe', qf, C0) * jnp.moveaxis(w_inter, 1, 2)[..., None])
        den = jnp.sum(s, -1) + jnp.einsum('bthd,bhd->bht', qf, n0) * w_inter
        denom = jnp.maximum(jnp.abs(den), jnp.exp(-m_t))
        h = num / jnp.moveaxis(denom, 1, 2)[..., None]
        b_last = bt[..., -1]
        d_last = b_last[..., None] - bt + it
        m_new = jnp.maximum(b_last + m0, jnp.max(d_last, -1))
        w_last = jnp.exp(d_last - m_new[..., None])
        decay = jnp.exp(b_last + m0 - m_new)
        C_new = decay[..., None, None] * C0 + jnp.einsum('bhs,bshd,bshe->bhde', w_last, kf, vf)
        n_new = decay[..., None] * n0 + jnp.einsum('bhs,bshd->bhd', w_last, kf)
        return (C_new, n_new, m_new), h

    state, hs = lax.scan(step, state, tuple(to_chunks(a, cl) for a in (q, k, v, ig, lf)))
    return from_chunks(hs), state


def ssd_chunked(xs, dt, Bm, Cm, A, S0):
    Bsz = xs.shape[0]
    cl = chunk_len(xs.shape[1])
    R = S_HEADS // S_GROUPS
    causal = jnp.tril(jnp.ones((cl, cl), dtype=bool))

    def step(S, inp):
        xc, dtc, Bc, Cc = inp
        xf = xc.astype(jnp.float32)
        Bf = Bc.astype(jnp.float32)
        Cf = Cc.astype(jnp.float32)
        bt = jnp.moveaxis(jnp.cumsum(dtc * A, axis=1), 1, 2)
        seg = jnp.where(causal, bt[..., :, None] - bt[..., None, :], -jnp.inf)
        decay_ts = jnp.exp(seg).reshape(Bsz, S_GROUPS, R, cl, cl)
        cb = jnp.einsum('btgn,bsgn->bgts', Cf, Bf)
        xdt = (xf * dtc[..., None]).reshape(Bsz, cl, S_GROUPS, R, S_HEADDIM)
        y_intra = jnp.einsum('bgrts,bsgrp->btgrp', cb[:, :, None] * decay_ts, xdt)
        Sg = S.reshape(Bsz, S_GROUPS, R, S_HEADDIM, S_STATE)
        decay_t = jnp.exp(bt).reshape(Bsz, S_GROUPS, R, cl)
        y_inter = jnp.einsum('btgn,bgrpn->btgrp', Cf, Sg) * jnp.moveaxis(decay_t, 3, 1)[..., None]
        b_last = bt[..., -1]
        w_end = jnp.exp(b_last[..., None] - bt).reshape(Bsz, S_GROUPS, R, cl)
        S_add = jnp.einsum('bgrs,bsgrp,bsgn->bgrpn', w_end, xdt, Bf).reshape(Bsz, S_HEADS, S_HEADDIM, S_STATE)
        S_new = jnp.exp(b_last)[..., None, None] * S + S_add
        return S_new, (y_intra + y_inter).reshape(Bsz, cl, S_HEADS, S_HEADDIM)

    S, ys = lax.scan(step, S0, tuple(to_chunks(a, cl) for a in (xs, dt, Bm, Cm)))
    return from_chunks(ys), S


def trunk_layer(x, seg_lens, state, w):
    mconv_buf, C0, n0, m0, sconv_buf, S0, fconv_buf = state
    f32 = jnp.float32
    Bsz, L, _ = x.shape
    u = x @ w['w_in']
    qk_pre, v, o_pre, i_pre, f_pre, z, xbc_pre, dt_pre, gate_pre = split_cols(u, IN_SPLITS)

    qk, mconv_new = causal_dwconv(qk_pre, mconv_buf, w['w_mconv'], w['b_mconv'])
    qk = jax.nn.silu(qk)
    q = qk[..., :M_QK].reshape(Bsz, L, M_HEADS, M_DK)
    k = qk[..., M_QK:].reshape(Bsz, L, M_HEADS, M_DK)
    vh = v.reshape(Bsz, L, M_HEADS, M_DV)
    ig = (i_pre + w['b_if'][:M_HEADS]).astype(f32)
    lf = jax.nn.log_sigmoid((f_pre + w['b_if'][M_HEADS:]).astype(f32))
    mstate = (C0.astype(f32), n0.astype(f32), m0.astype(f32))
    hs = []
    start = 0
    for sl in seg_lens:
        sel = slice(start, start + sl)
        h_seg, mstate = mlstm_chunked(q[:, sel], k[:, sel], vh[:, sel], ig[:, sel], lf[:, sel], mstate)
        hs.append(h_seg)
        start += sl
    h = jnp.concatenate(hs, axis=1)
    mu = jnp.mean(h, -1, keepdims=True)
    var = jnp.mean(jnp.square(h - mu), -1, keepdims=True)
    h = ((h - mu) * lax.rsqrt(var + LN_EPS)).reshape(Bsz, L, M_V) * w['mnorm_g'].astype(f32)
    ya = (jax.nn.sigmoid(o_pre) * h.astype(x.dtype)) @ w['w_proj_a']

    xbc, sconv_new = causal_dwconv(xbc_pre, sconv_buf, w['w_sconv'], w['b_sconv'])
    xbc = jax.nn.silu(xbc)
    xs = xbc[..., :S_INNER].reshape(Bsz, L, S_HEADS, S_HEADDIM)
    Bm = xbc[..., S_INNER:S_INNER + S_GROUPS * S_STATE].reshape(Bsz, L, S_GROUPS, S_STATE)
    Cm = xbc[..., S_INNER + S_GROUPS * S_STATE:].reshape(Bsz, L, S_GROUPS, S_STATE)
    dt = jax.nn.softplus((dt_pre + w['dt_bias']).astype(f32))
    A = -jnp.exp(w['A_log'].astype(f32))
    S = S0.astype(f32)
    ys = []
    start = 0
    for sl in seg_lens:
        sel = slice(start, start + sl)
        y_seg, S = ssd_chunked(xs[:, sel], dt[:, sel], Bm[:, sel], Cm[:, sel], A, S)
        ys.append(y_seg)
        start += sl
    y = jnp.concatenate(ys, axis=1) + w['D'].astype(f32)[:, None] * xs.astype(f32)
    y = y.reshape(Bsz, L, S_INNER) * jax.nn.silu(z.astype(f32))
    yg = y.reshape(Bsz, L, S_GROUPS, S_INNER // S_GROUPS)
    yg = yg * lax.rsqrt(jnp.mean(jnp.square(yg), -1, keepdims=True) + RMS_EPS)
    y = yg.reshape(Bsz, L, S_INNER) * w['snorm_g'].astype(f32)
    yb = y.astype(x.dtype) @ w['w_proj_b']

    g = jax.nn.sigmoid(gate_pre).reshape(Bsz, L, 2, D_MODEL)
    mixed = (g[:, :, 0] * ya + g[:, :, 1] * yb) @ w['w_out']
    x1 = layer_norm(ALPHA * x + mixed, w['ln1_g'], w['ln1_b'])

    up = x1 @ w['w_up']
    upc, fconv_new = causal_dwconv(up, fconv_buf, w['w_fconv'], w['b_fconv'])
    ff = (jax.nn.silu(upc[..., :D_FF]) * upc[..., D_FF:]) @ w['w_down']
    x2 = layer_norm(ALPHA * x1 + ff, w['ln2_g'], w['ln2_b'])
    C1, n1, m1 = mstate
    return x2, (mconv_new, C1, n1, m1, sconv_new, S, fconv_new)


def setup_inputs(seed: int = 0) -> dict:
    key = jax.random.key(seed)
    ks = iter(jax.random.split(key, 48))
    nrm = lambda shape, scale: jax.random.normal(next(ks), shape, jnp.float32) * scale
    x_prompt = nrm((BATCH, SEQ, D_MODEL), 1.0)
    x_sample = nrm((DEC_BATCH, DEC_SEQ, D_MODEL), 1.0)
    state_mlstm_conv = nrm((DEPTH, DEC_BATCH, M_CONV - 1, 2 * M_QK), 1.0)
    state_mlstm_C = nrm((DEPTH, DEC_BATCH, M_HEADS, M_DK, M_DV), 0.1)
    state_mlstm_n = nrm((DEPTH, DEC_BATCH, M_HEADS, M_DK), 0.5)
    state_mlstm_m = nrm((DEPTH, DEC_BATCH, M_HEADS), 1.0)
    state_ssm_conv = nrm((DEPTH, DEC_BATCH, S_CONV - 1, S_XBC), 1.0)
    state_ssm = nrm((DEPTH, DEC_BATCH, S_HEADS, S_HEADDIM, S_STATE), 0.1)
    state_ffn_conv = nrm((DEPTH, DEC_BATCH, F_CONV - 1, 2 * D_FF), 1.0)
    meta_tokens = nrm((N_META, D_MODEL), 1.0)
    ln0_g = 1.0 + nrm((D_MODEL,), 0.02)
    ln0_b = nrm((D_MODEL,), 0.02)
    w_in = nrm((DEPTH, D_MODEL, D_IN), D_MODEL ** -0.5)
    f_init = jnp.broadcast_to(jnp.linspace(3.0, 6.0, M_HEADS, dtype=jnp.float32), (DEPTH, M_HEADS))
    b_mlstm_if = jnp.concatenate([nrm((DEPTH, M_HEADS), 0.1), f_init + nrm((DEPTH, M_HEADS), 0.1)], axis=-1)
    w_mlstm_conv = nrm((DEPTH, M_CONV, 2 * M_QK), M_CONV ** -0.5)
    b_mlstm_conv = nrm((DEPTH, 2 * M_QK), 0.02)
    mlstm_norm_g = 1.0 + nrm((DEPTH, M_V), 0.02)
    w_proj_a = nrm((DEPTH, M_V, D_MODEL), M_V ** -0.5)
    w_ssm_conv = nrm((DEPTH, S_CONV, S_XBC), S_CONV ** -0.5)
    b_ssm_conv = nrm((DEPTH, S_XBC), 0.02)
    dt0 = jnp.exp(jax.random.uniform(next(ks), (DEPTH, S_HEADS), jnp.float32, math.log(1e-3), math.log(1e-1)))
    ssm_dt_bias = dt0 + jnp.log(-jnp.expm1(-dt0))
    ssm_A_log = jnp.log(jax.random.uniform(next(ks), (DEPTH, S_HEADS), jnp.float32, 1.0, 16.0))
    ssm_D = 1.0 + nrm((DEPTH, S_HEADS), 0.1)
    ssm_norm_g = 1.0 + nrm((DEPTH, S_INNER), 0.02)
    w_proj_b = nrm((DEPTH, S_INNER, D_MODEL), S_INNER ** -0.5)
    w_out = nrm((DEPTH, D_MODEL, D_MODEL), BETA * D_MODEL ** -0.5)
    ln1_g = 1.0 + nrm((DEPTH, D_MODEL), 0.02)
    ln1_b = nrm((DEPTH, D_MODEL), 0.02)
    w_up = nrm((DEPTH, D_MODEL, 2 * D_FF), D_MODEL ** -0.5)
    w_ffn_conv = nrm((DEPTH, F_CONV, 2 * D_FF), F_CONV ** -0.5)
    b_ffn_conv = nrm((DEPTH, 2 * D_FF), 0.02)
    w_down = nrm((DEPTH, D_FF, D_MODEL), BETA * D_FF ** -0.5)
    ln2_g = 1.0 + nrm((DEPTH, D_MODEL), 0.02)
    ln2_b = nrm((DEPTH, D_MODEL), 0.02)
    return {'x_prompt': x_prompt, 'x_sample': x_sample,
            'state_mlstm_conv': state_mlstm_conv, 'state_mlstm_C': state_mlstm_C,
            'state_mlstm_n': state_mlstm_n, 'state_mlstm_m': state_mlstm_m,
            'state_ssm_conv': state_ssm_conv, 'state_ssm': state_ssm, 'state_ffn_conv': state_ffn_conv,
            'meta_tokens': meta_tokens, 'ln0_g': ln0_g, 'ln0_b': ln0_b, 'w_in': w_in,
            'b_mlstm_if': b_mlstm_if, 'w_mlstm_conv': w_mlstm_conv, 'b_mlstm_conv': b_mlstm_conv,
            'mlstm_norm_g': mlstm_norm_g, 'w_proj_a': w_proj_a, 'w_ssm_conv': w_ssm_conv,
            'b_ssm_conv': b_ssm_conv, 'ssm_dt_bias': ssm_dt_bias, 'ssm_A_log': ssm_A_log, 'ssm_D': ssm_D,
            'ssm_norm_g': ssm_norm_g, 'w_proj_b': w_proj_b, 'w_out': w_out, 'ln1_g': ln1_g, 'ln1_b': ln1_b,
            'w_up': w_up, 'w_ffn_conv': w_ffn_conv, 'b_ffn_conv': b_ffn_conv, 'w_down': w_down,
            'ln2_g': ln2_g, 'ln2_b': ln2_b}


def reference(x_prompt, x_sample, state_mlstm_conv, state_mlstm_C, state_mlstm_n, state_mlstm_m,
              state_ssm_conv, state_ssm, state_ffn_conv, meta_tokens, ln0_g, ln0_b, w_in,
              b_mlstm_if, w_mlstm_conv, b_mlstm_conv, mlstm_norm_g, w_proj_a, w_ssm_conv,
              b_ssm_conv, ssm_dt_bias, ssm_A_log, ssm_D, ssm_norm_g, w_proj_b, w_out, ln1_g, ln1_b,
              w_up, w_ffn_conv, b_ffn_conv, w_down, ln2_g, ln2_b):
    f32 = jnp.float32
    Bp = x_prompt.shape[0]
    meta = jnp.broadcast_to(meta_tokens[None].astype(x_prompt.dtype), (Bp, N_META, D_MODEL))
    hp = layer_norm(jnp.concatenate([meta, x_prompt], axis=1), ln0_g, ln0_b)
    hs = layer_norm(x_sample, ln0_g, ln0_b)
    p_new = []
    s_new = []
    for l in range(DEPTH):
        w = {'w_in': w_in[l], 'b_if': b_mlstm_if[l], 'w_mconv': w_mlstm_conv[l], 'b_mconv': b_mlstm_conv[l],
             'mnorm_g': mlstm_norm_g[l], 'w_proj_a': w_proj_a[l], 'w_sconv': w_ssm_conv[l],
             'b_sconv': b_ssm_conv[l], 'dt_bias': ssm_dt_bias[l], 'A_log': ssm_A_log[l], 'D': ssm_D[l],
             'snorm_g': ssm_norm_g[l], 'w_proj_b': w_proj_b[l], 'w_out': w_out[l],
             'ln1_g': ln1_g[l], 'ln1_b': ln1_b[l], 'w_up': w_up[l], 'w_fconv': w_ffn_conv[l],
             'b_fconv': b_ffn_conv[l], 'w_down': w_down[l], 'ln2_g': ln2_g[l], 'ln2_b': ln2_b[l]}
        p0 = (jnp.zeros((Bp, M_CONV - 1, 2 * M_QK), hp.dtype),
              jnp.zeros((Bp, M_HEADS, M_DK, M_DV), f32),
              jnp.zeros((Bp, M_HEADS, M_DK), f32),
              jnp.zeros((Bp, M_HEADS), f32),
              jnp.zeros((Bp, S_CONV - 1, S_XBC), hp.dtype),
              jnp.zeros((Bp, S_HEADS, S_HEADDIM, S_STATE), f32),
              jnp.zeros((Bp, F_CONV - 1, 2 * D_FF), hp.dtype))
        hp, ps = trunk_layer(hp, (N_META, hp.shape[1] - N_META), p0, w)
        s0 = (state_mlstm_conv[l], state_mlstm_C[l], state_mlstm_n[l], state_mlstm_m[l],
              state_ssm_conv[l], state_ssm[l], state_ffn_conv[l])
        hs, ss = trunk_layer(hs, (hs.shape[1],), s0, w)
        p_new.append(ps)
        s_new.append(ss)
    pn = [jnp.stack([t[i] for t in p_new]) for i in range(7)]
    sn = [jnp.stack([t[i] for t in s_new]) for i in range(7)]
    y_prompt = hp[:, N_META:]
    y_sample = hs
    return (y_prompt, y_sample, pn[0], pn[1], pn[2], pn[3], pn[4], pn[5], pn[6],
            sn[0], sn[1], sn[2], sn[3], sn[4], sn[5], sn[6])
```

```python
import numpy as np
import ml_dtypes
import concourse.bass as bass
import concourse.mybir as mybir
from concourse.bass_utils import run_bass_kernel_spmd

F32 = mybir.dt.float32
BF16 = mybir.dt.bfloat16
ALU = mybir.AluOpType
AF = mybir.ActivationFunctionType
AX = mybir.AxisListType

D = 1024
DIN = 10280
DFF = 2816
NEG = -30000.0
ALPHA = 2.0 ** 0.25
LN_EPS = 1e-5
RMS_EPS = 1e-5
QSCALE = 128.0 ** -0.5


class Buf:
    def __init__(self, name, t, space):
        self.name = name
        self.t = t
        self.space = space
        self.last_w = None
        self.readers = []
        self.sem_in = None
        self.cnt_in = 0
        self.sem_out = None
        self.cnt_out = 0

    def __getitem__(self, idx):
        return View(self, self.t[idx])

    def ap(self):
        return View(self, self.t[:] if self.space != 'dram' else self.t)


class View:
    def __init__(self, buf, ap):
        self.buf = buf
        self.ap = ap

    def __getitem__(self, idx):
        return View(self.buf, self.ap[idx])

    def rearrange(self, *a, **k):
        return View(self.buf, self.ap.rearrange(*a, **k))

    def bc(self, axis, shape):
        return View(self.buf, self.ap.unsqueeze(axis).to_broadcast(list(shape)))

    def bitcast(self, dt):
        return View(self.buf, self.ap.bitcast(dt))


def _bufs(vs):
    out = []
    for v in vs:
        if v is None or isinstance(v, (int, float)):
            continue
        b = v.buf if isinstance(v, View) else v
        if b not in out:
            out.append(b)
    return out


class Sched:
    ENGS = ('pe', 'act', 'dve', 'pool', 'sp')

    def __init__(self, nc):
        self.nc = nc
        self.sem = {e: nc.alloc_semaphore('sem_' + e) for e in self.ENGS}
        self.cnt = {e: 0 for e in self.ENGS}
        self.ops = {e: [] for e in self.ENGS}
        self.seen = {e: {} for e in self.ENGS}
        self.final_tokens = []
        self.sb_off = 16512
        self.sb_end = 229376
        self.nsem = 5

    def sbuf(self, name, shape, dtype, at=None):
        nbytes = int(np.prod(shape[1:])) * (2 if dtype == BF16 else 4)
        nbytes = (nbytes + 31) // 32 * 32
        if at is None:
            at = self.sb_off
            self.sb_off += nbytes
            assert self.sb_off <= self.sb_end, ('SBUF overflow', name, self.sb_off)
        t = self.nc.alloc_sbuf_tensor_at(name, list(shape), dtype, offset=at)
        b = Buf(name, t, 'sbuf')
        b.off = at
        b.nbytes = nbytes
        return b

    def psum(self, name, shape, dtype=F32):
        t = self.nc.alloc_psum_tensor(name, list(shape), dtype)
        return Buf(name, t, 'psum')

    def dram(self, name, shape, dtype, kind):
        t = self.nc.dram_tensor(name, list(shape), dtype, kind=kind)
        return Buf(name, t.ap(), 'dram')

    def alias_phase(self, old, new):
        toks = []
        for b in old:
            if b.last_w is not None:
                toks.append(b.last_w)
            toks.extend(b.readers)
        for b in new:
            b.readers = list(b.readers) + toks

    def _need(self, eng, waits, tok):
        sem, val, teng = tok
        key = id(sem)
        if self.seen[eng].get(key, 0) >= val:
            return
        if key not in waits or waits[key][1] < val:
            waits[key] = (sem, val)

    def _deps(self, eng, reads, writes):
        waits = {}
        for b in reads:
            tok = b.last_w
            if tok is not None and not (tok[2] == eng and eng == 'pe'):
                self._need(eng, waits, tok)
            if b.space == 'psum':
                for r in b.readers:
                    if r[2] != eng:
                        self._need(eng, waits, r)
        for b in writes:
            tok = b.last_w
            if tok is not None and not (tok[2] == eng and eng == 'pe'):
                self._need(eng, waits, tok)
            for r in b.readers:
                if not (r[2] == eng and eng == 'pe'):
                    self._need(eng, waits, r)
        for key, (sem, val) in waits.items():
            self.seen[eng][key] = val
        return list(waits.values())

    def op(self, eng, fn, reads=(), writes=()):
        reads = _bufs(reads)
        writes = _bufs(writes)
        waits = self._deps(eng, reads, writes)
        self.cnt[eng] += 1
        tok = (self.sem[eng], self.cnt[eng], eng)
        self.ops[eng].append((waits, fn, (self.sem[eng], 1)))
        for b in writes:
            b.last_w = tok
            b.readers = []
        for b in reads:
            if b not in writes:
                b.readers.append(tok)
        return tok

    def dma(self, q, out, in_, **kw):
        ob, ib = out.buf, in_.buf
        waits = self._deps(q, [ib], [ob])
        if ob.space != 'dram':
            if ob.sem_in is None:
                ob.sem_in = self.nc.alloc_semaphore('din_' + ob.name)
                self.nsem += 1
            ob.cnt_in += 16
            sem, val = ob.sem_in, ob.cnt_in
        else:
            if ib.sem_out is None:
                ib.sem_out = self.nc.alloc_semaphore('dout_' + ib.name)
                self.nsem += 1
            ib.cnt_out += 16
            sem, val = ib.sem_out, ib.cnt_out
        tok = (sem, val, 'dma')
        oap, iap = out.ap, in_.ap

        def fn(e, oap=oap, iap=iap, kw=kw):
            return e.dma_start(out=oap, in_=iap, **kw)
        self.ops[q].append((waits, fn, (sem, 16)))
        ob.last_w = tok
        ob.readers = []
        ib.readers.append(tok)
        if ob.space == 'dram':
            self.final_tokens.append(tok)
        return tok

    def emit(self):
        nc = self.nc
        last = {}
        for sem, val, _ in self.final_tokens:
            k = id(sem)
            if k not in last or last[k][1] < val:
                last[k] = (sem, val)
        fin = list(last.values())
        eng_obj = {'pe': 'tensor', 'act': 'scalar', 'dve': 'vector', 'pool': 'gpsimd', 'sp': 'sync'}
        with nc.Block() as block:
            def mk(eng):
                def body(e):
                    for waits, fn, inc in self.ops[eng]:
                        for sem, val in waits:
                            e.wait_ge(sem, val)
                        fn(e).then_inc(inc[0], inc[1])
                    if eng == 'sp':
                        for sem, val in fin:
                            e.wait_ge(sem, val)
                return body
            for eng, attr in eng_obj.items():
                getattr(block, attr)(mk(eng))


def _const_tables():
    p = np.arange(128)[:, None]
    j = np.arange(128)[None, :]
    f = {}
    f['ident'] = (p == j)
    f['ones'] = np.ones((128, 128))
    f['U'] = (p <= j)
    f['LS'] = (p > j)
    sb = (p // 4 == j // 4) & (p < 64) & (j < 64)
    f['Us'] = ((p <= j) & sb)[:, :64]
    f['LSs'] = ((p > j) & sb)[:, :64]
    f['BOs'] = sb[:, :64]
    f['SELp'] = np.repeat(p == 127, 128, axis=1)
    f['SELm'] = np.repeat(p == 15, 128, axis=1)
    b16 = np.arange(16)[None, :]
    f['RS'] = (p == 4 * b16 + 3)
    f['BM'] = (p // 4 == b16) & (p < 64)
    f['BMT'] = ((p < 16) & (j // 4 == p))[:, :64]
    f['LSEL'] = ((j == 4 * (p // 4) + 3) & (p < 64))[:, :64]
    f['H0'] = np.repeat(p < 64, 128, axis=1) & (j < 64)
    f['H1'] = np.repeat(p < 64, 128, axis=1) & (j >= 64)
    cf_off, cols = {}, []
    o = 0
    for k, v in f.items():
        cf_off[k] = (o, v.shape[1])
        o += v.shape[1]
        cols.append(v.astype(np.float32))
    cf = np.concatenate(cols, axis=1)
    g = {}
    g['identb'] = (p == j).astype(np.float32)
    g['onesb'] = np.ones((128, 128), np.float32)
    g['M'] = np.where(j <= p, 0.0, NEG)
    g['MT'] = np.where(p <= j, 0.0, NEG)
    g['Ms'] = np.where((j <= p) & sb, 0.0, NEG)[:, :64]
    g['MTs'] = np.where((p <= j) & sb, 0.0, NEG)[:, :64]
    jj = np.arange(64)[None, None, :]
    bb = np.arange(16)[None, :, None]
    g['CM'] = np.broadcast_to((jj // 4 == bb), (128, 16, 64)).reshape(128, 1024).astype(np.float32)
    cb_off, cols = {}, []
    o = 0
    for k, v in g.items():
        cb_off[k] = (o, v.shape[1])
        o += v.shape[1]
        cols.append(np.asarray(v, np.float32))
    cbm = np.concatenate(cols, axis=1).astype(ml_dtypes.bfloat16)
    return cf, cf_off, cbm, cb_off


class Chunk:
    def __init__(self, slot, col0, n, kind, row0=0):
        self.slot, self.col0, self.n, self.kind, self.row0 = slot, col0, n, kind, row0


class Tile:
    def __init__(self, name, T, chunks, segs):
        self.name, self.T, self.chunks, self.segs = name, T, chunks, segs


W_SHAPES = {'w_in': (D, DIN), 'w_proj_a': (D, D), 'w_proj_b': (2 * D, D), 'w_out': (D, D),
            'w_up': (D, 2 * DFF), 'w_down': (DFF, D)}


def tile_blocks():
    bl = []
    for c in range(0, 3072, 256):
        bl.append(('w_in', 0, 8, [(c, 256)]))
    bl.append(('w_in', 0, 8, [(3072, 8), (8200, 32)]))
    for c in range(3080, 5128, 256):
        bl.append(('w_in', 0, 8, [(c, 256)]))
    for c in range(5128, 8200, 256):
        bl.append(('w_in', 0, 8, [(c, 256)]))
    for c in range(8232, 10280, 256):
        bl.append(('w_in', 0, 8, [(c, 256)]))
    for j in range(4):
        bl.append(('w_proj_a', 0, 8, [(256 * j, 256)]))
        bl.append(('w_proj_b', 0, 8, [(256 * j, 256)]))
        bl.append(('w_proj_b', 8, 8, [(256 * j, 256)]))
    for j in range(4):
        bl.append(('w_out', 0, 8, [(256 * j, 256)]))
    for j in range(11):
        bl.append(('w_up', 0, 8, [(256 * j, 256)]))
        bl.append(('w_up', 0, 8, [(DFF + 256 * j, 256)]))
    for j in range(4):
        for k0, nk in ((0, 8), (8, 8), (16, 6)):
            bl.append(('w_down', k0, nk, [(256 * j, 256)]))
    return bl


class K:
    def __init__(self, debug=None, tiles=('T0', 'T1', 'T2', 'T3', 'T4')):
        self.debug = debug or {}
        self.tile_sel = tuple(tiles)
        self.ntiles = len(self.tile_sel)
        nc = bass.Bass('TRN2', target_bir_lowering=False)
        self.nc = nc
        self.S = S = Sched(nc)
        self.dumps = {}
        cf, self.cfo, cbm, self.cbo = _const_tables()
        self.cf_np, self.cb_np = cf, cbm
        din = lambda n, s, dt=F32: S.dram(n, s, dt, 'ExternalInput')
        dout = lambda n, s: S.dram(n, s, F32, 'ExternalOutput')
        I = self.I = {}
        I['xp'] = din('xp', [2048, D]); I['xs'] = din('xs', [64, D]); I['meta'] = din('meta', [16, D])
        I['s_mconv'] = din('s_mconv', [48, 1024]); I['s_C'] = din('s_C', [16, 4, 128, 256])
        I['s_n'] = din('s_n', [64, 128]); I['s_m'] = din('s_m', [16, 4])
        I['s_sconv'] = din('s_sconv', [48, 3072]); I['s_ssm'] = din('s_ssm', [16, 2048, 128])
        I['s_fconv'] = din('s_fconv', [32, 2 * DFF])
        I['cf'] = din('cf', list(cf.shape)); I['cb'] = din('cb', list(cbm.shape), BF16)
        for n, s in (('ln0_g', [D]), ('ln0_b', [D]), ('b_if', [8]), ('w_mconv', [4, 1024]), ('b_mconv', [1, 1024]),
                     ('mnorm_g', [1, 1024]), ('w_sconv', [4, 3072]), ('b_sconv', [1, 3072]), ('dt_bias', [32]),
                     ('A_log', [32]), ('ssm_D', [32]), ('snorm_g', [1, 2048]), ('ln1_g', [D]), ('ln1_b', [D]),
                     ('w_fconv', [3, 2 * DFF]), ('b_fconv', [1, 2 * DFF]), ('ln2_g', [D]), ('ln2_b', [D])):
            I[n] = din(n, s)
        for n, s in W_SHAPES.items():
            I[n] = din(n, list(s))
        O = self.O = {}
        O['y_p'] = dout('y_p', [2048, D]); O['y_s'] = dout('y_s', [64, D])
        O['p_mconv'] = dout('p_mconv', [3, 1024]); O['p_C'] = dout('p_C', [4, 128, 256])
        O['p_n'] = dout('p_n', [4, 128]); O['p_m'] = dout('p_m', [1, 4])
        O['p_sconv'] = dout('p_sconv', [3, 3072]); O['p_ssm'] = dout('p_ssm', [2048, 128])
        O['p_fconv'] = dout('p_fconv', [2, 2 * DFF])
        O['o_mconv'] = dout('o_mconv', [48, 1024]); O['o_C'] = dout('o_C', [16, 4, 128, 256])
        O['o_n'] = dout('o_n', [64, 128]); O['o_m'] = dout('o_m', [16, 4])
        O['o_sconv'] = dout('o_sconv', [48, 3072]); O['o_ssm'] = dout('o_ssm', [16, 2048, 128])
        O['o_fconv'] = dout('o_fconv', [32, 2 * DFF])
        self.rr = {}
        self.build()
        S.emit()

    def rot(self, key, n):
        i = self.rr.get(key, 0)
        self.rr[key] = i + 1
        return i % n

    def mm(self, out, lhsT, rhs, start=True, stop=True):
        self.S.op('pe', lambda e: e.matmul(out.ap, lhsT=lhsT.ap, rhs=rhs.ap, start=start, stop=stop),
                  reads=[lhsT, rhs], writes=[out])

    def tr(self, out, in_, ident):
        self.S.op('pe', lambda e: e.transpose(out=out.ap, in_=in_.ap, identity=ident.ap),
                  reads=[in_, ident], writes=[out])

    def act(self, out, in_, func=AF.Copy, bias=None, scale=None, accum=None):
        kw = {}
        if bias is not None:
            kw['bias'] = bias.ap if isinstance(bias, View) else bias
        if scale is not None:
            kw['scale'] = scale.ap if isinstance(scale, View) else scale
        if accum is not None:
            kw['accum_out'] = accum.ap
        self.S.op('act', lambda e: e.activation(out=out.ap, in_=in_.ap, func=func, **kw),
                  reads=[in_, bias, scale], writes=[out, accum])

    def tt(self, out, a, b, op, eng='dve'):
        self.S.op(eng, lambda e: e.tensor_tensor(out=out.ap, in0=a.ap, in1=b.ap, op=op),
                  reads=[a, b], writes=[out])

    def ts(self, out, a, s1, op0, s2=None, op1=None, eng='dve', accum=None):
        v1 = s1.ap if isinstance(s1, View) else s1
        v2 = s2.ap if isinstance(s2, View) else s2
        kw = {}
        if op1 is not None:
            kw['op1'] = op1
        if accum is not None:
            kw['accum_out'] = accum.ap
        self.S.op(eng, lambda e: e.tensor_scalar(out=out.ap, in0=a.ap, scalar1=v1, scalar2=v2, op0=op0, **kw),
                  reads=[a, s1, s2], writes=[out, accum])

    def stt(self, out, a, s, b, op0, op1, eng='dve'):
        v = s.ap if isinstance(s, View) else s
        self.S.op(eng, lambda e: e.scalar_tensor_tensor(out=out.ap, in0=a.ap, scalar=v, in1=b.ap, op0=op0, op1=op1),
                  reads=[a, s, b], writes=[out])

    def cp(self, out, in_, eng='dve'):
        if eng == 'act':
            return self.act(out, in_)
        self.S.op(eng, lambda e: e.tensor_copy(out=out.ap, in_=in_.ap), reads=[in_], writes=[out])

    def memset(self, out, val, eng='dve'):
        self.S.op(eng, lambda e: e.memset(out.ap, val), writes=[out])

    def rmax(self, out, in_, eng='dve'):
        self.S.op(eng, lambda e: e.tensor_reduce(out=out.ap, in_=in_.ap, axis=AX.X, op=ALU.max),
                  reads=[in_], writes=[out])

    def dma(self, out, in_, q='sp'):
        self.S.dma(q, out, in_)

    def dump(self, name, view, shape):
        if name not in self.debug:
            return
        d = self.S.dram('dbg_' + name, list(shape), view.ap.dtype, 'ExternalOutput')
        self.dumps[name] = d
        self.dma(d.ap(), view)

    def cfv(self, name, rows=128, cols=None):
        o, w = self.cfo[name]
        cols = w if cols is None else cols
        return self.cf[:rows, o:o + cols]

    def cbv(self, name, rows=128, cols=None):
        o, w = self.cbo[name]
        cols = w if cols is None else cols
        return self.cb[:rows, o:o + cols]

    def ws_init(self):
        S = self.S
        self.wlist = tile_blocks()
        self.nbt = len(self.wlist)
        self.wblocks = self.wlist * self.ntiles
        self.wst = [S.sbuf(f'wst{i}', [128, 8, 256], F32) for i in range(2)]
        self.wbf = [S.sbuf(f'wbf{i}', [128, 8, 256], BF16) for i in range(2)]
        self.wx4 = [S.sbuf(f'wrx{i}', [128, 8, 256], BF16, at=self.wst[i // 2].off + 4096 * (i % 2)) for i in range(4)]
        self.wring = self.wbf + self.wx4
        self.wscr = [S.dram(f'wscr{j}', [128, 8, 256], BF16, 'Internal') for j in range(self.nbt)] if self.ntiles > 1 else None
        self.w_loaded = 0
        self.w_cast = 0
        self.w_next = 0
        self.w_ring_started = False

    def _w_load(self, i):
        name, k0, nk, parts = self.wblocks[i]
        st = self.wst[i % 2]
        W = self.I[name]
        c = 0
        for (c0, n) in parts:
            src = View(W, W.t[k0 * 128:(k0 + nk) * 128, c0:c0 + n].rearrange('(k p) c -> p k c', p=128))
            self.dma(st[:, 0:nk, c:c + n], src, q='sp')
            c += n

    def _w_castop(self, i):
        name, k0, nk, parts = self.wblocks[i]
        n = sum(p[1] for p in parts)
        eng = 'dve' if (i % 4) != 3 else 'act'
        self.cp(self.wbf[i % 2][:, 0:nk, 0:n], self.wst[i % 2][:, 0:nk, 0:n], eng)
        if self.wscr is not None:
            self.dma(self.wscr[i][:, 0:nk, 0:n], self.wbf[i % 2][:, 0:nk, 0:n], q='sp')

    def _w_ringload(self, i):
        name, k0, nk, parts = self.wblocks[i]
        n = sum(p[1] for p in parts)
        dst = self.wring[(i - self.nbt) % 6]
        self.dma(dst[:, 0:nk, 0:n], self.wscr[i % self.nbt][:, 0:nk, 0:n], q='sp')

    def wnext(self):
        i = self.w_next
        nb = len(self.wblocks)
        self.w_next += 1
        if i < self.nbt:
            lim = self.nbt
            while self.w_loaded < min(lim, i + 2):
                self._w_load(self.w_loaded)
                self.w_loaded += 1
            while self.w_cast < min(lim, i + 2):
                self._w_castop(self.w_cast)
                self.w_cast += 1
            while self.w_loaded < min(lim, i + 3):
                self._w_load(self.w_loaded)
                self.w_loaded += 1
            return self.wbf[i % 2], self.wblocks[i]
        if not self.w_ring_started:
            self.w_ring_started = True
            self.S.alias_phase(self.wst, self.wx4)
            self.w_loaded = self.nbt
        while self.w_loaded < min(nb, i + 6):
            self._w_ringload(self.w_loaded)
            self.w_loaded += 1
        return self.wring[(i - self.nbt) % 6], self.wblocks[i]

    def build(self):
        S = self.S
        sb = S.sbuf
        ncf, ncb = self.cf_np.shape[1], self.cb_np.shape[1]
        self.cf = sb('cf', [128, ncf], F32)
        self.cb = sb('cb', [128, ncb], BF16)
        self.lnc = sb('lnc', [128, 2, D], F32)
        self.bif_b = sb('bif_b', [128, 8], F32)
        self.dtb_b = sb('dtb_b', [128, 32], F32)
        self.A_b = sb('A_b', [128, 32], F32)
        self.D_b = sb('D_b', [128, 32], F32)
        self.Dfm = sb('Dfm', [128, 16], F32)
        self.cwm = sb('cwm', [128, 8, 6], F32)
        self.cws = sb('cws', [128, 24, 5], F32)
        self.sng = sb('sng', [128, 16], F32)
        self.cwf = sb('cwf', [128, 44, 4], F32)
        self.ws_init()
        self.xr = [sb(f'xr{i}', [128, D], F32) for i in range(4)]
        self.zs = [None] * 4
        self.xnT = sb('xnT', [128, 8, 512], BF16)
        self.hgT = sb('hgT', [128, 8, 512], BF16)
        self.ygT = sb('ygT', [128, 16, 512], BF16)
        self.Cf = sb('Cf', [128, 4, 256], F32); self.Cb = sb('Cb', [128, 4, 256], BF16)
        self.nf = sb('nf', [128, 4], F32); self.nb = sb('nb', [128, 4], BF16)
        self.m_b = sb('m_b', [128, 4], F32)
        self.STf = sb('STf', [128, 2048], F32); self.STb = sb('STb', [128, 2048], BF16)
        self.cq = sb('cq', [128, 8, 3], F32); self.cx = sb('cx', [128, 24, 3], F32)
        self.cff = sb('cff', [128, 44, 2], F32)
        self.scar = sb('scar', [128, 44 * 16 * 2], F32)
        self.gat = sb('gat', [128, 4, 8], F32)
        self.dta = sb('dta', [128, 4, 64], F32)
        self.ifdt = sb('ifdt', [128, 4, 40], F32)
        self.sm = [sb(f'sm{i}', [128, 32], F32) for i in range(16)]
        self.xb16 = sb('xb16', [128, D], BF16)
        self.cst = sb('cst', [128, 8], F32)
        self.pn_st = sb('pn_st', [128, 128], F32)
        R0 = S.sb_off
        o = R0
        def at(name, shape, dt):
            nonlocal o
            b = sb(name, shape, dt, at=o)
            o += b.nbytes
            return b
        self.cE = [at(f'cE{i}', [128, 520], F32) for i in range(2)]
        self.cacc = [at(f'cacc{i}', [128, 512], F32) for i in range(3)]
        self.cth = [at(f'cth{i}', [128, 512], F32) for i in range(2)]
        self.cacc2 = [at(f'cacc2_{i}', [128, 512], F32) for i in range(2)]
        e1 = o
        o = R0
        self.R1 = at('R1', [128, 4, 128], F32); self.R2 = at('R2', [128, 4, 128], F32)
        self.R3 = at('R3', [128, 4, 128], F32); self.wT = at('wT', [128, 4, 128], F32)
        self.ST = at('ST', [128, 4, 128], BF16); self.kTM = at('kTM', [128, 4, 128], BF16)
        self.hh = at('hh', [128, 4, 256], F32); self.vw = at('vw', [128, 4, 256], BF16)
        self.hgTM = at('hgTM', [128, D], BF16)
        e2 = o
        o = R0
        self.xdt = at('xdt', [128, 2048], BF16); self.xsD = at('xsD', [128, 2048], BF16)
        self.wx = at('wx', [128, 512], BF16); self.BTM = at('BTM', [128, 512], BF16)
        self.LT = at('LT', [128, 8, 128], BF16); self.MTt = at('MTt', [128, 8, 128], BF16)
        self.t1 = at('t1', [128, 512], F32)
        self.yz = at('yz', [128, 2048], BF16)
        self.ynTM = at('ynTM', [128, 2048], BF16)
        e3 = o
        F0 = max(e1, e2, e3)
        conv_end = F0
        o = F0
        self.qkT = at('qkT', [128, 8, 512], BF16)
        self.v = [at(f'v{i}', [128, D], BF16) for i in range(4)]
        self.oth = [at(f'oth{i}', [128, D], BF16) for i in range(4)]
        a1_end = o
        o = F0
        self.xbcT = at('xbcT', [128, 24, 512], BF16)
        for i in range(2):
            self.zs[i] = at(f'zs{i}', [128, 2048], BF16)
        self.LA = at('LA', [128, 8, 128], F32)
        a2_end = o
        o = F0
        self.gth = at('gth', [128, 16, 512], BF16)
        self.mixT = at('mixT', [128, 8, 512], BF16)
        self.hffT = at('hffT', [128, 22, 512], BF16)
        b_end = o
        S.sb_off = max(a1_end, a2_end, b_end)
        for i in range(2, 4):
            self.zs[i] = sb(f'zs{i}', [128, 2048], BF16)
        self.arenas = [(self.xr[2].off, 2 * self.xr[2].nbytes), (self.zs[2].off, 2 * self.zs[2].nbytes)]
        a0, a1 = self.arenas[0][0], self.arenas[1][0]
        self.C0b = [sb(f'C0b{i}', [128, 4, 256], F32, at=a0 + 4096 * i) for i in range(2)]
        self.C0b16 = [sb(f'C0b16_{i}', [128, 4, 258], BF16, at=a1 + 2080 * i) for i in range(2)]
        self.qmb = [sb(f'qmb{i}', [128, 4, 64], BF16, at=a1 + 4160 + 512 * i) for i in range(2)]
        self.kTMm = [sb(f'kTMm{i}', [128, 4, 128], BF16, at=a1 + 5184 + 1024 * i) for i in range(2)]
        self.n0T = sb('n0T', [128, 64], F32, at=a1 + 7232)
        self.n16 = sb('n16', [128, 64], BF16, at=a1 + 7488)
        self.decS = sb('decS', [128, 64], F32, at=a1 + 7616)
        self.Rm = sb('Rm', [128, 64], F32, at=a1 + 7872)
        self.S0b = [sb('S0b0', [128, 16, 128], F32, at=a0), sb('S0b1', [128, 16, 128], F32, at=self.LT.off)]
        assert self.LT.off + 8192 <= self.ynTM.off
        self.SbT = sb('SbT', [128, 2048], BF16, at=a1)
        self.wxm = sb('wxm', [128, 2048], BF16, at=a1 + 4096)
        self.ysi = sb('ysi', [128, 2048], BF16)
        self.decP = sb('decP', [128, 256], F32)
        self.Rr = sb('Rr', [128, 2, 256], F32)
        self.CTmb = [sb(f'CTmb{i}', [128, 4, 64], BF16) for i in range(2)]
        self.grpArena2 = [self.S0b[0], self.SbT, self.wxm]
        self.xsDT = sb('xsDT', [128, 8, 128], BF16, at=self.ynTM.off)
        self.MTt2 = sb('MTt2', [128, 8, 128], BF16, at=self.ynTM.off + 2048)
        self.grpArena = self.C0b + self.C0b16 + self.qmb + self.kTMm + [self.n0T, self.n16, self.decS, self.Rm]
        self.grpA1conv = self.cE + self.cacc + self.cth + self.cacc2
        self.grpA1rec = [self.R1, self.R2, self.R3, self.wT, self.ST, self.kTM, self.hh, self.vw, self.hgTM]
        self.grpA1fix = [self.qkT] + self.v + self.oth
        self.grpA2fix = [self.xbcT, self.zs[0], self.zs[1], self.LA]
        self.grpA2rec = [self.xdt, self.xsD, self.wx, self.BTM, self.LT, self.MTt, self.t1, self.yz, self.ynTM]
        self.grpB = [self.gth, self.mixT, self.hffT]
        print('SBUF used', S.sb_off, 'of', S.sb_end, 'R', R0, conv_end - R0, a1_end - R0, a2_end - R0, b_end - R0)
        self.ps = [S.psum(f'ps{i}', [128, 512], F32) for i in range(8)]
        self.setup()
        tiles = self.make_tiles()
        for tl in tiles:
            if tl.name in self.tile_sel:
                self.run_tile(tl, last=(tl.name == 'T4'))

    def make_tiles(self):
        def pch(slot, col0, c):
            ch = Chunk(slot, col0, 128, 'p', row0=128 * c)
            ch.final = (c == 15)
            return ch
        m = Chunk(0, 0, 16, 'm'); m.final = False
        tiles = [Tile('T0', 400, [m] + [pch(1 + i, 16 + 128 * i, i) for i in range(3)], [(0, 1, 400, 'p')])]
        for t in range(3):
            tiles.append(Tile(f'T{t + 1}', 512, [pch(i, 128 * i, 3 + 4 * t + i) for i in range(4)], [(0, 1, 512, 'p')]))
        sc = Chunk(1, 128, 64, 's'); sc.final = False
        tiles.append(Tile('T4', 192, [pch(0, 0, 15), sc], [(0, 1, 128, 'p'), (128, 16, 4, 's')]))
        return tiles

    def psb(self, i):
        return self.ps[i].ap().bitcast(BF16)

    def setup(self):
        I = self.I
        self.dma(self.cf.ap(), I['cf'].ap())
        self.dma(self.cb.ap(), I['cb'].ap())
        pb = lambda n: View(I[n], I[n].t.partition_broadcast(128))
        self.dma(self.bif_b.ap(), pb('b_if'))
        self.dma(self.dtb_b.ap(), pb('dt_bias'))
        self.dma(self.A_b.ap(), pb('A_log'))
        self.dma(self.D_b.ap(), pb('ssm_D'))
        self.act(self.A_b.ap(), self.A_b.ap(), AF.Exp)
        self.ts(self.A_b.ap(), self.A_b.ap(), -1.0, ALU.mult)
        D3 = self.D_b.ap().rearrange('p (g r) -> p g r', r=2)
        self.cp(self.Dfm[0:64, :], D3[0:64, :, 0], 'dve')
        self.cp(self.Dfm[64:128, :], D3[64:128, :, 1], 'dve')
        identf = self.cfv('ident')
        stg = self.cacc[0]
        def fm_params(dst, rows, G, scale_groups=None):
            R = sum(r for _, r in rows)
            for g0 in range(0, G, 4):
                gn = min(4, G - g0)
                r0 = 0
                for (nm, nr) in rows:
                    self.dma(stg[r0:r0 + nr, 0:gn * 128], I[nm][:, g0 * 128:(g0 + gn) * 128])
                    r0 += nr
                bank = self.ps[self.rot('setup', 2)]
                for g in range(gn):
                    self.tr(bank[:, g * R:(g + 1) * R], stg[0:R, g * 128:(g + 1) * 128], identf[0:R, 0:R])
                self.cp(dst[:, g0:g0 + gn, :], bank[:, 0:gn * R].rearrange('p (g r) -> p g r', r=R), 'act')
        fm_params(self.cwm, [('w_mconv', 4), ('b_mconv', 1), ('mnorm_g', 1)], 8)
        fm_params(self.cws, [('w_sconv', 4), ('b_sconv', 1)], 24)
        fm_params(self.cwf, [('w_fconv', 3), ('b_fconv', 1)], 44)
        sng3 = self.sng.ap().rearrange('p (g r) -> p g r', r=1)
        fm_params(sng3, [('snorm_g', 1)], 16)
        self.ts(self.cwm[:, :, 0:6], self.cwm[:, :, 0:6], 0.5, ALU.mult)
        self.ts(self.cws.ap(), self.cws.ap(), 0.5, ALU.mult)
        self.ts(self.cwf[:, 0:22, :], self.cwf[:, 0:22, :], 0.5, ALU.mult)
        self.memset(self.cst[:, 0:1], LN_EPS)
        self.memset(self.cst[:, 1:2], 0.5 * float(np.log(128.0)))
        self.memset(self.cst[:, 2:3], 1.0)
        self.eps_t = self.cst
        for b in (self.Cf, self.nf, self.m_b, self.STf, self.cq, self.cx, self.cff):
            self.memset(b.ap(), 0.0)
        for b in (self.Cb, self.nb, self.STb):
            self.memset(b.ap(), 0.0, 'pool')

    def kc(self, kind, n):
        if kind == 's':
            return dict(U=self.cfv('Us', 64), LS=self.cfv('LSs', 64), BO=self.cfv('BOs', 64),
                        M=self.cbv('Ms', 64), MT=self.cbv('MTs', 64))
        return dict(U=self.cfv('U', n, n), LS=self.cfv('LS', n, n), BO=self.cfv('ones', n, n),
                    M=self.cbv('M', n, n), MT=self.cbv('MT', n, n))

    def ln_rows(self, x, n, gname, bname):
        I = self.I
        st, mv, rs = self.sm[0], self.sm[1], self.sm[2]
        self.dma(self.lnc[:, 0, :], View(I[gname], I[gname].t.partition_broadcast(128)))
        self.dma(self.lnc[:, 1, :], View(I[bname], I[bname].t.partition_broadcast(128)))
        for i in range(2):
            self.S.op('dve', lambda e, i=i: e.bn_stats(out=st.t[:n, i * 6:(i + 1) * 6], in_=x.ap[:, i * 512:(i + 1) * 512]),
                      reads=[x], writes=[st])
        self.S.op('dve', lambda e: e.bn_aggr(out=mv.t[:n, 0:2], in_=st.t[:n, 0:12]), reads=[st], writes=[mv])
        self.act(rs[:n, 0:1], mv[:n, 1:2], AF.Ln, bias=self.eps_t[:n, 0:1])
        self.act(rs[:n, 0:1], rs[:n, 0:1], AF.Exp, scale=-0.5)
        self.ts(x, x, mv[:n, 0:1], ALU.subtract, rs[:n, 0:1], ALU.mult)
        self.tt(x, x, self.lnc[:n, 0, :], ALU.mult)
        self.tt(x, x, self.lnc[:n, 1, :], ALU.add)

    def to_fm(self, tl, src_of_chunk, dstT):
        identb = self.cbv('identb')
        for ch in tl.chunks:
            n = ch.n
            xb = self.xb16
            self.act(xb[:n, :], src_of_chunk(ch))
            bank = 6 + self.rot('tfm', 2)
            pv = self.psb(bank)
            for k in range(8):
                self.tr(pv[:, k * n:(k + 1) * n], xb[:n, k * 128:(k + 1) * 128], identb[:n, :n])
            self.cp(dstT[:, :, ch.col0:ch.col0 + n], pv[:, 0:8 * n].rearrange('p (k n) -> p k n', n=n), 'dve')

    def _conv_taps(self, tl, psv, W, wtab, g, carry_p, scar_view, E, acc):
        Wm = W - 1
        off = 0
        for (col0, nb, L, kind) in tl.segs:
            Ev = E[:, off:off + nb * (L + Wm)].rearrange('p (b l) -> p b l', b=nb)
            pseg = psv[:, col0:col0 + nb * L].rearrange('p (b l) -> p b l', b=nb)
            if kind == 'p':
                self.cp(Ev[:, :, 0:Wm], carry_p[:, g:g + 1, :], 'act')
            elif kind == 'm':
                self.memset(Ev[:, :, 0:Wm], 0.0, 'dve')
            else:
                self.cp(Ev[:, :, 0:Wm], scar_view[:, g, :, :], 'act')
            self.act(Ev[:, :, Wm:Wm + L], pseg)
            av = acc[:, col0:col0 + nb * L].rearrange('p (b l) -> p b l', b=nb)
            self.act(av, pseg, AF.Identity, scale=wtab[:, g, Wm:W], bias=wtab[:, g, W:W + 1])
            if kind == 's':
                self.cp(scar_view[:, g, :, :], Ev[:, :, L:L + Wm], 'act')
            else:
                self.cp(carry_p[:, g:g + 1, :], Ev[:, :, L:L + Wm], 'act')
            for j in range(Wm):
                self.stt(av, Ev[:, :, j:j + L], wtab[:, g, j:j + 1], av, ALU.mult, ALU.add)
            off += nb * (L + Wm)

    def conv_group(self, tl, psv, W, wtab, g, carry_p, scar_view, dst, final=True):
        E = self.cE[self.rot('cE', 2)]
        acc = self.cacc[self.rot('cacc', 3)]
        self._conv_taps(tl, psv, W, wtab, g, carry_p, scar_view, E, acc)
        T = tl.T
        if not final:
            return acc
        th = self.cth[self.rot('cth', 2)]
        self.act(th[:, 0:T], acc[:, 0:T], AF.Tanh)
        self.stt(dst, th[:, 0:T], 1.0, acc[:, 0:T], ALU.add, ALU.mult)
        return acc

    def carry_out(self, src, G, R, dst):
        identf = self.cfv('ident')
        for g0 in range(0, G, 4):
            gn = min(4, G - g0)
            bank = self.ps[self.rot('co', 2)]
            for g in range(gn):
                self.tr(bank[:R, g * 128:(g + 1) * 128], src[:, g0 + g, :], identf)
            stg = self.cacc[self.rot('cacc', 3)]
            self.cp(stg[:R, 0:gn * 128], bank[:R, 0:gn * 128], 'act')
            self.dma(dst[:, g0 * 128:(g0 + gn) * 128], stg[:R, 0:gn * 128])

    def scar_in(self, name, G, R):
        identf = self.cfv('ident')
        rows = 16 * R
        sv = self.scar[:, 0:G * rows].rearrange('p (g b r) -> p g b r', g=G, b=16)
        for g0 in range(0, G, 4):
            gn = min(4, G - g0)
            stg = self.cacc[self.rot('cacc', 3)]
            self.dma(stg[:rows, 0:gn * 128], self.I[name][:, g0 * 128:(g0 + gn) * 128])
            bank = self.ps[self.rot('co', 2)]
            for g in range(gn):
                self.tr(bank[:, g * rows:(g + 1) * rows], stg[:rows, g * 128:(g + 1) * 128], identf[:rows, :rows])
            self.cp(self.scar[:, g0 * rows:(g0 + gn) * rows], bank[:, 0:gn * rows], 'act')
        return sv

    def scar_out(self, name, G, R):
        rows = 16 * R
        src = self.scar[:, 0:G * rows].rearrange('p (g br) -> p g br', g=G)
        self.carry_out(src, G, rows, self.O[name].ap())

    def dense_fm(self, tl, actT, nkt_total, cb_group, kt0=0):
        Wb, (name, k0, nk, parts) = self.wnext()
        ncols = sum(p[1] for p in parts)
        T = tl.T
        for gl in range(ncols // 128):
            bank = self.ps[self.rot('mm', 4)]
            for k in range(nk):
                self.mm(bank[:, 0:T], Wb[:, k, gl * 128:(gl + 1) * 128], actT[:, k0 + k, 0:T],
                        start=(k0 + k == 0), stop=(k0 + k == nkt_total - 1))
            cb_group(gl, bank[:, 0:T])

    def dense_tm(self, tl, actT, cb_chunk):
        Wb, (name, k0, nk, parts) = self.wnext()
        ncols = sum(p[1] for p in parts)
        for ch in tl.chunks:
            bank = self.ps[self.rot('mm', 4)]
            for k in range(nk):
                self.mm(bank[:ch.n, 0:ncols], actT[:, k0 + k, ch.col0:ch.col0 + ch.n], Wb[:, k, 0:ncols],
                        start=(k == 0), stop=(k == nk - 1))
            cb_chunk(ch, bank[:ch.n, 0:ncols])

    def run_tile(self, tl, last):
        S, I, O = self.S, self.I, self.O
        T = tl.T
        isS = any(sg[3] == 's' for sg in tl.segs)
        if isS:
            S.alias_phase([self.xr[2], self.xr[3], self.zs[2], self.zs[3]], self.grpArena + self.grpArena2)
        for ch in tl.chunks:
            src = {'s': I['xs'].ap(), 'm': I['meta'].ap()}.get(ch.kind)
            if src is None:
                src = I['xp'][ch.row0:ch.row0 + ch.n, :]
            self.dma(self.xr[ch.slot][:ch.n, :], src)
            self.ln_rows(self.xr[ch.slot][:ch.n, :], ch.n, 'ln0_g', 'ln0_b')
        self.to_fm(tl, lambda ch: self.xr[ch.slot][:ch.n, :], self.xnT)
        for ch in tl.chunks:
            self.ts(self.xr[ch.slot][:ch.n, :], self.xr[ch.slot][:ch.n, :], ALPHA, ALU.mult)
        self.dump('xnT_' + tl.name, self.xnT[:, :, 0:T], [128, 8, T])
        if self.debug.get('stop') == 'p0':
            return
        S.alias_phase(self.grpA2fix + self.grpA2rec + self.grpB + self.grpA1rec, self.grpA1conv + self.grpA1fix)
        sq = self.scar_in('s_mconv', 8, 3) if isS else None
        for blk in range(4):
            def cbq(gl, psv, blk=blk):
                g = 2 * blk + gl
                self.conv_group(tl, psv, 4, self.cwm, g, self.cq, sq, self.qkT[:, g, 0:T])
            self.dense_fm(tl, self.xnT, 8, cbq)
        if isS:
            self.scar_out('o_mconv', 8, 3)
        if last:
            self.carry_out(self.cq.ap(), 8, 3, O['p_mconv'].ap())
        for blk in range(4):
            self.dense_tm(tl, self.xnT, lambda ch, psv, blk=blk: self.act(self.v[ch.slot][:ch.n, 256 * blk:256 * blk + 256], psv))
        for blk in range(4):
            self.dense_tm(tl, self.xnT, lambda ch, psv, blk=blk: self.act(self.oth[ch.slot][:ch.n, 256 * blk:256 * blk + 256], psv, AF.Tanh, scale=0.5))
        self.dense_tm(tl, self.xnT, lambda ch, psv: self.cp(self.ifdt[:ch.n, ch.slot, :], psv, 'dve'))
        for ch in tl.chunks:
            n, s = ch.n, ch.slot
            gi = self.gat[:n, s, 0:8]
            self.tt(gi, self.ifdt[:n, s, 0:8], self.bif_b[:n, :], ALU.add)
            e1 = self.sm[3]
            self.act(e1[:n, 0:4], self.gat[:n, s, 4:8], AF.Exp, scale=-1.0)
            self.act(e1[:n, 0:4], e1[:n, 0:4], AF.Ln, bias=self.cst[:n, 2:3])
            self.ts(self.gat[:n, s, 4:8], e1[:n, 0:4], -1.0, ALU.mult)
            d1 = self.sm[4]
            self.tt(d1[:n, 0:32], self.ifdt[:n, s, 8:40], self.dtb_b[:n, :], ALU.add)
            self.act(d1[:n, 0:32], d1[:n, 0:32], AF.Exp)
            self.act(self.dta[:n, s, 0:32], d1[:n, 0:32], AF.Ln, bias=self.cst[:n, 2:3])
            self.tt(self.dta[:n, s, 32:64], self.dta[:n, s, 0:32], self.A_b[:n, :], ALU.mult)
        self.dump('qkT_' + tl.name, self.qkT[:, :, 0:T], [128, 8, T])
        self.dump('gat_' + tl.name, self.gat.ap(), [128, 4, 8])
        self.dump('dta_' + tl.name, self.dta.ap(), [128, 4, 64])
        if self.debug.get('stop') == 'a1':
            return
        S.alias_phase(self.grpA1conv, self.grpA1rec)
        for ch in tl.chunks:
            self.mlstm_chunk(tl, ch, last)
        self.dump('hgT_' + tl.name, self.hgT[:, :, 0:T], [128, 8, T])
        if self.debug.get('stop') == 'mlstm':
            return
        S.alias_phase(self.grpA1rec + self.grpA1fix, self.grpA1conv + self.grpA2fix)
        for blk in range(8):
            def cbz(ch, psv, blk=blk):
                n = ch.n
                zc = self.cacc[self.rot('cacc', 3)]
                th = self.cth[self.rot('cth', 2)]
                self.cp(zc[:n, 0:256], psv, 'act')
                self.act(th[:n, 0:256], psv, AF.Tanh, scale=0.5)
                self.stt(self.zs[ch.slot][:n, 256 * blk:256 * blk + 256], th[:n, 0:256], 1.0, zc[:n, 0:256], ALU.add, ALU.mult)
            self.dense_tm(tl, self.xnT, cbz)
        if self.debug.get('stop') == 'a2z':
            return
        sx = self.scar_in('s_sconv', 24, 3) if isS else None
        for blk in range(12):
            def cbx(gl, psv, blk=blk):
                g = 2 * blk + gl
                self.conv_group(tl, psv, 4, self.cws, g, self.cx, sx, self.xbcT[:, g, 0:T])
            self.dense_fm(tl, self.xnT, 8, cbx)
        if isS:
            self.scar_out('o_sconv', 24, 3)
        if last:
            self.carry_out(self.cx.ap(), 24, 3, O['p_sconv'].ap())
        self.dump('xbcT_' + tl.name, self.xbcT[:, :, 0:T], [128, 24, T])
        if self.debug.get('stop') == 'a2':
            return
        S.alias_phase(self.grpA1conv, self.grpA2rec)
        for ch in tl.chunks:
            self.ssd_chunk(tl, ch, last)
        self.dump('ygT_' + tl.name, self.ygT[:, :, 0:T], [128, 16, T])
        if self.debug.get('stop') == 'ssd':
            return
        S.alias_phase(self.grpA2rec + self.grpA2fix, self.grpA1conv + self.grpB)
        for blk in range(8):
            def cbg(gl, psv, blk=blk):
                self.act(self.gth[:, 2 * blk + gl, 0:T], psv, AF.Tanh, scale=0.5)
            self.dense_fm(tl, self.xnT, 8, cbg)
        for j in range(4):
            Wb, (name, k0, nk, parts) = self.wnext()
            banksA = [self.ps[0], self.ps[1]]
            banksB = [self.ps[2], self.ps[3]]
            for gl in range(2):
                for k in range(8):
                    self.mm(banksA[gl][:, 0:T], Wb[:, k, gl * 128:(gl + 1) * 128], self.hgT[:, k, 0:T], start=(k == 0), stop=(k == 7))
            for half in range(2):
                Wb, (name, k0, nk, parts) = self.wnext()
                for gl in range(2):
                    for k in range(8):
                        kk = 8 * half + k
                        self.mm(banksB[gl][:, 0:T], Wb[:, k, gl * 128:(gl + 1) * 128], self.ygT[:, kk, 0:T], start=(kk == 0), stop=(kk == 15))
            for gl in range(2):
                g = 2 * j + gl
                m1 = self.cacc[self.rot('cacc', 3)]
                m2 = self.cacc2[self.rot('cacc2', 2)]
                self.stt(m1[:, 0:T], self.gth[:, g, 0:T], 1.0, banksA[gl][:, 0:T], ALU.add, ALU.mult)
                self.stt(m2[:, 0:T], self.gth[:, 8 + g, 0:T], 1.0, banksB[gl][:, 0:T], ALU.add, ALU.mult)
                self.tt(self.mixT[:, g, 0:T], m1[:, 0:T], m2[:, 0:T], ALU.add)
        self.rr['mm'] = 0
        for blk in range(4):
            def cbo(ch, psv, blk=blk):
                xv = self.xr[ch.slot][:ch.n, 256 * blk:256 * blk + 256]
                self.stt(xv, psv, 0.5, xv, ALU.mult, ALU.add)
            self.dense_tm(tl, self.mixT, cbo)
        for ch in tl.chunks:
            self.ln_rows(self.xr[ch.slot][:ch.n, :], ch.n, 'ln1_g', 'ln1_b')
        self.dump('x1_' + tl.name, self.xr[0].ap(), [128, D])
        self.to_fm(tl, lambda ch: self.xr[ch.slot][:ch.n, :], self.xnT)
        for ch in tl.chunks:
            self.ts(self.xr[ch.slot][:ch.n, :], self.xr[ch.slot][:ch.n, :], ALPHA, ALU.mult)
        sf = self.scar_in('s_fconv', 44, 2) if isS else None
        for j in range(11):
            accs = {}
            def cbua(gl, psv, j=j):
                g = 2 * j + gl
                accs[gl] = self.conv_group(tl, psv, 3, self.cwf, g, self.cff, sf, None, final=False)
            self.dense_fm(tl, self.xnT, 8, cbua)
            def cbub(gl, psv, j=j):
                g = 2 * j + gl
                E = self.cE[self.rot('cE', 2)]
                accb = self.cacc2[self.rot('cacc2', 2)]
                self._conv_taps(tl, psv, 3, self.cwf, 22 + g, self.cff, sf, E, accb)
                th = self.cth[self.rot('cth', 2)]
                acca = accs[gl]
                self.act(th[:, 0:T], acca[:, 0:T], AF.Tanh)
                self.stt(th[:, 0:T], th[:, 0:T], 1.0, acca[:, 0:T], ALU.add, ALU.mult)
                self.tt(self.hffT[:, g, 0:T], th[:, 0:T], accb[:, 0:T], ALU.mult)
            self.dense_fm(tl, self.xnT, 8, cbub)
        if isS:
            self.scar_out('o_fconv', 44, 2)
        if last:
            self.carry_out(self.cff.ap(), 44, 2, O['p_fconv'].ap())
        self.dump('hffT_' + tl.name, self.hffT[:, :, 0:T], [128, 22, T])
        for blk in range(4):
            banks = {ch.slot: self.ps[ch.slot] for ch in tl.chunks}
            for (k0, nk) in ((0, 8), (8, 8), (16, 6)):
                Wb, meta = self.wnext()
                for ch in tl.chunks:
                    for k in range(nk):
                        self.mm(banks[ch.slot][:ch.n, 0:256], self.hffT[:, k0 + k, ch.col0:ch.col0 + ch.n], Wb[:, k, 0:256],
                                start=(k0 + k == 0), stop=(k0 + k == 21))
            for ch in tl.chunks:
                xv = self.xr[ch.slot][:ch.n, 256 * blk:256 * blk + 256]
                self.tt(xv, banks[ch.slot][:ch.n, 0:256], xv, ALU.add)
        for ch in tl.chunks:
            self.ln_rows(self.xr[ch.slot][:ch.n, :], ch.n, 'ln2_g', 'ln2_b')
            if ch.kind == 'p':
                self.dma(O['y_p'][ch.row0:ch.row0 + ch.n, :], self.xr[ch.slot][:ch.n, :])
            elif ch.kind == 's':
                self.dma(O['y_s'].ap(), self.xr[ch.slot][:ch.n, :])

    def mlstm_chunk(self, tl, ch, last):
        I, O = self.I, self.O
        n, s, c0, kind = ch.n, ch.slot, ch.col0, ch.kind
        kc = self.kc(kind, n)
        U, LS, M, MT = kc['U'], kc['LS'], kc['M'], kc['MT']
        identf, onesf = self.cfv('ident', n, n), self.cfv('ones', n, n)
        identb, onesb = self.cbv('identb', n, n), self.cbv('onesb', n, n)
        ps = self.ps
        cols = slice(c0, c0 + n)
        ig = self.gat[:n, s, 0:4]
        lf = self.gat[:n, s, 4:8]
        sm = self.sm
        bt, mi, bm, mt, wi, emt, negm, den, rden, wi2 = (sm[i] for i in range(5, 15))
        R1, R2, R3, wT, ST, kTM, hh, vw = self.R1, self.R2, self.R3, self.wT, self.ST, self.kTM, self.hh, self.vw
        self.mm(ps[0][:n, 0:4], U, lf)
        for h in range(4):
            self.ts(R1[:n, h, :n], LS, self.gat[:n, s, 4 + h:5 + h], ALU.mult)
            self.ts(R2[:n, h, :n], identf, self.gat[:n, s, h:h + 1], ALU.mult, eng='pool')
        for h in range(4):
            self.mm(ps[1][:n, h * 128:h * 128 + n], U, R1[:n, h, :n], start=True, stop=False)
            self.mm(ps[1][:n, h * 128:h * 128 + n], onesf, R2[:n, h, :n], start=False, stop=False)
            self.mm(ps[1][:n, h * 128:h * 128 + n], identb, M, start=False, stop=True)
        self.rmax(mi[:n, 0:4], ps[1][:n, :].rearrange('p (h t) -> p h t', h=4)[:, :, 0:n])
        if kind == 's':
            m0s = sm[15]
            self.dma(m0s[:16, 0:4], I['s_m'].ap())
            self.mm(ps[0][:n, 4:8], self.cfv('BMT', 16), m0s[:16, 0:4])
            m0v = ps[0][:n, 4:8]
        else:
            m0v = self.m_b[:n, :]
        self.cp(bt[:n, 0:4], ps[0][:n, 0:4], 'act')
        self.tt(bm[:n, 0:4], bt[:n, 0:4], m0v, ALU.add)
        self.tt(mt[:n, 0:4], bm[:n, 0:4], mi[:n, 0:4], ALU.max)
        self.tt(bm[:n, 0:4], bm[:n, 0:4], mt[:n, 0:4], ALU.subtract)
        self.act(wi[:n, 0:4], bm[:n, 0:4], AF.Exp)
        self.act(emt[:n, 0:4], mt[:n, 0:4], AF.Exp, scale=-1.0, bias=self.cst[:n, 1:2])
        self.ts(negm[:n, 0:4], mt[:n, 0:4], -1.0, ALU.mult)
        for h in range(4):
            self.ts(R3[:n, h, :n], identf, negm[:n, h:h + 1], ALU.mult, eng='pool')
        for h in range(4):
            o = ps[2][:n, h * 128:h * 128 + n]
            self.mm(o, R1[:n, h, :n], U, start=True, stop=False)
            self.mm(o, R2[:n, h, :n], onesf, start=False, stop=False)
            self.mm(o, onesf, R3[:n, h, :n], start=False, stop=False)
            self.mm(o, identb, MT, start=False, stop=True)
        ps2v = ps[2][:n, :].rearrange('p (h t) -> p h t', h=4)[:, :, 0:n]
        self.act(wT[:n, :, :n], ps2v, AF.Exp)
        for h in range(4):
            self.mm(ps[1][:n, h * 128:h * 128 + n], self.qkT[:, 4 + h, cols], self.qkT[:, h, cols])
        ps1v = ps[1][:n, :].rearrange('p (h t) -> p h t', h=4)[:, :, 0:n]
        self.tt(ST[:n, :, :n], ps1v, wT[:n, :, :n], ALU.mult)
        pb7 = self.psb(7)
        for h in range(4):
            self.tr(pb7[:n, h * 128:(h + 1) * 128], self.qkT[:, 4 + h, cols], self.cbv('identb'))
        self.cp(kTM[:n, :, :], pb7[:n, 0:512].rearrange('p (h d) -> p h d', h=4), 'act')
        if kind == 's':
            self.mlstm_sample_states(tl, ch, wi, wT, kTM, mt)
        for h in range(4):
            self.mm(ps[3 + h // 2][:n, (h % 2) * 256:(h % 2) * 256 + 256], ST[:n, h, :n], self.v[s][:n, h * 256:(h + 1) * 256])
        for h in range(4):
            self.mm(ps[0][:n, 8 + h:9 + h], ST[:n, h, :n], onesb[:, 0:1])
        if kind != 's':
            for h in range(4):
                self.mm(ps[5 + h // 2][:n, (h % 2) * 256:(h % 2) * 256 + 256], self.qkT[:, h, cols], self.Cb[:, h, :])
            for h in range(4):
                self.mm(ps[0][:n, 12 + h:13 + h], self.qkT[:, h, cols], self.nb[:, h:h + 1])
        dint = ps[0][:n, 12:16] if kind != 's' else sm[12][:n, 0:4]
        self.tt(den[:n, 0:4], dint, wi[:n, 0:4], ALU.mult)
        self.tt(den[:n, 0:4], den[:n, 0:4], ps[0][:n, 8:12], ALU.add)
        self.ts(wi2[:n, 0:4], den[:n, 0:4], -1.0, ALU.mult)
        self.tt(den[:n, 0:4], den[:n, 0:4], wi2[:n, 0:4], ALU.max)
        self.tt(den[:n, 0:4], den[:n, 0:4], emt[:n, 0:4], ALU.max)
        self.S.op('dve', lambda e: e.reciprocal(out=rden.t[:n, 0:4], in_=den.t[:n, 0:4]), reads=[den], writes=[rden])
        self.tt(wi2[:n, 0:4], wi[:n, 0:4], rden[:n, 0:4], ALU.mult)
        for h in range(4):
            self.act(hh[:n, h, :], ps[3 + h // 2][:n, (h % 2) * 256:(h % 2) * 256 + 256], AF.Copy, scale=rden[:n, h:h + 1])
            if kind != 's':
                iv = ps[5 + h // 2][:n, (h % 2) * 256:(h % 2) * 256 + 256]
            else:
                iv = (R1 if h < 2 else R2)[:n, :, :].rearrange('p a b -> p (a b)')[:, (h % 2) * 256:(h % 2) * 256 + 256]
            self.stt(hh[:n, h, :], iv, wi2[:n, h:h + 1], hh[:n, h, :], ALU.mult, ALU.add)
        st, mv, rs = sm[0], sm[1], sm[2]
        for h in range(4):
            self.S.op('dve', lambda e, h=h: e.bn_stats(out=st.t[:n, h * 6:(h + 1) * 6], in_=hh.t[:n, h, :]), reads=[hh], writes=[st])
        for h in range(4):
            self.S.op('dve', lambda e, h=h: e.bn_aggr(out=mv.t[:n, 2 * h:2 * h + 2], in_=st.t[:n, h * 6:(h + 1) * 6]), reads=[st], writes=[mv])
        mvv = mv[:n, 0:8].rearrange('p (h t) -> p h t', t=2)
        self.act(rs[:n, 0:4], mvv[:, :, 1], AF.Ln, bias=self.cst[:n, 0:1])
        self.act(rs[:n, 0:4], rs[:n, 0:4], AF.Exp, scale=-0.5)
        for h in range(4):
            self.ts(hh[:n, h, :], hh[:n, h, :], mv[:n, 2 * h:2 * h + 1], ALU.subtract, rs[:n, h:h + 1], ALU.mult)
            self.stt(self.hgTM[:n, h * 256:(h + 1) * 256], self.oth[s][:n, h * 256:(h + 1) * 256], 1.0, hh[:n, h, :], ALU.add, ALU.mult)
        for k in range(8):
            self.tr(pb7[:, k * n:(k + 1) * n], self.hgTM[:n, k * 128:(k + 1) * 128], identb)
        self.tt(self.hgT[:, :, cols].rearrange('p k t -> p t k'), pb7[:, 0:8 * n].rearrange('p (k t) -> p t k', k=8),
                View(self.cwm, self.cwm.t[:, :, 5].unsqueeze(1).to_broadcast([128, n, 8])), ALU.mult)
        if kind != 's':
            wl16 = sm[15]
            self.cp(wl16.ap().bitcast(BF16)[:n, 0:4], wT[:n, :, n - 1], 'act')
            for h in range(4):
                self.ts(vw[:n, h, :], self.v[s][:n, h * 256:(h + 1) * 256], wT[:n, h, n - 1:n], ALU.mult)
            for h in range(4):
                self.mm(ps[5 + h // 2][:, (h % 2) * 256:(h % 2) * 256 + 256], kTM[:n, h, :], vw[:n, h, :])
            for h in range(4):
                self.mm(ps[0][:, 16 + h:17 + h], kTM[:n, h, :], wl16.ap().bitcast(BF16)[:n, h:h + 1])
            SEL = self.cfv('SELp' if n == 128 else 'SELm', n)
            self.mm(ps[0][:, 32:36], SEL, wi[:n, 0:4])
            self.mm(ps[0][:, 36:40], SEL, mt[:n, 0:4])
            dec = sm[3]
            self.cp(dec[:, 0:8], ps[0][:, 32:40], 'act')
            for h in range(4):
                self.stt(self.Cf[:, h, :], self.Cf[:, h, :], dec[:, h:h + 1], ps[5 + h // 2][:, (h % 2) * 256:(h % 2) * 256 + 256], ALU.mult, ALU.add)
            self.tt(self.nf.ap(), self.nf.ap(), dec[:, 0:4], ALU.mult)
            self.tt(self.nf.ap(), self.nf.ap(), ps[0][:, 16:20], ALU.add)
            self.cp(self.m_b.ap(), dec[:, 4:8], 'dve')
            self.cp(self.Cb.ap(), self.Cf.ap(), 'act')
            self.cp(self.nb.ap(), self.nf.ap(), 'pool')
            if ch.final:
                self.dma(O['p_C'].ap().rearrange('h d e -> d h e'), self.Cf.ap())
                identf128 = self.cfv('ident')
                self.tr(ps[0][:4, 128:256], self.nf.ap(), identf128)
                self.cp(self.pn_st[:4, :], ps[0][:4, 128:256], 'act')
                self.dma(O['p_n'].ap(), self.pn_st[:4, :])
                self.dma(O['p_m'].ap(), self.m_b[0:1, :])

    def mlstm_sample_states(self, tl, ch, wi, wT, kTM, mt):
        I, O, ps, sm = self.I, self.O, self.ps, self.sm
        n, s = 64, ch.slot
        identf = self.cfv('ident')
        RS, BM = self.cfv('RS', 64), self.cfv('BM', 64)
        CM = self.cbv('CM').rearrange('p (b j) -> p b j', b=16)
        vw = self.vw
        wl = sm[3]
        tmp = self.R3
        self.tt(tmp[:n, :, 0:64], wT[:n, :, 0:64], View(self.cf, self.cfv('LSEL', 64).ap.unsqueeze(1).to_broadcast([64, 4, 64])), ALU.mult)
        self.S.op('dve', lambda e: e.tensor_reduce(out=wl.t[:n, 0:4], in_=tmp.t[:n, :, 0:64], axis=AX.X, op=ALU.add), reads=[tmp], writes=[wl])
        wl16 = sm[4].ap().bitcast(BF16)
        self.cp(wl16[:n, 0:4], wl[:n, 0:4], 'act')
        for h in range(4):
            self.ts(vw[:n, h, :], self.v[s][:n, h * 256:(h + 1) * 256], wl[:n, h:h + 1], ALU.mult)
        Rm3 = self.Rm[:n, :].rearrange('p (b h) -> p b h', h=4)
        self.tt(Rm3, View(wi, wi.t[:n, 0:4].unsqueeze(1).to_broadcast([n, 16, 4])),
                View(self.cf, RS.ap.unsqueeze(2).to_broadcast([n, 16, 4])), ALU.mult)
        self.mm(ps[0][:, 64:128], self.cfv('ones', 64), self.Rm[:n, :])
        self.cp(self.decS.ap(), ps[0][:, 64:128], 'act')
        self.mm(ps[0][:16, 40:44], RS, mt[:n, 0:4])
        mo = sm[15]
        self.cp(mo[:16, 8:12], ps[0][:16, 40:44], 'act')
        self.dma(O['o_m'].ap(), mo[:16, 8:12])
        stg = self.pn_st
        self.dma(stg[:64, :], I['s_n'].ap())
        self.tr(ps[0][:, 192:256], stg[:64, :], identf[:64, :64])
        self.cp(self.n0T.ap(), ps[0][:, 192:256], 'act')
        self.cp(self.n16.ap(), self.n0T.ap(), 'pool')
        kTMflat = kTM[:n, :, :].rearrange('p h d -> p (h d)')
        for b in range(16):
            i = b % 2
            C0, C16, qm, km = self.C0b[i], self.C0b16[i], self.qmb[i], self.kTMm[i]
            self.dma(C0.ap(), I['s_C'][b].rearrange('h d e -> d h e'))
            self.cp(C16[:, :, 0:256], C0.ap(), 'act')
            self.cp(C16[:, :, 256:257], self.n16[:, 4 * b:4 * b + 4].rearrange('p (h o) -> p h o', o=1), 'dve')
            self.tt(qm.ap(), self.qkT[:, 0:4, ch.col0:ch.col0 + 64], View(self.cb, CM.ap[:, b, :].unsqueeze(1).to_broadcast([128, 4, 64])), ALU.mult)
            self.ts(km[:n, :, :].rearrange('p h d -> p (h d)'), kTMflat, BM[:, b:b + 1], ALU.mult)
            for h in range(4):
                self.mm(ps[3 + h][:n, 0:257], qm[:, h, :], C16[:, h, 0:257], start=(b == 0), stop=(b == 15))
            for h in range(4):
                self.mm(ps[1 + h // 2][:, (h % 2) * 256:(h % 2) * 256 + 256], km[:n, h, :], vw[:n, h, :])
            for h in range(4):
                self.mm(ps[0][:, 128 + 4 * b + h:129 + 4 * b + h], km[:n, h, :], wl16[:n, h:h + 1])
            for h in range(4):
                self.stt(C0[:, h, :], C0[:, h, :], self.decS[:, 4 * b + h:4 * b + h + 1],
                         ps[1 + h // 2][:, (h % 2) * 256:(h % 2) * 256 + 256], ALU.mult, ALU.add)
            self.dma(O['o_C'][b].rearrange('h d e -> d h e'), C0.ap())
        for h in range(4):
            dst = (self.R1 if h < 2 else self.R2)[:n, :, :].rearrange('p a b -> p (a b)')[:, (h % 2) * 256:(h % 2) * 256 + 256]
            self.cp(dst, ps[3 + h][:n, 0:256], 'act')
            self.cp(sm[12][:n, h:h + 1], ps[3 + h][:n, 256:257], 'act')
        self.tt(self.n0T.ap(), self.n0T.ap(), self.decS.ap(), ALU.mult)
        self.tt(self.n0T.ap(), self.n0T.ap(), ps[0][:, 128:192], ALU.add)
        self.tr(ps[0][:64, 256:384], self.n0T.ap(), identf)
        self.cp(stg[:64, :], ps[0][:64, 256:384], 'act')
        self.dma(O['o_n'].ap(), stg[:64, :])

    def ssd_chunk(self, tl, ch, last):
        I, O, ps, sm = self.I, self.O, self.ps, self.sm
        n, s, c0, kind = ch.n, ch.slot, ch.col0, ch.kind
        kc = self.kc(kind, n)
        U, LS, BO, MT = kc['U'], kc['LS'], kc['BO'], kc['MT']
        identb = self.cbv('identb')
        cols = slice(c0, c0 + n)
        dt = self.dta[:n, s, 0:32]
        a = self.dta[:n, s, 32:64]
        btsb, dect, wend, decS = sm[5], sm[6], sm[7], sm[8]
        self.mm(ps[0][:n, 0:32], U, a)
        self.mm(ps[0][:n, 32:64], BO, a)
        self.cp(btsb[:n, 0:32], ps[0][:n, 0:32], 'act')
        self.act(dect[:n, 0:32], ps[0][:n, 0:32], AF.Exp)
        self.tt(wend[:n, 0:32], ps[0][:n, 32:64], btsb[:n, 0:32], ALU.subtract)
        self.act(wend[:n, 0:32], wend[:n, 0:32], AF.Exp)
        if kind != 's':
            self.mm(ps[0][:, 64:96], self.cfv('ones', n), a)
            self.act(decS[:, 0:32], ps[0][:, 64:96], AF.Exp)
        S = self.S
        pb6, pb7 = self.psb(6), self.psb(7)
        xsTM = self.xdt
        for g in range(16):
            pv = pb6 if g < 8 else pb7
            self.tr(pv[:n, (g % 8) * 128:(g % 8 + 1) * 128], self.xbcT[:, g, cols], identb)
        self.act(xsTM[:n, 0:1024], pb6[:n, 0:1024])
        self.act(xsTM[:n, 1024:2048], pb7[:n, 0:1024])
        S.alias_phase([self.ynTM], [self.xsDT])
        for half in range(2):
            for g in range(8):
                self.ts(self.xsDT[:, g, 0:n], self.xbcT[:, 8 * half + g, cols], self.Dfm[:, 8 * half + g:8 * half + g + 1], ALU.mult)
            pv = pb6 if half == 0 else pb7
            for g in range(8):
                self.tr(pv[:n, g * 128:(g + 1) * 128], self.xsDT[:, g, 0:n], identb)
            self.cp(self.xsD[:n, 1024 * half:1024 * half + 1024], pv[:n, 0:1024], 'dve')
        for g in range(4):
            self.tr(pb6[:n, g * 128:(g + 1) * 128], self.xbcT[:, 16 + g, cols], identb)
        self.act(self.BTM[:n, :], pb6[:n, 0:512])
        dtw = sm[13]
        self.tt(dtw[:n, 0:32], dt, wend[:n, 0:32], ALU.mult)
        S.alias_phase([self.xsDT], [self.ynTM])
        if kind == 's':
            self.ssd_sample_states(tl, ch, dtw)
        S.alias_phase([self.ynTM], [self.MTt2])
        ssq = sm[9]
        self.memset(ssq[:n, 0:4], 0.0)
        MTs = [self.MTt, self.MTt2]

        def stageA(g):
            MTb = MTs[g % 2]
            for j in range(8):
                self.ts(self.LA[:n, j, :n], LS, self.dta[:n, s, 32 + 8 * g + j:33 + 8 * g + j], ALU.mult)
            for j in range(8):
                o = ps[1 + j // 4][:n, (j % 4) * 128:(j % 4) * 128 + n]
                self.mm(o, self.LA[:n, j, :n], U, start=True, stop=False)
                self.mm(o, identb[:n, :n], MT, start=False, stop=True)
            for half in range(2):
                self.act(self.LT[:n, 4 * half:4 * half + 4, :n],
                         ps[1 + half][:n, :].rearrange('p (h t) -> p h t', h=4)[:, :, 0:n], AF.Exp)
            self.mm(ps[3][:n, 0:n], self.xbcT[:, 16 + g, cols], self.xbcT[:, 20 + g, cols])
            for j in range(8):
                self.stt(MTb[:n, j, :n], self.LT[:n, j, :n], self.dta[:n, s, 8 * g + j:8 * g + j + 1], ps[3][:n, 0:n], ALU.mult, ALU.mult)

        def stageB(g):
            MTb = MTs[g % 2]
            for j in range(8):
                h = 8 * g + j
                self.mm(ps[4][:n, j * 64:(j + 1) * 64], MTb[:n, j, :n], xsTM[:n, h * 64:(h + 1) * 64])
            t1 = self.t1
            if kind != 's':
                self.mm(ps[5][:n, 0:512], self.xbcT[:, 20 + g, cols], self.STb[:, 512 * g:512 * g + 512])
                for j in range(4):
                    self.act(t1[:n, j * 64:(j + 1) * 64], ps[5][:n, j * 64:(j + 1) * 64], AF.Copy, scale=dect[:n, 8 * g + j:8 * g + j + 1])
                self.tt(t1[:n, 256:512].rearrange('p (h d) -> p d h', d=64), ps[5][:n, 256:512].rearrange('p (h d) -> p d h', d=64),
                        View(dect, dect.t[:n, 8 * g + 4:8 * g + 8].unsqueeze(1).to_broadcast([n, 64, 4])), ALU.mult)
            else:
                self.cp(t1[:n, :], self.ysi[:n, 512 * g:512 * g + 512], 'dve')
            self.tt(t1[:n, :], t1[:n, :], ps[4][:n, 0:512], ALU.add)
            self.tt(t1[:n, :], t1[:n, :], self.xsD[:n, 512 * g:512 * g + 512], ALU.add)
            self.tt(self.yz[:n, 512 * g:512 * g + 512], t1[:n, :], self.zs[s][:n, 512 * g:512 * g + 512], ALU.mult)
            self.act(t1[:n, :], self.yz[:n, 512 * g:512 * g + 512], AF.Square, accum=ssq[:n, g:g + 1])

        stageA(0)
        for g in range(4):
            if g + 1 < 4:
                stageA(g + 1)
            stageB(g)
        S.alias_phase([self.MTt2], [self.ynTM])
        rs = sm[10]
        self.act(rs[:n, 0:4], ssq[:n, 0:4], AF.Ln, scale=0.25 / 512.0, bias=self.cst[:n, 0:1])
        self.act(rs[:n, 0:4], rs[:n, 0:4], AF.Exp, scale=-0.5)
        self.ts(rs[:n, 0:4], rs[:n, 0:4], 0.5, ALU.mult)
        for g in range(4):
            self.ts(self.ynTM[:n, 512 * g:512 * g + 512], self.yz[:n, 512 * g:512 * g + 512], rs[:n, g:g + 1], ALU.mult)
        for k in range(16):
            pv = pb6 if k < 8 else pb7
            self.tr(pv[:, (k % 8) * n:(k % 8 + 1) * n], self.ynTM[:n, k * 128:(k + 1) * 128], identb[:n, :n])
        for half in range(2):
            pv = (pb6 if half == 0 else pb7)[:, 0:8 * n].rearrange('p (k t) -> p k t', k=8)
            self.tt(self.ygT[:, 8 * half:8 * half + 8, cols].rearrange('p k t -> p t k'), pv.rearrange('p k t -> p t k'),
                    View(self.sng, self.sng.t[:, 8 * half:8 * half + 8].unsqueeze(1).to_broadcast([128, n, 8])), ALU.mult)
        if kind != 's':
            for g in range(4):
                hs = slice(8 * g, 8 * g + 8)
                wxv = self.wx[:n, :].rearrange('p (h d) -> p d h', d=64)
                self.tt(wxv, self.xdt[:n, 512 * g:512 * g + 512].rearrange('p (h d) -> p d h', d=64),
                        View(dtw, dtw.t[:n, hs].unsqueeze(1).to_broadcast([n, 64, 8])), ALU.mult)
                bank = ps[3 + 2 * (g % 2)]
                self.mm(bank[:, 0:512], self.BTM[:n, g * 128:(g + 1) * 128], self.wx[:n, :])
                for j in range(8):
                    c0_ = 512 * g + 64 * j
                    self.act(self.STf[:, c0_:c0_ + 64], self.STf[:, c0_:c0_ + 64], AF.Copy, scale=decS[:, 8 * g + j:8 * g + j + 1])
                self.tt(self.STf[:, 512 * g:512 * g + 512], self.STf[:, 512 * g:512 * g + 512], bank[:, 0:512], ALU.add)
            self.cp(self.STb.ap(), self.STf.ap(), 'act')
            if ch.final:
                identf = self.cfv('ident')
                for j in range(16):
                    bank = ps[1 + (j // 4) % 2]
                    self.tr(bank[:, (j % 4) * 128:(j % 4 + 1) * 128], self.STf[:, j * 128:(j + 1) * 128], identf)
                    if j % 4 == 3:
                        stg = self.LA[:, 4 * ((j // 4) % 2):4 * ((j // 4) % 2) + 4, :]
                        self.cp(stg, bank[:, 0:512].rearrange('p (j n) -> p j n', j=4), 'act')
                        q = j // 4
                        self.dma(O['p_ssm'][512 * q:512 * q + 512, :].rearrange('(j p) n -> p j n', p=128), stg)

    def ssd_sample_states(self, tl, ch, wend):
        I, O, ps, sm, S = self.I, self.O, self.ps, self.sm, self.S
        n, s = 64, ch.slot
        identf = self.cfv('ident')
        RS, BM = self.cfv('RS', 64), self.cfv('BM', 64)
        CM = self.cbv('CM').rearrange('p (b j) -> p b j', b=16)
        dect = sm[6]
        S.alias_phase(self.grpArena, self.grpArena2)
        S.alias_phase([self.LT, self.MTt, self.t1, self.yz], [self.S0b[1]])
        blsb = sm[11]
        self.cp(blsb[:n, 0:32], ps[0][:n, 32:64], 'act')
        bl3 = blsb[:n, 0:32].rearrange('p (j r) -> p j r', r=2)
        for r in range(2):
            self.tt(self.Rr[:n, r, :].rearrange('p (b j) -> p b j', b=16),
                    View(blsb, bl3.ap[:, :, r].unsqueeze(1).to_broadcast([n, 16, 16])),
                    View(self.cf, RS.ap.unsqueeze(2).to_broadcast([n, 16, 16])), ALU.mult, eng='pool')
        self.mm(ps[0][:, 256:512], self.cfv('H0', 64), self.Rr[:n, 0, :], start=True, stop=False)
        self.mm(ps[0][:, 256:512], self.cfv('H1', 64), self.Rr[:n, 1, :], start=False, stop=True)
        self.act(self.decP.ap(), ps[0][:, 256:512], AF.Exp)
        wxA = self.ynTM
        self.tt(wxA[:n, :].rearrange('p (h d) -> p d h', d=64), self.xdt[:n, :].rearrange('p (h d) -> p d h', d=64),
                View(wend, wend.t[:n, 0:32].unsqueeze(1).to_broadcast([n, 64, 32])), ALU.mult)
        for b in range(16):
            Sb = self.S0b[b % 2]
            cm = self.CTmb[b % 2]
            self.dma(Sb.ap(), I['s_ssm'][b].rearrange('(j p) n -> p j n', p=128))
            self.tt(cm.ap(), self.xbcT[:, 20:24, ch.col0:ch.col0 + 64], View(self.cb, CM.ap[:, b, :].unsqueeze(1).to_broadcast([128, 4, 64])), ALU.mult)
            self.ts(self.wxm[:n, :], wxA[:n, :], BM[:, b:b + 1], ALU.mult)
            for q in range(4):
                bank = ps[5 + q % 2]
                for i in range(4):
                    self.tr(bank[:, i * 128:(i + 1) * 128], Sb[:, 4 * q + i, :], identf)
                self.cp(self.SbT[:, 512 * q:512 * q + 512], bank[:, 0:512], 'act')
            for g in range(4):
                self.mm(ps[1 + g][:n, 0:512], cm[:, g, :], self.SbT[:, 512 * g:512 * g + 512], start=(b == 0), stop=(b == 15))
            for q in range(4):
                bank = ps[7] if q % 2 == 0 else ps[0]
                for i in range(4):
                    j = 4 * q + i
                    self.mm(bank[:, i * 128:(i + 1) * 128], self.wxm[:n, j * 128:(j + 1) * 128], self.BTM[:n, q * 128:(q + 1) * 128])
                for i in range(4):
                    j = 4 * q + i
                    self.stt(Sb[:, j, :], Sb[:, j, :], self.decP[:, 16 * b + j:16 * b + j + 1], bank[:, i * 128:(i + 1) * 128], ALU.mult, ALU.add)
            self.dma(O['o_ssm'][b].rearrange('(j p) n -> p j n', p=128), Sb.ap())
        for g in range(4):
            self.tt(self.ysi[:n, 512 * g:512 * g + 512].rearrange('p (h d) -> p d h', d=64),
                    ps[1 + g][:n, 0:512].rearrange('p (h d) -> p d h', d=64),
                    View(dect, dect.t[:n, 8 * g:8 * g + 8].unsqueeze(1).to_broadcast([n, 64, 8])), ALU.mult)
        S.alias_phase([self.S0b[1]], [self.LT, self.MTt, self.t1, self.yz])


_CACHE = {}


def _get_kernel():
    if 'k' not in _CACHE:
        _CACHE['k'] = K()
    return _CACHE['k']


def make_in_maps(kb, inputs):
    f = lambda a: np.ascontiguousarray(np.asarray(a, dtype=np.float32))
    xp, xs = f(inputs['x_prompt']), f(inputs['x_sample'])
    shared = {'meta': f(inputs['meta_tokens']), 'cf': kb.cf_np, 'cb': kb.cb_np,
              'ln0_g': f(inputs['ln0_g']), 'ln0_b': f(inputs['ln0_b']),
              'b_if': f(inputs['b_mlstm_if'])[0], 'w_mconv': f(inputs['w_mlstm_conv'])[0],
              'b_mconv': f(inputs['b_mlstm_conv']), 'mnorm_g': f(inputs['mlstm_norm_g']),
              'w_sconv': f(inputs['w_ssm_conv'])[0], 'b_sconv': f(inputs['b_ssm_conv']),
              'dt_bias': f(inputs['ssm_dt_bias'])[0], 'A_log': f(inputs['ssm_A_log'])[0],
              'ssm_D': f(inputs['ssm_D'])[0], 'snorm_g': f(inputs['ssm_norm_g']),
              'ln1_g': f(inputs['ln1_g'])[0], 'ln1_b': f(inputs['ln1_b'])[0],
              'w_fconv': f(inputs['w_ffn_conv'])[0], 'b_fconv': f(inputs['b_ffn_conv']),
              'ln2_g': f(inputs['ln2_g'])[0], 'ln2_b': f(inputs['ln2_b'])[0],
              'w_in': f(inputs['w_in'])[0], 'w_proj_a': f(inputs['w_proj_a'])[0],
              'w_proj_b': f(inputs['w_proj_b'])[0], 'w_out': f(inputs['w_out'])[0],
              'w_up': f(inputs['w_up'])[0], 'w_down': f(inputs['w_down'])[0]}
    maps = []
    for c in range(8):
        b = slice(16 * c, 16 * c + 16)
        m = dict(shared)
        m['xp'] = xp[c]
        m['xs'] = xs[b].reshape(64, D)
        m['s_mconv'] = f(inputs['state_mlstm_conv'])[0, b].reshape(48, 1024)
        m['s_C'] = f(inputs['state_mlstm_C'])[0, b]
        m['s_n'] = f(inputs['state_mlstm_n'])[0, b].reshape(64, 128)
        m['s_m'] = f(inputs['state_mlstm_m'])[0, b]
        m['s_sconv'] = f(inputs['state_ssm_conv'])[0, b].reshape(48, 3072)
        m['s_ssm'] = f(inputs['state_ssm'])[0, b].reshape(16, 2048, 128)
        m['s_fconv'] = f(inputs['state_ffn_conv'])[0, b].reshape(32, 2 * DFF)
        maps.append(m)
    return maps


def kernel(**inputs):
    kb = _get_kernel()
    maps = make_in_maps(kb, inputs)
    res = run_bass_kernel_spmd(kb.nc, maps, core_ids=list(range(8)))
    R = res.results
    cat = lambda k: np.stack([np.asarray(r[k], dtype=np.float32) for r in R])
    y_p = cat('y_p')
    y_s = cat('y_s').reshape(128, 4, D)
    p_mconv = cat('p_mconv')[None]
    p_C = cat('p_C')[None]
    p_n = cat('p_n')[None]
    p_m = cat('p_m').reshape(8, 4)[None]
    p_sconv = cat('p_sconv')[None]
    p_ssm = cat('p_ssm').reshape(8, 32, 64, 128)[None]
    p_fconv = cat('p_fconv')[None]
    s_mconv = cat('o_mconv').reshape(128, 3, 1024)[None]
    s_C = cat('o_C').reshape(128, 4, 128, 256)[None]
    s_n = cat('o_n').reshape(128, 4, 128)[None]
    s_m = cat('o_m').reshape(128, 4)[None]
    s_sconv = cat('o_sconv').reshape(128, 3, 3072)[None]
    s_ssm = cat('o_ssm').reshape(128, 32, 64, 128)[None]
    s_fconv = cat('o_fconv').reshape(128, 2, 2 * DFF)[None]
    return (y_p, y_s, p_mconv, p_C, p_n, p_m, p_sconv, p_ssm, p_fconv,
            s_mconv, s_C, s_n, s_m, s_sconv, s_ssm, s_fconv)
```

```python
import numpy as np
import ml_dtypes
import concourse.bass as bass
import concourse.mybir as mybir
from concourse.bass_utils import run_bass_kernel_spmd

F32 = mybir.dt.float32
BF16 = mybir.dt.bfloat16
ALU = mybir.AluOpType
AF = mybir.ActivationFunctionType
AX = mybir.AxisListType

D = 1024
DIN = 10280
DFF = 2816
NEG = -30000.0
ALPHA = 2.0 ** 0.25
LN_EPS = 1e-5
RMS_EPS = 1e-5
QSCALE = 128.0 ** -0.5


class Buf:
    def __init__(self, name, t, space):
        self.name = name
        self.t = t
        self.space = space
        self.last_w = None
        self.readers = []
        self.sem_in = None
        self.cnt_in = 0
        self.sem_out = None
        self.cnt_out = 0

    def __getitem__(self, idx):
        return View(self, self.t[idx])

    def ap(self):
        return View(self, self.t[:] if self.space != 'dram' else self.t)


class View:
    def __init__(self, buf, ap):
        self.buf = buf
        self.ap = ap

    def __getitem__(self, idx):
        return View(self.buf, self.ap[idx])

    def rearrange(self, *a, **k):
        return View(self.buf, self.ap.rearrange(*a, **k))

    def bc(self, axis, shape):
        return View(self.buf, self.ap.unsqueeze(axis).to_broadcast(list(shape)))

    def bitcast(self, dt):
        return View(self.buf, self.ap.bitcast(dt))


def _bufs(vs):
    out = []
    for v in vs:
        if v is None or isinstance(v, (int, float)):
            continue
        b = v.buf if isinstance(v, View) else v
        if b not in out:
            out.append(b)
    return out


class Sched:
    ENGS = ('pe', 'act', 'dve', 'pool', 'sp')

    def __init__(self, nc):
        self.nc = nc
        self.sem = {e: nc.alloc_semaphore('sem_' + e) for e in self.ENGS}
        self.cnt = {e: 0 for e in self.ENGS}
        self.ops = {e: [] for e in self.ENGS}
        self.seen = {e: {} for e in self.ENGS}
        self.final_tokens = []
        self.sb_off = 16512
        self.sb_end = 229376
        self.nsem = 5

    def sbuf(self, name, shape, dtype, at=None):
        nbytes = int(np.prod(shape[1:])) * (2 if dtype == BF16 else 4)
        nbytes = (nbytes + 31) // 32 * 32
        if at is None:
            at = self.sb_off
            self.sb_off += nbytes
            assert self.sb_off <= self.sb_end, ('SBUF overflow', name, self.sb_off)
        t = self.nc.alloc_sbuf_tensor_at(name, list(shape), dtype, offset=at)
        b = Buf(name, t, 'sbuf')
        b.off = at
        b.nbytes = nbytes
        return b

    def psum(self, name, shape, dtype=F32):
        t = self.nc.alloc_psum_tensor(name, list(shape), dtype)
        return Buf(name, t, 'psum')

    def dram(self, name, shape, dtype, kind):
        t = self.nc.dram_tensor(name, list(shape), dtype, kind=kind)
        return Buf(name, t.ap(), 'dram')

    def alias_phase(self, old, new):
        toks = []
        for b in old:
            if b.last_w is not None:
                toks.append(b.last_w)
            toks.extend(b.readers)
        for b in new:
            b.readers = list(b.readers) + toks

    def _need(self, eng, waits, tok):
        sem, val, teng = tok
        key = id(sem)
        if self.seen[eng].get(key, 0) >= val:
            return
        if key not in waits or waits[key][1] < val:
            waits[key] = (sem, val)

    def _deps(self, eng, reads, writes):
        waits = {}
        for b in reads:
            tok = b.last_w
            if tok is not None and not (tok[2] == eng and eng == 'pe'):
                self._need(eng, waits, tok)
            if b.space == 'psum':
                for r in b.readers:
                    if r[2] != eng:
                        self._need(eng, waits, r)
        for b in writes:
            tok = b.last_w
            if tok is not None and not (tok[2] == eng and eng == 'pe'):
                self._need(eng, waits, tok)
            for r in b.readers:
                if not (r[2] == eng and eng == 'pe'):
                    self._need(eng, waits, r)
        for key, (sem, val) in waits.items():
            self.seen[eng][key] = val
        return list(waits.values())

    def op(self, eng, fn, reads=(), writes=()):
        reads = _bufs(reads)
        writes = _bufs(writes)
        waits = self._deps(eng, reads, writes)
        self.cnt[eng] += 1
        tok = (self.sem[eng], self.cnt[eng], eng)
        self.ops[eng].append((waits, fn, (self.sem[eng], 1)))
        for b in writes:
            b.last_w = tok
            b.readers = []
        for b in reads:
            if b not in writes:
                b.readers.append(tok)
        return tok

    def dma(self, q, out, in_, **kw):
        ob, ib = out.buf, in_.buf
        waits = self._deps(q, [ib], [ob])
        if ob.space != 'dram':
            if ob.sem_in is None:
                ob.sem_in = self.nc.alloc_semaphore('din_' + ob.name)
                self.nsem += 1
            ob.cnt_in += 16
            sem, val = ob.sem_in, ob.cnt_in
        else:
            if ib.sem_out is None:
                ib.sem_out = self.nc.alloc_semaphore('dout_' + ib.name)
                self.nsem += 1
            ib.cnt_out += 16
            sem, val = ib.sem_out, ib.cnt_out
        tok = (sem, val, 'dma')
        oap, iap = out.ap, in_.ap

        def fn(e, oap=oap, iap=iap, kw=kw):
            return e.dma_start(out=oap, in_=iap, **kw)
        self.ops[q].append((waits, fn, (sem, 16)))
        ob.last_w = tok
        ob.readers = []
        ib.readers.append(tok)
        if ob.space == 'dram':
            self.final_tokens.append(tok)
        return tok

    def emit(self):
        nc = self.nc
        last = {}
        for sem, val, _ in self.final_tokens:
            k = id(sem)
            if k not in last or last[k][1] < val:
                last[k] = (sem, val)
        fin = list(last.values())
        eng_obj = {'pe': 'tensor', 'act': 'scalar', 'dve': 'vector', 'pool': 'gpsimd', 'sp': 'sync'}
        with nc.Block() as block:
            def mk(eng):
                def body(e):
                    for waits, fn, inc in self.ops[eng]:
                        for sem, val in waits:
                            e.wait_ge(sem, val)
                        fn(e).then_inc(inc[0], inc[1])
                    if eng == 'sp':
                        for sem, val in fin:
                            e.wait_ge(sem, val)
                return body
            for eng, attr in eng_obj.items():
                getattr(block, attr)(mk(eng))


def _const_tables():
    p = np.arange(128)[:, None]
    j = np.arange(128)[None, :]
    f = {}
    f['ident'] = (p == j)
    f['ones'] = np.ones((128, 128))
    f['U'] = (p <= j)
    f['LS'] = (p > j)
    sb = (p // 4 == j // 4) & (p < 64) & (j < 64)
    f['Us'] = ((p <= j) & sb)[:, :64]
    f['LSs'] = ((p > j) & sb)[:, :64]
    f['BOs'] = sb[:, :64]
    f['SELp'] = np.repeat(p == 127, 128, axis=1)
    f['SELm'] = np.repeat(p == 15, 128, axis=1)
    b16 = np.arange(16)[None, :]
    f['RS'] = (p == 4 * b16 + 3)
    f['BM'] = (p // 4 == b16) & (p < 64)
    f['BMT'] = ((p < 16) & (j // 4 == p))[:, :64]
    f['LSEL'] = ((j == 4 * (p // 4) + 3) & (p < 64))[:, :64]
    f['H0'] = np.repeat(p < 64, 128, axis=1) & (j < 64)
    f['H1'] = np.repeat(p < 64, 128, axis=1) & (j >= 64)
    cf_off, cols = {}, []
    o = 0
    for k, v in f.items():
        cf_off[k] = (o, v.shape[1])
        o += v.shape[1]
        cols.append(v.astype(np.float32))
    cf = np.concatenate(cols, axis=1)
    g = {}
    g['identb'] = (p == j).astype(np.float32)
    g['onesb'] = np.ones((128, 128), np.float32)
    g['M'] = np.where(j <= p, 0.0, NEG)
    g['MT'] = np.where(p <= j, 0.0, NEG)
    g['Ms'] = np.where((j <= p) & sb, 0.0, NEG)[:, :64]
    g['MTs'] = np.where((p <= j) & sb, 0.0, NEG)[:, :64]
    jj = np.arange(64)[None, None, :]
    bb = np.arange(16)[None, :, None]
    g['CM'] = np.broadcast_to((jj // 4 == bb), (128, 16, 64)).reshape(128, 1024).astype(np.float32)
    cb_off, cols = {}, []
    o = 0
    for k, v in g.items():
        cb_off[k] = (o, v.shape[1])
        o += v.shape[1]
        cols.append(np.asarray(v, np.float32))
    cbm = np.concatenate(cols, axis=1).astype(ml_dtypes.bfloat16)
    return cf, cf_off, cbm, cb_off


class Chunk:
    def __init__(self, slot, col0, n, kind, row0=0):
        self.slot, self.col0, self.n, self.kind, self.row0 = slot, col0, n, kind, row0


class Tile:
    def __init__(self, name, T, chunks, segs):
        self.name, self.T, self.chunks, self.segs = name, T, chunks, segs


W_SHAPES = {'w_in': (D, DIN), 'w_proj_a': (D, D), 'w_proj_b': (2 * D, D), 'w_out': (D, D),
            'w_up': (D, 2 * DFF), 'w_down': (DFF, D)}


def tile_blocks():
    bl = []
    for c in range(0, 3072, 256):
        bl.append(('w_in', 0, 8, [(c, 256)]))
    bl.append(('w_in', 0, 8, [(3072, 8), (8200, 32)]))
    for c in range(3080, 5128, 256):
        bl.append(('w_in', 0, 8, [(c, 256)]))
    for c in range(5128, 8200, 256):
        bl.append(('w_in', 0, 8, [(c, 256)]))
    for c in range(8232, 10280, 256):
        bl.append(('w_in', 0, 8, [(c, 256)]))
    for j in range(4):
        bl.append(('w_proj_a', 0, 8, [(256 * j, 256)]))
        bl.append(('w_proj_b', 0, 8, [(256 * j, 256)]))
        bl.append(('w_proj_b', 8, 8, [(256 * j, 256)]))
    for j in range(4):
        bl.append(('w_out', 0, 8, [(256 * j, 256)]))
    for j in range(11):
        bl.append(('w_up', 0, 8, [(256 * j, 256)]))
        bl.append(('w_up', 0, 8, [(DFF + 256 * j, 256)]))
    for j in range(4):
        for k0, nk in ((0, 8), (8, 8), (16, 6)):
            bl.append(('w_down', k0, nk, [(256 * j, 256)]))
    return bl


class K:
    def __init__(self, debug=None, tiles=('T0', 'T1', 'T2', 'T3', 'T4')):
        self.debug = debug or {}
        self.tile_sel = tuple(tiles)
        self.ntiles = len(self.tile_sel)
        nc = bass.Bass('TRN2', target_bir_lowering=False)
        self.nc = nc
        self.S = S = Sched(nc)
        self.dumps = {}
        cf, self.cfo, cbm, self.cbo = _const_tables()
        self.cf_np, self.cb_np = cf, cbm
        din = lambda n, s, dt=F32: S.dram(n, s, dt, 'ExternalInput')
        dout = lambda n, s: S.dram(n, s, F32, 'ExternalOutput')
        I = self.I = {}
        I['xp'] = din('xp', [2048, D]); I['xs'] = din('xs', [64, D]); I['meta'] = din('meta', [16, D])
        I['s_mconv'] = din('s_mconv', [48, 1024]); I['s_C'] = din('s_C', [16, 4, 128, 256])
        I['s_n'] = din('s_n', [64, 128]); I['s_m'] = din('s_m', [16, 4])
        I['s_sconv'] = din('s_sconv', [48, 3072]); I['s_ssm'] = din('s_ssm', [16, 2048, 128])
        I['s_fconv'] = din('s_fconv', [32, 2 * DFF])
        I['cf'] = din('cf', list(cf.shape)); I['cb'] = din('cb', list(cbm.shape), BF16)
        for n, s in (('ln0_g', [D]), ('ln0_b', [D]), ('b_if', [8]), ('w_mconv', [4, 1024]), ('b_mconv', [1, 1024]),
                     ('mnorm_g', [1, 1024]), ('w_sconv', [4, 3072]), ('b_sconv', [1, 3072]), ('dt_bias', [32]),
                     ('A_log', [32]), ('ssm_D', [32]), ('snorm_g', [1, 2048]), ('ln1_g', [D]), ('ln1_b', [D]),
                     ('w_fconv', [3, 2 * DFF]), ('b_fconv', [1, 2 * DFF]), ('ln2_g', [D]), ('ln2_b', [D])):
            I[n] = din(n, s)
        for n, s in W_SHAPES.items():
            I[n] = din(n, list(s))
        O = self.O = {}
        O['y_p'] = dout('y_p', [2048, D]); O['y_s'] = dout('y_s', [64, D])
        O['p_mconv'] = dout('p_mconv', [3, 1024]); O['p_C'] = dout('p_C', [4, 128, 256])
        O['p_n'] = dout('p_n', [4, 128]); O['p_m'] = dout('p_m', [1, 4])
        O['p_sconv'] = dout('p_sconv', [3, 3072]); O['p_ssm'] = dout('p_ssm', [2048, 128])
        O['p_fconv'] = dout('p_fconv', [2, 2 * DFF])
        O['o_mconv'] = dout('o_mconv', [48, 1024]); O['o_C'] = dout('o_C', [16, 4, 128, 256])
        O['o_n'] = dout('o_n', [64, 128]); O['o_m'] = dout('o_m', [16, 4])
        O['o_sconv'] = dout('o_sconv', [48, 3072]); O['o_ssm'] = dout('o_ssm', [16, 2048, 128])
        O['o_fconv'] = dout('o_fconv', [32, 2 * DFF])
        self.rr = {}
        self.build()
        S.emit()

    def rot(self, key, n):
        i = self.rr.get(key, 0)
        self.rr[key] = i + 1
        return i % n

    def mm(self, out, lhsT, rhs, start=True, stop=True):
        self.S.op('pe', lambda e: e.matmul(out.ap, lhsT=lhsT.ap, rhs=rhs.ap, start=start, stop=stop),
                  reads=[lhsT, rhs], writes=[out])

    def tr(self, out, in_, ident):
        self.S.op('pe', lambda e: e.transpose(out=out.ap, in_=in_.ap, identity=ident.ap),
                  reads=[in_, ident], writes=[out])

    def act(self, out, in_, func=AF.Copy, bias=None, scale=None, accum=None):
        kw = {}
        if bias is not None:
            kw['bias'] = bias.ap if isinstance(bias, View) else bias
        if scale is not None:
            kw['scale'] = scale.ap if isinstance(scale, View) else scale
        if accum is not None:
            kw['accum_out'] = accum.ap
        self.S.op('act', lambda e: e.activation(out=out.ap, in_=in_.ap, func=func, **kw),
                  reads=[in_, bias, scale], writes=[out, accum])

    def tt(self, out, a, b, op, eng='dve'):
        self.S.op(eng, lambda e: e.tensor_tensor(out=out.ap, in0=a.ap, in1=b.ap, op=op),
                  reads=[a, b], writes=[out])

    def ts(self, out, a, s1, op0, s2=None, op1=None, eng='dve', accum=None):
        v1 = s1.ap if isinstance(s1, View) else s1
        v2 = s2.ap if isinstance(s2, View) else s2
        kw = {}
        if op1 is not None:
            kw['op1'] = op1
        if accum is not None:
            kw['accum_out'] = accum.ap
        self.S.op(eng, lambda e: e.tensor_scalar(out=out.ap, in0=a.ap, scalar1=v1, scalar2=v2, op0=op0, **kw),
                  reads=[a, s1, s2], writes=[out, accum])

    def stt(self, out, a, s, b, op0, op1, eng='dve'):
        v = s.ap if isinstance(s, View) else s
        self.S.op(eng, lambda e: e.scalar_tensor_tensor(out=out.ap, in0=a.ap, scalar=v, in1=b.ap, op0=op0, op1=op1),
                  reads=[a, s, b], writes=[out])

    def cp(self, out, in_, eng='dve'):
        if eng == 'act':
            return self.act(out, in_)
        self.S.op(eng, lambda e: e.tensor_copy(out=out.ap, in_=in_.ap), reads=[in_], writes=[out])

    def memset(self, out, val, eng='dve'):
        self.S.op(eng, lambda e: e.memset(out.ap, val), writes=[out])

    def rmax(self, out, in_, eng='dve'):
        self.S.op(eng, lambda e: e.tensor_reduce(out=out.ap, in_=in_.ap, axis=AX.X, op=ALU.max),
                  reads=[in_], writes=[out])

    def dma(self, out, in_, q='sp'):
        self.S.dma(q, out, in_)

    def dump(self, name, view, shape):
        if name not in self.debug:
            return
        d = self.S.dram('dbg_' + name, list(shape), view.ap.dtype, 'ExternalOutput')
        self.dumps[name] = d
        self.dma(d.ap(), view)

    def cfv(self, name, rows=128, cols=None):
        o, w = self.cfo[name]
        cols = w if cols is None else cols
        return self.cf[:rows, o:o + cols]

    def cbv(self, name, rows=128, cols=None):
        o, w = self.cbo[name]
        cols = w if cols is None else cols
        return self.cb[:rows, o:o + cols]

    def ws_init(self):
        S = self.S
        self.wlist = tile_blocks()
        self.nbt = len(self.wlist)
        self.wblocks = self.wlist * self.ntiles
        self.wst = [S.sbuf(f'wst{i}', [128, 8, 256], F32) for i in range(2)]
        self.wbf = [S.sbuf(f'wbf{i}', [128, 8, 256], BF16) for i in range(2)]
        self.wx4 = [S.sbuf(f'wrx{i}', [128, 8, 256], BF16, at=self.wst[i // 2].off + 4096 * (i % 2)) for i in range(4)]
        self.wring = self.wbf + self.wx4
        self.wscr = [S.dram(f'wscr{j}', [128, 8, 256], BF16, 'Internal') for j in range(self.nbt)] if self.ntiles > 1 else None
        self.w_loaded = 0
        self.w_cast = 0
        self.w_next = 0
        self.w_ring_started = False

    def _w_load(self, i):
        name, k0, nk, parts = self.wblocks[i]
        st = self.wst[i % 2]
        W = self.I[name]
        c = 0
        for (c0, n) in parts:
            src = View(W, W.t[k0 * 128:(k0 + nk) * 128, c0:c0 + n].rearrange('(k p) c -> p k c', p=128))
            self.dma(st[:, 0:nk, c:c + n], src, q='sp')
            c += n

    def _w_castop(self, i):
        name, k0, nk, parts = self.wblocks[i]
        n = sum(p[1] for p in parts)
        eng = 'dve' if (i % 4) != 3 else 'act'
        self.cp(self.wbf[i % 2][:, 0:nk, 0:n], self.wst[i % 2][:, 0:nk, 0:n], eng)
        if self.wscr is not None:
            self.dma(self.wscr[i][:, 0:nk, 0:n], self.wbf[i % 2][:, 0:nk, 0:n], q='sp')

    def _w_ringload(self, i):
        name, k0, nk, parts = self.wblocks[i]
        n = sum(p[1] for p in parts)
        dst = self.wring[(i - self.nbt) % 6]
        self.dma(dst[:, 0:nk, 0:n], self.wscr[i % self.nbt][:, 0:nk, 0:n], q='sp')

    def wnext(self):
        i = self.w_next
        nb = len(self.wblocks)
        self.w_next += 1
        if i < self.nbt:
            lim = self.nbt
            while self.w_loaded < min(lim, i + 2):
                self._w_load(self.w_loaded)
                self.w_loaded += 1
            while self.w_cast < min(lim, i + 2):
                self._w_castop(self.w_cast)
                self.w_cast += 1
            while self.w_loaded < min(lim, i + 3):
                self._w_load(self.w_loaded)
                self.w_loaded += 1
            return self.wbf[i % 2], self.wblocks[i]
        if not self.w_ring_started:
            self.w_ring_started = True
            self.S.alias_phase(self.wst, self.wx4)
            self.w_loaded = self.nbt
        while self.w_loaded < min(nb, i + 6):
            self._w_ringload(self.w_loaded)
            self.w_loaded += 1
        return self.wring[(i - self.nbt) % 6], self.wblocks[i]

    def build(self):
        S = self.S
        sb = S.sbuf
        ncf, ncb = self.cf_np.shape[1], self.cb_np.shape[1]
        self.cf = sb('cf', [128, ncf], F32)
        self.cb = sb('cb', [128, ncb], BF16)
        self.lnc = sb('lnc', [128, 2, D], F32)
        self.bif_b = sb('bif_b', [128, 8], F32)
        self.dtb_b = sb('dtb_b', [128, 32], F32)
        self.A_b = sb('A_b', [128, 32], F32)
        self.D_b = sb('D_b', [128, 32], F32)
        self.Dfm = sb('Dfm', [128, 16], F32)
        self.cwm = sb('cwm', [128, 8, 6], F32)
        self.cws = sb('cws', [128, 24, 5], F32)
        self.sng = sb('sng', [128, 16], F32)
        self.cwf = sb('cwf', [128, 44, 4], F32)
        self.ws_init()
        self.xr = [sb(f'xr{i}', [128, D], F32) for i in range(4)]
        self.zs = [None] * 4
        self.xnT = sb('xnT', [128, 8, 512], BF16)
        self.hgT = sb('hgT', [128, 8, 512], BF16)
        self.ygT = sb('ygT', [128, 16, 512], BF16)
        self.Cf = sb('Cf', [128, 4, 256], F32); self.Cb = sb('Cb', [128, 4, 256], BF16)
        self.nf = sb('nf', [128, 4], F32); self.nb = sb('nb', [128, 4], BF16)
        self.m_b = sb('m_b', [128, 4], F32)
        self.STf = sb('STf', [128, 2048], F32); self.STb = sb('STb', [128, 2048], BF16)
        self.cq = sb('cq', [128, 8, 3], F32); self.cx = sb('cx', [128, 24, 3], F32)
        self.cff = sb('cff', [128, 44, 2], F32)
        self.scar = sb('scar', [128, 44 * 16 * 2], F32)
        self.gat = sb('gat', [128, 4, 8], F32)
        self.dta = sb('dta', [128, 4, 64], F32)
        self.ifdt = sb('ifdt', [128, 4, 40], F32)
        self.sm = [sb(f'sm{i}', [128, 32], F32) for i in range(16)]
        self.xb16 = sb('xb16', [128, D], BF16)
        self.cst = sb('cst', [128, 8], F32)
        self.pn_st = sb('pn_st', [128, 128], F32)
        R0 = S.sb_off
        o = R0
        def at(name, shape, dt):
            nonlocal o
            b = sb(name, shape, dt, at=o)
            o += b.nbytes
            return b
        self.cE = [at(f'cE{i}', [128, 520], F32) for i in range(2)]
        self.cacc = [at(f'cacc{i}', [128, 512], F32) for i in range(3)]
        self.cth = [at(f'cth{i}', [128, 512], F32) for i in range(2)]
        self.cacc2 = [at(f'cacc2_{i}', [128, 512], F32) for i in range(2)]
        e1 = o
        o = R0
        self.R1 = at('R1', [128, 4, 128], F32); self.R2 = at('R2', [128, 4, 128], F32)
        self.R3 = at('R3', [128, 4, 128], F32); self.wT = at('wT', [128, 4, 128], F32)
        self.ST = at('ST', [128, 4, 128], BF16); self.kTM = at('kTM', [128, 4, 128], BF16)
        self.hh = at('hh', [128, 4, 256], F32); self.vw = at('vw', [128, 4, 256], BF16)
        self.hgTM = at('hgTM', [128, D], BF16)
        e2 = o
        o = R0
        self.xdt = at('xdt', [128, 2048], BF16); self.xsD = at('xsD', [128, 2048], BF16)
        self.wx = at('wx', [128, 512], BF16); self.BTM = at('BTM', [128, 512], BF16)
        self.LT = at('LT', [128, 8, 128], BF16); self.MTt = at('MTt', [128, 8, 128], BF16)
        self.t1 = at('t1', [128, 512], F32)
        self.yz = at('yz', [128, 2048], BF16)
        self.ynTM = at('ynTM', [128, 2048], BF16)
        e3 = o
        F0 = max(e1, e2, e3)
        conv_end = F0
        o = F0
        self.qkT = at('qkT', [128, 8, 512], BF16)
        self.v = [at(f'v{i}', [128, D], BF16) for i in range(4)]
        self.oth = [at(f'oth{i}', [128, D], BF16) for i in range(4)]
        a1_end = o
        o = F0
        self.xbcT = at('xbcT', [128, 24, 512], BF16)
        for i in range(2):
            self.zs[i] = at(f'zs{i}', [128, 2048], BF16)
        self.LA = at('LA', [128, 8, 128], F32)
        a2_end = o
        o = F0
        self.gth = at('gth', [128, 16, 512], BF16)
        self.mixT = at('mixT', [128, 8, 512], BF16)
        self.hffT = at('hffT', [128, 22, 512], BF16)
        b_end = o
        S.sb_off = max(a1_end, a2_end, b_end)
        for i in range(2, 4):
            self.zs[i] = sb(f'zs{i}', [128, 2048], BF16)
        self.arenas = [(self.xr[2].off, 2 * self.xr[2].nbytes), (self.zs[2].off, 2 * self.zs[2].nbytes)]
        a0, a1 = self.arenas[0][0], self.arenas[1][0]
        self.C0b = [sb(f'C0b{i}', [128, 4, 256], F32, at=a0 + 4096 * i) for i in range(2)]
        self.C0b16 = [sb(f'C0b16_{i}', [128, 4, 258], BF16, at=a1 + 2080 * i) for i in range(2)]
        self.qmb = [sb(f'qmb{i}', [128, 4, 64], BF16, at=a1 + 4160 + 512 * i) for i in range(2)]
        self.kTMm = [sb(f'kTMm{i}', [128, 4, 128], BF16, at=a1 + 5184 + 1024 * i) for i in range(2)]
        self.n0T = sb('n0T', [128, 64], F32, at=a1 + 7232)
        self.n16 = sb('n16', [128, 64], BF16, at=a1 + 7488)
        self.decS = sb('decS', [128, 64], F32, at=a1 + 7616)
        self.Rm = sb('Rm', [128, 64], F32, at=a1 + 7872)
        self.S0b = [sb('S0b0', [128, 16, 128], F32, at=a0), sb('S0b1', [128, 16, 128], F32, at=self.LT.off)]
        assert self.LT.off + 8192 <= self.ynTM.off
        self.SbT = sb('SbT', [128, 2048], BF16, at=a1)
        self.wxm = sb('wxm', [128, 2048], BF16, at=a1 + 4096)
        self.ysi = sb('ysi', [128, 2048], BF16)
        self.decP = sb('decP', [128, 256], F32)
        self.Rr = sb('Rr', [128, 2, 256], F32)
        self.CTmb = [sb(f'CTmb{i}', [128, 4, 64], BF16) for i in range(2)]
        self.grpArena2 = [self.S0b[0], self.SbT, self.wxm]
        self.xsDT = sb('xsDT', [128, 8, 128], BF16, at=self.ynTM.off)
        self.LAh = [sb(f'LAh{j}', [128, 128], F32, at=self.LA.off + 512 * j) for j in range(8)]
        self.LTh = [sb(f'LTh{j}', [128, 4, 128], BF16, at=self.LT.off + 1024 * j) for j in range(2)]
        self.MTh = [[sb(f'MTh{b}_{j}', [128, 128], BF16, at=base + 256 * j) for j in range(8)]
                    for b, base in enumerate((self.MTt.off, self.ynTM.off + 2048))]
        self.grpArena = self.C0b + self.C0b16 + self.qmb + self.kTMm + [self.n0T, self.n16, self.decS, self.Rm]
        self.grpA1conv = self.cE + self.cacc + self.cth + self.cacc2
        self.grpA1rec = [self.R1, self.R2, self.R3, self.wT, self.ST, self.kTM, self.hh, self.vw, self.hgTM]
        self.grpA1fix = [self.qkT] + self.v + self.oth
        self.grpA2fix = [self.xbcT, self.zs[0], self.zs[1]] + self.LAh
        self.grpA2rec = [self.xdt, self.xsD, self.wx, self.BTM, self.t1, self.yz, self.ynTM] + self.LTh + self.MTh[0]
        self.grpB = [self.gth, self.mixT, self.hffT]
        print('SBUF used', S.sb_off, 'of', S.sb_end, 'R', R0, conv_end - R0, a1_end - R0, a2_end - R0, b_end - R0)
        self.ps = [S.psum(f'ps{i}', [128, 512], F32) for i in range(8)]
        self.setup()
        tiles = self.make_tiles()
        for tl in tiles:
            if tl.name in self.tile_sel:
                self.run_tile(tl, last=(tl.name == 'T4'))

    def make_tiles(self):
        def pch(slot, col0, c):
            ch = Chunk(slot, col0, 128, 'p', row0=128 * c)
            ch.final = (c == 15)
            return ch
        m = Chunk(0, 0, 16, 'm'); m.final = False
        tiles = [Tile('T0', 400, [m] + [pch(1 + i, 16 + 128 * i, i) for i in range(3)], [(0, 1, 400, 'p')])]
        for t in range(3):
            tiles.append(Tile(f'T{t + 1}', 512, [pch(i, 128 * i, 3 + 4 * t + i) for i in range(4)], [(0, 1, 512, 'p')]))
        sc = Chunk(1, 128, 64, 's'); sc.final = False
        tiles.append(Tile('T4', 192, [pch(0, 0, 15), sc], [(0, 1, 128, 'p'), (128, 16, 4, 's')]))
        return tiles

    def psb(self, i):
        return self.ps[i].ap().bitcast(BF16)

    def setup(self):
        I = self.I
        self.dma(self.cf.ap(), I['cf'].ap())
        self.dma(self.cb.ap(), I['cb'].ap())
        pb = lambda n: View(I[n], I[n].t.partition_broadcast(128))
        self.dma(self.bif_b.ap(), pb('b_if'))
        self.dma(self.dtb_b.ap(), pb('dt_bias'))
        self.dma(self.A_b.ap(), pb('A_log'))
        self.dma(self.D_b.ap(), pb('ssm_D'))
        self.act(self.A_b.ap(), self.A_b.ap(), AF.Exp)
        self.ts(self.A_b.ap(), self.A_b.ap(), -1.0, ALU.mult)
        D3 = self.D_b.ap().rearrange('p (g r) -> p g r', r=2)
        self.cp(self.Dfm[0:64, :], D3[0:64, :, 0], 'dve')
        self.cp(self.Dfm[64:128, :], D3[64:128, :, 1], 'dve')
        identf = self.cfv('ident')
        stg = self.cacc[0]
        def fm_params(dst, rows, G, scale_groups=None):
            R = sum(r for _, r in rows)
            for g0 in range(0, G, 4):
                gn = min(4, G - g0)
                r0 = 0
                for (nm, nr) in rows:
                    self.dma(stg[r0:r0 + nr, 0:gn * 128], I[nm][:, g0 * 128:(g0 + gn) * 128])
                    r0 += nr
                bank = self.ps[self.rot('setup', 2)]
                for g in range(gn):
                    self.tr(bank[:, g * R:(g + 1) * R], stg[0:R, g * 128:(g + 1) * 128], identf[0:R, 0:R])
                self.cp(dst[:, g0:g0 + gn, :], bank[:, 0:gn * R].rearrange('p (g r) -> p g r', r=R), 'act')
        fm_params(self.cwm, [('w_mconv', 4), ('b_mconv', 1), ('mnorm_g', 1)], 8)
        fm_params(self.cws, [('w_sconv', 4), ('b_sconv', 1)], 24)
        fm_params(self.cwf, [('w_fconv', 3), ('b_fconv', 1)], 44)
        sng3 = self.sng.ap().rearrange('p (g r) -> p g r', r=1)
        fm_params(sng3, [('snorm_g', 1)], 16)
        self.ts(self.cwm[:, :, 0:6], self.cwm[:, :, 0:6], 0.5, ALU.mult)
        self.ts(self.cws.ap(), self.cws.ap(), 0.5, ALU.mult)
        self.ts(self.cwf[:, 0:22, :], self.cwf[:, 0:22, :], 0.5, ALU.mult)
        self.memset(self.cst[:, 0:1], LN_EPS)
        self.memset(self.cst[:, 1:2], 0.5 * float(np.log(128.0)))
        self.memset(self.cst[:, 2:3], 1.0)
        self.eps_t = self.cst
        for b in (self.Cf, self.nf, self.m_b, self.STf, self.cq, self.cx, self.cff):
            self.memset(b.ap(), 0.0)
        for b in (self.Cb, self.nb, self.STb):
            self.memset(b.ap(), 0.0, 'pool')

    def kc(self, kind, n):
        if kind == 's':
            return dict(U=self.cfv('Us', 64), LS=self.cfv('LSs', 64), BO=self.cfv('BOs', 64),
                        M=self.cbv('Ms', 64), MT=self.cbv('MTs', 64))
        return dict(U=self.cfv('U', n, n), LS=self.cfv('LS', n, n), BO=self.cfv('ones', n, n),
                    M=self.cbv('M', n, n), MT=self.cbv('MT', n, n))

    def ln_rows(self, x, n, gname, bname):
        I = self.I
        st, mv, rs = self.sm[0], self.sm[1], self.sm[2]
        self.dma(self.lnc[:, 0, :], View(I[gname], I[gname].t.partition_broadcast(128)))
        self.dma(self.lnc[:, 1, :], View(I[bname], I[bname].t.partition_broadcast(128)))
        for i in range(2):
            self.S.op('dve', lambda e, i=i: e.bn_stats(out=st.t[:n, i * 6:(i + 1) * 6], in_=x.ap[:, i * 512:(i + 1) * 512]),
                      reads=[x], writes=[st])
        self.S.op('dve', lambda e: e.bn_aggr(out=mv.t[:n, 0:2], in_=st.t[:n, 0:12]), reads=[st], writes=[mv])
        self.act(rs[:n, 0:1], mv[:n, 1:2], AF.Ln, bias=self.eps_t[:n, 0:1])
        self.act(rs[:n, 0:1], rs[:n, 0:1], AF.Exp, scale=-0.5)
        self.ts(x, x, mv[:n, 0:1], ALU.subtract, rs[:n, 0:1], ALU.mult)
        self.tt(x, x, self.lnc[:n, 0, :], ALU.mult)
        self.tt(x, x, self.lnc[:n, 1, :], ALU.add)

    def to_fm(self, tl, src_of_chunk, dstT):
        identb = self.cbv('identb')
        for ch in tl.chunks:
            n = ch.n
            xb = self.xb16
            self.act(xb[:n, :], src_of_chunk(ch))
            bank = 6 + self.rot('tfm', 2)
            pv = self.psb(bank)
            for k in range(8):
                self.tr(pv[:, k * n:(k + 1) * n], xb[:n, k * 128:(k + 1) * 128], identb[:n, :n])
            self.cp(dstT[:, :, ch.col0:ch.col0 + n], pv[:, 0:8 * n].rearrange('p (k n) -> p k n', n=n), 'dve')

    def _conv_taps(self, tl, psv, W, wtab, g, carry_p, scar_view, E, acc):
        Wm = W - 1
        off = 0
        for (col0, nb, L, kind) in tl.segs:
            Ev = E[:, off:off + nb * (L + Wm)].rearrange('p (b l) -> p b l', b=nb)
            pseg = psv[:, col0:col0 + nb * L].rearrange('p (b l) -> p b l', b=nb)
            if kind == 'p':
                self.cp(Ev[:, :, 0:Wm], carry_p[:, g:g + 1, :], 'act')
            elif kind == 'm':
                self.memset(Ev[:, :, 0:Wm], 0.0, 'dve')
            else:
                self.cp(Ev[:, :, 0:Wm], scar_view[:, g, :, :], 'act')
            self.act(Ev[:, :, Wm:Wm + L], pseg)
            av = acc[:, col0:col0 + nb * L].rearrange('p (b l) -> p b l', b=nb)
            self.act(av, pseg, AF.Identity, scale=wtab[:, g, Wm:W], bias=wtab[:, g, W:W + 1])
            if kind == 's':
                self.cp(scar_view[:, g, :, :], Ev[:, :, L:L + Wm], 'act')
            else:
                self.cp(carry_p[:, g:g + 1, :], Ev[:, :, L:L + Wm], 'act')
            for j in range(Wm):
                self.stt(av, Ev[:, :, j:j + L], wtab[:, g, j:j + 1], av, ALU.mult, ALU.add)
            off += nb * (L + Wm)

    def conv_group(self, tl, psv, W, wtab, g, carry_p, scar_view, dst, final=True):
        E = self.cE[self.rot('cE', 2)]
        acc = self.cacc[self.rot('cacc', 3)]
        self._conv_taps(tl, psv, W, wtab, g, carry_p, scar_view, E, acc)
        T = tl.T
        if not final:
            return acc
        prev = getattr(self, '_conv_pending', None)

        def stage2(acc=acc, dst=dst, T=T):
            th = self.cth[self.rot('cth', 2)]
            self.act(th[:, 0:T], acc[:, 0:T], AF.Tanh)
            self.stt(dst, th[:, 0:T], 1.0, acc[:, 0:T], ALU.add, ALU.mult)
        self._conv_pending = stage2
        if prev is not None:
            prev()
        return acc

    def conv_flush(self):
        prev = getattr(self, '_conv_pending', None)
        self._conv_pending = None
        if prev is not None:
            prev()

    def carry_out(self, src, G, R, dst):
        identf = self.cfv('ident')
        for g0 in range(0, G, 4):
            gn = min(4, G - g0)
            bank = self.ps[self.rot('co', 2)]
            for g in range(gn):
                self.tr(bank[:R, g * 128:(g + 1) * 128], src[:, g0 + g, :], identf)
            stg = self.cacc[self.rot('cacc', 3)]
            self.cp(stg[:R, 0:gn * 128], bank[:R, 0:gn * 128], 'act')
            self.dma(dst[:, g0 * 128:(g0 + gn) * 128], stg[:R, 0:gn * 128])

    def scar_in(self, name, G, R):
        identf = self.cfv('ident')
        rows = 16 * R
        sv = self.scar[:, 0:G * rows].rearrange('p (g b r) -> p g b r', g=G, b=16)
        for g0 in range(0, G, 4):
            gn = min(4, G - g0)
            stg = self.cacc[self.rot('cacc', 3)]
            self.dma(stg[:rows, 0:gn * 128], self.I[name][:, g0 * 128:(g0 + gn) * 128])
            bank = self.ps[self.rot('co', 2)]
            for g in range(gn):
                self.tr(bank[:, g * rows:(g + 1) * rows], stg[:rows, g * 128:(g + 1) * 128], identf[:rows, :rows])
            self.cp(self.scar[:, g0 * rows:(g0 + gn) * rows], bank[:, 0:gn * rows], 'act')
        return sv

    def scar_out(self, name, G, R):
        rows = 16 * R
        src = self.scar[:, 0:G * rows].rearrange('p (g br) -> p g br', g=G)
        self.carry_out(src, G, rows, self.O[name].ap())

    def dense_fm(self, tl, actT, nkt_total, cb_group, kt0=0):
        Wb, (name, k0, nk, parts) = self.wnext()
        ncols = sum(p[1] for p in parts)
        T = tl.T
        for gl in range(ncols // 128):
            bank = self.ps[self.rot('mm', 4)]
            for k in range(nk):
                self.mm(bank[:, 0:T], Wb[:, k, gl * 128:(gl + 1) * 128], actT[:, k0 + k, 0:T],
                        start=(k0 + k == 0), stop=(k0 + k == nkt_total - 1))
            cb_group(gl, bank[:, 0:T])

    def dense_tm(self, tl, actT, cb_chunk):
        Wb, (name, k0, nk, parts) = self.wnext()
        ncols = sum(p[1] for p in parts)
        for ch in tl.chunks:
            bank = self.ps[self.rot('mm', 4)]
            for k in range(nk):
                self.mm(bank[:ch.n, 0:ncols], actT[:, k0 + k, ch.col0:ch.col0 + ch.n], Wb[:, k, 0:ncols],
                        start=(k == 0), stop=(k == nk - 1))
            cb_chunk(ch, bank[:ch.n, 0:ncols])

    def run_tile(self, tl, last):
        S, I, O = self.S, self.I, self.O
        T = tl.T
        isS = any(sg[3] == 's' for sg in tl.segs)
        if isS:
            S.alias_phase([self.xr[2], self.xr[3], self.zs[2], self.zs[3]], self.grpArena + self.grpArena2)
        for ch in tl.chunks:
            src = {'s': I['xs'].ap(), 'm': I['meta'].ap()}.get(ch.kind)
            if src is None:
                src = I['xp'][ch.row0:ch.row0 + ch.n, :]
            self.dma(self.xr[ch.slot][:ch.n, :], src)
            self.ln_rows(self.xr[ch.slot][:ch.n, :], ch.n, 'ln0_g', 'ln0_b')
        self.to_fm(tl, lambda ch: self.xr[ch.slot][:ch.n, :], self.xnT)
        for ch in tl.chunks:
            self.ts(self.xr[ch.slot][:ch.n, :], self.xr[ch.slot][:ch.n, :], ALPHA, ALU.mult)
        self.dump('xnT_' + tl.name, self.xnT[:, :, 0:T], [128, 8, T])
        if self.debug.get('stop') == 'p0':
            return
        S.alias_phase(self.grpA2fix + self.grpA2rec + self.grpB + self.grpA1rec, self.grpA1conv + self.grpA1fix)
        sq = self.scar_in('s_mconv', 8, 3) if isS else None
        for blk in range(4):
            def cbq(gl, psv, blk=blk):
                g = 2 * blk + gl
                self.conv_group(tl, psv, 4, self.cwm, g, self.cq, sq, self.qkT[:, g, 0:T])
            self.dense_fm(tl, self.xnT, 8, cbq)
        self.conv_flush()
        if isS:
            self.scar_out('o_mconv', 8, 3)
        if last:
            self.carry_out(self.cq.ap(), 8, 3, O['p_mconv'].ap())
        for blk in range(4):
            self.dense_tm(tl, self.xnT, lambda ch, psv, blk=blk: self.act(self.v[ch.slot][:ch.n, 256 * blk:256 * blk + 256], psv))
        for blk in range(4):
            self.dense_tm(tl, self.xnT, lambda ch, psv, blk=blk: self.act(self.oth[ch.slot][:ch.n, 256 * blk:256 * blk + 256], psv, AF.Tanh, scale=0.5))
        self.dense_tm(tl, self.xnT, lambda ch, psv: self.cp(self.ifdt[:ch.n, ch.slot, :], psv, 'dve'))
        for ch in tl.chunks:
            n, s = ch.n, ch.slot
            gi = self.gat[:n, s, 0:8]
            self.tt(gi, self.ifdt[:n, s, 0:8], self.bif_b[:n, :], ALU.add)
            e1 = self.sm[3]
            self.act(e1[:n, 0:4], self.gat[:n, s, 4:8], AF.Exp, scale=-1.0)
            self.act(e1[:n, 0:4], e1[:n, 0:4], AF.Ln, bias=self.cst[:n, 2:3])
            self.ts(self.gat[:n, s, 4:8], e1[:n, 0:4], -1.0, ALU.mult)
            d1 = self.sm[4]
            self.tt(d1[:n, 0:32], self.ifdt[:n, s, 8:40], self.dtb_b[:n, :], ALU.add)
            self.act(d1[:n, 0:32], d1[:n, 0:32], AF.Exp)
            self.act(self.dta[:n, s, 0:32], d1[:n, 0:32], AF.Ln, bias=self.cst[:n, 2:3])
            self.tt(self.dta[:n, s, 32:64], self.dta[:n, s, 0:32], self.A_b[:n, :], ALU.mult)
        self.dump('qkT_' + tl.name, self.qkT[:, :, 0:T], [128, 8, T])
        self.dump('gat_' + tl.name, self.gat.ap(), [128, 4, 8])
        self.dump('dta_' + tl.name, self.dta.ap(), [128, 4, 64])
        if self.debug.get('stop') == 'a1':
            return
        S.alias_phase(self.grpA1conv, self.grpA1rec)
        for ch in tl.chunks:
            self.mlstm_chunk(tl, ch, last)
        self.dump('hgT_' + tl.name, self.hgT[:, :, 0:T], [128, 8, T])
        if self.debug.get('stop') == 'mlstm':
            return
        S.alias_phase(self.grpA1rec + self.grpA1fix, self.grpA1conv + self.grpA2fix)
        for blk in range(8):
            def cbz(ch, psv, blk=blk):
                n = ch.n
                zc = self.cacc[self.rot('cacc', 3)]
                th = self.cth[self.rot('cth', 2)]
                self.cp(zc[:n, 0:256], psv, 'act')
                self.act(th[:n, 0:256], psv, AF.Tanh, scale=0.5)
                self.stt(self.zs[ch.slot][:n, 256 * blk:256 * blk + 256], th[:n, 0:256], 1.0, zc[:n, 0:256], ALU.add, ALU.mult)
            self.dense_tm(tl, self.xnT, cbz)
        if self.debug.get('stop') == 'a2z':
            return
        sx = self.scar_in('s_sconv', 24, 3) if isS else None
        for blk in range(12):
            def cbx(gl, psv, blk=blk):
                g = 2 * blk + gl
                self.conv_group(tl, psv, 4, self.cws, g, self.cx, sx, self.xbcT[:, g, 0:T])
            self.dense_fm(tl, self.xnT, 8, cbx)
        self.conv_flush()
        if isS:
            self.scar_out('o_sconv', 24, 3)
        if last:
            self.carry_out(self.cx.ap(), 24, 3, O['p_sconv'].ap())
        self.dump('xbcT_' + tl.name, self.xbcT[:, :, 0:T], [128, 24, T])
        if self.debug.get('stop') == 'a2':
            return
        S.alias_phase(self.grpA1conv, self.grpA2rec)
        for ch in tl.chunks:
            self.ssd_chunk(tl, ch, last)
        self.dump('ygT_' + tl.name, self.ygT[:, :, 0:T], [128, 16, T])
        if self.debug.get('stop') == 'ssd':
            return
        S.alias_phase(self.grpA2rec + self.grpA2fix, self.grpA1conv + self.grpB)
        for blk in range(8):
            def cbg(gl, psv, blk=blk):
                self.act(self.gth[:, 2 * blk + gl, 0:T], psv, AF.Tanh, scale=0.5)
            self.dense_fm(tl, self.xnT, 8, cbg)
        for j in range(4):
            Wb, (name, k0, nk, parts) = self.wnext()
            banksA = [self.ps[0], self.ps[1]]
            banksB = [self.ps[2], self.ps[3]]
            for gl in range(2):
                for k in range(8):
                    self.mm(banksA[gl][:, 0:T], Wb[:, k, gl * 128:(gl + 1) * 128], self.hgT[:, k, 0:T], start=(k == 0), stop=(k == 7))
            for half in range(2):
                Wb, (name, k0, nk, parts) = self.wnext()
                for gl in range(2):
                    for k in range(8):
                        kk = 8 * half + k
                        self.mm(banksB[gl][:, 0:T], Wb[:, k, gl * 128:(gl + 1) * 128], self.ygT[:, kk, 0:T], start=(kk == 0), stop=(kk == 15))
            for gl in range(2):
                g = 2 * j + gl
                m1 = self.cacc[self.rot('cacc', 3)]
                m2 = self.cacc2[self.rot('cacc2', 2)]
                self.stt(m1[:, 0:T], self.gth[:, g, 0:T], 1.0, banksA[gl][:, 0:T], ALU.add, ALU.mult)
                self.stt(m2[:, 0:T], self.gth[:, 8 + g, 0:T], 1.0, banksB[gl][:, 0:T], ALU.add, ALU.mult)
                self.tt(self.mixT[:, g, 0:T], m1[:, 0:T], m2[:, 0:T], ALU.add)
        self.rr['mm'] = 0
        for blk in range(4):
            def cbo(ch, psv, blk=blk):
                xv = self.xr[ch.slot][:ch.n, 256 * blk:256 * blk + 256]
                self.stt(xv, psv, 0.5, xv, ALU.mult, ALU.add)
            self.dense_tm(tl, self.mixT, cbo)
        for ch in tl.chunks:
            self.ln_rows(self.xr[ch.slot][:ch.n, :], ch.n, 'ln1_g', 'ln1_b')
        self.dump('x1_' + tl.name, self.xr[0].ap(), [128, D])
        self.to_fm(tl, lambda ch: self.xr[ch.slot][:ch.n, :], self.xnT)
        for ch in tl.chunks:
            self.ts(self.xr[ch.slot][:ch.n, :], self.xr[ch.slot][:ch.n, :], ALPHA, ALU.mult)
        sf = self.scar_in('s_fconv', 44, 2) if isS else None
        for j in range(11):
            accs = {}
            def cbua(gl, psv, j=j):
                g = 2 * j + gl
                accs[gl] = self.conv_group(tl, psv, 3, self.cwf, g, self.cff, sf, None, final=False)
            self.dense_fm(tl, self.xnT, 8, cbua)
            def cbub(gl, psv, j=j):
                g = 2 * j + gl
                E = self.cE[self.rot('cE', 2)]
                accb = self.cacc2[self.rot('cacc2', 2)]
                self._conv_taps(tl, psv, 3, self.cwf, 22 + g, self.cff, sf, E, accb)
                th = self.cth[self.rot('cth', 2)]
                acca = accs[gl]
                self.act(th[:, 0:T], acca[:, 0:T], AF.Tanh)
                self.stt(th[:, 0:T], th[:, 0:T], 1.0, acca[:, 0:T], ALU.add, ALU.mult)
                self.tt(self.hffT[:, g, 0:T], th[:, 0:T], accb[:, 0:T], ALU.mult)
            self.dense_fm(tl, self.xnT, 8, cbub)
        if isS:
            self.scar_out('o_fconv', 44, 2)
        if last:
            self.carry_out(self.cff.ap(), 44, 2, O['p_fconv'].ap())
        self.dump('hffT_' + tl.name, self.hffT[:, :, 0:T], [128, 22, T])
        for blk in range(4):
            banks = {ch.slot: self.ps[ch.slot] for ch in tl.chunks}
            for (k0, nk) in ((0, 8), (8, 8), (16, 6)):
                Wb, meta = self.wnext()
                for ch in tl.chunks:
                    for k in range(nk):
                        self.mm(banks[ch.slot][:ch.n, 0:256], self.hffT[:, k0 + k, ch.col0:ch.col0 + ch.n], Wb[:, k, 0:256],
                                start=(k0 + k == 0), stop=(k0 + k == 21))
            for ch in tl.chunks:
                xv = self.xr[ch.slot][:ch.n, 256 * blk:256 * blk + 256]
                self.tt(xv, banks[ch.slot][:ch.n, 0:256], xv, ALU.add)
        for ch in tl.chunks:
            self.ln_rows(self.xr[ch.slot][:ch.n, :], ch.n, 'ln2_g', 'ln2_b')
            if ch.kind == 'p':
                self.dma(O['y_p'][ch.row0:ch.row0 + ch.n, :], self.xr[ch.slot][:ch.n, :])
            elif ch.kind == 's':
                self.dma(O['y_s'].ap(), self.xr[ch.slot][:ch.n, :])

    def mlstm_chunk(self, tl, ch, last):
        I, O = self.I, self.O
        n, s, c0, kind = ch.n, ch.slot, ch.col0, ch.kind
        kc = self.kc(kind, n)
        U, LS, M, MT = kc['U'], kc['LS'], kc['M'], kc['MT']
        identf, onesf = self.cfv('ident', n, n), self.cfv('ones', n, n)
        identb, onesb = self.cbv('identb', n, n), self.cbv('onesb', n, n)
        ps = self.ps
        cols = slice(c0, c0 + n)
        ig = self.gat[:n, s, 0:4]
        lf = self.gat[:n, s, 4:8]
        sm = self.sm
        bt, mi, bm, mt, wi, emt, negm, den, rden, wi2 = (sm[i] for i in range(5, 15))
        R1, R2, R3, wT, ST, kTM, hh, vw = self.R1, self.R2, self.R3, self.wT, self.ST, self.kTM, self.hh, self.vw
        self.mm(ps[0][:n, 0:4], U, lf)
        for h in range(4):
            self.ts(R1[:n, h, :n], LS, self.gat[:n, s, 4 + h:5 + h], ALU.mult)
            self.ts(R2[:n, h, :n], identf, self.gat[:n, s, h:h + 1], ALU.mult, eng='pool')
        for h in range(4):
            self.mm(ps[1][:n, h * 128:h * 128 + n], U, R1[:n, h, :n], start=True, stop=False)
            self.mm(ps[1][:n, h * 128:h * 128 + n], onesf, R2[:n, h, :n], start=False, stop=False)
            self.mm(ps[1][:n, h * 128:h * 128 + n], identb, M, start=False, stop=True)
        self.rmax(mi[:n, 0:4], ps[1][:n, :].rearrange('p (h t) -> p h t', h=4)[:, :, 0:n])
        if kind == 's':
            m0s = sm[15]
            self.dma(m0s[:16, 0:4], I['s_m'].ap())
            self.mm(ps[0][:n, 4:8], self.cfv('BMT', 16), m0s[:16, 0:4])
            m0v = ps[0][:n, 4:8]
        else:
            m0v = self.m_b[:n, :]
        self.cp(bt[:n, 0:4], ps[0][:n, 0:4], 'act')
        self.tt(bm[:n, 0:4], bt[:n, 0:4], m0v, ALU.add)
        self.tt(mt[:n, 0:4], bm[:n, 0:4], mi[:n, 0:4], ALU.max)
        self.tt(bm[:n, 0:4], bm[:n, 0:4], mt[:n, 0:4], ALU.subtract)
        self.act(wi[:n, 0:4], bm[:n, 0:4], AF.Exp)
        self.act(emt[:n, 0:4], mt[:n, 0:4], AF.Exp, scale=-1.0, bias=self.cst[:n, 1:2])
        self.ts(negm[:n, 0:4], mt[:n, 0:4], -1.0, ALU.mult)
        for h in range(4):
            self.ts(R3[:n, h, :n], identf, negm[:n, h:h + 1], ALU.mult, eng='pool')
        for h in range(4):
            o = ps[2][:n, h * 128:h * 128 + n]
            self.mm(o, R1[:n, h, :n], U, start=True, stop=False)
            self.mm(o, R2[:n, h, :n], onesf, start=False, stop=False)
            self.mm(o, onesf, R3[:n, h, :n], start=False, stop=False)
            self.mm(o, identb, MT, start=False, stop=True)
        ps2v = ps[2][:n, :].rearrange('p (h t) -> p h t', h=4)[:, :, 0:n]
        self.act(wT[:n, :, :n], ps2v, AF.Exp)
        for h in range(4):
            self.mm(ps[1][:n, h * 128:h * 128 + n], self.qkT[:, 4 + h, cols], self.qkT[:, h, cols])
        ps1v = ps[1][:n, :].rearrange('p (h t) -> p h t', h=4)[:, :, 0:n]
        self.tt(ST[:n, :, :n], ps1v, wT[:n, :, :n], ALU.mult)
        pb7 = self.psb(7)
        for h in range(4):
            self.tr(pb7[:n, h * 128:(h + 1) * 128], self.qkT[:, 4 + h, cols], self.cbv('identb'))
        self.cp(kTM[:n, :, :], pb7[:n, 0:512].rearrange('p (h d) -> p h d', h=4), 'act')
        if kind == 's':
            self.mlstm_sample_states(tl, ch, wi, wT, kTM, mt)
        for h in range(4):
            self.mm(ps[3 + h // 2][:n, (h % 2) * 256:(h % 2) * 256 + 256], ST[:n, h, :n], self.v[s][:n, h * 256:(h + 1) * 256])
        for h in range(4):
            self.mm(ps[0][:n, 8 + h:9 + h], ST[:n, h, :n], onesb[:, 0:1])
        if kind != 's':
            for h in range(4):
                self.mm(ps[5 + h // 2][:n, (h % 2) * 256:(h % 2) * 256 + 256], self.qkT[:, h, cols], self.Cb[:, h, :])
            for h in range(4):
                self.mm(ps[0][:n, 12 + h:13 + h], self.qkT[:, h, cols], self.nb[:, h:h + 1])
        dint = ps[0][:n, 12:16] if kind != 's' else sm[12][:n, 0:4]
        self.tt(den[:n, 0:4], dint, wi[:n, 0:4], ALU.mult)
        self.tt(den[:n, 0:4], den[:n, 0:4], ps[0][:n, 8:12], ALU.add)
        self.ts(wi2[:n, 0:4], den[:n, 0:4], -1.0, ALU.mult)
        self.tt(den[:n, 0:4], den[:n, 0:4], wi2[:n, 0:4], ALU.max)
        self.tt(den[:n, 0:4], den[:n, 0:4], emt[:n, 0:4], ALU.max)
        self.S.op('dve', lambda e: e.reciprocal(out=rden.t[:n, 0:4], in_=den.t[:n, 0:4]), reads=[den], writes=[rden])
        self.tt(wi2[:n, 0:4], wi[:n, 0:4], rden[:n, 0:4], ALU.mult)
        for h in range(4):
            self.act(hh[:n, h, :], ps[3 + h // 2][:n, (h % 2) * 256:(h % 2) * 256 + 256], AF.Copy, scale=rden[:n, h:h + 1])
            if kind != 's':
                iv = ps[5 + h // 2][:n, (h % 2) * 256:(h % 2) * 256 + 256]
            else:
                iv = (R1 if h < 2 else R2)[:n, :, :].rearrange('p a b -> p (a b)')[:, (h % 2) * 256:(h % 2) * 256 + 256]
            self.stt(hh[:n, h, :], iv, wi2[:n, h:h + 1], hh[:n, h, :], ALU.mult, ALU.add)
        st, mv, rs = sm[0], sm[1], sm[2]
        for h in range(4):
            self.S.op('dve', lambda e, h=h: e.bn_stats(out=st.t[:n, h * 6:(h + 1) * 6], in_=hh.t[:n, h, :]), reads=[hh], writes=[st])
        for h in range(4):
            self.S.op('dve', lambda e, h=h: e.bn_aggr(out=mv.t[:n, 2 * h:2 * h + 2], in_=st.t[:n, h * 6:(h + 1) * 6]), reads=[st], writes=[mv])
        mvv = mv[:n, 0:8].rearrange('p (h t) -> p h t', t=2)
        self.act(rs[:n, 0:4], mvv[:, :, 1], AF.Ln, bias=self.cst[:n, 0:1])
        self.act(rs[:n, 0:4], rs[:n, 0:4], AF.Exp, scale=-0.5)
        for h in range(4):
            self.ts(hh[:n, h, :], hh[:n, h, :], mv[:n, 2 * h:2 * h + 1], ALU.subtract, rs[:n, h:h + 1], ALU.mult)
            self.stt(self.hgTM[:n, h * 256:(h + 1) * 256], self.oth[s][:n, h * 256:(h + 1) * 256], 1.0, hh[:n, h, :], ALU.add, ALU.mult)
        for k in range(8):
            self.tr(pb7[:, k * n:(k + 1) * n], self.hgTM[:n, k * 128:(k + 1) * 128], identb)
        self.tt(self.hgT[:, :, cols].rearrange('p k t -> p t k'), pb7[:, 0:8 * n].rearrange('p (k t) -> p t k', k=8),
                View(self.cwm, self.cwm.t[:, :, 5].unsqueeze(1).to_broadcast([128, n, 8])), ALU.mult)
        if kind != 's':
            wl16 = sm[15]
            self.cp(wl16.ap().bitcast(BF16)[:n, 0:4], wT[:n, :, n - 1], 'act')
            for h in range(4):
                self.ts(vw[:n, h, :], self.v[s][:n, h * 256:(h + 1) * 256], wT[:n, h, n - 1:n], ALU.mult)
            for h in range(4):
                self.mm(ps[5 + h // 2][:, (h % 2) * 256:(h % 2) * 256 + 256], kTM[:n, h, :], vw[:n, h, :])
            for h in range(4):
                self.mm(ps[0][:, 16 + h:17 + h], kTM[:n, h, :], wl16.ap().bitcast(BF16)[:n, h:h + 1])
            SEL = self.cfv('SELp' if n == 128 else 'SELm', n)
            self.mm(ps[0][:, 32:36], SEL, wi[:n, 0:4])
            self.mm(ps[0][:, 36:40], SEL, mt[:n, 0:4])
            dec = sm[3]
            self.cp(dec[:, 0:8], ps[0][:, 32:40], 'act')
            for h in range(4):
                self.stt(self.Cf[:, h, :], self.Cf[:, h, :], dec[:, h:h + 1], ps[5 + h // 2][:, (h % 2) * 256:(h % 2) * 256 + 256], ALU.mult, ALU.add)
            self.tt(self.nf.ap(), self.nf.ap(), dec[:, 0:4], ALU.mult)
            self.tt(self.nf.ap(), self.nf.ap(), ps[0][:, 16:20], ALU.add)
            self.cp(self.m_b.ap(), dec[:, 4:8], 'dve')
            self.cp(self.Cb.ap(), self.Cf.ap(), 'act')
            self.cp(self.nb.ap(), self.nf.ap(), 'pool')
            if ch.final:
                self.dma(O['p_C'].ap().rearrange('h d e -> d h e'), self.Cf.ap())
                identf128 = self.cfv('ident')
                self.tr(ps[0][:4, 128:256], self.nf.ap(), identf128)
                self.cp(self.pn_st[:4, :], ps[0][:4, 128:256], 'act')
                self.dma(O['p_n'].ap(), self.pn_st[:4, :])
                self.dma(O['p_m'].ap(), self.m_b[0:1, :])

    def mlstm_sample_states(self, tl, ch, wi, wT, kTM, mt):
        I, O, ps, sm = self.I, self.O, self.ps, self.sm
        n, s = 64, ch.slot
        identf = self.cfv('ident')
        RS, BM = self.cfv('RS', 64), self.cfv('BM', 64)
        CM = self.cbv('CM').rearrange('p (b j) -> p b j', b=16)
        vw = self.vw
        wl = sm[3]
        tmp = self.R3
        self.tt(tmp[:n, :, 0:64], wT[:n, :, 0:64], View(self.cf, self.cfv('LSEL', 64).ap.unsqueeze(1).to_broadcast([64, 4, 64])), ALU.mult)
        self.S.op('dve', lambda e: e.tensor_reduce(out=wl.t[:n, 0:4], in_=tmp.t[:n, :, 0:64], axis=AX.X, op=ALU.add), reads=[tmp], writes=[wl])
        wl16 = sm[4].ap().bitcast(BF16)
        self.cp(wl16[:n, 0:4], wl[:n, 0:4], 'act')
        for h in range(4):
            self.ts(vw[:n, h, :], self.v[s][:n, h * 256:(h + 1) * 256], wl[:n, h:h + 1], ALU.mult)
        Rm3 = self.Rm[:n, :].rearrange('p (b h) -> p b h', h=4)
        self.tt(Rm3, View(wi, wi.t[:n, 0:4].unsqueeze(1).to_broadcast([n, 16, 4])),
                View(self.cf, RS.ap.unsqueeze(2).to_broadcast([n, 16, 4])), ALU.mult)
        self.mm(ps[0][:, 64:128], self.cfv('ones', 64), self.Rm[:n, :])
        self.cp(self.decS.ap(), ps[0][:, 64:128], 'act')
        self.mm(ps[0][:16, 40:44], RS, mt[:n, 0:4])
        mo = sm[15]
        self.cp(mo[:16, 8:12], ps[0][:16, 40:44], 'act')
        self.dma(O['o_m'].ap(), mo[:16, 8:12])
        stg = self.pn_st
        self.dma(stg[:64, :], I['s_n'].ap())
        self.tr(ps[0][:, 192:256], stg[:64, :], identf[:64, :64])
        self.cp(self.n0T.ap(), ps[0][:, 192:256], 'act')
        self.cp(self.n16.ap(), self.n0T.ap(), 'pool')
        kTMflat = kTM[:n, :, :].rearrange('p h d -> p (h d)')
        for b in range(16):
            i = b % 2
            C0, C16, qm, km = self.C0b[i], self.C0b16[i], self.qmb[i], self.kTMm[i]
            self.dma(C0.ap(), I['s_C'][b].rearrange('h d e -> d h e'))
            self.cp(C16[:, :, 0:256], C0.ap(), 'act')
            self.cp(C16[:, :, 256:257], self.n16[:, 4 * b:4 * b + 4].rearrange('p (h o) -> p h o', o=1), 'dve')
            self.tt(qm.ap(), self.qkT[:, 0:4, ch.col0:ch.col0 + 64], View(self.cb, CM.ap[:, b, :].unsqueeze(1).to_broadcast([128, 4, 64])), ALU.mult)
            self.ts(km[:n, :, :].rearrange('p h d -> p (h d)'), kTMflat, BM[:, b:b + 1], ALU.mult)
            for h in range(4):
                self.mm(ps[3 + h][:n, 0:257], qm[:, h, :], C16[:, h, 0:257], start=(b == 0), stop=(b == 15))
            for h in range(4):
                self.mm(ps[1 + h // 2][:, (h % 2) * 256:(h % 2) * 256 + 256], km[:n, h, :], vw[:n, h, :])
            for h in range(4):
                self.mm(ps[0][:, 128 + 4 * b + h:129 + 4 * b + h], km[:n, h, :], wl16[:n, h:h + 1])
            for h in range(4):
                self.stt(C0[:, h, :], C0[:, h, :], self.decS[:, 4 * b + h:4 * b + h + 1],
                         ps[1 + h // 2][:, (h % 2) * 256:(h % 2) * 256 + 256], ALU.mult, ALU.add)
            self.dma(O['o_C'][b].rearrange('h d e -> d h e'), C0.ap())
        for h in range(4):
            dst = (self.R1 if h < 2 else self.R2)[:n, :, :].rearrange('p a b -> p (a b)')[:, (h % 2) * 256:(h % 2) * 256 + 256]
            self.cp(dst, ps[3 + h][:n, 0:256], 'act')
            self.cp(sm[12][:n, h:h + 1], ps[3 + h][:n, 256:257], 'act')
        self.tt(self.n0T.ap(), self.n0T.ap(), self.decS.ap(), ALU.mult)
        self.tt(self.n0T.ap(), self.n0T.ap(), ps[0][:, 128:192], ALU.add)
        self.tr(ps[0][:64, 256:384], self.n0T.ap(), identf)
        self.cp(stg[:64, :], ps[0][:64, 256:384], 'act')
        self.dma(O['o_n'].ap(), stg[:64, :])

    def ssd_chunk(self, tl, ch, last):
        I, O, ps, sm = self.I, self.O, self.ps, self.sm
        n, s, c0, kind = ch.n, ch.slot, ch.col0, ch.kind
        kc = self.kc(kind, n)
        U, LS, BO, MT = kc['U'], kc['LS'], kc['BO'], kc['MT']
        identb = self.cbv('identb')
        cols = slice(c0, c0 + n)
        dt = self.dta[:n, s, 0:32]
        a = self.dta[:n, s, 32:64]
        btsb, dect, wend, decS = sm[5], sm[6], sm[7], sm[8]
        self.mm(ps[0][:n, 0:32], U, a)
        self.mm(ps[0][:n, 32:64], BO, a)
        self.cp(btsb[:n, 0:32], ps[0][:n, 0:32], 'act')
        self.act(dect[:n, 0:32], ps[0][:n, 0:32], AF.Exp)
        self.tt(wend[:n, 0:32], ps[0][:n, 32:64], btsb[:n, 0:32], ALU.subtract)
        self.act(wend[:n, 0:32], wend[:n, 0:32], AF.Exp)
        if kind != 's':
            self.mm(ps[0][:, 64:96], self.cfv('ones', n), a)
            self.act(decS[:, 0:32], ps[0][:, 64:96], AF.Exp)
        S = self.S
        pb6, pb7 = self.psb(6), self.psb(7)
        xsTM = self.xdt
        for g in range(16):
            pv = pb6 if g < 8 else pb7
            self.tr(pv[:n, (g % 8) * 128:(g % 8 + 1) * 128], self.xbcT[:, g, cols], identb)
        self.act(xsTM[:n, 0:1024], pb6[:n, 0:1024])
        self.act(xsTM[:n, 1024:2048], pb7[:n, 0:1024])
        S.alias_phase([self.ynTM], [self.xsDT])
        for half in range(2):
            for g in range(8):
                self.ts(self.xsDT[:, g, 0:n], self.xbcT[:, 8 * half + g, cols], self.Dfm[:, 8 * half + g:8 * half + g + 1], ALU.mult)
            pv = pb6 if half == 0 else pb7
            for g in range(8):
                self.tr(pv[:n, g * 128:(g + 1) * 128], self.xsDT[:, g, 0:n], identb)
            self.cp(self.xsD[:n, 1024 * half:1024 * half + 1024], pv[:n, 0:1024], 'dve')
        for g in range(4):
            self.tr(pb6[:n, g * 128:(g + 1) * 128], self.xbcT[:, 16 + g, cols], identb)
        self.act(self.BTM[:n, :], pb6[:n, 0:512])
        dtw = sm[13]
        self.tt(dtw[:n, 0:32], dt, wend[:n, 0:32], ALU.mult)
        S.alias_phase([self.xsDT], [self.ynTM])
        if kind == 's':
            self.ssd_sample_states(tl, ch, dtw)
        S.alias_phase([self.ynTM], self.MTh[1])
        ssq = sm[9]
        self.memset(ssq[:n, 0:4], 0.0)
        def stageA(g):
            MTb = self.MTh[g % 2]
            for j in range(8):
                self.act(self.LAh[j][:n, :n], LS, AF.Copy, scale=self.dta[:n, s, 32 + 8 * g + j:33 + 8 * g + j])
            for j in range(8):
                o = ps[1 + j // 4][:n, (j % 4) * 128:(j % 4) * 128 + n]
                self.mm(o, self.LAh[j][:n, :n], U, start=True, stop=False)
                self.mm(o, identb[:n, :n], MT, start=False, stop=True)
            for half in range(2):
                self.act(self.LTh[half][:n, :, :n],
                         ps[1 + half][:n, :].rearrange('p (h t) -> p h t', h=4)[:, :, 0:n], AF.Exp)
            self.mm(ps[3][:n, 0:n], self.xbcT[:, 16 + g, cols], self.xbcT[:, 20 + g, cols])
            for j in range(8):
                self.stt(MTb[j][:n, :n], self.LTh[j // 4][:n, j % 4, :n], self.dta[:n, s, 8 * g + j:8 * g + j + 1], ps[3][:n, 0:n], ALU.mult, ALU.mult)

        def stageB(g):
            MTb = self.MTh[g % 2]
            for j in range(8):
                h = 8 * g + j
                self.mm(ps[4][:n, j * 64:(j + 1) * 64], MTb[j][:n, :n], xsTM[:n, h * 64:(h + 1) * 64])
            t1 = self.t1
            if kind != 's':
                self.mm(ps[5][:n, 0:512], self.xbcT[:, 20 + g, cols], self.STb[:, 512 * g:512 * g + 512])
                for j in range(4):
                    self.act(t1[:n, j * 64:(j + 1) * 64], ps[5][:n, j * 64:(j + 1) * 64], AF.Copy, scale=dect[:n, 8 * g + j:8 * g + j + 1])
                self.tt(t1[:n, 256:512].rearrange('p (h d) -> p d h', d=64), ps[5][:n, 256:512].rearrange('p (h d) -> p d h', d=64),
                        View(dect, dect.t[:n, 8 * g + 4:8 * g + 8].unsqueeze(1).to_broadcast([n, 64, 4])), ALU.mult)
            else:
                self.cp(t1[:n, :], self.ysi[:n, 512 * g:512 * g + 512], 'dve')
            self.tt(t1[:n, :], t1[:n, :], ps[4][:n, 0:512], ALU.add)
            self.tt(t1[:n, :], t1[:n, :], self.xsD[:n, 512 * g:512 * g + 512], ALU.add)
            self.tt(self.yz[:n, 512 * g:512 * g + 512], t1[:n, :], self.zs[s][:n, 512 * g:512 * g + 512], ALU.mult)
            self.act(t1[:n, :], self.yz[:n, 512 * g:512 * g + 512], AF.Square, accum=ssq[:n, g:g + 1])

        stageA(0)
        for g in range(4):
            if g + 1 < 4:
                stageA(g + 1)
            stageB(g)
        S.alias_phase(self.MTh[1], [self.ynTM])
        rs = sm[10]
        self.act(rs[:n, 0:4], ssq[:n, 0:4], AF.Ln, scale=0.25 / 512.0, bias=self.cst[:n, 0:1])
        self.act(rs[:n, 0:4], rs[:n, 0:4], AF.Exp, scale=-0.5)
        self.ts(rs[:n, 0:4], rs[:n, 0:4], 0.5, ALU.mult)
        for g in range(4):
            self.ts(self.ynTM[:n, 512 * g:512 * g + 512], self.yz[:n, 512 * g:512 * g + 512], rs[:n, g:g + 1], ALU.mult)
        for k in range(16):
            pv = pb6 if k < 8 else pb7
            self.tr(pv[:, (k % 8) * n:(k % 8 + 1) * n], self.ynTM[:n, k * 128:(k + 1) * 128], identb[:n, :n])
        for half in range(2):
            pv = (pb6 if half == 0 else pb7)[:, 0:8 * n].rearrange('p (k t) -> p k t', k=8)
            self.tt(self.ygT[:, 8 * half:8 * half + 8, cols].rearrange('p k t -> p t k'), pv.rearrange('p k t -> p t k'),
                    View(self.sng, self.sng.t[:, 8 * half:8 * half + 8].unsqueeze(1).to_broadcast([128, n, 8])), ALU.mult)
        if kind != 's':
            for g in range(4):
                hs = slice(8 * g, 8 * g + 8)
                wxv = self.wx[:n, :].rearrange('p (h d) -> p d h', d=64)
                self.tt(wxv, self.xdt[:n, 512 * g:512 * g + 512].rearrange('p (h d) -> p d h', d=64),
                        View(dtw, dtw.t[:n, hs].unsqueeze(1).to_broadcast([n, 64, 8])), ALU.mult)
                bank = ps[3 + 2 * (g % 2)]
                self.mm(bank[:, 0:512], self.BTM[:n, g * 128:(g + 1) * 128], self.wx[:n, :])
                for j in range(8):
                    c0_ = 512 * g + 64 * j
                    self.act(self.STf[:, c0_:c0_ + 64], self.STf[:, c0_:c0_ + 64], AF.Copy, scale=decS[:, 8 * g + j:8 * g + j + 1])
                self.tt(self.STf[:, 512 * g:512 * g + 512], self.STf[:, 512 * g:512 * g + 512], bank[:, 0:512], ALU.add)
            self.cp(self.STb.ap(), self.STf.ap(), 'act')
            if ch.final:
                identf = self.cfv('ident')
                for j in range(16):
                    bank = ps[1 + (j // 4) % 2]
                    self.tr(bank[:, (j % 4) * 128:(j % 4 + 1) * 128], self.STf[:, j * 128:(j + 1) * 128], identf)
                    if j % 4 == 3:
                        stg = self.LA[:, 4 * ((j // 4) % 2):4 * ((j // 4) % 2) + 4, :]
                        self.S.alias_phase(self.LAh, [self.LA])
                        self.cp(stg, bank[:, 0:512].rearrange('p (j n) -> p j n', j=4), 'act')
                        q = j // 4
                        self.dma(O['p_ssm'][512 * q:512 * q + 512, :].rearrange('(j p) n -> p j n', p=128), stg)
                self.S.alias_phase([self.LA], self.LAh)

    def ssd_sample_states(self, tl, ch, wend):
        I, O, ps, sm, S = self.I, self.O, self.ps, self.sm, self.S
        n, s = 64, ch.slot
        identf = self.cfv('ident')
        RS, BM = self.cfv('RS', 64), self.cfv('BM', 64)
        CM = self.cbv('CM').rearrange('p (b j) -> p b j', b=16)
        dect = sm[6]
        S.alias_phase(self.grpArena, self.grpArena2)
        S.alias_phase(self.LTh + self.MTh[0] + [self.t1, self.yz], [self.S0b[1]])
        blsb = sm[11]
        self.cp(blsb[:n, 0:32], ps[0][:n, 32:64], 'act')
        bl3 = blsb[:n, 0:32].rearrange('p (j r) -> p j r', r=2)
        for r in range(2):
            self.tt(self.Rr[:n, r, :].rearrange('p (b j) -> p b j', b=16),
                    View(blsb, bl3.ap[:, :, r].unsqueeze(1).to_broadcast([n, 16, 16])),
                    View(self.cf, RS.ap.unsqueeze(2).to_broadcast([n, 16, 16])), ALU.mult, eng='pool')
        self.mm(ps[0][:, 256:512], self.cfv('H0', 64), self.Rr[:n, 0, :], start=True, stop=False)
        self.mm(ps[0][:, 256:512], self.cfv('H1', 64), self.Rr[:n, 1, :], start=False, stop=True)
        self.act(self.decP.ap(), ps[0][:, 256:512], AF.Exp)
        wxA = self.ynTM
        self.tt(wxA[:n, :].rearrange('p (h d) -> p d h', d=64), self.xdt[:n, :].rearrange('p (h d) -> p d h', d=64),
                View(wend, wend.t[:n, 0:32].unsqueeze(1).to_broadcast([n, 64, 32])), ALU.mult)
        for b in range(16):
            Sb = self.S0b[b % 2]
            cm = self.CTmb[b % 2]
            self.dma(Sb.ap(), I['s_ssm'][b].rearrange('(j p) n -> p j n', p=128))
            self.tt(cm.ap(), self.xbcT[:, 20:24, ch.col0:ch.col0 + 64], View(self.cb, CM.ap[:, b, :].unsqueeze(1).to_broadcast([128, 4, 64])), ALU.mult)
            self.ts(self.wxm[:n, :], wxA[:n, :], BM[:, b:b + 1], ALU.mult)
            for q in range(4):
                bank = ps[5 + q % 2]
                for i in range(4):
                    self.tr(bank[:, i * 128:(i + 1) * 128], Sb[:, 4 * q + i, :], identf)
                self.cp(self.SbT[:, 512 * q:512 * q + 512], bank[:, 0:512], 'act')
            for g in range(4):
                self.mm(ps[1 + g][:n, 0:512], cm[:, g, :], self.SbT[:, 512 * g:512 * g + 512], start=(b == 0), stop=(b == 15))
            for q in range(4):
                bank = ps[7] if q % 2 == 0 else ps[0]
                for i in range(4):
                    j = 4 * q + i
                    self.mm(bank[:, i * 128:(i + 1) * 128], self.wxm[:n, j * 128:(j + 1) * 128], self.BTM[:n, q * 128:(q + 1) * 128])
                for i in range(4):
                    j = 4 * q + i
                    self.stt(Sb[:, j, :], Sb[:, j, :], self.decP[:, 16 * b + j:16 * b + j + 1], bank[:, i * 128:(i + 1) * 128], ALU.mult, ALU.add)
            self.dma(O['o_ssm'][b].rearrange('(j p) n -> p j n', p=128), Sb.ap())
        for g in range(4):
            self.tt(self.ysi[:n, 512 * g:512 * g + 512].rearrange('p (h d) -> p d h', d=64),
                    ps[1 + g][:n, 0:512].rearrange('p (h d) -> p d h', d=64),
                    View(dect, dect.t[:n, 8 * g:8 * g + 8].unsqueeze(1).to_broadcast([n, 64, 8])), ALU.mult)
        S.alias_phase([self.S0b[1]], self.LTh + self.MTh[0] + [self.t1, self.yz])


_CACHE = {}


def _get_kernel():
    if 'k' not in _CACHE:
        _CACHE['k'] = K()
    return _CACHE['k']


def make_in_maps(kb, inputs):
    f = lambda a: np.ascontiguousarray(np.asarray(a, dtype=np.float32))
    xp, xs = f(inputs['x_prompt']), f(inputs['x_sample'])
    shared = {'meta': f(inputs['meta_tokens']), 'cf': kb.cf_np, 'cb': kb.cb_np,
              'ln0_g': f(inputs['ln0_g']), 'ln0_b': f(inputs['ln0_b']),
              'b_if': f(inputs['b_mlstm_if'])[0], 'w_mconv': f(inputs['w_mlstm_conv'])[0],
              'b_mconv': f(inputs['b_mlstm_conv']), 'mnorm_g': f(inputs['mlstm_norm_g']),
              'w_sconv': f(inputs['w_ssm_conv'])[0], 'b_sconv': f(inputs['b_ssm_conv']),
              'dt_bias': f(inputs['ssm_dt_bias'])[0], 'A_log': f(inputs['ssm_A_log'])[0],
              'ssm_D': f(inputs['ssm_D'])[0], 'snorm_g': f(inputs['ssm_norm_g']),
              'ln1_g': f(inputs['ln1_g'])[0], 'ln1_b': f(inputs['ln1_b'])[0],
              'w_fconv': f(inputs['w_ffn_conv'])[0], 'b_fconv': f(inputs['b_ffn_conv']),
              'ln2_g': f(inputs['ln2_g'])[0], 'ln2_b': f(inputs['ln2_b'])[0],
              'w_in': f(inputs['w_in'])[0], 'w_proj_a': f(inputs['w_proj_a'])[0],
              'w_proj_b': f(inputs['w_proj_b'])[0], 'w_out': f(inputs['w_out'])[0],
              'w_up': f(inputs['w_up'])[0], 'w_down': f(inputs['w_down'])[0]}
    maps = []
    for c in range(8):
        b = slice(16 * c, 16 * c + 16)
        m = dict(shared)
        m['xp'] = xp[c]
        m['xs'] = xs[b].reshape(64, D)
        m['s_mconv'] = f(inputs['state_mlstm_conv'])[0, b].reshape(48, 1024)
        m['s_C'] = f(inputs['state_mlstm_C'])[0, b]
        m['s_n'] = f(inputs['state_mlstm_n'])[0, b].reshape(64, 128)
        m['s_m'] = f(inputs['state_mlstm_m'])[0, b]
        m['s_sconv'] = f(inputs['state_ssm_conv'])[0, b].reshape(48, 3072)
        m['s_ssm'] = f(inputs['state_ssm'])[0, b].reshape(16, 2048, 128)
        m['s_fconv'] = f(inputs['state_ffn_conv'])[0, b].reshape(32, 2 * DFF)
        maps.append(m)
    return maps


def kernel(**inputs):
    kb = _get_kernel()
    maps = make_in_maps(kb, inputs)
    res = run_bass_kernel_spmd(kb.nc, maps, core_ids=list(range(8)))
    R = res.results
    cat = lambda k: np.stack([np.asarray(r[k], dtype=np.float32) for r in R])
    y_p = cat('y_p')
    y_s = cat('y_s').reshape(128, 4, D)
    p_mconv = cat('p_mconv')[None]
    p_C = cat('p_C')[None]
    p_n = cat('p_n')[None]
    p_m = cat('p_m').reshape(8, 4)[None]
    p_sconv = cat('p_sconv')[None]
    p_ssm = cat('p_ssm').reshape(8, 32, 64, 128)[None]
    p_fconv = cat('p_fconv')[None]
    s_mconv = cat('o_mconv').reshape(128, 3, 1024)[None]
    s_C = cat('o_C').reshape(128, 4, 128, 256)[None]
    s_n = cat('o_n').reshape(128, 4, 128)[None]
    s_m = cat('o_m').reshape(128, 4)[None]
    s_sconv = cat('o_sconv').reshape(128, 3, 3072)[None]
    s_ssm = cat('o_ssm').reshape(128, 32, 64, 128)[None]
    s_fconv = cat('o_fconv').reshape(128, 2, 2 * DFF)[None]
    return (y_p, y_s, p_mconv, p_C, p_n, p_m, p_sconv, p_ssm, p_fconv,
            s_mconv, s_C, s_n, s_m, s_sconv, s_ssm, s_fconv)
```

```python
import numpy as np
import ml_dtypes
import concourse.bass as bass
import concourse.mybir as mybir
from concourse.bass_utils import run_bass_kernel_spmd

F32 = mybir.dt.float32
BF16 = mybir.dt.bfloat16
ALU = mybir.AluOpType
AF = mybir.ActivationFunctionType
AX = mybir.AxisListType

D = 1024
DIN = 10280
DFF = 2816
NEG = -30000.0
ALPHA = 2.0 ** 0.25
LN_EPS = 1e-5
RMS_EPS = 1e-5
QSCALE = 128.0 ** -0.5


class Buf:
    def __init__(self, name, t, space):
        self.name = name
        self.t = t
        self.space = space
        self.last_w = None
        self.readers = []
        self.sem_in = None
        self.cnt_in = 0
        self.sem_out = None
        self.cnt_out = 0

    def __getitem__(self, idx):
        return View(self, self.t[idx])

    def ap(self):
        return View(self, self.t[:] if self.space != 'dram' else self.t)


class View:
    def __init__(self, buf, ap):
        self.buf = buf
        self.ap = ap

    def __getitem__(self, idx):
        return View(self.buf, self.ap[idx])

    def rearrange(self, *a, **k):
        return View(self.buf, self.ap.rearrange(*a, **k))

    def bc(self, axis, shape):
        return View(self.buf, self.ap.unsqueeze(axis).to_broadcast(list(shape)))

    def bitcast(self, dt):
        return View(self.buf, self.ap.bitcast(dt))


def _bufs(vs):
    out = []
    for v in vs:
        if v is None or isinstance(v, (int, float)):
            continue
        b = v.buf if isinstance(v, View) else v
        if b not in out:
            out.append(b)
    return out


class Sched:
    ENGS = ('pe', 'act', 'dve', 'pool', 'sp')

    def __init__(self, nc):
        self.nc = nc
        self.sem = {e: nc.alloc_semaphore('sem_' + e) for e in self.ENGS}
        self.cnt = {e: 0 for e in self.ENGS}
        self.ops = {e: [] for e in self.ENGS}
        self.seen = {e: {} for e in self.ENGS}
        self.final_tokens = []
        self.sb_off = 16512
        self.sb_end = 229376
        self.nsem = 5

    def sbuf(self, name, shape, dtype, at=None):
        nbytes = int(np.prod(shape[1:])) * (2 if dtype == BF16 else 4)
        nbytes = (nbytes + 31) // 32 * 32
        if at is None:
            at = self.sb_off
            self.sb_off += nbytes
            assert self.sb_off <= self.sb_end, ('SBUF overflow', name, self.sb_off)
        t = self.nc.alloc_sbuf_tensor_at(name, list(shape), dtype, offset=at)
        b = Buf(name, t, 'sbuf')
        b.off = at
        b.nbytes = nbytes
        return b

    def psum(self, name, shape, dtype=F32):
        t = self.nc.alloc_psum_tensor(name, list(shape), dtype)
        return Buf(name, t, 'psum')

    def dram(self, name, shape, dtype, kind):
        t = self.nc.dram_tensor(name, list(shape), dtype, kind=kind)
        return Buf(name, t.ap(), 'dram')

    def alias_phase(self, old, new):
        toks = []
        for b in old:
            if b.last_w is not None:
                toks.append(b.last_w)
            toks.extend(b.readers)
        for b in new:
            b.readers = list(b.readers) + toks

    def _need(self, eng, waits, tok):
        sem, val, teng = tok
        key = id(sem)
        if self.seen[eng].get(key, 0) >= val:
            return
        if key not in waits or waits[key][1] < val:
            waits[key] = (sem, val)

    def _deps(self, eng, reads, writes):
        waits = {}
        for b in reads:
            tok = b.last_w
            if tok is not None and not (tok[2] == eng and eng == 'pe'):
                self._need(eng, waits, tok)
            if b.space == 'psum':
                for r in b.readers:
                    if r[2] != eng:
                        self._need(eng, waits, r)
        for b in writes:
            tok = b.last_w
            if tok is not None and not (tok[2] == eng and eng == 'pe'):
                self._need(eng, waits, tok)
            for r in b.readers:
                if not (r[2] == eng and eng == 'pe'):
                    self._need(eng, waits, r)
        for key, (sem, val) in waits.items():
            self.seen[eng][key] = val
        return list(waits.values())

    def op(self, eng, fn, reads=(), writes=()):
        reads = _bufs(reads)
        writes = _bufs(writes)
        waits = self._deps(eng, reads, writes)
        self.cnt[eng] += 1
        tok = (self.sem[eng], self.cnt[eng], eng)
        self.ops[eng].append((waits, fn, (self.sem[eng], 1)))
        for b in writes:
            b.last_w = tok
            b.readers = []
        for b in reads:
            if b not in writes:
                b.readers.append(tok)
        return tok

    def dma(self, q, out, in_, **kw):
        ob, ib = out.buf, in_.buf
        waits = self._deps(q, [ib], [ob])
        if ob.space != 'dram':
            if ob.sem_in is None:
                ob.sem_in = self.nc.alloc_semaphore('din_' + ob.name)
                self.nsem += 1
            ob.cnt_in += 16
            sem, val = ob.sem_in, ob.cnt_in
        else:
            if ib.sem_out is None:
                ib.sem_out = self.nc.alloc_semaphore('dout_' + ib.name)
                self.nsem += 1
            ib.cnt_out += 16
            sem, val = ib.sem_out, ib.cnt_out
        tok = (sem, val, 'dma')
        oap, iap = out.ap, in_.ap

        def fn(e, oap=oap, iap=iap, kw=kw):
            return e.dma_start(out=oap, in_=iap, **kw)
        self.ops[q].append((waits, fn, (sem, 16)))
        ob.last_w = tok
        ob.readers = []
        ib.readers.append(tok)
        if ob.space == 'dram':
            self.final_tokens.append(tok)
        return tok

    def emit(self):
        nc = self.nc
        last = {}
        for sem, val, _ in self.final_tokens:
            k = id(sem)
            if k not in last or last[k][1] < val:
                last[k] = (sem, val)
        fin = list(last.values())
        eng_obj = {'pe': 'tensor', 'act': 'scalar', 'dve': 'vector', 'pool': 'gpsimd', 'sp': 'sync'}
        with nc.Block() as block:
            def mk(eng):
                def body(e):
                    for waits, fn, inc in self.ops[eng]:
                        for sem, val in waits:
                            e.wait_ge(sem, val)
                        fn(e).then_inc(inc[0], inc[1])
                    if eng == 'sp':
                        for sem, val in fin:
                            e.wait_ge(sem, val)
                return body
            for eng, attr in eng_obj.items():
                getattr(block, attr)(mk(eng))


def _const_tables():
    p = np.arange(128)[:, None]
    j = np.arange(128)[None, :]
    f = {}
    f['ident'] = (p == j)
    f['ones'] = np.ones((128, 128))
    f['U'] = (p <= j)
    f['LS'] = (p > j)
    sb = (p // 4 == j // 4) & (p < 64) & (j < 64)
    f['Us'] = ((p <= j) & sb)[:, :64]
    f['LSs'] = ((p > j) & sb)[:, :64]
    f['BOs'] = sb[:, :64]
    f['SELp'] = np.repeat(p == 127, 128, axis=1)
    f['SELm'] = np.repeat(p == 15, 128, axis=1)
    b16 = np.arange(16)[None, :]
    f['RS'] = (p == 4 * b16 + 3)
    f['BM'] = (p // 4 == b16) & (p < 64)
    f['BMT'] = ((p < 16) & (j // 4 == p))[:, :64]
    f['LSEL'] = ((j == 4 * (p // 4) + 3) & (p < 64))[:, :64]
    f['H0'] = np.repeat(p < 64, 128, axis=1) & (j < 64)
    f['H1'] = np.repeat(p < 64, 128, axis=1) & (j >= 64)
    cf_off, cols = {}, []
    o = 0
    for k, v in f.items():
        cf_off[k] = (o, v.shape[1])
        o += v.shape[1]
        cols.append(v.astype(np.float32))
    cf = np.concatenate(cols, axis=1)
    g = {}
    g['identb'] = (p == j).astype(np.float32)
    g['onesb'] = np.ones((128, 128), np.float32)
    g['M'] = np.where(j <= p, 0.0, NEG)
    g['MT'] = np.where(p <= j, 0.0, NEG)
    g['Ms'] = np.where((j <= p) & sb, 0.0, NEG)[:, :64]
    g['MTs'] = np.where((p <= j) & sb, 0.0, NEG)[:, :64]
    jj = np.arange(64)[None, None, :]
    bb = np.arange(16)[None, :, None]
    g['CM'] = np.broadcast_to((jj // 4 == bb), (128, 16, 64)).reshape(128, 1024).astype(np.float32)
    cb_off, cols = {}, []
    o = 0
    for k, v in g.items():
        cb_off[k] = (o, v.shape[1])
        o += v.shape[1]
        cols.append(np.asarray(v, np.float32))
    cbm = np.concatenate(cols, axis=1).astype(ml_dtypes.bfloat16)
    return cf, cf_off, cbm, cb_off


class Chunk:
    def __init__(self, slot, col0, n, kind, row0=0):
        self.slot, self.col0, self.n, self.kind, self.row0 = slot, col0, n, kind, row0


class Tile:
    def __init__(self, name, T, chunks, segs):
        self.name, self.T, self.chunks, self.segs = name, T, chunks, segs


W_SHAPES = {'w_in': (D, DIN), 'w_proj_a': (D, D), 'w_proj_b': (2 * D, D), 'w_out': (D, D),
            'w_up': (D, 2 * DFF), 'w_down': (DFF, D)}


def tile_blocks():
    bl = []
    for c in range(0, 3072, 256):
        bl.append(('w_in', 0, 8, [(c, 256)]))
    bl.append(('w_in', 0, 8, [(3072, 8), (8200, 32)]))
    for c in range(3080, 5128, 256):
        bl.append(('w_in', 0, 8, [(c, 256)]))
    for c in range(5128, 8200, 256):
        bl.append(('w_in', 0, 8, [(c, 256)]))
    for c in range(8232, 10280, 256):
        bl.append(('w_in', 0, 8, [(c, 256)]))
    for j in range(4):
        bl.append(('w_proj_a', 0, 8, [(256 * j, 256)]))
        bl.append(('w_proj_b', 0, 8, [(256 * j, 256)]))
        bl.append(('w_proj_b', 8, 8, [(256 * j, 256)]))
    for j in range(4):
        bl.append(('w_out', 0, 8, [(256 * j, 256)]))
    for j in range(11):
        bl.append(('w_up', 0, 8, [(256 * j, 256)]))
        bl.append(('w_up', 0, 8, [(DFF + 256 * j, 256)]))
    for j in range(4):
        for k0, nk in ((0, 8), (8, 8), (16, 6)):
            bl.append(('w_down', k0, nk, [(256 * j, 256)]))
    return bl


class K:
    def __init__(self, debug=None, tiles=('T0', 'T1', 'T2', 'T3', 'T4')):
        self.debug = debug or {}
        self.tile_sel = tuple(tiles)
        self.ntiles = len(self.tile_sel)
        nc = bass.Bass('TRN2', target_bir_lowering=False)
        self.nc = nc
        self.S = S = Sched(nc)
        self.dumps = {}
        cf, self.cfo, cbm, self.cbo = _const_tables()
        self.cf_np, self.cb_np = cf, cbm
        din = lambda n, s, dt=F32: S.dram(n, s, dt, 'ExternalInput')
        dout = lambda n, s: S.dram(n, s, F32, 'ExternalOutput')
        I = self.I = {}
        I['xp'] = din('xp', [2048, D]); I['xs'] = din('xs', [64, D]); I['meta'] = din('meta', [16, D])
        I['s_mconv'] = din('s_mconv', [48, 1024]); I['s_C'] = din('s_C', [16, 4, 128, 256])
        I['s_n'] = din('s_n', [64, 128]); I['s_m'] = din('s_m', [16, 4])
        I['s_sconv'] = din('s_sconv', [48, 3072]); I['s_ssm'] = din('s_ssm', [16, 2048, 128])
        I['s_fconv'] = din('s_fconv', [32, 2 * DFF])
        I['cf'] = din('cf', list(cf.shape)); I['cb'] = din('cb', list(cbm.shape), BF16)
        for n, s in (('ln0_g', [D]), ('ln0_b', [D]), ('b_if', [8]), ('w_mconv', [4, 1024]), ('b_mconv', [1, 1024]),
                     ('mnorm_g', [1, 1024]), ('w_sconv', [4, 3072]), ('b_sconv', [1, 3072]), ('dt_bias', [32]),
                     ('A_log', [32]), ('ssm_D', [32]), ('snorm_g', [1, 2048]), ('ln1_g', [D]), ('ln1_b', [D]),
                     ('w_fconv', [3, 2 * DFF]), ('b_fconv', [1, 2 * DFF]), ('ln2_g', [D]), ('ln2_b', [D])):
            I[n] = din(n, s)
        for n, s in W_SHAPES.items():
            I[n] = din(n, list(s))
        O = self.O = {}
        O['y_p'] = dout('y_p', [2048, D]); O['y_s'] = dout('y_s', [64, D])
        O['p_mconv'] = dout('p_mconv', [3, 1024]); O['p_C'] = dout('p_C', [4, 128, 256])
        O['p_n'] = dout('p_n', [4, 128]); O['p_m'] = dout('p_m', [1, 4])
        O['p_sconv'] = dout('p_sconv', [3, 3072]); O['p_ssm'] = dout('p_ssm', [2048, 128])
        O['p_fconv'] = dout('p_fconv', [2, 2 * DFF])
        O['o_mconv'] = dout('o_mconv', [48, 1024]); O['o_C'] = dout('o_C', [16, 4, 128, 256])
        O['o_n'] = dout('o_n', [64, 128]); O['o_m'] = dout('o_m', [16, 4])
        O['o_sconv'] = dout('o_sconv', [48, 3072]); O['o_ssm'] = dout('o_ssm', [16, 2048, 128])
        O['o_fconv'] = dout('o_fconv', [32, 2 * DFF])
        self.rr = {}
        self.build()
        S.emit()

    def rot(self, key, n):
        i = self.rr.get(key, 0)
        self.rr[key] = i + 1
        return i % n

    def mm(self, out, lhsT, rhs, start=True, stop=True):
        self.S.op('pe', lambda e: e.matmul(out.ap, lhsT=lhsT.ap, rhs=rhs.ap, start=start, stop=stop),
                  reads=[lhsT, rhs], writes=[out])

    def tr(self, out, in_, ident):
        self.S.op('pe', lambda e: e.transpose(out=out.ap, in_=in_.ap, identity=ident.ap),
                  reads=[in_, ident], writes=[out])

    def act(self, out, in_, func=AF.Copy, bias=None, scale=None, accum=None):
        kw = {}
        if bias is not None:
            kw['bias'] = bias.ap if isinstance(bias, View) else bias
        if scale is not None:
            kw['scale'] = scale.ap if isinstance(scale, View) else scale
        if accum is not None:
            kw['accum_out'] = accum.ap
        self.S.op('act', lambda e: e.activation(out=out.ap, in_=in_.ap, func=func, **kw),
                  reads=[in_, bias, scale], writes=[out, accum])

    def tt(self, out, a, b, op, eng='dve'):
        self.S.op(eng, lambda e: e.tensor_tensor(out=out.ap, in0=a.ap, in1=b.ap, op=op),
                  reads=[a, b], writes=[out])

    def ts(self, out, a, s1, op0, s2=None, op1=None, eng='dve', accum=None):
        v1 = s1.ap if isinstance(s1, View) else s1
        v2 = s2.ap if isinstance(s2, View) else s2
        kw = {}
        if op1 is not None:
            kw['op1'] = op1
        if accum is not None:
            kw['accum_out'] = accum.ap
        self.S.op(eng, lambda e: e.tensor_scalar(out=out.ap, in0=a.ap, scalar1=v1, scalar2=v2, op0=op0, **kw),
                  reads=[a, s1, s2], writes=[out, accum])

    def stt(self, out, a, s, b, op0, op1, eng='dve'):
        v = s.ap if isinstance(s, View) else s
        self.S.op(eng, lambda e: e.scalar_tensor_tensor(out=out.ap, in0=a.ap, scalar=v, in1=b.ap, op0=op0, op1=op1),
                  reads=[a, s, b], writes=[out])

    def cp(self, out, in_, eng='dve'):
        if eng == 'act':
            return self.act(out, in_)
        self.S.op(eng, lambda e: e.tensor_copy(out=out.ap, in_=in_.ap), reads=[in_], writes=[out])

    def memset(self, out, val, eng='dve'):
        self.S.op(eng, lambda e: e.memset(out.ap, val), writes=[out])

    def rmax(self, out, in_, eng='dve'):
        self.S.op(eng, lambda e: e.tensor_reduce(out=out.ap, in_=in_.ap, axis=AX.X, op=ALU.max),
                  reads=[in_], writes=[out])

    def dma(self, out, in_, q='sp'):
        self.S.dma(q, out, in_)

    def dump(self, name, view, shape):
        if name not in self.debug:
            return
        d = self.S.dram('dbg_' + name, list(shape), view.ap.dtype, 'ExternalOutput')
        self.dumps[name] = d
        self.dma(d.ap(), view)

    def cfv(self, name, rows=128, cols=None):
        o, w = self.cfo[name]
        cols = w if cols is None else cols
        return self.cf[:rows, o:o + cols]

    def cbv(self, name, rows=128, cols=None):
        o, w = self.cbo[name]
        cols = w if cols is None else cols
        return self.cb[:rows, o:o + cols]

    def ws_init(self):
        S = self.S
        self.wlist = tile_blocks()
        self.nbt = len(self.wlist)
        self.wblocks = self.wlist * self.ntiles
        self.wst = [S.sbuf(f'wst{i}', [128, 8, 256], F32) for i in range(2)]
        self.wbf = [S.sbuf(f'wbf{i}', [128, 8, 256], BF16) for i in range(2)]
        self.wx4 = [S.sbuf(f'wrx{i}', [128, 8, 256], BF16, at=self.wst[i // 2].off + 4096 * (i % 2)) for i in range(4)]
        self.wring = self.wbf + self.wx4
        self.wscr = [S.dram(f'wscr{j}', [128, 8, 256], BF16, 'Internal') for j in range(self.nbt)] if self.ntiles > 1 else None
        self.w_loaded = 0
        self.w_cast = 0
        self.w_next = 0
        self.w_ring_started = False

    def _w_load(self, i):
        name, k0, nk, parts = self.wblocks[i]
        st = self.wst[i % 2]
        W = self.I[name]
        c = 0
        for (c0, n) in parts:
            src = View(W, W.t[k0 * 128:(k0 + nk) * 128, c0:c0 + n].rearrange('(k p) c -> p k c', p=128))
            self.dma(st[:, 0:nk, c:c + n], src, q='sp')
            c += n

    def _w_castop(self, i):
        name, k0, nk, parts = self.wblocks[i]
        n = sum(p[1] for p in parts)
        eng = 'dve' if (i % 4) != 3 else 'act'
        if name in ('w_proj_a', 'w_proj_b'):
            for k in range(nk):
                gcol = self.cwm[:, k0 + k, 5:6] if name == 'w_proj_a' else self.sng[:, k0 + k:k0 + k + 1]
                self.ts(self.wbf[i % 2][:, k, 0:n], self.wst[i % 2][:, k, 0:n], gcol, ALU.mult)
        else:
            self.cp(self.wbf[i % 2][:, 0:nk, 0:n], self.wst[i % 2][:, 0:nk, 0:n], eng)
        if self.wscr is not None:
            self.dma(self.wscr[i][:, 0:nk, 0:n], self.wbf[i % 2][:, 0:nk, 0:n], q='sp')

    def _w_ringload(self, i):
        name, k0, nk, parts = self.wblocks[i]
        n = sum(p[1] for p in parts)
        dst = self.wring[(i - self.nbt) % 6]
        self.dma(dst[:, 0:nk, 0:n], self.wscr[i % self.nbt][:, 0:nk, 0:n], q='sp')

    def wnext(self):
        i = self.w_next
        nb = len(self.wblocks)
        self.w_next += 1
        if i < self.nbt:
            lim = self.nbt
            while self.w_loaded < min(lim, i + 2):
                self._w_load(self.w_loaded)
                self.w_loaded += 1
            while self.w_cast < min(lim, i + 2):
                self._w_castop(self.w_cast)
                self.w_cast += 1
            while self.w_loaded < min(lim, i + 3):
                self._w_load(self.w_loaded)
                self.w_loaded += 1
            return self.wbf[i % 2], self.wblocks[i]
        if not self.w_ring_started:
            self.w_ring_started = True
            self.S.alias_phase(self.wst, self.wx4)
            self.w_loaded = self.nbt
        while self.w_loaded < min(nb, i + 6):
            self._w_ringload(self.w_loaded)
            self.w_loaded += 1
        return self.wring[(i - self.nbt) % 6], self.wblocks[i]

    def build(self):
        S = self.S
        sb = S.sbuf
        ncf, ncb = self.cf_np.shape[1], self.cb_np.shape[1]
        self.cf = sb('cf', [128, ncf], F32)
        self.cb = sb('cb', [128, ncb], BF16)
        self.lnc = sb('lnc', [128, 2, D], F32)
        self.bif_b = sb('bif_b', [128, 8], F32)
        self.dtb_b = sb('dtb_b', [128, 32], F32)
        self.A_b = sb('A_b', [128, 32], F32)
        self.D_b = sb('D_b', [128, 32], F32)
        self.Dfm = sb('Dfm', [128, 16], F32)
        self.cwm = sb('cwm', [128, 8, 6], F32)
        self.cws = sb('cws', [128, 24, 5], F32)
        self.sng = sb('sng', [128, 16], F32)
        self.cwf = sb('cwf', [128, 44, 4], F32)
        self.ws_init()
        self.xr = [sb(f'xr{i}', [128, D], F32) for i in range(4)]
        self.zs = [None] * 4
        self.xnT = sb('xnT', [128, 8, 512], BF16)
        self.hgT = sb('hgT', [128, 8, 512], BF16)
        self.ygT = sb('ygT', [128, 16, 512], BF16)
        self.Cf = sb('Cf', [128, 4, 256], F32); self.Cb = sb('Cb', [128, 4, 256], BF16)
        self.nf = sb('nf', [128, 4], F32); self.nb = sb('nb', [128, 4], BF16)
        self.m_b = sb('m_b', [128, 4], F32)
        self.STf = sb('STf', [128, 2048], F32); self.STb = sb('STb', [128, 2048], BF16)
        self.cq = sb('cq', [128, 8, 3], F32); self.cx = sb('cx', [128, 24, 3], F32)
        self.cff = sb('cff', [128, 44, 2], F32)
        self.scar = sb('scar', [128, 44 * 16 * 2], F32)
        self.gat = sb('gat', [128, 4, 8], F32)
        self.dta = sb('dta', [128, 4, 64], F32)
        self.ifdt = sb('ifdt', [128, 4, 40], F32)
        self.sm = [sb(f'sm{i}', [128, 32], F32) for i in range(16)]
        self.xb16 = sb('xb16', [128, D], BF16)
        self.cst = sb('cst', [128, 8], F32)
        self.pn_st = sb('pn_st', [128, 128], F32)
        R0 = S.sb_off
        o = R0
        def at(name, shape, dt):
            nonlocal o
            b = sb(name, shape, dt, at=o)
            o += b.nbytes
            return b
        self.cE = [at(f'cE{i}', [128, 520], F32) for i in range(2)]
        self.cacc = [at(f'cacc{i}', [128, 512], F32) for i in range(3)]
        self.cth = [at(f'cth{i}', [128, 512], F32) for i in range(2)]
        self.cacc2 = [at(f'cacc2_{i}', [128, 512], F32) for i in range(2)]
        e1 = o
        o = R0
        self.R1 = at('R1', [128, 4, 128], F32); self.R2 = at('R2', [128, 4, 128], F32)
        self.R3 = at('R3', [128, 4, 128], F32); self.wT = at('wT', [128, 4, 128], F32)
        self.ST = at('ST', [128, 4, 128], BF16); self.kTM = at('kTM', [128, 4, 128], BF16)
        self.hh = at('hh', [128, 4, 256], F32); self.vw = at('vw', [128, 4, 256], BF16)
        self.hgTM = at('hgTM', [128, D], BF16)
        e2 = o
        o = R0
        self.xdt = at('xdt', [128, 2048], BF16); self.xsD = at('xsD', [128, 2048], BF16)
        self.wx = at('wx', [128, 512], BF16); self.BTM = at('BTM', [128, 512], BF16)
        self.LT = at('LT', [128, 8, 128], BF16); self.MTt = at('MTt', [128, 8, 128], BF16)
        self.t1 = at('t1', [128, 512], F32)
        self.yz = at('yz', [128, 2048], BF16)
        self.ynTM = at('ynTM', [128, 2048], BF16)
        e3 = o
        F0 = max(e1, e2, e3)
        conv_end = F0
        o = F0
        self.qkT = at('qkT', [128, 8, 512], BF16)
        self.v = [at(f'v{i}', [128, D], BF16) for i in range(4)]
        self.oth = [at(f'oth{i}', [128, D], BF16) for i in range(4)]
        a1_end = o
        o = F0
        self.xbcT = at('xbcT', [128, 24, 512], BF16)
        for i in range(2):
            self.zs[i] = at(f'zs{i}', [128, 2048], BF16)
        self.LA = at('LA', [128, 8, 128], F32)
        a2_end = o
        o = F0
        self.gth = at('gth', [128, 16, 512], BF16)
        self.mixT = at('mixT', [128, 8, 512], BF16)
        self.hffT = at('hffT', [128, 22, 512], BF16)
        b_end = o
        S.sb_off = max(a1_end, a2_end, b_end)
        for i in range(2, 4):
            self.zs[i] = sb(f'zs{i}', [128, 2048], BF16)
        self.arenas = [(self.xr[2].off, 2 * self.xr[2].nbytes), (self.zs[2].off, 2 * self.zs[2].nbytes)]
        a0, a1 = self.arenas[0][0], self.arenas[1][0]
        self.C0b = [sb(f'C0b{i}', [128, 4, 256], F32, at=a0 + 4096 * i) for i in range(2)]
        self.C0b16 = [sb(f'C0b16_{i}', [128, 4, 258], BF16, at=a1 + 2080 * i) for i in range(2)]
        self.qmb = [sb(f'qmb{i}', [128, 4, 64], BF16, at=a1 + 4160 + 512 * i) for i in range(2)]
        self.kTMm = [sb(f'kTMm{i}', [128, 4, 128], BF16, at=a1 + 5184 + 1024 * i) for i in range(2)]
        self.n0T = sb('n0T', [128, 64], F32, at=a1 + 7232)
        self.n16 = sb('n16', [128, 64], BF16, at=a1 + 7488)
        self.decS = sb('decS', [128, 64], F32, at=a1 + 7616)
        self.Rm = sb('Rm', [128, 64], F32, at=a1 + 7872)
        self.S0b = [sb('S0b0', [128, 16, 128], F32, at=a0), sb('S0b1', [128, 16, 128], F32, at=self.LT.off)]
        assert self.LT.off + 8192 <= self.ynTM.off
        self.SbT = sb('SbT', [128, 2048], BF16, at=a1)
        self.wxm = sb('wxm', [128, 2048], BF16, at=a1 + 4096)
        self.ysi = sb('ysi', [128, 2048], BF16)
        self.decP = sb('decP', [128, 256], F32)
        self.Rr = sb('Rr', [128, 2, 256], F32)
        self.CTmb = [sb(f'CTmb{i}', [128, 4, 64], BF16) for i in range(2)]
        self.grpArena2 = [self.S0b[0], self.SbT, self.wxm]
        self.xsDT = sb('xsDT', [128, 8, 128], BF16, at=self.ynTM.off)
        self.LAh = [sb(f'LAh{j}', [128, 128], F32, at=self.LA.off + 512 * j) for j in range(8)]
        self.LTh = [sb(f'LTh{j}', [128, 4, 128], BF16, at=self.LT.off + 1024 * j) for j in range(2)]
        self.MTh = [[sb(f'MTh{b}_{j}', [128, 128], BF16, at=base + 256 * j) for j in range(8)]
                    for b, base in enumerate((self.MTt.off, self.ynTM.off + 2048))]
        self.grpArena = self.C0b + self.C0b16 + self.qmb + self.kTMm + [self.n0T, self.n16, self.decS, self.Rm]
        self.grpA1conv = self.cE + self.cacc + self.cth + self.cacc2
        self.grpA1rec = [self.R1, self.R2, self.R3, self.wT, self.ST, self.kTM, self.hh, self.vw, self.hgTM]
        self.grpA1fix = [self.qkT] + self.v + self.oth
        self.grpA2fix = [self.xbcT, self.zs[0], self.zs[1]] + self.LAh
        self.grpA2rec = [self.xdt, self.xsD, self.wx, self.BTM, self.t1, self.yz, self.ynTM] + self.LTh + self.MTh[0]
        self.grpB = [self.gth, self.mixT, self.hffT]
        print('SBUF used', S.sb_off, 'of', S.sb_end, 'R', R0, conv_end - R0, a1_end - R0, a2_end - R0, b_end - R0)
        self.ps = [S.psum(f'ps{i}', [128, 512], F32) for i in range(8)]
        self.setup()
        tiles = self.make_tiles()
        for tl in tiles:
            if tl.name in self.tile_sel:
                self.run_tile(tl, last=(tl.name == 'T4'))

    def make_tiles(self):
        def pch(slot, col0, c):
            ch = Chunk(slot, col0, 128, 'p', row0=128 * c)
            ch.final = (c == 15)
            return ch
        m = Chunk(0, 0, 16, 'm'); m.final = False
        tiles = [Tile('T0', 400, [m] + [pch(1 + i, 16 + 128 * i, i) for i in range(3)], [(0, 1, 400, 'p')])]
        for t in range(3):
            tiles.append(Tile(f'T{t + 1}', 512, [pch(i, 128 * i, 3 + 4 * t + i) for i in range(4)], [(0, 1, 512, 'p')]))
        sc = Chunk(1, 128, 64, 's'); sc.final = False
        tiles.append(Tile('T4', 192, [pch(0, 0, 15), sc], [(0, 1, 128, 'p'), (128, 16, 4, 's')]))
        return tiles

    def psb(self, i):
        return self.ps[i].ap().bitcast(BF16)

    def setup(self):
        I = self.I
        self.dma(self.cf.ap(), I['cf'].ap())
        self.dma(self.cb.ap(), I['cb'].ap())
        pb = lambda n: View(I[n], I[n].t.partition_broadcast(128))
        self.dma(self.bif_b.ap(), pb('b_if'))
        self.dma(self.dtb_b.ap(), pb('dt_bias'))
        self.dma(self.A_b.ap(), pb('A_log'))
        self.dma(self.D_b.ap(), pb('ssm_D'))
        self.act(self.A_b.ap(), self.A_b.ap(), AF.Exp)
        self.ts(self.A_b.ap(), self.A_b.ap(), -1.0, ALU.mult)
        D3 = self.D_b.ap().rearrange('p (g r) -> p g r', r=2)
        self.cp(self.Dfm[0:64, :], D3[0:64, :, 0], 'dve')
        self.cp(self.Dfm[64:128, :], D3[64:128, :, 1], 'dve')
        identf = self.cfv('ident')
        stg = self.cacc[0]
        def fm_params(dst, rows, G, scale_groups=None):
            R = sum(r for _, r in rows)
            for g0 in range(0, G, 4):
                gn = min(4, G - g0)
                r0 = 0
                for (nm, nr) in rows:
                    self.dma(stg[r0:r0 + nr, 0:gn * 128], I[nm][:, g0 * 128:(g0 + gn) * 128])
                    r0 += nr
                bank = self.ps[self.rot('setup', 2)]
                for g in range(gn):
                    self.tr(bank[:, g * R:(g + 1) * R], stg[0:R, g * 128:(g + 1) * 128], identf[0:R, 0:R])
                self.cp(dst[:, g0:g0 + gn, :], bank[:, 0:gn * R].rearrange('p (g r) -> p g r', r=R), 'act')
        fm_params(self.cwm, [('w_mconv', 4), ('b_mconv', 1), ('mnorm_g', 1)], 8)
        fm_params(self.cws, [('w_sconv', 4), ('b_sconv', 1)], 24)
        fm_params(self.cwf, [('w_fconv', 3), ('b_fconv', 1)], 44)
        sng3 = self.sng.ap().rearrange('p (g r) -> p g r', r=1)
        fm_params(sng3, [('snorm_g', 1)], 16)
        self.ts(self.cwm[:, :, 0:6], self.cwm[:, :, 0:6], 0.5, ALU.mult)
        self.ts(self.cws.ap(), self.cws.ap(), 0.5, ALU.mult)
        self.ts(self.cwf[:, 0:22, :], self.cwf[:, 0:22, :], 0.5, ALU.mult)
        self.memset(self.cst[:, 0:1], LN_EPS)
        self.memset(self.cst[:, 1:2], 0.5 * float(np.log(128.0)))
        self.memset(self.cst[:, 2:3], 1.0)
        self.eps_t = self.cst
        for b in (self.Cf, self.nf, self.m_b, self.STf, self.cq, self.cx, self.cff):
            self.memset(b.ap(), 0.0)
        for b in (self.Cb, self.nb, self.STb):
            self.memset(b.ap(), 0.0, 'pool')

    def kc(self, kind, n):
        if kind == 's':
            return dict(U=self.cfv('Us', 64), LS=self.cfv('LSs', 64), BO=self.cfv('BOs', 64),
                        M=self.cbv('Ms', 64), MT=self.cbv('MTs', 64))
        return dict(U=self.cfv('U', n, n), LS=self.cfv('LS', n, n), BO=self.cfv('ones', n, n),
                    M=self.cbv('M', n, n), MT=self.cbv('MT', n, n))

    def ln_rows(self, x, n, gname, bname):
        I = self.I
        st, mv, rs = self.sm[0], self.sm[1], self.sm[2]
        self.dma(self.lnc[:, 0, :], View(I[gname], I[gname].t.partition_broadcast(128)))
        self.dma(self.lnc[:, 1, :], View(I[bname], I[bname].t.partition_broadcast(128)))
        for i in range(2):
            self.S.op('dve', lambda e, i=i: e.bn_stats(out=st.t[:n, i * 6:(i + 1) * 6], in_=x.ap[:, i * 512:(i + 1) * 512]),
                      reads=[x], writes=[st])
        self.S.op('dve', lambda e: e.bn_aggr(out=mv.t[:n, 0:2], in_=st.t[:n, 0:12]), reads=[st], writes=[mv])
        self.act(rs[:n, 0:1], mv[:n, 1:2], AF.Ln, bias=self.eps_t[:n, 0:1])
        self.act(rs[:n, 0:1], rs[:n, 0:1], AF.Exp, scale=-0.5)
        self.ts(x, x, mv[:n, 0:1], ALU.subtract, rs[:n, 0:1], ALU.mult)
        self.tt(x, x, self.lnc[:n, 0, :], ALU.mult)
        self.tt(x, x, self.lnc[:n, 1, :], ALU.add)

    def to_fm(self, tl, src_of_chunk, dstT):
        identb = self.cbv('identb')
        for ch in tl.chunks:
            n = ch.n
            xb = self.xb16
            self.act(xb[:n, :], src_of_chunk(ch))
            bank = 6 + self.rot('tfm', 2)
            pv = self.psb(bank)
            for k in range(8):
                self.tr(pv[:, k * n:(k + 1) * n], xb[:n, k * 128:(k + 1) * 128], identb[:n, :n])
            self.cp(dstT[:, :, ch.col0:ch.col0 + n], pv[:, 0:8 * n].rearrange('p (k n) -> p k n', n=n), 'dve')

    def _conv_taps(self, tl, psv, W, wtab, g, carry_p, scar_view, E, acc):
        Wm = W - 1
        off = 0
        for (col0, nb, L, kind) in tl.segs:
            Ev = E[:, off:off + nb * (L + Wm)].rearrange('p (b l) -> p b l', b=nb)
            pseg = psv[:, col0:col0 + nb * L].rearrange('p (b l) -> p b l', b=nb)
            if kind == 'p':
                self.cp(Ev[:, :, 0:Wm], carry_p[:, g:g + 1, :], 'act')
            elif kind == 'm':
                self.memset(Ev[:, :, 0:Wm], 0.0, 'dve')
            else:
                self.cp(Ev[:, :, 0:Wm], scar_view[:, g, :, :], 'act')
            self.act(Ev[:, :, Wm:Wm + L], pseg)
            av = acc[:, col0:col0 + nb * L].rearrange('p (b l) -> p b l', b=nb)
            self.act(av, pseg, AF.Identity, scale=wtab[:, g, Wm:W], bias=wtab[:, g, W:W + 1])
            if kind == 's':
                self.cp(scar_view[:, g, :, :], Ev[:, :, L:L + Wm], 'act')
            else:
                self.cp(carry_p[:, g:g + 1, :], Ev[:, :, L:L + Wm], 'act')
            for j in range(Wm):
                self.stt(av, Ev[:, :, j:j + L], wtab[:, g, j:j + 1], av, ALU.mult, ALU.add)
            off += nb * (L + Wm)

    def conv_group(self, tl, psv, W, wtab, g, carry_p, scar_view, dst, final=True):
        E = self.cE[self.rot('cE', 2)]
        acc = self.cacc[self.rot('cacc', 3)]
        self._conv_taps(tl, psv, W, wtab, g, carry_p, scar_view, E, acc)
        T = tl.T
        if not final:
            return acc
        prev = getattr(self, '_conv_pending', None)

        def stage2(acc=acc, dst=dst, T=T):
            th = self.cth[self.rot('cth', 2)]
            self.act(th[:, 0:T], acc[:, 0:T], AF.Tanh)
            self.stt(dst, th[:, 0:T], 1.0, acc[:, 0:T], ALU.add, ALU.mult)
        self._conv_pending = stage2
        if prev is not None:
            prev()
        return acc

    def conv_flush(self):
        prev = getattr(self, '_conv_pending', None)
        self._conv_pending = None
        if prev is not None:
            prev()

    def carry_out(self, src, G, R, dst):
        identf = self.cfv('ident')
        for g0 in range(0, G, 4):
            gn = min(4, G - g0)
            bank = self.ps[self.rot('co', 2)]
            for g in range(gn):
                self.tr(bank[:R, g * 128:(g + 1) * 128], src[:, g0 + g, :], identf)
            stg = self.cacc[self.rot('cacc', 3)]
            self.cp(stg[:R, 0:gn * 128], bank[:R, 0:gn * 128], 'act')
            self.dma(dst[:, g0 * 128:(g0 + gn) * 128], stg[:R, 0:gn * 128])

    def scar_in(self, name, G, R):
        identf = self.cfv('ident')
        rows = 16 * R
        sv = self.scar[:, 0:G * rows].rearrange('p (g b r) -> p g b r', g=G, b=16)
        for g0 in range(0, G, 4):
            gn = min(4, G - g0)
            stg = self.cacc[self.rot('cacc', 3)]
            self.dma(stg[:rows, 0:gn * 128], self.I[name][:, g0 * 128:(g0 + gn) * 128])
            bank = self.ps[self.rot('co', 2)]
            for g in range(gn):
                self.tr(bank[:, g * rows:(g + 1) * rows], stg[:rows, g * 128:(g + 1) * 128], identf[:rows, :rows])
            self.cp(self.scar[:, g0 * rows:(g0 + gn) * rows], bank[:, 0:gn * rows], 'act')
        return sv

    def scar_out(self, name, G, R):
        rows = 16 * R
        src = self.scar[:, 0:G * rows].rearrange('p (g br) -> p g br', g=G)
        self.carry_out(src, G, rows, self.O[name].ap())

    def dense_fm(self, tl, actT, nkt_total, cb_group, kt0=0):
        Wb, (name, k0, nk, parts) = self.wnext()
        ncols = sum(p[1] for p in parts)
        T = tl.T
        for gl in range(ncols // 128):
            bank = self.ps[self.rot('mm', 4)]
            for k in range(nk):
                self.mm(bank[:, 0:T], Wb[:, k, gl * 128:(gl + 1) * 128], actT[:, k0 + k, 0:T],
                        start=(k0 + k == 0), stop=(k0 + k == nkt_total - 1))
            cb_group(gl, bank[:, 0:T])

    def dense_tm(self, tl, actT, cb_chunk):
        Wb, (name, k0, nk, parts) = self.wnext()
        ncols = sum(p[1] for p in parts)
        for ch in tl.chunks:
            bank = self.ps[self.rot('mm', 4)]
            for k in range(nk):
                self.mm(bank[:ch.n, 0:ncols], actT[:, k0 + k, ch.col0:ch.col0 + ch.n], Wb[:, k, 0:ncols],
                        start=(k == 0), stop=(k == nk - 1))
            cb_chunk(ch, bank[:ch.n, 0:ncols])

    def run_tile(self, tl, last):
        S, I, O = self.S, self.I, self.O
        T = tl.T
        isS = any(sg[3] == 's' for sg in tl.segs)
        if isS:
            S.alias_phase([self.xr[2], self.xr[3], self.zs[2], self.zs[3]], self.grpArena + self.grpArena2)
        for ch in tl.chunks:
            src = {'s': I['xs'].ap(), 'm': I['meta'].ap()}.get(ch.kind)
            if src is None:
                src = I['xp'][ch.row0:ch.row0 + ch.n, :]
            self.dma(self.xr[ch.slot][:ch.n, :], src)
            self.ln_rows(self.xr[ch.slot][:ch.n, :], ch.n, 'ln0_g', 'ln0_b')
        self.to_fm(tl, lambda ch: self.xr[ch.slot][:ch.n, :], self.xnT)
        for ch in tl.chunks:
            self.ts(self.xr[ch.slot][:ch.n, :], self.xr[ch.slot][:ch.n, :], ALPHA, ALU.mult)
        self.dump('xnT_' + tl.name, self.xnT[:, :, 0:T], [128, 8, T])
        if self.debug.get('stop') == 'p0':
            return
        S.alias_phase(self.grpA2fix + self.grpA2rec + self.grpB + self.grpA1rec, self.grpA1conv + self.grpA1fix)
        sq = self.scar_in('s_mconv', 8, 3) if isS else None
        for blk in range(4):
            def cbq(gl, psv, blk=blk):
                g = 2 * blk + gl
                self.conv_group(tl, psv, 4, self.cwm, g, self.cq, sq, self.qkT[:, g, 0:T])
            self.dense_fm(tl, self.xnT, 8, cbq)
        self.conv_flush()
        if isS:
            self.scar_out('o_mconv', 8, 3)
        if last:
            self.carry_out(self.cq.ap(), 8, 3, O['p_mconv'].ap())
        for blk in range(4):
            self.dense_tm(tl, self.xnT, lambda ch, psv, blk=blk: self.act(self.v[ch.slot][:ch.n, 256 * blk:256 * blk + 256], psv))
        for blk in range(4):
            self.dense_tm(tl, self.xnT, lambda ch, psv, blk=blk: self.act(self.oth[ch.slot][:ch.n, 256 * blk:256 * blk + 256], psv, AF.Tanh, scale=0.5))
        self.dense_tm(tl, self.xnT, lambda ch, psv: self.cp(self.ifdt[:ch.n, ch.slot, :], psv, 'dve'))
        for ch in tl.chunks:
            n, s = ch.n, ch.slot
            gi = self.gat[:n, s, 0:8]
            self.tt(gi, self.ifdt[:n, s, 0:8], self.bif_b[:n, :], ALU.add)
            e1 = self.sm[3]
            self.act(e1[:n, 0:4], self.gat[:n, s, 4:8], AF.Exp, scale=-1.0)
            self.act(e1[:n, 0:4], e1[:n, 0:4], AF.Ln, bias=self.cst[:n, 2:3])
            self.ts(self.gat[:n, s, 4:8], e1[:n, 0:4], -1.0, ALU.mult)
            d1 = self.sm[4]
            self.tt(d1[:n, 0:32], self.ifdt[:n, s, 8:40], self.dtb_b[:n, :], ALU.add)
            self.act(d1[:n, 0:32], d1[:n, 0:32], AF.Exp)
            self.act(self.dta[:n, s, 0:32], d1[:n, 0:32], AF.Ln, bias=self.cst[:n, 2:3])
            self.tt(self.dta[:n, s, 32:64], self.dta[:n, s, 0:32], self.A_b[:n, :], ALU.mult)
        self.dump('qkT_' + tl.name, self.qkT[:, :, 0:T], [128, 8, T])
        self.dump('gat_' + tl.name, self.gat.ap(), [128, 4, 8])
        self.dump('dta_' + tl.name, self.dta.ap(), [128, 4, 64])
        if self.debug.get('stop') == 'a1':
            return
        S.alias_phase(self.grpA1conv, self.grpA1rec)
        for ch in tl.chunks:
            self.mlstm_chunk(tl, ch, last)
        self.dump('hgT_' + tl.name, self.hgT[:, :, 0:T], [128, 8, T])
        if self.debug.get('stop') == 'mlstm':
            return
        S.alias_phase(self.grpA1rec + self.grpA1fix, self.grpA1conv + self.grpA2fix)
        for blk in range(8):
            def cbz(ch, psv, blk=blk):
                n = ch.n
                zc = self.cacc[self.rot('cacc', 3)]
                th = self.cth[self.rot('cth', 2)]
                self.cp(zc[:n, 0:256], psv, 'act')
                self.act(th[:n, 0:256], psv, AF.Tanh, scale=0.5)
                self.stt(self.zs[ch.slot][:n, 256 * blk:256 * blk + 256], th[:n, 0:256], 1.0, zc[:n, 0:256], ALU.add, ALU.mult)
            self.dense_tm(tl, self.xnT, cbz)
        if self.debug.get('stop') == 'a2z':
            return
        sx = self.scar_in('s_sconv', 24, 3) if isS else None
        for blk in range(12):
            def cbx(gl, psv, blk=blk):
                g = 2 * blk + gl
                self.conv_group(tl, psv, 4, self.cws, g, self.cx, sx, self.xbcT[:, g, 0:T])
            self.dense_fm(tl, self.xnT, 8, cbx)
        self.conv_flush()
        if isS:
            self.scar_out('o_sconv', 24, 3)
        if last:
            self.carry_out(self.cx.ap(), 24, 3, O['p_sconv'].ap())
        self.dump('xbcT_' + tl.name, self.xbcT[:, :, 0:T], [128, 24, T])
        if self.debug.get('stop') == 'a2':
            return
        S.alias_phase(self.grpA1conv, self.grpA2rec)
        for ch in tl.chunks:
            self.ssd_chunk(tl, ch, last)
        self.dump('ygT_' + tl.name, self.ygT[:, :, 0:T], [128, 16, T])
        if self.debug.get('stop') == 'ssd':
            return
        S.alias_phase(self.grpA2rec + self.grpA2fix, self.grpA1conv + self.grpB)
        for blk in range(8):
            def cbg(gl, psv, blk=blk):
                self.act(self.gth[:, 2 * blk + gl, 0:T], psv, AF.Tanh, scale=0.5)
            self.dense_fm(tl, self.xnT, 8, cbg)
        for j in range(4):
            Wb, (name, k0, nk, parts) = self.wnext()
            banksA = [self.ps[0], self.ps[1]]
            banksB = [self.ps[2], self.ps[3]]
            for gl in range(2):
                for k in range(8):
                    self.mm(banksA[gl][:, 0:T], Wb[:, k, gl * 128:(gl + 1) * 128], self.hgT[:, k, 0:T], start=(k == 0), stop=(k == 7))
            for half in range(2):
                Wb, (name, k0, nk, parts) = self.wnext()
                for gl in range(2):
                    for k in range(8):
                        kk = 8 * half + k
                        self.mm(banksB[gl][:, 0:T], Wb[:, k, gl * 128:(gl + 1) * 128], self.ygT[:, kk, 0:T], start=(kk == 0), stop=(kk == 15))
            for gl in range(2):
                g = 2 * j + gl
                m1 = self.cacc[self.rot('cacc', 3)]
                m2 = self.cacc2[self.rot('cacc2', 2)]
                self.stt(m1[:, 0:T], self.gth[:, g, 0:T], 1.0, banksA[gl][:, 0:T], ALU.add, ALU.mult)
                self.stt(m2[:, 0:T], self.gth[:, 8 + g, 0:T], 1.0, banksB[gl][:, 0:T], ALU.add, ALU.mult)
                self.tt(self.mixT[:, g, 0:T], m1[:, 0:T], m2[:, 0:T], ALU.add)
        self.rr['mm'] = 0
        for blk in range(4):
            def cbo(ch, psv, blk=blk):
                xv = self.xr[ch.slot][:ch.n, 256 * blk:256 * blk + 256]
                self.stt(xv, psv, 0.5, xv, ALU.mult, ALU.add)
            self.dense_tm(tl, self.mixT, cbo)
        for ch in tl.chunks:
            self.ln_rows(self.xr[ch.slot][:ch.n, :], ch.n, 'ln1_g', 'ln1_b')
        self.dump('x1_' + tl.name, self.xr[0].ap(), [128, D])
        self.to_fm(tl, lambda ch: self.xr[ch.slot][:ch.n, :], self.xnT)
        for ch in tl.chunks:
            self.ts(self.xr[ch.slot][:ch.n, :], self.xr[ch.slot][:ch.n, :], ALPHA, ALU.mult)
        sf = self.scar_in('s_fconv', 44, 2) if isS else None
        for j in range(11):
            accs = {}
            def cbua(gl, psv, j=j):
                g = 2 * j + gl
                accs[gl] = self.conv_group(tl, psv, 3, self.cwf, g, self.cff, sf, None, final=False)
            self.dense_fm(tl, self.xnT, 8, cbua)
            def cbub(gl, psv, j=j):
                g = 2 * j + gl
                E = self.cE[self.rot('cE', 2)]
                accb = self.cacc2[self.rot('cacc2', 2)]
                self._conv_taps(tl, psv, 3, self.cwf, 22 + g, self.cff, sf, E, accb)
                th = self.cth[self.rot('cth', 2)]
                acca = accs[gl]
                self.act(th[:, 0:T], acca[:, 0:T], AF.Tanh)
                self.stt(th[:, 0:T], th[:, 0:T], 1.0, acca[:, 0:T], ALU.add, ALU.mult)
                self.tt(self.hffT[:, g, 0:T], th[:, 0:T], accb[:, 0:T], ALU.mult)
            self.dense_fm(tl, self.xnT, 8, cbub)
        if isS:
            self.scar_out('o_fconv', 44, 2)
        if last:
            self.carry_out(self.cff.ap(), 44, 2, O['p_fconv'].ap())
        self.dump('hffT_' + tl.name, self.hffT[:, :, 0:T], [128, 22, T])
        for blk in range(4):
            banks = {ch.slot: self.ps[ch.slot] for ch in tl.chunks}
            for (k0, nk) in ((0, 8), (8, 8), (16, 6)):
                Wb, meta = self.wnext()
                for ch in tl.chunks:
                    for k in range(nk):
                        self.mm(banks[ch.slot][:ch.n, 0:256], self.hffT[:, k0 + k, ch.col0:ch.col0 + ch.n], Wb[:, k, 0:256],
                                start=(k0 + k == 0), stop=(k0 + k == 21))
            for ch in tl.chunks:
                xv = self.xr[ch.slot][:ch.n, 256 * blk:256 * blk + 256]
                self.tt(xv, banks[ch.slot][:ch.n, 0:256], xv, ALU.add)
        for ch in tl.chunks:
            self.ln_rows(self.xr[ch.slot][:ch.n, :], ch.n, 'ln2_g', 'ln2_b')
            if ch.kind == 'p':
                self.dma(O['y_p'][ch.row0:ch.row0 + ch.n, :], self.xr[ch.slot][:ch.n, :])
            elif ch.kind == 's':
                self.dma(O['y_s'].ap(), self.xr[ch.slot][:ch.n, :])

    def mlstm_chunk(self, tl, ch, last):
        I, O = self.I, self.O
        n, s, c0, kind = ch.n, ch.slot, ch.col0, ch.kind
        kc = self.kc(kind, n)
        U, LS, M, MT = kc['U'], kc['LS'], kc['M'], kc['MT']
        identf, onesf = self.cfv('ident', n, n), self.cfv('ones', n, n)
        identb, onesb = self.cbv('identb', n, n), self.cbv('onesb', n, n)
        ps = self.ps
        cols = slice(c0, c0 + n)
        ig = self.gat[:n, s, 0:4]
        lf = self.gat[:n, s, 4:8]
        sm = self.sm
        bt, mi, bm, mt, wi, emt, negm, den, rden, wi2 = (sm[i] for i in range(5, 15))
        R1, R2, R3, wT, ST, kTM, hh, vw = self.R1, self.R2, self.R3, self.wT, self.ST, self.kTM, self.hh, self.vw
        self.mm(ps[0][:n, 0:4], U, lf)
        for h in range(4):
            self.ts(R1[:n, h, :n], LS, self.gat[:n, s, 4 + h:5 + h], ALU.mult)
            self.ts(R2[:n, h, :n], identf, self.gat[:n, s, h:h + 1], ALU.mult, eng='pool')
        for h in range(4):
            self.mm(ps[1][:n, h * 128:h * 128 + n], U, R1[:n, h, :n], start=True, stop=False)
            self.mm(ps[1][:n, h * 128:h * 128 + n], onesf, R2[:n, h, :n], start=False, stop=False)
            self.mm(ps[1][:n, h * 128:h * 128 + n], identb, M, start=False, stop=True)
        self.rmax(mi[:n, 0:4], ps[1][:n, :].rearrange('p (h t) -> p h t', h=4)[:, :, 0:n])
        if kind == 's':
            m0s = sm[15]
            self.dma(m0s[:16, 0:4], I['s_m'].ap())
            self.mm(ps[0][:n, 4:8], self.cfv('BMT', 16), m0s[:16, 0:4])
            m0v = ps[0][:n, 4:8]
        else:
            m0v = self.m_b[:n, :]
        self.cp(bt[:n, 0:4], ps[0][:n, 0:4], 'act')
        self.tt(bm[:n, 0:4], bt[:n, 0:4], m0v, ALU.add)
        self.tt(mt[:n, 0:4], bm[:n, 0:4], mi[:n, 0:4], ALU.max)
        self.tt(bm[:n, 0:4], bm[:n, 0:4], mt[:n, 0:4], ALU.subtract)
        self.act(wi[:n, 0:4], bm[:n, 0:4], AF.Exp)
        self.act(emt[:n, 0:4], mt[:n, 0:4], AF.Exp, scale=-1.0, bias=self.cst[:n, 1:2])
        self.ts(negm[:n, 0:4], mt[:n, 0:4], -1.0, ALU.mult)
        for h in range(4):
            self.ts(R3[:n, h, :n], identf, negm[:n, h:h + 1], ALU.mult, eng='pool')
        for h in range(4):
            o = ps[2][:n, h * 128:h * 128 + n]
            self.mm(o, R1[:n, h, :n], U, start=True, stop=False)
            self.mm(o, R2[:n, h, :n], onesf, start=False, stop=False)
            self.mm(o, onesf, R3[:n, h, :n], start=False, stop=False)
            self.mm(o, identb, MT, start=False, stop=True)
        ps2v = ps[2][:n, :].rearrange('p (h t) -> p h t', h=4)[:, :, 0:n]
        self.act(wT[:n, :, :n], ps2v, AF.Exp)
        for h in range(4):
            self.mm(ps[1][:n, h * 128:h * 128 + n], self.qkT[:, 4 + h, cols], self.qkT[:, h, cols])
        ps1v = ps[1][:n, :].rearrange('p (h t) -> p h t', h=4)[:, :, 0:n]
        self.tt(ST[:n, :, :n], ps1v, wT[:n, :, :n], ALU.mult)
        pb7 = self.psb(7)
        for h in range(4):
            self.tr(pb7[:n, h * 128:(h + 1) * 128], self.qkT[:, 4 + h, cols], self.cbv('identb'))
        self.cp(kTM[:n, :, :], pb7[:n, 0:512].rearrange('p (h d) -> p h d', h=4), 'act')
        if kind == 's':
            self.mlstm_sample_states(tl, ch, wi, wT, kTM, mt)
        for h in range(4):
            self.mm(ps[3 + h // 2][:n, (h % 2) * 256:(h % 2) * 256 + 256], ST[:n, h, :n], self.v[s][:n, h * 256:(h + 1) * 256])
        for h in range(4):
            self.mm(ps[0][:n, 8 + h:9 + h], ST[:n, h, :n], onesb[:, 0:1])
        if kind != 's':
            for h in range(4):
                self.mm(ps[5 + h // 2][:n, (h % 2) * 256:(h % 2) * 256 + 256], self.qkT[:, h, cols], self.Cb[:, h, :])
            for h in range(4):
                self.mm(ps[0][:n, 12 + h:13 + h], self.qkT[:, h, cols], self.nb[:, h:h + 1])
        dint = ps[0][:n, 12:16] if kind != 's' else sm[12][:n, 0:4]
        self.tt(den[:n, 0:4], dint, wi[:n, 0:4], ALU.mult)
        self.tt(den[:n, 0:4], den[:n, 0:4], ps[0][:n, 8:12], ALU.add)
        self.ts(wi2[:n, 0:4], den[:n, 0:4], -1.0, ALU.mult)
        self.tt(den[:n, 0:4], den[:n, 0:4], wi2[:n, 0:4], ALU.max)
        self.tt(den[:n, 0:4], den[:n, 0:4], emt[:n, 0:4], ALU.max)
        self.S.op('dve', lambda e: e.reciprocal(out=rden.t[:n, 0:4], in_=den.t[:n, 0:4]), reads=[den], writes=[rden])
        self.tt(wi2[:n, 0:4], wi[:n, 0:4], rden[:n, 0:4], ALU.mult)
        for h in range(4):
            self.act(hh[:n, h, :], ps[3 + h // 2][:n, (h % 2) * 256:(h % 2) * 256 + 256], AF.Copy, scale=rden[:n, h:h + 1])
            if kind != 's':
                iv = ps[5 + h // 2][:n, (h % 2) * 256:(h % 2) * 256 + 256]
            else:
                iv = (R1 if h < 2 else R2)[:n, :, :].rearrange('p a b -> p (a b)')[:, (h % 2) * 256:(h % 2) * 256 + 256]
            self.stt(hh[:n, h, :], iv, wi2[:n, h:h + 1], hh[:n, h, :], ALU.mult, ALU.add)
        st, mv, rs = sm[0], sm[1], sm[2]
        for h in range(4):
            self.S.op('dve', lambda e, h=h: e.bn_stats(out=st.t[:n, h * 6:(h + 1) * 6], in_=hh.t[:n, h, :]), reads=[hh], writes=[st])
        for h in range(4):
            self.S.op('dve', lambda e, h=h: e.bn_aggr(out=mv.t[:n, 2 * h:2 * h + 2], in_=st.t[:n, h * 6:(h + 1) * 6]), reads=[st], writes=[mv])
        mvv = mv[:n, 0:8].rearrange('p (h t) -> p h t', t=2)
        self.act(rs[:n, 0:4], mvv[:, :, 1], AF.Ln, bias=self.cst[:n, 0:1])
        self.act(rs[:n, 0:4], rs[:n, 0:4], AF.Exp, scale=-0.5)
        for h in range(4):
            self.ts(hh[:n, h, :], hh[:n, h, :], mv[:n, 2 * h:2 * h + 1], ALU.subtract, rs[:n, h:h + 1], ALU.mult)
            self.stt(self.hgTM[:n, h * 256:(h + 1) * 256], self.oth[s][:n, h * 256:(h + 1) * 256], 1.0, hh[:n, h, :], ALU.add, ALU.mult)
        for k in range(8):
            self.tr(pb7[:, k * n:(k + 1) * n], self.hgTM[:n, k * 128:(k + 1) * 128], identb)
        self.cp(self.hgT[:, :, cols], pb7[:, 0:8 * n].rearrange('p (k t) -> p k t', k=8), 'dve')
        if kind != 's':
            wl16 = sm[15]
            self.cp(wl16.ap().bitcast(BF16)[:n, 0:4], wT[:n, :, n - 1], 'act')
            for h in range(4):
                self.ts(vw[:n, h, :], self.v[s][:n, h * 256:(h + 1) * 256], wT[:n, h, n - 1:n], ALU.mult)
            for h in range(4):
                self.mm(ps[5 + h // 2][:, (h % 2) * 256:(h % 2) * 256 + 256], kTM[:n, h, :], vw[:n, h, :])
            for h in range(4):
                self.mm(ps[0][:, 16 + h:17 + h], kTM[:n, h, :], wl16.ap().bitcast(BF16)[:n, h:h + 1])
            SEL = self.cfv('SELp' if n == 128 else 'SELm', n)
            self.mm(ps[0][:, 32:36], SEL, wi[:n, 0:4])
            self.mm(ps[0][:, 36:40], SEL, mt[:n, 0:4])
            dec = sm[3]
            self.cp(dec[:, 0:8], ps[0][:, 32:40], 'act')
            for h in range(4):
                self.stt(self.Cf[:, h, :], self.Cf[:, h, :], dec[:, h:h + 1], ps[5 + h // 2][:, (h % 2) * 256:(h % 2) * 256 + 256], ALU.mult, ALU.add)
            self.tt(self.nf.ap(), self.nf.ap(), dec[:, 0:4], ALU.mult)
            self.tt(self.nf.ap(), self.nf.ap(), ps[0][:, 16:20], ALU.add)
            self.cp(self.m_b.ap(), dec[:, 4:8], 'dve')
            self.cp(self.Cb.ap(), self.Cf.ap(), 'act')
            self.cp(self.nb.ap(), self.nf.ap(), 'pool')
            if ch.final:
                self.dma(O['p_C'].ap().rearrange('h d e -> d h e'), self.Cf.ap())
                identf128 = self.cfv('ident')
                self.tr(ps[0][:4, 128:256], self.nf.ap(), identf128)
                self.cp(self.pn_st[:4, :], ps[0][:4, 128:256], 'act')
                self.dma(O['p_n'].ap(), self.pn_st[:4, :])
                self.dma(O['p_m'].ap(), self.m_b[0:1, :])

    def mlstm_sample_states(self, tl, ch, wi, wT, kTM, mt):
        I, O, ps, sm = self.I, self.O, self.ps, self.sm
        n, s = 64, ch.slot
        identf = self.cfv('ident')
        RS, BM = self.cfv('RS', 64), self.cfv('BM', 64)
        CM = self.cbv('CM').rearrange('p (b j) -> p b j', b=16)
        vw = self.vw
        wl = sm[3]
        tmp = self.R3
        self.tt(tmp[:n, :, 0:64], wT[:n, :, 0:64], View(self.cf, self.cfv('LSEL', 64).ap.unsqueeze(1).to_broadcast([64, 4, 64])), ALU.mult)
        self.S.op('dve', lambda e: e.tensor_reduce(out=wl.t[:n, 0:4], in_=tmp.t[:n, :, 0:64], axis=AX.X, op=ALU.add), reads=[tmp], writes=[wl])
        wl16 = sm[4].ap().bitcast(BF16)
        self.cp(wl16[:n, 0:4], wl[:n, 0:4], 'act')
        for h in range(4):
            self.ts(vw[:n, h, :], self.v[s][:n, h * 256:(h + 1) * 256], wl[:n, h:h + 1], ALU.mult)
        Rm3 = self.Rm[:n, :].rearrange('p (b h) -> p b h', h=4)
        self.tt(Rm3, View(wi, wi.t[:n, 0:4].unsqueeze(1).to_broadcast([n, 16, 4])),
                View(self.cf, RS.ap.unsqueeze(2).to_broadcast([n, 16, 4])), ALU.mult)
        self.mm(ps[0][:, 64:128], self.cfv('ones', 64), self.Rm[:n, :])
        self.cp(self.decS.ap(), ps[0][:, 64:128], 'act')
        self.mm(ps[0][:16, 40:44], RS, mt[:n, 0:4])
        mo = sm[15]
        self.cp(mo[:16, 8:12], ps[0][:16, 40:44], 'act')
        self.dma(O['o_m'].ap(), mo[:16, 8:12])
        stg = self.pn_st
        self.dma(stg[:64, :], I['s_n'].ap())
        self.tr(ps[0][:, 192:256], stg[:64, :], identf[:64, :64])
        self.cp(self.n0T.ap(), ps[0][:, 192:256], 'act')
        self.cp(self.n16.ap(), self.n0T.ap(), 'pool')
        kTMflat = kTM[:n, :, :].rearrange('p h d -> p (h d)')
        for b in range(16):
            i = b % 2
            C0, C16, qm, km = self.C0b[i], self.C0b16[i], self.qmb[i], self.kTMm[i]
            self.dma(C0.ap(), I['s_C'][b].rearrange('h d e -> d h e'))
            self.cp(C16[:, :, 0:256], C0.ap(), 'act')
            self.cp(C16[:, :, 256:257], self.n16[:, 4 * b:4 * b + 4].rearrange('p (h o) -> p h o', o=1), 'dve')
            self.tt(qm.ap(), self.qkT[:, 0:4, ch.col0:ch.col0 + 64], View(self.cb, CM.ap[:, b, :].unsqueeze(1).to_broadcast([128, 4, 64])), ALU.mult)
            self.ts(km[:n, :, :].rearrange('p h d -> p (h d)'), kTMflat, BM[:, b:b + 1], ALU.mult)
            for h in range(4):
                self.mm(ps[3 + h][:n, 0:257], qm[:, h, :], C16[:, h, 0:257], start=(b == 0), stop=(b == 15))
            for h in range(4):
                self.mm(ps[1 + h // 2][:, (h % 2) * 256:(h % 2) * 256 + 256], km[:n, h, :], vw[:n, h, :])
            for h in range(4):
                self.mm(ps[0][:, 128 + 4 * b + h:129 + 4 * b + h], km[:n, h, :], wl16[:n, h:h + 1])
            for h in range(4):
                self.stt(C0[:, h, :], C0[:, h, :], self.decS[:, 4 * b + h:4 * b + h + 1],
                         ps[1 + h // 2][:, (h % 2) * 256:(h % 2) * 256 + 256], ALU.mult, ALU.add)
            self.dma(O['o_C'][b].rearrange('h d e -> d h e'), C0.ap())
        for h in range(4):
            dst = (self.R1 if h < 2 else self.R2)[:n, :, :].rearrange('p a b -> p (a b)')[:, (h % 2) * 256:(h % 2) * 256 + 256]
            self.cp(dst, ps[3 + h][:n, 0:256], 'act')
            self.cp(sm[12][:n, h:h + 1], ps[3 + h][:n, 256:257], 'act')
        self.tt(self.n0T.ap(), self.n0T.ap(), self.decS.ap(), ALU.mult)
        self.tt(self.n0T.ap(), self.n0T.ap(), ps[0][:, 128:192], ALU.add)
        self.tr(ps[0][:64, 256:384], self.n0T.ap(), identf)
        self.cp(stg[:64, :], ps[0][:64, 256:384], 'act')
        self.dma(O['o_n'].ap(), stg[:64, :])

    def ssd_chunk(self, tl, ch, last):
        I, O, ps, sm = self.I, self.O, self.ps, self.sm
        n, s, c0, kind = ch.n, ch.slot, ch.col0, ch.kind
        kc = self.kc(kind, n)
        U, LS, BO, MT = kc['U'], kc['LS'], kc['BO'], kc['MT']
        identb = self.cbv('identb')
        cols = slice(c0, c0 + n)
        dt = self.dta[:n, s, 0:32]
        a = self.dta[:n, s, 32:64]
        btsb, dect, wend, decS = sm[5], sm[6], sm[7], sm[8]
        self.mm(ps[0][:n, 0:32], U, a)
        self.mm(ps[0][:n, 32:64], BO, a)
        self.cp(btsb[:n, 0:32], ps[0][:n, 0:32], 'act')
        self.act(dect[:n, 0:32], ps[0][:n, 0:32], AF.Exp)
        self.tt(wend[:n, 0:32], ps[0][:n, 32:64], btsb[:n, 0:32], ALU.subtract)
        self.act(wend[:n, 0:32], wend[:n, 0:32], AF.Exp)
        if kind != 's':
            self.mm(ps[0][:, 64:96], self.cfv('ones', n), a)
            self.act(decS[:, 0:32], ps[0][:, 64:96], AF.Exp)
        S = self.S
        pb6, pb7 = self.psb(6), self.psb(7)
        xsTM = self.xdt
        for g in range(16):
            pv = pb6 if g < 8 else pb7
            self.tr(pv[:n, (g % 8) * 128:(g % 8 + 1) * 128], self.xbcT[:, g, cols], identb)
        self.act(xsTM[:n, 0:1024], pb6[:n, 0:1024])
        self.act(xsTM[:n, 1024:2048], pb7[:n, 0:1024])
        S.alias_phase([self.ynTM], [self.xsDT])
        for half in range(2):
            for g in range(8):
                self.ts(self.xsDT[:, g, 0:n], self.xbcT[:, 8 * half + g, cols], self.Dfm[:, 8 * half + g:8 * half + g + 1], ALU.mult)
            pv = pb6 if half == 0 else pb7
            for g in range(8):
                self.tr(pv[:n, g * 128:(g + 1) * 128], self.xsDT[:, g, 0:n], identb)
            self.cp(self.xsD[:n, 1024 * half:1024 * half + 1024], pv[:n, 0:1024], 'dve')
        for g in range(4):
            self.tr(pb6[:n, g * 128:(g + 1) * 128], self.xbcT[:, 16 + g, cols], identb)
        self.act(self.BTM[:n, :], pb6[:n, 0:512])
        dtw = sm[13]
        self.tt(dtw[:n, 0:32], dt, wend[:n, 0:32], ALU.mult)
        S.alias_phase([self.xsDT], [self.ynTM])
        if kind == 's':
            self.ssd_sample_states(tl, ch, dtw)
        S.alias_phase([self.ynTM], self.MTh[1])
        ssq = sm[9]
        self.memset(ssq[:n, 0:4], 0.0)
        def stageA(g):
            MTb = self.MTh[g % 2]
            for j in range(8):
                self.act(self.LAh[j][:n, :n], LS, AF.Copy, scale=self.dta[:n, s, 32 + 8 * g + j:33 + 8 * g + j])
            for j in range(8):
                o = ps[1 + j // 4][:n, (j % 4) * 128:(j % 4) * 128 + n]
                self.mm(o, self.LAh[j][:n, :n], U, start=True, stop=False)
                self.mm(o, identb[:n, :n], MT, start=False, stop=True)
            for half in range(2):
                self.act(self.LTh[half][:n, :, :n],
                         ps[1 + half][:n, :].rearrange('p (h t) -> p h t', h=4)[:, :, 0:n], AF.Exp)
            self.mm(ps[3][:n, 0:n], self.xbcT[:, 16 + g, cols], self.xbcT[:, 20 + g, cols])
            for j in range(8):
                self.stt(MTb[j][:n, :n], self.LTh[j // 4][:n, j % 4, :n], self.dta[:n, s, 8 * g + j:8 * g + j + 1], ps[3][:n, 0:n], ALU.mult, ALU.mult)

        def stageB(g):
            MTb = self.MTh[g % 2]
            for j in range(8):
                h = 8 * g + j
                self.mm(ps[4][:n, j * 64:(j + 1) * 64], MTb[j][:n, :n], xsTM[:n, h * 64:(h + 1) * 64])
            t1 = self.t1
            if kind != 's':
                self.mm(ps[5][:n, 0:512], self.xbcT[:, 20 + g, cols], self.STb[:, 512 * g:512 * g + 512])
                for j in range(4):
                    self.act(t1[:n, j * 64:(j + 1) * 64], ps[5][:n, j * 64:(j + 1) * 64], AF.Copy, scale=dect[:n, 8 * g + j:8 * g + j + 1])
                self.tt(t1[:n, 256:512].rearrange('p (h d) -> p d h', d=64), ps[5][:n, 256:512].rearrange('p (h d) -> p d h', d=64),
                        View(dect, dect.t[:n, 8 * g + 4:8 * g + 8].unsqueeze(1).to_broadcast([n, 64, 4])), ALU.mult)
            else:
                self.cp(t1[:n, :], self.ysi[:n, 512 * g:512 * g + 512], 'dve')
            self.tt(t1[:n, :], t1[:n, :], ps[4][:n, 0:512], ALU.add)
            self.tt(t1[:n, :], t1[:n, :], self.xsD[:n, 512 * g:512 * g + 512], ALU.add)
            self.tt(self.yz[:n, 512 * g:512 * g + 512], t1[:n, :], self.zs[s][:n, 512 * g:512 * g + 512], ALU.mult)

        stageA(0)
        for g in range(4):
            if g + 1 < 4:
                stageA(g + 1)
            stageB(g)
        S.alias_phase(self.MTh[1], [self.ynTM])
        for g in range(4):
            self.act(self.ynTM[:n, 512 * g:512 * g + 512], self.yz[:n, 512 * g:512 * g + 512], AF.Square, accum=ssq[:n, g:g + 1])
        rs = sm[10]
        self.act(rs[:n, 0:4], ssq[:n, 0:4], AF.Ln, scale=0.25 / 512.0, bias=self.cst[:n, 0:1])
        self.act(rs[:n, 0:4], rs[:n, 0:4], AF.Exp, scale=-0.5)
        self.ts(rs[:n, 0:4], rs[:n, 0:4], 0.5, ALU.mult)
        for g in range(4):
            self.ts(self.ynTM[:n, 512 * g:512 * g + 512], self.yz[:n, 512 * g:512 * g + 512], rs[:n, g:g + 1], ALU.mult)
        for k in range(16):
            pv = pb6 if k < 8 else pb7
            self.tr(pv[:, (k % 8) * n:(k % 8 + 1) * n], self.ynTM[:n, k * 128:(k + 1) * 128], identb[:n, :n])
        for half in range(2):
            pv = (pb6 if half == 0 else pb7)[:, 0:8 * n].rearrange('p (k t) -> p k t', k=8)
            self.cp(self.ygT[:, 8 * half:8 * half + 8, cols], pv, 'dve' if half == 0 else 'act')
        if kind != 's':
            S.alias_phase([self.ynTM], self.MTh[1])
            for g in range(4):
                hs = slice(8 * g, 8 * g + 8)
                bank = ps[3 + 2 * (g % 2)]
                Bs = self.MTh[g % 2]
                for j in range(8):
                    h = 8 * g + j
                    self.ts(Bs[j][:n, :], self.BTM[:n, g * 128:(g + 1) * 128], dtw[:n, h:h + 1], ALU.mult)
                for j in range(8):
                    h = 8 * g + j
                    self.mm(bank[:, j * 64:(j + 1) * 64], Bs[j][:n, :], xsTM[:n, h * 64:(h + 1) * 64])
                if g % 2 == 0:
                    for j in range(8):
                        c0_ = 512 * g + 64 * j
                        self.act(self.STf[:, c0_:c0_ + 64], self.STf[:, c0_:c0_ + 64], AF.Copy, scale=decS[:, 8 * g + j:8 * g + j + 1])
                else:
                    sv = self.STf[:, 512 * g:512 * g + 512].rearrange('p (h d) -> p d h', d=64)
                    self.tt(sv, sv, View(decS, decS.t[:, hs].unsqueeze(1).to_broadcast([128, 64, 8])), ALU.mult)
                self.tt(self.STf[:, 512 * g:512 * g + 512], self.STf[:, 512 * g:512 * g + 512], bank[:, 0:512], ALU.add)
            S.alias_phase(self.MTh[1], [self.ynTM])
            self.cp(self.STb.ap(), self.STf.ap(), 'act')
            if ch.final:
                identf = self.cfv('ident')
                for j in range(16):
                    bank = ps[1 + (j // 4) % 2]
                    self.tr(bank[:, (j % 4) * 128:(j % 4 + 1) * 128], self.STf[:, j * 128:(j + 1) * 128], identf)
                    if j % 4 == 3:
                        stg = self.LA[:, 4 * ((j // 4) % 2):4 * ((j // 4) % 2) + 4, :]
                        self.S.alias_phase(self.LAh, [self.LA])
                        self.cp(stg, bank[:, 0:512].rearrange('p (j n) -> p j n', j=4), 'act')
                        q = j // 4
                        self.dma(O['p_ssm'][512 * q:512 * q + 512, :].rearrange('(j p) n -> p j n', p=128), stg)
                self.S.alias_phase([self.LA], self.LAh)

    def ssd_sample_states(self, tl, ch, wend):
        I, O, ps, sm, S = self.I, self.O, self.ps, self.sm, self.S
        n, s = 64, ch.slot
        identf = self.cfv('ident')
        RS, BM = self.cfv('RS', 64), self.cfv('BM', 64)
        CM = self.cbv('CM').rearrange('p (b j) -> p b j', b=16)
        dect = sm[6]
        S.alias_phase(self.grpArena, self.grpArena2)
        S.alias_phase(self.LTh + self.MTh[0] + [self.t1, self.yz], [self.S0b[1]])
        blsb = sm[11]
        self.cp(blsb[:n, 0:32], ps[0][:n, 32:64], 'act')
        bl3 = blsb[:n, 0:32].rearrange('p (j r) -> p j r', r=2)
        for r in range(2):
            self.tt(self.Rr[:n, r, :].rearrange('p (b j) -> p b j', b=16),
                    View(blsb, bl3.ap[:, :, r].unsqueeze(1).to_broadcast([n, 16, 16])),
                    View(self.cf, RS.ap.unsqueeze(2).to_broadcast([n, 16, 16])), ALU.mult, eng='pool')
        self.mm(ps[0][:, 256:512], self.cfv('H0', 64), self.Rr[:n, 0, :], start=True, stop=False)
        self.mm(ps[0][:, 256:512], self.cfv('H1', 64), self.Rr[:n, 1, :], start=False, stop=True)
        self.act(self.decP.ap(), ps[0][:, 256:512], AF.Exp)
        wxA = self.ynTM
        self.tt(wxA[:n, :].rearrange('p (h d) -> p d h', d=64), self.xdt[:n, :].rearrange('p (h d) -> p d h', d=64),
                View(wend, wend.t[:n, 0:32].unsqueeze(1).to_broadcast([n, 64, 32])), ALU.mult)
        for b in range(16):
            Sb = self.S0b[b % 2]
            cm = self.CTmb[b % 2]
            self.dma(Sb.ap(), I['s_ssm'][b].rearrange('(j p) n -> p j n', p=128))
            self.tt(cm.ap(), self.xbcT[:, 20:24, ch.col0:ch.col0 + 64], View(self.cb, CM.ap[:, b, :].unsqueeze(1).to_broadcast([128, 4, 64])), ALU.mult)
            self.ts(self.wxm[:n, :], wxA[:n, :], BM[:, b:b + 1], ALU.mult)
            for q in range(4):
                bank = ps[5 + q % 2]
                for i in range(4):
                    self.tr(bank[:, i * 128:(i + 1) * 128], Sb[:, 4 * q + i, :], identf)
                self.cp(self.SbT[:, 512 * q:512 * q + 512], bank[:, 0:512], 'act')
            for g in range(4):
                self.mm(ps[1 + g][:n, 0:512], cm[:, g, :], self.SbT[:, 512 * g:512 * g + 512], start=(b == 0), stop=(b == 15))
            for q in range(4):
                bank = ps[7] if q % 2 == 0 else ps[0]
                for i in range(4):
                    j = 4 * q + i
                    self.mm(bank[:, i * 128:(i + 1) * 128], self.wxm[:n, j * 128:(j + 1) * 128], self.BTM[:n, q * 128:(q + 1) * 128])
                for i in range(4):
                    j = 4 * q + i
                    self.stt(Sb[:, j, :], Sb[:, j, :], self.decP[:, 16 * b + j:16 * b + j + 1], bank[:, i * 128:(i + 1) * 128], ALU.mult, ALU.add)
            self.dma(O['o_ssm'][b].rearrange('(j p) n -> p j n', p=128), Sb.ap())
        for g in range(4):
            self.tt(self.ysi[:n, 512 * g:512 * g + 512].rearrange('p (h d) -> p d h', d=64),
                    ps[1 + g][:n, 0:512].rearrange('p (h d) -> p d h', d=64),
                    View(dect, dect.t[:n, 8 * g:8 * g + 8].unsqueeze(1).to_broadcast([n, 64, 8])), ALU.mult)
        S.alias_phase([self.S0b[1]], self.LTh + self.MTh[0] + [self.t1, self.yz])


_CACHE = {}


def _get_kernel():
    if 'k' not in _CACHE:
        _CACHE['k'] = K()
    return _CACHE['k']


def make_in_maps(kb, inputs):
    f = lambda a: np.ascontiguousarray(np.asarray(a, dtype=np.float32))
    xp, xs = f(inputs['x_prompt']), f(inputs['x_sample'])
    shared = {'meta': f(inputs['meta_tokens']), 'cf': kb.cf_np, 'cb': kb.cb_np,
              'ln0_g': f(inputs['ln0_g']), 'ln0_b': f(inputs['ln0_b']),
              'b_if': f(inputs['b_mlstm_if'])[0], 'w_mconv': f(inputs['w_mlstm_conv'])[0],
              'b_mconv': f(inputs['b_mlstm_conv']), 'mnorm_g': f(inputs['mlstm_norm_g']),
              'w_sconv': f(inputs['w_ssm_conv'])[0], 'b_sconv': f(inputs['b_ssm_conv']),
              'dt_bias': f(inputs['ssm_dt_bias'])[0], 'A_log': f(inputs['ssm_A_log'])[0],
              'ssm_D': f(inputs['ssm_D'])[0], 'snorm_g': f(inputs['ssm_norm_g']),
              'ln1_g': f(inputs['ln1_g'])[0], 'ln1_b': f(inputs['ln1_b'])[0],
              'w_fconv': f(inputs['w_ffn_conv'])[0], 'b_fconv': f(inputs['b_ffn_conv']),
              'ln2_g': f(inputs['ln2_g'])[0], 'ln2_b': f(inputs['ln2_b'])[0],
              'w_in': f(inputs['w_in'])[0], 'w_proj_a': f(inputs['w_proj_a'])[0],
              'w_proj_b': f(inputs['w_proj_b'])[0], 'w_out': f(inputs['w_out'])[0],
              'w_up': f(inputs['w_up'])[0], 'w_down': f(inputs['w_down'])[0]}
    maps = []
    for c in range(8):
        b = slice(16 * c, 16 * c + 16)
        m = dict(shared)
        m['xp'] = xp[c]
        m['xs'] = xs[b].reshape(64, D)
        m['s_mconv'] = f(inputs['state_mlstm_conv'])[0, b].reshape(48, 1024)
        m['s_C'] = f(inputs['state_mlstm_C'])[0, b]
        m['s_n'] = f(inputs['state_mlstm_n'])[0, b].reshape(64, 128)
        m['s_m'] = f(inputs['state_mlstm_m'])[0, b]
        m['s_sconv'] = f(inputs['state_ssm_conv'])[0, b].reshape(48, 3072)
        m['s_ssm'] = f(inputs['state_ssm'])[0, b].reshape(16, 2048, 128)
        m['s_fconv'] = f(inputs['state_ffn_conv'])[0, b].reshape(32, 2 * DFF)
        maps.append(m)
    return maps


def kernel(**inputs):
    kb = _get_kernel()
    maps = make_in_maps(kb, inputs)
    res = run_bass_kernel_spmd(kb.nc, maps, core_ids=list(range(8)))
    R = res.results
    cat = lambda k: np.stack([np.asarray(r[k], dtype=np.float32) for r in R])
    y_p = cat('y_p')
    y_s = cat('y_s').reshape(128, 4, D)
    p_mconv = cat('p_mconv')[None]
    p_C = cat('p_C')[None]
    p_n = cat('p_n')[None]
    p_m = cat('p_m').reshape(8, 4)[None]
    p_sconv = cat('p_sconv')[None]
    p_ssm = cat('p_ssm').reshape(8, 32, 64, 128)[None]
    p_fconv = cat('p_fconv')[None]
    s_mconv = cat('o_mconv').reshape(128, 3, 1024)[None]
    s_C = cat('o_C').reshape(128, 4, 128, 256)[None]
    s_n = cat('o_n').reshape(128, 4, 128)[None]
    s_m = cat('o_m').reshape(128, 4)[None]
    s_sconv = cat('o_sconv').reshape(128, 3, 3072)[None]
    s_ssm = cat('o_ssm').reshape(128, 32, 64, 128)[None]
    s_fconv = cat('o_fconv').reshape(128, 2, 2 * DFF)[None]
    return (y_p, y_s, p_mconv, p_C, p_n, p_m, p_sconv, p_ssm, p_fconv,
            s_mconv, s_C, s_n, s_m, s_sconv, s_ssm, s_fconv)
```

```python
import numpy as np
import ml_dtypes
import concourse.bass as bass
import concourse.mybir as mybir
from concourse.bass_utils import run_bass_kernel_spmd

F32 = mybir.dt.float32
BF16 = mybir.dt.bfloat16
ALU = mybir.AluOpType
AF = mybir.ActivationFunctionType
AX = mybir.AxisListType

D = 1024
DIN = 10280
DFF = 2816
NEG = -30000.0
ALPHA = 2.0 ** 0.25
LN_EPS = 1e-5
RMS_EPS = 1e-5
QSCALE = 128.0 ** -0.5


class Buf:
    def __init__(self, name, t, space):
        self.name = name
        self.t = t
        self.space = space
        self.last_w = None
        self.readers = []
        self.sem_in = None
        self.cnt_in = 0
        self.sem_out = None
        self.cnt_out = 0

    def __getitem__(self, idx):
        return View(self, self.t[idx])

    def ap(self):
        return View(self, self.t[:] if self.space != 'dram' else self.t)


class View:
    def __init__(self, buf, ap):
        self.buf = buf
        self.ap = ap

    def __getitem__(self, idx):
        return View(self.buf, self.ap[idx])

    def rearrange(self, *a, **k):
        return View(self.buf, self.ap.rearrange(*a, **k))

    def bc(self, axis, shape):
        return View(self.buf, self.ap.unsqueeze(axis).to_broadcast(list(shape)))

    def bitcast(self, dt):
        return View(self.buf, self.ap.bitcast(dt))


def _bufs(vs):
    out = []
    for v in vs:
        if v is None or isinstance(v, (int, float)):
            continue
        b = v.buf if isinstance(v, View) else v
        if b not in out:
            out.append(b)
    return out


class Sched:
    ENGS = ('pe', 'act', 'dve', 'pool', 'sp')

    def __init__(self, nc):
        self.nc = nc
        self.sem = {e: nc.alloc_semaphore('sem_' + e) for e in self.ENGS}
        self.cnt = {e: 0 for e in self.ENGS}
        self.ops = {e: [] for e in self.ENGS}
        self.seen = {e: {} for e in self.ENGS}
        self.final_tokens = []
        self.sb_off = 16512
        self.sb_end = 229376
        self.nsem = 5

    def sbuf(self, name, shape, dtype, at=None):
        nbytes = int(np.prod(shape[1:])) * (2 if dtype == BF16 else 4)
        nbytes = (nbytes + 31) // 32 * 32
        if at is None:
            at = self.sb_off
            self.sb_off += nbytes
            assert self.sb_off <= self.sb_end, ('SBUF overflow', name, self.sb_off)
        t = self.nc.alloc_sbuf_tensor_at(name, list(shape), dtype, offset=at)
        b = Buf(name, t, 'sbuf')
        b.off = at
        b.nbytes = nbytes
        return b

    def psum(self, name, shape, dtype=F32):
        t = self.nc.alloc_psum_tensor(name, list(shape), dtype)
        return Buf(name, t, 'psum')

    def dram(self, name, shape, dtype, kind):
        t = self.nc.dram_tensor(name, list(shape), dtype, kind=kind)
        return Buf(name, t.ap(), 'dram')

    def alias_phase(self, old, new):
        toks = []
        for b in old:
            if b.last_w is not None:
                toks.append(b.last_w)
            toks.extend(b.readers)
        for b in new:
            b.readers = list(b.readers) + toks

    def _need(self, eng, waits, tok):
        sem, val, teng = tok
        key = id(sem)
        if self.seen[eng].get(key, 0) >= val:
            return
        if key not in waits or waits[key][1] < val:
            waits[key] = (sem, val)

    def _deps(self, eng, reads, writes):
        waits = {}
        for b in reads:
            tok = b.last_w
            if tok is not None and not (tok[2] == eng and eng == 'pe'):
                self._need(eng, waits, tok)
            if b.space == 'psum':
                for r in b.readers:
                    if r[2] != eng:
                        self._need(eng, waits, r)
        for b in writes:
            tok = b.last_w
            if tok is not None and not (tok[2] == eng and eng == 'pe'):
                self._need(eng, waits, tok)
            for r in b.readers:
                if not (r[2] == eng and eng == 'pe'):
                    self._need(eng, waits, r)
        for key, (sem, val) in waits.items():
            self.seen[eng][key] = val
        return list(waits.values())

    def op(self, eng, fn, reads=(), writes=()):
        reads = _bufs(reads)
        writes = _bufs(writes)
        waits = self._deps(eng, reads, writes)
        self.cnt[eng] += 1
        tok = (self.sem[eng], self.cnt[eng], eng)
        self.ops[eng].append((waits, fn, (self.sem[eng], 1)))
        for b in writes:
            b.last_w = tok
            b.readers = []
        for b in reads:
            if b not in writes:
                b.readers.append(tok)
        return tok

    def dma(self, q, out, in_, **kw):
        ob, ib = out.buf, in_.buf
        waits = self._deps(q, [ib], [ob])
        if ob.space != 'dram':
            if ob.sem_in is None:
                ob.sem_in = self.nc.alloc_semaphore('din_' + ob.name)
                self.nsem += 1
            ob.cnt_in += 16
            sem, val = ob.sem_in, ob.cnt_in
        else:
            if ib.sem_out is None:
                ib.sem_out = self.nc.alloc_semaphore('dout_' + ib.name)
                self.nsem += 1
            ib.cnt_out += 16
            sem, val = ib.sem_out, ib.cnt_out
        tok = (sem, val, 'dma')
        oap, iap = out.ap, in_.ap

        def fn(e, oap=oap, iap=iap, kw=kw):
            return e.dma_start(out=oap, in_=iap, **kw)
        self.ops[q].append((waits, fn, (sem, 16)))
        ob.last_w = tok
        ob.readers = []
        ib.readers.append(tok)
        if ob.space == 'dram':
            self.final_tokens.append(tok)
        return tok

    def emit(self):
        nc = self.nc
        last = {}
        for sem, val, _ in self.final_tokens:
            k = id(sem)
            if k not in last or last[k][1] < val:
                last[k] = (sem, val)
        fin = list(last.values())
        eng_obj = {'pe': 'tensor', 'act': 'scalar', 'dve': 'vector', 'pool': 'gpsimd', 'sp': 'sync'}
        with nc.Block() as block:
            def mk(eng):
                def body(e):
                    for waits, fn, inc in self.ops[eng]:
                        for sem, val in waits:
                            e.wait_ge(sem, val)
                        fn(e).then_inc(inc[0], inc[1])
                    if eng == 'sp':
                        for sem, val in fin:
                            e.wait_ge(sem, val)
                return body
            for eng, attr in eng_obj.items():
                getattr(block, attr)(mk(eng))


def _const_tables():
    p = np.arange(128)[:, None]
    j = np.arange(128)[None, :]
    f = {}
    f['ident'] = (p == j)
    f['ones'] = np.ones((128, 128))
    f['U'] = (p <= j)
    f['LS'] = (p > j)
    sb = (p // 4 == j // 4) & (p < 64) & (j < 64)
    f['Us'] = ((p <= j) & sb)[:, :64]
    f['LSs'] = ((p > j) & sb)[:, :64]
    f['BOs'] = sb[:, :64]
    f['SELp'] = np.repeat(p == 127, 128, axis=1)
    f['SELm'] = np.repeat(p == 15, 128, axis=1)
    b16 = np.arange(16)[None, :]
    f['RS'] = (p == 4 * b16 + 3)
    f['BM'] = (p // 4 == b16) & (p < 64)
    f['BMT'] = ((p < 16) & (j // 4 == p))[:, :64]
    f['LSEL'] = ((j == 4 * (p // 4) + 3) & (p < 64))[:, :64]
    f['H0'] = np.repeat(p < 64, 128, axis=1) & (j < 64)
    f['H1'] = np.repeat(p < 64, 128, axis=1) & (j >= 64)
    cf_off, cols = {}, []
    o = 0
    for k, v in f.items():
        cf_off[k] = (o, v.shape[1])
        o += v.shape[1]
        cols.append(v.astype(np.float32))
    cf = np.concatenate(cols, axis=1)
    g = {}
    g['identb'] = (p == j).astype(np.float32)
    g['onesb'] = np.ones((128, 128), np.float32)
    g['M'] = np.where(j <= p, 0.0, NEG)
    g['MT'] = np.where(p <= j, 0.0, NEG)
    g['Ms'] = np.where((j <= p) & sb, 0.0, NEG)[:, :64]
    g['MTs'] = np.where((p <= j) & sb, 0.0, NEG)[:, :64]
    jj = np.arange(64)[None, None, :]
    bb = np.arange(16)[None, :, None]
    g['CM'] = np.broadcast_to((jj // 4 == bb), (128, 16, 64)).reshape(128, 1024).astype(np.float32)
    cb_off, cols = {}, []
    o = 0
    for k, v in g.items():
        cb_off[k] = (o, v.shape[1])
        o += v.shape[1]
        cols.append(np.asarray(v, np.float32))
    cbm = np.concatenate(cols, axis=1).astype(ml_dtypes.bfloat16)
    return cf, cf_off, cbm, cb_off


class Chunk:
    def __init__(self, slot, col0, n, kind, row0=0):
        self.slot, self.col0, self.n, self.kind, self.row0 = slot, col0, n, kind, row0


class Tile:
    def __init__(self, name, T, chunks, segs):
        self.name, self.T, self.chunks, self.segs = name, T, chunks, segs


W_SHAPES = {'w_in': (D, DIN), 'w_proj_a': (D, D), 'w_proj_b': (2 * D, D), 'w_out': (D, D),
            'w_up': (D, 2 * DFF), 'w_down': (DFF, D)}


def tile_blocks():
    bl = []
    for c in range(0, 3072, 256):
        bl.append(('w_in', 0, 8, [(c, 256)]))
    bl.append(('w_in', 0, 8, [(3072, 8), (8200, 32)]))
    for c in range(3080, 5128, 256):
        bl.append(('w_in', 0, 8, [(c, 256)]))
    for c in range(5128, 8200, 256):
        bl.append(('w_in', 0, 8, [(c, 256)]))
    for c in range(8232, 10280, 256):
        bl.append(('w_in', 0, 8, [(c, 256)]))
    for j in range(4):
        bl.append(('w_proj_a', 0, 8, [(256 * j, 256)]))
        bl.append(('w_proj_b', 0, 8, [(256 * j, 256)]))
        bl.append(('w_proj_b', 8, 8, [(256 * j, 256)]))
    for j in range(4):
        bl.append(('w_out', 0, 8, [(256 * j, 256)]))
    for j in range(11):
        bl.append(('w_up', 0, 8, [(256 * j, 256)]))
        bl.append(('w_up', 0, 8, [(DFF + 256 * j, 256)]))
    for j in range(4):
        for k0, nk in ((0, 8), (8, 8), (16, 6)):
            bl.append(('w_down', k0, nk, [(256 * j, 256)]))
    return bl


class K:
    def __init__(self, debug=None, tiles=('T0', 'T1', 'T2', 'T3', 'T4')):
        self.debug = debug or {}
        self.tile_sel = tuple(tiles)
        self.ntiles = len(self.tile_sel)
        nc = bass.Bass('TRN2', target_bir_lowering=False)
        self.nc = nc
        self.S = S = Sched(nc)
        self.dumps = {}
        cf, self.cfo, cbm, self.cbo = _const_tables()
        self.cf_np, self.cb_np = cf, cbm
        din = lambda n, s, dt=F32: S.dram(n, s, dt, 'ExternalInput')
        dout = lambda n, s: S.dram(n, s, F32, 'ExternalOutput')
        I = self.I = {}
        I['xp'] = din('xp', [2048, D]); I['xs'] = din('xs', [64, D]); I['meta'] = din('meta', [16, D])
        I['s_mconv'] = din('s_mconv', [48, 1024]); I['s_C'] = din('s_C', [16, 4, 128, 256])
        I['s_n'] = din('s_n', [64, 128]); I['s_m'] = din('s_m', [16, 4])
        I['s_sconv'] = din('s_sconv', [48, 3072]); I['s_ssm'] = din('s_ssm', [16, 2048, 128])
        I['s_fconv'] = din('s_fconv', [32, 2 * DFF])
        I['cf'] = din('cf', list(cf.shape)); I['cb'] = din('cb', list(cbm.shape), BF16)
        for n, s in (('ln0_g', [D]), ('ln0_b', [D]), ('b_if', [8]), ('w_mconv', [4, 1024]), ('b_mconv', [1, 1024]),
                     ('mnorm_g', [1, 1024]), ('w_sconv', [4, 3072]), ('b_sconv', [1, 3072]), ('dt_bias', [32]),
                     ('A_log', [32]), ('ssm_D', [32]), ('snorm_g', [1, 2048]), ('ln1_g', [D]), ('ln1_b', [D]),
                     ('w_fconv', [3, 2 * DFF]), ('b_fconv', [1, 2 * DFF]), ('ln2_g', [D]), ('ln2_b', [D])):
            I[n] = din(n, s)
        for n, s in W_SHAPES.items():
            I[n] = din(n, list(s))
        O = self.O = {}
        O['y_p'] = dout('y_p', [2048, D]); O['y_s'] = dout('y_s', [64, D])
        O['p_mconv'] = dout('p_mconv', [3, 1024]); O['p_C'] = dout('p_C', [4, 128, 256])
        O['p_n'] = dout('p_n', [4, 128]); O['p_m'] = dout('p_m', [1, 4])
        O['p_sconv'] = dout('p_sconv', [3, 3072]); O['p_ssm'] = dout('p_ssm', [2048, 128])
        O['p_fconv'] = dout('p_fconv', [2, 2 * DFF])
        O['o_mconv'] = dout('o_mconv', [48, 1024]); O['o_C'] = dout('o_C', [16, 4, 128, 256])
        O['o_n'] = dout('o_n', [64, 128]); O['o_m'] = dout('o_m', [16, 4])
        O['o_sconv'] = dout('o_sconv', [48, 3072]); O['o_ssm'] = dout('o_ssm', [16, 2048, 128])
        O['o_fconv'] = dout('o_fconv', [32, 2 * DFF])
        self.rr = {}
        self.build()
        S.emit()

    def rot(self, key, n):
        i = self.rr.get(key, 0)
        self.rr[key] = i + 1
        return i % n

    def mm(self, out, lhsT, rhs, start=True, stop=True):
        self.S.op('pe', lambda e: e.matmul(out.ap, lhsT=lhsT.ap, rhs=rhs.ap, start=start, stop=stop),
                  reads=[lhsT, rhs], writes=[out])

    def tr(self, out, in_, ident):
        self.S.op('pe', lambda e: e.transpose(out=out.ap, in_=in_.ap, identity=ident.ap),
                  reads=[in_, ident], writes=[out])

    def act(self, out, in_, func=AF.Copy, bias=None, scale=None, accum=None):
        kw = {}
        if bias is not None:
            kw['bias'] = bias.ap if isinstance(bias, View) else bias
        if scale is not None:
            kw['scale'] = scale.ap if isinstance(scale, View) else scale
        if accum is not None:
            kw['accum_out'] = accum.ap
        self.S.op('act', lambda e: e.activation(out=out.ap, in_=in_.ap, func=func, **kw),
                  reads=[in_, bias, scale], writes=[out, accum])

    def tt(self, out, a, b, op, eng='dve'):
        self.S.op(eng, lambda e: e.tensor_tensor(out=out.ap, in0=a.ap, in1=b.ap, op=op),
                  reads=[a, b], writes=[out])

    def ts(self, out, a, s1, op0, s2=None, op1=None, eng='dve', accum=None):
        v1 = s1.ap if isinstance(s1, View) else s1
        v2 = s2.ap if isinstance(s2, View) else s2
        kw = {}
        if op1 is not None:
            kw['op1'] = op1
        if accum is not None:
            kw['accum_out'] = accum.ap
        self.S.op(eng, lambda e: e.tensor_scalar(out=out.ap, in0=a.ap, scalar1=v1, scalar2=v2, op0=op0, **kw),
                  reads=[a, s1, s2], writes=[out, accum])

    def stt(self, out, a, s, b, op0, op1, eng='dve'):
        v = s.ap if isinstance(s, View) else s
        self.S.op(eng, lambda e: e.scalar_tensor_tensor(out=out.ap, in0=a.ap, scalar=v, in1=b.ap, op0=op0, op1=op1),
                  reads=[a, s, b], writes=[out])

    def cp(self, out, in_, eng='dve'):
        if eng == 'act':
            return self.act(out, in_)
        self.S.op(eng, lambda e: e.tensor_copy(out=out.ap, in_=in_.ap), reads=[in_], writes=[out])

    def memset(self, out, val, eng='dve'):
        self.S.op(eng, lambda e: e.memset(out.ap, val), writes=[out])

    def rmax(self, out, in_, eng='dve'):
        self.S.op(eng, lambda e: e.tensor_reduce(out=out.ap, in_=in_.ap, axis=AX.X, op=ALU.max),
                  reads=[in_], writes=[out])

    def dma(self, out, in_, q='sp'):
        self.S.dma(q, out, in_)

    def dump(self, name, view, shape):
        if name not in self.debug:
            return
        d = self.S.dram('dbg_' + name, list(shape), view.ap.dtype, 'ExternalOutput')
        self.dumps[name] = d
        self.dma(d.ap(), view)

    def cfv(self, name, rows=128, cols=None):
        o, w = self.cfo[name]
        cols = w if cols is None else cols
        return self.cf[:rows, o:o + cols]

    def cbv(self, name, rows=128, cols=None):
        o, w = self.cbo[name]
        cols = w if cols is None else cols
        return self.cb[:rows, o:o + cols]

    def ws_init(self):
        S = self.S
        self.wlist = tile_blocks()
        self.nbt = len(self.wlist)
        self.wblocks = self.wlist * self.ntiles
        self.wst = [S.sbuf(f'wst{i}', [128, 8, 256], F32) for i in range(2)]
        self.wbf = [S.sbuf(f'wbf{i}', [128, 8, 256], BF16) for i in range(2)]
        self.wx4 = [S.sbuf(f'wrx{i}', [128, 8, 256], BF16, at=self.wst[i // 2].off + 4096 * (i % 2)) for i in range(4)]
        self.wring = self.wbf + self.wx4
        self.wscr = [S.dram(f'wscr{j}', [128, 8, 256], BF16, 'Internal') for j in range(self.nbt)] if self.ntiles > 1 else None
        self.w_loaded = 0
        self.w_cast = 0
        self.w_next = 0
        self.w_ring_started = False

    def _w_load(self, i):
        name, k0, nk, parts = self.wblocks[i]
        st = self.wst[i % 2]
        W = self.I[name]
        c = 0
        for (c0, n) in parts:
            src = View(W, W.t[k0 * 128:(k0 + nk) * 128, c0:c0 + n].rearrange('(k p) c -> p k c', p=128))
            self.dma(st[:, 0:nk, c:c + n], src, q='sp')
            c += n

    def _w_castop(self, i):
        name, k0, nk, parts = self.wblocks[i]
        n = sum(p[1] for p in parts)
        eng = 'dve' if (i % 4) != 3 else 'act'
        if name in ('w_proj_a', 'w_proj_b'):
            for k in range(nk):
                gcol = self.cwm[:, k0 + k, 5:6] if name == 'w_proj_a' else self.sng[:, k0 + k:k0 + k + 1]
                self.ts(self.wbf[i % 2][:, k, 0:n], self.wst[i % 2][:, k, 0:n], gcol, ALU.mult)
        else:
            self.cp(self.wbf[i % 2][:, 0:nk, 0:n], self.wst[i % 2][:, 0:nk, 0:n], eng)
        if self.wscr is not None:
            self.dma(self.wscr[i][:, 0:nk, 0:n], self.wbf[i % 2][:, 0:nk, 0:n], q='pool')

    def _w_ringload(self, i):
        name, k0, nk, parts = self.wblocks[i]
        n = sum(p[1] for p in parts)
        dst = self.wring[(i - self.nbt) % 6]
        self.dma(dst[:, 0:nk, 0:n], self.wscr[i % self.nbt][:, 0:nk, 0:n], q='sp')

    def wnext(self):
        i = self.w_next
        nb = len(self.wblocks)
        self.w_next += 1
        if i < self.nbt:
            lim = self.nbt
            while self.w_loaded < min(lim, i + 2):
                self._w_load(self.w_loaded)
                self.w_loaded += 1
            while self.w_cast < min(lim, i + 2):
                self._w_castop(self.w_cast)
                self.w_cast += 1
            while self.w_loaded < min(lim, i + 3):
                self._w_load(self.w_loaded)
                self.w_loaded += 1
            return self.wbf[i % 2], self.wblocks[i]
        if not self.w_ring_started:
            self.w_ring_started = True
            self.S.alias_phase(self.wst, self.wx4)
            self.w_loaded = self.nbt
        while self.w_loaded < min(nb, i + 6):
            self._w_ringload(self.w_loaded)
            self.w_loaded += 1
        return self.wring[(i - self.nbt) % 6], self.wblocks[i]

    def build(self):
        S = self.S
        sb = S.sbuf
        ncf, ncb = self.cf_np.shape[1], self.cb_np.shape[1]
        self.cf = sb('cf', [128, ncf], F32)
        self.cb = sb('cb', [128, ncb], BF16)
        self.lnc = sb('lnc', [128, 2, D], F32)
        self.bif_b = sb('bif_b', [128, 8], F32)
        self.dtb_b = sb('dtb_b', [128, 32], F32)
        self.A_b = sb('A_b', [128, 32], F32)
        self.D_b = sb('D_b', [128, 32], F32)
        self.Dfm = sb('Dfm', [128, 16], F32)
        self.cwm = sb('cwm', [128, 8, 6], F32)
        self.cws = sb('cws', [128, 24, 5], F32)
        self.sng = sb('sng', [128, 16], F32)
        self.cwf = sb('cwf', [128, 44, 4], F32)
        self.ws_init()
        self.xr = [sb(f'xr{i}', [128, D], F32) for i in range(4)]
        self.zs = [None] * 4
        self.xnT = sb('xnT', [128, 8, 512], BF16)
        self.hgT = sb('hgT', [128, 8, 512], BF16)
        self.ygT = sb('ygT', [128, 16, 512], BF16)
        self.Cf = sb('Cf', [128, 4, 256], F32); self.Cb = sb('Cb', [128, 4, 256], BF16)
        self.nf = sb('nf', [128, 4], F32); self.nb = sb('nb', [128, 4], BF16)
        self.m_b = sb('m_b', [128, 4], F32)
        self.STf = sb('STf', [128, 2048], F32); self.STb = sb('STb', [128, 2048], BF16)
        self.cq = sb('cq', [128, 8, 3], F32); self.cx = sb('cx', [128, 24, 3], F32)
        self.cff = sb('cff', [128, 44, 2], F32)
        self.scar = sb('scar', [128, 44 * 16 * 2], F32)
        self.gat = sb('gat', [128, 4, 8], F32)
        self.dta = sb('dta', [128, 4, 64], F32)
        self.ifdt = sb('ifdt', [128, 4, 40], F32)
        self.sm = [sb(f'sm{i}', [128, 32], F32) for i in range(16)]
        self.xb16 = sb('xb16', [128, D], BF16)
        self.lnsc = [sb(f'lnsc{i}', [128, 16], F32) for i in range(4)]
        self.cst = sb('cst', [128, 8], F32)
        self.pn_st = sb('pn_st', [128, 128], F32)
        R0 = S.sb_off
        o = R0
        def at(name, shape, dt):
            nonlocal o
            b = sb(name, shape, dt, at=o)
            o += b.nbytes
            return b
        self.cE = [at(f'cE{i}', [128, 520], F32) for i in range(2)]
        self.cacc = [at(f'cacc{i}', [128, 512], F32) for i in range(3)]
        self.cth = [at(f'cth{i}', [128, 512], F32) for i in range(2)]
        self.cacc2 = [at(f'cacc2_{i}', [128, 512], F32) for i in range(2)]
        e1 = o
        o = R0
        self.R1 = at('R1', [128, 4, 128], F32); self.R2 = at('R2', [128, 4, 128], F32)
        self.R3 = at('R3', [128, 4, 128], F32); self.wT = at('wT', [128, 4, 128], F32)
        self.ST = at('ST', [128, 4, 128], BF16); self.kTM = at('kTM', [128, 4, 128], BF16)
        self.hh = at('hh', [128, 4, 256], F32); self.vw = at('vw', [128, 4, 256], BF16)
        self.hgTM = at('hgTM', [128, D], BF16)
        e2 = o
        o = R0
        self.xdt = at('xdt', [128, 2048], BF16); self.xsD = at('xsD', [128, 2048], BF16)
        self.wx = at('wx', [128, 512], BF16); self.BTM = at('BTM', [128, 512], BF16)
        self.LT = at('LT', [128, 8, 128], BF16); self.MTt = at('MTt', [128, 8, 128], BF16)
        self.t1 = at('t1', [128, 512], F32)
        self.yz = at('yz', [128, 2048], BF16)
        self.ynTM = at('ynTM', [128, 2048], BF16)
        e3 = o
        F0 = max(e1, e2, e3)
        conv_end = F0
        o = F0
        self.qkT = at('qkT', [128, 8, 512], BF16)
        self.v = [at(f'v{i}', [128, D], BF16) for i in range(4)]
        self.oth = [at(f'oth{i}', [128, D], BF16) for i in range(4)]
        a1_end = o
        o = F0
        self.xbcT = at('xbcT', [128, 24, 512], BF16)
        for i in range(2):
            self.zs[i] = at(f'zs{i}', [128, 2048], BF16)
        self.LA = at('LA', [128, 8, 128], F32)
        a2_end = o
        o = F0
        self.gth = at('gth', [128, 16, 512], BF16)
        self.mixT = at('mixT', [128, 8, 512], BF16)
        self.hffT = at('hffT', [128, 22, 512], BF16)
        b_end = o
        S.sb_off = max(a1_end, a2_end, b_end)
        for i in range(2, 4):
            self.zs[i] = sb(f'zs{i}', [128, 2048], BF16)
        self.arenas = [(self.xr[2].off, 2 * self.xr[2].nbytes), (self.zs[2].off, 2 * self.zs[2].nbytes)]
        a0, a1 = self.arenas[0][0], self.arenas[1][0]
        self.C0b = [sb(f'C0b{i}', [128, 4, 256], F32, at=a0 + 4096 * i) for i in range(2)]
        self.C0b16 = [sb(f'C0b16_{i}', [128, 4, 258], BF16, at=a1 + 2080 * i) for i in range(2)]
        self.qmb = [sb(f'qmb{i}', [128, 4, 64], BF16, at=a1 + 4160 + 512 * i) for i in range(2)]
        self.kTMm = [sb(f'kTMm{i}', [128, 4, 128], BF16, at=a1 + 5184 + 1024 * i) for i in range(2)]
        self.n0T = sb('n0T', [128, 64], F32, at=a1 + 7232)
        self.n16 = sb('n16', [128, 64], BF16, at=a1 + 7488)
        self.decS = sb('decS', [128, 64], F32, at=a1 + 7616)
        self.Rm = sb('Rm', [128, 64], F32, at=a1 + 7872)
        self.S0b = [sb('S0b0', [128, 16, 128], F32, at=a0), sb('S0b1', [128, 16, 128], F32, at=self.LT.off)]
        assert self.LT.off + 8192 <= self.ynTM.off
        self.SbT = sb('SbT', [128, 2048], BF16, at=a1)
        self.wxm = sb('wxm', [128, 2048], BF16, at=a1 + 4096)
        self.ysi = sb('ysi', [128, 2048], BF16)
        self.decP = sb('decP', [128, 256], F32)
        self.Rr = sb('Rr', [128, 2, 256], F32)
        self.CTmb = [sb(f'CTmb{i}', [128, 4, 64], BF16) for i in range(2)]
        self.grpArena2 = [self.S0b[0], self.SbT, self.wxm]
        self.xsDT = sb('xsDT', [128, 8, 128], BF16, at=self.ynTM.off)
        self.LAh = [sb(f'LAh{j}', [128, 128], F32, at=self.LA.off + 512 * j) for j in range(8)]
        self.LTh = [sb(f'LTh{j}', [128, 4, 128], BF16, at=self.LT.off + 1024 * j) for j in range(2)]
        self.MTh = [[sb(f'MTh{b}_{j}', [128, 128], BF16, at=base + 256 * j) for j in range(8)]
                    for b, base in enumerate((self.MTt.off, self.ynTM.off + 2048))]
        self.grpArena = self.C0b + self.C0b16 + self.qmb + self.kTMm + [self.n0T, self.n16, self.decS, self.Rm]
        self.grpA1conv = self.cE + self.cacc + self.cth + self.cacc2
        self.grpA1rec = [self.R1, self.R2, self.R3, self.wT, self.ST, self.kTM, self.hh, self.vw, self.hgTM]
        self.grpA1fix = [self.qkT] + self.v + self.oth
        self.grpA2fix = [self.xbcT, self.zs[0], self.zs[1]] + self.LAh
        self.grpA2rec = [self.xdt, self.xsD, self.wx, self.BTM, self.t1, self.yz, self.ynTM] + self.LTh + self.MTh[0]
        self.grpB = [self.gth, self.mixT, self.hffT]
        print('SBUF used', S.sb_off, 'of', S.sb_end, 'R', R0, conv_end - R0, a1_end - R0, a2_end - R0, b_end - R0)
        self.ps = [S.psum(f'ps{i}', [128, 512], F32) for i in range(8)]
        self.setup()
        tiles = self.make_tiles()
        for tl in tiles:
            if tl.name in self.tile_sel:
                self.run_tile(tl, last=(tl.name == 'T4'))

    def make_tiles(self):
        def pch(slot, col0, c):
            ch = Chunk(slot, col0, 128, 'p', row0=128 * c)
            ch.final = (c == 15)
            return ch
        m = Chunk(0, 0, 16, 'm'); m.final = False
        tiles = [Tile('T0', 400, [m] + [pch(1 + i, 16 + 128 * i, i) for i in range(3)], [(0, 1, 400, 'p')])]
        for t in range(3):
            tiles.append(Tile(f'T{t + 1}', 512, [pch(i, 128 * i, 3 + 4 * t + i) for i in range(4)], [(0, 1, 512, 'p')]))
        sc = Chunk(1, 128, 64, 's'); sc.final = False
        tiles.append(Tile('T4', 192, [pch(0, 0, 15), sc], [(0, 1, 128, 'p'), (128, 16, 4, 's')]))
        return tiles

    def psb(self, i):
        return self.ps[i].ap().bitcast(BF16)

    def setup(self):
        I = self.I
        self.dma(self.cf.ap(), I['cf'].ap())
        self.dma(self.cb.ap(), I['cb'].ap())
        pb = lambda n: View(I[n], I[n].t.partition_broadcast(128))
        self.dma(self.bif_b.ap(), pb('b_if'))
        self.dma(self.dtb_b.ap(), pb('dt_bias'))
        self.dma(self.A_b.ap(), pb('A_log'))
        self.dma(self.D_b.ap(), pb('ssm_D'))
        self.act(self.A_b.ap(), self.A_b.ap(), AF.Exp)
        self.ts(self.A_b.ap(), self.A_b.ap(), -1.0, ALU.mult)
        D3 = self.D_b.ap().rearrange('p (g r) -> p g r', r=2)
        self.cp(self.Dfm[0:64, :], D3[0:64, :, 0], 'dve')
        self.cp(self.Dfm[64:128, :], D3[64:128, :, 1], 'dve')
        identf = self.cfv('ident')
        stg = self.cacc[0]
        def fm_params(dst, rows, G, scale_groups=None):
            R = sum(r for _, r in rows)
            for g0 in range(0, G, 4):
                gn = min(4, G - g0)
                r0 = 0
                for (nm, nr) in rows:
                    self.dma(stg[r0:r0 + nr, 0:gn * 128], I[nm][:, g0 * 128:(g0 + gn) * 128])
                    r0 += nr
                bank = self.ps[self.rot('setup', 2)]
                for g in range(gn):
                    self.tr(bank[:, g * R:(g + 1) * R], stg[0:R, g * 128:(g + 1) * 128], identf[0:R, 0:R])
                self.cp(dst[:, g0:g0 + gn, :], bank[:, 0:gn * R].rearrange('p (g r) -> p g r', r=R), 'act')
        fm_params(self.cwm, [('w_mconv', 4), ('b_mconv', 1), ('mnorm_g', 1)], 8)
        fm_params(self.cws, [('w_sconv', 4), ('b_sconv', 1)], 24)
        fm_params(self.cwf, [('w_fconv', 3), ('b_fconv', 1)], 44)
        sng3 = self.sng.ap().rearrange('p (g r) -> p g r', r=1)
        fm_params(sng3, [('snorm_g', 1)], 16)
        self.ts(self.cwm[:, :, 0:6], self.cwm[:, :, 0:6], 0.5, ALU.mult)
        self.ts(self.cws.ap(), self.cws.ap(), 0.5, ALU.mult)
        self.ts(self.cwf[:, 0:22, :], self.cwf[:, 0:22, :], 0.5, ALU.mult)
        self.memset(self.cst[:, 0:1], LN_EPS)
        self.memset(self.cst[:, 1:2], 0.5 * float(np.log(128.0)))
        self.memset(self.cst[:, 2:3], 1.0)
        self.eps_t = self.cst
        for b in (self.Cf, self.nf, self.m_b, self.STf, self.cq, self.cx, self.cff):
            self.memset(b.ap(), 0.0)
        for b in (self.Cb, self.nb, self.STb):
            self.memset(b.ap(), 0.0, 'pool')

    def kc(self, kind, n):
        if kind == 's':
            return dict(U=self.cfv('Us', 64), LS=self.cfv('LSs', 64), BO=self.cfv('BOs', 64),
                        M=self.cbv('Ms', 64), MT=self.cbv('MTs', 64))
        return dict(U=self.cfv('U', n, n), LS=self.cfv('LS', n, n), BO=self.cfv('ones', n, n),
                    M=self.cbv('M', n, n), MT=self.cbv('MT', n, n))

    def ln_load(self, gname, bname):
        I = self.I
        self.dma(self.lnc[:, 0, :], View(I[gname], I[gname].t.partition_broadcast(128)))
        self.dma(self.lnc[:, 1, :], View(I[bname], I[bname].t.partition_broadcast(128)))

    def ln_rows(self, x, n, slot):
        sc = self.lnsc[slot]
        st, mv, rs = sc[:n, 0:12], sc[:n, 12:14], sc[:n, 14:15]
        for i in range(2):
            self.S.op('dve', lambda e, i=i: e.bn_stats(out=sc.t[:n, i * 6:(i + 1) * 6], in_=x.ap[:, i * 512:(i + 1) * 512]),
                      reads=[x], writes=[sc])
        self.S.op('dve', lambda e: e.bn_aggr(out=sc.t[:n, 12:14], in_=sc.t[:n, 0:12]), reads=[sc], writes=[sc])
        self.act(rs, sc[:n, 13:14], AF.Ln, bias=self.eps_t[:n, 0:1])
        self.act(rs, rs, AF.Exp, scale=-0.5)
        self.ts(x, x, sc[:n, 12:13], ALU.subtract, rs, ALU.mult)
        self.tt(x, x, self.lnc[:n, 0, :], ALU.mult)
        self.tt(x, x, self.lnc[:n, 1, :], ALU.add)

    def to_fm(self, tl, src_of_chunk, dstT):
        identb = self.cbv('identb')
        for ch in tl.chunks:
            n = ch.n
            xb = self.xb16
            self.act(xb[:n, :], src_of_chunk(ch))
            bank = 6 + self.rot('tfm', 2)
            pv = self.psb(bank)
            for k in range(8):
                self.tr(pv[:, k * n:(k + 1) * n], xb[:n, k * 128:(k + 1) * 128], identb[:n, :n])
            self.cp(dstT[:, :, ch.col0:ch.col0 + n], pv[:, 0:8 * n].rearrange('p (k n) -> p k n', n=n), 'dve')

    def _conv_taps(self, tl, psv, W, wtab, g, carry_p, scar_view, E, acc):
        Wm = W - 1
        off = 0
        for (col0, nb, L, kind) in tl.segs:
            Ev = E[:, off:off + nb * (L + Wm)].rearrange('p (b l) -> p b l', b=nb)
            pseg = psv[:, col0:col0 + nb * L].rearrange('p (b l) -> p b l', b=nb)
            if kind == 'p':
                self.cp(Ev[:, :, 0:Wm], carry_p[:, g:g + 1, :], 'act')
            elif kind == 'm':
                self.memset(Ev[:, :, 0:Wm], 0.0, 'dve')
            else:
                self.cp(Ev[:, :, 0:Wm], scar_view[:, g, :, :], 'act')
            self.act(Ev[:, :, Wm:Wm + L], pseg)
            av = acc[:, col0:col0 + nb * L].rearrange('p (b l) -> p b l', b=nb)
            self.act(av, pseg, AF.Identity, scale=wtab[:, g, Wm:W], bias=wtab[:, g, W:W + 1])
            if kind == 's':
                self.cp(scar_view[:, g, :, :], Ev[:, :, L:L + Wm], 'act')
            else:
                self.cp(carry_p[:, g:g + 1, :], Ev[:, :, L:L + Wm], 'act')
            for j in range(Wm):
                self.stt(av, Ev[:, :, j:j + L], wtab[:, g, j:j + 1], av, ALU.mult, ALU.add)
            off += nb * (L + Wm)

    def conv_group(self, tl, psv, W, wtab, g, carry_p, scar_view, dst, final=True):
        E = self.cE[self.rot('cE', 2)]
        acc = self.cacc[self.rot('cacc', 3)]
        self._conv_taps(tl, psv, W, wtab, g, carry_p, scar_view, E, acc)
        T = tl.T
        if not final:
            return acc
        prev = getattr(self, '_conv_pending', None)

        def stage2(acc=acc, dst=dst, T=T):
            th = self.cth[self.rot('cth', 2)]
            self.act(th[:, 0:T], acc[:, 0:T], AF.Tanh)
            self.stt(dst, th[:, 0:T], 1.0, acc[:, 0:T], ALU.add, ALU.mult)
        self._conv_pending = stage2
        if prev is not None:
            prev()
        return acc

    def conv_flush(self):
        prev = getattr(self, '_conv_pending', None)
        self._conv_pending = None
        if prev is not None:
            prev()

    def carry_out(self, src, G, R, dst):
        identf = self.cfv('ident')
        for g0 in range(0, G, 4):
            gn = min(4, G - g0)
            bank = self.ps[self.rot('co', 2)]
            for g in range(gn):
                self.tr(bank[:R, g * 128:(g + 1) * 128], src[:, g0 + g, :], identf)
            stg = self.cacc[self.rot('cacc', 3)]
            self.cp(stg[:R, 0:gn * 128], bank[:R, 0:gn * 128], 'act')
            self.dma(dst[:, g0 * 128:(g0 + gn) * 128], stg[:R, 0:gn * 128])

    def scar_in(self, name, G, R):
        identf = self.cfv('ident')
        rows = 16 * R
        sv = self.scar[:, 0:G * rows].rearrange('p (g b r) -> p g b r', g=G, b=16)
        for g0 in range(0, G, 4):
            gn = min(4, G - g0)
            stg = self.cacc[self.rot('cacc', 3)]
            self.dma(stg[:rows, 0:gn * 128], self.I[name][:, g0 * 128:(g0 + gn) * 128])
            bank = self.ps[self.rot('co', 2)]
            for g in range(gn):
                self.tr(bank[:, g * rows:(g + 1) * rows], stg[:rows, g * 128:(g + 1) * 128], identf[:rows, :rows])
            self.cp(self.scar[:, g0 * rows:(g0 + gn) * rows], bank[:, 0:gn * rows], 'act')
        return sv

    def scar_out(self, name, G, R):
        rows = 16 * R
        src = self.scar[:, 0:G * rows].rearrange('p (g br) -> p g br', g=G)
        self.carry_out(src, G, rows, self.O[name].ap())

    def dense_fm(self, tl, actT, nkt_total, cb_group, kt0=0):
        Wb, (name, k0, nk, parts) = self.wnext()
        ncols = sum(p[1] for p in parts)
        T = tl.T
        for gl in range(ncols // 128):
            bank = self.ps[self.rot('mm', 4)]
            for k in range(nk):
                self.mm(bank[:, 0:T], Wb[:, k, gl * 128:(gl + 1) * 128], actT[:, k0 + k, 0:T],
                        start=(k0 + k == 0), stop=(k0 + k == nkt_total - 1))
            cb_group(gl, bank[:, 0:T])

    def dense_tm(self, tl, actT, cb_chunk):
        Wb, (name, k0, nk, parts) = self.wnext()
        ncols = sum(p[1] for p in parts)
        for ch in tl.chunks:
            bank = self.ps[self.rot('mm', 4)]
            for k in range(nk):
                self.mm(bank[:ch.n, 0:ncols], actT[:, k0 + k, ch.col0:ch.col0 + ch.n], Wb[:, k, 0:ncols],
                        start=(k == 0), stop=(k == nk - 1))
            cb_chunk(ch, bank[:ch.n, 0:ncols])

    def run_tile(self, tl, last):
        S, I, O = self.S, self.I, self.O
        T = tl.T
        isS = any(sg[3] == 's' for sg in tl.segs)
        if isS:
            S.alias_phase([self.xr[2], self.xr[3], self.zs[2], self.zs[3]], self.grpArena + self.grpArena2)
        for ch in tl.chunks:
            src = {'s': I['xs'].ap(), 'm': I['meta'].ap()}.get(ch.kind)
            if src is None:
                src = I['xp'][ch.row0:ch.row0 + ch.n, :]
            self.dma(self.xr[ch.slot][:ch.n, :], src)
        self.ln_load('ln0_g', 'ln0_b')
        for ch in tl.chunks:
            self.ln_rows(self.xr[ch.slot][:ch.n, :], ch.n, ch.slot)
        self.to_fm(tl, lambda ch: self.xr[ch.slot][:ch.n, :], self.xnT)
        for ch in tl.chunks:
            self.ts(self.xr[ch.slot][:ch.n, :], self.xr[ch.slot][:ch.n, :], ALPHA, ALU.mult)
        self.dump('xnT_' + tl.name, self.xnT[:, :, 0:T], [128, 8, T])
        if self.debug.get('stop') == 'p0':
            return
        S.alias_phase(self.grpA2fix + self.grpA2rec + self.grpB + self.grpA1rec, self.grpA1conv + self.grpA1fix)
        sq = self.scar_in('s_mconv', 8, 3) if isS else None
        for blk in range(4):
            def cbq(gl, psv, blk=blk):
                g = 2 * blk + gl
                self.conv_group(tl, psv, 4, self.cwm, g, self.cq, sq, self.qkT[:, g, 0:T])
            self.dense_fm(tl, self.xnT, 8, cbq)
        self.conv_flush()
        if isS:
            self.scar_out('o_mconv', 8, 3)
        if last:
            self.carry_out(self.cq.ap(), 8, 3, O['p_mconv'].ap())
        for blk in range(4):
            self.dense_tm(tl, self.xnT, lambda ch, psv, blk=blk: self.act(self.v[ch.slot][:ch.n, 256 * blk:256 * blk + 256], psv))
        for blk in range(4):
            self.dense_tm(tl, self.xnT, lambda ch, psv, blk=blk: self.act(self.oth[ch.slot][:ch.n, 256 * blk:256 * blk + 256], psv, AF.Tanh, scale=0.5))
        self.dense_tm(tl, self.xnT, lambda ch, psv: self.cp(self.ifdt[:ch.n, ch.slot, :], psv, 'dve'))
        for ch in tl.chunks:
            n, s = ch.n, ch.slot
            gi = self.gat[:n, s, 0:8]
            self.tt(gi, self.ifdt[:n, s, 0:8], self.bif_b[:n, :], ALU.add)
            e1 = self.sm[3]
            self.act(e1[:n, 0:4], self.gat[:n, s, 4:8], AF.Exp, scale=-1.0)
            self.act(e1[:n, 0:4], e1[:n, 0:4], AF.Ln, bias=self.cst[:n, 2:3])
            self.ts(self.gat[:n, s, 4:8], e1[:n, 0:4], -1.0, ALU.mult)
            d1 = self.sm[4]
            self.tt(d1[:n, 0:32], self.ifdt[:n, s, 8:40], self.dtb_b[:n, :], ALU.add)
            self.act(d1[:n, 0:32], d1[:n, 0:32], AF.Exp)
            self.act(self.dta[:n, s, 0:32], d1[:n, 0:32], AF.Ln, bias=self.cst[:n, 2:3])
            self.tt(self.dta[:n, s, 32:64], self.dta[:n, s, 0:32], self.A_b[:n, :], ALU.mult)
        self.dump('qkT_' + tl.name, self.qkT[:, :, 0:T], [128, 8, T])
        self.dump('gat_' + tl.name, self.gat.ap(), [128, 4, 8])
        self.dump('dta_' + tl.name, self.dta.ap(), [128, 4, 64])
        if self.debug.get('stop') == 'a1':
            return
        S.alias_phase(self.grpA1conv, self.grpA1rec)
        for ch in tl.chunks:
            self.mlstm_chunk(tl, ch, last)
        self.dump('hgT_' + tl.name, self.hgT[:, :, 0:T], [128, 8, T])
        if self.debug.get('stop') == 'mlstm':
            return
        S.alias_phase(self.grpA1rec + self.grpA1fix, self.grpA1conv + self.grpA2fix)
        for blk in range(8):
            def cbz(ch, psv, blk=blk):
                n = ch.n
                zc = self.cacc[self.rot('cacc', 3)]
                th = self.cth[self.rot('cth', 2)]
                self.cp(zc[:n, 0:256], psv, 'act')
                self.act(th[:n, 0:256], psv, AF.Tanh, scale=0.5)
                self.stt(self.zs[ch.slot][:n, 256 * blk:256 * blk + 256], th[:n, 0:256], 1.0, zc[:n, 0:256], ALU.add, ALU.mult)
            self.dense_tm(tl, self.xnT, cbz)
        if self.debug.get('stop') == 'a2z':
            return
        sx = self.scar_in('s_sconv', 24, 3) if isS else None
        for blk in range(12):
            def cbx(gl, psv, blk=blk):
                g = 2 * blk + gl
                self.conv_group(tl, psv, 4, self.cws, g, self.cx, sx, self.xbcT[:, g, 0:T])
            self.dense_fm(tl, self.xnT, 8, cbx)
        self.conv_flush()
        if isS:
            self.scar_out('o_sconv', 24, 3)
        if last:
            self.carry_out(self.cx.ap(), 24, 3, O['p_sconv'].ap())
        self.dump('xbcT_' + tl.name, self.xbcT[:, :, 0:T], [128, 24, T])
        if self.debug.get('stop') == 'a2':
            return
        S.alias_phase(self.grpA1conv, self.grpA2rec)
        for ch in tl.chunks:
            self.ssd_chunk(tl, ch, last)
        self.dump('ygT_' + tl.name, self.ygT[:, :, 0:T], [128, 16, T])
        if self.debug.get('stop') == 'ssd':
            return
        S.alias_phase(self.grpA2rec + self.grpA2fix, self.grpA1conv + self.grpB)
        for blk in range(8):
            def cbg(gl, psv, blk=blk):
                self.act(self.gth[:, 2 * blk + gl, 0:T], psv, AF.Tanh, scale=0.5)
            self.dense_fm(tl, self.xnT, 8, cbg)
        for j in range(4):
            Wb, (name, k0, nk, parts) = self.wnext()
            banksA = [self.ps[0], self.ps[1]]
            banksB = [self.ps[2], self.ps[3]]
            for gl in range(2):
                for k in range(8):
                    self.mm(banksA[gl][:, 0:T], Wb[:, k, gl * 128:(gl + 1) * 128], self.hgT[:, k, 0:T], start=(k == 0), stop=(k == 7))
            for half in range(2):
                Wb, (name, k0, nk, parts) = self.wnext()
                for gl in range(2):
                    for k in range(8):
                        kk = 8 * half + k
                        self.mm(banksB[gl][:, 0:T], Wb[:, k, gl * 128:(gl + 1) * 128], self.ygT[:, kk, 0:T], start=(kk == 0), stop=(kk == 15))
            for gl in range(2):
                g = 2 * j + gl
                m1 = self.cacc[self.rot('cacc', 3)]
                m2 = self.cacc2[self.rot('cacc2', 2)]
                self.stt(m1[:, 0:T], self.gth[:, g, 0:T], 1.0, banksA[gl][:, 0:T], ALU.add, ALU.mult)
                self.stt(m2[:, 0:T], self.gth[:, 8 + g, 0:T], 1.0, banksB[gl][:, 0:T], ALU.add, ALU.mult)
                self.tt(self.mixT[:, g, 0:T], m1[:, 0:T], m2[:, 0:T], ALU.add)
        self.rr['mm'] = 0
        for blk in range(4):
            def cbo(ch, psv, blk=blk):
                xv = self.xr[ch.slot][:ch.n, 256 * blk:256 * blk + 256]
                self.stt(xv, psv, 0.5, xv, ALU.mult, ALU.add)
            self.dense_tm(tl, self.mixT, cbo)
        self.ln_load('ln1_g', 'ln1_b')
        for ch in tl.chunks:
            self.ln_rows(self.xr[ch.slot][:ch.n, :], ch.n, ch.slot)
        self.dump('x1_' + tl.name, self.xr[0].ap(), [128, D])
        self.to_fm(tl, lambda ch: self.xr[ch.slot][:ch.n, :], self.xnT)
        for ch in tl.chunks:
            self.ts(self.xr[ch.slot][:ch.n, :], self.xr[ch.slot][:ch.n, :], ALPHA, ALU.mult)
        sf = self.scar_in('s_fconv', 44, 2) if isS else None
        for j in range(11):
            accs = {}
            def cbua(gl, psv, j=j):
                g = 2 * j + gl
                accs[gl] = self.conv_group(tl, psv, 3, self.cwf, g, self.cff, sf, None, final=False)
            self.dense_fm(tl, self.xnT, 8, cbua)
            def cbub(gl, psv, j=j):
                g = 2 * j + gl
                E = self.cE[self.rot('cE', 2)]
                accb = self.cacc2[self.rot('cacc2', 2)]
                self._conv_taps(tl, psv, 3, self.cwf, 22 + g, self.cff, sf, E, accb)
                th = self.cth[self.rot('cth', 2)]
                acca = accs[gl]
                self.act(th[:, 0:T], acca[:, 0:T], AF.Tanh)
                self.stt(th[:, 0:T], th[:, 0:T], 1.0, acca[:, 0:T], ALU.add, ALU.mult)
                self.tt(self.hffT[:, g, 0:T], th[:, 0:T], accb[:, 0:T], ALU.mult)
            self.dense_fm(tl, self.xnT, 8, cbub)
        if isS:
            self.scar_out('o_fconv', 44, 2)
        if last:
            self.carry_out(self.cff.ap(), 44, 2, O['p_fconv'].ap())
        self.dump('hffT_' + tl.name, self.hffT[:, :, 0:T], [128, 22, T])
        for blk in range(4):
            banks = {ch.slot: self.ps[ch.slot] for ch in tl.chunks}
            for (k0, nk) in ((0, 8), (8, 8), (16, 6)):
                Wb, meta = self.wnext()
                for ch in tl.chunks:
                    for k in range(nk):
                        self.mm(banks[ch.slot][:ch.n, 0:256], self.hffT[:, k0 + k, ch.col0:ch.col0 + ch.n], Wb[:, k, 0:256],
                                start=(k0 + k == 0), stop=(k0 + k == 21))
            for ch in tl.chunks:
                xv = self.xr[ch.slot][:ch.n, 256 * blk:256 * blk + 256]
                self.tt(xv, banks[ch.slot][:ch.n, 0:256], xv, ALU.add)
        self.ln_load('ln2_g', 'ln2_b')
        for ch in tl.chunks:
            self.ln_rows(self.xr[ch.slot][:ch.n, :], ch.n, ch.slot)
            if ch.kind == 'p':
                self.dma(O['y_p'][ch.row0:ch.row0 + ch.n, :], self.xr[ch.slot][:ch.n, :])
            elif ch.kind == 's':
                self.dma(O['y_s'].ap(), self.xr[ch.slot][:ch.n, :])

    def mlstm_chunk(self, tl, ch, last):
        I, O = self.I, self.O
        n, s, c0, kind = ch.n, ch.slot, ch.col0, ch.kind
        kc = self.kc(kind, n)
        U, LS, M, MT = kc['U'], kc['LS'], kc['M'], kc['MT']
        identf, onesf = self.cfv('ident', n, n), self.cfv('ones', n, n)
        identb, onesb = self.cbv('identb', n, n), self.cbv('onesb', n, n)
        ps = self.ps
        cols = slice(c0, c0 + n)
        ig = self.gat[:n, s, 0:4]
        lf = self.gat[:n, s, 4:8]
        sm = self.sm
        bt, mi, bm, mt, wi, emt, negm, den, rden, wi2 = (sm[i] for i in range(5, 15))
        R1, R2, R3, wT, ST, kTM, hh, vw = self.R1, self.R2, self.R3, self.wT, self.ST, self.kTM, self.hh, self.vw
        self.mm(ps[0][:n, 0:4], U, lf)
        for h in range(4):
            self.ts(R1[:n, h, :n], LS, self.gat[:n, s, 4 + h:5 + h], ALU.mult)
            self.ts(R2[:n, h, :n], identf, self.gat[:n, s, h:h + 1], ALU.mult, eng='pool')
        for h in range(4):
            self.mm(ps[1][:n, h * 128:h * 128 + n], U, R1[:n, h, :n], start=True, stop=False)
            self.mm(ps[1][:n, h * 128:h * 128 + n], onesf, R2[:n, h, :n], start=False, stop=False)
            self.mm(ps[1][:n, h * 128:h * 128 + n], identb, M, start=False, stop=True)
        self.rmax(mi[:n, 0:4], ps[1][:n, :].rearrange('p (h t) -> p h t', h=4)[:, :, 0:n])
        if kind == 's':
            m0s = sm[15]
            self.dma(m0s[:16, 0:4], I['s_m'].ap())
            self.mm(ps[0][:n, 4:8], self.cfv('BMT', 16), m0s[:16, 0:4])
            m0v = ps[0][:n, 4:8]
        else:
            m0v = self.m_b[:n, :]
        self.cp(bt[:n, 0:4], ps[0][:n, 0:4], 'act')
        self.tt(bm[:n, 0:4], bt[:n, 0:4], m0v, ALU.add)
        self.tt(mt[:n, 0:4], bm[:n, 0:4], mi[:n, 0:4], ALU.max)
        self.tt(bm[:n, 0:4], bm[:n, 0:4], mt[:n, 0:4], ALU.subtract)
        self.act(wi[:n, 0:4], bm[:n, 0:4], AF.Exp)
        self.act(emt[:n, 0:4], mt[:n, 0:4], AF.Exp, scale=-1.0, bias=self.cst[:n, 1:2])
        self.ts(negm[:n, 0:4], mt[:n, 0:4], -1.0, ALU.mult)
        for h in range(4):
            self.ts(R3[:n, h, :n], identf, negm[:n, h:h + 1], ALU.mult, eng='pool')
        for h in range(4):
            o = ps[2][:n, h * 128:h * 128 + n]
            self.mm(o, R1[:n, h, :n], U, start=True, stop=False)
            self.mm(o, R2[:n, h, :n], onesf, start=False, stop=False)
            self.mm(o, onesf, R3[:n, h, :n], start=False, stop=False)
            self.mm(o, identb, MT, start=False, stop=True)
        ps2v = ps[2][:n, :].rearrange('p (h t) -> p h t', h=4)[:, :, 0:n]
        self.act(wT[:n, :, :n], ps2v, AF.Exp)
        for h in range(4):
            self.mm(ps[1][:n, h * 128:h * 128 + n], self.qkT[:, 4 + h, cols], self.qkT[:, h, cols])
        ps1v = ps[1][:n, :].rearrange('p (h t) -> p h t', h=4)[:, :, 0:n]
        self.tt(ST[:n, :, :n], ps1v, wT[:n, :, :n], ALU.mult)
        pb7 = self.psb(7)
        for h in range(4):
            self.tr(pb7[:n, h * 128:(h + 1) * 128], self.qkT[:, 4 + h, cols], self.cbv('identb'))
        self.cp(kTM[:n, :, :], pb7[:n, 0:512].rearrange('p (h d) -> p h d', h=4), 'act')
        if kind == 's':
            self.mlstm_sample_states(tl, ch, wi, wT, kTM, mt)
        for h in range(4):
            self.mm(ps[3 + h // 2][:n, (h % 2) * 256:(h % 2) * 256 + 256], ST[:n, h, :n], self.v[s][:n, h * 256:(h + 1) * 256])
        for h in range(4):
            self.mm(ps[0][:n, 8 + h:9 + h], ST[:n, h, :n], onesb[:, 0:1])
        if kind != 's':
            for h in range(4):
                self.mm(ps[5 + h // 2][:n, (h % 2) * 256:(h % 2) * 256 + 256], self.qkT[:, h, cols], self.Cb[:, h, :])
            for h in range(4):
                self.mm(ps[0][:n, 12 + h:13 + h], self.qkT[:, h, cols], self.nb[:, h:h + 1])
        dint = ps[0][:n, 12:16] if kind != 's' else sm[12][:n, 0:4]
        self.tt(den[:n, 0:4], dint, wi[:n, 0:4], ALU.mult)
        self.tt(den[:n, 0:4], den[:n, 0:4], ps[0][:n, 8:12], ALU.add)
        self.ts(wi2[:n, 0:4], den[:n, 0:4], -1.0, ALU.mult)
        self.tt(den[:n, 0:4], den[:n, 0:4], wi2[:n, 0:4], ALU.max)
        self.tt(den[:n, 0:4], den[:n, 0:4], emt[:n, 0:4], ALU.max)
        self.S.op('dve', lambda e: e.reciprocal(out=rden.t[:n, 0:4], in_=den.t[:n, 0:4]), reads=[den], writes=[rden])
        self.tt(wi2[:n, 0:4], wi[:n, 0:4], rden[:n, 0:4], ALU.mult)
        for h in range(4):
            self.act(hh[:n, h, :], ps[3 + h // 2][:n, (h % 2) * 256:(h % 2) * 256 + 256], AF.Copy, scale=rden[:n, h:h + 1])
            if kind != 's':
                iv = ps[5 + h // 2][:n, (h % 2) * 256:(h % 2) * 256 + 256]
            else:
                iv = (R1 if h < 2 else R2)[:n, :, :].rearrange('p a b -> p (a b)')[:, (h % 2) * 256:(h % 2) * 256 + 256]
            self.stt(hh[:n, h, :], iv, wi2[:n, h:h + 1], hh[:n, h, :], ALU.mult, ALU.add)
        st, mv, rs = sm[0], sm[1], sm[2]
        for h in range(4):
            self.S.op('dve', lambda e, h=h: e.bn_stats(out=st.t[:n, h * 6:(h + 1) * 6], in_=hh.t[:n, h, :]), reads=[hh], writes=[st])
        for h in range(4):
            self.S.op('dve', lambda e, h=h: e.bn_aggr(out=mv.t[:n, 2 * h:2 * h + 2], in_=st.t[:n, h * 6:(h + 1) * 6]), reads=[st], writes=[mv])
        mvv = mv[:n, 0:8].rearrange('p (h t) -> p h t', t=2)
        self.act(rs[:n, 0:4], mvv[:, :, 1], AF.Ln, bias=self.cst[:n, 0:1])
        self.act(rs[:n, 0:4], rs[:n, 0:4], AF.Exp, scale=-0.5)
        for h in range(4):
            self.ts(hh[:n, h, :], hh[:n, h, :], mv[:n, 2 * h:2 * h + 1], ALU.subtract, rs[:n, h:h + 1], ALU.mult)
            self.stt(self.hgTM[:n, h * 256:(h + 1) * 256], self.oth[s][:n, h * 256:(h + 1) * 256], 1.0, hh[:n, h, :], ALU.add, ALU.mult)
        for k in range(8):
            self.tr(pb7[:, k * n:(k + 1) * n], self.hgTM[:n, k * 128:(k + 1) * 128], identb)
        self.cp(self.hgT[:, :, cols], pb7[:, 0:8 * n].rearrange('p (k t) -> p k t', k=8), 'dve')
        if kind != 's':
            wl16 = sm[15]
            self.cp(wl16.ap().bitcast(BF16)[:n, 0:4], wT[:n, :, n - 1], 'act')
            for h in range(4):
                self.ts(vw[:n, h, :], self.v[s][:n, h * 256:(h + 1) * 256], wT[:n, h, n - 1:n], ALU.mult)
            for h in range(4):
                self.mm(ps[5 + h // 2][:, (h % 2) * 256:(h % 2) * 256 + 256], kTM[:n, h, :], vw[:n, h, :])
            for h in range(4):
                self.mm(ps[0][:, 16 + h:17 + h], kTM[:n, h, :], wl16.ap().bitcast(BF16)[:n, h:h + 1])
            SEL = self.cfv('SELp' if n == 128 else 'SELm', n)
            self.mm(ps[0][:, 32:36], SEL, wi[:n, 0:4])
            self.mm(ps[0][:, 36:40], SEL, mt[:n, 0:4])
            dec = sm[3]
            self.cp(dec[:, 0:8], ps[0][:, 32:40], 'act')
            for h in range(4):
                self.stt(self.Cf[:, h, :], self.Cf[:, h, :], dec[:, h:h + 1], ps[5 + h // 2][:, (h % 2) * 256:(h % 2) * 256 + 256], ALU.mult, ALU.add)
            self.tt(self.nf.ap(), self.nf.ap(), dec[:, 0:4], ALU.mult)
            self.tt(self.nf.ap(), self.nf.ap(), ps[0][:, 16:20], ALU.add)
            self.cp(self.m_b.ap(), dec[:, 4:8], 'dve')
            self.cp(self.Cb.ap(), self.Cf.ap(), 'act')
            self.cp(self.nb.ap(), self.nf.ap(), 'pool')
            if ch.final:
                self.dma(O['p_C'].ap().rearrange('h d e -> d h e'), self.Cf.ap())
                identf128 = self.cfv('ident')
                self.tr(ps[0][:4, 128:256], self.nf.ap(), identf128)
                self.cp(self.pn_st[:4, :], ps[0][:4, 128:256], 'act')
                self.dma(O['p_n'].ap(), self.pn_st[:4, :])
                self.dma(O['p_m'].ap(), self.m_b[0:1, :])

    def mlstm_sample_states(self, tl, ch, wi, wT, kTM, mt):
        I, O, ps, sm = self.I, self.O, self.ps, self.sm
        n, s = 64, ch.slot
        identf = self.cfv('ident')
        RS, BM = self.cfv('RS', 64), self.cfv('BM', 64)
        CM = self.cbv('CM').rearrange('p (b j) -> p b j', b=16)
        vw = self.vw
        wl = sm[3]
        tmp = self.R3
        self.tt(tmp[:n, :, 0:64], wT[:n, :, 0:64], View(self.cf, self.cfv('LSEL', 64).ap.unsqueeze(1).to_broadcast([64, 4, 64])), ALU.mult)
        self.S.op('dve', lambda e: e.tensor_reduce(out=wl.t[:n, 0:4], in_=tmp.t[:n, :, 0:64], axis=AX.X, op=ALU.add), reads=[tmp], writes=[wl])
        wl16 = sm[4].ap().bitcast(BF16)
        self.cp(wl16[:n, 0:4], wl[:n, 0:4], 'act')
        for h in range(4):
            self.ts(vw[:n, h, :], self.v[s][:n, h * 256:(h + 1) * 256], wl[:n, h:h + 1], ALU.mult)
        Rm3 = self.Rm[:n, :].rearrange('p (b h) -> p b h', h=4)
        self.tt(Rm3, View(wi, wi.t[:n, 0:4].unsqueeze(1).to_broadcast([n, 16, 4])),
                View(self.cf, RS.ap.unsqueeze(2).to_broadcast([n, 16, 4])), ALU.mult)
        self.mm(ps[0][:, 64:128], self.cfv('ones', 64), self.Rm[:n, :])
        self.cp(self.decS.ap(), ps[0][:, 64:128], 'act')
        self.mm(ps[0][:16, 40:44], RS, mt[:n, 0:4])
        mo = sm[15]
        self.cp(mo[:16, 8:12], ps[0][:16, 40:44], 'act')
        self.dma(O['o_m'].ap(), mo[:16, 8:12])
        stg = self.pn_st
        self.dma(stg[:64, :], I['s_n'].ap())
        self.tr(ps[0][:, 192:256], stg[:64, :], identf[:64, :64])
        self.cp(self.n0T.ap(), ps[0][:, 192:256], 'act')
        self.cp(self.n16.ap(), self.n0T.ap(), 'pool')
        kTMflat = kTM[:n, :, :].rearrange('p h d -> p (h d)')
        for b in range(16):
            i = b % 2
            C0, C16, qm, km = self.C0b[i], self.C0b16[i], self.qmb[i], self.kTMm[i]
            self.dma(C0.ap(), I['s_C'][b].rearrange('h d e -> d h e'))
            self.cp(C16[:, :, 0:256], C0.ap(), 'act')
            self.cp(C16[:, :, 256:257], self.n16[:, 4 * b:4 * b + 4].rearrange('p (h o) -> p h o', o=1), 'dve')
            self.tt(qm.ap(), self.qkT[:, 0:4, ch.col0:ch.col0 + 64], View(self.cb, CM.ap[:, b, :].unsqueeze(1).to_broadcast([128, 4, 64])), ALU.mult)
            self.ts(km[:n, :, :].rearrange('p h d -> p (h d)'), kTMflat, BM[:, b:b + 1], ALU.mult)
            for h in range(4):
                self.mm(ps[3 + h][:n, 0:257], qm[:, h, :], C16[:, h, 0:257], start=(b == 0), stop=(b == 15))
            for h in range(4):
                self.mm(ps[1 + h // 2][:, (h % 2) * 256:(h % 2) * 256 + 256], km[:n, h, :], vw[:n, h, :])
            for h in range(4):
                self.mm(ps[0][:, 128 + 4 * b + h:129 + 4 * b + h], km[:n, h, :], wl16[:n, h:h + 1])
            for h in range(4):
                self.stt(C0[:, h, :], C0[:, h, :], self.decS[:, 4 * b + h:4 * b + h + 1],
                         ps[1 + h // 2][:, (h % 2) * 256:(h % 2) * 256 + 256], ALU.mult, ALU.add)
            self.dma(O['o_C'][b].rearrange('h d e -> d h e'), C0.ap())
        for h in range(4):
            dst = (self.R1 if h < 2 else self.R2)[:n, :, :].rearrange('p a b -> p (a b)')[:, (h % 2) * 256:(h % 2) * 256 + 256]
            self.cp(dst, ps[3 + h][:n, 0:256], 'act')
            self.cp(sm[12][:n, h:h + 1], ps[3 + h][:n, 256:257], 'act')
        self.tt(self.n0T.ap(), self.n0T.ap(), self.decS.ap(), ALU.mult)
        self.tt(self.n0T.ap(), self.n0T.ap(), ps[0][:, 128:192], ALU.add)
        self.tr(ps[0][:64, 256:384], self.n0T.ap(), identf)
        self.cp(stg[:64, :], ps[0][:64, 256:384], 'act')
        self.dma(O['o_n'].ap(), stg[:64, :])

    def ssd_chunk(self, tl, ch, last):
        I, O, ps, sm = self.I, self.O, self.ps, self.sm
        n, s, c0, kind = ch.n, ch.slot, ch.col0, ch.kind
        kc = self.kc(kind, n)
        U, LS, BO, MT = kc['U'], kc['LS'], kc['BO'], kc['MT']
        identb = self.cbv('identb')
        cols = slice(c0, c0 + n)
        dt = self.dta[:n, s, 0:32]
        a = self.dta[:n, s, 32:64]
        btsb, dect, wend, decS = sm[5], sm[6], sm[7], sm[8]
        self.mm(ps[0][:n, 0:32], U, a)
        self.mm(ps[0][:n, 32:64], BO, a)
        self.cp(btsb[:n, 0:32], ps[0][:n, 0:32], 'act')
        self.act(dect[:n, 0:32], ps[0][:n, 0:32], AF.Exp)
        self.tt(wend[:n, 0:32], ps[0][:n, 32:64], btsb[:n, 0:32], ALU.subtract)
        self.act(wend[:n, 0:32], wend[:n, 0:32], AF.Exp)
        if kind != 's':
            self.mm(ps[0][:, 64:96], self.cfv('ones', n), a)
            self.act(decS[:, 0:32], ps[0][:, 64:96], AF.Exp)
        S = self.S
        pb6, pb7 = self.psb(6), self.psb(7)
        xsTM = self.xdt
        for g in range(16):
            pv = pb6 if g < 8 else pb7
            self.tr(pv[:n, (g % 8) * 128:(g % 8 + 1) * 128], self.xbcT[:, g, cols], identb)
        self.act(xsTM[:n, 0:1024], pb6[:n, 0:1024])
        self.act(xsTM[:n, 1024:2048], pb7[:n, 0:1024])
        S.alias_phase([self.ynTM], [self.xsDT])
        for half in range(2):
            for g in range(8):
                self.ts(self.xsDT[:, g, 0:n], self.xbcT[:, 8 * half + g, cols], self.Dfm[:, 8 * half + g:8 * half + g + 1], ALU.mult)
            pv = pb6 if half == 0 else pb7
            for g in range(8):
                self.tr(pv[:n, g * 128:(g + 1) * 128], self.xsDT[:, g, 0:n], identb)
            self.cp(self.xsD[:n, 1024 * half:1024 * half + 1024], pv[:n, 0:1024], 'dve')
        for g in range(4):
            self.tr(pb6[:n, g * 128:(g + 1) * 128], self.xbcT[:, 16 + g, cols], identb)
        self.act(self.BTM[:n, :], pb6[:n, 0:512])
        dtw = sm[13]
        self.tt(dtw[:n, 0:32], dt, wend[:n, 0:32], ALU.mult)
        S.alias_phase([self.xsDT], [self.ynTM])
        if kind == 's':
            self.ssd_sample_states(tl, ch, dtw)
        S.alias_phase([self.ynTM], self.MTh[1])
        ssq = sm[9]
        self.memset(ssq[:n, 0:4], 0.0)
        def stageA(g):
            MTb = self.MTh[g % 2]
            for j in range(8):
                self.act(self.LAh[j][:n, :n], LS, AF.Copy, scale=self.dta[:n, s, 32 + 8 * g + j:33 + 8 * g + j])
            for j in range(8):
                o = ps[1 + j // 4][:n, (j % 4) * 128:(j % 4) * 128 + n]
                self.mm(o, self.LAh[j][:n, :n], U, start=True, stop=False)
                self.mm(o, identb[:n, :n], MT, start=False, stop=True)
            for half in range(2):
                self.act(self.LTh[half][:n, :, :n],
                         ps[1 + half][:n, :].rearrange('p (h t) -> p h t', h=4)[:, :, 0:n], AF.Exp)
            self.mm(ps[3][:n, 0:n], self.xbcT[:, 16 + g, cols], self.xbcT[:, 20 + g, cols])
            for j in range(8):
                self.stt(MTb[j][:n, :n], self.LTh[j // 4][:n, j % 4, :n], self.dta[:n, s, 8 * g + j:8 * g + j + 1], ps[3][:n, 0:n], ALU.mult, ALU.mult)

        def stageB(g):
            MTb = self.MTh[g % 2]
            for j in range(8):
                h = 8 * g + j
                self.mm(ps[4][:n, j * 64:(j + 1) * 64], MTb[j][:n, :n], xsTM[:n, h * 64:(h + 1) * 64])
            t1 = self.t1
            if kind != 's':
                self.mm(ps[5][:n, 0:512], self.xbcT[:, 20 + g, cols], self.STb[:, 512 * g:512 * g + 512])
                for j in range(4):
                    self.act(t1[:n, j * 64:(j + 1) * 64], ps[5][:n, j * 64:(j + 1) * 64], AF.Copy, scale=dect[:n, 8 * g + j:8 * g + j + 1])
                self.tt(t1[:n, 256:512].rearrange('p (h d) -> p d h', d=64), ps[5][:n, 256:512].rearrange('p (h d) -> p d h', d=64),
                        View(dect, dect.t[:n, 8 * g + 4:8 * g + 8].unsqueeze(1).to_broadcast([n, 64, 4])), ALU.mult)
            else:
                self.cp(t1[:n, :], self.ysi[:n, 512 * g:512 * g + 512], 'dve')
            self.tt(t1[:n, :], t1[:n, :], ps[4][:n, 0:512], ALU.add)
            self.tt(t1[:n, :], t1[:n, :], self.xsD[:n, 512 * g:512 * g + 512], ALU.add)
            self.tt(self.yz[:n, 512 * g:512 * g + 512], t1[:n, :], self.zs[s][:n, 512 * g:512 * g + 512], ALU.mult)

        stageA(0)
        for g in range(4):
            if g + 1 < 4:
                stageA(g + 1)
            stageB(g)
        S.alias_phase(self.MTh[1], [self.ynTM])
        for g in range(4):
            self.act(self.ynTM[:n, 512 * g:512 * g + 512], self.yz[:n, 512 * g:512 * g + 512], AF.Square, accum=ssq[:n, g:g + 1])
        rs = sm[10]
        self.act(rs[:n, 0:4], ssq[:n, 0:4], AF.Ln, scale=0.25 / 512.0, bias=self.cst[:n, 0:1])
        self.act(rs[:n, 0:4], rs[:n, 0:4], AF.Exp, scale=-0.5)
        self.ts(rs[:n, 0:4], rs[:n, 0:4], 0.5, ALU.mult)
        for g in range(4):
            self.ts(self.ynTM[:n, 512 * g:512 * g + 512], self.yz[:n, 512 * g:512 * g + 512], rs[:n, g:g + 1], ALU.mult)
        for k in range(16):
            pv = pb6 if k < 8 else pb7
            self.tr(pv[:, (k % 8) * n:(k % 8 + 1) * n], self.ynTM[:n, k * 128:(k + 1) * 128], identb[:n, :n])
        for half in range(2):
            pv = (pb6 if half == 0 else pb7)[:, 0:8 * n].rearrange('p (k t) -> p k t', k=8)
            self.cp(self.ygT[:, 8 * half:8 * half + 8, cols], pv, 'dve' if half == 0 else 'act')
        if kind != 's':
            S.alias_phase([self.ynTM], self.MTh[1])
            for g in range(4):
                hs = slice(8 * g, 8 * g + 8)
                bank = ps[3 + 2 * (g % 2)]
                Bs = self.MTh[g % 2]
                for j in range(8):
                    h = 8 * g + j
                    self.ts(Bs[j][:n, :], self.BTM[:n, g * 128:(g + 1) * 128], dtw[:n, h:h + 1], ALU.mult)
                for j in range(8):
                    h = 8 * g + j
                    self.mm(bank[:, j * 64:(j + 1) * 64], Bs[j][:n, :], xsTM[:n, h * 64:(h + 1) * 64])
                if g % 2 == 0:
                    for j in range(8):
                        c0_ = 512 * g + 64 * j
                        self.act(self.STf[:, c0_:c0_ + 64], self.STf[:, c0_:c0_ + 64], AF.Copy, scale=decS[:, 8 * g + j:8 * g + j + 1])
                else:
                    sv = self.STf[:, 512 * g:512 * g + 512].rearrange('p (h d) -> p d h', d=64)
                    self.tt(sv, sv, View(decS, decS.t[:, hs].unsqueeze(1).to_broadcast([128, 64, 8])), ALU.mult)
                self.tt(self.STf[:, 512 * g:512 * g + 512], self.STf[:, 512 * g:512 * g + 512], bank[:, 0:512], ALU.add)
            S.alias_phase(self.MTh[1], [self.ynTM])
            self.cp(self.STb.ap(), self.STf.ap(), 'act')
            if ch.final:
                identf = self.cfv('ident')
                for j in range(16):
                    bank = ps[1 + (j // 4) % 2]
                    self.tr(bank[:, (j % 4) * 128:(j % 4 + 1) * 128], self.STf[:, j * 128:(j + 1) * 128], identf)
                    if j % 4 == 3:
                        stg = self.LA[:, 4 * ((j // 4) % 2):4 * ((j // 4) % 2) + 4, :]
                        self.S.alias_phase(self.LAh, [self.LA])
                        self.cp(stg, bank[:, 0:512].rearrange('p (j n) -> p j n', j=4), 'act')
                        q = j // 4
                        self.dma(O['p_ssm'][512 * q:512 * q + 512, :].rearrange('(j p) n -> p j n', p=128), stg)
                self.S.alias_phase([self.LA], self.LAh)

    def ssd_sample_states(self, tl, ch, wend):
        I, O, ps, sm, S = self.I, self.O, self.ps, self.sm, self.S
        n, s = 64, ch.slot
        identf = self.cfv('ident')
        RS, BM = self.cfv('RS', 64), self.cfv('BM', 64)
        CM = self.cbv('CM').rearrange('p (b j) -> p b j', b=16)
        dect = sm[6]
        S.alias_phase(self.grpArena, self.grpArena2)
        S.alias_phase(self.LTh + self.MTh[0] + [self.t1, self.yz], [self.S0b[1]])
        blsb = sm[11]
        self.cp(blsb[:n, 0:32], ps[0][:n, 32:64], 'act')
        bl3 = blsb[:n, 0:32].rearrange('p (j r) -> p j r', r=2)
        for r in range(2):
            self.tt(self.Rr[:n, r, :].rearrange('p (b j) -> p b j', b=16),
                    View(blsb, bl3.ap[:, :, r].unsqueeze(1).to_broadcast([n, 16, 16])),
                    View(self.cf, RS.ap.unsqueeze(2).to_broadcast([n, 16, 16])), ALU.mult, eng='pool')
        self.mm(ps[0][:, 256:512], self.cfv('H0', 64), self.Rr[:n, 0, :], start=True, stop=False)
        self.mm(ps[0][:, 256:512], self.cfv('H1', 64), self.Rr[:n, 1, :], start=False, stop=True)
        self.act(self.decP.ap(), ps[0][:, 256:512], AF.Exp)
        wxA = self.ynTM
        self.tt(wxA[:n, :].rearrange('p (h d) -> p d h', d=64), self.xdt[:n, :].rearrange('p (h d) -> p d h', d=64),
                View(wend, wend.t[:n, 0:32].unsqueeze(1).to_broadcast([n, 64, 32])), ALU.mult)
        for b in range(16):
            Sb = self.S0b[b % 2]
            cm = self.CTmb[b % 2]
            self.dma(Sb.ap(), I['s_ssm'][b].rearrange('(j p) n -> p j n', p=128))
            self.tt(cm.ap(), self.xbcT[:, 20:24, ch.col0:ch.col0 + 64], View(self.cb, CM.ap[:, b, :].unsqueeze(1).to_broadcast([128, 4, 64])), ALU.mult)
            self.ts(self.wxm[:n, :], wxA[:n, :], BM[:, b:b + 1], ALU.mult)
            for q in range(4):
                bank = ps[5 + q % 2]
                for i in range(4):
                    self.tr(bank[:, i * 128:(i + 1) * 128], Sb[:, 4 * q + i, :], identf)
                self.cp(self.SbT[:, 512 * q:512 * q + 512], bank[:, 0:512], 'act')
            for g in range(4):
                self.mm(ps[1 + g][:n, 0:512], cm[:, g, :], self.SbT[:, 512 * g:512 * g + 512], start=(b == 0), stop=(b == 15))
            for q in range(4):
                bank = ps[7] if q % 2 == 0 else ps[0]
                for i in range(4):
                    j = 4 * q + i
                    self.mm(bank[:, i * 128:(i + 1) * 128], self.wxm[:n, j * 128:(j + 1) * 128], self.BTM[:n, q * 128:(q + 1) * 128])
                for i in range(4):
                    j = 4 * q + i
                    self.stt(Sb[:, j, :], Sb[:, j, :], self.decP[:, 16 * b + j:16 * b + j + 1], bank[:, i * 128:(i + 1) * 128], ALU.mult, ALU.add)
            self.dma(O['o_ssm'][b].rearrange('(j p) n -> p j n', p=128), Sb.ap())
        for g in range(4):
            self.tt(self.ysi[:n, 512 * g:512 * g + 512].rearrange('p (h d) -> p d h', d=64),
                    ps[1 + g][:n, 0:512].rearrange('p (h d) -> p d h', d=64),
                    View(dect, dect.t[:n, 8 * g:8 * g + 8].unsqueeze(1).to_broadcast([n, 64, 8])), ALU.mult)
        S.alias_phase([self.S0b[1]], self.LTh + self.MTh[0] + [self.t1, self.yz])


_CACHE = {}


def _get_kernel():
    if 'k' not in _CACHE:
        _CACHE['k'] = K()
    return _CACHE['k']


def make_in_maps(kb, inputs):
    f = lambda a: np.ascontiguousarray(np.asarray(a, dtype=np.float32))
    xp, xs = f(inputs['x_prompt']), f(inputs['x_sample'])
    shared = {'meta': f(inputs['meta_tokens']), 'cf': kb.cf_np, 'cb': kb.cb_np,
              'ln0_g': f(inputs['ln0_g']), 'ln0_b': f(inputs['ln0_b']),
              'b_if': f(inputs['b_mlstm_if'])[0], 'w_mconv': f(inputs['w_mlstm_conv'])[0],
              'b_mconv': f(inputs['b_mlstm_conv']), 'mnorm_g': f(inputs['mlstm_norm_g']),
              'w_sconv': f(inputs['w_ssm_conv'])[0], 'b_sconv': f(inputs['b_ssm_conv']),
              'dt_bias': f(inputs['ssm_dt_bias'])[0], 'A_log': f(inputs['ssm_A_log'])[0],
              'ssm_D': f(inputs['ssm_D'])[0], 'snorm_g': f(inputs['ssm_norm_g']),
              'ln1_g': f(inputs['ln1_g'])[0], 'ln1_b': f(inputs['ln1_b'])[0],
              'w_fconv': f(inputs['w_ffn_conv'])[0], 'b_fconv': f(inputs['b_ffn_conv']),
              'ln2_g': f(inputs['ln2_g'])[0], 'ln2_b': f(inputs['ln2_b'])[0],
              'w_in': f(inputs['w_in'])[0], 'w_proj_a': f(inputs['w_proj_a'])[0],
              'w_proj_b': f(inputs['w_proj_b'])[0], 'w_out': f(inputs['w_out'])[0],
              'w_up': f(inputs['w_up'])[0], 'w_down': f(inputs['w_down'])[0]}
    maps = []
    for c in range(8):
        b = slice(16 * c, 16 * c + 16)
        m = dict(shared)
        m['xp'] = xp[c]
        m['xs'] = xs[b].reshape(64, D)
        m['s_mconv'] = f(inputs['state_mlstm_conv'])[0, b].reshape(48, 1024)
        m['s_C'] = f(inputs['state_mlstm_C'])[0, b]
        m['s_n'] = f(inputs['state_mlstm_n'])[0, b].reshape(64, 128)
        m['s_m'] = f(inputs['state_mlstm_m'])[0, b]
        m['s_sconv'] = f(inputs['state_ssm_conv'])[0, b].reshape(48, 3072)
        m['s_ssm'] = f(inputs['state_ssm'])[0, b].reshape(16, 2048, 128)
        m['s_fconv'] = f(inputs['state_ffn_conv'])[0, b].reshape(32, 2 * DFF)
        maps.append(m)
    return maps


def kernel(**inputs):
    kb = _get_kernel()
    maps = make_in_maps(kb, inputs)
    res = run_bass_kernel_spmd(kb.nc, maps, core_ids=list(range(8)))
    R = res.results
    cat = lambda k: np.stack([np.asarray(r[k], dtype=np.float32) for r in R])
    y_p = cat('y_p')
    y_s = cat('y_s').reshape(128, 4, D)
    p_mconv = cat('p_mconv')[None]
    p_C = cat('p_C')[None]
    p_n = cat('p_n')[None]
    p_m = cat('p_m').reshape(8, 4)[None]
    p_sconv = cat('p_sconv')[None]
    p_ssm = cat('p_ssm').reshape(8, 32, 64, 128)[None]
    p_fconv = cat('p_fconv')[None]
    s_mconv = cat('o_mconv').reshape(128, 3, 1024)[None]
    s_C = cat('o_C').reshape(128, 4, 128, 256)[None]
    s_n = cat('o_n').reshape(128, 4, 128)[None]
    s_m = cat('o_m').reshape(128, 4)[None]
    s_sconv = cat('o_sconv').reshape(128, 3, 3072)[None]
    s_ssm = cat('o_ssm').reshape(128, 32, 64, 128)[None]
    s_fconv = cat('o_fconv').reshape(128, 2, 2 * DFF)[None]
    return (y_p, y_s, p_mconv, p_C, p_n, p_m, p_sconv, p_ssm, p_fconv,
            s_mconv, s_C, s_n, s_m, s_sconv, s_ssm, s_fconv)
```

```python
import numpy as np
import ml_dtypes
import concourse.bass as bass
import concourse.mybir as mybir
from concourse.bass_utils import run_bass_kernel_spmd

F32 = mybir.dt.float32
BF16 = mybir.dt.bfloat16
ALU = mybir.AluOpType
AF = mybir.ActivationFunctionType
AX = mybir.AxisListType

D = 1024
DIN = 10280
DFF = 2816
NEG = -30000.0
ALPHA = 2.0 ** 0.25
LN_EPS = 1e-5
RMS_EPS = 1e-5
QSCALE = 128.0 ** -0.5


class Buf:
    def __init__(self, name, t, space):
        self.name = name
        self.t = t
        self.space = space
        self.last_w = None
        self.readers = []
        self.sem_in = None
        self.cnt_in = 0
        self.sem_out = None
        self.cnt_out = 0

    def __getitem__(self, idx):
        return View(self, self.t[idx])

    def ap(self):
        return View(self, self.t[:] if self.space != 'dram' else self.t)


class View:
    def __init__(self, buf, ap):
        self.buf = buf
        self.ap = ap

    def __getitem__(self, idx):
        return View(self.buf, self.ap[idx])

    def rearrange(self, *a, **k):
        return View(self.buf, self.ap.rearrange(*a, **k))

    def bc(self, axis, shape):
        return View(self.buf, self.ap.unsqueeze(axis).to_broadcast(list(shape)))

    def bitcast(self, dt):
        return View(self.buf, self.ap.bitcast(dt))


def _bufs(vs):
    out = []
    for v in vs:
        if v is None or isinstance(v, (int, float)):
            continue
        b = v.buf if isinstance(v, View) else v
        if b not in out:
            out.append(b)
    return out


class Sched:
    ENGS = ('pe', 'act', 'dve', 'pool', 'sp')

    def __init__(self, nc):
        self.nc = nc
        self.sem = {e: nc.alloc_semaphore('sem_' + e) for e in self.ENGS}
        self.cnt = {e: 0 for e in self.ENGS}
        self.ops = {e: [] for e in self.ENGS}
        self.seen = {e: {} for e in self.ENGS}
        self.final_tokens = []
        self.sb_off = 16512
        self.sb_end = 229376
        self.nsem = 5

    def sbuf(self, name, shape, dtype, at=None):
        nbytes = int(np.prod(shape[1:])) * (2 if dtype == BF16 else 4)
        nbytes = (nbytes + 31) // 32 * 32
        if at is None:
            at = self.sb_off
            self.sb_off += nbytes
            assert self.sb_off <= self.sb_end, ('SBUF overflow', name, self.sb_off)
        t = self.nc.alloc_sbuf_tensor_at(name, list(shape), dtype, offset=at)
        b = Buf(name, t, 'sbuf')
        b.off = at
        b.nbytes = nbytes
        return b

    def psum(self, name, shape, dtype=F32):
        t = self.nc.alloc_psum_tensor(name, list(shape), dtype)
        return Buf(name, t, 'psum')

    def dram(self, name, shape, dtype, kind):
        t = self.nc.dram_tensor(name, list(shape), dtype, kind=kind)
        return Buf(name, t.ap(), 'dram')

    def alias_phase(self, old, new):
        toks = []
        for b in old:
            if b.last_w is not None:
                toks.append(b.last_w)
            toks.extend(b.readers)
        for b in new:
            b.readers = list(b.readers) + toks

    def _need(self, eng, waits, tok):
        sem, val, teng = tok
        key = id(sem)
        if self.seen[eng].get(key, 0) >= val:
            return
        if key not in waits or waits[key][1] < val:
            waits[key] = (sem, val)

    def _deps(self, eng, reads, writes):
        waits = {}
        for b in reads:
            tok = b.last_w
            if tok is not None and not (tok[2] == eng and eng == 'pe'):
                self._need(eng, waits, tok)
            if b.space == 'psum':
                for r in b.readers:
                    if r[2] != eng:
                        self._need(eng, waits, r)
        for b in writes:
            tok = b.last_w
            if tok is not None and not (tok[2] == eng and eng == 'pe'):
                self._need(eng, waits, tok)
            for r in b.readers:
                if not (r[2] == eng and eng == 'pe'):
                    self._need(eng, waits, r)
        for key, (sem, val) in waits.items():
            self.seen[eng][key] = val
        return list(waits.values())

    def op(self, eng, fn, reads=(), writes=()):
        reads = _bufs(reads)
        writes = _bufs(writes)
        waits = self._deps(eng, reads, writes)
        self.cnt[eng] += 1
        tok = (self.sem[eng], self.cnt[eng], eng)
        self.ops[eng].append((waits, fn, (self.sem[eng], 1)))
        for b in writes:
            b.last_w = tok
            b.readers = []
        for b in reads:
            if b not in writes:
                b.readers.append(tok)
        return tok

    def dma(self, q, out, in_, **kw):
        ob, ib = out.buf, in_.buf
        waits = self._deps(q, [ib], [ob])
        if ob.space != 'dram':
            if ob.sem_in is None:
                ob.sem_in = self.nc.alloc_semaphore('din_' + ob.name)
                self.nsem += 1
            ob.cnt_in += 16
            sem, val = ob.sem_in, ob.cnt_in
        else:
            if ib.sem_out is None:
                ib.sem_out = self.nc.alloc_semaphore('dout_' + ib.name)
                self.nsem += 1
            ib.cnt_out += 16
            sem, val = ib.sem_out, ib.cnt_out
        tok = (sem, val, 'dma')
        oap, iap = out.ap, in_.ap

        def fn(e, oap=oap, iap=iap, kw=kw):
            return e.dma_start(out=oap, in_=iap, **kw)
        self.ops[q].append((waits, fn, (sem, 16)))
        ob.last_w = tok
        ob.readers = []
        ib.readers.append(tok)
        if ob.space == 'dram':
            self.final_tokens.append(tok)
        return tok

    def emit(self):
        nc = self.nc
        last = {}
        for sem, val, _ in self.final_tokens:
            k = id(sem)
            if k not in last or last[k][1] < val:
                last[k] = (sem, val)
        fin = list(last.values())
        eng_obj = {'pe': 'tensor', 'act': 'scalar', 'dve': 'vector', 'pool': 'gpsimd', 'sp': 'sync'}
        with nc.Block() as block:
            def mk(eng):
                def body(e):
                    for waits, fn, inc in self.ops[eng]:
                        for sem, val in waits:
                            e.wait_ge(sem, val)
                        fn(e).then_inc(inc[0], inc[1])
                    if eng == 'sp':
                        for sem, val in fin:
                            e.wait_ge(sem, val)
                return body
            for eng, attr in eng_obj.items():
                getattr(block, attr)(mk(eng))


def _const_tables():
    p = np.arange(128)[:, None]
    j = np.arange(128)[None, :]
    f = {}
    f['ident'] = (p == j)
    f['ones'] = np.ones((128, 128))
    f['U'] = (p <= j)
    f['LS'] = (p > j)
    sb = (p // 4 == j // 4) & (p < 64) & (j < 64)
    f['Us'] = ((p <= j) & sb)[:, :64]
    f['LSs'] = ((p > j) & sb)[:, :64]
    f['BOs'] = sb[:, :64]
    f['SELp'] = np.repeat(p == 127, 128, axis=1)
    f['SELm'] = np.repeat(p == 15, 128, axis=1)
    b16 = np.arange(16)[None, :]
    f['RS'] = (p == 4 * b16 + 3)
    f['BM'] = (p // 4 == b16) & (p < 64)
    f['BMT'] = ((p < 16) & (j // 4 == p))[:, :64]
    f['LSEL'] = ((j == 4 * (p // 4) + 3) & (p < 64))[:, :64]
    f['H0'] = np.repeat(p < 64, 128, axis=1) & (j < 64)
    f['H1'] = np.repeat(p < 64, 128, axis=1) & (j >= 64)
    cf_off, cols = {}, []
    o = 0
    for k, v in f.items():
        cf_off[k] = (o, v.shape[1])
        o += v.shape[1]
        cols.append(v.astype(np.float32))
    cf = np.concatenate(cols, axis=1)
    g = {}
    g['identb'] = (p == j).astype(np.float32)
    g['onesb'] = np.ones((128, 128), np.float32)
    g['M'] = np.where(j <= p, 0.0, NEG)
    g['MT'] = np.where(p <= j, 0.0, NEG)
    g['Ms'] = np.where((j <= p) & sb, 0.0, NEG)[:, :64]
    g['MTs'] = np.where((p <= j) & sb, 0.0, NEG)[:, :64]
    jj = np.arange(64)[None, None, :]
    bb = np.arange(16)[None, :, None]
    g['CM'] = np.broadcast_to((jj // 4 == bb), (128, 16, 64)).reshape(128, 1024).astype(np.float32)
    cb_off, cols = {}, []
    o = 0
    for k, v in g.items():
        cb_off[k] = (o, v.shape[1])
        o += v.shape[1]
        cols.append(np.asarray(v, np.float32))
    cbm = np.concatenate(cols, axis=1).astype(ml_dtypes.bfloat16)
    return cf, cf_off, cbm, cb_off


class Chunk:
    def __init__(self, slot, col0, n, kind, row0=0):
        self.slot, self.col0, self.n, self.kind, self.row0 = slot, col0, n, kind, row0


class Tile:
    def __init__(self, name, T, chunks, segs):
        self.name, self.T, self.chunks, self.segs = name, T, chunks, segs


W_SHAPES = {'w_in': (D, DIN), 'w_proj_a': (D, D), 'w_proj_b': (2 * D, D), 'w_out': (D, D),
            'w_up': (D, 2 * DFF), 'w_down': (DFF, D)}


def tile_blocks():
    bl = []
    for c in range(0, 3072, 256):
        bl.append(('w_in', 0, 8, [(c, 256)]))
    bl.append(('w_in', 0, 8, [(3072, 8), (8200, 32)]))
    for c in range(3080, 5128, 256):
        bl.append(('w_in', 0, 8, [(c, 256)]))
    for c in range(5128, 8200, 256):
        bl.append(('w_in', 0, 8, [(c, 256)]))
    for c in range(8232, 10280, 256):
        bl.append(('w_in', 0, 8, [(c, 256)]))
    for j in range(4):
        bl.append(('w_proj_a', 0, 8, [(256 * j, 256)]))
        bl.append(('w_proj_b', 0, 8, [(256 * j, 256)]))
        bl.append(('w_proj_b', 8, 8, [(256 * j, 256)]))
    for j in range(4):
        bl.append(('w_out', 0, 8, [(256 * j, 256)]))
    for j in range(11):
        bl.append(('w_up', 0, 8, [(256 * j, 256)]))
        bl.append(('w_up', 0, 8, [(DFF + 256 * j, 256)]))
    for j in range(4):
        for k0, nk in ((0, 8), (8, 8), (16, 6)):
            bl.append(('w_down', k0, nk, [(256 * j, 256)]))
    return bl


class K:
    def __init__(self, debug=None, tiles=('T0', 'T1', 'T2', 'T3', 'T4')):
        self.debug = debug or {}
        self.tile_sel = tuple(tiles)
        self.ntiles = len(self.tile_sel)
        nc = bass.Bass('TRN2', target_bir_lowering=False)
        self.nc = nc
        self.S = S = Sched(nc)
        self.dumps = {}
        cf, self.cfo, cbm, self.cbo = _const_tables()
        self.cf_np, self.cb_np = cf, cbm
        din = lambda n, s, dt=F32: S.dram(n, s, dt, 'ExternalInput')
        dout = lambda n, s: S.dram(n, s, F32, 'ExternalOutput')
        I = self.I = {}
        I['xp'] = din('xp', [2048, D]); I['xs'] = din('xs', [64, D]); I['meta'] = din('meta', [16, D])
        I['s_mconv'] = din('s_mconv', [48, 1024]); I['s_C'] = din('s_C', [16, 4, 128, 256])
        I['s_n'] = din('s_n', [64, 128]); I['s_m'] = din('s_m', [16, 4])
        I['s_sconv'] = din('s_sconv', [48, 3072]); I['s_ssm'] = din('s_ssm', [16, 2048, 128])
        I['s_fconv'] = din('s_fconv', [32, 2 * DFF])
        I['cf'] = din('cf', list(cf.shape)); I['cb'] = din('cb', list(cbm.shape), BF16)
        for n, s in (('ln0_g', [D]), ('ln0_b', [D]), ('b_if', [8]), ('w_mconv', [4, 1024]), ('b_mconv', [1, 1024]),
                     ('mnorm_g', [1, 1024]), ('w_sconv', [4, 3072]), ('b_sconv', [1, 3072]), ('dt_bias', [32]),
                     ('A_log', [32]), ('ssm_D', [32]), ('snorm_g', [1, 2048]), ('ln1_g', [D]), ('ln1_b', [D]),
                     ('w_fconv', [3, 2 * DFF]), ('b_fconv', [1, 2 * DFF]), ('ln2_g', [D]), ('ln2_b', [D])):
            I[n] = din(n, s)
        for n, s in W_SHAPES.items():
            I[n] = din(n, list(s))
        O = self.O = {}
        O['y_p'] = dout('y_p', [2048, D]); O['y_s'] = dout('y_s', [64, D])
        O['p_mconv'] = dout('p_mconv', [3, 1024]); O['p_C'] = dout('p_C', [4, 128, 256])
        O['p_n'] = dout('p_n', [4, 128]); O['p_m'] = dout('p_m', [1, 4])
        O['p_sconv'] = dout('p_sconv', [3, 3072]); O['p_ssm'] = dout('p_ssm', [2048, 128])
        O['p_fconv'] = dout('p_fconv', [2, 2 * DFF])
        O['o_mconv'] = dout('o_mconv', [48, 1024]); O['o_C'] = dout('o_C', [16, 4, 128, 256])
        O['o_n'] = dout('o_n', [64, 128]); O['o_m'] = dout('o_m', [16, 4])
        O['o_sconv'] = dout('o_sconv', [48, 3072]); O['o_ssm'] = dout('o_ssm', [16, 2048, 128])
        O['o_fconv'] = dout('o_fconv', [32, 2 * DFF])
        self.rr = {}
        self.build()
        S.emit()

    def rot(self, key, n):
        i = self.rr.get(key, 0)
        self.rr[key] = i + 1
        return i % n

    def mm(self, out, lhsT, rhs, start=True, stop=True):
        self.S.op('pe', lambda e: e.matmul(out.ap, lhsT=lhsT.ap, rhs=rhs.ap, start=start, stop=stop),
                  reads=[lhsT, rhs], writes=[out])

    def tr(self, out, in_, ident):
        self.S.op('pe', lambda e: e.transpose(out=out.ap, in_=in_.ap, identity=ident.ap),
                  reads=[in_, ident], writes=[out])

    def act(self, out, in_, func=AF.Copy, bias=None, scale=None, accum=None):
        kw = {}
        if bias is not None:
            kw['bias'] = bias.ap if isinstance(bias, View) else bias
        if scale is not None:
            kw['scale'] = scale.ap if isinstance(scale, View) else scale
        if accum is not None:
            kw['accum_out'] = accum.ap
        self.S.op('act', lambda e: e.activation(out=out.ap, in_=in_.ap, func=func, **kw),
                  reads=[in_, bias, scale], writes=[out, accum])

    def tt(self, out, a, b, op, eng='dve'):
        self.S.op(eng, lambda e: e.tensor_tensor(out=out.ap, in0=a.ap, in1=b.ap, op=op),
                  reads=[a, b], writes=[out])

    def ts(self, out, a, s1, op0, s2=None, op1=None, eng='dve', accum=None):
        v1 = s1.ap if isinstance(s1, View) else s1
        v2 = s2.ap if isinstance(s2, View) else s2
        kw = {}
        if op1 is not None:
            kw['op1'] = op1
        if accum is not None:
            kw['accum_out'] = accum.ap
        self.S.op(eng, lambda e: e.tensor_scalar(out=out.ap, in0=a.ap, scalar1=v1, scalar2=v2, op0=op0, **kw),
                  reads=[a, s1, s2], writes=[out, accum])

    def stt(self, out, a, s, b, op0, op1, eng='dve'):
        v = s.ap if isinstance(s, View) else s
        self.S.op(eng, lambda e: e.scalar_tensor_tensor(out=out.ap, in0=a.ap, scalar=v, in1=b.ap, op0=op0, op1=op1),
                  reads=[a, s, b], writes=[out])

    def cp(self, out, in_, eng='dve'):
        if eng == 'act':
            return self.act(out, in_)
        self.S.op(eng, lambda e: e.tensor_copy(out=out.ap, in_=in_.ap), reads=[in_], writes=[out])

    def memset(self, out, val, eng='dve'):
        self.S.op(eng, lambda e: e.memset(out.ap, val), writes=[out])

    def rmax(self, out, in_, eng='dve'):
        self.S.op(eng, lambda e: e.tensor_reduce(out=out.ap, in_=in_.ap, axis=AX.X, op=ALU.max),
                  reads=[in_], writes=[out])

    def dma(self, out, in_, q='sp'):
        self.S.dma(q, out, in_)

    def dump(self, name, view, shape):
        if name not in self.debug:
            return
        d = self.S.dram('dbg_' + name, list(shape), view.ap.dtype, 'ExternalOutput')
        self.dumps[name] = d
        self.dma(d.ap(), view)

    def cfv(self, name, rows=128, cols=None):
        o, w = self.cfo[name]
        cols = w if cols is None else cols
        return self.cf[:rows, o:o + cols]

    def cbv(self, name, rows=128, cols=None):
        o, w = self.cbo[name]
        cols = w if cols is None else cols
        return self.cb[:rows, o:o + cols]

    def ws_init(self):
        S = self.S
        self.wlist = tile_blocks()
        self.nbt = len(self.wlist)
        self.wblocks = self.wlist * self.ntiles
        self.wst = [S.sbuf(f'wst{i}', [128, 8, 256], F32) for i in range(2)]
        self.wbf = [S.sbuf(f'wbf{i}', [128, 8, 256], BF16) for i in range(2)]
        self.wx4 = [S.sbuf(f'wrx{i}', [128, 8, 256], BF16, at=self.wst[i // 2].off + 4096 * (i % 2)) for i in range(4)]
        self.wring = self.wbf + self.wx4
        self.wscr = [S.dram(f'wscr{j}', [128, 8, 256], BF16, 'Internal') for j in range(self.nbt)] if self.ntiles > 1 else None
        self.w_loaded = 0
        self.w_cast = 0
        self.w_next = 0
        self.w_ring_started = False

    def _w_load(self, i):
        name, k0, nk, parts = self.wblocks[i]
        st = self.wst[i % 2]
        W = self.I[name]
        c = 0
        for (c0, n) in parts:
            src = View(W, W.t[k0 * 128:(k0 + nk) * 128, c0:c0 + n].rearrange('(k p) c -> p k c', p=128))
            self.dma(st[:, 0:nk, c:c + n], src, q='sp')
            c += n

    def _w_castop(self, i):
        name, k0, nk, parts = self.wblocks[i]
        n = sum(p[1] for p in parts)
        eng = 'dve' if (i % 4) != 3 else 'act'
        if name in ('w_proj_a', 'w_proj_b'):
            for k in range(nk):
                gcol = self.cwm[:, k0 + k, 5:6] if name == 'w_proj_a' else self.sng[:, k0 + k:k0 + k + 1]
                self.ts(self.wbf[i % 2][:, k, 0:n], self.wst[i % 2][:, k, 0:n], gcol, ALU.mult)
        else:
            self.cp(self.wbf[i % 2][:, 0:nk, 0:n], self.wst[i % 2][:, 0:nk, 0:n], eng)
        if self.wscr is not None:
            self.dma(self.wscr[i][:, 0:nk, 0:n], self.wbf[i % 2][:, 0:nk, 0:n], q='pool')

    def _w_ringload(self, i):
        name, k0, nk, parts = self.wblocks[i]
        n = sum(p[1] for p in parts)
        dst = self.wring[(i - self.nbt) % 6]
        self.dma(dst[:, 0:nk, 0:n], self.wscr[i % self.nbt][:, 0:nk, 0:n], q='sp')

    def wnext(self):
        i = self.w_next
        nb = len(self.wblocks)
        self.w_next += 1
        if i < self.nbt:
            lim = self.nbt
            while self.w_loaded < min(lim, i + 2):
                self._w_load(self.w_loaded)
                self.w_loaded += 1
            while self.w_cast < min(lim, i + 2):
                self._w_castop(self.w_cast)
                self.w_cast += 1
            while self.w_loaded < min(lim, i + 3):
                self._w_load(self.w_loaded)
                self.w_loaded += 1
            return self.wbf[i % 2], self.wblocks[i]
        if not self.w_ring_started:
            self.w_ring_started = True
            self.S.alias_phase(self.wst, self.wx4)
            self.w_loaded = self.nbt
        while self.w_loaded < min(nb, i + 6):
            self._w_ringload(self.w_loaded)
            self.w_loaded += 1
        return self.wring[(i - self.nbt) % 6], self.wblocks[i]

    def build(self):
        S = self.S
        sb = S.sbuf
        ncf, ncb = self.cf_np.shape[1], self.cb_np.shape[1]
        self.cf = sb('cf', [128, ncf], F32)
        self.cb = sb('cb', [128, ncb], BF16)
        self.lnc = sb('lnc', [128, 2, D], F32)
        self.bif_b = sb('bif_b', [128, 8], F32)
        self.dtb_b = sb('dtb_b', [128, 32], F32)
        self.A_b = sb('A_b', [128, 32], F32)
        self.D_b = sb('D_b', [128, 32], F32)
        self.Dfm = sb('Dfm', [128, 16], F32)
        self.cwm = sb('cwm', [128, 8, 6], F32)
        self.cws = sb('cws', [128, 24, 5], F32)
        self.sng = sb('sng', [128, 16], F32)
        self.cwf = sb('cwf', [128, 44, 4], F32)
        self.ws_init()
        self.xr = [sb(f'xr{i}', [128, D], F32) for i in range(4)]
        self.zs = [None] * 4
        self.xnT = sb('xnT', [128, 8, 512], BF16)
        self.hgT = sb('hgT', [128, 8, 512], BF16)
        self.ygT = sb('ygT', [128, 16, 512], BF16)
        self.Cf = sb('Cf', [128, 4, 256], F32); self.Cb = sb('Cb', [128, 4, 256], BF16)
        self.nf = sb('nf', [128, 4], F32); self.nb = sb('nb', [128, 4], BF16)
        self.m_b = sb('m_b', [128, 4], F32)
        self.STf = sb('STf', [128, 2048], F32); self.STb = sb('STb', [128, 2048], BF16)
        self.cq = sb('cq', [128, 8, 3], F32); self.cx = sb('cx', [128, 24, 3], F32)
        self.cff = sb('cff', [128, 44, 2], F32)
        self.scar = sb('scar', [128, 44 * 16 * 2], F32)
        self.gat = sb('gat', [128, 4, 8], F32)
        self.dta = sb('dta', [128, 4, 64], F32)
        self.ifdt = sb('ifdt', [128, 4, 40], F32)
        self.sm = [sb(f'sm{i}', [128, 32], F32) for i in range(16)]
        self.xb16 = sb('xb16', [128, D], BF16)
        self.lnsc = [sb(f'lnsc{i}', [128, 16], F32) for i in range(4)]
        self.cst = sb('cst', [128, 8], F32)
        self.pn_st = sb('pn_st', [128, 128], F32)
        R0 = S.sb_off
        o = R0
        def at(name, shape, dt):
            nonlocal o
            b = sb(name, shape, dt, at=o)
            o += b.nbytes
            return b
        self.cE = [at(f'cE{i}', [128, 520], F32) for i in range(2)]
        self.cacc = [at(f'cacc{i}', [128, 512], F32) for i in range(3)]
        self.cth = [at(f'cth{i}', [128, 512], F32) for i in range(2)]
        self.cacc2 = [at(f'cacc2_{i}', [128, 512], F32) for i in range(2)]
        e1 = o
        o = R0
        self.R1 = at('R1', [128, 4, 128], F32); self.R2 = at('R2', [128, 4, 128], F32)
        self.R3 = at('R3', [128, 4, 128], F32); self.wT = at('wT', [128, 4, 128], F32)
        self.ST = at('ST', [128, 4, 128], BF16); self.kTM = at('kTM', [128, 4, 128], BF16)
        self.hh = at('hh', [128, 4, 256], F32); self.vw = at('vw', [128, 4, 256], BF16)
        self.hgTM = at('hgTM', [128, D], BF16)
        e2 = o
        o = R0
        self.xdt = at('xdt', [128, 2048], BF16); self.xsD = at('xsD', [128, 2048], BF16)
        self.wx = at('wx', [128, 512], BF16); self.BTM = at('BTM', [128, 512], BF16)
        self.LT = at('LT', [128, 8, 128], BF16); self.MTt = at('MTt', [128, 8, 128], BF16)
        self.t1 = at('t1', [128, 512], F32)
        self.yz = at('yz', [128, 2048], BF16)
        self.ynTM = at('ynTM', [128, 2048], BF16)
        e3 = o
        F0 = max(e1, e2, e3)
        conv_end = F0
        o = F0
        self.qkT = at('qkT', [128, 8, 512], BF16)
        self.v = [at(f'v{i}', [128, D], BF16) for i in range(4)]
        self.oth = [at(f'oth{i}', [128, D], BF16) for i in range(4)]
        a1_end = o
        o = F0
        self.xbcT = at('xbcT', [128, 24, 512], BF16)
        for i in range(2):
            self.zs[i] = at(f'zs{i}', [128, 2048], BF16)
        self.LA = at('LA', [128, 8, 128], F32)
        a2_end = o
        o = F0
        self.gth = at('gth', [128, 16, 512], BF16)
        self.mixT = at('mixT', [128, 8, 512], BF16)
        self.hffT = at('hffT', [128, 22, 512], BF16)
        b_end = o
        S.sb_off = max(a1_end, a2_end, b_end)
        for i in range(2, 4):
            self.zs[i] = sb(f'zs{i}', [128, 2048], BF16)
        self.arenas = [(self.xr[2].off, 2 * self.xr[2].nbytes), (self.zs[2].off, 2 * self.zs[2].nbytes)]
        a0, a1 = self.arenas[0][0], self.arenas[1][0]
        self.C0b = [sb(f'C0b{i}', [128, 4, 256], F32, at=a0 + 4096 * i) for i in range(2)]
        self.C0b16 = [sb(f'C0b16_{i}', [128, 4, 258], BF16, at=a1 + 2080 * i) for i in range(2)]
        self.qmb = [sb(f'qmb{i}', [128, 4, 64], BF16, at=a1 + 4160 + 512 * i) for i in range(2)]
        self.kTMm = [sb(f'kTMm{i}', [128, 4, 128], BF16, at=a1 + 5184 + 1024 * i) for i in range(2)]
        self.n0T = sb('n0T', [128, 64], F32, at=a1 + 7232)
        self.n16 = sb('n16', [128, 64], BF16, at=a1 + 7488)
        self.decS = sb('decS', [128, 64], F32, at=a1 + 7616)
        self.Rm = sb('Rm', [128, 64], F32, at=a1 + 7872)
        self.S0b = [sb('S0b0', [128, 16, 128], F32, at=a0), sb('S0b1', [128, 16, 128], F32, at=self.LT.off)]
        assert self.LT.off + 8192 <= self.ynTM.off
        self.SbT = sb('SbT', [128, 2048], BF16, at=a1)
        self.wxm = sb('wxm', [128, 2048], BF16, at=a1 + 4096)
        self.ysi = sb('ysi', [128, 2048], BF16)
        self.decP = sb('decP', [128, 256], F32)
        self.Rr = sb('Rr', [128, 2, 256], F32)
        self.CTmb = [sb(f'CTmb{i}', [128, 4, 64], BF16) for i in range(2)]
        self.grpArena2 = [self.S0b[0], self.SbT, self.wxm]
        self.xsDT = sb('xsDT', [128, 8, 128], BF16, at=self.ynTM.off)
        self.LAh = [sb(f'LAh{j}', [128, 128], F32, at=self.LA.off + 512 * j) for j in range(8)]
        self.LTh = [sb(f'LTh{j}', [128, 4, 128], BF16, at=self.LT.off + 1024 * j) for j in range(2)]
        self.MTh = [[sb(f'MTh{b}_{j}', [128, 128], BF16, at=base + 256 * j) for j in range(8)]
                    for b, base in enumerate((self.MTt.off, self.ynTM.off + 2048))]
        self.grpArena = self.C0b + self.C0b16 + self.qmb + self.kTMm + [self.n0T, self.n16, self.decS, self.Rm]
        self.grpA1conv = self.cE + self.cacc + self.cth + self.cacc2
        self.grpA1rec = [self.R1, self.R2, self.R3, self.wT, self.ST, self.kTM, self.hh, self.vw, self.hgTM]
        self.grpA1fix = [self.qkT] + self.v + self.oth
        self.grpA2fix = [self.xbcT, self.zs[0], self.zs[1]] + self.LAh
        self.grpA2rec = [self.xdt, self.xsD, self.wx, self.BTM, self.t1, self.yz, self.ynTM] + self.LTh + self.MTh[0]
        self.grpB = [self.gth, self.mixT, self.hffT]
        print('SBUF used', S.sb_off, 'of', S.sb_end, 'R', R0, conv_end - R0, a1_end - R0, a2_end - R0, b_end - R0)
        self.ps = [S.psum(f'ps{i}', [128, 512], F32) for i in range(8)]
        self.setup()
        tiles = self.make_tiles()
        for tl in tiles:
            if tl.name in self.tile_sel:
                self.run_tile(tl, last=(tl.name == 'T4'))

    def make_tiles(self):
        def pch(slot, col0, c):
            ch = Chunk(slot, col0, 128, 'p', row0=128 * c)
            ch.final = (c == 15)
            return ch
        m = Chunk(0, 0, 16, 'm'); m.final = False
        tiles = [Tile('T0', 400, [m] + [pch(1 + i, 16 + 128 * i, i) for i in range(3)], [(0, 1, 400, 'p')])]
        for t in range(3):
            tiles.append(Tile(f'T{t + 1}', 512, [pch(i, 128 * i, 3 + 4 * t + i) for i in range(4)], [(0, 1, 512, 'p')]))
        sc = Chunk(1, 128, 64, 's'); sc.final = False
        tiles.append(Tile('T4', 192, [pch(0, 0, 15), sc], [(0, 1, 128, 'p'), (128, 16, 4, 's')]))
        return tiles

    def psb(self, i):
        return self.ps[i].ap().bitcast(BF16)

    def setup(self):
        I = self.I
        self.dma(self.cf.ap(), I['cf'].ap())
        self.dma(self.cb.ap(), I['cb'].ap())
        pb = lambda n: View(I[n], I[n].t.partition_broadcast(128))
        self.dma(self.bif_b.ap(), pb('b_if'))
        self.dma(self.dtb_b.ap(), pb('dt_bias'))
        self.dma(self.A_b.ap(), pb('A_log'))
        self.dma(self.D_b.ap(), pb('ssm_D'))
        self.act(self.A_b.ap(), self.A_b.ap(), AF.Exp)
        self.ts(self.A_b.ap(), self.A_b.ap(), -1.0, ALU.mult)
        D3 = self.D_b.ap().rearrange('p (g r) -> p g r', r=2)
        self.cp(self.Dfm[0:64, :], D3[0:64, :, 0], 'dve')
        self.cp(self.Dfm[64:128, :], D3[64:128, :, 1], 'dve')
        identf = self.cfv('ident')
        stg = self.cacc[0]
        def fm_params(dst, rows, G, scale_groups=None):
            R = sum(r for _, r in rows)
            for g0 in range(0, G, 4):
                gn = min(4, G - g0)
                r0 = 0
                for (nm, nr) in rows:
                    self.dma(stg[r0:r0 + nr, 0:gn * 128], I[nm][:, g0 * 128:(g0 + gn) * 128])
                    r0 += nr
                bank = self.ps[self.rot('setup', 2)]
                for g in range(gn):
                    self.tr(bank[:, g * R:(g + 1) * R], stg[0:R, g * 128:(g + 1) * 128], identf[0:R, 0:R])
                self.cp(dst[:, g0:g0 + gn, :], bank[:, 0:gn * R].rearrange('p (g r) -> p g r', r=R), 'act')
        fm_params(self.cwm, [('w_mconv', 4), ('b_mconv', 1), ('mnorm_g', 1)], 8)
        fm_params(self.cws, [('w_sconv', 4), ('b_sconv', 1)], 24)
        fm_params(self.cwf, [('w_fconv', 3), ('b_fconv', 1)], 44)
        sng3 = self.sng.ap().rearrange('p (g r) -> p g r', r=1)
        fm_params(sng3, [('snorm_g', 1)], 16)
        self.ts(self.cwm[:, :, 0:6], self.cwm[:, :, 0:6], 0.5, ALU.mult)
        self.ts(self.cws.ap(), self.cws.ap(), 0.5, ALU.mult)
        self.ts(self.cwf[:, 0:22, :], self.cwf[:, 0:22, :], 0.5, ALU.mult)
        self.memset(self.cst[:, 0:1], LN_EPS)
        self.memset(self.cst[:, 1:2], 0.5 * float(np.log(128.0)))
        self.memset(self.cst[:, 2:3], 1.0)
        self.eps_t = self.cst
        for b in (self.Cf, self.nf, self.m_b, self.STf, self.cq, self.cx, self.cff):
            self.memset(b.ap(), 0.0)
        for b in (self.Cb, self.nb, self.STb):
            self.memset(b.ap(), 0.0, 'pool')

    def kc(self, kind, n):
        if kind == 's':
            return dict(U=self.cfv('Us', 64), LS=self.cfv('LSs', 64), BO=self.cfv('BOs', 64),
                        M=self.cbv('Ms', 64), MT=self.cbv('MTs', 64))
        return dict(U=self.cfv('U', n, n), LS=self.cfv('LS', n, n), BO=self.cfv('ones', n, n),
                    M=self.cbv('M', n, n), MT=self.cbv('MT', n, n))

    def ln_load(self, gname, bname):
        I = self.I
        self.dma(self.lnc[:, 0, :], View(I[gname], I[gname].t.partition_broadcast(128)))
        self.dma(self.lnc[:, 1, :], View(I[bname], I[bname].t.partition_broadcast(128)))

    def ln_rows(self, x, n, slot):
        sc = self.lnsc[slot]
        st, mv, rs = sc[:n, 0:12], sc[:n, 12:14], sc[:n, 14:15]
        for i in range(2):
            self.S.op('dve', lambda e, i=i: e.bn_stats(out=sc.t[:n, i * 6:(i + 1) * 6], in_=x.ap[:, i * 512:(i + 1) * 512]),
                      reads=[x], writes=[sc])
        self.S.op('dve', lambda e: e.bn_aggr(out=sc.t[:n, 12:14], in_=sc.t[:n, 0:12]), reads=[sc], writes=[sc])
        self.act(rs, sc[:n, 13:14], AF.Ln, bias=self.eps_t[:n, 0:1])
        self.act(rs, rs, AF.Exp, scale=-0.5)
        self.ts(x, x, sc[:n, 12:13], ALU.subtract, rs, ALU.mult)
        self.tt(x, x, self.lnc[:n, 0, :], ALU.mult)
        self.tt(x, x, self.lnc[:n, 1, :], ALU.add)

    def to_fm(self, tl, src_of_chunk, dstT):
        identb = self.cbv('identb')
        for ch in tl.chunks:
            n = ch.n
            xb = self.xb16
            self.act(xb[:n, :], src_of_chunk(ch))
            bank = 6 + self.rot('tfm', 2)
            pv = self.psb(bank)
            for k in range(8):
                self.tr(pv[:, k * n:(k + 1) * n], xb[:n, k * 128:(k + 1) * 128], identb[:n, :n])
            self.cp(dstT[:, :, ch.col0:ch.col0 + n], pv[:, 0:8 * n].rearrange('p (k n) -> p k n', n=n), 'dve')

    def _conv_taps(self, tl, psv, W, wtab, g, carry_p, scar_view, E, acc):
        Wm = W - 1
        off = 0
        for (col0, nb, L, kind) in tl.segs:
            Ev = E[:, off:off + nb * (L + Wm)].rearrange('p (b l) -> p b l', b=nb)
            pseg = psv[:, col0:col0 + nb * L].rearrange('p (b l) -> p b l', b=nb)
            if kind == 'p':
                self.cp(Ev[:, :, 0:Wm], carry_p[:, g:g + 1, :], 'act')
            elif kind == 'm':
                self.memset(Ev[:, :, 0:Wm], 0.0, 'dve')
            else:
                self.cp(Ev[:, :, 0:Wm], scar_view[:, g, :, :], 'act')
            self.act(Ev[:, :, Wm:Wm + L], pseg)
            av = acc[:, col0:col0 + nb * L].rearrange('p (b l) -> p b l', b=nb)
            self.act(av, pseg, AF.Identity, scale=wtab[:, g, Wm:W], bias=wtab[:, g, W:W + 1])
            if kind == 's':
                self.cp(scar_view[:, g, :, :], Ev[:, :, L:L + Wm], 'act')
            else:
                self.cp(carry_p[:, g:g + 1, :], Ev[:, :, L:L + Wm], 'act')
            for j in range(Wm):
                self.stt(av, Ev[:, :, j:j + L], wtab[:, g, j:j + 1], av, ALU.mult, ALU.add)
            off += nb * (L + Wm)

    def conv_group(self, tl, psv, W, wtab, g, carry_p, scar_view, dst, final=True):
        E = self.cE[self.rot('cE', 2)]
        acc = self.cacc[self.rot('cacc', 3)]
        self._conv_taps(tl, psv, W, wtab, g, carry_p, scar_view, E, acc)
        T = tl.T
        if not final:
            return acc
        prev = getattr(self, '_conv_pending', None)

        def stage2(acc=acc, dst=dst, T=T):
            th = self.cth[self.rot('cth', 2)]
            self.act(th[:, 0:T], acc[:, 0:T], AF.Tanh)
            self.stt(dst, th[:, 0:T], 1.0, acc[:, 0:T], ALU.add, ALU.mult)
        self._conv_pending = stage2
        if prev is not None:
            prev()
        return acc

    def conv_flush(self):
        prev = getattr(self, '_conv_pending', None)
        self._conv_pending = None
        if prev is not None:
            prev()

    def carry_out(self, src, G, R, dst):
        identf = self.cfv('ident')
        for g0 in range(0, G, 4):
            gn = min(4, G - g0)
            bank = self.ps[self.rot('co', 2)]
            for g in range(gn):
                self.tr(bank[:R, g * 128:(g + 1) * 128], src[:, g0 + g, :], identf)
            stg = self.cacc[self.rot('cacc', 3)]
            self.cp(stg[:R, 0:gn * 128], bank[:R, 0:gn * 128], 'act')
            self.dma(dst[:, g0 * 128:(g0 + gn) * 128], stg[:R, 0:gn * 128])

    def scar_in(self, name, G, R):
        identf = self.cfv('ident')
        rows = 16 * R
        sv = self.scar[:, 0:G * rows].rearrange('p (g b r) -> p g b r', g=G, b=16)
        for g0 in range(0, G, 4):
            gn = min(4, G - g0)
            stg = self.cacc[self.rot('cacc', 3)]
            self.dma(stg[:rows, 0:gn * 128], self.I[name][:, g0 * 128:(g0 + gn) * 128])
            bank = self.ps[self.rot('co', 2)]
            for g in range(gn):
                self.tr(bank[:, g * rows:(g + 1) * rows], stg[:rows, g * 128:(g + 1) * 128], identf[:rows, :rows])
            self.cp(self.scar[:, g0 * rows:(g0 + gn) * rows], bank[:, 0:gn * rows], 'act')
        return sv

    def scar_out(self, name, G, R):
        rows = 16 * R
        src = self.scar[:, 0:G * rows].rearrange('p (g br) -> p g br', g=G)
        self.carry_out(src, G, rows, self.O[name].ap())

    def dense_fm(self, tl, actT, nkt_total, cb_group, kt0=0):
        Wb, (name, k0, nk, parts) = self.wnext()
        ncols = sum(p[1] for p in parts)
        T = tl.T
        for gl in range(ncols // 128):
            bank = self.ps[self.rot('mm', 4)]
            for k in range(nk):
                self.mm(bank[:, 0:T], Wb[:, k, gl * 128:(gl + 1) * 128], actT[:, k0 + k, 0:T],
                        start=(k0 + k == 0), stop=(k0 + k == nkt_total - 1))
            cb_group(gl, bank[:, 0:T])

    def dense_tm(self, tl, actT, cb_chunk):
        Wb, (name, k0, nk, parts) = self.wnext()
        ncols = sum(p[1] for p in parts)
        for ch in tl.chunks:
            bank = self.ps[self.rot('mm', 4)]
            for k in range(nk):
                self.mm(bank[:ch.n, 0:ncols], actT[:, k0 + k, ch.col0:ch.col0 + ch.n], Wb[:, k, 0:ncols],
                        start=(k == 0), stop=(k == nk - 1))
            cb_chunk(ch, bank[:ch.n, 0:ncols])

    def run_tile(self, tl, last):
        S, I, O = self.S, self.I, self.O
        T = tl.T
        isS = any(sg[3] == 's' for sg in tl.segs)
        if isS:
            S.alias_phase([self.xr[2], self.xr[3], self.zs[2], self.zs[3]], self.grpArena + self.grpArena2)
        for ch in tl.chunks:
            src = {'s': I['xs'].ap(), 'm': I['meta'].ap()}.get(ch.kind)
            if src is None:
                src = I['xp'][ch.row0:ch.row0 + ch.n, :]
            self.dma(self.xr[ch.slot][:ch.n, :], src)
        self.ln_load('ln0_g', 'ln0_b')
        for ch in tl.chunks:
            self.ln_rows(self.xr[ch.slot][:ch.n, :], ch.n, ch.slot)
        self.to_fm(tl, lambda ch: self.xr[ch.slot][:ch.n, :], self.xnT)
        for ch in tl.chunks:
            self.ts(self.xr[ch.slot][:ch.n, :], self.xr[ch.slot][:ch.n, :], ALPHA, ALU.mult)
        self.dump('xnT_' + tl.name, self.xnT[:, :, 0:T], [128, 8, T])
        if self.debug.get('stop') == 'p0':
            return
        S.alias_phase(self.grpA2fix + self.grpA2rec + self.grpB + self.grpA1rec, self.grpA1conv + self.grpA1fix)
        sq = self.scar_in('s_mconv', 8, 3) if isS else None
        for blk in range(4):
            def cbq(gl, psv, blk=blk):
                g = 2 * blk + gl
                self.conv_group(tl, psv, 4, self.cwm, g, self.cq, sq, self.qkT[:, g, 0:T])
            self.dense_fm(tl, self.xnT, 8, cbq)
        self.conv_flush()
        if isS:
            self.scar_out('o_mconv', 8, 3)
        if last:
            self.carry_out(self.cq.ap(), 8, 3, O['p_mconv'].ap())
        for blk in range(4):
            self.dense_tm(tl, self.xnT, lambda ch, psv, blk=blk: self.act(self.v[ch.slot][:ch.n, 256 * blk:256 * blk + 256], psv))
        for blk in range(4):
            self.dense_tm(tl, self.xnT, lambda ch, psv, blk=blk: self.act(self.oth[ch.slot][:ch.n, 256 * blk:256 * blk + 256], psv, AF.Tanh, scale=0.5))
        self.dense_tm(tl, self.xnT, lambda ch, psv: self.cp(self.ifdt[:ch.n, ch.slot, :], psv, 'dve'))
        for ch in tl.chunks:
            n, s = ch.n, ch.slot
            gi = self.gat[:n, s, 0:8]
            self.tt(gi, self.ifdt[:n, s, 0:8], self.bif_b[:n, :], ALU.add)
            e1 = self.sm[3]
            self.act(e1[:n, 0:4], self.gat[:n, s, 4:8], AF.Exp, scale=-1.0)
            self.act(e1[:n, 0:4], e1[:n, 0:4], AF.Ln, bias=self.cst[:n, 2:3])
            self.ts(self.gat[:n, s, 4:8], e1[:n, 0:4], -1.0, ALU.mult)
            d1 = self.sm[4]
            self.tt(d1[:n, 0:32], self.ifdt[:n, s, 8:40], self.dtb_b[:n, :], ALU.add)
            self.act(d1[:n, 0:32], d1[:n, 0:32], AF.Exp)
            self.act(self.dta[:n, s, 0:32], d1[:n, 0:32], AF.Ln, bias=self.cst[:n, 2:3])
            self.tt(self.dta[:n, s, 32:64], self.dta[:n, s, 0:32], self.A_b[:n, :], ALU.mult)
        self.dump('qkT_' + tl.name, self.qkT[:, :, 0:T], [128, 8, T])
        self.dump('gat_' + tl.name, self.gat.ap(), [128, 4, 8])
        self.dump('dta_' + tl.name, self.dta.ap(), [128, 4, 64])
        if self.debug.get('stop') == 'a1':
            return
        S.alias_phase(self.grpA1conv, self.grpA1rec)
        for ch in tl.chunks:
            self.mlstm_chunk(tl, ch, last)
        self.dump('hgT_' + tl.name, self.hgT[:, :, 0:T], [128, 8, T])
        if self.debug.get('stop') == 'mlstm':
            return
        S.alias_phase(self.grpA1rec + self.grpA1fix, self.grpA1conv + self.grpA2fix)
        for blk in range(8):
            def cbz(ch, psv, blk=blk):
                n = ch.n
                zc = self.cacc[self.rot('cacc', 3)]
                th = self.cth[self.rot('cth', 2)]
                self.cp(zc[:n, 0:256], psv, 'act')
                self.act(th[:n, 0:256], psv, AF.Tanh, scale=0.5)
                self.stt(self.zs[ch.slot][:n, 256 * blk:256 * blk + 256], th[:n, 0:256], 1.0, zc[:n, 0:256], ALU.add, ALU.mult)
            self.dense_tm(tl, self.xnT, cbz)
        if self.debug.get('stop') == 'a2z':
            return
        sx = self.scar_in('s_sconv', 24, 3) if isS else None
        for blk in range(12):
            def cbx(gl, psv, blk=blk):
                g = 2 * blk + gl
                self.conv_group(tl, psv, 4, self.cws, g, self.cx, sx, self.xbcT[:, g, 0:T])
            self.dense_fm(tl, self.xnT, 8, cbx)
        self.conv_flush()
        if isS:
            self.scar_out('o_sconv', 24, 3)
        if last:
            self.carry_out(self.cx.ap(), 24, 3, O['p_sconv'].ap())
        self.dump('xbcT_' + tl.name, self.xbcT[:, :, 0:T], [128, 24, T])
        if self.debug.get('stop') == 'a2':
            return
        S.alias_phase(self.grpA1conv, self.grpA2rec)
        for ch in tl.chunks:
            self.ssd_chunk(tl, ch, last)
        self.dump('ygT_' + tl.name, self.ygT[:, :, 0:T], [128, 16, T])
        if self.debug.get('stop') == 'ssd':
            return
        S.alias_phase(self.grpA2rec + self.grpA2fix, self.grpA1conv + self.grpB)
        for blk in range(8):
            def cbg(gl, psv, blk=blk):
                self.act(self.gth[:, 2 * blk + gl, 0:T], psv, AF.Tanh, scale=0.5)
            self.dense_fm(tl, self.xnT, 8, cbg)
        for j in range(4):
            Wb, (name, k0, nk, parts) = self.wnext()
            banksA = [self.ps[0], self.ps[1]]
            banksB = [self.ps[2], self.ps[3]]
            for gl in range(2):
                for k in range(8):
                    self.mm(banksA[gl][:, 0:T], Wb[:, k, gl * 128:(gl + 1) * 128], self.hgT[:, k, 0:T], start=(k == 0), stop=(k == 7))
            for half in range(2):
                Wb, (name, k0, nk, parts) = self.wnext()
                for gl in range(2):
                    for k in range(8):
                        kk = 8 * half + k
                        self.mm(banksB[gl][:, 0:T], Wb[:, k, gl * 128:(gl + 1) * 128], self.ygT[:, kk, 0:T], start=(kk == 0), stop=(kk == 15))
            for gl in range(2):
                g = 2 * j + gl
                m1 = self.cacc[self.rot('cacc', 3)]
                m2 = self.cacc2[self.rot('cacc2', 2)]
                self.stt(m1[:, 0:T], self.gth[:, g, 0:T], 1.0, banksA[gl][:, 0:T], ALU.add, ALU.mult)
                self.stt(m2[:, 0:T], self.gth[:, 8 + g, 0:T], 1.0, banksB[gl][:, 0:T], ALU.add, ALU.mult)
                self.tt(self.mixT[:, g, 0:T], m1[:, 0:T], m2[:, 0:T], ALU.add)
        self.rr['mm'] = 0
        for blk in range(4):
            def cbo(ch, psv, blk=blk):
                xv = self.xr[ch.slot][:ch.n, 256 * blk:256 * blk + 256]
                self.stt(xv, psv, 0.5, xv, ALU.mult, ALU.add)
            self.dense_tm(tl, self.mixT, cbo)
        self.ln_load('ln1_g', 'ln1_b')
        for ch in tl.chunks:
            self.ln_rows(self.xr[ch.slot][:ch.n, :], ch.n, ch.slot)
        self.dump('x1_' + tl.name, self.xr[0].ap(), [128, D])
        self.to_fm(tl, lambda ch: self.xr[ch.slot][:ch.n, :], self.xnT)
        for ch in tl.chunks:
            self.ts(self.xr[ch.slot][:ch.n, :], self.xr[ch.slot][:ch.n, :], ALPHA, ALU.mult)
        sf = self.scar_in('s_fconv', 44, 2) if isS else None
        for j in range(11):
            accs = {}
            def cbua(gl, psv, j=j):
                g = 2 * j + gl
                accs[gl] = self.conv_group(tl, psv, 3, self.cwf, g, self.cff, sf, None, final=False)
            self.dense_fm(tl, self.xnT, 8, cbua)
            def cbub(gl, psv, j=j):
                g = 2 * j + gl
                E = self.cE[self.rot('cE', 2)]
                accb = self.cacc2[self.rot('cacc2', 2)]
                self._conv_taps(tl, psv, 3, self.cwf, 22 + g, self.cff, sf, E, accb)
                th = self.cth[self.rot('cth', 2)]
                acca = accs[gl]
                self.act(th[:, 0:T], acca[:, 0:T], AF.Tanh)
                self.stt(th[:, 0:T], th[:, 0:T], 1.0, acca[:, 0:T], ALU.add, ALU.mult)
                self.tt(self.hffT[:, g, 0:T], th[:, 0:T], accb[:, 0:T], ALU.mult)
            self.dense_fm(tl, self.xnT, 8, cbub)
        if isS:
            self.scar_out('o_fconv', 44, 2)
        if last:
            self.carry_out(self.cff.ap(), 44, 2, O['p_fconv'].ap())
        self.dump('hffT_' + tl.name, self.hffT[:, :, 0:T], [128, 22, T])
        for blk in range(4):
            banks = {ch.slot: self.ps[ch.slot] for ch in tl.chunks}
            for (k0, nk) in ((0, 8), (8, 8), (16, 6)):
                Wb, meta = self.wnext()
                for ch in tl.chunks:
                    for k in range(nk):
                        self.mm(banks[ch.slot][:ch.n, 0:256], self.hffT[:, k0 + k, ch.col0:ch.col0 + ch.n], Wb[:, k, 0:256],
                                start=(k0 + k == 0), stop=(k0 + k == 21))
            for ch in tl.chunks:
                xv = self.xr[ch.slot][:ch.n, 256 * blk:256 * blk + 256]
                self.tt(xv, banks[ch.slot][:ch.n, 0:256], xv, ALU.add)
        self.ln_load('ln2_g', 'ln2_b')
        for ch in tl.chunks:
            self.ln_rows(self.xr[ch.slot][:ch.n, :], ch.n, ch.slot)
            if ch.kind == 'p':
                self.dma(O['y_p'][ch.row0:ch.row0 + ch.n, :], self.xr[ch.slot][:ch.n, :])
            elif ch.kind == 's':
                self.dma(O['y_s'].ap(), self.xr[ch.slot][:ch.n, :])

    def mlstm_chunk(self, tl, ch, last):
        I, O = self.I, self.O
        n, s, c0, kind = ch.n, ch.slot, ch.col0, ch.kind
        kc = self.kc(kind, n)
        U, LS, M, MT = kc['U'], kc['LS'], kc['M'], kc['MT']
        identf, onesf = self.cfv('ident', n, n), self.cfv('ones', n, n)
        identb, onesb = self.cbv('identb', n, n), self.cbv('onesb', n, n)
        ps = self.ps
        cols = slice(c0, c0 + n)
        ig = self.gat[:n, s, 0:4]
        lf = self.gat[:n, s, 4:8]
        sm = self.sm
        bt, mi, bm, mt, wi, emt, negm, den, rden, wi2 = (sm[i] for i in range(5, 15))
        R1, R2, R3, wT, ST, kTM, hh, vw = self.R1, self.R2, self.R3, self.wT, self.ST, self.kTM, self.hh, self.vw
        self.mm(ps[0][:n, 0:4], U, lf)
        for h in range(4):
            self.ts(R1[:n, h, :n], LS, self.gat[:n, s, 4 + h:5 + h], ALU.mult)
            self.act(R2[:n, h, :n], identf, AF.Copy, scale=self.gat[:n, s, h:h + 1])
        for h in range(4):
            self.mm(ps[1][:n, h * 128:h * 128 + n], U, R1[:n, h, :n], start=True, stop=False)
            self.mm(ps[1][:n, h * 128:h * 128 + n], onesf, R2[:n, h, :n], start=False, stop=False)
            self.mm(ps[1][:n, h * 128:h * 128 + n], identb, M, start=False, stop=True)
        self.rmax(mi[:n, 0:4], ps[1][:n, :].rearrange('p (h t) -> p h t', h=4)[:, :, 0:n])
        if kind == 's':
            m0s = sm[15]
            self.dma(m0s[:16, 0:4], I['s_m'].ap())
            self.mm(ps[0][:n, 4:8], self.cfv('BMT', 16), m0s[:16, 0:4])
            m0v = ps[0][:n, 4:8]
        else:
            m0v = self.m_b[:n, :]
        self.cp(bt[:n, 0:4], ps[0][:n, 0:4], 'act')
        self.tt(bm[:n, 0:4], bt[:n, 0:4], m0v, ALU.add)
        self.tt(mt[:n, 0:4], bm[:n, 0:4], mi[:n, 0:4], ALU.max)
        self.tt(bm[:n, 0:4], bm[:n, 0:4], mt[:n, 0:4], ALU.subtract)
        self.act(wi[:n, 0:4], bm[:n, 0:4], AF.Exp)
        self.act(emt[:n, 0:4], mt[:n, 0:4], AF.Exp, scale=-1.0, bias=self.cst[:n, 1:2])
        self.ts(negm[:n, 0:4], mt[:n, 0:4], -1.0, ALU.mult)
        for h in range(4):
            self.act(R3[:n, h, :n], identf, AF.Copy, scale=negm[:n, h:h + 1])
        for h in range(4):
            o = ps[2][:n, h * 128:h * 128 + n]
            self.mm(o, R1[:n, h, :n], U, start=True, stop=False)
            self.mm(o, R2[:n, h, :n], onesf, start=False, stop=False)
            self.mm(o, onesf, R3[:n, h, :n], start=False, stop=False)
            self.mm(o, identb, MT, start=False, stop=True)
        ps2v = ps[2][:n, :].rearrange('p (h t) -> p h t', h=4)[:, :, 0:n]
        self.act(wT[:n, :, :n], ps2v, AF.Exp)
        for h in range(4):
            self.mm(ps[1][:n, h * 128:h * 128 + n], self.qkT[:, 4 + h, cols], self.qkT[:, h, cols])
        ps1v = ps[1][:n, :].rearrange('p (h t) -> p h t', h=4)[:, :, 0:n]
        self.tt(ST[:n, :, :n], ps1v, wT[:n, :, :n], ALU.mult)
        pb7 = self.psb(7)
        for h in range(4):
            self.tr(pb7[:n, h * 128:(h + 1) * 128], self.qkT[:, 4 + h, cols], self.cbv('identb'))
        self.cp(kTM[:n, :, :], pb7[:n, 0:512].rearrange('p (h d) -> p h d', h=4), 'act')
        if kind == 's':
            self.mlstm_sample_states(tl, ch, wi, wT, kTM, mt)
        for h in range(4):
            self.mm(ps[3 + h // 2][:n, (h % 2) * 256:(h % 2) * 256 + 256], ST[:n, h, :n], self.v[s][:n, h * 256:(h + 1) * 256])
        for h in range(4):
            self.mm(ps[0][:n, 8 + h:9 + h], ST[:n, h, :n], onesb[:, 0:1])
        if kind != 's':
            for h in range(4):
                self.mm(ps[5 + h // 2][:n, (h % 2) * 256:(h % 2) * 256 + 256], self.qkT[:, h, cols], self.Cb[:, h, :])
            for h in range(4):
                self.mm(ps[0][:n, 12 + h:13 + h], self.qkT[:, h, cols], self.nb[:, h:h + 1])
        dint = ps[0][:n, 12:16] if kind != 's' else sm[12][:n, 0:4]
        self.tt(den[:n, 0:4], dint, wi[:n, 0:4], ALU.mult)
        self.tt(den[:n, 0:4], den[:n, 0:4], ps[0][:n, 8:12], ALU.add)
        self.ts(wi2[:n, 0:4], den[:n, 0:4], -1.0, ALU.mult)
        self.tt(den[:n, 0:4], den[:n, 0:4], wi2[:n, 0:4], ALU.max)
        self.tt(den[:n, 0:4], den[:n, 0:4], emt[:n, 0:4], ALU.max)
        self.S.op('dve', lambda e: e.reciprocal(out=rden.t[:n, 0:4], in_=den.t[:n, 0:4]), reads=[den], writes=[rden])
        self.tt(wi2[:n, 0:4], wi[:n, 0:4], rden[:n, 0:4], ALU.mult)
        for h in range(4):
            self.act(hh[:n, h, :], ps[3 + h // 2][:n, (h % 2) * 256:(h % 2) * 256 + 256], AF.Copy, scale=rden[:n, h:h + 1])
            if kind != 's':
                iv = ps[5 + h // 2][:n, (h % 2) * 256:(h % 2) * 256 + 256]
            else:
                iv = (R1 if h < 2 else R2)[:n, :, :].rearrange('p a b -> p (a b)')[:, (h % 2) * 256:(h % 2) * 256 + 256]
            self.stt(hh[:n, h, :], iv, wi2[:n, h:h + 1], hh[:n, h, :], ALU.mult, ALU.add)
        st, mv, rs = sm[0], sm[1], sm[2]
        for h in range(4):
            self.S.op('dve', lambda e, h=h: e.bn_stats(out=st.t[:n, h * 6:(h + 1) * 6], in_=hh.t[:n, h, :]), reads=[hh], writes=[st])
        for h in range(4):
            self.S.op('dve', lambda e, h=h: e.bn_aggr(out=mv.t[:n, 2 * h:2 * h + 2], in_=st.t[:n, h * 6:(h + 1) * 6]), reads=[st], writes=[mv])
        mvv = mv[:n, 0:8].rearrange('p (h t) -> p h t', t=2)
        self.act(rs[:n, 0:4], mvv[:, :, 1], AF.Ln, bias=self.cst[:n, 0:1])
        self.act(rs[:n, 0:4], rs[:n, 0:4], AF.Exp, scale=-0.5)
        for h in range(4):
            self.ts(hh[:n, h, :], hh[:n, h, :], mv[:n, 2 * h:2 * h + 1], ALU.subtract, rs[:n, h:h + 1], ALU.mult)
            self.stt(self.hgTM[:n, h * 256:(h + 1) * 256], self.oth[s][:n, h * 256:(h + 1) * 256], 1.0, hh[:n, h, :], ALU.add, ALU.mult)
        for k in range(8):
            self.tr(pb7[:, k * n:(k + 1) * n], self.hgTM[:n, k * 128:(k + 1) * 128], identb)
        self.cp(self.hgT[:, :, cols], pb7[:, 0:8 * n].rearrange('p (k t) -> p k t', k=8), 'dve')
        if kind != 's':
            wl16 = sm[15]
            self.cp(wl16.ap().bitcast(BF16)[:n, 0:4], wT[:n, :, n - 1], 'act')
            for h in range(4):
                self.ts(vw[:n, h, :], self.v[s][:n, h * 256:(h + 1) * 256], wT[:n, h, n - 1:n], ALU.mult)
            for h in range(4):
                self.mm(ps[5 + h // 2][:, (h % 2) * 256:(h % 2) * 256 + 256], kTM[:n, h, :], vw[:n, h, :])
            for h in range(4):
                self.mm(ps[0][:, 16 + h:17 + h], kTM[:n, h, :], wl16.ap().bitcast(BF16)[:n, h:h + 1])
            SEL = self.cfv('SELp' if n == 128 else 'SELm', n)
            self.mm(ps[0][:, 32:36], SEL, wi[:n, 0:4])
            self.mm(ps[0][:, 36:40], SEL, mt[:n, 0:4])
            dec = sm[3]
            self.cp(dec[:, 0:8], ps[0][:, 32:40], 'act')
            for h in range(4):
                self.stt(self.Cf[:, h, :], self.Cf[:, h, :], dec[:, h:h + 1], ps[5 + h // 2][:, (h % 2) * 256:(h % 2) * 256 + 256], ALU.mult, ALU.add)
            self.tt(self.nf.ap(), self.nf.ap(), dec[:, 0:4], ALU.mult)
            self.tt(self.nf.ap(), self.nf.ap(), ps[0][:, 16:20], ALU.add)
            self.cp(self.m_b.ap(), dec[:, 4:8], 'dve')
            self.cp(self.Cb.ap(), self.Cf.ap(), 'act')
            self.cp(self.nb.ap(), self.nf.ap(), 'act')
            if ch.final:
                self.dma(O['p_C'].ap().rearrange('h d e -> d h e'), self.Cf.ap())
                identf128 = self.cfv('ident')
                self.tr(ps[0][:4, 128:256], self.nf.ap(), identf128)
                self.cp(self.pn_st[:4, :], ps[0][:4, 128:256], 'act')
                self.dma(O['p_n'].ap(), self.pn_st[:4, :])
                self.dma(O['p_m'].ap(), self.m_b[0:1, :])

    def mlstm_sample_states(self, tl, ch, wi, wT, kTM, mt):
        I, O, ps, sm = self.I, self.O, self.ps, self.sm
        n, s = 64, ch.slot
        identf = self.cfv('ident')
        RS, BM = self.cfv('RS', 64), self.cfv('BM', 64)
        CM = self.cbv('CM').rearrange('p (b j) -> p b j', b=16)
        vw = self.vw
        wl = sm[3]
        tmp = self.R3
        self.tt(tmp[:n, :, 0:64], wT[:n, :, 0:64], View(self.cf, self.cfv('LSEL', 64).ap.unsqueeze(1).to_broadcast([64, 4, 64])), ALU.mult)
        self.S.op('dve', lambda e: e.tensor_reduce(out=wl.t[:n, 0:4], in_=tmp.t[:n, :, 0:64], axis=AX.X, op=ALU.add), reads=[tmp], writes=[wl])
        wl16 = sm[4].ap().bitcast(BF16)
        self.cp(wl16[:n, 0:4], wl[:n, 0:4], 'act')
        for h in range(4):
            self.ts(vw[:n, h, :], self.v[s][:n, h * 256:(h + 1) * 256], wl[:n, h:h + 1], ALU.mult)
        Rm3 = self.Rm[:n, :].rearrange('p (b h) -> p b h', h=4)
        self.tt(Rm3, View(wi, wi.t[:n, 0:4].unsqueeze(1).to_broadcast([n, 16, 4])),
                View(self.cf, RS.ap.unsqueeze(2).to_broadcast([n, 16, 4])), ALU.mult)
        self.mm(ps[0][:, 64:128], self.cfv('ones', 64), self.Rm[:n, :])
        self.cp(self.decS.ap(), ps[0][:, 64:128], 'act')
        self.mm(ps[0][:16, 40:44], RS, mt[:n, 0:4])
        mo = sm[15]
        self.cp(mo[:16, 8:12], ps[0][:16, 40:44], 'act')
        self.dma(O['o_m'].ap(), mo[:16, 8:12])
        stg = self.pn_st
        self.dma(stg[:64, :], I['s_n'].ap())
        self.tr(ps[0][:, 192:256], stg[:64, :], identf[:64, :64])
        self.cp(self.n0T.ap(), ps[0][:, 192:256], 'act')
        self.cp(self.n16.ap(), self.n0T.ap(), 'pool')
        kTMflat = kTM[:n, :, :].rearrange('p h d -> p (h d)')
        for b in range(16):
            i = b % 2
            C0, C16, qm, km = self.C0b[i], self.C0b16[i], self.qmb[i], self.kTMm[i]
            self.dma(C0.ap(), I['s_C'][b].rearrange('h d e -> d h e'))
            self.cp(C16[:, :, 0:256], C0.ap(), 'act')
            self.cp(C16[:, :, 256:257], self.n16[:, 4 * b:4 * b + 4].rearrange('p (h o) -> p h o', o=1), 'dve')
            self.tt(qm.ap(), self.qkT[:, 0:4, ch.col0:ch.col0 + 64], View(self.cb, CM.ap[:, b, :].unsqueeze(1).to_broadcast([128, 4, 64])), ALU.mult)
            self.ts(km[:n, :, :].rearrange('p h d -> p (h d)'), kTMflat, BM[:, b:b + 1], ALU.mult)
            for h in range(4):
                self.mm(ps[3 + h][:n, 0:257], qm[:, h, :], C16[:, h, 0:257], start=(b == 0), stop=(b == 15))
            for h in range(4):
                self.mm(ps[1 + h // 2][:, (h % 2) * 256:(h % 2) * 256 + 256], km[:n, h, :], vw[:n, h, :])
            for h in range(4):
                self.mm(ps[0][:, 128 + 4 * b + h:129 + 4 * b + h], km[:n, h, :], wl16[:n, h:h + 1])
            for h in range(4):
                self.stt(C0[:, h, :], C0[:, h, :], self.decS[:, 4 * b + h:4 * b + h + 1],
                         ps[1 + h // 2][:, (h % 2) * 256:(h % 2) * 256 + 256], ALU.mult, ALU.add)
            self.dma(O['o_C'][b].rearrange('h d e -> d h e'), C0.ap())
        for h in range(4):
            dst = (self.R1 if h < 2 else self.R2)[:n, :, :].rearrange('p a b -> p (a b)')[:, (h % 2) * 256:(h % 2) * 256 + 256]
            self.cp(dst, ps[3 + h][:n, 0:256], 'act')
            self.cp(sm[12][:n, h:h + 1], ps[3 + h][:n, 256:257], 'act')
        self.tt(self.n0T.ap(), self.n0T.ap(), self.decS.ap(), ALU.mult)
        self.tt(self.n0T.ap(), self.n0T.ap(), ps[0][:, 128:192], ALU.add)
        self.tr(ps[0][:64, 256:384], self.n0T.ap(), identf)
        self.cp(stg[:64, :], ps[0][:64, 256:384], 'act')
        self.dma(O['o_n'].ap(), stg[:64, :])

    def ssd_chunk(self, tl, ch, last):
        I, O, ps, sm = self.I, self.O, self.ps, self.sm
        n, s, c0, kind = ch.n, ch.slot, ch.col0, ch.kind
        kc = self.kc(kind, n)
        U, LS, BO, MT = kc['U'], kc['LS'], kc['BO'], kc['MT']
        identb = self.cbv('identb')
        cols = slice(c0, c0 + n)
        dt = self.dta[:n, s, 0:32]
        a = self.dta[:n, s, 32:64]
        btsb, dect, wend, decS = sm[5], sm[6], sm[7], sm[8]
        self.mm(ps[0][:n, 0:32], U, a)
        self.mm(ps[0][:n, 32:64], BO, a)
        self.cp(btsb[:n, 0:32], ps[0][:n, 0:32], 'act')
        self.act(dect[:n, 0:32], ps[0][:n, 0:32], AF.Exp)
        self.tt(wend[:n, 0:32], ps[0][:n, 32:64], btsb[:n, 0:32], ALU.subtract)
        self.act(wend[:n, 0:32], wend[:n, 0:32], AF.Exp)
        if kind != 's':
            self.mm(ps[0][:, 64:96], self.cfv('ones', n), a)
            self.act(decS[:, 0:32], ps[0][:, 64:96], AF.Exp)
        S = self.S
        pb6, pb7 = self.psb(6), self.psb(7)
        xsTM = self.xdt
        for g in range(16):
            pv = pb6 if g < 8 else pb7
            self.tr(pv[:n, (g % 8) * 128:(g % 8 + 1) * 128], self.xbcT[:, g, cols], identb)
        self.act(xsTM[:n, 0:1024], pb6[:n, 0:1024])
        self.act(xsTM[:n, 1024:2048], pb7[:n, 0:1024])
        S.alias_phase([self.ynTM], [self.xsDT])
        for half in range(2):
            for g in range(8):
                self.ts(self.xsDT[:, g, 0:n], self.xbcT[:, 8 * half + g, cols], self.Dfm[:, 8 * half + g:8 * half + g + 1], ALU.mult)
            pv = pb6 if half == 0 else pb7
            for g in range(8):
                self.tr(pv[:n, g * 128:(g + 1) * 128], self.xsDT[:, g, 0:n], identb)
            self.cp(self.xsD[:n, 1024 * half:1024 * half + 1024], pv[:n, 0:1024], 'dve')
        for g in range(4):
            self.tr(pb6[:n, g * 128:(g + 1) * 128], self.xbcT[:, 16 + g, cols], identb)
        self.act(self.BTM[:n, :], pb6[:n, 0:512])
        dtw = sm[13]
        self.tt(dtw[:n, 0:32], dt, wend[:n, 0:32], ALU.mult)
        S.alias_phase([self.xsDT], [self.ynTM])
        if kind == 's':
            self.ssd_sample_states(tl, ch, dtw)
        S.alias_phase([self.ynTM], self.MTh[1])
        ssq = sm[9]
        self.memset(ssq[:n, 0:4], 0.0)
        def stageA(g):
            MTb = self.MTh[g % 2]
            for j in range(8):
                self.act(self.LAh[j][:n, :n], LS, AF.Copy, scale=self.dta[:n, s, 32 + 8 * g + j:33 + 8 * g + j])
            for j in range(8):
                o = ps[1 + j // 4][:n, (j % 4) * 128:(j % 4) * 128 + n]
                self.mm(o, self.LAh[j][:n, :n], U, start=True, stop=False)
                self.mm(o, identb[:n, :n], MT, start=False, stop=True)
            for half in range(2):
                self.act(self.LTh[half][:n, :, :n],
                         ps[1 + half][:n, :].rearrange('p (h t) -> p h t', h=4)[:, :, 0:n], AF.Exp)
            self.mm(ps[3][:n, 0:n], self.xbcT[:, 16 + g, cols], self.xbcT[:, 20 + g, cols])
            for j in range(8):
                self.stt(MTb[j][:n, :n], self.LTh[j // 4][:n, j % 4, :n], self.dta[:n, s, 8 * g + j:8 * g + j + 1], ps[3][:n, 0:n], ALU.mult, ALU.mult)

        def stageB(g):
            MTb = self.MTh[g % 2]
            for j in range(8):
                h = 8 * g + j
                self.mm(ps[4][:n, j * 64:(j + 1) * 64], MTb[j][:n, :n], xsTM[:n, h * 64:(h + 1) * 64])
            t1 = self.t1
            if kind != 's':
                self.mm(ps[5][:n, 0:512], self.xbcT[:, 20 + g, cols], self.STb[:, 512 * g:512 * g + 512])
                for j in range(4):
                    self.act(t1[:n, j * 64:(j + 1) * 64], ps[5][:n, j * 64:(j + 1) * 64], AF.Copy, scale=dect[:n, 8 * g + j:8 * g + j + 1])
                self.tt(t1[:n, 256:512].rearrange('p (h d) -> p d h', d=64), ps[5][:n, 256:512].rearrange('p (h d) -> p d h', d=64),
                        View(dect, dect.t[:n, 8 * g + 4:8 * g + 8].unsqueeze(1).to_broadcast([n, 64, 4])), ALU.mult)
            else:
                self.cp(t1[:n, :], self.ysi[:n, 512 * g:512 * g + 512], 'dve')
            self.tt(t1[:n, :], t1[:n, :], ps[4][:n, 0:512], ALU.add)
            self.tt(t1[:n, :], t1[:n, :], self.xsD[:n, 512 * g:512 * g + 512], ALU.add)
            self.tt(self.yz[:n, 512 * g:512 * g + 512], t1[:n, :], self.zs[s][:n, 512 * g:512 * g + 512], ALU.mult)

        stageA(0)
        for g in range(4):
            if g + 1 < 4:
                stageA(g + 1)
            stageB(g)
        S.alias_phase(self.MTh[1], [self.ynTM])
        for g in range(4):
            self.act(self.ynTM[:n, 512 * g:512 * g + 512], self.yz[:n, 512 * g:512 * g + 512], AF.Square, accum=ssq[:n, g:g + 1])
        rs = sm[10]
        self.act(rs[:n, 0:4], ssq[:n, 0:4], AF.Ln, scale=0.25 / 512.0, bias=self.cst[:n, 0:1])
        self.act(rs[:n, 0:4], rs[:n, 0:4], AF.Exp, scale=-0.5)
        self.ts(rs[:n, 0:4], rs[:n, 0:4], 0.5, ALU.mult)
        for g in range(4):
            self.ts(self.ynTM[:n, 512 * g:512 * g + 512], self.yz[:n, 512 * g:512 * g + 512], rs[:n, g:g + 1], ALU.mult)
        for k in range(16):
            pv = pb6 if k < 8 else pb7
            self.tr(pv[:, (k % 8) * n:(k % 8 + 1) * n], self.ynTM[:n, k * 128:(k + 1) * 128], identb[:n, :n])
        for half in range(2):
            pv = (pb6 if half == 0 else pb7)[:, 0:8 * n].rearrange('p (k t) -> p k t', k=8)
            self.cp(self.ygT[:, 8 * half:8 * half + 8, cols], pv, 'dve' if half == 0 else 'act')
        if kind != 's':
            S.alias_phase([self.ynTM], self.MTh[1])
            for g in range(4):
                hs = slice(8 * g, 8 * g + 8)
                bank = ps[3 + 2 * (g % 2)]
                Bs = self.MTh[g % 2]
                for j in range(8):
                    h = 8 * g + j
                    self.ts(Bs[j][:n, :], self.BTM[:n, g * 128:(g + 1) * 128], dtw[:n, h:h + 1], ALU.mult)
                for j in range(8):
                    h = 8 * g + j
                    self.mm(bank[:, j * 64:(j + 1) * 64], Bs[j][:n, :], xsTM[:n, h * 64:(h + 1) * 64])
                if g % 2 == 0:
                    for j in range(8):
                        c0_ = 512 * g + 64 * j
                        self.act(self.STf[:, c0_:c0_ + 64], self.STf[:, c0_:c0_ + 64], AF.Copy, scale=decS[:, 8 * g + j:8 * g + j + 1])
                else:
                    sv = self.STf[:, 512 * g:512 * g + 512].rearrange('p (h d) -> p d h', d=64)
                    self.tt(sv, sv, View(decS, decS.t[:, hs].unsqueeze(1).to_broadcast([128, 64, 8])), ALU.mult)
                self.tt(self.STf[:, 512 * g:512 * g + 512], self.STf[:, 512 * g:512 * g + 512], bank[:, 0:512], ALU.add)
            S.alias_phase(self.MTh[1], [self.ynTM])
            self.cp(self.STb.ap(), self.STf.ap(), 'act')
            if ch.final:
                identf = self.cfv('ident')
                for j in range(16):
                    bank = ps[1 + (j // 4) % 2]
                    self.tr(bank[:, (j % 4) * 128:(j % 4 + 1) * 128], self.STf[:, j * 128:(j + 1) * 128], identf)
                    if j % 4 == 3:
                        stg = self.LA[:, 4 * ((j // 4) % 2):4 * ((j // 4) % 2) + 4, :]
                        self.S.alias_phase(self.LAh, [self.LA])
                        self.cp(stg, bank[:, 0:512].rearrange('p (j n) -> p j n', j=4), 'act')
                        q = j // 4
                        self.dma(O['p_ssm'][512 * q:512 * q + 512, :].rearrange('(j p) n -> p j n', p=128), stg)
                self.S.alias_phase([self.LA], self.LAh)

    def ssd_sample_states(self, tl, ch, wend):
        I, O, ps, sm, S = self.I, self.O, self.ps, self.sm, self.S
        n, s = 64, ch.slot
        identf = self.cfv('ident')
        RS, BM = self.cfv('RS', 64), self.cfv('BM', 64)
        CM = self.cbv('CM').rearrange('p (b j) -> p b j', b=16)
        dect = sm[6]
        S.alias_phase(self.grpArena, self.grpArena2)
        S.alias_phase(self.LTh + self.MTh[0] + [self.t1, self.yz], [self.S0b[1]])
        blsb = sm[11]
        self.cp(blsb[:n, 0:32], ps[0][:n, 32:64], 'act')
        bl3 = blsb[:n, 0:32].rearrange('p (j r) -> p j r', r=2)
        for r in range(2):
            self.tt(self.Rr[:n, r, :].rearrange('p (b j) -> p b j', b=16),
                    View(blsb, bl3.ap[:, :, r].unsqueeze(1).to_broadcast([n, 16, 16])),
                    View(self.cf, RS.ap.unsqueeze(2).to_broadcast([n, 16, 16])), ALU.mult, eng='pool')
        self.mm(ps[0][:, 256:512], self.cfv('H0', 64), self.Rr[:n, 0, :], start=True, stop=False)
        self.mm(ps[0][:, 256:512], self.cfv('H1', 64), self.Rr[:n, 1, :], start=False, stop=True)
        self.act(self.decP.ap(), ps[0][:, 256:512], AF.Exp)
        wxA = self.ynTM
        self.tt(wxA[:n, :].rearrange('p (h d) -> p d h', d=64), self.xdt[:n, :].rearrange('p (h d) -> p d h', d=64),
                View(wend, wend.t[:n, 0:32].unsqueeze(1).to_broadcast([n, 64, 32])), ALU.mult)
        for b in range(16):
            Sb = self.S0b[b % 2]
            cm = self.CTmb[b % 2]
            self.dma(Sb.ap(), I['s_ssm'][b].rearrange('(j p) n -> p j n', p=128))
            self.tt(cm.ap(), self.xbcT[:, 20:24, ch.col0:ch.col0 + 64], View(self.cb, CM.ap[:, b, :].unsqueeze(1).to_broadcast([128, 4, 64])), ALU.mult)
            self.ts(self.wxm[:n, :], wxA[:n, :], BM[:, b:b + 1], ALU.mult)
            for q in range(4):
                bank = ps[5 + q % 2]
                for i in range(4):
                    self.tr(bank[:, i * 128:(i + 1) * 128], Sb[:, 4 * q + i, :], identf)
                self.cp(self.SbT[:, 512 * q:512 * q + 512], bank[:, 0:512], 'act')
            for g in range(4):
                self.mm(ps[1 + g][:n, 0:512], cm[:, g, :], self.SbT[:, 512 * g:512 * g + 512], start=(b == 0), stop=(b == 15))
            for q in range(4):
                bank = ps[7] if q % 2 == 0 else ps[0]
                for i in range(4):
                    j = 4 * q + i
                    self.mm(bank[:, i * 128:(i + 1) * 128], self.wxm[:n, j * 128:(j + 1) * 128], self.BTM[:n, q * 128:(q + 1) * 128])
                for i in range(4):
                    j = 4 * q + i
                    self.stt(Sb[:, j, :], Sb[:, j, :], self.decP[:, 16 * b + j:16 * b + j + 1], bank[:, i * 128:(i + 1) * 128], ALU.mult, ALU.add)
            self.dma(O['o_ssm'][b].rearrange('(j p) n -> p j n', p=128), Sb.ap())
        for g in range(4):
            self.tt(self.ysi[:n, 512 * g:512 * g + 512].rearrange('p (h d) -> p d h', d=64),
                    ps[1 + g][:n, 0:512].rearrange('p (h d) -> p d h', d=64),
                    View(dect, dect.t[:n, 8 * g:8 * g + 8].unsqueeze(1).to_broadcast([n, 64, 8])), ALU.mult)
        S.alias_phase([self.S0b[1]], self.LTh + self.MTh[0] + [self.t1, self.yz])


_CACHE = {}


def _get_kernel():
    if 'k' not in _CACHE:
        _CACHE['k'] = K()
    return _CACHE['k']


def make_in_maps(kb, inputs):
    f = lambda a: np.ascontiguousarray(np.asarray(a, dtype=np.float32))
    xp, xs = f(inputs['x_prompt']), f(inputs['x_sample'])
    shared = {'meta': f(inputs['meta_tokens']), 'cf': kb.cf_np, 'cb': kb.cb_np,
              'ln0_g': f(inputs['ln0_g']), 'ln0_b': f(inputs['ln0_b']),
              'b_if': f(inputs['b_mlstm_if'])[0], 'w_mconv': f(inputs['w_mlstm_conv'])[0],
              'b_mconv': f(inputs['b_mlstm_conv']), 'mnorm_g': f(inputs['mlstm_norm_g']),
              'w_sconv': f(inputs['w_ssm_conv'])[0], 'b_sconv': f(inputs['b_ssm_conv']),
              'dt_bias': f(inputs['ssm_dt_bias'])[0], 'A_log': f(inputs['ssm_A_log'])[0],
              'ssm_D': f(inputs['ssm_D'])[0], 'snorm_g': f(inputs['ssm_norm_g']),
              'ln1_g': f(inputs['ln1_g'])[0], 'ln1_b': f(inputs['ln1_b'])[0],
              'w_fconv': f(inputs['w_ffn_conv'])[0], 'b_fconv': f(inputs['b_ffn_conv']),
              'ln2_g': f(inputs['ln2_g'])[0], 'ln2_b': f(inputs['ln2_b'])[0],
              'w_in': f(inputs['w_in'])[0], 'w_proj_a': f(inputs['w_proj_a'])[0],
              'w_proj_b': f(inputs['w_proj_b'])[0], 'w_out': f(inputs['w_out'])[0],
              'w_up': f(inputs['w_up'])[0], 'w_down': f(inputs['w_down'])[0]}
    maps = []
    for c in range(8):
        b = slice(16 * c, 16 * c + 16)
        m = dict(shared)
        m['xp'] = xp[c]
        m['xs'] = xs[b].reshape(64, D)
        m['s_mconv'] = f(inputs['state_mlstm_conv'])[0, b].reshape(48, 1024)
        m['s_C'] = f(inputs['state_mlstm_C'])[0, b]
        m['s_n'] = f(inputs['state_mlstm_n'])[0, b].reshape(64, 128)
        m['s_m'] = f(inputs['state_mlstm_m'])[0, b]
        m['s_sconv'] = f(inputs['state_ssm_conv'])[0, b].reshape(48, 3072)
        m['s_ssm'] = f(inputs['state_ssm'])[0, b].reshape(16, 2048, 128)
        m['s_fconv'] = f(inputs['state_ffn_conv'])[0, b].reshape(32, 2 * DFF)
        maps.append(m)
    return maps


def kernel(**inputs):
    kb = _get_kernel()
    maps = make_in_maps(kb, inputs)
    res = run_bass_kernel_spmd(kb.nc, maps, core_ids=list(range(8)))
    R = res.results
    cat = lambda k: np.stack([np.asarray(r[k], dtype=np.float32) for r in R])
    y_p = cat('y_p')
    y_s = cat('y_s').reshape(128, 4, D)
    p_mconv = cat('p_mconv')[None]
    p_C = cat('p_C')[None]
    p_n = cat('p_n')[None]
    p_m = cat('p_m').reshape(8, 4)[None]
    p_sconv = cat('p_sconv')[None]
    p_ssm = cat('p_ssm').reshape(8, 32, 64, 128)[None]
    p_fconv = cat('p_fconv')[None]
    s_mconv = cat('o_mconv').reshape(128, 3, 1024)[None]
    s_C = cat('o_C').reshape(128, 4, 128, 256)[None]
    s_n = cat('o_n').reshape(128, 4, 128)[None]
    s_m = cat('o_m').reshape(128, 4)[None]
    s_sconv = cat('o_sconv').reshape(128, 3, 3072)[None]
    s_ssm = cat('o_ssm').reshape(128, 32, 64, 128)[None]
    s_fconv = cat('o_fconv').reshape(128, 2, 2 * DFF)[None]
    return (y_p, y_s, p_mconv, p_C, p_n, p_m, p_sconv, p_ssm, p_fconv,
            s_mconv, s_C, s_n, s_m, s_sconv, s_ssm, s_fconv)
```

```python
import numpy as np
import ml_dtypes
import concourse.bass as bass
import concourse.mybir as mybir
from concourse.bass_utils import run_bass_kernel_spmd

F32 = mybir.dt.float32
BF16 = mybir.dt.bfloat16
ALU = mybir.AluOpType
AF = mybir.ActivationFunctionType
AX = mybir.AxisListType

D = 1024
DIN = 10280
DFF = 2816
NEG = -30000.0
ALPHA = 2.0 ** 0.25
LN_EPS = 1e-5
RMS_EPS = 1e-5
QSCALE = 128.0 ** -0.5


class Buf:
    def __init__(self, name, t, space):
        self.name = name
        self.t = t
        self.space = space
        self.last_w = None
        self.readers = []
        self.sem_in = None
        self.cnt_in = 0
        self.sem_out = None
        self.cnt_out = 0

    def __getitem__(self, idx):
        return View(self, self.t[idx])

    def ap(self):
        return View(self, self.t[:] if self.space != 'dram' else self.t)


class View:
    def __init__(self, buf, ap):
        self.buf = buf
        self.ap = ap

    def __getitem__(self, idx):
        return View(self.buf, self.ap[idx])

    def rearrange(self, *a, **k):
        return View(self.buf, self.ap.rearrange(*a, **k))

    def bc(self, axis, shape):
        return View(self.buf, self.ap.unsqueeze(axis).to_broadcast(list(shape)))

    def bitcast(self, dt):
        return View(self.buf, self.ap.bitcast(dt))


def _bufs(vs):
    out = []
    for v in vs:
        if v is None or isinstance(v, (int, float)):
            continue
        b = v.buf if isinstance(v, View) else v
        if b not in out:
            out.append(b)
    return out


class Sched:
    ENGS = ('pe', 'act', 'dve', 'pool', 'sp')

    def __init__(self, nc):
        self.nc = nc
        self.sem = {e: nc.alloc_semaphore('sem_' + e) for e in self.ENGS}
        self.cnt = {e: 0 for e in self.ENGS}
        self.ops = {e: [] for e in self.ENGS}
        self.seen = {e: {} for e in self.ENGS}
        self.final_tokens = []
        self.sb_off = 16512
        self.sb_end = 229376
        self.nsem = 5

    def sbuf(self, name, shape, dtype, at=None):
        nbytes = int(np.prod(shape[1:])) * (2 if dtype == BF16 else 4)
        nbytes = (nbytes + 31) // 32 * 32
        if at is None:
            at = self.sb_off
            self.sb_off += nbytes
            assert self.sb_off <= self.sb_end, ('SBUF overflow', name, self.sb_off)
        t = self.nc.alloc_sbuf_tensor_at(name, list(shape), dtype, offset=at)
        b = Buf(name, t, 'sbuf')
        b.off = at
        b.nbytes = nbytes
        return b

    def psum(self, name, shape, dtype=F32):
        t = self.nc.alloc_psum_tensor(name, list(shape), dtype)
        return Buf(name, t, 'psum')

    def dram(self, name, shape, dtype, kind):
        t = self.nc.dram_tensor(name, list(shape), dtype, kind=kind)
        return Buf(name, t.ap(), 'dram')

    def alias_phase(self, old, new):
        toks = []
        for b in old:
            if b.last_w is not None:
                toks.append(b.last_w)
            toks.extend(b.readers)
        for b in new:
            b.readers = list(b.readers) + toks

    def _need(self, eng, waits, tok):
        sem, val, teng = tok
        key = id(sem)
        if self.seen[eng].get(key, 0) >= val:
            return
        if key not in waits or waits[key][1] < val:
            waits[key] = (sem, val)

    def _deps(self, eng, reads, writes):
        waits = {}
        for b in reads:
            tok = b.last_w
            if tok is not None and not (tok[2] == eng and eng == 'pe'):
                self._need(eng, waits, tok)
            if b.space == 'psum':
                for r in b.readers:
                    if r[2] != eng:
                        self._need(eng, waits, r)
        for b in writes:
            tok = b.last_w
            if tok is not None and not (tok[2] == eng and eng == 'pe'):
                self._need(eng, waits, tok)
            for r in b.readers:
                if not (r[2] == eng and eng == 'pe'):
                    self._need(eng, waits, r)
        for key, (sem, val) in waits.items():
            self.seen[eng][key] = val
        return list(waits.values())

    def op(self, eng, fn, reads=(), writes=()):
        reads = _bufs(reads)
        writes = _bufs(writes)
        waits = self._deps(eng, reads, writes)
        self.cnt[eng] += 1
        tok = (self.sem[eng], self.cnt[eng], eng)
        self.ops[eng].append((waits, fn, (self.sem[eng], 1)))
        for b in writes:
            b.last_w = tok
            b.readers = []
        for b in reads:
            if b not in writes:
                b.readers.append(tok)
        return tok

    def dma(self, q, out, in_, **kw):
        ob, ib = out.buf, in_.buf
        waits = self._deps(q, [ib], [ob])
        if ob.space != 'dram':
            if ob.sem_in is None:
                ob.sem_in = self.nc.alloc_semaphore('din_' + ob.name)
                self.nsem += 1
            ob.cnt_in += 16
            sem, val = ob.sem_in, ob.cnt_in
        else:
            if ib.sem_out is None:
                ib.sem_out = self.nc.alloc_semaphore('dout_' + ib.name)
                self.nsem += 1
            ib.cnt_out += 16
            sem, val = ib.sem_out, ib.cnt_out
        tok = (sem, val, 'dma')
        oap, iap = out.ap, in_.ap

        def fn(e, oap=oap, iap=iap, kw=kw):
            return e.dma_start(out=oap, in_=iap, **kw)
        self.ops[q].append((waits, fn, (sem, 16)))
        ob.last_w = tok
        ob.readers = []
        ib.readers.append(tok)
        if ob.space == 'dram':
            self.final_tokens.append(tok)
        return tok

    def emit(self):
        nc = self.nc
        last = {}
        for sem, val, _ in self.final_tokens:
            k = id(sem)
            if k not in last or last[k][1] < val:
                last[k] = (sem, val)
        fin = list(last.values())
        eng_obj = {'pe': 'tensor', 'act': 'scalar', 'dve': 'vector', 'pool': 'gpsimd', 'sp': 'sync'}
        with nc.Block() as block:
            def mk(eng):
                def body(e):
                    for waits, fn, inc in self.ops[eng]:
                        for sem, val in waits:
                            e.wait_ge(sem, val)
                        fn(e).then_inc(inc[0], inc[1])
                    if eng == 'sp':
                        for sem, val in fin:
                            e.wait_ge(sem, val)
                return body
            for eng, attr in eng_obj.items():
                getattr(block, attr)(mk(eng))


def _const_tables():
    p = np.arange(128)[:, None]
    j = np.arange(128)[None, :]
    f = {}
    f['ident'] = (p == j)
    f['ones'] = np.ones((128, 128))
    f['U'] = (p <= j)
    f['LS'] = (p > j)
    sb = (p // 4 == j // 4) & (p < 64) & (j < 64)
    f['Us'] = ((p <= j) & sb)[:, :64]
    f['LSs'] = ((p > j) & sb)[:, :64]
    f['BOs'] = sb[:, :64]
    f['SELp'] = np.repeat(p == 127, 128, axis=1)
    f['SELm'] = np.repeat(p == 15, 128, axis=1)
    b16 = np.arange(16)[None, :]
    f['RS'] = (p == 4 * b16 + 3)
    f['BM'] = (p // 4 == b16) & (p < 64)
    f['BMT'] = ((p < 16) & (j // 4 == p))[:, :64]
    f['LSEL'] = ((j == 4 * (p // 4) + 3) & (p < 64))[:, :64]
    f['H0'] = np.repeat(p < 64, 128, axis=1) & (j < 64)
    f['H1'] = np.repeat(p < 64, 128, axis=1) & (j >= 64)
    cf_off, cols = {}, []
    o = 0
    for k, v in f.items():
        cf_off[k] = (o, v.shape[1])
        o += v.shape[1]
        cols.append(v.astype(np.float32))
    cf = np.concatenate(cols, axis=1)
    g = {}
    g['identb'] = (p == j).astype(np.float32)
    g['onesb'] = np.ones((128, 128), np.float32)
    g['M'] = np.where(j <= p, 0.0, NEG)
    g['MT'] = np.where(p <= j, 0.0, NEG)
    g['Ms'] = np.where((j <= p) & sb, 0.0, NEG)[:, :64]
    g['MTs'] = np.where((p <= j) & sb, 0.0, NEG)[:, :64]
    jj = np.arange(64)[None, None, :]
    bb = np.arange(16)[None, :, None]
    g['CM'] = np.broadcast_to((jj // 4 == bb), (128, 16, 64)).reshape(128, 1024).astype(np.float32)
    cb_off, cols = {}, []
    o = 0
    for k, v in g.items():
        cb_off[k] = (o, v.shape[1])
        o += v.shape[1]
        cols.append(np.asarray(v, np.float32))
    cbm = np.concatenate(cols, axis=1).astype(ml_dtypes.bfloat16)
    return cf, cf_off, cbm, cb_off


class Chunk:
    def __init__(self, slot, col0, n, kind, row0=0):
        self.slot, self.col0, self.n, self.kind, self.row0 = slot, col0, n, kind, row0


class Tile:
    def __init__(self, name, T, chunks, segs):
        self.name, self.T, self.chunks, self.segs = name, T, chunks, segs


W_SHAPES = {'w_in': (D, DIN), 'w_proj_a': (D, D), 'w_proj_b': (2 * D, D), 'w_out': (D, D),
            'w_up': (D, 2 * DFF), 'w_down': (DFF, D)}


def tile_blocks():
    bl = []
    for c in range(0, 3072, 256):
        bl.append(('w_in', 0, 8, [(c, 256)]))
    bl.append(('w_in', 0, 8, [(3072, 8), (8200, 32)]))
    for c in range(3080, 5128, 256):
        bl.append(('w_in', 0, 8, [(c, 256)]))
    for c in range(5128, 8200, 256):
        bl.append(('w_in', 0, 8, [(c, 256)]))
    for c in range(8232, 10280, 256):
        bl.append(('w_in', 0, 8, [(c, 256)]))
    for j in range(4):
        bl.append(('w_proj_a', 0, 8, [(256 * j, 256)]))
        bl.append(('w_proj_b', 0, 8, [(256 * j, 256)]))
        bl.append(('w_proj_b', 8, 8, [(256 * j, 256)]))
    for j in range(4):
        bl.append(('w_out', 0, 8, [(256 * j, 256)]))
    for j in range(11):
        bl.append(('w_up', 0, 8, [(256 * j, 256)]))
        bl.append(('w_up', 0, 8, [(DFF + 256 * j, 256)]))
    for j in range(4):
        for k0, nk in ((0, 8), (8, 8), (16, 6)):
            bl.append(('w_down', k0, nk, [(256 * j, 256)]))
    return bl


class K:
    def __init__(self, debug=None, tiles=('T0', 'T1', 'T2', 'T3', 'T4')):
        self.debug = debug or {}
        self.tile_sel = tuple(tiles)
        self.ntiles = len(self.tile_sel)
        nc = bass.Bass('TRN2', target_bir_lowering=False)
        self.nc = nc
        self.S = S = Sched(nc)
        self.dumps = {}
        cf, self.cfo, cbm, self.cbo = _const_tables()
        self.cf_np, self.cb_np = cf, cbm
        din = lambda n, s, dt=F32: S.dram(n, s, dt, 'ExternalInput')
        dout = lambda n, s: S.dram(n, s, F32, 'ExternalOutput')
        I = self.I = {}
        I['xp'] = din('xp', [2048, D]); I['xs'] = din('xs', [64, D]); I['meta'] = din('meta', [16, D])
        I['s_mconv'] = din('s_mconv', [48, 1024]); I['s_C'] = din('s_C', [16, 4, 128, 256])
        I['s_n'] = din('s_n', [64, 128]); I['s_m'] = din('s_m', [16, 4])
        I['s_sconv'] = din('s_sconv', [48, 3072]); I['s_ssm'] = din('s_ssm', [16, 2048, 128])
        I['s_fconv'] = din('s_fconv', [32, 2 * DFF])
        I['cf'] = din('cf', list(cf.shape)); I['cb'] = din('cb', list(cbm.shape), BF16)
        for n, s in (('ln0_g', [D]), ('ln0_b', [D]), ('b_if', [8]), ('w_mconv', [4, 1024]), ('b_mconv', [1, 1024]),
                     ('mnorm_g', [1, 1024]), ('w_sconv', [4, 3072]), ('b_sconv', [1, 3072]), ('dt_bias', [32]),
                     ('A_log', [32]), ('ssm_D', [32]), ('snorm_g', [1, 2048]), ('ln1_g', [D]), ('ln1_b', [D]),
                     ('w_fconv', [3, 2 * DFF]), ('b_fconv', [1, 2 * DFF]), ('ln2_g', [D]), ('ln2_b', [D])):
            I[n] = din(n, s)
        for n, s in W_SHAPES.items():
            I[n] = din(n, list(s))
        O = self.O = {}
        O['y_p'] = dout('y_p', [2048, D]); O['y_s'] = dout('y_s', [64, D])
        O['p_mconv'] = dout('p_mconv', [3, 1024]); O['p_C'] = dout('p_C', [4, 128, 256])
        O['p_n'] = dout('p_n', [4, 128]); O['p_m'] = dout('p_m', [1, 4])
        O['p_sconv'] = dout('p_sconv', [3, 3072]); O['p_ssm'] = dout('p_ssm', [2048, 128])
        O['p_fconv'] = dout('p_fconv', [2, 2 * DFF])
        O['o_mconv'] = dout('o_mconv', [48, 1024]); O['o_C'] = dout('o_C', [16, 4, 128, 256])
        O['o_n'] = dout('o_n', [64, 128]); O['o_m'] = dout('o_m', [16, 4])
        O['o_sconv'] = dout('o_sconv', [48, 3072]); O['o_ssm'] = dout('o_ssm', [16, 2048, 128])
        O['o_fconv'] = dout('o_fconv', [32, 2 * DFF])
        self.rr = {}
        self.build()
        S.emit()

    def rot(self, key, n):
        i = self.rr.get(key, 0)
        self.rr[key] = i + 1
        return i % n

    def mm(self, out, lhsT, rhs, start=True, stop=True):
        self.S.op('pe', lambda e: e.matmul(out.ap, lhsT=lhsT.ap, rhs=rhs.ap, start=start, stop=stop),
                  reads=[lhsT, rhs], writes=[out])

    def tr(self, out, in_, ident):
        self.S.op('pe', lambda e: e.transpose(out=out.ap, in_=in_.ap, identity=ident.ap),
                  reads=[in_, ident], writes=[out])

    def act(self, out, in_, func=AF.Copy, bias=None, scale=None, accum=None):
        kw = {}
        if bias is not None:
            kw['bias'] = bias.ap if isinstance(bias, View) else bias
        if scale is not None:
            kw['scale'] = scale.ap if isinstance(scale, View) else scale
        if accum is not None:
            kw['accum_out'] = accum.ap
        self.S.op('act', lambda e: e.activation(out=out.ap, in_=in_.ap, func=func, **kw),
                  reads=[in_, bias, scale], writes=[out, accum])

    def tt(self, out, a, b, op, eng='dve'):
        self.S.op(eng, lambda e: e.tensor_tensor(out=out.ap, in0=a.ap, in1=b.ap, op=op),
                  reads=[a, b], writes=[out])

    def ts(self, out, a, s1, op0, s2=None, op1=None, eng='dve', accum=None):
        v1 = s1.ap if isinstance(s1, View) else s1
        v2 = s2.ap if isinstance(s2, View) else s2
        kw = {}
        if op1 is not None:
            kw['op1'] = op1
        if accum is not None:
            kw['accum_out'] = accum.ap
        self.S.op(eng, lambda e: e.tensor_scalar(out=out.ap, in0=a.ap, scalar1=v1, scalar2=v2, op0=op0, **kw),
                  reads=[a, s1, s2], writes=[out, accum])

    def stt(self, out, a, s, b, op0, op1, eng='dve'):
        v = s.ap if isinstance(s, View) else s
        self.S.op(eng, lambda e: e.scalar_tensor_tensor(out=out.ap, in0=a.ap, scalar=v, in1=b.ap, op0=op0, op1=op1),
                  reads=[a, s, b], writes=[out])

    def cp(self, out, in_, eng='dve'):
        if eng == 'act':
            return self.act(out, in_)
        self.S.op(eng, lambda e: e.tensor_copy(out=out.ap, in_=in_.ap), reads=[in_], writes=[out])

    def memset(self, out, val, eng='dve'):
        self.S.op(eng, lambda e: e.memset(out.ap, val), writes=[out])

    def rmax(self, out, in_, eng='dve'):
        self.S.op(eng, lambda e: e.tensor_reduce(out=out.ap, in_=in_.ap, axis=AX.X, op=ALU.max),
                  reads=[in_], writes=[out])

    def dma(self, out, in_, q='sp'):
        if out.buf.space == 'dram' and q == 'sp':
            q = 'pool'
        self.S.dma(q, out, in_)

    def dump(self, name, view, shape):
        if name not in self.debug:
            return
        d = self.S.dram('dbg_' + name, list(shape), view.ap.dtype, 'ExternalOutput')
        self.dumps[name] = d
        self.dma(d.ap(), view)

    def cfv(self, name, rows=128, cols=None):
        o, w = self.cfo[name]
        cols = w if cols is None else cols
        return self.cf[:rows, o:o + cols]

    def cbv(self, name, rows=128, cols=None):
        o, w = self.cbo[name]
        cols = w if cols is None else cols
        return self.cb[:rows, o:o + cols]

    def ws_init(self):
        S = self.S
        self.wlist = tile_blocks()
        self.nbt = len(self.wlist)
        self.wblocks = self.wlist * self.ntiles
        self.wst = [S.sbuf(f'wst{i}', [128, 8, 256], F32) for i in range(2)]
        self.wbf = [S.sbuf(f'wbf{i}', [128, 8, 256], BF16) for i in range(2)]
        self.wx4 = [S.sbuf(f'wrx{i}', [128, 8, 256], BF16, at=self.wst[i // 2].off + 4096 * (i % 2)) for i in range(4)]
        self.wring = self.wbf + self.wx4
        self.wscr = [S.dram(f'wscr{j}', [128, 8, 256], BF16, 'Internal') for j in range(self.nbt)] if self.ntiles > 1 else None
        self.w_loaded = 0
        self.w_cast = 0
        self.w_next = 0
        self.w_ring_started = False

    def _w_load(self, i):
        name, k0, nk, parts = self.wblocks[i]
        st = self.wst[i % 2]
        W = self.I[name]
        c = 0
        for (c0, n) in parts:
            src = View(W, W.t[k0 * 128:(k0 + nk) * 128, c0:c0 + n].rearrange('(k p) c -> p k c', p=128))
            self.dma(st[:, 0:nk, c:c + n], src, q='sp')
            c += n

    def _w_castop(self, i):
        name, k0, nk, parts = self.wblocks[i]
        n = sum(p[1] for p in parts)
        eng = 'dve' if (i % 4) != 3 else 'act'
        if name in ('w_proj_a', 'w_proj_b'):
            for k in range(nk):
                gcol = self.cwm[:, k0 + k, 5:6] if name == 'w_proj_a' else self.sng[:, k0 + k:k0 + k + 1]
                self.ts(self.wbf[i % 2][:, k, 0:n], self.wst[i % 2][:, k, 0:n], gcol, ALU.mult)
        else:
            self.cp(self.wbf[i % 2][:, 0:nk, 0:n], self.wst[i % 2][:, 0:nk, 0:n], eng)
        if self.wscr is not None:
            self.dma(self.wscr[i][:, 0:nk, 0:n], self.wbf[i % 2][:, 0:nk, 0:n], q='pool')

    def _w_ringload(self, i):
        name, k0, nk, parts = self.wblocks[i]
        n = sum(p[1] for p in parts)
        dst = self.wring[(i - self.nbt) % 6]
        self.dma(dst[:, 0:nk, 0:n], self.wscr[i % self.nbt][:, 0:nk, 0:n], q='sp')

    def wnext(self):
        i = self.w_next
        nb = len(self.wblocks)
        self.w_next += 1
        if i < self.nbt:
            lim = self.nbt
            while self.w_loaded < min(lim, i + 2):
                self._w_load(self.w_loaded)
                self.w_loaded += 1
            while self.w_cast < min(lim, i + 2):
                self._w_castop(self.w_cast)
                self.w_cast += 1
            while self.w_loaded < min(lim, i + 3):
                self._w_load(self.w_loaded)
                self.w_loaded += 1
            return self.wbf[i % 2], self.wblocks[i]
        if not self.w_ring_started:
            self.w_ring_started = True
            self.S.alias_phase(self.wst, self.wx4)
            self.w_loaded = self.nbt
        while self.w_loaded < min(nb, i + 6):
            self._w_ringload(self.w_loaded)
            self.w_loaded += 1
        return self.wring[(i - self.nbt) % 6], self.wblocks[i]

    def build(self):
        S = self.S
        sb = S.sbuf
        ncf, ncb = self.cf_np.shape[1], self.cb_np.shape[1]
        self.cf = sb('cf', [128, ncf], F32)
        self.cb = sb('cb', [128, ncb], BF16)
        self.lnc = sb('lnc', [128, 2, D], F32)
        self.bif_b = sb('bif_b', [128, 8], F32)
        self.dtb_b = sb('dtb_b', [128, 32], F32)
        self.A_b = sb('A_b', [128, 32], F32)
        self.D_b = sb('D_b', [128, 32], F32)
        self.Dfm = sb('Dfm', [128, 16], F32)
        self.cwm = sb('cwm', [128, 8, 6], F32)
        self.cws = sb('cws', [128, 24, 5], F32)
        self.sng = sb('sng', [128, 16], F32)
        self.cwf = sb('cwf', [128, 44, 4], F32)
        self.ws_init()
        self.xr = [sb(f'xr{i}', [128, D], F32) for i in range(4)]
        self.zs = [None] * 4
        self.xnT = sb('xnT', [128, 8, 512], BF16)
        self.hgT = sb('hgT', [128, 8, 512], BF16)
        self.ygT = sb('ygT', [128, 16, 512], BF16)
        self.Cf = sb('Cf', [128, 4, 256], F32); self.Cb = sb('Cb', [128, 4, 256], BF16)
        self.nf = sb('nf', [128, 4], F32); self.nb = sb('nb', [128, 4], BF16)
        self.m_b = sb('m_b', [128, 4], F32)
        self.STf = sb('STf', [128, 2048], F32); self.STb = sb('STb', [128, 2048], BF16)
        self.cq = sb('cq', [128, 8, 3], F32); self.cx = sb('cx', [128, 24, 3], F32)
        self.cff = sb('cff', [128, 44, 2], F32)
        self.scar = sb('scar', [128, 44 * 16 * 2], F32)
        self.gat = sb('gat', [128, 4, 8], F32)
        self.dta = sb('dta', [128, 4, 64], F32)
        self.ifdt = sb('ifdt', [128, 4, 40], F32)
        self.sm = [sb(f'sm{i}', [128, 32], F32) for i in range(16)]
        self.xb16 = sb('xb16', [128, D], BF16)
        self.lnsc = [sb(f'lnsc{i}', [128, 16], F32) for i in range(4)]
        self.cst = sb('cst', [128, 8], F32)
        self.pn_st = sb('pn_st', [128, 128], F32)
        R0 = S.sb_off
        o = R0
        def at(name, shape, dt):
            nonlocal o
            b = sb(name, shape, dt, at=o)
            o += b.nbytes
            return b
        self.cE = [at(f'cE{i}', [128, 520], F32) for i in range(2)]
        self.cacc = [at(f'cacc{i}', [128, 512], F32) for i in range(3)]
        self.cth = [at(f'cth{i}', [128, 512], F32) for i in range(2)]
        self.cacc2 = [at(f'cacc2_{i}', [128, 512], F32) for i in range(2)]
        e1 = o
        o = R0
        self.R1 = at('R1', [128, 4, 128], F32); self.R2 = at('R2', [128, 4, 128], F32)
        self.R3 = at('R3', [128, 4, 128], F32); self.wT = at('wT', [128, 4, 128], F32)
        self.ST = at('ST', [128, 4, 128], BF16); self.kTM = at('kTM', [128, 4, 128], BF16)
        self.hh = at('hh', [128, 4, 256], F32); self.vw = at('vw', [128, 4, 256], BF16)
        self.hgTM = at('hgTM', [128, D], BF16)
        e2 = o
        o = R0
        self.xdt = at('xdt', [128, 2048], BF16); self.xsD = at('xsD', [128, 2048], BF16)
        self.wx = at('wx', [128, 512], BF16); self.BTM = at('BTM', [128, 512], BF16)
        self.LT = at('LT', [128, 8, 128], BF16); self.MTt = at('MTt', [128, 8, 128], BF16)
        self.t1 = at('t1', [128, 512], F32)
        self.yz = at('yz', [128, 2048], BF16)
        self.ynTM = at('ynTM', [128, 2048], BF16)
        e3 = o
        F0 = max(e1, e2, e3)
        conv_end = F0
        o = F0
        self.qkT = at('qkT', [128, 8, 512], BF16)
        self.v = [at(f'v{i}', [128, D], BF16) for i in range(4)]
        self.oth = [at(f'oth{i}', [128, D], BF16) for i in range(4)]
        a1_end = o
        o = F0
        self.xbcT = at('xbcT', [128, 24, 512], BF16)
        for i in range(2):
            self.zs[i] = at(f'zs{i}', [128, 2048], BF16)
        self.LA = at('LA', [128, 8, 128], F32)
        a2_end = o
        o = F0
        self.gth = at('gth', [128, 16, 512], BF16)
        self.mixT = at('mixT', [128, 8, 512], BF16)
        self.hffT = at('hffT', [128, 22, 512], BF16)
        b_end = o
        S.sb_off = max(a1_end, a2_end, b_end)
        for i in range(2, 4):
            self.zs[i] = sb(f'zs{i}', [128, 2048], BF16)
        self.arenas = [(self.xr[2].off, 2 * self.xr[2].nbytes), (self.zs[2].off, 2 * self.zs[2].nbytes)]
        a0, a1 = self.arenas[0][0], self.arenas[1][0]
        self.C0b = [sb(f'C0b{i}', [128, 4, 256], F32, at=a0 + 4096 * i) for i in range(2)]
        self.C0b16 = [sb(f'C0b16_{i}', [128, 4, 258], BF16, at=a1 + 2080 * i) for i in range(2)]
        self.qmb = [sb(f'qmb{i}', [128, 4, 64], BF16, at=a1 + 4160 + 512 * i) for i in range(2)]
        self.kTMm = [sb(f'kTMm{i}', [128, 4, 128], BF16, at=a1 + 5184 + 1024 * i) for i in range(2)]
        self.n0T = sb('n0T', [128, 64], F32, at=a1 + 7232)
        self.n16 = sb('n16', [128, 64], BF16, at=a1 + 7488)
        self.decS = sb('decS', [128, 64], F32, at=a1 + 7616)
        self.Rm = sb('Rm', [128, 64], F32, at=a1 + 7872)
        self.S0b = [sb('S0b0', [128, 16, 128], F32, at=a0), sb('S0b1', [128, 16, 128], F32, at=self.LT.off)]
        assert self.LT.off + 8192 <= self.ynTM.off
        self.SbT = sb('SbT', [128, 2048], BF16, at=a1)
        self.wxm = sb('wxm', [128, 2048], BF16, at=a1 + 4096)
        self.ysi = sb('ysi', [128, 2048], BF16)
        self.decP = sb('decP', [128, 256], F32)
        self.Rr = sb('Rr', [128, 2, 256], F32)
        self.CTmb = [sb(f'CTmb{i}', [128, 4, 64], BF16) for i in range(2)]
        self.grpArena2 = [self.S0b[0], self.SbT, self.wxm]
        self.xsDT = sb('xsDT', [128, 8, 128], BF16, at=self.ynTM.off)
        self.LAh = [sb(f'LAh{j}', [128, 128], F32, at=self.LA.off + 512 * j) for j in range(8)]
        self.LTh = [sb(f'LTh{j}', [128, 4, 128], BF16, at=self.LT.off + 1024 * j) for j in range(2)]
        self.MTh = [[sb(f'MTh{b}_{j}', [128, 128], BF16, at=base + 256 * j) for j in range(8)]
                    for b, base in enumerate((self.MTt.off, self.ynTM.off + 2048))]
        self.grpArena = self.C0b + self.C0b16 + self.qmb + self.kTMm + [self.n0T, self.n16, self.decS, self.Rm]
        self.grpA1conv = self.cE + self.cacc + self.cth + self.cacc2
        self.grpA1rec = [self.R1, self.R2, self.R3, self.wT, self.ST, self.kTM, self.hh, self.vw, self.hgTM]
        self.grpA1fix = [self.qkT] + self.v + self.oth
        self.grpA2fix = [self.xbcT, self.zs[0], self.zs[1]] + self.LAh
        self.grpA2rec = [self.xdt, self.xsD, self.wx, self.BTM, self.t1, self.yz, self.ynTM] + self.LTh + self.MTh[0]
        self.grpB = [self.gth, self.mixT, self.hffT]
        print('SBUF used', S.sb_off, 'of', S.sb_end, 'R', R0, conv_end - R0, a1_end - R0, a2_end - R0, b_end - R0)
        self.ps = [S.psum(f'ps{i}', [128, 512], F32) for i in range(8)]
        self.setup()
        tiles = self.make_tiles()
        for tl in tiles:
            if tl.name in self.tile_sel:
                self.run_tile(tl, last=(tl.name == 'T4'))

    def make_tiles(self):
        def pch(slot, col0, c):
            ch = Chunk(slot, col0, 128, 'p', row0=128 * c)
            ch.final = (c == 15)
            return ch
        m = Chunk(0, 0, 16, 'm'); m.final = False
        tiles = [Tile('T0', 400, [m] + [pch(1 + i, 16 + 128 * i, i) for i in range(3)], [(0, 1, 400, 'p')])]
        for t in range(3):
            tiles.append(Tile(f'T{t + 1}', 512, [pch(i, 128 * i, 3 + 4 * t + i) for i in range(4)], [(0, 1, 512, 'p')]))
        sc = Chunk(1, 128, 64, 's'); sc.final = False
        tiles.append(Tile('T4', 192, [pch(0, 0, 15), sc], [(0, 1, 128, 'p'), (128, 16, 4, 's')]))
        return tiles

    def psb(self, i):
        return self.ps[i].ap().bitcast(BF16)

    def setup(self):
        I = self.I
        self.dma(self.cf.ap(), I['cf'].ap())
        self.dma(self.cb.ap(), I['cb'].ap())
        pb = lambda n: View(I[n], I[n].t.partition_broadcast(128))
        self.dma(self.bif_b.ap(), pb('b_if'))
        self.dma(self.dtb_b.ap(), pb('dt_bias'))
        self.dma(self.A_b.ap(), pb('A_log'))
        self.dma(self.D_b.ap(), pb('ssm_D'))
        self.act(self.A_b.ap(), self.A_b.ap(), AF.Exp)
        self.ts(self.A_b.ap(), self.A_b.ap(), -1.0, ALU.mult)
        D3 = self.D_b.ap().rearrange('p (g r) -> p g r', r=2)
        self.cp(self.Dfm[0:64, :], D3[0:64, :, 0], 'dve')
        self.cp(self.Dfm[64:128, :], D3[64:128, :, 1], 'dve')
        identf = self.cfv('ident')
        stg = self.cacc[0]
        def fm_params(dst, rows, G, scale_groups=None):
            R = sum(r for _, r in rows)
            for g0 in range(0, G, 4):
                gn = min(4, G - g0)
                r0 = 0
                for (nm, nr) in rows:
                    self.dma(stg[r0:r0 + nr, 0:gn * 128], I[nm][:, g0 * 128:(g0 + gn) * 128])
                    r0 += nr
                bank = self.ps[self.rot('setup', 2)]
                for g in range(gn):
                    self.tr(bank[:, g * R:(g + 1) * R], stg[0:R, g * 128:(g + 1) * 128], identf[0:R, 0:R])
                self.cp(dst[:, g0:g0 + gn, :], bank[:, 0:gn * R].rearrange('p (g r) -> p g r', r=R), 'act')
        fm_params(self.cwm, [('w_mconv', 4), ('b_mconv', 1), ('mnorm_g', 1)], 8)
        fm_params(self.cws, [('w_sconv', 4), ('b_sconv', 1)], 24)
        fm_params(self.cwf, [('w_fconv', 3), ('b_fconv', 1)], 44)
        sng3 = self.sng.ap().rearrange('p (g r) -> p g r', r=1)
        fm_params(sng3, [('snorm_g', 1)], 16)
        self.ts(self.cwm[:, :, 0:6], self.cwm[:, :, 0:6], 0.5, ALU.mult)
        self.ts(self.cws.ap(), self.cws.ap(), 0.5, ALU.mult)
        self.ts(self.cwf[:, 0:22, :], self.cwf[:, 0:22, :], 0.5, ALU.mult)
        self.memset(self.cst[:, 0:1], LN_EPS)
        self.memset(self.cst[:, 1:2], 0.5 * float(np.log(128.0)))
        self.memset(self.cst[:, 2:3], 1.0)
        self.eps_t = self.cst
        for b in (self.Cf, self.nf, self.m_b, self.STf, self.cq, self.cx, self.cff):
            self.memset(b.ap(), 0.0)
        for b in (self.Cb, self.nb, self.STb):
            self.memset(b.ap(), 0.0, 'pool')

    def kc(self, kind, n):
        if kind == 's':
            return dict(U=self.cfv('Us', 64), LS=self.cfv('LSs', 64), BO=self.cfv('BOs', 64),
                        M=self.cbv('Ms', 64), MT=self.cbv('MTs', 64))
        return dict(U=self.cfv('U', n, n), LS=self.cfv('LS', n, n), BO=self.cfv('ones', n, n),
                    M=self.cbv('M', n, n), MT=self.cbv('MT', n, n))

    def ln_load(self, gname, bname):
        I = self.I
        self.dma(self.lnc[:, 0, :], View(I[gname], I[gname].t.partition_broadcast(128)))
        self.dma(self.lnc[:, 1, :], View(I[bname], I[bname].t.partition_broadcast(128)))

    def ln_rows(self, x, n, slot):
        sc = self.lnsc[slot]
        st, mv, rs = sc[:n, 0:12], sc[:n, 12:14], sc[:n, 14:15]
        for i in range(2):
            self.S.op('dve', lambda e, i=i: e.bn_stats(out=sc.t[:n, i * 6:(i + 1) * 6], in_=x.ap[:, i * 512:(i + 1) * 512]),
                      reads=[x], writes=[sc])
        self.S.op('dve', lambda e: e.bn_aggr(out=sc.t[:n, 12:14], in_=sc.t[:n, 0:12]), reads=[sc], writes=[sc])
        self.act(rs, sc[:n, 13:14], AF.Ln, bias=self.eps_t[:n, 0:1])
        self.act(rs, rs, AF.Exp, scale=-0.5)
        self.ts(x, x, sc[:n, 12:13], ALU.subtract, rs, ALU.mult)
        self.tt(x, x, self.lnc[:n, 0, :], ALU.mult)
        self.tt(x, x, self.lnc[:n, 1, :], ALU.add)

    def to_fm(self, tl, src_of_chunk, dstT):
        identb = self.cbv('identb')
        for ch in tl.chunks:
            n = ch.n
            xb = self.xb16
            self.act(xb[:n, :], src_of_chunk(ch))
            bank = 6 + self.rot('tfm', 2)
            pv = self.psb(bank)
            for k in range(8):
                self.tr(pv[:, k * n:(k + 1) * n], xb[:n, k * 128:(k + 1) * 128], identb[:n, :n])
            self.cp(dstT[:, :, ch.col0:ch.col0 + n], pv[:, 0:8 * n].rearrange('p (k n) -> p k n', n=n), 'dve')

    def _conv_taps(self, tl, psv, W, wtab, g, carry_p, scar_view, E, acc):
        Wm = W - 1
        off = 0
        for (col0, nb, L, kind) in tl.segs:
            Ev = E[:, off:off + nb * (L + Wm)].rearrange('p (b l) -> p b l', b=nb)
            pseg = psv[:, col0:col0 + nb * L].rearrange('p (b l) -> p b l', b=nb)
            if kind == 'p':
                self.cp(Ev[:, :, 0:Wm], carry_p[:, g:g + 1, :], 'act')
            elif kind == 'm':
                self.memset(Ev[:, :, 0:Wm], 0.0, 'dve')
            else:
                self.cp(Ev[:, :, 0:Wm], scar_view[:, g, :, :], 'act')
            self.act(Ev[:, :, Wm:Wm + L], pseg)
            av = acc[:, col0:col0 + nb * L].rearrange('p (b l) -> p b l', b=nb)
            self.act(av, pseg, AF.Identity, scale=wtab[:, g, Wm:W], bias=wtab[:, g, W:W + 1])
            if kind == 's':
                self.cp(scar_view[:, g, :, :], Ev[:, :, L:L + Wm], 'act')
            else:
                self.cp(carry_p[:, g:g + 1, :], Ev[:, :, L:L + Wm], 'act')
            for j in range(Wm):
                self.stt(av, Ev[:, :, j:j + L], wtab[:, g, j:j + 1], av, ALU.mult, ALU.add)
            off += nb * (L + Wm)

    def conv_group(self, tl, psv, W, wtab, g, carry_p, scar_view, dst, final=True):
        E = self.cE[self.rot('cE', 2)]
        acc = self.cacc[self.rot('cacc', 3)]
        self._conv_taps(tl, psv, W, wtab, g, carry_p, scar_view, E, acc)
        T = tl.T
        if not final:
            return acc
        prev = getattr(self, '_conv_pending', None)

        def stage2(acc=acc, dst=dst, T=T):
            th = self.cth[self.rot('cth', 2)]
            self.act(th[:, 0:T], acc[:, 0:T], AF.Tanh)
            self.stt(dst, th[:, 0:T], 1.0, acc[:, 0:T], ALU.add, ALU.mult)
        self._conv_pending = stage2
        if prev is not None:
            prev()
        return acc

    def conv_flush(self):
        prev = getattr(self, '_conv_pending', None)
        self._conv_pending = None
        if prev is not None:
            prev()

    def carry_out(self, src, G, R, dst):
        identf = self.cfv('ident')
        for g0 in range(0, G, 4):
            gn = min(4, G - g0)
            bank = self.ps[self.rot('co', 2)]
            for g in range(gn):
                self.tr(bank[:R, g * 128:(g + 1) * 128], src[:, g0 + g, :], identf)
            stg = self.cacc[self.rot('cacc', 3)]
            self.cp(stg[:R, 0:gn * 128], bank[:R, 0:gn * 128], 'act')
            self.dma(dst[:, g0 * 128:(g0 + gn) * 128], stg[:R, 0:gn * 128])

    def scar_in(self, name, G, R):
        identf = self.cfv('ident')
        rows = 16 * R
        sv = self.scar[:, 0:G * rows].rearrange('p (g b r) -> p g b r', g=G, b=16)
        for g0 in range(0, G, 4):
            gn = min(4, G - g0)
            stg = self.cacc[self.rot('cacc', 3)]
            self.dma(stg[:rows, 0:gn * 128], self.I[name][:, g0 * 128:(g0 + gn) * 128])
            bank = self.ps[self.rot('co', 2)]
            for g in range(gn):
                self.tr(bank[:, g * rows:(g + 1) * rows], stg[:rows, g * 128:(g + 1) * 128], identf[:rows, :rows])
            self.cp(self.scar[:, g0 * rows:(g0 + gn) * rows], bank[:, 0:gn * rows], 'act')
        return sv

    def scar_out(self, name, G, R):
        rows = 16 * R
        src = self.scar[:, 0:G * rows].rearrange('p (g br) -> p g br', g=G)
        self.carry_out(src, G, rows, self.O[name].ap())

    def dense_fm(self, tl, actT, nkt_total, cb_group, kt0=0):
        Wb, (name, k0, nk, parts) = self.wnext()
        ncols = sum(p[1] for p in parts)
        T = tl.T
        for gl in range(ncols // 128):
            bank = self.ps[self.rot('mm', 4)]
            for k in range(nk):
                self.mm(bank[:, 0:T], Wb[:, k, gl * 128:(gl + 1) * 128], actT[:, k0 + k, 0:T],
                        start=(k0 + k == 0), stop=(k0 + k == nkt_total - 1))
            cb_group(gl, bank[:, 0:T])

    def dense_tm(self, tl, actT, cb_chunk):
        Wb, (name, k0, nk, parts) = self.wnext()
        ncols = sum(p[1] for p in parts)
        for ch in tl.chunks:
            bank = self.ps[self.rot('mm', 4)]
            for k in range(nk):
                self.mm(bank[:ch.n, 0:ncols], actT[:, k0 + k, ch.col0:ch.col0 + ch.n], Wb[:, k, 0:ncols],
                        start=(k == 0), stop=(k == nk - 1))
            cb_chunk(ch, bank[:ch.n, 0:ncols])

    def run_tile(self, tl, last):
        S, I, O = self.S, self.I, self.O
        T = tl.T
        isS = any(sg[3] == 's' for sg in tl.segs)
        if isS:
            S.alias_phase([self.xr[2], self.xr[3], self.zs[2], self.zs[3]], self.grpArena + self.grpArena2)
        for ch in tl.chunks:
            src = {'s': I['xs'].ap(), 'm': I['meta'].ap()}.get(ch.kind)
            if src is None:
                src = I['xp'][ch.row0:ch.row0 + ch.n, :]
            self.dma(self.xr[ch.slot][:ch.n, :], src)
        self.ln_load('ln0_g', 'ln0_b')
        for ch in tl.chunks:
            self.ln_rows(self.xr[ch.slot][:ch.n, :], ch.n, ch.slot)
        self.to_fm(tl, lambda ch: self.xr[ch.slot][:ch.n, :], self.xnT)
        for ch in tl.chunks:
            self.ts(self.xr[ch.slot][:ch.n, :], self.xr[ch.slot][:ch.n, :], ALPHA, ALU.mult)
        self.dump('xnT_' + tl.name, self.xnT[:, :, 0:T], [128, 8, T])
        if self.debug.get('stop') == 'p0':
            return
        S.alias_phase(self.grpA2fix + self.grpA2rec + self.grpB + self.grpA1rec, self.grpA1conv + self.grpA1fix)
        sq = self.scar_in('s_mconv', 8, 3) if isS else None
        for blk in range(4):
            def cbq(gl, psv, blk=blk):
                g = 2 * blk + gl
                self.conv_group(tl, psv, 4, self.cwm, g, self.cq, sq, self.qkT[:, g, 0:T])
            self.dense_fm(tl, self.xnT, 8, cbq)
        self.conv_flush()
        if isS:
            self.scar_out('o_mconv', 8, 3)
        if last:
            self.carry_out(self.cq.ap(), 8, 3, O['p_mconv'].ap())
        for blk in range(4):
            self.dense_tm(tl, self.xnT, lambda ch, psv, blk=blk: self.act(self.v[ch.slot][:ch.n, 256 * blk:256 * blk + 256], psv))
        for blk in range(4):
            self.dense_tm(tl, self.xnT, lambda ch, psv, blk=blk: self.act(self.oth[ch.slot][:ch.n, 256 * blk:256 * blk + 256], psv, AF.Tanh, scale=0.5))
        self.dense_tm(tl, self.xnT, lambda ch, psv: self.cp(self.ifdt[:ch.n, ch.slot, :], psv, 'dve'))
        for ch in tl.chunks:
            n, s = ch.n, ch.slot
            gi = self.gat[:n, s, 0:8]
            self.tt(gi, self.ifdt[:n, s, 0:8], self.bif_b[:n, :], ALU.add)
            e1 = self.sm[3]
            self.act(e1[:n, 0:4], self.gat[:n, s, 4:8], AF.Exp, scale=-1.0)
            self.act(e1[:n, 0:4], e1[:n, 0:4], AF.Ln, bias=self.cst[:n, 2:3])
            self.ts(self.gat[:n, s, 4:8], e1[:n, 0:4], -1.0, ALU.mult)
            d1 = self.sm[4]
            self.tt(d1[:n, 0:32], self.ifdt[:n, s, 8:40], self.dtb_b[:n, :], ALU.add)
            self.act(d1[:n, 0:32], d1[:n, 0:32], AF.Exp)
            self.act(self.dta[:n, s, 0:32], d1[:n, 0:32], AF.Ln, bias=self.cst[:n, 2:3])
            self.tt(self.dta[:n, s, 32:64], self.dta[:n, s, 0:32], self.A_b[:n, :], ALU.mult)
        self.dump('qkT_' + tl.name, self.qkT[:, :, 0:T], [128, 8, T])
        self.dump('gat_' + tl.name, self.gat.ap(), [128, 4, 8])
        self.dump('dta_' + tl.name, self.dta.ap(), [128, 4, 64])
        if self.debug.get('stop') == 'a1':
            return
        S.alias_phase(self.grpA1conv, self.grpA1rec)
        for ch in tl.chunks:
            self.mlstm_chunk(tl, ch, last)
        self.dump('hgT_' + tl.name, self.hgT[:, :, 0:T], [128, 8, T])
        if self.debug.get('stop') == 'mlstm':
            return
        S.alias_phase(self.grpA1rec + self.grpA1fix, self.grpA1conv + self.grpA2fix)
        for blk in range(8):
            def cbz(ch, psv, blk=blk):
                n = ch.n
                zc = self.cacc[self.rot('cacc', 3)]
                th = self.cth[self.rot('cth', 2)]
                self.cp(zc[:n, 0:256], psv, 'act')
                self.act(th[:n, 0:256], psv, AF.Tanh, scale=0.5)
                self.stt(self.zs[ch.slot][:n, 256 * blk:256 * blk + 256], th[:n, 0:256], 1.0, zc[:n, 0:256], ALU.add, ALU.mult)
            self.dense_tm(tl, self.xnT, cbz)
        if self.debug.get('stop') == 'a2z':
            return
        sx = self.scar_in('s_sconv', 24, 3) if isS else None
        for blk in range(12):
            def cbx(gl, psv, blk=blk):
                g = 2 * blk + gl
                self.conv_group(tl, psv, 4, self.cws, g, self.cx, sx, self.xbcT[:, g, 0:T])
            self.dense_fm(tl, self.xnT, 8, cbx)
        self.conv_flush()
        if isS:
            self.scar_out('o_sconv', 24, 3)
        if last:
            self.carry_out(self.cx.ap(), 24, 3, O['p_sconv'].ap())
        self.dump('xbcT_' + tl.name, self.xbcT[:, :, 0:T], [128, 24, T])
        if self.debug.get('stop') == 'a2':
            return
        S.alias_phase(self.grpA1conv, self.grpA2rec)
        for ch in tl.chunks:
            self.ssd_chunk(tl, ch, last)
        self.dump('ygT_' + tl.name, self.ygT[:, :, 0:T], [128, 16, T])
        if self.debug.get('stop') == 'ssd':
            return
        S.alias_phase(self.grpA2rec + self.grpA2fix, self.grpA1conv + self.grpB)
        for blk in range(8):
            def cbg(gl, psv, blk=blk):
                self.act(self.gth[:, 2 * blk + gl, 0:T], psv, AF.Tanh, scale=0.5)
            self.dense_fm(tl, self.xnT, 8, cbg)
        for j in range(4):
            Wb, (name, k0, nk, parts) = self.wnext()
            banksA = [self.ps[0], self.ps[1]]
            banksB = [self.ps[2], self.ps[3]]
            for gl in range(2):
                for k in range(8):
                    self.mm(banksA[gl][:, 0:T], Wb[:, k, gl * 128:(gl + 1) * 128], self.hgT[:, k, 0:T], start=(k == 0), stop=(k == 7))
            for half in range(2):
                Wb, (name, k0, nk, parts) = self.wnext()
                for gl in range(2):
                    for k in range(8):
                        kk = 8 * half + k
                        self.mm(banksB[gl][:, 0:T], Wb[:, k, gl * 128:(gl + 1) * 128], self.ygT[:, kk, 0:T], start=(kk == 0), stop=(kk == 15))
            for gl in range(2):
                g = 2 * j + gl
                m1 = self.cacc[self.rot('cacc', 3)]
                m2 = self.cacc2[self.rot('cacc2', 2)]
                self.stt(m1[:, 0:T], self.gth[:, g, 0:T], 1.0, banksA[gl][:, 0:T], ALU.add, ALU.mult)
                self.stt(m2[:, 0:T], self.gth[:, 8 + g, 0:T], 1.0, banksB[gl][:, 0:T], ALU.add, ALU.mult)
                self.tt(self.mixT[:, g, 0:T], m1[:, 0:T], m2[:, 0:T], ALU.add)
        self.rr['mm'] = 0
        for blk in range(4):
            def cbo(ch, psv, blk=blk):
                xv = self.xr[ch.slot][:ch.n, 256 * blk:256 * blk + 256]
                self.stt(xv, psv, 0.5, xv, ALU.mult, ALU.add)
            self.dense_tm(tl, self.mixT, cbo)
        self.ln_load('ln1_g', 'ln1_b')
        for ch in tl.chunks:
            self.ln_rows(self.xr[ch.slot][:ch.n, :], ch.n, ch.slot)
        self.dump('x1_' + tl.name, self.xr[0].ap(), [128, D])
        self.to_fm(tl, lambda ch: self.xr[ch.slot][:ch.n, :], self.xnT)
        for ch in tl.chunks:
            self.ts(self.xr[ch.slot][:ch.n, :], self.xr[ch.slot][:ch.n, :], ALPHA, ALU.mult)
        sf = self.scar_in('s_fconv', 44, 2) if isS else None
        for j in range(11):
            accs = {}
            def cbua(gl, psv, j=j):
                g = 2 * j + gl
                accs[gl] = self.conv_group(tl, psv, 3, self.cwf, g, self.cff, sf, None, final=False)
            self.dense_fm(tl, self.xnT, 8, cbua)
            def cbub(gl, psv, j=j):
                g = 2 * j + gl
                E = self.cE[self.rot('cE', 2)]
                accb = self.cacc2[self.rot('cacc2', 2)]
                self._conv_taps(tl, psv, 3, self.cwf, 22 + g, self.cff, sf, E, accb)
                th = self.cth[self.rot('cth', 2)]
                acca = accs[gl]
                self.act(th[:, 0:T], acca[:, 0:T], AF.Tanh)
                self.stt(th[:, 0:T], th[:, 0:T], 1.0, acca[:, 0:T], ALU.add, ALU.mult)
                self.tt(self.hffT[:, g, 0:T], th[:, 0:T], accb[:, 0:T], ALU.mult)
            self.dense_fm(tl, self.xnT, 8, cbub)
        if isS:
            self.scar_out('o_fconv', 44, 2)
        if last:
            self.carry_out(self.cff.ap(), 44, 2, O['p_fconv'].ap())
        self.dump('hffT_' + tl.name, self.hffT[:, :, 0:T], [128, 22, T])
        for blk in range(4):
            banks = {ch.slot: self.ps[ch.slot] for ch in tl.chunks}
            for (k0, nk) in ((0, 8), (8, 8), (16, 6)):
                Wb, meta = self.wnext()
                for ch in tl.chunks:
                    for k in range(nk):
                        self.mm(banks[ch.slot][:ch.n, 0:256], self.hffT[:, k0 + k, ch.col0:ch.col0 + ch.n], Wb[:, k, 0:256],
                                start=(k0 + k == 0), stop=(k0 + k == 21))
            for ch in tl.chunks:
                xv = self.xr[ch.slot][:ch.n, 256 * blk:256 * blk + 256]
                self.tt(xv, banks[ch.slot][:ch.n, 0:256], xv, ALU.add)
        self.ln_load('ln2_g', 'ln2_b')
        for ch in tl.chunks:
            self.ln_rows(self.xr[ch.slot][:ch.n, :], ch.n, ch.slot)
            if ch.kind == 'p':
                self.dma(O['y_p'][ch.row0:ch.row0 + ch.n, :], self.xr[ch.slot][:ch.n, :])
            elif ch.kind == 's':
                self.dma(O['y_s'].ap(), self.xr[ch.slot][:ch.n, :])

    def mlstm_chunk(self, tl, ch, last):
        I, O = self.I, self.O
        n, s, c0, kind = ch.n, ch.slot, ch.col0, ch.kind
        kc = self.kc(kind, n)
        U, LS, M, MT = kc['U'], kc['LS'], kc['M'], kc['MT']
        identf, onesf = self.cfv('ident', n, n), self.cfv('ones', n, n)
        identb, onesb = self.cbv('identb', n, n), self.cbv('onesb', n, n)
        ps = self.ps
        cols = slice(c0, c0 + n)
        ig = self.gat[:n, s, 0:4]
        lf = self.gat[:n, s, 4:8]
        sm = self.sm
        bt, mi, bm, mt, wi, emt, negm, den, rden, wi2 = (sm[i] for i in range(5, 15))
        R1, R2, R3, wT, ST, kTM, hh, vw = self.R1, self.R2, self.R3, self.wT, self.ST, self.kTM, self.hh, self.vw
        self.mm(ps[0][:n, 0:4], U, lf)
        for h in range(4):
            self.ts(R1[:n, h, :n], LS, self.gat[:n, s, 4 + h:5 + h], ALU.mult)
            self.act(R2[:n, h, :n], identf, AF.Copy, scale=self.gat[:n, s, h:h + 1])
        for h in range(4):
            self.mm(ps[1][:n, h * 128:h * 128 + n], U, R1[:n, h, :n], start=True, stop=False)
            self.mm(ps[1][:n, h * 128:h * 128 + n], onesf, R2[:n, h, :n], start=False, stop=False)
            self.mm(ps[1][:n, h * 128:h * 128 + n], identb, M, start=False, stop=True)
        self.rmax(mi[:n, 0:4], ps[1][:n, :].rearrange('p (h t) -> p h t', h=4)[:, :, 0:n])
        if kind == 's':
            m0s = sm[15]
            self.dma(m0s[:16, 0:4], I['s_m'].ap())
            self.mm(ps[0][:n, 4:8], self.cfv('BMT', 16), m0s[:16, 0:4])
            m0v = ps[0][:n, 4:8]
        else:
            m0v = self.m_b[:n, :]
        self.cp(bt[:n, 0:4], ps[0][:n, 0:4], 'act')
        self.tt(bm[:n, 0:4], bt[:n, 0:4], m0v, ALU.add)
        self.tt(mt[:n, 0:4], bm[:n, 0:4], mi[:n, 0:4], ALU.max)
        self.tt(bm[:n, 0:4], bm[:n, 0:4], mt[:n, 0:4], ALU.subtract)
        self.act(wi[:n, 0:4], bm[:n, 0:4], AF.Exp)
        self.act(emt[:n, 0:4], mt[:n, 0:4], AF.Exp, scale=-1.0, bias=self.cst[:n, 1:2])
        self.ts(negm[:n, 0:4], mt[:n, 0:4], -1.0, ALU.mult)
        for h in range(4):
            self.act(R3[:n, h, :n], identf, AF.Copy, scale=negm[:n, h:h + 1])
        for h in range(4):
            o = ps[2][:n, h * 128:h * 128 + n]
            self.mm(o, R1[:n, h, :n], U, start=True, stop=False)
            self.mm(o, R2[:n, h, :n], onesf, start=False, stop=False)
            self.mm(o, onesf, R3[:n, h, :n], start=False, stop=False)
            self.mm(o, identb, MT, start=False, stop=True)
        ps2v = ps[2][:n, :].rearrange('p (h t) -> p h t', h=4)[:, :, 0:n]
        self.act(wT[:n, :, :n], ps2v, AF.Exp)
        for h in range(4):
            self.mm(ps[1][:n, h * 128:h * 128 + n], self.qkT[:, 4 + h, cols], self.qkT[:, h, cols])
        ps1v = ps[1][:n, :].rearrange('p (h t) -> p h t', h=4)[:, :, 0:n]
        self.tt(ST[:n, :, :n], ps1v, wT[:n, :, :n], ALU.mult)
        pb7 = self.psb(7)
        for h in range(4):
            self.tr(pb7[:n, h * 128:(h + 1) * 128], self.qkT[:, 4 + h, cols], self.cbv('identb'))
        self.cp(kTM[:n, :, :], pb7[:n, 0:512].rearrange('p (h d) -> p h d', h=4), 'act')
        if kind == 's':
            self.mlstm_sample_states(tl, ch, wi, wT, kTM, mt)
        for h in range(4):
            self.mm(ps[3 + h // 2][:n, (h % 2) * 256:(h % 2) * 256 + 256], ST[:n, h, :n], self.v[s][:n, h * 256:(h + 1) * 256])
        for h in range(4):
            self.mm(ps[0][:n, 8 + h:9 + h], ST[:n, h, :n], onesb[:, 0:1])
        if kind != 's':
            for h in range(4):
                self.mm(ps[5 + h // 2][:n, (h % 2) * 256:(h % 2) * 256 + 256], self.qkT[:, h, cols], self.Cb[:, h, :])
            for h in range(4):
                self.mm(ps[0][:n, 12 + h:13 + h], self.qkT[:, h, cols], self.nb[:, h:h + 1])
        dint = ps[0][:n, 12:16] if kind != 's' else sm[12][:n, 0:4]
        self.tt(den[:n, 0:4], dint, wi[:n, 0:4], ALU.mult)
        self.tt(den[:n, 0:4], den[:n, 0:4], ps[0][:n, 8:12], ALU.add)
        self.ts(wi2[:n, 0:4], den[:n, 0:4], -1.0, ALU.mult)
        self.tt(den[:n, 0:4], den[:n, 0:4], wi2[:n, 0:4], ALU.max)
        self.tt(den[:n, 0:4], den[:n, 0:4], emt[:n, 0:4], ALU.max)
        self.S.op('dve', lambda e: e.reciprocal(out=rden.t[:n, 0:4], in_=den.t[:n, 0:4]), reads=[den], writes=[rden])
        self.tt(wi2[:n, 0:4], wi[:n, 0:4], rden[:n, 0:4], ALU.mult)
        for h in range(4):
            self.act(hh[:n, h, :], ps[3 + h // 2][:n, (h % 2) * 256:(h % 2) * 256 + 256], AF.Copy, scale=rden[:n, h:h + 1])
            if kind != 's':
                iv = ps[5 + h // 2][:n, (h % 2) * 256:(h % 2) * 256 + 256]
            else:
                iv = (R1 if h < 2 else R2)[:n, :, :].rearrange('p a b -> p (a b)')[:, (h % 2) * 256:(h % 2) * 256 + 256]
            self.stt(hh[:n, h, :], iv, wi2[:n, h:h + 1], hh[:n, h, :], ALU.mult, ALU.add)
        st, mv, rs = sm[0], sm[1], sm[2]
        for h in range(4):
            self.S.op('dve', lambda e, h=h: e.bn_stats(out=st.t[:n, h * 6:(h + 1) * 6], in_=hh.t[:n, h, :]), reads=[hh], writes=[st])
        for h in range(4):
            self.S.op('dve', lambda e, h=h: e.bn_aggr(out=mv.t[:n, 2 * h:2 * h + 2], in_=st.t[:n, h * 6:(h + 1) * 6]), reads=[st], writes=[mv])
        mvv = mv[:n, 0:8].rearrange('p (h t) -> p h t', t=2)
        self.act(rs[:n, 0:4], mvv[:, :, 1], AF.Ln, bias=self.cst[:n, 0:1])
        self.act(rs[:n, 0:4], rs[:n, 0:4], AF.Exp, scale=-0.5)
        for h in range(4):
            self.ts(hh[:n, h, :], hh[:n, h, :], mv[:n, 2 * h:2 * h + 1], ALU.subtract, rs[:n, h:h + 1], ALU.mult)
            self.stt(self.hgTM[:n, h * 256:(h + 1) * 256], self.oth[s][:n, h * 256:(h + 1) * 256], 1.0, hh[:n, h, :], ALU.add, ALU.mult)
        for k in range(8):
            self.tr(pb7[:, k * n:(k + 1) * n], self.hgTM[:n, k * 128:(k + 1) * 128], identb)
        self.cp(self.hgT[:, :, cols], pb7[:, 0:8 * n].rearrange('p (k t) -> p k t', k=8), 'dve')
        if kind != 's':
            wl16 = sm[15]
            self.cp(wl16.ap().bitcast(BF16)[:n, 0:4], wT[:n, :, n - 1], 'act')
            for h in range(4):
                self.ts(vw[:n, h, :], self.v[s][:n, h * 256:(h + 1) * 256], wT[:n, h, n - 1:n], ALU.mult)
            for h in range(4):
                self.mm(ps[5 + h // 2][:, (h % 2) * 256:(h % 2) * 256 + 256], kTM[:n, h, :], vw[:n, h, :])
            for h in range(4):
                self.mm(ps[0][:, 16 + h:17 + h], kTM[:n, h, :], wl16.ap().bitcast(BF16)[:n, h:h + 1])
            SEL = self.cfv('SELp' if n == 128 else 'SELm', n)
            self.mm(ps[0][:, 32:36], SEL, wi[:n, 0:4])
            self.mm(ps[0][:, 36:40], SEL, mt[:n, 0:4])
            dec = sm[3]
            self.cp(dec[:, 0:8], ps[0][:, 32:40], 'act')
            for h in range(4):
                self.stt(self.Cf[:, h, :], self.Cf[:, h, :], dec[:, h:h + 1], ps[5 + h // 2][:, (h % 2) * 256:(h % 2) * 256 + 256], ALU.mult, ALU.add)
            self.tt(self.nf.ap(), self.nf.ap(), dec[:, 0:4], ALU.mult)
            self.tt(self.nf.ap(), self.nf.ap(), ps[0][:, 16:20], ALU.add)
            self.cp(self.m_b.ap(), dec[:, 4:8], 'dve')
            self.cp(self.Cb.ap(), self.Cf.ap(), 'act')
            self.cp(self.nb.ap(), self.nf.ap(), 'act')
            if ch.final:
                self.dma(O['p_C'].ap().rearrange('h d e -> d h e'), self.Cf.ap())
                identf128 = self.cfv('ident')
                self.tr(ps[0][:4, 128:256], self.nf.ap(), identf128)
                self.cp(self.pn_st[:4, :], ps[0][:4, 128:256], 'act')
                self.dma(O['p_n'].ap(), self.pn_st[:4, :])
                self.dma(O['p_m'].ap(), self.m_b[0:1, :])

    def mlstm_sample_states(self, tl, ch, wi, wT, kTM, mt):
        I, O, ps, sm = self.I, self.O, self.ps, self.sm
        n, s = 64, ch.slot
        identf = self.cfv('ident')
        RS, BM = self.cfv('RS', 64), self.cfv('BM', 64)
        CM = self.cbv('CM').rearrange('p (b j) -> p b j', b=16)
        vw = self.vw
        wl = sm[3]
        tmp = self.R3
        self.tt(tmp[:n, :, 0:64], wT[:n, :, 0:64], View(self.cf, self.cfv('LSEL', 64).ap.unsqueeze(1).to_broadcast([64, 4, 64])), ALU.mult)
        self.S.op('dve', lambda e: e.tensor_reduce(out=wl.t[:n, 0:4], in_=tmp.t[:n, :, 0:64], axis=AX.X, op=ALU.add), reads=[tmp], writes=[wl])
        wl16 = sm[4].ap().bitcast(BF16)
        self.cp(wl16[:n, 0:4], wl[:n, 0:4], 'act')
        for h in range(4):
            self.ts(vw[:n, h, :], self.v[s][:n, h * 256:(h + 1) * 256], wl[:n, h:h + 1], ALU.mult)
        Rm3 = self.Rm[:n, :].rearrange('p (b h) -> p b h', h=4)
        self.tt(Rm3, View(wi, wi.t[:n, 0:4].unsqueeze(1).to_broadcast([n, 16, 4])),
                View(self.cf, RS.ap.unsqueeze(2).to_broadcast([n, 16, 4])), ALU.mult)
        self.mm(ps[0][:, 64:128], self.cfv('ones', 64), self.Rm[:n, :])
        self.cp(self.decS.ap(), ps[0][:, 64:128], 'act')
        self.mm(ps[0][:16, 40:44], RS, mt[:n, 0:4])
        mo = sm[15]
        self.cp(mo[:16, 8:12], ps[0][:16, 40:44], 'act')
        self.dma(O['o_m'].ap(), mo[:16, 8:12])
        stg = self.pn_st
        self.dma(stg[:64, :], I['s_n'].ap())
        self.tr(ps[0][:, 192:256], stg[:64, :], identf[:64, :64])
        self.cp(self.n0T.ap(), ps[0][:, 192:256], 'act')
        self.cp(self.n16.ap(), self.n0T.ap(), 'pool')
        kTMflat = kTM[:n, :, :].rearrange('p h d -> p (h d)')
        for b in range(16):
            i = b % 2
            C0, C16, qm, km = self.C0b[i], self.C0b16[i], self.qmb[i], self.kTMm[i]
            self.dma(C0.ap(), I['s_C'][b].rearrange('h d e -> d h e'))
            self.cp(C16[:, :, 0:256], C0.ap(), 'act')
            self.cp(C16[:, :, 256:257], self.n16[:, 4 * b:4 * b + 4].rearrange('p (h o) -> p h o', o=1), 'dve')
            self.tt(qm.ap(), self.qkT[:, 0:4, ch.col0:ch.col0 + 64], View(self.cb, CM.ap[:, b, :].unsqueeze(1).to_broadcast([128, 4, 64])), ALU.mult)
            self.ts(km[:n, :, :].rearrange('p h d -> p (h d)'), kTMflat, BM[:, b:b + 1], ALU.mult)
            for h in range(4):
                self.mm(ps[3 + h][:n, 0:257], qm[:, h, :], C16[:, h, 0:257], start=(b == 0), stop=(b == 15))
            for h in range(4):
                self.mm(ps[1 + h // 2][:, (h % 2) * 256:(h % 2) * 256 + 256], km[:n, h, :], vw[:n, h, :])
            for h in range(4):
                self.mm(ps[0][:, 128 + 4 * b + h:129 + 4 * b + h], km[:n, h, :], wl16[:n, h:h + 1])
            for h in range(4):
                self.stt(C0[:, h, :], C0[:, h, :], self.decS[:, 4 * b + h:4 * b + h + 1],
                         ps[1 + h // 2][:, (h % 2) * 256:(h % 2) * 256 + 256], ALU.mult, ALU.add)
            self.dma(O['o_C'][b].rearrange('h d e -> d h e'), C0.ap())
        for h in range(4):
            dst = (self.R1 if h < 2 else self.R2)[:n, :, :].rearrange('p a b -> p (a b)')[:, (h % 2) * 256:(h % 2) * 256 + 256]
            self.cp(dst, ps[3 + h][:n, 0:256], 'act')
            self.cp(sm[12][:n, h:h + 1], ps[3 + h][:n, 256:257], 'act')
        self.tt(self.n0T.ap(), self.n0T.ap(), self.decS.ap(), ALU.mult)
        self.tt(self.n0T.ap(), self.n0T.ap(), ps[0][:, 128:192], ALU.add)
        self.tr(ps[0][:64, 256:384], self.n0T.ap(), identf)
        self.cp(stg[:64, :], ps[0][:64, 256:384], 'act')
        self.dma(O['o_n'].ap(), stg[:64, :])

    def ssd_chunk(self, tl, ch, last):
        I, O, ps, sm = self.I, self.O, self.ps, self.sm
        n, s, c0, kind = ch.n, ch.slot, ch.col0, ch.kind
        kc = self.kc(kind, n)
        U, LS, BO, MT = kc['U'], kc['LS'], kc['BO'], kc['MT']
        identb = self.cbv('identb')
        cols = slice(c0, c0 + n)
        dt = self.dta[:n, s, 0:32]
        a = self.dta[:n, s, 32:64]
        btsb, dect, wend, decS = sm[5], sm[6], sm[7], sm[8]
        self.mm(ps[0][:n, 0:32], U, a)
        self.mm(ps[0][:n, 32:64], BO, a)
        self.cp(btsb[:n, 0:32], ps[0][:n, 0:32], 'act')
        self.act(dect[:n, 0:32], ps[0][:n, 0:32], AF.Exp)
        self.tt(wend[:n, 0:32], ps[0][:n, 32:64], btsb[:n, 0:32], ALU.subtract)
        self.act(wend[:n, 0:32], wend[:n, 0:32], AF.Exp)
        if kind != 's':
            self.mm(ps[0][:, 64:96], self.cfv('ones', n), a)
            self.act(decS[:, 0:32], ps[0][:, 64:96], AF.Exp)
        S = self.S
        pb6, pb7 = self.psb(6), self.psb(7)
        xsTM = self.xdt
        for g in range(16):
            pv = pb6 if g < 8 else pb7
            self.tr(pv[:n, (g % 8) * 128:(g % 8 + 1) * 128], self.xbcT[:, g, cols], identb)
        self.act(xsTM[:n, 0:1024], pb6[:n, 0:1024])
        self.act(xsTM[:n, 1024:2048], pb7[:n, 0:1024])
        S.alias_phase([self.ynTM], [self.xsDT])
        for half in range(2):
            for g in range(8):
                self.ts(self.xsDT[:, g, 0:n], self.xbcT[:, 8 * half + g, cols], self.Dfm[:, 8 * half + g:8 * half + g + 1], ALU.mult)
            pv = pb6 if half == 0 else pb7
            for g in range(8):
                self.tr(pv[:n, g * 128:(g + 1) * 128], self.xsDT[:, g, 0:n], identb)
            self.cp(self.xsD[:n, 1024 * half:1024 * half + 1024], pv[:n, 0:1024], 'dve')
        for g in range(4):
            self.tr(pb6[:n, g * 128:(g + 1) * 128], self.xbcT[:, 16 + g, cols], identb)
        self.act(self.BTM[:n, :], pb6[:n, 0:512])
        dtw = sm[13]
        self.tt(dtw[:n, 0:32], dt, wend[:n, 0:32], ALU.mult)
        S.alias_phase([self.xsDT], [self.ynTM])
        if kind == 's':
            self.ssd_sample_states(tl, ch, dtw)
        S.alias_phase([self.ynTM], self.MTh[1])
        ssq = sm[9]
        self.memset(ssq[:n, 0:4], 0.0)
        def stageA(g):
            MTb = self.MTh[g % 2]
            for j in range(8):
                self.act(self.LAh[j][:n, :n], LS, AF.Copy, scale=self.dta[:n, s, 32 + 8 * g + j:33 + 8 * g + j])
            for j in range(8):
                o = ps[1 + j // 4][:n, (j % 4) * 128:(j % 4) * 128 + n]
                self.mm(o, self.LAh[j][:n, :n], U, start=True, stop=False)
                self.mm(o, identb[:n, :n], MT, start=False, stop=True)
            for half in range(2):
                self.act(self.LTh[half][:n, :, :n],
                         ps[1 + half][:n, :].rearrange('p (h t) -> p h t', h=4)[:, :, 0:n], AF.Exp)
            self.mm(ps[3][:n, 0:n], self.xbcT[:, 16 + g, cols], self.xbcT[:, 20 + g, cols])
            for j in range(8):
                self.stt(MTb[j][:n, :n], self.LTh[j // 4][:n, j % 4, :n], self.dta[:n, s, 8 * g + j:8 * g + j + 1], ps[3][:n, 0:n], ALU.mult, ALU.mult)

        def stageB(g):
            MTb = self.MTh[g % 2]
            for j in range(8):
                h = 8 * g + j
                self.mm(ps[4][:n, j * 64:(j + 1) * 64], MTb[j][:n, :n], xsTM[:n, h * 64:(h + 1) * 64])
            t1 = self.t1
            if kind != 's':
                self.mm(ps[5][:n, 0:512], self.xbcT[:, 20 + g, cols], self.STb[:, 512 * g:512 * g + 512])
                for j in range(4):
                    self.act(t1[:n, j * 64:(j + 1) * 64], ps[5][:n, j * 64:(j + 1) * 64], AF.Copy, scale=dect[:n, 8 * g + j:8 * g + j + 1])
                self.tt(t1[:n, 256:512].rearrange('p (h d) -> p d h', d=64), ps[5][:n, 256:512].rearrange('p (h d) -> p d h', d=64),
                        View(dect, dect.t[:n, 8 * g + 4:8 * g + 8].unsqueeze(1).to_broadcast([n, 64, 4])), ALU.mult)
            else:
                self.cp(t1[:n, :], self.ysi[:n, 512 * g:512 * g + 512], 'dve')
            self.tt(t1[:n, :], t1[:n, :], ps[4][:n, 0:512], ALU.add)
            self.tt(t1[:n, :], t1[:n, :], self.xsD[:n, 512 * g:512 * g + 512], ALU.add)
            self.tt(self.yz[:n, 512 * g:512 * g + 512], t1[:n, :], self.zs[s][:n, 512 * g:512 * g + 512], ALU.mult)

        stageA(0)
        for g in range(4):
            if g + 1 < 4:
                stageA(g + 1)
            stageB(g)
        S.alias_phase(self.MTh[1], [self.ynTM])
        for g in range(4):
            self.act(self.ynTM[:n, 512 * g:512 * g + 512], self.yz[:n, 512 * g:512 * g + 512], AF.Square, accum=ssq[:n, g:g + 1])
        rs = sm[10]
        self.act(rs[:n, 0:4], ssq[:n, 0:4], AF.Ln, scale=0.25 / 512.0, bias=self.cst[:n, 0:1])
        self.act(rs[:n, 0:4], rs[:n, 0:4], AF.Exp, scale=-0.5)
        self.ts(rs[:n, 0:4], rs[:n, 0:4], 0.5, ALU.mult)
        for g in range(4):
            self.ts(self.ynTM[:n, 512 * g:512 * g + 512], self.yz[:n, 512 * g:512 * g + 512], rs[:n, g:g + 1], ALU.mult)
        for k in range(16):
            pv = pb6 if k < 8 else pb7
            self.tr(pv[:, (k % 8) * n:(k % 8 + 1) * n], self.ynTM[:n, k * 128:(k + 1) * 128], identb[:n, :n])
        for half in range(2):
            pv = (pb6 if half == 0 else pb7)[:, 0:8 * n].rearrange('p (k t) -> p k t', k=8)
            self.cp(self.ygT[:, 8 * half:8 * half + 8, cols], pv, 'dve' if half == 0 else 'act')
        if kind != 's':
            S.alias_phase([self.ynTM], self.MTh[1])
            for g in range(4):
                hs = slice(8 * g, 8 * g + 8)
                bank = ps[3 + 2 * (g % 2)]
                Bs = self.MTh[g % 2]
                for j in range(8):
                    h = 8 * g + j
                    self.ts(Bs[j][:n, :], self.BTM[:n, g * 128:(g + 1) * 128], dtw[:n, h:h + 1], ALU.mult)
                for j in range(8):
                    h = 8 * g + j
                    self.mm(bank[:, j * 64:(j + 1) * 64], Bs[j][:n, :], xsTM[:n, h * 64:(h + 1) * 64])
                if g % 2 == 0:
                    for j in range(8):
                        c0_ = 512 * g + 64 * j
                        self.act(self.STf[:, c0_:c0_ + 64], self.STf[:, c0_:c0_ + 64], AF.Copy, scale=decS[:, 8 * g + j:8 * g + j + 1])
                else:
                    sv = self.STf[:, 512 * g:512 * g + 512].rearrange('p (h d) -> p d h', d=64)
                    self.tt(sv, sv, View(decS, decS.t[:, hs].unsqueeze(1).to_broadcast([128, 64, 8])), ALU.mult)
                self.tt(self.STf[:, 512 * g:512 * g + 512], self.STf[:, 512 * g:512 * g + 512], bank[:, 0:512], ALU.add)
            S.alias_phase(self.MTh[1], [self.ynTM])
            self.cp(self.STb.ap(), self.STf.ap(), 'act')
            if ch.final:
                identf = self.cfv('ident')
                for j in range(16):
                    bank = ps[1 + (j // 4) % 2]
                    self.tr(bank[:, (j % 4) * 128:(j % 4 + 1) * 128], self.STf[:, j * 128:(j + 1) * 128], identf)
                    if j % 4 == 3:
                        stg = self.LA[:, 4 * ((j // 4) % 2):4 * ((j // 4) % 2) + 4, :]
                        self.S.alias_phase(self.LAh, [self.LA])
                        self.cp(stg, bank[:, 0:512].rearrange('p (j n) -> p j n', j=4), 'act')
                        q = j // 4
                        self.dma(O['p_ssm'][512 * q:512 * q + 512, :].rearrange('(j p) n -> p j n', p=128), stg)
                self.S.alias_phase([self.LA], self.LAh)

    def ssd_sample_states(self, tl, ch, wend):
        I, O, ps, sm, S = self.I, self.O, self.ps, self.sm, self.S
        n, s = 64, ch.slot
        identf = self.cfv('ident')
        RS, BM = self.cfv('RS', 64), self.cfv('BM', 64)
        CM = self.cbv('CM').rearrange('p (b j) -> p b j', b=16)
        dect = sm[6]
        S.alias_phase(self.grpArena, self.grpArena2)
        S.alias_phase(self.LTh + self.MTh[0] + [self.t1, self.yz], [self.S0b[1]])
        blsb = sm[11]
        self.cp(blsb[:n, 0:32], ps[0][:n, 32:64], 'act')
        bl3 = blsb[:n, 0:32].rearrange('p (j r) -> p j r', r=2)
        for r in range(2):
            self.tt(self.Rr[:n, r, :].rearrange('p (b j) -> p b j', b=16),
                    View(blsb, bl3.ap[:, :, r].unsqueeze(1).to_broadcast([n, 16, 16])),
                    View(self.cf, RS.ap.unsqueeze(2).to_broadcast([n, 16, 16])), ALU.mult, eng='pool')
        self.mm(ps[0][:, 256:512], self.cfv('H0', 64), self.Rr[:n, 0, :], start=True, stop=False)
        self.mm(ps[0][:, 256:512], self.cfv('H1', 64), self.Rr[:n, 1, :], start=False, stop=True)
        self.act(self.decP.ap(), ps[0][:, 256:512], AF.Exp)
        wxA = self.ynTM
        self.tt(wxA[:n, :].rearrange('p (h d) -> p d h', d=64), self.xdt[:n, :].rearrange('p (h d) -> p d h', d=64),
                View(wend, wend.t[:n, 0:32].unsqueeze(1).to_broadcast([n, 64, 32])), ALU.mult)
        for b in range(16):
            Sb = self.S0b[b % 2]
            cm = self.CTmb[b % 2]
            self.dma(Sb.ap(), I['s_ssm'][b].rearrange('(j p) n -> p j n', p=128))
            self.tt(cm.ap(), self.xbcT[:, 20:24, ch.col0:ch.col0 + 64], View(self.cb, CM.ap[:, b, :].unsqueeze(1).to_broadcast([128, 4, 64])), ALU.mult)
            self.ts(self.wxm[:n, :], wxA[:n, :], BM[:, b:b + 1], ALU.mult)
            for q in range(4):
                bank = ps[5 + q % 2]
                for i in range(4):
                    self.tr(bank[:, i * 128:(i + 1) * 128], Sb[:, 4 * q + i, :], identf)
                self.cp(self.SbT[:, 512 * q:512 * q + 512], bank[:, 0:512], 'act')
            for g in range(4):
                self.mm(ps[1 + g][:n, 0:512], cm[:, g, :], self.SbT[:, 512 * g:512 * g + 512], start=(b == 0), stop=(b == 15))
            for q in range(4):
                bank = ps[7] if q % 2 == 0 else ps[0]
                for i in range(4):
                    j = 4 * q + i
                    self.mm(bank[:, i * 128:(i + 1) * 128], self.wxm[:n, j * 128:(j + 1) * 128], self.BTM[:n, q * 128:(q + 1) * 128])
                for i in range(4):
                    j = 4 * q + i
                    self.stt(Sb[:, j, :], Sb[:, j, :], self.decP[:, 16 * b + j:16 * b + j + 1], bank[:, i * 128:(i + 1) * 128], ALU.mult, ALU.add)
            self.dma(O['o_ssm'][b].rearrange('(j p) n -> p j n', p=128), Sb.ap())
        for g in range(4):
            self.tt(self.ysi[:n, 512 * g:512 * g + 512].rearrange('p (h d) -> p d h', d=64),
                    ps[1 + g][:n, 0:512].rearrange('p (h d) -> p d h', d=64),
                    View(dect, dect.t[:n, 8 * g:8 * g + 8].unsqueeze(1).to_broadcast([n, 64, 8])), ALU.mult)
        S.alias_phase([self.S0b[1]], self.LTh + self.MTh[0] + [self.t1, self.yz])


_CACHE = {}


def _get_kernel():
    if 'k' not in _CACHE:
        _CACHE['k'] = K()
    return _CACHE['k']


def make_in_maps(kb, inputs):
    f = lambda a: np.ascontiguousarray(np.asarray(a, dtype=np.float32))
    xp, xs = f(inputs['x_prompt']), f(inputs['x_sample'])
    shared = {'meta': f(inputs['meta_tokens']), 'cf': kb.cf_np, 'cb': kb.cb_np,
              'ln0_g': f(inputs['ln0_g']), 'ln0_b': f(inputs['ln0_b']),
              'b_if': f(inputs['b_mlstm_if'])[0], 'w_mconv': f(inputs['w_mlstm_conv'])[0],
              'b_mconv': f(inputs['b_mlstm_conv']), 'mnorm_g': f(inputs['mlstm_norm_g']),
              'w_sconv': f(inputs['w_ssm_conv'])[0], 'b_sconv': f(inputs['b_ssm_conv']),
              'dt_bias': f(inputs['ssm_dt_bias'])[0], 'A_log': f(inputs['ssm_A_log'])[0],
              'ssm_D': f(inputs['ssm_D'])[0], 'snorm_g': f(inputs['ssm_norm_g']),
              'ln1_g': f(inputs['ln1_g'])[0], 'ln1_b': f(inputs['ln1_b'])[0],
              'w_fconv': f(inputs['w_ffn_conv'])[0], 'b_fconv': f(inputs['b_ffn_conv']),
              'ln2_g': f(inputs['ln2_g'])[0], 'ln2_b': f(inputs['ln2_b'])[0],
              'w_in': f(inputs['w_in'])[0], 'w_proj_a': f(inputs['w_proj_a'])[0],
              'w_proj_b': f(inputs['w_proj_b'])[0], 'w_out': f(inputs['w_out'])[0],
              'w_up': f(inputs['w_up'])[0], 'w_down': f(inputs['w_down'])[0]}
    maps = []
    for c in range(8):
        b = slice(16 * c, 16 * c + 16)
        m = dict(shared)
        m['xp'] = xp[c]
        m['xs'] = xs[b].reshape(64, D)
        m['s_mconv'] = f(inputs['state_mlstm_conv'])[0, b].reshape(48, 1024)
        m['s_C'] = f(inputs['state_mlstm_C'])[0, b]
        m['s_n'] = f(inputs['state_mlstm_n'])[0, b].reshape(64, 128)
        m['s_m'] = f(inputs['state_mlstm_m'])[0, b]
        m['s_sconv'] = f(inputs['state_ssm_conv'])[0, b].reshape(48, 3072)
        m['s_ssm'] = f(inputs['state_ssm'])[0, b].reshape(16, 2048, 128)
        m['s_fconv'] = f(inputs['state_ffn_conv'])[0, b].reshape(32, 2 * DFF)
        maps.append(m)
    return maps


def kernel(**inputs):
    kb = _get_kernel()
    maps = make_in_maps(kb, inputs)
    res = run_bass_kernel_spmd(kb.nc, maps, core_ids=list(range(8)))
    R = res.results
    cat = lambda k: np.stack([np.asarray(r[k], dtype=np.float32) for r in R])
    y_p = cat('y_p')
    y_s = cat('y_s').reshape(128, 4, D)
    p_mconv = cat('p_mconv')[None]
    p_C = cat('p_C')[None]
    p_n = cat('p_n')[None]
    p_m = cat('p_m').reshape(8, 4)[None]
    p_sconv = cat('p_sconv')[None]
    p_ssm = cat('p_ssm').reshape(8, 32, 64, 128)[None]
    p_fconv = cat('p_fconv')[None]
    s_mconv = cat('o_mconv').reshape(128, 3, 1024)[None]
    s_C = cat('o_C').reshape(128, 4, 128, 256)[None]
    s_n = cat('o_n').reshape(128, 4, 128)[None]
    s_m = cat('o_m').reshape(128, 4)[None]
    s_sconv = cat('o_sconv').reshape(128, 3, 3072)[None]
    s_ssm = cat('o_ssm').reshape(128, 32, 64, 128)[None]
    s_fconv = cat('o_fconv').reshape(128, 2, 2 * DFF)[None]
    return (y_p, y_s, p_mconv, p_C, p_n, p_m, p_sconv, p_ssm, p_fconv,
            s_mconv, s_C, s_n, s_m, s_sconv, s_ssm, s_fconv)
```

```python
import numpy as np
import ml_dtypes
import concourse.bass as bass
import concourse.mybir as mybir
from concourse.bass_utils import run_bass_kernel_spmd

F32 = mybir.dt.float32
BF16 = mybir.dt.bfloat16
ALU = mybir.AluOpType
AF = mybir.ActivationFunctionType
AX = mybir.AxisListType

D = 1024
DIN = 10280
DFF = 2816
NEG = -30000.0
ALPHA = 2.0 ** 0.25
LN_EPS = 1e-5
RMS_EPS = 1e-5
QSCALE = 128.0 ** -0.5


class Buf:
    def __init__(self, name, t, space):
        self.name = name
        self.t = t
        self.space = space
        self.last_w = None
        self.readers = []
        self.sem_in = None
        self.cnt_in = 0
        self.sem_out = None
        self.cnt_out = 0

    def __getitem__(self, idx):
        return View(self, self.t[idx])

    def ap(self):
        return View(self, self.t[:] if self.space != 'dram' else self.t)


class View:
    def __init__(self, buf, ap):
        self.buf = buf
        self.ap = ap

    def __getitem__(self, idx):
        return View(self.buf, self.ap[idx])

    def rearrange(self, *a, **k):
        return View(self.buf, self.ap.rearrange(*a, **k))

    def bc(self, axis, shape):
        return View(self.buf, self.ap.unsqueeze(axis).to_broadcast(list(shape)))

    def bitcast(self, dt):
        return View(self.buf, self.ap.bitcast(dt))


def _bufs(vs):
    out = []
    for v in vs:
        if v is None or isinstance(v, (int, float)):
            continue
        b = v.buf if isinstance(v, View) else v
        if b not in out:
            out.append(b)
    return out


class Sched:
    ENGS = ('pe', 'act', 'dve', 'pool', 'sp')

    def __init__(self, nc):
        self.nc = nc
        self.sem = {e: nc.alloc_semaphore('sem_' + e) for e in self.ENGS}
        self.cnt = {e: 0 for e in self.ENGS}
        self.ops = {e: [] for e in self.ENGS}
        self.seen = {e: {} for e in self.ENGS}
        self.final_tokens = []
        self.sb_off = 16512
        self.sb_end = 229376
        self.nsem = 5

    def sbuf(self, name, shape, dtype, at=None):
        nbytes = int(np.prod(shape[1:])) * (2 if dtype == BF16 else 4)
        nbytes = (nbytes + 31) // 32 * 32
        if at is None:
            at = self.sb_off
            self.sb_off += nbytes
            assert self.sb_off <= self.sb_end, ('SBUF overflow', name, self.sb_off)
        t = self.nc.alloc_sbuf_tensor_at(name, list(shape), dtype, offset=at)
        b = Buf(name, t, 'sbuf')
        b.off = at
        b.nbytes = nbytes
        return b

    def psum(self, name, shape, dtype=F32):
        t = self.nc.alloc_psum_tensor(name, list(shape), dtype)
        return Buf(name, t, 'psum')

    def dram(self, name, shape, dtype, kind):
        t = self.nc.dram_tensor(name, list(shape), dtype, kind=kind)
        return Buf(name, t.ap(), 'dram')

    def alias_phase(self, old, new):
        toks = []
        for b in old:
            if b.last_w is not None:
                toks.append(b.last_w)
            toks.extend(b.readers)
        for b in new:
            b.readers = list(b.readers) + toks

    def _need(self, eng, waits, tok):
        sem, val, teng = tok
        key = id(sem)
        if self.seen[eng].get(key, 0) >= val:
            return
        if key not in waits or waits[key][1] < val:
            waits[key] = (sem, val)

    def _deps(self, eng, reads, writes):
        waits = {}
        for b in reads:
            tok = b.last_w
            if tok is not None and not (tok[2] == eng and eng == 'pe'):
                self._need(eng, waits, tok)
            if b.space == 'psum':
                for r in b.readers:
                    if r[2] != eng:
                        self._need(eng, waits, r)
        for b in writes:
            tok = b.last_w
            if tok is not None and not (tok[2] == eng and eng == 'pe'):
                self._need(eng, waits, tok)
            for r in b.readers:
                if not (r[2] == eng and eng == 'pe'):
                    self._need(eng, waits, r)
        for key, (sem, val) in waits.items():
            self.seen[eng][key] = val
        return list(waits.values())

    def op(self, eng, fn, reads=(), writes=()):
        reads = _bufs(reads)
        writes = _bufs(writes)
        waits = self._deps(eng, reads, writes)
        self.cnt[eng] += 1
        tok = (self.sem[eng], self.cnt[eng], eng)
        self.ops[eng].append((waits, fn, (self.sem[eng], 1)))
        for b in writes:
            b.last_w = tok
            b.readers = []
        for b in reads:
            if b not in writes:
                b.readers.append(tok)
        return tok

    def dma(self, q, out, in_, **kw):
        ob, ib = out.buf, in_.buf
        waits = self._deps(q, [ib], [ob])
        if ob.space != 'dram':
            if ob.sem_in is None:
                ob.sem_in = self.nc.alloc_semaphore('din_' + ob.name)
                self.nsem += 1
            ob.cnt_in += 16
            sem, val = ob.sem_in, ob.cnt_in
        else:
            if ib.sem_out is None:
                ib.sem_out = self.nc.alloc_semaphore('dout_' + ib.name)
                self.nsem += 1
            ib.cnt_out += 16
            sem, val = ib.sem_out, ib.cnt_out
        tok = (sem, val, 'dma')
        oap, iap = out.ap, in_.ap

        def fn(e, oap=oap, iap=iap, kw=kw):
            return e.dma_start(out=oap, in_=iap, **kw)
        self.ops[q].append((waits, fn, (sem, 16)))
        ob.last_w = tok
        ob.readers = []
        ib.readers.append(tok)
        if ob.space == 'dram':
            self.final_tokens.append(tok)
        return tok

    def emit(self):
        nc = self.nc
        last = {}
        for sem, val, _ in self.final_tokens:
            k = id(sem)
            if k not in last or last[k][1] < val:
                last[k] = (sem, val)
        fin = list(last.values())
        eng_obj = {'pe': 'tensor', 'act': 'scalar', 'dve': 'vector', 'pool': 'gpsimd', 'sp': 'sync'}
        with nc.Block() as block:
            def mk(eng):
                def body(e):
                    for waits, fn, inc in self.ops[eng]:
                        for sem, val in waits:
                            e.wait_ge(sem, val)
                        fn(e).then_inc(inc[0], inc[1])
                    if eng == 'sp':
                        for sem, val in fin:
                            e.wait_ge(sem, val)
                return body
            for eng, attr in eng_obj.items():
                getattr(block, attr)(mk(eng))


def _const_tables():
    p = np.arange(128)[:, None]
    j = np.arange(128)[None, :]
    f = {}
    f['ident'] = (p == j)
    f['ones'] = np.ones((128, 128))
    f['U'] = (p <= j)
    f['LS'] = (p > j)
    sb = (p // 4 == j // 4) & (p < 64) & (j < 64)
    f['Us'] = ((p <= j) & sb)[:, :64]
    f['LSs'] = ((p > j) & sb)[:, :64]
    f['BOs'] = sb[:, :64]
    f['SELp'] = np.repeat(p == 127, 128, axis=1)
    f['SELm'] = np.repeat(p == 15, 128, axis=1)
    b16 = np.arange(16)[None, :]
    f['RS'] = (p == 4 * b16 + 3)
    f['BM'] = (p // 4 == b16) & (p < 64)
    f['BMT'] = ((p < 16) & (j // 4 == p))[:, :64]
    f['LSEL'] = ((j == 4 * (p // 4) + 3) & (p < 64))[:, :64]
    f['H0'] = np.repeat(p < 64, 128, axis=1) & (j < 64)
    f['H1'] = np.repeat(p < 64, 128, axis=1) & (j >= 64)
    cf_off, cols = {}, []
    o = 0
    for k, v in f.items():
        cf_off[k] = (o, v.shape[1])
        o += v.shape[1]
        cols.append(v.astype(np.float32))
    cf = np.concatenate(cols, axis=1)
    g = {}
    g['identb'] = (p == j).astype(np.float32)
    g['onesb'] = np.ones((128, 128), np.float32)
    g['M'] = np.where(j <= p, 0.0, NEG)
    g['MT'] = np.where(p <= j, 0.0, NEG)
    g['Ms'] = np.where((j <= p) & sb, 0.0, NEG)[:, :64]
    g['MTs'] = np.where((p <= j) & sb, 0.0, NEG)[:, :64]
    jj = np.arange(64)[None, None, :]
    bb = np.arange(16)[None, :, None]
    g['CM'] = np.broadcast_to((jj // 4 == bb), (128, 16, 64)).reshape(128, 1024).astype(np.float32)
    cb_off, cols = {}, []
    o = 0
    for k, v in g.items():
        cb_off[k] = (o, v.shape[1])
        o += v.shape[1]
        cols.append(np.asarray(v, np.float32))
    cbm = np.concatenate(cols, axis=1).astype(ml_dtypes.bfloat16)
    return cf, cf_off, cbm, cb_off


class Chunk:
    def __init__(self, slot, col0, n, kind, row0=0):
        self.slot, self.col0, self.n, self.kind, self.row0 = slot, col0, n, kind, row0


class Tile:
    def __init__(self, name, T, chunks, segs):
        self.name, self.T, self.chunks, self.segs = name, T, chunks, segs


W_SHAPES = {'w_in': (D, DIN), 'w_proj_a': (D, D), 'w_proj_b': (2 * D, D), 'w_out': (D, D),
            'w_up': (D, 2 * DFF), 'w_down': (DFF, D)}


def tile_blocks():
    bl = []
    for c in range(0, 3072, 256):
        bl.append(('w_in', 0, 8, [(c, 256)]))
    bl.append(('w_in', 0, 8, [(3072, 8), (8200, 32)]))
    for c in range(3080, 5128, 256):
        bl.append(('w_in', 0, 8, [(c, 256)]))
    for c in range(5128, 8200, 256):
        bl.append(('w_in', 0, 8, [(c, 256)]))
    for c in range(8232, 10280, 256):
        bl.append(('w_in', 0, 8, [(c, 256)]))
    for j in range(4):
        bl.append(('w_proj_a', 0, 8, [(256 * j, 256)]))
        bl.append(('w_proj_b', 0, 8, [(256 * j, 256)]))
        bl.append(('w_proj_b', 8, 8, [(256 * j, 256)]))
    for j in range(4):
        bl.append(('w_out', 0, 8, [(256 * j, 256)]))
    for j in range(11):
        bl.append(('w_up', 0, 8, [(256 * j, 256)]))
        bl.append(('w_up', 0, 8, [(DFF + 256 * j, 256)]))
    for j in range(4):
        for k0, nk in ((0, 8), (8, 8), (16, 6)):
            bl.append(('w_down', k0, nk, [(256 * j, 256)]))
    return bl


class K:
    def __init__(self, debug=None, tiles=('T0', 'T1', 'T2', 'T3', 'T4')):
        self.debug = debug or {}
        self.tile_sel = tuple(tiles)
        self.ntiles = len(self.tile_sel)
        nc = bass.Bass('TRN2', target_bir_lowering=False)
        self.nc = nc
        self.S = S = Sched(nc)
        self.dumps = {}
        cf, self.cfo, cbm, self.cbo = _const_tables()
        self.cf_np, self.cb_np = cf, cbm
        din = lambda n, s, dt=F32: S.dram(n, s, dt, 'ExternalInput')
        dout = lambda n, s: S.dram(n, s, F32, 'ExternalOutput')
        I = self.I = {}
        I['xp'] = din('xp', [2048, D]); I['xs'] = din('xs', [64, D]); I['meta'] = din('meta', [16, D])
        I['s_mconv'] = din('s_mconv', [48, 1024]); I['s_C'] = din('s_C', [16, 4, 128, 256])
        I['s_n'] = din('s_n', [64, 128]); I['s_m'] = din('s_m', [16, 4])
        I['s_sconv'] = din('s_sconv', [48, 3072]); I['s_ssm'] = din('s_ssm', [16, 2048, 128])
        I['s_fconv'] = din('s_fconv', [32, 2 * DFF])
        I['cf'] = din('cf', list(cf.shape)); I['cb'] = din('cb', list(cbm.shape), BF16)
        for n, s in (('ln0_g', [D]), ('ln0_b', [D]), ('b_if', [8]), ('w_mconv', [4, 1024]), ('b_mconv', [1, 1024]),
                     ('mnorm_g', [1, 1024]), ('w_sconv', [4, 3072]), ('b_sconv', [1, 3072]), ('dt_bias', [32]),
                     ('A_log', [32]), ('ssm_D', [32]), ('snorm_g', [1, 2048]), ('ln1_g', [D]), ('ln1_b', [D]),
                     ('w_fconv', [3, 2 * DFF]), ('b_fconv', [1, 2 * DFF]), ('ln2_g', [D]), ('ln2_b', [D])):
            I[n] = din(n, s)
        for n, s in W_SHAPES.items():
            I[n] = din(n, list(s))
        O = self.O = {}
        O['y_p'] = dout('y_p', [2048, D]); O['y_s'] = dout('y_s', [64, D])
        O['p_mconv'] = dout('p_mconv', [3, 1024]); O['p_C'] = dout('p_C', [4, 128, 256])
        O['p_n'] = dout('p_n', [4, 128]); O['p_m'] = dout('p_m', [1, 4])
        O['p_sconv'] = dout('p_sconv', [3, 3072]); O['p_ssm'] = dout('p_ssm', [2048, 128])
        O['p_fconv'] = dout('p_fconv', [2, 2 * DFF])
        O['o_mconv'] = dout('o_mconv', [48, 1024]); O['o_C'] = dout('o_C', [16, 4, 128, 256])
        O['o_n'] = dout('o_n', [64, 128]); O['o_m'] = dout('o_m', [16, 4])
        O['o_sconv'] = dout('o_sconv', [48, 3072]); O['o_ssm'] = dout('o_ssm', [16, 2048, 128])
        O['o_fconv'] = dout('o_fconv', [32, 2 * DFF])
        self.rr = {}
        self.build()
        S.emit()

    def rot(self, key, n):
        i = self.rr.get(key, 0)
        self.rr[key] = i + 1
        return i % n

    def mm(self, out, lhsT, rhs, start=True, stop=True):
        self.S.op('pe', lambda e: e.matmul(out.ap, lhsT=lhsT.ap, rhs=rhs.ap, start=start, stop=stop),
                  reads=[lhsT, rhs], writes=[out])

    def tr(self, out, in_, ident):
        self.S.op('pe', lambda e: e.transpose(out=out.ap, in_=in_.ap, identity=ident.ap),
                  reads=[in_, ident], writes=[out])

    def act(self, out, in_, func=AF.Copy, bias=None, scale=None, accum=None):
        kw = {}
        if bias is not None:
            kw['bias'] = bias.ap if isinstance(bias, View) else bias
        if scale is not None:
            kw['scale'] = scale.ap if isinstance(scale, View) else scale
        if accum is not None:
            kw['accum_out'] = accum.ap
        self.S.op('act', lambda e: e.activation(out=out.ap, in_=in_.ap, func=func, **kw),
                  reads=[in_, bias, scale], writes=[out, accum])

    def tt(self, out, a, b, op, eng='dve'):
        self.S.op(eng, lambda e: e.tensor_tensor(out=out.ap, in0=a.ap, in1=b.ap, op=op),
                  reads=[a, b], writes=[out])

    def ts(self, out, a, s1, op0, s2=None, op1=None, eng='dve', accum=None):
        v1 = s1.ap if isinstance(s1, View) else s1
        v2 = s2.ap if isinstance(s2, View) else s2
        kw = {}
        if op1 is not None:
            kw['op1'] = op1
        if accum is not None:
            kw['accum_out'] = accum.ap
        self.S.op(eng, lambda e: e.tensor_scalar(out=out.ap, in0=a.ap, scalar1=v1, scalar2=v2, op0=op0, **kw),
                  reads=[a, s1, s2], writes=[out, accum])

    def stt(self, out, a, s, b, op0, op1, eng='dve'):
        v = s.ap if isinstance(s, View) else s
        self.S.op(eng, lambda e: e.scalar_tensor_tensor(out=out.ap, in0=a.ap, scalar=v, in1=b.ap, op0=op0, op1=op1),
                  reads=[a, s, b], writes=[out])

    def cp(self, out, in_, eng='dve'):
        if eng == 'act':
            return self.act(out, in_)
        self.S.op(eng, lambda e: e.tensor_copy(out=out.ap, in_=in_.ap), reads=[in_], writes=[out])

    def memset(self, out, val, eng='dve'):
        self.S.op(eng, lambda e: e.memset(out.ap, val), writes=[out])

    def rmax(self, out, in_, eng='dve'):
        self.S.op(eng, lambda e: e.tensor_reduce(out=out.ap, in_=in_.ap, axis=AX.X, op=ALU.max),
                  reads=[in_], writes=[out])

    def dma(self, out, in_, q='sp'):
        if out.buf.space == 'dram' and q == 'sp' and not getattr(self, 'pass0', False):
            q = 'pool'
        self.S.dma(q, out, in_)

    def dump(self, name, view, shape):
        if name not in self.debug:
            return
        d = self.S.dram('dbg_' + name, list(shape), view.ap.dtype, 'ExternalOutput')
        self.dumps[name] = d
        self.dma(d.ap(), view)

    def cfv(self, name, rows=128, cols=None):
        o, w = self.cfo[name]
        cols = w if cols is None else cols
        return self.cf[:rows, o:o + cols]

    def cbv(self, name, rows=128, cols=None):
        o, w = self.cbo[name]
        cols = w if cols is None else cols
        return self.cb[:rows, o:o + cols]

    def ws_init(self):
        S = self.S
        self.wlist = tile_blocks()
        self.nbt = len(self.wlist)
        self.wblocks = self.wlist * self.ntiles
        self.wring = [S.sbuf(f'wring{i}', [128, 8, 256], BF16) for i in range(6)]
        self.wscr = [S.dram(f'wscr{j}', [128, 8, 256], BF16, 'Internal') for j in range(self.nbt)] if self.ntiles > 1 else None
        self.w_loaded = 0
        self.w_next = 0

    def _w_load0(self, i):
        name, k0, nk, parts = self.wblocks[i]
        dst = self.wring[i % 6]
        W = self.I[name]
        c = 0
        for (c0, n) in parts:
            src = View(W, W.t[k0 * 128:(k0 + nk) * 128, c0:c0 + n].rearrange('(k p) c -> p k c', p=128))
            self.S.dma('pool', dst[:, 0:nk, c:c + n], src)
            c += n
        n = c
        if name in ('w_proj_a', 'w_proj_b'):
            for k in range(nk):
                gcol = self.cwm[:, k0 + k, 5:6] if name == 'w_proj_a' else self.sng[:, k0 + k:k0 + k + 1]
                self.ts(dst[:, k, 0:n], dst[:, k, 0:n], gcol, ALU.mult)
        if self.wscr is not None:
            self.S.dma('sp', self.wscr[i][:, 0:nk, 0:n], dst[:, 0:nk, 0:n])

    def _w_ringload(self, i):
        name, k0, nk, parts = self.wblocks[i]
        n = sum(p[1] for p in parts)
        dst = self.wring[i % 6]
        self.dma(dst[:, 0:nk, 0:n], self.wscr[i % self.nbt][:, 0:nk, 0:n], q='sp')

    def wnext(self):
        i = self.w_next
        nb = len(self.wblocks)
        self.w_next += 1
        while self.w_loaded < min(nb, i + 6):
            if self.w_loaded < self.nbt:
                self._w_load0(self.w_loaded)
            else:
                self._w_ringload(self.w_loaded)
            self.w_loaded += 1
        return self.wring[i % 6], self.wblocks[i]

    def build(self):
        S = self.S
        sb = S.sbuf
        ncf, ncb = self.cf_np.shape[1], self.cb_np.shape[1]
        self.cf = sb('cf', [128, ncf], F32)
        self.cb = sb('cb', [128, ncb], BF16)
        self.lnc = sb('lnc', [128, 2, D], F32)
        self.bif_b = sb('bif_b', [128, 8], F32)
        self.dtb_b = sb('dtb_b', [128, 32], F32)
        self.A_b = sb('A_b', [128, 32], F32)
        self.D_b = sb('D_b', [128, 32], F32)
        self.Dfm = sb('Dfm', [128, 16], F32)
        self.cwm = sb('cwm', [128, 8, 6], F32)
        self.cws = sb('cws', [128, 24, 5], F32)
        self.sng = sb('sng', [128, 16], F32)
        self.cwf = sb('cwf', [128, 44, 4], F32)
        self.ws_init()
        self.xr = [sb(f'xr{i}', [128, D], F32) for i in range(4)]
        self.zs = [None] * 4
        self.xnT = sb('xnT', [128, 8, 512], BF16)
        self.hgT = sb('hgT', [128, 8, 512], BF16)
        self.ygT = sb('ygT', [128, 16, 512], BF16)
        self.Cf = sb('Cf', [128, 4, 256], F32); self.Cb = sb('Cb', [128, 4, 256], BF16)
        self.nf = sb('nf', [128, 4], F32); self.nb = sb('nb', [128, 4], BF16)
        self.m_b = sb('m_b', [128, 4], F32)
        self.STf = sb('STf', [128, 2048], F32); self.STb = sb('STb', [128, 2048], BF16)
        self.cq = sb('cq', [128, 8, 3], F32); self.cx = sb('cx', [128, 24, 3], F32)
        self.cff = sb('cff', [128, 44, 2], F32)
        self.scar = sb('scar', [128, 44 * 16 * 2], F32)
        self.gat = sb('gat', [128, 4, 8], F32)
        self.dta = sb('dta', [128, 4, 64], F32)
        self.ifdt = sb('ifdt', [128, 4, 40], F32)
        self.sm = [sb(f'sm{i}', [128, 32], F32) for i in range(16)]
        self.xb16 = sb('xb16', [128, D], BF16)
        self.lnsc = [sb(f'lnsc{i}', [128, 16], F32) for i in range(4)]
        self.cst = sb('cst', [128, 8], F32)
        self.pn_st = sb('pn_st', [128, 128], F32)
        R0 = S.sb_off
        o = R0
        def at(name, shape, dt):
            nonlocal o
            b = sb(name, shape, dt, at=o)
            o += b.nbytes
            return b
        self.cE = [at(f'cE{i}', [128, 520], F32) for i in range(2)]
        self.cacc = [at(f'cacc{i}', [128, 512], F32) for i in range(3)]
        self.cth = [at(f'cth{i}', [128, 512], F32) for i in range(2)]
        self.cacc2 = [at(f'cacc2_{i}', [128, 512], F32) for i in range(2)]
        e1 = o
        o = R0
        self.R1 = at('R1', [128, 4, 128], F32); self.R2 = at('R2', [128, 4, 128], F32)
        self.R3 = at('R3', [128, 4, 128], F32); self.wT = at('wT', [128, 4, 128], F32)
        self.ST = at('ST', [128, 4, 128], BF16); self.kTM = at('kTM', [128, 4, 128], BF16)
        self.hh = at('hh', [128, 4, 256], F32); self.vw = at('vw', [128, 4, 256], BF16)
        self.hgTM = at('hgTM', [128, D], BF16)
        e2 = o
        o = R0
        self.xdt = at('xdt', [128, 2048], BF16); self.xsD = at('xsD', [128, 2048], BF16)
        self.wx = at('wx', [128, 512], BF16); self.BTM = at('BTM', [128, 512], BF16)
        self.LT = at('LT', [128, 8, 128], BF16); self.MTt = at('MTt', [128, 8, 128], BF16)
        self.t1 = at('t1', [128, 512], F32)
        self.yz = at('yz', [128, 2048], BF16)
        self.ynTM = at('ynTM', [128, 2048], BF16)
        e3 = o
        F0 = max(e1, e2, e3)
        conv_end = F0
        o = F0
        self.qkT = at('qkT', [128, 8, 512], BF16)
        self.v = [at(f'v{i}', [128, D], BF16) for i in range(4)]
        self.oth = [at(f'oth{i}', [128, D], BF16) for i in range(4)]
        a1_end = o
        o = F0
        self.xbcT = at('xbcT', [128, 24, 512], BF16)
        for i in range(2):
            self.zs[i] = at(f'zs{i}', [128, 2048], BF16)
        self.LA = at('LA', [128, 8, 128], F32)
        a2_end = o
        o = F0
        self.gth = at('gth', [128, 16, 512], BF16)
        self.mixT = at('mixT', [128, 8, 512], BF16)
        self.hffT = at('hffT', [128, 22, 512], BF16)
        b_end = o
        S.sb_off = max(a1_end, a2_end, b_end)
        for i in range(2, 4):
            self.zs[i] = sb(f'zs{i}', [128, 2048], BF16)
        self.arenas = [(self.xr[2].off, 2 * self.xr[2].nbytes), (self.zs[2].off, 2 * self.zs[2].nbytes)]
        a0, a1 = self.arenas[0][0], self.arenas[1][0]
        self.C0b = [sb(f'C0b{i}', [128, 4, 256], F32, at=a0 + 4096 * i) for i in range(2)]
        self.C0b16 = [sb(f'C0b16_{i}', [128, 4, 258], BF16, at=a1 + 2080 * i) for i in range(2)]
        self.qmb = [sb(f'qmb{i}', [128, 4, 64], BF16, at=a1 + 4160 + 512 * i) for i in range(2)]
        self.kTMm = [sb(f'kTMm{i}', [128, 4, 128], BF16, at=a1 + 5184 + 1024 * i) for i in range(2)]
        self.n0T = sb('n0T', [128, 64], F32, at=a1 + 7232)
        self.n16 = sb('n16', [128, 64], BF16, at=a1 + 7488)
        self.decS = sb('decS', [128, 64], F32, at=a1 + 7616)
        self.Rm = sb('Rm', [128, 64], F32, at=a1 + 7872)
        self.S0b = [sb('S0b0', [128, 16, 128], F32, at=a0), sb('S0b1', [128, 16, 128], F32, at=self.LT.off)]
        assert self.LT.off + 8192 <= self.ynTM.off
        self.SbT = sb('SbT', [128, 2048], BF16, at=a1)
        self.wxm = sb('wxm', [128, 2048], BF16, at=a1 + 4096)
        self.ysi = sb('ysi', [128, 2048], BF16)
        self.decP = sb('decP', [128, 256], F32)
        self.Rr = sb('Rr', [128, 2, 256], F32)
        self.CTmb = [sb(f'CTmb{i}', [128, 4, 64], BF16) for i in range(2)]
        self.grpArena2 = [self.S0b[0], self.SbT, self.wxm]
        self.xsDT = sb('xsDT', [128, 8, 128], BF16, at=self.ynTM.off)
        self.LAh = [sb(f'LAh{j}', [128, 128], F32, at=self.LA.off + 512 * j) for j in range(8)]
        self.LTh = [sb(f'LTh{j}', [128, 4, 128], BF16, at=self.LT.off + 1024 * j) for j in range(2)]
        self.MTh = [[sb(f'MTh{b}_{j}', [128, 128], BF16, at=base + 256 * j) for j in range(8)]
                    for b, base in enumerate((self.MTt.off, self.ynTM.off + 2048))]
        self.grpArena = self.C0b + self.C0b16 + self.qmb + self.kTMm + [self.n0T, self.n16, self.decS, self.Rm]
        self.grpA1conv = self.cE + self.cacc + self.cth + self.cacc2
        self.grpA1rec = [self.R1, self.R2, self.R3, self.wT, self.ST, self.kTM, self.hh, self.vw, self.hgTM]
        self.grpA1fix = [self.qkT] + self.v + self.oth
        self.grpA2fix = [self.xbcT, self.zs[0], self.zs[1]] + self.LAh
        self.grpA2rec = [self.xdt, self.xsD, self.wx, self.BTM, self.t1, self.yz, self.ynTM] + self.LTh + self.MTh[0]
        self.grpB = [self.gth, self.mixT, self.hffT]
        print('SBUF used', S.sb_off, 'of', S.sb_end, 'R', R0, conv_end - R0, a1_end - R0, a2_end - R0, b_end - R0)
        self.ps = [S.psum(f'ps{i}', [128, 512], F32) for i in range(8)]
        self.setup()
        tiles = self.make_tiles()
        first = True
        for tl in tiles:
            if tl.name in self.tile_sel:
                self.pass0 = first
                self.run_tile(tl, last=(tl.name == 'T4'))
                first = False

    def make_tiles(self):
        def pch(slot, col0, c):
            ch = Chunk(slot, col0, 128, 'p', row0=128 * c)
            ch.final = (c == 15)
            return ch
        m = Chunk(0, 0, 16, 'm'); m.final = False
        tiles = [Tile('T0', 400, [m] + [pch(1 + i, 16 + 128 * i, i) for i in range(3)], [(0, 1, 400, 'p')])]
        for t in range(3):
            tiles.append(Tile(f'T{t + 1}', 512, [pch(i, 128 * i, 3 + 4 * t + i) for i in range(4)], [(0, 1, 512, 'p')]))
        sc = Chunk(1, 128, 64, 's'); sc.final = False
        tiles.append(Tile('T4', 192, [pch(0, 0, 15), sc], [(0, 1, 128, 'p'), (128, 16, 4, 's')]))
        return tiles

    def psb(self, i):
        return self.ps[i].ap().bitcast(BF16)

    def setup(self):
        I = self.I
        self.dma(self.cf.ap(), I['cf'].ap())
        self.dma(self.cb.ap(), I['cb'].ap())
        pb = lambda n: View(I[n], I[n].t.partition_broadcast(128))
        self.dma(self.bif_b.ap(), pb('b_if'))
        self.dma(self.dtb_b.ap(), pb('dt_bias'))
        self.dma(self.A_b.ap(), pb('A_log'))
        self.dma(self.D_b.ap(), pb('ssm_D'))
        self.act(self.A_b.ap(), self.A_b.ap(), AF.Exp)
        self.ts(self.A_b.ap(), self.A_b.ap(), -1.0, ALU.mult)
        D3 = self.D_b.ap().rearrange('p (g r) -> p g r', r=2)
        self.cp(self.Dfm[0:64, :], D3[0:64, :, 0], 'dve')
        self.cp(self.Dfm[64:128, :], D3[64:128, :, 1], 'dve')
        identf = self.cfv('ident')
        stg = self.cacc[0]
        def fm_params(dst, rows, G, scale_groups=None):
            R = sum(r for _, r in rows)
            for g0 in range(0, G, 4):
                gn = min(4, G - g0)
                r0 = 0
                for (nm, nr) in rows:
                    self.dma(stg[r0:r0 + nr, 0:gn * 128], I[nm][:, g0 * 128:(g0 + gn) * 128])
                    r0 += nr
                bank = self.ps[self.rot('setup', 2)]
                for g in range(gn):
                    self.tr(bank[:, g * R:(g + 1) * R], stg[0:R, g * 128:(g + 1) * 128], identf[0:R, 0:R])
                self.cp(dst[:, g0:g0 + gn, :], bank[:, 0:gn * R].rearrange('p (g r) -> p g r', r=R), 'act')
        fm_params(self.cwm, [('w_mconv', 4), ('b_mconv', 1), ('mnorm_g', 1)], 8)
        fm_params(self.cws, [('w_sconv', 4), ('b_sconv', 1)], 24)
        fm_params(self.cwf, [('w_fconv', 3), ('b_fconv', 1)], 44)
        sng3 = self.sng.ap().rearrange('p (g r) -> p g r', r=1)
        fm_params(sng3, [('snorm_g', 1)], 16)
        self.ts(self.cwm[:, :, 0:6], self.cwm[:, :, 0:6], 0.5, ALU.mult)
        self.ts(self.cws.ap(), self.cws.ap(), 0.5, ALU.mult)
        self.ts(self.cwf[:, 0:22, :], self.cwf[:, 0:22, :], 0.5, ALU.mult)
        self.memset(self.cst[:, 0:1], LN_EPS)
        self.memset(self.cst[:, 1:2], 0.5 * float(np.log(128.0)))
        self.memset(self.cst[:, 2:3], 1.0)
        self.eps_t = self.cst
        for b in (self.Cf, self.nf, self.m_b, self.STf, self.cq, self.cx, self.cff):
            self.memset(b.ap(), 0.0)
        for b in (self.Cb, self.nb, self.STb):
            self.memset(b.ap(), 0.0, 'pool')

    def kc(self, kind, n):
        if kind == 's':
            return dict(U=self.cfv('Us', 64), LS=self.cfv('LSs', 64), BO=self.cfv('BOs', 64),
                        M=self.cbv('Ms', 64), MT=self.cbv('MTs', 64))
        return dict(U=self.cfv('U', n, n), LS=self.cfv('LS', n, n), BO=self.cfv('ones', n, n),
                    M=self.cbv('M', n, n), MT=self.cbv('MT', n, n))

    def ln_load(self, gname, bname):
        I = self.I
        self.dma(self.lnc[:, 0, :], View(I[gname], I[gname].t.partition_broadcast(128)))
        self.dma(self.lnc[:, 1, :], View(I[bname], I[bname].t.partition_broadcast(128)))

    def ln_rows(self, x, n, slot):
        sc = self.lnsc[slot]
        st, mv, rs = sc[:n, 0:12], sc[:n, 12:14], sc[:n, 14:15]
        for i in range(2):
            self.S.op('dve', lambda e, i=i: e.bn_stats(out=sc.t[:n, i * 6:(i + 1) * 6], in_=x.ap[:, i * 512:(i + 1) * 512]),
                      reads=[x], writes=[sc])
        self.S.op('dve', lambda e: e.bn_aggr(out=sc.t[:n, 12:14], in_=sc.t[:n, 0:12]), reads=[sc], writes=[sc])
        self.act(rs, sc[:n, 13:14], AF.Ln, bias=self.eps_t[:n, 0:1])
        self.act(rs, rs, AF.Exp, scale=-0.5)
        self.ts(x, x, sc[:n, 12:13], ALU.subtract, rs, ALU.mult)
        self.tt(x, x, self.lnc[:n, 0, :], ALU.mult)
        self.tt(x, x, self.lnc[:n, 1, :], ALU.add)

    def to_fm(self, tl, src_of_chunk, dstT):
        identb = self.cbv('identb')
        for ch in tl.chunks:
            n = ch.n
            xb = self.xb16
            self.act(xb[:n, :], src_of_chunk(ch))
            bank = 6 + self.rot('tfm', 2)
            pv = self.psb(bank)
            for k in range(8):
                self.tr(pv[:, k * n:(k + 1) * n], xb[:n, k * 128:(k + 1) * 128], identb[:n, :n])
            self.cp(dstT[:, :, ch.col0:ch.col0 + n], pv[:, 0:8 * n].rearrange('p (k n) -> p k n', n=n), 'dve')

    def _conv_taps(self, tl, psv, W, wtab, g, carry_p, scar_view, E, acc):
        Wm = W - 1
        off = 0
        for (col0, nb, L, kind) in tl.segs:
            Ev = E[:, off:off + nb * (L + Wm)].rearrange('p (b l) -> p b l', b=nb)
            pseg = psv[:, col0:col0 + nb * L].rearrange('p (b l) -> p b l', b=nb)
            if kind == 'p':
                self.cp(Ev[:, :, 0:Wm], carry_p[:, g:g + 1, :], 'act')
            elif kind == 'm':
                self.memset(Ev[:, :, 0:Wm], 0.0, 'dve')
            else:
                self.cp(Ev[:, :, 0:Wm], scar_view[:, g, :, :], 'act')
            self.act(Ev[:, :, Wm:Wm + L], pseg)
            av = acc[:, col0:col0 + nb * L].rearrange('p (b l) -> p b l', b=nb)
            self.act(av, pseg, AF.Identity, scale=wtab[:, g, Wm:W], bias=wtab[:, g, W:W + 1])
            if kind == 's':
                self.cp(scar_view[:, g, :, :], Ev[:, :, L:L + Wm], 'act')
            else:
                self.cp(carry_p[:, g:g + 1, :], Ev[:, :, L:L + Wm], 'act')
            for j in range(Wm):
                self.stt(av, Ev[:, :, j:j + L], wtab[:, g, j:j + 1], av, ALU.mult, ALU.add)
            off += nb * (L + Wm)

    def conv_group(self, tl, psv, W, wtab, g, carry_p, scar_view, dst, final=True):
        E = self.cE[self.rot('cE', 2)]
        acc = self.cacc[self.rot('cacc', 3)]
        self._conv_taps(tl, psv, W, wtab, g, carry_p, scar_view, E, acc)
        T = tl.T
        if not final:
            return acc
        prev = getattr(self, '_conv_pending', None)

        def stage2(acc=acc, dst=dst, T=T):
            th = self.cth[self.rot('cth', 2)]
            self.act(th[:, 0:T], acc[:, 0:T], AF.Tanh)
            self.stt(dst, th[:, 0:T], 1.0, acc[:, 0:T], ALU.add, ALU.mult)
        self._conv_pending = stage2
        if prev is not None:
            prev()
        return acc

    def conv_flush(self):
        prev = getattr(self, '_conv_pending', None)
        self._conv_pending = None
        if prev is not None:
            prev()

    def carry_out(self, src, G, R, dst):
        identf = self.cfv('ident')
        for g0 in range(0, G, 4):
            gn = min(4, G - g0)
            bank = self.ps[self.rot('co', 2)]
            for g in range(gn):
                self.tr(bank[:R, g * 128:(g + 1) * 128], src[:, g0 + g, :], identf)
            stg = self.cacc[self.rot('cacc', 3)]
            self.cp(stg[:R, 0:gn * 128], bank[:R, 0:gn * 128], 'act')
            self.dma(dst[:, g0 * 128:(g0 + gn) * 128], stg[:R, 0:gn * 128])

    def scar_in(self, name, G, R):
        identf = self.cfv('ident')
        rows = 16 * R
        sv = self.scar[:, 0:G * rows].rearrange('p (g b r) -> p g b r', g=G, b=16)
        for g0 in range(0, G, 4):
            gn = min(4, G - g0)
            stg = self.cacc[self.rot('cacc', 3)]
            self.dma(stg[:rows, 0:gn * 128], self.I[name][:, g0 * 128:(g0 + gn) * 128])
            bank = self.ps[self.rot('co', 2)]
            for g in range(gn):
                self.tr(bank[:, g * rows:(g + 1) * rows], stg[:rows, g * 128:(g + 1) * 128], identf[:rows, :rows])
            self.cp(self.scar[:, g0 * rows:(g0 + gn) * rows], bank[:, 0:gn * rows], 'act')
        return sv

    def scar_out(self, name, G, R):
        rows = 16 * R
        src = self.scar[:, 0:G * rows].rearrange('p (g br) -> p g br', g=G)
        self.carry_out(src, G, rows, self.O[name].ap())

    def dense_fm(self, tl, actT, nkt_total, cb_group, kt0=0):
        Wb, (name, k0, nk, parts) = self.wnext()
        ncols = sum(p[1] for p in parts)
        T = tl.T
        for gl in range(ncols // 128):
            bank = self.ps[self.rot('mm', 4)]
            for k in range(nk):
                self.mm(bank[:, 0:T], Wb[:, k, gl * 128:(gl + 1) * 128], actT[:, k0 + k, 0:T],
                        start=(k0 + k == 0), stop=(k0 + k == nkt_total - 1))
            cb_group(gl, bank[:, 0:T])

    def dense_tm(self, tl, actT, cb_chunk):
        Wb, (name, k0, nk, parts) = self.wnext()
        ncols = sum(p[1] for p in parts)
        for ch in tl.chunks:
            bank = self.ps[self.rot('mm', 4)]
            for k in range(nk):
                self.mm(bank[:ch.n, 0:ncols], actT[:, k0 + k, ch.col0:ch.col0 + ch.n], Wb[:, k, 0:ncols],
                        start=(k == 0), stop=(k == nk - 1))
            cb_chunk(ch, bank[:ch.n, 0:ncols])

    def run_tile(self, tl, last):
        S, I, O = self.S, self.I, self.O
        T = tl.T
        isS = any(sg[3] == 's' for sg in tl.segs)
        if isS:
            S.alias_phase([self.xr[2], self.xr[3], self.zs[2], self.zs[3]], self.grpArena + self.grpArena2)
        for ch in tl.chunks:
            src = {'s': I['xs'].ap(), 'm': I['meta'].ap()}.get(ch.kind)
            if src is None:
                src = I['xp'][ch.row0:ch.row0 + ch.n, :]
            self.dma(self.xr[ch.slot][:ch.n, :], src)
        self.ln_load('ln0_g', 'ln0_b')
        for ch in tl.chunks:
            self.ln_rows(self.xr[ch.slot][:ch.n, :], ch.n, ch.slot)
        self.to_fm(tl, lambda ch: self.xr[ch.slot][:ch.n, :], self.xnT)
        for ch in tl.chunks:
            self.ts(self.xr[ch.slot][:ch.n, :], self.xr[ch.slot][:ch.n, :], ALPHA, ALU.mult)
        self.dump('xnT_' + tl.name, self.xnT[:, :, 0:T], [128, 8, T])
        if self.debug.get('stop') == 'p0':
            return
        S.alias_phase(self.grpA2fix + self.grpA2rec + self.grpB + self.grpA1rec, self.grpA1conv + self.grpA1fix)
        sq = self.scar_in('s_mconv', 8, 3) if isS else None
        for blk in range(4):
            def cbq(gl, psv, blk=blk):
                g = 2 * blk + gl
                self.conv_group(tl, psv, 4, self.cwm, g, self.cq, sq, self.qkT[:, g, 0:T])
            self.dense_fm(tl, self.xnT, 8, cbq)
        self.conv_flush()
        if isS:
            self.scar_out('o_mconv', 8, 3)
        if last:
            self.carry_out(self.cq.ap(), 8, 3, O['p_mconv'].ap())
        for blk in range(4):
            self.dense_tm(tl, self.xnT, lambda ch, psv, blk=blk: self.act(self.v[ch.slot][:ch.n, 256 * blk:256 * blk + 256], psv))
        for blk in range(4):
            self.dense_tm(tl, self.xnT, lambda ch, psv, blk=blk: self.act(self.oth[ch.slot][:ch.n, 256 * blk:256 * blk + 256], psv, AF.Tanh, scale=0.5))
        self.dense_tm(tl, self.xnT, lambda ch, psv: self.cp(self.ifdt[:ch.n, ch.slot, :], psv, 'dve'))
        for ch in tl.chunks:
            n, s = ch.n, ch.slot
            gi = self.gat[:n, s, 0:8]
            self.tt(gi, self.ifdt[:n, s, 0:8], self.bif_b[:n, :], ALU.add)
            e1 = self.sm[3]
            self.act(e1[:n, 0:4], self.gat[:n, s, 4:8], AF.Exp, scale=-1.0)
            self.act(e1[:n, 0:4], e1[:n, 0:4], AF.Ln, bias=self.cst[:n, 2:3])
            self.ts(self.gat[:n, s, 4:8], e1[:n, 0:4], -1.0, ALU.mult)
            d1 = self.sm[4]
            self.tt(d1[:n, 0:32], self.ifdt[:n, s, 8:40], self.dtb_b[:n, :], ALU.add)
            self.act(d1[:n, 0:32], d1[:n, 0:32], AF.Exp)
            self.act(self.dta[:n, s, 0:32], d1[:n, 0:32], AF.Ln, bias=self.cst[:n, 2:3])
            self.tt(self.dta[:n, s, 32:64], self.dta[:n, s, 0:32], self.A_b[:n, :], ALU.mult)
        self.dump('qkT_' + tl.name, self.qkT[:, :, 0:T], [128, 8, T])
        self.dump('gat_' + tl.name, self.gat.ap(), [128, 4, 8])
        self.dump('dta_' + tl.name, self.dta.ap(), [128, 4, 64])
        if self.debug.get('stop') == 'a1':
            return
        S.alias_phase(self.grpA1conv, self.grpA1rec)
        for ch in tl.chunks:
            self.mlstm_chunk(tl, ch, last)
        self.dump('hgT_' + tl.name, self.hgT[:, :, 0:T], [128, 8, T])
        if self.debug.get('stop') == 'mlstm':
            return
        S.alias_phase(self.grpA1rec + self.grpA1fix, self.grpA1conv + self.grpA2fix)
        for blk in range(8):
            def cbz(ch, psv, blk=blk):
                n = ch.n
                zc = self.cacc[self.rot('cacc', 3)]
                th = self.cth[self.rot('cth', 2)]
                self.cp(zc[:n, 0:256], psv, 'act')
                self.act(th[:n, 0:256], psv, AF.Tanh, scale=0.5)
                self.stt(self.zs[ch.slot][:n, 256 * blk:256 * blk + 256], th[:n, 0:256], 1.0, zc[:n, 0:256], ALU.add, ALU.mult)
            self.dense_tm(tl, self.xnT, cbz)
        if self.debug.get('stop') == 'a2z':
            return
        sx = self.scar_in('s_sconv', 24, 3) if isS else None
        for blk in range(12):
            def cbx(gl, psv, blk=blk):
                g = 2 * blk + gl
                self.conv_group(tl, psv, 4, self.cws, g, self.cx, sx, self.xbcT[:, g, 0:T])
            self.dense_fm(tl, self.xnT, 8, cbx)
        self.conv_flush()
        if isS:
            self.scar_out('o_sconv', 24, 3)
        if last:
            self.carry_out(self.cx.ap(), 24, 3, O['p_sconv'].ap())
        self.dump('xbcT_' + tl.name, self.xbcT[:, :, 0:T], [128, 24, T])
        if self.debug.get('stop') == 'a2':
            return
        S.alias_phase(self.grpA1conv, self.grpA2rec)
        for ch in tl.chunks:
            self.ssd_chunk(tl, ch, last)
        self.dump('ygT_' + tl.name, self.ygT[:, :, 0:T], [128, 16, T])
        if self.debug.get('stop') == 'ssd':
            return
        S.alias_phase(self.grpA2rec + self.grpA2fix, self.grpA1conv + self.grpB)
        for blk in range(8):
            def cbg(gl, psv, blk=blk):
                self.act(self.gth[:, 2 * blk + gl, 0:T], psv, AF.Tanh, scale=0.5)
            self.dense_fm(tl, self.xnT, 8, cbg)
        for j in range(4):
            Wb, (name, k0, nk, parts) = self.wnext()
            banksA = [self.ps[0], self.ps[1]]
            banksB = [self.ps[2], self.ps[3]]
            for gl in range(2):
                for k in range(8):
                    self.mm(banksA[gl][:, 0:T], Wb[:, k, gl * 128:(gl + 1) * 128], self.hgT[:, k, 0:T], start=(k == 0), stop=(k == 7))
            for half in range(2):
                Wb, (name, k0, nk, parts) = self.wnext()
                for gl in range(2):
                    for k in range(8):
                        kk = 8 * half + k
                        self.mm(banksB[gl][:, 0:T], Wb[:, k, gl * 128:(gl + 1) * 128], self.ygT[:, kk, 0:T], start=(kk == 0), stop=(kk == 15))
            for gl in range(2):
                g = 2 * j + gl
                m1 = self.cacc[self.rot('cacc', 3)]
                m2 = self.cacc2[self.rot('cacc2', 2)]
                self.stt(m1[:, 0:T], self.gth[:, g, 0:T], 1.0, banksA[gl][:, 0:T], ALU.add, ALU.mult)
                self.stt(m2[:, 0:T], self.gth[:, 8 + g, 0:T], 1.0, banksB[gl][:, 0:T], ALU.add, ALU.mult)
                self.tt(self.mixT[:, g, 0:T], m1[:, 0:T], m2[:, 0:T], ALU.add)
        self.rr['mm'] = 0
        for blk in range(4):
            def cbo(ch, psv, blk=blk):
                xv = self.xr[ch.slot][:ch.n, 256 * blk:256 * blk + 256]
                self.stt(xv, psv, 0.5, xv, ALU.mult, ALU.add)
            self.dense_tm(tl, self.mixT, cbo)
        self.ln_load('ln1_g', 'ln1_b')
        for ch in tl.chunks:
            self.ln_rows(self.xr[ch.slot][:ch.n, :], ch.n, ch.slot)
        self.dump('x1_' + tl.name, self.xr[0].ap(), [128, D])
        self.to_fm(tl, lambda ch: self.xr[ch.slot][:ch.n, :], self.xnT)
        for ch in tl.chunks:
            self.ts(self.xr[ch.slot][:ch.n, :], self.xr[ch.slot][:ch.n, :], ALPHA, ALU.mult)
        sf = self.scar_in('s_fconv', 44, 2) if isS else None
        for j in range(11):
            accs = {}
            def cbua(gl, psv, j=j):
                g = 2 * j + gl
                accs[gl] = self.conv_group(tl, psv, 3, self.cwf, g, self.cff, sf, None, final=False)
            self.dense_fm(tl, self.xnT, 8, cbua)
            def cbub(gl, psv, j=j):
                g = 2 * j + gl
                E = self.cE[self.rot('cE', 2)]
                accb = self.cacc2[self.rot('cacc2', 2)]
                self._conv_taps(tl, psv, 3, self.cwf, 22 + g, self.cff, sf, E, accb)
                th = self.cth[self.rot('cth', 2)]
                acca = accs[gl]
                self.act(th[:, 0:T], acca[:, 0:T], AF.Tanh)
                self.stt(th[:, 0:T], th[:, 0:T], 1.0, acca[:, 0:T], ALU.add, ALU.mult)
                self.tt(self.hffT[:, g, 0:T], th[:, 0:T], accb[:, 0:T], ALU.mult)
            self.dense_fm(tl, self.xnT, 8, cbub)
        if isS:
            self.scar_out('o_fconv', 44, 2)
        if last:
            self.carry_out(self.cff.ap(), 44, 2, O['p_fconv'].ap())
        self.dump('hffT_' + tl.name, self.hffT[:, :, 0:T], [128, 22, T])
        for blk in range(4):
            banks = {ch.slot: self.ps[ch.slot] for ch in tl.chunks}
            for (k0, nk) in ((0, 8), (8, 8), (16, 6)):
                Wb, meta = self.wnext()
                for ch in tl.chunks:
                    for k in range(nk):
                        self.mm(banks[ch.slot][:ch.n, 0:256], self.hffT[:, k0 + k, ch.col0:ch.col0 + ch.n], Wb[:, k, 0:256],
                                start=(k0 + k == 0), stop=(k0 + k == 21))
            for ch in tl.chunks:
                xv = self.xr[ch.slot][:ch.n, 256 * blk:256 * blk + 256]
                self.tt(xv, banks[ch.slot][:ch.n, 0:256], xv, ALU.add)
        self.ln_load('ln2_g', 'ln2_b')
        for ch in tl.chunks:
            self.ln_rows(self.xr[ch.slot][:ch.n, :], ch.n, ch.slot)
            if ch.kind == 'p':
                self.dma(O['y_p'][ch.row0:ch.row0 + ch.n, :], self.xr[ch.slot][:ch.n, :])
            elif ch.kind == 's':
                self.dma(O['y_s'].ap(), self.xr[ch.slot][:ch.n, :])

    def mlstm_chunk(self, tl, ch, last):
        I, O = self.I, self.O
        n, s, c0, kind = ch.n, ch.slot, ch.col0, ch.kind
        kc = self.kc(kind, n)
        U, LS, M, MT = kc['U'], kc['LS'], kc['M'], kc['MT']
        identf, onesf = self.cfv('ident', n, n), self.cfv('ones', n, n)
        identb, onesb = self.cbv('identb', n, n), self.cbv('onesb', n, n)
        ps = self.ps
        cols = slice(c0, c0 + n)
        ig = self.gat[:n, s, 0:4]
        lf = self.gat[:n, s, 4:8]
        sm = self.sm
        bt, mi, bm, mt, wi, emt, negm, den, rden, wi2 = (sm[i] for i in range(5, 15))
        R1, R2, R3, wT, ST, kTM, hh, vw = self.R1, self.R2, self.R3, self.wT, self.ST, self.kTM, self.hh, self.vw
        self.mm(ps[0][:n, 0:4], U, lf)
        for h in range(4):
            self.ts(R1[:n, h, :n], LS, self.gat[:n, s, 4 + h:5 + h], ALU.mult)
            self.act(R2[:n, h, :n], identf, AF.Copy, scale=self.gat[:n, s, h:h + 1])
        for h in range(4):
            self.mm(ps[1][:n, h * 128:h * 128 + n], U, R1[:n, h, :n], start=True, stop=False)
            self.mm(ps[1][:n, h * 128:h * 128 + n], onesf, R2[:n, h, :n], start=False, stop=False)
            self.mm(ps[1][:n, h * 128:h * 128 + n], identb, M, start=False, stop=True)
        self.rmax(mi[:n, 0:4], ps[1][:n, :].rearrange('p (h t) -> p h t', h=4)[:, :, 0:n])
        if kind == 's':
            m0s = sm[15]
            self.dma(m0s[:16, 0:4], I['s_m'].ap())
            self.mm(ps[0][:n, 4:8], self.cfv('BMT', 16), m0s[:16, 0:4])
            m0v = ps[0][:n, 4:8]
        else:
            m0v = self.m_b[:n, :]
        self.cp(bt[:n, 0:4], ps[0][:n, 0:4], 'act')
        self.tt(bm[:n, 0:4], bt[:n, 0:4], m0v, ALU.add)
        self.tt(mt[:n, 0:4], bm[:n, 0:4], mi[:n, 0:4], ALU.max)
        self.tt(bm[:n, 0:4], bm[:n, 0:4], mt[:n, 0:4], ALU.subtract)
        self.act(wi[:n, 0:4], bm[:n, 0:4], AF.Exp)
        self.act(emt[:n, 0:4], mt[:n, 0:4], AF.Exp, scale=-1.0, bias=self.cst[:n, 1:2])
        self.ts(negm[:n, 0:4], mt[:n, 0:4], -1.0, ALU.mult)
        for h in range(4):
            self.act(R3[:n, h, :n], identf, AF.Copy, scale=negm[:n, h:h + 1])
        for h in range(4):
            o = ps[2][:n, h * 128:h * 128 + n]
            self.mm(o, R1[:n, h, :n], U, start=True, stop=False)
            self.mm(o, R2[:n, h, :n], onesf, start=False, stop=False)
            self.mm(o, onesf, R3[:n, h, :n], start=False, stop=False)
            self.mm(o, identb, MT, start=False, stop=True)
        ps2v = ps[2][:n, :].rearrange('p (h t) -> p h t', h=4)[:, :, 0:n]
        self.act(wT[:n, :, :n], ps2v, AF.Exp)
        for h in range(4):
            self.mm(ps[1][:n, h * 128:h * 128 + n], self.qkT[:, 4 + h, cols], self.qkT[:, h, cols])
        ps1v = ps[1][:n, :].rearrange('p (h t) -> p h t', h=4)[:, :, 0:n]
        self.tt(ST[:n, :, :n], ps1v, wT[:n, :, :n], ALU.mult)
        pb7 = self.psb(7)
        for h in range(4):
            self.tr(pb7[:n, h * 128:(h + 1) * 128], self.qkT[:, 4 + h, cols], self.cbv('identb'))
        self.cp(kTM[:n, :, :], pb7[:n, 0:512].rearrange('p (h d) -> p h d', h=4), 'act')
        if kind == 's':
            self.mlstm_sample_states(tl, ch, wi, wT, kTM, mt)
        for h in range(4):
            self.mm(ps[3 + h // 2][:n, (h % 2) * 256:(h % 2) * 256 + 256], ST[:n, h, :n], self.v[s][:n, h * 256:(h + 1) * 256])
        for h in range(4):
            self.mm(ps[0][:n, 8 + h:9 + h], ST[:n, h, :n], onesb[:, 0:1])
        if kind != 's':
            for h in range(4):
                self.mm(ps[5 + h // 2][:n, (h % 2) * 256:(h % 2) * 256 + 256], self.qkT[:, h, cols], self.Cb[:, h, :])
            for h in range(4):
                self.mm(ps[0][:n, 12 + h:13 + h], self.qkT[:, h, cols], self.nb[:, h:h + 1])
        dint = ps[0][:n, 12:16] if kind != 's' else sm[12][:n, 0:4]
        self.tt(den[:n, 0:4], dint, wi[:n, 0:4], ALU.mult)
        self.tt(den[:n, 0:4], den[:n, 0:4], ps[0][:n, 8:12], ALU.add)
        self.ts(wi2[:n, 0:4], den[:n, 0:4], -1.0, ALU.mult)
        self.tt(den[:n, 0:4], den[:n, 0:4], wi2[:n, 0:4], ALU.max)
        self.tt(den[:n, 0:4], den[:n, 0:4], emt[:n, 0:4], ALU.max)
        self.S.op('dve', lambda e: e.reciprocal(out=rden.t[:n, 0:4], in_=den.t[:n, 0:4]), reads=[den], writes=[rden])
        self.tt(wi2[:n, 0:4], wi[:n, 0:4], rden[:n, 0:4], ALU.mult)
        for h in range(4):
            self.act(hh[:n, h, :], ps[3 + h // 2][:n, (h % 2) * 256:(h % 2) * 256 + 256], AF.Copy, scale=rden[:n, h:h + 1])
            if kind != 's':
                iv = ps[5 + h // 2][:n, (h % 2) * 256:(h % 2) * 256 + 256]
            else:
                iv = (R1 if h < 2 else R2)[:n, :, :].rearrange('p a b -> p (a b)')[:, (h % 2) * 256:(h % 2) * 256 + 256]
            self.stt(hh[:n, h, :], iv, wi2[:n, h:h + 1], hh[:n, h, :], ALU.mult, ALU.add)
        st, mv, rs = sm[0], sm[1], sm[2]
        for h in range(4):
            self.S.op('dve', lambda e, h=h: e.bn_stats(out=st.t[:n, h * 6:(h + 1) * 6], in_=hh.t[:n, h, :]), reads=[hh], writes=[st])
        for h in range(4):
            self.S.op('dve', lambda e, h=h: e.bn_aggr(out=mv.t[:n, 2 * h:2 * h + 2], in_=st.t[:n, h * 6:(h + 1) * 6]), reads=[st], writes=[mv])
        mvv = mv[:n, 0:8].rearrange('p (h t) -> p h t', t=2)
        self.act(rs[:n, 0:4], mvv[:, :, 1], AF.Ln, bias=self.cst[:n, 0:1])
        self.act(rs[:n, 0:4], rs[:n, 0:4], AF.Exp, scale=-0.5)
        for h in range(4):
            self.ts(hh[:n, h, :], hh[:n, h, :], mv[:n, 2 * h:2 * h + 1], ALU.subtract, rs[:n, h:h + 1], ALU.mult)
            self.stt(self.hgTM[:n, h * 256:(h + 1) * 256], self.oth[s][:n, h * 256:(h + 1) * 256], 1.0, hh[:n, h, :], ALU.add, ALU.mult)
        for k in range(8):
            self.tr(pb7[:, k * n:(k + 1) * n], self.hgTM[:n, k * 128:(k + 1) * 128], identb)
        self.cp(self.hgT[:, :, cols], pb7[:, 0:8 * n].rearrange('p (k t) -> p k t', k=8), 'dve')
        if kind != 's':
            wl16 = sm[15]
            self.cp(wl16.ap().bitcast(BF16)[:n, 0:4], wT[:n, :, n - 1], 'act')
            for h in range(4):
                self.ts(vw[:n, h, :], self.v[s][:n, h * 256:(h + 1) * 256], wT[:n, h, n - 1:n], ALU.mult)
            for h in range(4):
                self.mm(ps[5 + h // 2][:, (h % 2) * 256:(h % 2) * 256 + 256], kTM[:n, h, :], vw[:n, h, :])
            for h in range(4):
                self.mm(ps[0][:, 16 + h:17 + h], kTM[:n, h, :], wl16.ap().bitcast(BF16)[:n, h:h + 1])
            SEL = self.cfv('SELp' if n == 128 else 'SELm', n)
            self.mm(ps[0][:, 32:36], SEL, wi[:n, 0:4])
            self.mm(ps[0][:, 36:40], SEL, mt[:n, 0:4])
            dec = sm[3]
            self.cp(dec[:, 0:8], ps[0][:, 32:40], 'act')
            for h in range(4):
                self.stt(self.Cf[:, h, :], self.Cf[:, h, :], dec[:, h:h + 1], ps[5 + h // 2][:, (h % 2) * 256:(h % 2) * 256 + 256], ALU.mult, ALU.add)
            self.tt(self.nf.ap(), self.nf.ap(), dec[:, 0:4], ALU.mult)
            self.tt(self.nf.ap(), self.nf.ap(), ps[0][:, 16:20], ALU.add)
            self.cp(self.m_b.ap(), dec[:, 4:8], 'dve')
            self.cp(self.Cb.ap(), self.Cf.ap(), 'act')
            self.cp(self.nb.ap(), self.nf.ap(), 'act')
            if ch.final:
                self.dma(O['p_C'].ap().rearrange('h d e -> d h e'), self.Cf.ap())
                identf128 = self.cfv('ident')
                self.tr(ps[0][:4, 128:256], self.nf.ap(), identf128)
                self.cp(self.pn_st[:4, :], ps[0][:4, 128:256], 'act')
                self.dma(O['p_n'].ap(), self.pn_st[:4, :])
                self.dma(O['p_m'].ap(), self.m_b[0:1, :])

    def mlstm_sample_states(self, tl, ch, wi, wT, kTM, mt):
        I, O, ps, sm = self.I, self.O, self.ps, self.sm
        n, s = 64, ch.slot
        identf = self.cfv('ident')
        RS, BM = self.cfv('RS', 64), self.cfv('BM', 64)
        CM = self.cbv('CM').rearrange('p (b j) -> p b j', b=16)
        vw = self.vw
        wl = sm[3]
        tmp = self.R3
        self.tt(tmp[:n, :, 0:64], wT[:n, :, 0:64], View(self.cf, self.cfv('LSEL', 64).ap.unsqueeze(1).to_broadcast([64, 4, 64])), ALU.mult)
        self.S.op('dve', lambda e: e.tensor_reduce(out=wl.t[:n, 0:4], in_=tmp.t[:n, :, 0:64], axis=AX.X, op=ALU.add), reads=[tmp], writes=[wl])
        wl16 = sm[4].ap().bitcast(BF16)
        self.cp(wl16[:n, 0:4], wl[:n, 0:4], 'act')
        for h in range(4):
            self.ts(vw[:n, h, :], self.v[s][:n, h * 256:(h + 1) * 256], wl[:n, h:h + 1], ALU.mult)
        Rm3 = self.Rm[:n, :].rearrange('p (b h) -> p b h', h=4)
        self.tt(Rm3, View(wi, wi.t[:n, 0:4].unsqueeze(1).to_broadcast([n, 16, 4])),
                View(self.cf, RS.ap.unsqueeze(2).to_broadcast([n, 16, 4])), ALU.mult)
        self.mm(ps[0][:, 64:128], self.cfv('ones', 64), self.Rm[:n, :])
        self.cp(self.decS.ap(), ps[0][:, 64:128], 'act')
        self.mm(ps[0][:16, 40:44], RS, mt[:n, 0:4])
        mo = sm[15]
        self.cp(mo[:16, 8:12], ps[0][:16, 40:44], 'act')
        self.dma(O['o_m'].ap(), mo[:16, 8:12])
        stg = self.pn_st
        self.dma(stg[:64, :], I['s_n'].ap())
        self.tr(ps[0][:, 192:256], stg[:64, :], identf[:64, :64])
        self.cp(self.n0T.ap(), ps[0][:, 192:256], 'act')
        self.cp(self.n16.ap(), self.n0T.ap(), 'pool')
        kTMflat = kTM[:n, :, :].rearrange('p h d -> p (h d)')
        for b in range(16):
            i = b % 2
            C0, C16, qm, km = self.C0b[i], self.C0b16[i], self.qmb[i], self.kTMm[i]
            self.dma(C0.ap(), I['s_C'][b].rearrange('h d e -> d h e'))
            self.cp(C16[:, :, 0:256], C0.ap(), 'act')
            self.cp(C16[:, :, 256:257], self.n16[:, 4 * b:4 * b + 4].rearrange('p (h o) -> p h o', o=1), 'dve')
            self.tt(qm.ap(), self.qkT[:, 0:4, ch.col0:ch.col0 + 64], View(self.cb, CM.ap[:, b, :].unsqueeze(1).to_broadcast([128, 4, 64])), ALU.mult)
            self.ts(km[:n, :, :].rearrange('p h d -> p (h d)'), kTMflat, BM[:, b:b + 1], ALU.mult)
            for h in range(4):
                self.mm(ps[3 + h][:n, 0:257], qm[:, h, :], C16[:, h, 0:257], start=(b == 0), stop=(b == 15))
            for h in range(4):
                self.mm(ps[1 + h // 2][:, (h % 2) * 256:(h % 2) * 256 + 256], km[:n, h, :], vw[:n, h, :])
            for h in range(4):
                self.mm(ps[0][:, 128 + 4 * b + h:129 + 4 * b + h], km[:n, h, :], wl16[:n, h:h + 1])
            for h in range(4):
                self.stt(C0[:, h, :], C0[:, h, :], self.decS[:, 4 * b + h:4 * b + h + 1],
                         ps[1 + h // 2][:, (h % 2) * 256:(h % 2) * 256 + 256], ALU.mult, ALU.add)
            self.dma(O['o_C'][b].rearrange('h d e -> d h e'), C0.ap())
        for h in range(4):
            dst = (self.R1 if h < 2 else self.R2)[:n, :, :].rearrange('p a b -> p (a b)')[:, (h % 2) * 256:(h % 2) * 256 + 256]
            self.cp(dst, ps[3 + h][:n, 0:256], 'act')
            self.cp(sm[12][:n, h:h + 1], ps[3 + h][:n, 256:257], 'act')
        self.tt(self.n0T.ap(), self.n0T.ap(), self.decS.ap(), ALU.mult)
        self.tt(self.n0T.ap(), self.n0T.ap(), ps[0][:, 128:192], ALU.add)
        self.tr(ps[0][:64, 256:384], self.n0T.ap(), identf)
        self.cp(stg[:64, :], ps[0][:64, 256:384], 'act')
        self.dma(O['o_n'].ap(), stg[:64, :])

    def ssd_chunk(self, tl, ch, last):
        I, O, ps, sm = self.I, self.O, self.ps, self.sm
        n, s, c0, kind = ch.n, ch.slot, ch.col0, ch.kind
        kc = self.kc(kind, n)
        U, LS, BO, MT = kc['U'], kc['LS'], kc['BO'], kc['MT']
        identb = self.cbv('identb')
        cols = slice(c0, c0 + n)
        dt = self.dta[:n, s, 0:32]
        a = self.dta[:n, s, 32:64]
        btsb, dect, wend, decS = sm[5], sm[6], sm[7], sm[8]
        self.mm(ps[0][:n, 0:32], U, a)
        self.mm(ps[0][:n, 32:64], BO, a)
        self.cp(btsb[:n, 0:32], ps[0][:n, 0:32], 'act')
        self.act(dect[:n, 0:32], ps[0][:n, 0:32], AF.Exp)
        self.tt(wend[:n, 0:32], ps[0][:n, 32:64], btsb[:n, 0:32], ALU.subtract)
        self.act(wend[:n, 0:32], wend[:n, 0:32], AF.Exp)
        if kind != 's':
            self.mm(ps[0][:, 64:96], self.cfv('ones', n), a)
            self.act(decS[:, 0:32], ps[0][:, 64:96], AF.Exp)
        S = self.S
        pb6, pb7 = self.psb(6), self.psb(7)
        xsTM = self.xdt
        for g in range(16):
            pv = pb6 if g < 8 else pb7
            self.tr(pv[:n, (g % 8) * 128:(g % 8 + 1) * 128], self.xbcT[:, g, cols], identb)
        self.act(xsTM[:n, 0:1024], pb6[:n, 0:1024])
        self.act(xsTM[:n, 1024:2048], pb7[:n, 0:1024])
        S.alias_phase([self.ynTM], [self.xsDT])
        for half in range(2):
            for g in range(8):
                self.ts(self.xsDT[:, g, 0:n], self.xbcT[:, 8 * half + g, cols], self.Dfm[:, 8 * half + g:8 * half + g + 1], ALU.mult)
            pv = pb6 if half == 0 else pb7
            for g in range(8):
                self.tr(pv[:n, g * 128:(g + 1) * 128], self.xsDT[:, g, 0:n], identb)
            self.cp(self.xsD[:n, 1024 * half:1024 * half + 1024], pv[:n, 0:1024], 'dve')
        for g in range(4):
            self.tr(pb6[:n, g * 128:(g + 1) * 128], self.xbcT[:, 16 + g, cols], identb)
        self.act(self.BTM[:n, :], pb6[:n, 0:512])
        dtw = sm[13]
        self.tt(dtw[:n, 0:32], dt, wend[:n, 0:32], ALU.mult)
        S.alias_phase([self.xsDT], [self.ynTM])
        if kind == 's':
            self.ssd_sample_states(tl, ch, dtw)
        S.alias_phase([self.ynTM], self.MTh[1])
        ssq = sm[9]
        self.memset(ssq[:n, 0:4], 0.0)
        def stageA(g):
            MTb = self.MTh[g % 2]
            for j in range(8):
                self.act(self.LAh[j][:n, :n], LS, AF.Copy, scale=self.dta[:n, s, 32 + 8 * g + j:33 + 8 * g + j])
            for j in range(8):
                o = ps[1 + j // 4][:n, (j % 4) * 128:(j % 4) * 128 + n]
                self.mm(o, self.LAh[j][:n, :n], U, start=True, stop=False)
                self.mm(o, identb[:n, :n], MT, start=False, stop=True)
            for half in range(2):
                self.act(self.LTh[half][:n, :, :n],
                         ps[1 + half][:n, :].rearrange('p (h t) -> p h t', h=4)[:, :, 0:n], AF.Exp)
            self.mm(ps[3][:n, 0:n], self.xbcT[:, 16 + g, cols], self.xbcT[:, 20 + g, cols])
            for j in range(8):
                self.stt(MTb[j][:n, :n], self.LTh[j // 4][:n, j % 4, :n], self.dta[:n, s, 8 * g + j:8 * g + j + 1], ps[3][:n, 0:n], ALU.mult, ALU.mult)

        def stageB(g):
            MTb = self.MTh[g % 2]
            for j in range(8):
                h = 8 * g + j
                self.mm(ps[4][:n, j * 64:(j + 1) * 64], MTb[j][:n, :n], xsTM[:n, h * 64:(h + 1) * 64])
            t1 = self.t1
            if kind != 's':
                self.mm(ps[5][:n, 0:512], self.xbcT[:, 20 + g, cols], self.STb[:, 512 * g:512 * g + 512])
                for j in range(4):
                    self.act(t1[:n, j * 64:(j + 1) * 64], ps[5][:n, j * 64:(j + 1) * 64], AF.Copy, scale=dect[:n, 8 * g + j:8 * g + j + 1])
                self.tt(t1[:n, 256:512].rearrange('p (h d) -> p d h', d=64), ps[5][:n, 256:512].rearrange('p (h d) -> p d h', d=64),
                        View(dect, dect.t[:n, 8 * g + 4:8 * g + 8].unsqueeze(1).to_broadcast([n, 64, 4])), ALU.mult)
            else:
                self.cp(t1[:n, :], self.ysi[:n, 512 * g:512 * g + 512], 'dve')
            self.tt(t1[:n, :], t1[:n, :], ps[4][:n, 0:512], ALU.add)
            self.tt(t1[:n, :], t1[:n, :], self.xsD[:n, 512 * g:512 * g + 512], ALU.add)
            self.tt(self.yz[:n, 512 * g:512 * g + 512], t1[:n, :], self.zs[s][:n, 512 * g:512 * g + 512], ALU.mult)

        stageA(0)
        for g in range(4):
            if g + 1 < 4:
                stageA(g + 1)
            stageB(g)
        S.alias_phase(self.MTh[1], [self.ynTM])
        for g in range(4):
            self.act(self.ynTM[:n, 512 * g:512 * g + 512], self.yz[:n, 512 * g:512 * g + 512], AF.Square, accum=ssq[:n, g:g + 1])
        rs = sm[10]
        self.act(rs[:n, 0:4], ssq[:n, 0:4], AF.Ln, scale=0.25 / 512.0, bias=self.cst[:n, 0:1])
        self.act(rs[:n, 0:4], rs[:n, 0:4], AF.Exp, scale=-0.5)
        self.ts(rs[:n, 0:4], rs[:n, 0:4], 0.5, ALU.mult)
        for g in range(4):
            self.ts(self.ynTM[:n, 512 * g:512 * g + 512], self.yz[:n, 512 * g:512 * g + 512], rs[:n, g:g + 1], ALU.mult)
        for k in range(16):
            pv = pb6 if k < 8 else pb7
            self.tr(pv[:, (k % 8) * n:(k % 8 + 1) * n], self.ynTM[:n, k * 128:(k + 1) * 128], identb[:n, :n])
        for half in range(2):
            pv = (pb6 if half == 0 else pb7)[:, 0:8 * n].rearrange('p (k t) -> p k t', k=8)
            self.cp(self.ygT[:, 8 * half:8 * half + 8, cols], pv, 'dve' if half == 0 else 'act')
        if kind != 's':
            S.alias_phase([self.ynTM], self.MTh[1])
            for g in range(4):
                hs = slice(8 * g, 8 * g + 8)
                bank = ps[3 + 2 * (g % 2)]
                Bs = self.MTh[g % 2]
                for j in range(8):
                    h = 8 * g + j
                    self.ts(Bs[j][:n, :], self.BTM[:n, g * 128:(g + 1) * 128], dtw[:n, h:h + 1], ALU.mult)
                for j in range(8):
                    h = 8 * g + j
                    self.mm(bank[:, j * 64:(j + 1) * 64], Bs[j][:n, :], xsTM[:n, h * 64:(h + 1) * 64])
                if g % 2 == 0:
                    for j in range(8):
                        c0_ = 512 * g + 64 * j
                        self.act(self.STf[:, c0_:c0_ + 64], self.STf[:, c0_:c0_ + 64], AF.Copy, scale=decS[:, 8 * g + j:8 * g + j + 1])
                else:
                    sv = self.STf[:, 512 * g:512 * g + 512].rearrange('p (h d) -> p d h', d=64)
                    self.tt(sv, sv, View(decS, decS.t[:, hs].unsqueeze(1).to_broadcast([128, 64, 8])), ALU.mult)
                self.tt(self.STf[:, 512 * g:512 * g + 512], self.STf[:, 512 * g:512 * g + 512], bank[:, 0:512], ALU.add)
            S.alias_phase(self.MTh[1], [self.ynTM])
            self.cp(self.STb.ap(), self.STf.ap(), 'act')
            if ch.final:
                identf = self.cfv('ident')
                for j in range(16):
                    bank = ps[1 + (j // 4) % 2]
                    self.tr(bank[:, (j % 4) * 128:(j % 4 + 1) * 128], self.STf[:, j * 128:(j + 1) * 128], identf)
                    if j % 4 == 3:
                        stg = self.LA[:, 4 * ((j // 4) % 2):4 * ((j // 4) % 2) + 4, :]
                        self.S.alias_phase(self.LAh, [self.LA])
                        self.cp(stg, bank[:, 0:512].rearrange('p (j n) -> p j n', j=4), 'act')
                        q = j // 4
                        self.dma(O['p_ssm'][512 * q:512 * q + 512, :].rearrange('(j p) n -> p j n', p=128), stg)
                self.S.alias_phase([self.LA], self.LAh)

    def ssd_sample_states(self, tl, ch, wend):
        I, O, ps, sm, S = self.I, self.O, self.ps, self.sm, self.S
        n, s = 64, ch.slot
        identf = self.cfv('ident')
        RS, BM = self.cfv('RS', 64), self.cfv('BM', 64)
        CM = self.cbv('CM').rearrange('p (b j) -> p b j', b=16)
        dect = sm[6]
        S.alias_phase(self.grpArena, self.grpArena2)
        S.alias_phase(self.LTh + self.MTh[0] + [self.t1, self.yz], [self.S0b[1]])
        blsb = sm[11]
        self.cp(blsb[:n, 0:32], ps[0][:n, 32:64], 'act')
        bl3 = blsb[:n, 0:32].rearrange('p (j r) -> p j r', r=2)
        for r in range(2):
            self.tt(self.Rr[:n, r, :].rearrange('p (b j) -> p b j', b=16),
                    View(blsb, bl3.ap[:, :, r].unsqueeze(1).to_broadcast([n, 16, 16])),
                    View(self.cf, RS.ap.unsqueeze(2).to_broadcast([n, 16, 16])), ALU.mult, eng='pool')
        self.mm(ps[0][:, 256:512], self.cfv('H0', 64), self.Rr[:n, 0, :], start=True, stop=False)
        self.mm(ps[0][:, 256:512], self.cfv('H1', 64), self.Rr[:n, 1, :], start=False, stop=True)
        self.act(self.decP.ap(), ps[0][:, 256:512], AF.Exp)
        wxA = self.ynTM
        self.tt(wxA[:n, :].rearrange('p (h d) -> p d h', d=64), self.xdt[:n, :].rearrange('p (h d) -> p d h', d=64),
                View(wend, wend.t[:n, 0:32].unsqueeze(1).to_broadcast([n, 64, 32])), ALU.mult)
        for b in range(16):
            Sb = self.S0b[b % 2]
            cm = self.CTmb[b % 2]
            self.dma(Sb.ap(), I['s_ssm'][b].rearrange('(j p) n -> p j n', p=128))
            self.tt(cm.ap(), self.xbcT[:, 20:24, ch.col0:ch.col0 + 64], View(self.cb, CM.ap[:, b, :].unsqueeze(1).to_broadcast([128, 4, 64])), ALU.mult)
            self.ts(self.wxm[:n, :], wxA[:n, :], BM[:, b:b + 1], ALU.mult)
            for q in range(4):
                bank = ps[5 + q % 2]
                for i in range(4):
                    self.tr(bank[:, i * 128:(i + 1) * 128], Sb[:, 4 * q + i, :], identf)
                self.cp(self.SbT[:, 512 * q:512 * q + 512], bank[:, 0:512], 'act')
            for g in range(4):
                self.mm(ps[1 + g][:n, 0:512], cm[:, g, :], self.SbT[:, 512 * g:512 * g + 512], start=(b == 0), stop=(b == 15))
            for q in range(4):
                bank = ps[7] if q % 2 == 0 else ps[0]
                for i in range(4):
                    j = 4 * q + i
                    self.mm(bank[:, i * 128:(i + 1) * 128], self.wxm[:n, j * 128:(j + 1) * 128], self.BTM[:n, q * 128:(q + 1) * 128])
                for i in range(4):
                    j = 4 * q + i
                    self.stt(Sb[:, j, :], Sb[:, j, :], self.decP[:, 16 * b + j:16 * b + j + 1], bank[:, i * 128:(i + 1) * 128], ALU.mult, ALU.add)
            self.dma(O['o_ssm'][b].rearrange('(j p) n -> p j n', p=128), Sb.ap())
        for g in range(4):
            self.tt(self.ysi[:n, 512 * g:512 * g + 512].rearrange('p (h d) -> p d h', d=64),
                    ps[1 + g][:n, 0:512].rearrange('p (h d) -> p d h', d=64),
                    View(dect, dect.t[:n, 8 * g:8 * g + 8].unsqueeze(1).to_broadcast([n, 64, 8])), ALU.mult)
        S.alias_phase([self.S0b[1]], self.LTh + self.MTh[0] + [self.t1, self.yz])


_CACHE = {}


def _get_kernel():
    if 'k' not in _CACHE:
        _CACHE['k'] = K()
    return _CACHE['k']


def make_in_maps(kb, inputs):
    f = lambda a: np.ascontiguousarray(np.asarray(a, dtype=np.float32))
    xp, xs = f(inputs['x_prompt']), f(inputs['x_sample'])
    shared = {'meta': f(inputs['meta_tokens']), 'cf': kb.cf_np, 'cb': kb.cb_np,
              'ln0_g': f(inputs['ln0_g']), 'ln0_b': f(inputs['ln0_b']),
              'b_if': f(inputs['b_mlstm_if'])[0], 'w_mconv': f(inputs['w_mlstm_conv'])[0],
              'b_mconv': f(inputs['b_mlstm_conv']), 'mnorm_g': f(inputs['mlstm_norm_g']),
              'w_sconv': f(inputs['w_ssm_conv'])[0], 'b_sconv': f(inputs['b_ssm_conv']),
              'dt_bias': f(inputs['ssm_dt_bias'])[0], 'A_log': f(inputs['ssm_A_log'])[0],
              'ssm_D': f(inputs['ssm_D'])[0], 'snorm_g': f(inputs['ssm_norm_g']),
              'ln1_g': f(inputs['ln1_g'])[0], 'ln1_b': f(inputs['ln1_b'])[0],
              'w_fconv': f(inputs['w_ffn_conv'])[0], 'b_fconv': f(inputs['b_ffn_conv']),
              'ln2_g': f(inputs['ln2_g'])[0], 'ln2_b': f(inputs['ln2_b'])[0],
              'w_in': f(inputs['w_in'])[0], 'w_proj_a': f(inputs['w_proj_a'])[0],
              'w_proj_b': f(inputs['w_proj_b'])[0], 'w_out': f(inputs['w_out'])[0],
              'w_up': f(inputs['w_up'])[0], 'w_down': f(inputs['w_down'])[0]}
    maps = []
    for c in range(8):
        b = slice(16 * c, 16 * c + 16)
        m = dict(shared)
        m['xp'] = xp[c]
        m['xs'] = xs[b].reshape(64, D)
        m['s_mconv'] = f(inputs['state_mlstm_conv'])[0, b].reshape(48, 1024)
        m['s_C'] = f(inputs['state_mlstm_C'])[0, b]
        m['s_n'] = f(inputs['state_mlstm_n'])[0, b].reshape(64, 128)
        m['s_m'] = f(inputs['state_mlstm_m'])[0, b]
        m['s_sconv'] = f(inputs['state_ssm_conv'])[0, b].reshape(48, 3072)
        m['s_ssm'] = f(inputs['state_ssm'])[0, b].reshape(16, 2048, 128)
        m['s_fconv'] = f(inputs['state_ffn_conv'])[0, b].reshape(32, 2 * DFF)
        maps.append(m)
    return maps


def kernel(**inputs):
    kb = _get_kernel()
    maps = make_in_maps(kb, inputs)
    res = run_bass_kernel_spmd(kb.nc, maps, core_ids=list(range(8)))
    R = res.results
    cat = lambda k: np.stack([np.asarray(r[k], dtype=np.float32) for r in R])
    y_p = cat('y_p')
    y_s = cat('y_s').reshape(128, 4, D)
    p_mconv = cat('p_mconv')[None]
    p_C = cat('p_C')[None]
    p_n = cat('p_n')[None]
    p_m = cat('p_m').reshape(8, 4)[None]
    p_sconv = cat('p_sconv')[None]
    p_ssm = cat('p_ssm').reshape(8, 32, 64, 128)[None]
    p_fconv = cat('p_fconv')[None]
    s_mconv = cat('o_mconv').reshape(128, 3, 1024)[None]
    s_C = cat('o_C').reshape(128, 4, 128, 256)[None]
    s_n = cat('o_n').reshape(128, 4, 128)[None]
    s_m = cat('o_m').reshape(128, 4)[None]
    s_sconv = cat('o_sconv').reshape(128, 3, 3072)[None]
    s_ssm = cat('o_ssm').reshape(128, 32, 64, 128)[None]
    s_fconv = cat('o_fconv').reshape(128, 2, 2 * DFF)[None]
    return (y_p, y_s, p_mconv, p_C, p_n, p_m, p_sconv, p_ssm, p_fconv,
            s_mconv, s_C, s_n, s_m, s_sconv, s_ssm, s_fconv)
```

```python
import numpy as np
import ml_dtypes
import concourse.bass as bass
import concourse.mybir as mybir
from concourse.bass_utils import run_bass_kernel_spmd

F32 = mybir.dt.float32
BF16 = mybir.dt.bfloat16
ALU = mybir.AluOpType
AF = mybir.ActivationFunctionType
AX = mybir.AxisListType

D = 1024
DIN = 10280
DFF = 2816
NEG = -30000.0
ALPHA = 2.0 ** 0.25
LN_EPS = 1e-5
RMS_EPS = 1e-5
QSCALE = 128.0 ** -0.5


class Buf:
    def __init__(self, name, t, space):
        self.name = name
        self.t = t
        self.space = space
        self.last_w = None
        self.readers = []
        self.sem_in = None
        self.cnt_in = 0
        self.sem_out = None
        self.cnt_out = 0

    def __getitem__(self, idx):
        return View(self, self.t[idx])

    def ap(self):
        return View(self, self.t[:] if self.space != 'dram' else self.t)


class View:
    def __init__(self, buf, ap):
        self.buf = buf
        self.ap = ap

    def __getitem__(self, idx):
        return View(self.buf, self.ap[idx])

    def rearrange(self, *a, **k):
        return View(self.buf, self.ap.rearrange(*a, **k))

    def bc(self, axis, shape):
        return View(self.buf, self.ap.unsqueeze(axis).to_broadcast(list(shape)))

    def bitcast(self, dt):
        return View(self.buf, self.ap.bitcast(dt))


def _bufs(vs):
    out = []
    for v in vs:
        if v is None or isinstance(v, (int, float)):
            continue
        b = v.buf if isinstance(v, View) else v
        if b not in out:
            out.append(b)
    return out


class Sched:
    ENGS = ('pe', 'act', 'dve', 'pool', 'sp')

    def __init__(self, nc):
        self.nc = nc
        self.sem = {e: nc.alloc_semaphore('sem_' + e) for e in self.ENGS}
        self.cnt = {e: 0 for e in self.ENGS}
        self.ops = {e: [] for e in self.ENGS}
        self.seen = {e: {} for e in self.ENGS}
        self.final_tokens = []
        self.sb_off = 16512
        self.sb_end = 229376
        self.nsem = 5

    def sbuf(self, name, shape, dtype, at=None):
        nbytes = int(np.prod(shape[1:])) * (2 if dtype == BF16 else 4)
        nbytes = (nbytes + 31) // 32 * 32
        if at is None:
            at = self.sb_off
            self.sb_off += nbytes
            assert self.sb_off <= self.sb_end, ('SBUF overflow', name, self.sb_off)
        t = self.nc.alloc_sbuf_tensor_at(name, list(shape), dtype, offset=at)
        b = Buf(name, t, 'sbuf')
        b.off = at
        b.nbytes = nbytes
        return b

    def psum(self, name, shape, dtype=F32):
        t = self.nc.alloc_psum_tensor(name, list(shape), dtype)
        return Buf(name, t, 'psum')

    def dram(self, name, shape, dtype, kind):
        t = self.nc.dram_tensor(name, list(shape), dtype, kind=kind)
        return Buf(name, t.ap(), 'dram')

    def alias_phase(self, old, new):
        toks = []
        for b in old:
            if b.last_w is not None:
                toks.append(b.last_w)
            toks.extend(b.readers)
        for b in new:
            b.readers = list(b.readers) + toks

    def _need(self, eng, waits, tok):
        sem, val, teng = tok
        key = id(sem)
        if self.seen[eng].get(key, 0) >= val:
            return
        if key not in waits or waits[key][1] < val:
            waits[key] = (sem, val)

    def _deps(self, eng, reads, writes):
        waits = {}
        for b in reads:
            tok = b.last_w
            if tok is not None and not (tok[2] == eng and eng == 'pe'):
                self._need(eng, waits, tok)
            if b.space == 'psum':
                for r in b.readers:
                    if r[2] != eng:
                        self._need(eng, waits, r)
        for b in writes:
            tok = b.last_w
            if tok is not None and not (tok[2] == eng and eng == 'pe'):
                self._need(eng, waits, tok)
            for r in b.readers:
                if not (r[2] == eng and eng == 'pe'):
                    self._need(eng, waits, r)
        for key, (sem, val) in waits.items():
            self.seen[eng][key] = val
        return list(waits.values())

    def op(self, eng, fn, reads=(), writes=()):
        reads = _bufs(reads)
        writes = _bufs(writes)
        waits = self._deps(eng, reads, writes)
        self.cnt[eng] += 1
        tok = (self.sem[eng], self.cnt[eng], eng)
        self.ops[eng].append((waits, fn, (self.sem[eng], 1)))
        for b in writes:
            b.last_w = tok
            b.readers = []
        for b in reads:
            if b not in writes:
                b.readers.append(tok)
        return tok

    def dma(self, q, out, in_, **kw):
        ob, ib = out.buf, in_.buf
        waits = self._deps(q, [ib], [ob])
        kind = 'sw' if q == 'pool' else 'hw'
        if ob.space != 'dram':
            tab = ob.__dict__.setdefault('sems_in', {})
            if kind not in tab:
                tab[kind] = [self.nc.alloc_semaphore('din_%s_%s' % (kind, ob.name)), 0]
                self.nsem += 1
            tab[kind][1] += 16
            sem, val = tab[kind]
        else:
            tab = ib.__dict__.setdefault('sems_out', {})
            if kind not in tab:
                tab[kind] = [self.nc.alloc_semaphore('dout_%s_%s' % (kind, ib.name)), 0]
                self.nsem += 1
            tab[kind][1] += 16
            sem, val = tab[kind]
        tok = (sem, val, 'dma')
        oap, iap = out.ap, in_.ap

        def fn(e, oap=oap, iap=iap, kw=kw):
            return e.dma_start(out=oap, in_=iap, **kw)
        self.ops[q].append((waits, fn, (sem, 16)))
        ob.last_w = tok
        ob.readers = []
        ib.readers.append(tok)
        if ob.space == 'dram':
            self.final_tokens.append(tok)
        return tok

    def emit(self):
        nc = self.nc
        last = {}
        for sem, val, _ in self.final_tokens:
            k = id(sem)
            if k not in last or last[k][1] < val:
                last[k] = (sem, val)
        fin = list(last.values())
        eng_obj = {'pe': 'tensor', 'act': 'scalar', 'dve': 'vector', 'pool': 'gpsimd', 'sp': 'sync'}
        with nc.Block() as block:
            def mk(eng):
                def body(e):
                    for waits, fn, inc in self.ops[eng]:
                        for sem, val in waits:
                            e.wait_ge(sem, val)
                        fn(e).then_inc(inc[0], inc[1])
                    if eng == 'sp':
                        for sem, val in fin:
                            e.wait_ge(sem, val)
                return body
            for eng, attr in eng_obj.items():
                getattr(block, attr)(mk(eng))


def _const_tables():
    p = np.arange(128)[:, None]
    j = np.arange(128)[None, :]
    f = {}
    f['ident'] = (p == j)
    f['ones'] = np.ones((128, 128))
    f['U'] = (p <= j)
    f['LS'] = (p > j)
    sb = (p // 4 == j // 4) & (p < 64) & (j < 64)
    f['Us'] = ((p <= j) & sb)[:, :64]
    f['LSs'] = ((p > j) & sb)[:, :64]
    f['BOs'] = sb[:, :64]
    f['SELp'] = np.repeat(p == 127, 128, axis=1)
    f['SELm'] = np.repeat(p == 15, 128, axis=1)
    b16 = np.arange(16)[None, :]
    f['RS'] = (p == 4 * b16 + 3)
    f['BM'] = (p // 4 == b16) & (p < 64)
    f['BMT'] = ((p < 16) & (j // 4 == p))[:, :64]
    f['LSEL'] = ((j == 4 * (p // 4) + 3) & (p < 64))[:, :64]
    f['H0'] = np.repeat(p < 64, 128, axis=1) & (j < 64)
    f['H1'] = np.repeat(p < 64, 128, axis=1) & (j >= 64)
    cf_off, cols = {}, []
    o = 0
    for k, v in f.items():
        cf_off[k] = (o, v.shape[1])
        o += v.shape[1]
        cols.append(v.astype(np.float32))
    cf = np.concatenate(cols, axis=1)
    g = {}
    g['identb'] = (p == j).astype(np.float32)
    g['onesb'] = np.ones((128, 128), np.float32)
    g['M'] = np.where(j <= p, 0.0, NEG)
    g['MT'] = np.where(p <= j, 0.0, NEG)
    g['Ms'] = np.where((j <= p) & sb, 0.0, NEG)[:, :64]
    g['MTs'] = np.where((p <= j) & sb, 0.0, NEG)[:, :64]
    jj = np.arange(64)[None, None, :]
    bb = np.arange(16)[None, :, None]
    g['CM'] = np.broadcast_to((jj // 4 == bb), (128, 16, 64)).reshape(128, 1024).astype(np.float32)
    cb_off, cols = {}, []
    o = 0
    for k, v in g.items():
        cb_off[k] = (o, v.shape[1])
        o += v.shape[1]
        cols.append(np.asarray(v, np.float32))
    cbm = np.concatenate(cols, axis=1).astype(ml_dtypes.bfloat16)
    return cf, cf_off, cbm, cb_off


class Chunk:
    def __init__(self, slot, col0, n, kind, row0=0):
        self.slot, self.col0, self.n, self.kind, self.row0 = slot, col0, n, kind, row0


class Tile:
    def __init__(self, name, T, chunks, segs):
        self.name, self.T, self.chunks, self.segs = name, T, chunks, segs


W_SHAPES = {'w_in': (D, DIN), 'w_proj_a': (D, D), 'w_proj_b': (2 * D, D), 'w_out': (D, D),
            'w_up': (D, 2 * DFF), 'w_down': (DFF, D)}


def tile_blocks():
    bl = []
    for c in range(0, 3072, 256):
        bl.append(('w_in', 0, 8, [(c, 256)]))
    bl.append(('w_in', 0, 8, [(3072, 8), (8200, 32)]))
    for c in range(3080, 5128, 256):
        bl.append(('w_in', 0, 8, [(c, 256)]))
    for c in range(5128, 8200, 256):
        bl.append(('w_in', 0, 8, [(c, 256)]))
    for c in range(8232, 10280, 256):
        bl.append(('w_in', 0, 8, [(c, 256)]))
    for j in range(4):
        bl.append(('w_proj_a', 0, 8, [(256 * j, 256)]))
        bl.append(('w_proj_b', 0, 8, [(256 * j, 256)]))
        bl.append(('w_proj_b', 8, 8, [(256 * j, 256)]))
    for j in range(4):
        bl.append(('w_out', 0, 8, [(256 * j, 256)]))
    for j in range(11):
        bl.append(('w_up', 0, 8, [(256 * j, 256)]))
        bl.append(('w_up', 0, 8, [(DFF + 256 * j, 256)]))
    for j in range(4):
        for k0, nk in ((0, 8), (8, 8), (16, 6)):
            bl.append(('w_down', k0, nk, [(256 * j, 256)]))
    return bl


class K:
    def __init__(self, debug=None, tiles=('T0', 'T1', 'T2', 'T3', 'T4')):
        self.debug = debug or {}
        self.tile_sel = tuple(tiles)
        self.ntiles = len(self.tile_sel)
        nc = bass.Bass('TRN2', target_bir_lowering=False)
        self.nc = nc
        self.S = S = Sched(nc)
        self.dumps = {}
        cf, self.cfo, cbm, self.cbo = _const_tables()
        self.cf_np, self.cb_np = cf, cbm
        din = lambda n, s, dt=F32: S.dram(n, s, dt, 'ExternalInput')
        dout = lambda n, s: S.dram(n, s, F32, 'ExternalOutput')
        I = self.I = {}
        I['xp'] = din('xp', [2048, D]); I['xs'] = din('xs', [64, D]); I['meta'] = din('meta', [16, D])
        I['s_mconv'] = din('s_mconv', [48, 1024]); I['s_C'] = din('s_C', [16, 4, 128, 256])
        I['s_n'] = din('s_n', [64, 128]); I['s_m'] = din('s_m', [16, 4])
        I['s_sconv'] = din('s_sconv', [48, 3072]); I['s_ssm'] = din('s_ssm', [16, 2048, 128])
        I['s_fconv'] = din('s_fconv', [32, 2 * DFF])
        I['cf'] = din('cf', list(cf.shape)); I['cb'] = din('cb', list(cbm.shape), BF16)
        for n, s in (('ln0_g', [D]), ('ln0_b', [D]), ('b_if', [8]), ('w_mconv', [4, 1024]), ('b_mconv', [1, 1024]),
                     ('mnorm_g', [1, 1024]), ('w_sconv', [4, 3072]), ('b_sconv', [1, 3072]), ('dt_bias', [32]),
                     ('A_log', [32]), ('ssm_D', [32]), ('snorm_g', [1, 2048]), ('ln1_g', [D]), ('ln1_b', [D]),
                     ('w_fconv', [3, 2 * DFF]), ('b_fconv', [1, 2 * DFF]), ('ln2_g', [D]), ('ln2_b', [D])):
            I[n] = din(n, s)
        for n, s in W_SHAPES.items():
            I[n] = din(n, list(s))
        O = self.O = {}
        O['y_p'] = dout('y_p', [2048, D]); O['y_s'] = dout('y_s', [64, D])
        O['p_mconv'] = dout('p_mconv', [3, 1024]); O['p_C'] = dout('p_C', [4, 128, 256])
        O['p_n'] = dout('p_n', [4, 128]); O['p_m'] = dout('p_m', [1, 4])
        O['p_sconv'] = dout('p_sconv', [3, 3072]); O['p_ssm'] = dout('p_ssm', [2048, 128])
        O['p_fconv'] = dout('p_fconv', [2, 2 * DFF])
        O['o_mconv'] = dout('o_mconv', [48, 1024]); O['o_C'] = dout('o_C', [16, 4, 128, 256])
        O['o_n'] = dout('o_n', [64, 128]); O['o_m'] = dout('o_m', [16, 4])
        O['o_sconv'] = dout('o_sconv', [48, 3072]); O['o_ssm'] = dout('o_ssm', [16, 2048, 128])
        O['o_fconv'] = dout('o_fconv', [32, 2 * DFF])
        self.rr = {}
        self.build()
        S.emit()

    def rot(self, key, n):
        i = self.rr.get(key, 0)
        self.rr[key] = i + 1
        return i % n

    def mm(self, out, lhsT, rhs, start=True, stop=True):
        self.S.op('pe', lambda e: e.matmul(out.ap, lhsT=lhsT.ap, rhs=rhs.ap, start=start, stop=stop),
                  reads=[lhsT, rhs], writes=[out])

    def tr(self, out, in_, ident):
        self.S.op('pe', lambda e: e.transpose(out=out.ap, in_=in_.ap, identity=ident.ap),
                  reads=[in_, ident], writes=[out])

    def act(self, out, in_, func=AF.Copy, bias=None, scale=None, accum=None):
        kw = {}
        if bias is not None:
            kw['bias'] = bias.ap if isinstance(bias, View) else bias
        if scale is not None:
            kw['scale'] = scale.ap if isinstance(scale, View) else scale
        if accum is not None:
            kw['accum_out'] = accum.ap
        self.S.op('act', lambda e: e.activation(out=out.ap, in_=in_.ap, func=func, **kw),
                  reads=[in_, bias, scale], writes=[out, accum])

    def tt(self, out, a, b, op, eng='dve'):
        self.S.op(eng, lambda e: e.tensor_tensor(out=out.ap, in0=a.ap, in1=b.ap, op=op),
                  reads=[a, b], writes=[out])

    def ts(self, out, a, s1, op0, s2=None, op1=None, eng='dve', accum=None):
        v1 = s1.ap if isinstance(s1, View) else s1
        v2 = s2.ap if isinstance(s2, View) else s2
        kw = {}
        if op1 is not None:
            kw['op1'] = op1
        if accum is not None:
            kw['accum_out'] = accum.ap
        self.S.op(eng, lambda e: e.tensor_scalar(out=out.ap, in0=a.ap, scalar1=v1, scalar2=v2, op0=op0, **kw),
                  reads=[a, s1, s2], writes=[out, accum])

    def stt(self, out, a, s, b, op0, op1, eng='dve'):
        v = s.ap if isinstance(s, View) else s
        self.S.op(eng, lambda e: e.scalar_tensor_tensor(out=out.ap, in0=a.ap, scalar=v, in1=b.ap, op0=op0, op1=op1),
                  reads=[a, s, b], writes=[out])

    def cp(self, out, in_, eng='dve'):
        if eng == 'act':
            return self.act(out, in_)
        self.S.op(eng, lambda e: e.tensor_copy(out=out.ap, in_=in_.ap), reads=[in_], writes=[out])

    def memset(self, out, val, eng='dve'):
        self.S.op(eng, lambda e: e.memset(out.ap, val), writes=[out])

    def rmax(self, out, in_, eng='dve'):
        self.S.op(eng, lambda e: e.tensor_reduce(out=out.ap, in_=in_.ap, axis=AX.X, op=ALU.max),
                  reads=[in_], writes=[out])

    def dma(self, out, in_, q='sp'):
        if out.buf.space == 'dram' and q == 'sp' and not getattr(self, 'pass0', False):
            q = 'pool'
        self.S.dma(q, out, in_)

    def dump(self, name, view, shape):
        if name not in self.debug:
            return
        d = self.S.dram('dbg_' + name, list(shape), view.ap.dtype, 'ExternalOutput')
        self.dumps[name] = d
        self.dma(d.ap(), view)

    def cfv(self, name, rows=128, cols=None):
        o, w = self.cfo[name]
        cols = w if cols is None else cols
        return self.cf[:rows, o:o + cols]

    def cbv(self, name, rows=128, cols=None):
        o, w = self.cbo[name]
        cols = w if cols is None else cols
        return self.cb[:rows, o:o + cols]

    def ws_init(self):
        S = self.S
        self.wlist = tile_blocks()
        self.nbt = len(self.wlist)
        self.wblocks = self.wlist * self.ntiles
        self.wring = [S.sbuf(f'wring{i}', [128, 8, 256], BF16) for i in range(6)]
        self.wscr = [S.dram(f'wscr{j}', [128, 8, 256], BF16, 'Internal') for j in range(self.nbt)] if self.ntiles > 1 else None
        self.w_loaded = 0
        self.w_next = 0

    def _w_load0(self, i):
        name, k0, nk, parts = self.wblocks[i]
        dst = self.wring[i % 6]
        W = self.I[name]
        c = 0
        for (c0, n) in parts:
            src = View(W, W.t[k0 * 128:(k0 + nk) * 128, c0:c0 + n].rearrange('(k p) c -> p k c', p=128))
            self.S.dma('pool', dst[:, 0:nk, c:c + n], src)
            c += n
        n = c
        if name in ('w_proj_a', 'w_proj_b'):
            for k in range(nk):
                gcol = self.cwm[:, k0 + k, 5:6] if name == 'w_proj_a' else self.sng[:, k0 + k:k0 + k + 1]
                self.ts(dst[:, k, 0:n], dst[:, k, 0:n], gcol, ALU.mult)
        if self.wscr is not None:
            self.S.dma('sp', self.wscr[i][:, 0:nk, 0:n], dst[:, 0:nk, 0:n])

    def _w_ringload(self, i):
        name, k0, nk, parts = self.wblocks[i]
        n = sum(p[1] for p in parts)
        dst = self.wring[i % 6]
        self.dma(dst[:, 0:nk, 0:n], self.wscr[i % self.nbt][:, 0:nk, 0:n], q='sp')

    def wnext(self):
        i = self.w_next
        nb = len(self.wblocks)
        self.w_next += 1
        while self.w_loaded < min(nb, i + 6):
            if self.w_loaded < self.nbt:
                self._w_load0(self.w_loaded)
            else:
                self._w_ringload(self.w_loaded)
            self.w_loaded += 1
        return self.wring[i % 6], self.wblocks[i]

    def build(self):
        S = self.S
        sb = S.sbuf
        ncf, ncb = self.cf_np.shape[1], self.cb_np.shape[1]
        self.cf = sb('cf', [128, ncf], F32)
        self.cb = sb('cb', [128, ncb], BF16)
        self.lnc = sb('lnc', [128, 2, D], F32)
        self.bif_b = sb('bif_b', [128, 8], F32)
        self.dtb_b = sb('dtb_b', [128, 32], F32)
        self.A_b = sb('A_b', [128, 32], F32)
        self.D_b = sb('D_b', [128, 32], F32)
        self.Dfm = sb('Dfm', [128, 16], F32)
        self.cwm = sb('cwm', [128, 8, 6], F32)
        self.cws = sb('cws', [128, 24, 5], F32)
        self.sng = sb('sng', [128, 16], F32)
        self.cwf = sb('cwf', [128, 44, 4], F32)
        self.ws_init()
        self.xr = [sb(f'xr{i}', [128, D], F32) for i in range(4)]
        self.zs = [None] * 4
        self.xnT = sb('xnT', [128, 8, 512], BF16)
        self.hgT = sb('hgT', [128, 8, 512], BF16)
        self.ygT = sb('ygT', [128, 16, 512], BF16)
        self.Cf = sb('Cf', [128, 4, 256], F32); self.Cb = sb('Cb', [128, 4, 256], BF16)
        self.nf = sb('nf', [128, 4], F32); self.nb = sb('nb', [128, 4], BF16)
        self.m_b = sb('m_b', [128, 4], F32)
        self.STf = sb('STf', [128, 2048], F32); self.STb = sb('STb', [128, 2048], BF16)
        self.cq = sb('cq', [128, 8, 3], F32); self.cx = sb('cx', [128, 24, 3], F32)
        self.cff = sb('cff', [128, 44, 2], F32)
        self.scar = sb('scar', [128, 44 * 16 * 2], F32)
        self.gat = sb('gat', [128, 4, 8], F32)
        self.dta = sb('dta', [128, 4, 64], F32)
        self.ifdt = sb('ifdt', [128, 4, 40], F32)
        self.sm = [sb(f'sm{i}', [128, 32], F32) for i in range(16)]
        self.xb16 = sb('xb16', [128, D], BF16)
        self.lnsc = [sb(f'lnsc{i}', [128, 16], F32) for i in range(4)]
        self.cst = sb('cst', [128, 8], F32)
        self.pn_st = sb('pn_st', [128, 128], F32)
        R0 = S.sb_off
        o = R0
        def at(name, shape, dt):
            nonlocal o
            b = sb(name, shape, dt, at=o)
            o += b.nbytes
            return b
        self.cE = [at(f'cE{i}', [128, 520], F32) for i in range(2)]
        self.cacc = [at(f'cacc{i}', [128, 512], F32) for i in range(3)]
        self.cth = [at(f'cth{i}', [128, 512], F32) for i in range(2)]
        self.cacc2 = [at(f'cacc2_{i}', [128, 512], F32) for i in range(2)]
        e1 = o
        o = R0
        self.R1 = at('R1', [128, 4, 128], F32); self.R2 = at('R2', [128, 4, 128], F32)
        self.R3 = at('R3', [128, 4, 128], F32); self.wT = at('wT', [128, 4, 128], F32)
        self.ST = at('ST', [128, 4, 128], BF16); self.kTM = at('kTM', [128, 4, 128], BF16)
        self.hh = at('hh', [128, 4, 256], F32); self.vw = at('vw', [128, 4, 256], BF16)
        self.hgTM = at('hgTM', [128, D], BF16)
        e2 = o
        o = R0
        self.xdt = at('xdt', [128, 2048], BF16); self.xsD = at('xsD', [128, 2048], BF16)
        self.wx = at('wx', [128, 512], BF16); self.BTM = at('BTM', [128, 512], BF16)
        self.LT = at('LT', [128, 8, 128], BF16); self.MTt = at('MTt', [128, 8, 128], BF16)
        self.t1 = at('t1', [128, 512], F32)
        self.yz = at('yz', [128, 2048], BF16)
        self.ynTM = at('ynTM', [128, 2048], BF16)
        e3 = o
        F0 = max(e1, e2, e3)
        conv_end = F0
        o = F0
        self.qkT = at('qkT', [128, 8, 512], BF16)
        self.v = [at(f'v{i}', [128, D], BF16) for i in range(4)]
        self.oth = [at(f'oth{i}', [128, D], BF16) for i in range(4)]
        a1_end = o
        o = F0
        self.xbcT = at('xbcT', [128, 24, 512], BF16)
        for i in range(2):
            self.zs[i] = at(f'zs{i}', [128, 2048], BF16)
        self.LA = at('LA', [128, 8, 128], F32)
        a2_end = o
        o = F0
        self.gth = at('gth', [128, 16, 512], BF16)
        self.mixT = at('mixT', [128, 8, 512], BF16)
        self.hffT = at('hffT', [128, 22, 512], BF16)
        b_end = o
        S.sb_off = max(a1_end, a2_end, b_end)
        for i in range(2, 4):
            self.zs[i] = sb(f'zs{i}', [128, 2048], BF16)
        self.arenas = [(self.xr[2].off, 2 * self.xr[2].nbytes), (self.zs[2].off, 2 * self.zs[2].nbytes)]
        a0, a1 = self.arenas[0][0], self.arenas[1][0]
        self.C0b = [sb(f'C0b{i}', [128, 4, 256], F32, at=a0 + 4096 * i) for i in range(2)]
        self.C0b16 = [sb(f'C0b16_{i}', [128, 4, 258], BF16, at=a1 + 2080 * i) for i in range(2)]
        self.qmb = [sb(f'qmb{i}', [128, 4, 64], BF16, at=a1 + 4160 + 512 * i) for i in range(2)]
        self.kTMm = [sb(f'kTMm{i}', [128, 4, 128], BF16, at=a1 + 5184 + 1024 * i) for i in range(2)]
        self.n0T = sb('n0T', [128, 64], F32, at=a1 + 7232)
        self.n16 = sb('n16', [128, 64], BF16, at=a1 + 7488)
        self.decS = sb('decS', [128, 64], F32, at=a1 + 7616)
        self.Rm = sb('Rm', [128, 64], F32, at=a1 + 7872)
        self.S0b = [sb('S0b0', [128, 16, 128], F32, at=a0), sb('S0b1', [128, 16, 128], F32, at=self.LT.off)]
        assert self.LT.off + 8192 <= self.ynTM.off
        self.SbT = sb('SbT', [128, 2048], BF16, at=a1)
        self.wxm = sb('wxm', [128, 2048], BF16, at=a1 + 4096)
        self.ysi = sb('ysi', [128, 2048], BF16)
        self.decP = sb('decP', [128, 256], F32)
        self.Rr = sb('Rr', [128, 2, 256], F32)
        self.CTmb = [sb(f'CTmb{i}', [128, 4, 64], BF16) for i in range(2)]
        self.grpArena2 = [self.S0b[0], self.SbT, self.wxm]
        self.xsDT = sb('xsDT', [128, 8, 128], BF16, at=self.ynTM.off)
        self.LAh = [sb(f'LAh{j}', [128, 128], F32, at=self.LA.off + 512 * j) for j in range(8)]
        self.LTh = [sb(f'LTh{j}', [128, 4, 128], BF16, at=self.LT.off + 1024 * j) for j in range(2)]
        self.MTh = [[sb(f'MTh{b}_{j}', [128, 128], BF16, at=base + 256 * j) for j in range(8)]
                    for b, base in enumerate((self.MTt.off, self.ynTM.off + 2048))]
        self.grpArena = self.C0b + self.C0b16 + self.qmb + self.kTMm + [self.n0T, self.n16, self.decS, self.Rm]
        self.grpA1conv = self.cE + self.cacc + self.cth + self.cacc2
        self.grpA1rec = [self.R1, self.R2, self.R3, self.wT, self.ST, self.kTM, self.hh, self.vw, self.hgTM]
        self.grpA1fix = [self.qkT] + self.v + self.oth
        self.grpA2fix = [self.xbcT, self.zs[0], self.zs[1]] + self.LAh
        self.grpA2rec = [self.xdt, self.xsD, self.wx, self.BTM, self.t1, self.yz, self.ynTM] + self.LTh + self.MTh[0]
        self.grpB = [self.gth, self.mixT, self.hffT]
        print('SBUF used', S.sb_off, 'of', S.sb_end, 'R', R0, conv_end - R0, a1_end - R0, a2_end - R0, b_end - R0)
        self.ps = [S.psum(f'ps{i}', [128, 512], F32) for i in range(8)]
        self.setup()
        tiles = self.make_tiles()
        first = True
        for tl in tiles:
            if tl.name in self.tile_sel:
                self.pass0 = first
                self.run_tile(tl, last=(tl.name == 'T4'))
                first = False

    def make_tiles(self):
        def pch(slot, col0, c):
            ch = Chunk(slot, col0, 128, 'p', row0=128 * c)
            ch.final = (c == 15)
            return ch
        m = Chunk(0, 0, 16, 'm'); m.final = False
        tiles = [Tile('T0', 400, [m] + [pch(1 + i, 16 + 128 * i, i) for i in range(3)], [(0, 1, 400, 'p')])]
        for t in range(3):
            tiles.append(Tile(f'T{t + 1}', 512, [pch(i, 128 * i, 3 + 4 * t + i) for i in range(4)], [(0, 1, 512, 'p')]))
        sc = Chunk(1, 128, 64, 's'); sc.final = False
        tiles.append(Tile('T4', 192, [pch(0, 0, 15), sc], [(0, 1, 128, 'p'), (128, 16, 4, 's')]))
        return tiles

    def psb(self, i):
        return self.ps[i].ap().bitcast(BF16)

    def setup(self):
        I = self.I
        self.dma(self.cf.ap(), I['cf'].ap())
        self.dma(self.cb.ap(), I['cb'].ap())
        pb = lambda n: View(I[n], I[n].t.partition_broadcast(128))
        self.dma(self.bif_b.ap(), pb('b_if'))
        self.dma(self.dtb_b.ap(), pb('dt_bias'))
        self.dma(self.A_b.ap(), pb('A_log'))
        self.dma(self.D_b.ap(), pb('ssm_D'))
        self.act(self.A_b.ap(), self.A_b.ap(), AF.Exp)
        self.ts(self.A_b.ap(), self.A_b.ap(), -1.0, ALU.mult)
        D3 = self.D_b.ap().rearrange('p (g r) -> p g r', r=2)
        self.cp(self.Dfm[0:64, :], D3[0:64, :, 0], 'dve')
        self.cp(self.Dfm[64:128, :], D3[64:128, :, 1], 'dve')
        identf = self.cfv('ident')
        stg = self.cacc[0]
        def fm_params(dst, rows, G, scale_groups=None):
            R = sum(r for _, r in rows)
            for g0 in range(0, G, 4):
                gn = min(4, G - g0)
                r0 = 0
                for (nm, nr) in rows:
                    self.dma(stg[r0:r0 + nr, 0:gn * 128], I[nm][:, g0 * 128:(g0 + gn) * 128])
                    r0 += nr
                bank = self.ps[self.rot('setup', 2)]
                for g in range(gn):
                    self.tr(bank[:, g * R:(g + 1) * R], stg[0:R, g * 128:(g + 1) * 128], identf[0:R, 0:R])
                self.cp(dst[:, g0:g0 + gn, :], bank[:, 0:gn * R].rearrange('p (g r) -> p g r', r=R), 'act')
        fm_params(self.cwm, [('w_mconv', 4), ('b_mconv', 1), ('mnorm_g', 1)], 8)
        fm_params(self.cws, [('w_sconv', 4), ('b_sconv', 1)], 24)
        fm_params(self.cwf, [('w_fconv', 3), ('b_fconv', 1)], 44)
        sng3 = self.sng.ap().rearrange('p (g r) -> p g r', r=1)
        fm_params(sng3, [('snorm_g', 1)], 16)
        self.ts(self.cwm[:, :, 0:6], self.cwm[:, :, 0:6], 0.5, ALU.mult)
        self.ts(self.cws.ap(), self.cws.ap(), 0.5, ALU.mult)
        self.ts(self.cwf[:, 0:22, :], self.cwf[:, 0:22, :], 0.5, ALU.mult)
        self.memset(self.cst[:, 0:1], LN_EPS)
        self.memset(self.cst[:, 1:2], 0.5 * float(np.log(128.0)))
        self.memset(self.cst[:, 2:3], 1.0)
        self.eps_t = self.cst
        for b in (self.Cf, self.nf, self.m_b, self.STf, self.cq, self.cx, self.cff):
            self.memset(b.ap(), 0.0)
        for b in (self.Cb, self.nb, self.STb):
            self.memset(b.ap(), 0.0, 'pool')

    def kc(self, kind, n):
        if kind == 's':
            return dict(U=self.cfv('Us', 64), LS=self.cfv('LSs', 64), BO=self.cfv('BOs', 64),
                        M=self.cbv('Ms', 64), MT=self.cbv('MTs', 64))
        return dict(U=self.cfv('U', n, n), LS=self.cfv('LS', n, n), BO=self.cfv('ones', n, n),
                    M=self.cbv('M', n, n), MT=self.cbv('MT', n, n))

    def ln_load(self, gname, bname):
        I = self.I
        self.dma(self.lnc[:, 0, :], View(I[gname], I[gname].t.partition_broadcast(128)))
        self.dma(self.lnc[:, 1, :], View(I[bname], I[bname].t.partition_broadcast(128)))

    def ln_rows(self, x, n, slot):
        sc = self.lnsc[slot]
        st, mv, rs = sc[:n, 0:12], sc[:n, 12:14], sc[:n, 14:15]
        for i in range(2):
            self.S.op('dve', lambda e, i=i: e.bn_stats(out=sc.t[:n, i * 6:(i + 1) * 6], in_=x.ap[:, i * 512:(i + 1) * 512]),
                      reads=[x], writes=[sc])
        self.S.op('dve', lambda e: e.bn_aggr(out=sc.t[:n, 12:14], in_=sc.t[:n, 0:12]), reads=[sc], writes=[sc])
        self.act(rs, sc[:n, 13:14], AF.Ln, bias=self.eps_t[:n, 0:1])
        self.act(rs, rs, AF.Exp, scale=-0.5)
        self.ts(x, x, sc[:n, 12:13], ALU.subtract, rs, ALU.mult)
        self.tt(x, x, self.lnc[:n, 0, :], ALU.mult)
        self.tt(x, x, self.lnc[:n, 1, :], ALU.add)

    def to_fm(self, tl, src_of_chunk, dstT):
        identb = self.cbv('identb')
        for ch in tl.chunks:
            n = ch.n
            xb = self.xb16
            self.act(xb[:n, :], src_of_chunk(ch))
            bank = 6 + self.rot('tfm', 2)
            pv = self.psb(bank)
            for k in range(8):
                self.tr(pv[:, k * n:(k + 1) * n], xb[:n, k * 128:(k + 1) * 128], identb[:n, :n])
            self.cp(dstT[:, :, ch.col0:ch.col0 + n], pv[:, 0:8 * n].rearrange('p (k n) -> p k n', n=n), 'dve')

    def _conv_taps(self, tl, psv, W, wtab, g, carry_p, scar_view, E, acc):
        Wm = W - 1
        off = 0
        for (col0, nb, L, kind) in tl.segs:
            Ev = E[:, off:off + nb * (L + Wm)].rearrange('p (b l) -> p b l', b=nb)
            pseg = psv[:, col0:col0 + nb * L].rearrange('p (b l) -> p b l', b=nb)
            if kind == 'p':
                self.cp(Ev[:, :, 0:Wm], carry_p[:, g:g + 1, :], 'act')
            elif kind == 'm':
                self.memset(Ev[:, :, 0:Wm], 0.0, 'dve')
            else:
                self.cp(Ev[:, :, 0:Wm], scar_view[:, g, :, :], 'act')
            self.act(Ev[:, :, Wm:Wm + L], pseg)
            av = acc[:, col0:col0 + nb * L].rearrange('p (b l) -> p b l', b=nb)
            self.act(av, pseg, AF.Identity, scale=wtab[:, g, Wm:W], bias=wtab[:, g, W:W + 1])
            if kind == 's':
                self.cp(scar_view[:, g, :, :], Ev[:, :, L:L + Wm], 'act')
            else:
                self.cp(carry_p[:, g:g + 1, :], Ev[:, :, L:L + Wm], 'act')
            for j in range(Wm):
                self.stt(av, Ev[:, :, j:j + L], wtab[:, g, j:j + 1], av, ALU.mult, ALU.add)
            off += nb * (L + Wm)

    def conv_group(self, tl, psv, W, wtab, g, carry_p, scar_view, dst, final=True):
        E = self.cE[self.rot('cE', 2)]
        acc = self.cacc[self.rot('cacc', 3)]
        self._conv_taps(tl, psv, W, wtab, g, carry_p, scar_view, E, acc)
        T = tl.T
        if not final:
            return acc
        prev = getattr(self, '_conv_pending', None)

        def stage2(acc=acc, dst=dst, T=T):
            th = self.cth[self.rot('cth', 2)]
            self.act(th[:, 0:T], acc[:, 0:T], AF.Tanh)
            self.stt(dst, th[:, 0:T], 1.0, acc[:, 0:T], ALU.add, ALU.mult)
        self._conv_pending = stage2
        if prev is not None:
            prev()
        return acc

    def conv_flush(self):
        prev = getattr(self, '_conv_pending', None)
        self._conv_pending = None
        if prev is not None:
            prev()

    def carry_out(self, src, G, R, dst):
        identf = self.cfv('ident')
        for g0 in range(0, G, 4):
            gn = min(4, G - g0)
            bank = self.ps[self.rot('co', 2)]
            for g in range(gn):
                self.tr(bank[:R, g * 128:(g + 1) * 128], src[:, g0 + g, :], identf)
            stg = self.cacc[self.rot('cacc', 3)]
            self.cp(stg[:R, 0:gn * 128], bank[:R, 0:gn * 128], 'act')
            self.dma(dst[:, g0 * 128:(g0 + gn) * 128], stg[:R, 0:gn * 128])

    def scar_in(self, name, G, R):
        identf = self.cfv('ident')
        rows = 16 * R
        sv = self.scar[:, 0:G * rows].rearrange('p (g b r) -> p g b r', g=G, b=16)
        for g0 in range(0, G, 4):
            gn = min(4, G - g0)
            stg = self.cacc[self.rot('cacc', 3)]
            self.dma(stg[:rows, 0:gn * 128], self.I[name][:, g0 * 128:(g0 + gn) * 128])
            bank = self.ps[self.rot('co', 2)]
            for g in range(gn):
                self.tr(bank[:, g * rows:(g + 1) * rows], stg[:rows, g * 128:(g + 1) * 128], identf[:rows, :rows])
            self.cp(self.scar[:, g0 * rows:(g0 + gn) * rows], bank[:, 0:gn * rows], 'act')
        return sv

    def scar_out(self, name, G, R):
        rows = 16 * R
        src = self.scar[:, 0:G * rows].rearrange('p (g br) -> p g br', g=G)
        self.carry_out(src, G, rows, self.O[name].ap())

    def dense_fm(self, tl, actT, nkt_total, cb_group, kt0=0):
        Wb, (name, k0, nk, parts) = self.wnext()
        ncols = sum(p[1] for p in parts)
        T = tl.T
        for gl in range(ncols // 128):
            bank = self.ps[self.rot('mm', 4)]
            for k in range(nk):
                self.mm(bank[:, 0:T], Wb[:, k, gl * 128:(gl + 1) * 128], actT[:, k0 + k, 0:T],
                        start=(k0 + k == 0), stop=(k0 + k == nkt_total - 1))
            cb_group(gl, bank[:, 0:T])

    def dense_tm(self, tl, actT, cb_chunk):
        Wb, (name, k0, nk, parts) = self.wnext()
        ncols = sum(p[1] for p in parts)
        for ch in tl.chunks:
            bank = self.ps[self.rot('mm', 4)]
            for k in range(nk):
                self.mm(bank[:ch.n, 0:ncols], actT[:, k0 + k, ch.col0:ch.col0 + ch.n], Wb[:, k, 0:ncols],
                        start=(k == 0), stop=(k == nk - 1))
            cb_chunk(ch, bank[:ch.n, 0:ncols])

    def run_tile(self, tl, last):
        S, I, O = self.S, self.I, self.O
        T = tl.T
        isS = any(sg[3] == 's' for sg in tl.segs)
        if isS:
            S.alias_phase([self.xr[2], self.xr[3], self.zs[2], self.zs[3]], self.grpArena + self.grpArena2)
        for ch in tl.chunks:
            src = {'s': I['xs'].ap(), 'm': I['meta'].ap()}.get(ch.kind)
            if src is None:
                src = I['xp'][ch.row0:ch.row0 + ch.n, :]
            self.dma(self.xr[ch.slot][:ch.n, :], src)
        self.ln_load('ln0_g', 'ln0_b')
        for ch in tl.chunks:
            self.ln_rows(self.xr[ch.slot][:ch.n, :], ch.n, ch.slot)
        self.to_fm(tl, lambda ch: self.xr[ch.slot][:ch.n, :], self.xnT)
        for ch in tl.chunks:
            self.ts(self.xr[ch.slot][:ch.n, :], self.xr[ch.slot][:ch.n, :], ALPHA, ALU.mult)
        self.dump('xnT_' + tl.name, self.xnT[:, :, 0:T], [128, 8, T])
        if self.debug.get('stop') == 'p0':
            return
        S.alias_phase(self.grpA2fix + self.grpA2rec + self.grpB + self.grpA1rec, self.grpA1conv + self.grpA1fix)
        sq = self.scar_in('s_mconv', 8, 3) if isS else None
        for blk in range(4):
            def cbq(gl, psv, blk=blk):
                g = 2 * blk + gl
                self.conv_group(tl, psv, 4, self.cwm, g, self.cq, sq, self.qkT[:, g, 0:T])
            self.dense_fm(tl, self.xnT, 8, cbq)
        self.conv_flush()
        if isS:
            self.scar_out('o_mconv', 8, 3)
        if last:
            self.carry_out(self.cq.ap(), 8, 3, O['p_mconv'].ap())
        for blk in range(4):
            self.dense_tm(tl, self.xnT, lambda ch, psv, blk=blk: self.act(self.v[ch.slot][:ch.n, 256 * blk:256 * blk + 256], psv))
        for blk in range(4):
            self.dense_tm(tl, self.xnT, lambda ch, psv, blk=blk: self.act(self.oth[ch.slot][:ch.n, 256 * blk:256 * blk + 256], psv, AF.Tanh, scale=0.5))
        self.dense_tm(tl, self.xnT, lambda ch, psv: self.cp(self.ifdt[:ch.n, ch.slot, :], psv, 'dve'))
        for ch in tl.chunks:
            n, s = ch.n, ch.slot
            gi = self.gat[:n, s, 0:8]
            self.tt(gi, self.ifdt[:n, s, 0:8], self.bif_b[:n, :], ALU.add)
            e1 = self.sm[3]
            self.act(e1[:n, 0:4], self.gat[:n, s, 4:8], AF.Exp, scale=-1.0)
            self.act(e1[:n, 0:4], e1[:n, 0:4], AF.Ln, bias=self.cst[:n, 2:3])
            self.ts(self.gat[:n, s, 4:8], e1[:n, 0:4], -1.0, ALU.mult)
            d1 = self.sm[4]
            self.tt(d1[:n, 0:32], self.ifdt[:n, s, 8:40], self.dtb_b[:n, :], ALU.add)
            self.act(d1[:n, 0:32], d1[:n, 0:32], AF.Exp)
            self.act(self.dta[:n, s, 0:32], d1[:n, 0:32], AF.Ln, bias=self.cst[:n, 2:3])
            self.tt(self.dta[:n, s, 32:64], self.dta[:n, s, 0:32], self.A_b[:n, :], ALU.mult)
        self.dump('qkT_' + tl.name, self.qkT[:, :, 0:T], [128, 8, T])
        self.dump('gat_' + tl.name, self.gat.ap(), [128, 4, 8])
        self.dump('dta_' + tl.name, self.dta.ap(), [128, 4, 64])
        if self.debug.get('stop') == 'a1':
            return
        S.alias_phase(self.grpA1conv, self.grpA1rec)
        for ch in tl.chunks:
            self.mlstm_chunk(tl, ch, last)
        self.dump('hgT_' + tl.name, self.hgT[:, :, 0:T], [128, 8, T])
        if self.debug.get('stop') == 'mlstm':
            return
        S.alias_phase(self.grpA1rec + self.grpA1fix, self.grpA1conv + self.grpA2fix)
        for blk in range(8):
            def cbz(ch, psv, blk=blk):
                n = ch.n
                zc = self.cacc[self.rot('cacc', 3)]
                th = self.cth[self.rot('cth', 2)]
                self.cp(zc[:n, 0:256], psv, 'act')
                self.act(th[:n, 0:256], psv, AF.Tanh, scale=0.5)
                self.stt(self.zs[ch.slot][:n, 256 * blk:256 * blk + 256], th[:n, 0:256], 1.0, zc[:n, 0:256], ALU.add, ALU.mult)
            self.dense_tm(tl, self.xnT, cbz)
        if self.debug.get('stop') == 'a2z':
            return
        sx = self.scar_in('s_sconv', 24, 3) if isS else None
        for blk in range(12):
            def cbx(gl, psv, blk=blk):
                g = 2 * blk + gl
                self.conv_group(tl, psv, 4, self.cws, g, self.cx, sx, self.xbcT[:, g, 0:T])
            self.dense_fm(tl, self.xnT, 8, cbx)
        self.conv_flush()
        if isS:
            self.scar_out('o_sconv', 24, 3)
        if last:
            self.carry_out(self.cx.ap(), 24, 3, O['p_sconv'].ap())
        self.dump('xbcT_' + tl.name, self.xbcT[:, :, 0:T], [128, 24, T])
        if self.debug.get('stop') == 'a2':
            return
        S.alias_phase(self.grpA1conv, self.grpA2rec)
        for ch in tl.chunks:
            self.ssd_chunk(tl, ch, last)
        self.dump('ygT_' + tl.name, self.ygT[:, :, 0:T], [128, 16, T])
        if self.debug.get('stop') == 'ssd':
            return
        S.alias_phase(self.grpA2rec + self.grpA2fix, self.grpA1conv + self.grpB)
        for blk in range(8):
            def cbg(gl, psv, blk=blk):
                self.act(self.gth[:, 2 * blk + gl, 0:T], psv, AF.Tanh, scale=0.5)
            self.dense_fm(tl, self.xnT, 8, cbg)
        for j in range(4):
            Wb, (name, k0, nk, parts) = self.wnext()
            banksA = [self.ps[0], self.ps[1]]
            banksB = [self.ps[2], self.ps[3]]
            for gl in range(2):
                for k in range(8):
                    self.mm(banksA[gl][:, 0:T], Wb[:, k, gl * 128:(gl + 1) * 128], self.hgT[:, k, 0:T], start=(k == 0), stop=(k == 7))
            for half in range(2):
                Wb, (name, k0, nk, parts) = self.wnext()
                for gl in range(2):
                    for k in range(8):
                        kk = 8 * half + k
                        self.mm(banksB[gl][:, 0:T], Wb[:, k, gl * 128:(gl + 1) * 128], self.ygT[:, kk, 0:T], start=(kk == 0), stop=(kk == 15))
            for gl in range(2):
                g = 2 * j + gl
                m1 = self.cacc[self.rot('cacc', 3)]
                m2 = self.cacc2[self.rot('cacc2', 2)]
                self.stt(m1[:, 0:T], self.gth[:, g, 0:T], 1.0, banksA[gl][:, 0:T], ALU.add, ALU.mult)
                self.stt(m2[:, 0:T], self.gth[:, 8 + g, 0:T], 1.0, banksB[gl][:, 0:T], ALU.add, ALU.mult)
                self.tt(self.mixT[:, g, 0:T], m1[:, 0:T], m2[:, 0:T], ALU.add)
        self.rr['mm'] = 0
        for blk in range(4):
            def cbo(ch, psv, blk=blk):
                xv = self.xr[ch.slot][:ch.n, 256 * blk:256 * blk + 256]
                self.stt(xv, psv, 0.5, xv, ALU.mult, ALU.add)
            self.dense_tm(tl, self.mixT, cbo)
        self.ln_load('ln1_g', 'ln1_b')
        for ch in tl.chunks:
            self.ln_rows(self.xr[ch.slot][:ch.n, :], ch.n, ch.slot)
        self.dump('x1_' + tl.name, self.xr[0].ap(), [128, D])
        self.to_fm(tl, lambda ch: self.xr[ch.slot][:ch.n, :], self.xnT)
        for ch in tl.chunks:
            self.ts(self.xr[ch.slot][:ch.n, :], self.xr[ch.slot][:ch.n, :], ALPHA, ALU.mult)
        sf = self.scar_in('s_fconv', 44, 2) if isS else None
        for j in range(11):
            accs = {}
            def cbua(gl, psv, j=j):
                g = 2 * j + gl
                accs[gl] = self.conv_group(tl, psv, 3, self.cwf, g, self.cff, sf, None, final=False)
            self.dense_fm(tl, self.xnT, 8, cbua)
            def cbub(gl, psv, j=j):
                g = 2 * j + gl
                E = self.cE[self.rot('cE', 2)]
                accb = self.cacc2[self.rot('cacc2', 2)]
                self._conv_taps(tl, psv, 3, self.cwf, 22 + g, self.cff, sf, E, accb)
                th = self.cth[self.rot('cth', 2)]
                acca = accs[gl]
                self.act(th[:, 0:T], acca[:, 0:T], AF.Tanh)
                self.stt(th[:, 0:T], th[:, 0:T], 1.0, acca[:, 0:T], ALU.add, ALU.mult)
                self.tt(self.hffT[:, g, 0:T], th[:, 0:T], accb[:, 0:T], ALU.mult)
            self.dense_fm(tl, self.xnT, 8, cbub)
        if isS:
            self.scar_out('o_fconv', 44, 2)
        if last:
            self.carry_out(self.cff.ap(), 44, 2, O['p_fconv'].ap())
        self.dump('hffT_' + tl.name, self.hffT[:, :, 0:T], [128, 22, T])
        for blk in range(4):
            banks = {ch.slot: self.ps[ch.slot] for ch in tl.chunks}
            for (k0, nk) in ((0, 8), (8, 8), (16, 6)):
                Wb, meta = self.wnext()
                for ch in tl.chunks:
                    for k in range(nk):
                        self.mm(banks[ch.slot][:ch.n, 0:256], self.hffT[:, k0 + k, ch.col0:ch.col0 + ch.n], Wb[:, k, 0:256],
                                start=(k0 + k == 0), stop=(k0 + k == 21))
            for ch in tl.chunks:
                xv = self.xr[ch.slot][:ch.n, 256 * blk:256 * blk + 256]
                self.tt(xv, banks[ch.slot][:ch.n, 0:256], xv, ALU.add)
        self.ln_load('ln2_g', 'ln2_b')
        for ch in tl.chunks:
            self.ln_rows(self.xr[ch.slot][:ch.n, :], ch.n, ch.slot)
            if ch.kind == 'p':
                self.dma(O['y_p'][ch.row0:ch.row0 + ch.n, :], self.xr[ch.slot][:ch.n, :])
            elif ch.kind == 's':
                self.dma(O['y_s'].ap(), self.xr[ch.slot][:ch.n, :])

    def mlstm_chunk(self, tl, ch, last):
        I, O = self.I, self.O
        n, s, c0, kind = ch.n, ch.slot, ch.col0, ch.kind
        kc = self.kc(kind, n)
        U, LS, M, MT = kc['U'], kc['LS'], kc['M'], kc['MT']
        identf, onesf = self.cfv('ident', n, n), self.cfv('ones', n, n)
        identb, onesb = self.cbv('identb', n, n), self.cbv('onesb', n, n)
        ps = self.ps
        cols = slice(c0, c0 + n)
        ig = self.gat[:n, s, 0:4]
        lf = self.gat[:n, s, 4:8]
        sm = self.sm
        bt, mi, bm, mt, wi, emt, negm, den, rden, wi2 = (sm[i] for i in range(5, 15))
        R1, R2, R3, wT, ST, kTM, hh, vw = self.R1, self.R2, self.R3, self.wT, self.ST, self.kTM, self.hh, self.vw
        self.mm(ps[0][:n, 0:4], U, lf)
        for h in range(4):
            self.ts(R1[:n, h, :n], LS, self.gat[:n, s, 4 + h:5 + h], ALU.mult)
            self.act(R2[:n, h, :n], identf, AF.Copy, scale=self.gat[:n, s, h:h + 1])
        for h in range(4):
            self.mm(ps[1][:n, h * 128:h * 128 + n], U, R1[:n, h, :n], start=True, stop=False)
            self.mm(ps[1][:n, h * 128:h * 128 + n], onesf, R2[:n, h, :n], start=False, stop=False)
            self.mm(ps[1][:n, h * 128:h * 128 + n], identb, M, start=False, stop=True)
        self.rmax(mi[:n, 0:4], ps[1][:n, :].rearrange('p (h t) -> p h t', h=4)[:, :, 0:n])
        if kind == 's':
            m0s = sm[15]
            self.dma(m0s[:16, 0:4], I['s_m'].ap())
            self.mm(ps[0][:n, 4:8], self.cfv('BMT', 16), m0s[:16, 0:4])
            m0v = ps[0][:n, 4:8]
        else:
            m0v = self.m_b[:n, :]
        self.cp(bt[:n, 0:4], ps[0][:n, 0:4], 'act')
        self.tt(bm[:n, 0:4], bt[:n, 0:4], m0v, ALU.add)
        self.tt(mt[:n, 0:4], bm[:n, 0:4], mi[:n, 0:4], ALU.max)
        self.tt(bm[:n, 0:4], bm[:n, 0:4], mt[:n, 0:4], ALU.subtract)
        self.act(wi[:n, 0:4], bm[:n, 0:4], AF.Exp)
        self.act(emt[:n, 0:4], mt[:n, 0:4], AF.Exp, scale=-1.0, bias=self.cst[:n, 1:2])
        self.ts(negm[:n, 0:4], mt[:n, 0:4], -1.0, ALU.mult)
        for h in range(4):
            self.act(R3[:n, h, :n], identf, AF.Copy, scale=negm[:n, h:h + 1])
        for h in range(4):
            o = ps[2][:n, h * 128:h * 128 + n]
            self.mm(o, R1[:n, h, :n], U, start=True, stop=False)
            self.mm(o, R2[:n, h, :n], onesf, start=False, stop=False)
            self.mm(o, onesf, R3[:n, h, :n], start=False, stop=False)
            self.mm(o, identb, MT, start=False, stop=True)
        ps2v = ps[2][:n, :].rearrange('p (h t) -> p h t', h=4)[:, :, 0:n]
        self.act(wT[:n, :, :n], ps2v, AF.Exp)
        for h in range(4):
            self.mm(ps[1][:n, h * 128:h * 128 + n], self.qkT[:, 4 + h, cols], self.qkT[:, h, cols])
        ps1v = ps[1][:n, :].rearrange('p (h t) -> p h t', h=4)[:, :, 0:n]
        self.tt(ST[:n, :, :n], ps1v, wT[:n, :, :n], ALU.mult)
        pb7 = self.psb(7)
        for h in range(4):
            self.tr(pb7[:n, h * 128:(h + 1) * 128], self.qkT[:, 4 + h, cols], self.cbv('identb'))
        self.cp(kTM[:n, :, :], pb7[:n, 0:512].rearrange('p (h d) -> p h d', h=4), 'act')
        if kind == 's':
            self.mlstm_sample_states(tl, ch, wi, wT, kTM, mt)
        for h in range(4):
            self.mm(ps[3 + h // 2][:n, (h % 2) * 256:(h % 2) * 256 + 256], ST[:n, h, :n], self.v[s][:n, h * 256:(h + 1) * 256])
        for h in range(4):
            self.mm(ps[0][:n, 8 + h:9 + h], ST[:n, h, :n], onesb[:, 0:1])
        if kind != 's':
            for h in range(4):
                self.mm(ps[5 + h // 2][:n, (h % 2) * 256:(h % 2) * 256 + 256], self.qkT[:, h, cols], self.Cb[:, h, :])
            for h in range(4):
                self.mm(ps[0][:n, 12 + h:13 + h], self.qkT[:, h, cols], self.nb[:, h:h + 1])
        dint = ps[0][:n, 12:16] if kind != 's' else sm[12][:n, 0:4]
        self.tt(den[:n, 0:4], dint, wi[:n, 0:4], ALU.mult)
        self.tt(den[:n, 0:4], den[:n, 0:4], ps[0][:n, 8:12], ALU.add)
        self.ts(wi2[:n, 0:4], den[:n, 0:4], -1.0, ALU.mult)
        self.tt(den[:n, 0:4], den[:n, 0:4], wi2[:n, 0:4], ALU.max)
        self.tt(den[:n, 0:4], den[:n, 0:4], emt[:n, 0:4], ALU.max)
        self.S.op('dve', lambda e: e.reciprocal(out=rden.t[:n, 0:4], in_=den.t[:n, 0:4]), reads=[den], writes=[rden])
        self.tt(wi2[:n, 0:4], wi[:n, 0:4], rden[:n, 0:4], ALU.mult)
        for h in range(4):
            self.act(hh[:n, h, :], ps[3 + h // 2][:n, (h % 2) * 256:(h % 2) * 256 + 256], AF.Copy, scale=rden[:n, h:h + 1])
            if kind != 's':
                iv = ps[5 + h // 2][:n, (h % 2) * 256:(h % 2) * 256 + 256]
            else:
                iv = (R1 if h < 2 else R2)[:n, :, :].rearrange('p a b -> p (a b)')[:, (h % 2) * 256:(h % 2) * 256 + 256]
            self.stt(hh[:n, h, :], iv, wi2[:n, h:h + 1], hh[:n, h, :], ALU.mult, ALU.add)
        st, mv, rs = sm[0], sm[1], sm[2]
        for h in range(4):
            self.S.op('dve', lambda e, h=h: e.bn_stats(out=st.t[:n, h * 6:(h + 1) * 6], in_=hh.t[:n, h, :]), reads=[hh], writes=[st])
        for h in range(4):
            self.S.op('dve', lambda e, h=h: e.bn_aggr(out=mv.t[:n, 2 * h:2 * h + 2], in_=st.t[:n, h * 6:(h + 1) * 6]), reads=[st], writes=[mv])
        mvv = mv[:n, 0:8].rearrange('p (h t) -> p h t', t=2)
        self.act(rs[:n, 0:4], mvv[:, :, 1], AF.Ln, bias=self.cst[:n, 0:1])
        self.act(rs[:n, 0:4], rs[:n, 0:4], AF.Exp, scale=-0.5)
        for h in range(4):
            self.ts(hh[:n, h, :], hh[:n, h, :], mv[:n, 2 * h:2 * h + 1], ALU.subtract, rs[:n, h:h + 1], ALU.mult)
            self.stt(self.hgTM[:n, h * 256:(h + 1) * 256], self.oth[s][:n, h * 256:(h + 1) * 256], 1.0, hh[:n, h, :], ALU.add, ALU.mult)
        for k in range(8):
            self.tr(pb7[:, k * n:(k + 1) * n], self.hgTM[:n, k * 128:(k + 1) * 128], identb)
        self.cp(self.hgT[:, :, cols], pb7[:, 0:8 * n].rearrange('p (k t) -> p k t', k=8), 'dve')
        if kind != 's':
            wl16 = sm[15]
            self.cp(wl16.ap().bitcast(BF16)[:n, 0:4], wT[:n, :, n - 1], 'act')
            for h in range(4):
                self.ts(vw[:n, h, :], self.v[s][:n, h * 256:(h + 1) * 256], wT[:n, h, n - 1:n], ALU.mult)
            for h in range(4):
                self.mm(ps[5 + h // 2][:, (h % 2) * 256:(h % 2) * 256 + 256], kTM[:n, h, :], vw[:n, h, :])
            for h in range(4):
                self.mm(ps[0][:, 16 + h:17 + h], kTM[:n, h, :], wl16.ap().bitcast(BF16)[:n, h:h + 1])
            SEL = self.cfv('SELp' if n == 128 else 'SELm', n)
            self.mm(ps[0][:, 32:36], SEL, wi[:n, 0:4])
            self.mm(ps[0][:, 36:40], SEL, mt[:n, 0:4])
            dec = sm[3]
            self.cp(dec[:, 0:8], ps[0][:, 32:40], 'act')
            for h in range(4):
                self.stt(self.Cf[:, h, :], self.Cf[:, h, :], dec[:, h:h + 1], ps[5 + h // 2][:, (h % 2) * 256:(h % 2) * 256 + 256], ALU.mult, ALU.add)
            self.tt(self.nf.ap(), self.nf.ap(), dec[:, 0:4], ALU.mult)
            self.tt(self.nf.ap(), self.nf.ap(), ps[0][:, 16:20], ALU.add)
            self.cp(self.m_b.ap(), dec[:, 4:8], 'dve')
            self.cp(self.Cb.ap(), self.Cf.ap(), 'act')
            self.cp(self.nb.ap(), self.nf.ap(), 'act')
            if ch.final:
                self.dma(O['p_C'].ap().rearrange('h d e -> d h e'), self.Cf.ap())
                identf128 = self.cfv('ident')
                self.tr(ps[0][:4, 128:256], self.nf.ap(), identf128)
                self.cp(self.pn_st[:4, :], ps[0][:4, 128:256], 'act')
                self.dma(O['p_n'].ap(), self.pn_st[:4, :])
                self.dma(O['p_m'].ap(), self.m_b[0:1, :])

    def mlstm_sample_states(self, tl, ch, wi, wT, kTM, mt):
        I, O, ps, sm = self.I, self.O, self.ps, self.sm
        n, s = 64, ch.slot
        identf = self.cfv('ident')
        RS, BM = self.cfv('RS', 64), self.cfv('BM', 64)
        CM = self.cbv('CM').rearrange('p (b j) -> p b j', b=16)
        vw = self.vw
        wl = sm[3]
        tmp = self.R3
        self.tt(tmp[:n, :, 0:64], wT[:n, :, 0:64], View(self.cf, self.cfv('LSEL', 64).ap.unsqueeze(1).to_broadcast([64, 4, 64])), ALU.mult)
        self.S.op('dve', lambda e: e.tensor_reduce(out=wl.t[:n, 0:4], in_=tmp.t[:n, :, 0:64], axis=AX.X, op=ALU.add), reads=[tmp], writes=[wl])
        wl16 = sm[4].ap().bitcast(BF16)
        self.cp(wl16[:n, 0:4], wl[:n, 0:4], 'act')
        for h in range(4):
            self.ts(vw[:n, h, :], self.v[s][:n, h * 256:(h + 1) * 256], wl[:n, h:h + 1], ALU.mult)
        Rm3 = self.Rm[:n, :].rearrange('p (b h) -> p b h', h=4)
        self.tt(Rm3, View(wi, wi.t[:n, 0:4].unsqueeze(1).to_broadcast([n, 16, 4])),
                View(self.cf, RS.ap.unsqueeze(2).to_broadcast([n, 16, 4])), ALU.mult)
        self.mm(ps[0][:, 64:128], self.cfv('ones', 64), self.Rm[:n, :])
        self.cp(self.decS.ap(), ps[0][:, 64:128], 'act')
        self.mm(ps[0][:16, 40:44], RS, mt[:n, 0:4])
        mo = sm[15]
        self.cp(mo[:16, 8:12], ps[0][:16, 40:44], 'act')
        self.dma(O['o_m'].ap(), mo[:16, 8:12])
        stg = self.pn_st
        self.dma(stg[:64, :], I['s_n'].ap())
        self.tr(ps[0][:, 192:256], stg[:64, :], identf[:64, :64])
        self.cp(self.n0T.ap(), ps[0][:, 192:256], 'act')
        self.cp(self.n16.ap(), self.n0T.ap(), 'pool')
        kTMflat = kTM[:n, :, :].rearrange('p h d -> p (h d)')
        for b in range(16):
            i = b % 2
            C0, C16, qm, km = self.C0b[i], self.C0b16[i], self.qmb[i], self.kTMm[i]
            self.dma(C0.ap(), I['s_C'][b].rearrange('h d e -> d h e'))
            self.cp(C16[:, :, 0:256], C0.ap(), 'act')
            self.cp(C16[:, :, 256:257], self.n16[:, 4 * b:4 * b + 4].rearrange('p (h o) -> p h o', o=1), 'dve')
            self.tt(qm.ap(), self.qkT[:, 0:4, ch.col0:ch.col0 + 64], View(self.cb, CM.ap[:, b, :].unsqueeze(1).to_broadcast([128, 4, 64])), ALU.mult)
            self.ts(km[:n, :, :].rearrange('p h d -> p (h d)'), kTMflat, BM[:, b:b + 1], ALU.mult)
            for h in range(4):
                self.mm(ps[3 + h][:n, 0:257], qm[:, h, :], C16[:, h, 0:257], start=(b == 0), stop=(b == 15))
            for h in range(4):
                self.mm(ps[1 + h // 2][:, (h % 2) * 256:(h % 2) * 256 + 256], km[:n, h, :], vw[:n, h, :])
            for h in range(4):
                self.mm(ps[0][:, 128 + 4 * b + h:129 + 4 * b + h], km[:n, h, :], wl16[:n, h:h + 1])
            for h in range(4):
                self.stt(C0[:, h, :], C0[:, h, :], self.decS[:, 4 * b + h:4 * b + h + 1],
                         ps[1 + h // 2][:, (h % 2) * 256:(h % 2) * 256 + 256], ALU.mult, ALU.add)
            self.dma(O['o_C'][b].rearrange('h d e -> d h e'), C0.ap())
        for h in range(4):
            dst = (self.R1 if h < 2 else self.R2)[:n, :, :].rearrange('p a b -> p (a b)')[:, (h % 2) * 256:(h % 2) * 256 + 256]
            self.cp(dst, ps[3 + h][:n, 0:256], 'act')
            self.cp(sm[12][:n, h:h + 1], ps[3 + h][:n, 256:257], 'act')
        self.tt(self.n0T.ap(), self.n0T.ap(), self.decS.ap(), ALU.mult)
        self.tt(self.n0T.ap(), self.n0T.ap(), ps[0][:, 128:192], ALU.add)
        self.tr(ps[0][:64, 256:384], self.n0T.ap(), identf)
        self.cp(stg[:64, :], ps[0][:64, 256:384], 'act')
        self.dma(O['o_n'].ap(), stg[:64, :])

    def ssd_chunk(self, tl, ch, last):
        I, O, ps, sm = self.I, self.O, self.ps, self.sm
        n, s, c0, kind = ch.n, ch.slot, ch.col0, ch.kind
        kc = self.kc(kind, n)
        U, LS, BO, MT = kc['U'], kc['LS'], kc['BO'], kc['MT']
        identb = self.cbv('identb')
        cols = slice(c0, c0 + n)
        dt = self.dta[:n, s, 0:32]
        a = self.dta[:n, s, 32:64]
        btsb, dect, wend, decS = sm[5], sm[6], sm[7], sm[8]
        self.mm(ps[0][:n, 0:32], U, a)
        self.mm(ps[0][:n, 32:64], BO, a)
        self.cp(btsb[:n, 0:32], ps[0][:n, 0:32], 'act')
        self.act(dect[:n, 0:32], ps[0][:n, 0:32], AF.Exp)
        self.tt(wend[:n, 0:32], ps[0][:n, 32:64], btsb[:n, 0:32], ALU.subtract)
        self.act(wend[:n, 0:32], wend[:n, 0:32], AF.Exp)
        if kind != 's':
            self.mm(ps[0][:, 64:96], self.cfv('ones', n), a)
            self.act(decS[:, 0:32], ps[0][:, 64:96], AF.Exp)
        S = self.S
        pb6, pb7 = self.psb(6), self.psb(7)
        xsTM = self.xdt
        for g in range(16):
            pv = pb6 if g < 8 else pb7
            self.tr(pv[:n, (g % 8) * 128:(g % 8 + 1) * 128], self.xbcT[:, g, cols], identb)
        self.act(xsTM[:n, 0:1024], pb6[:n, 0:1024])
        self.act(xsTM[:n, 1024:2048], pb7[:n, 0:1024])
        S.alias_phase([self.ynTM], [self.xsDT])
        for half in range(2):
            for g in range(8):
                self.ts(self.xsDT[:, g, 0:n], self.xbcT[:, 8 * half + g, cols], self.Dfm[:, 8 * half + g:8 * half + g + 1], ALU.mult)
            pv = pb6 if half == 0 else pb7
            for g in range(8):
                self.tr(pv[:n, g * 128:(g + 1) * 128], self.xsDT[:, g, 0:n], identb)
            self.cp(self.xsD[:n, 1024 * half:1024 * half + 1024], pv[:n, 0:1024], 'dve')
        for g in range(4):
            self.tr(pb6[:n, g * 128:(g + 1) * 128], self.xbcT[:, 16 + g, cols], identb)
        self.act(self.BTM[:n, :], pb6[:n, 0:512])
        dtw = sm[13]
        self.tt(dtw[:n, 0:32], dt, wend[:n, 0:32], ALU.mult)
        S.alias_phase([self.xsDT], [self.ynTM])
        if kind == 's':
            self.ssd_sample_states(tl, ch, dtw)
        S.alias_phase([self.ynTM], self.MTh[1])
        ssq = sm[9]
        self.memset(ssq[:n, 0:4], 0.0)
        def stageA(g):
            MTb = self.MTh[g % 2]
            for j in range(8):
                self.act(self.LAh[j][:n, :n], LS, AF.Copy, scale=self.dta[:n, s, 32 + 8 * g + j:33 + 8 * g + j])
            for j in range(8):
                o = ps[1 + j // 4][:n, (j % 4) * 128:(j % 4) * 128 + n]
                self.mm(o, self.LAh[j][:n, :n], U, start=True, stop=False)
                self.mm(o, identb[:n, :n], MT, start=False, stop=True)
            for half in range(2):
                self.act(self.LTh[half][:n, :, :n],
                         ps[1 + half][:n, :].rearrange('p (h t) -> p h t', h=4)[:, :, 0:n], AF.Exp)
            self.mm(ps[3][:n, 0:n], self.xbcT[:, 16 + g, cols], self.xbcT[:, 20 + g, cols])
            for j in range(8):
                self.stt(MTb[j][:n, :n], self.LTh[j // 4][:n, j % 4, :n], self.dta[:n, s, 8 * g + j:8 * g + j + 1], ps[3][:n, 0:n], ALU.mult, ALU.mult)

        def stageB(g):
            MTb = self.MTh[g % 2]
            for j in range(8):
                h = 8 * g + j
                self.mm(ps[4][:n, j * 64:(j + 1) * 64], MTb[j][:n, :n], xsTM[:n, h * 64:(h + 1) * 64])
            t1 = self.t1
            if kind != 's':
                self.mm(ps[5][:n, 0:512], self.xbcT[:, 20 + g, cols], self.STb[:, 512 * g:512 * g + 512])
                for j in range(4):
                    self.act(t1[:n, j * 64:(j + 1) * 64], ps[5][:n, j * 64:(j + 1) * 64], AF.Copy, scale=dect[:n, 8 * g + j:8 * g + j + 1])
                self.tt(t1[:n, 256:512].rearrange('p (h d) -> p d h', d=64), ps[5][:n, 256:512].rearrange('p (h d) -> p d h', d=64),
                        View(dect, dect.t[:n, 8 * g + 4:8 * g + 8].unsqueeze(1).to_broadcast([n, 64, 4])), ALU.mult)
            else:
                self.cp(t1[:n, :], self.ysi[:n, 512 * g:512 * g + 512], 'dve')
            self.tt(t1[:n, :], t1[:n, :], ps[4][:n, 0:512], ALU.add)
            self.tt(t1[:n, :], t1[:n, :], self.xsD[:n, 512 * g:512 * g + 512], ALU.add)
            self.tt(self.yz[:n, 512 * g:512 * g + 512], t1[:n, :], self.zs[s][:n, 512 * g:512 * g + 512], ALU.mult)

        stageA(0)
        for g in range(4):
            if g + 1 < 4:
                stageA(g + 1)
            stageB(g)
        S.alias_phase(self.MTh[1], [self.ynTM])
        for g in range(4):
            self.act(self.ynTM[:n, 512 * g:512 * g + 512], self.yz[:n, 512 * g:512 * g + 512], AF.Square, accum=ssq[:n, g:g + 1])
        rs = sm[10]
        self.act(rs[:n, 0:4], ssq[:n, 0:4], AF.Ln, scale=0.25 / 512.0, bias=self.cst[:n, 0:1])
        self.act(rs[:n, 0:4], rs[:n, 0:4], AF.Exp, scale=-0.5)
        self.ts(rs[:n, 0:4], rs[:n, 0:4], 0.5, ALU.mult)
        for g in range(4):
            self.ts(self.ynTM[:n, 512 * g:512 * g + 512], self.yz[:n, 512 * g:512 * g + 512], rs[:n, g:g + 1], ALU.mult)
        for k in range(16):
            pv = pb6 if k < 8 else pb7
            self.tr(pv[:, (k % 8) * n:(k % 8 + 1) * n], self.ynTM[:n, k * 128:(k + 1) * 128], identb[:n, :n])
        for half in range(2):
            pv = (pb6 if half == 0 else pb7)[:, 0:8 * n].rearrange('p (k t) -> p k t', k=8)
            self.cp(self.ygT[:, 8 * half:8 * half + 8, cols], pv, 'dve' if half == 0 else 'act')
        if kind != 's':
            S.alias_phase([self.ynTM], self.MTh[1])
            for g in range(4):
                hs = slice(8 * g, 8 * g + 8)
                bank = ps[3 + 2 * (g % 2)]
                Bs = self.MTh[g % 2]
                for j in range(8):
                    h = 8 * g + j
                    self.ts(Bs[j][:n, :], self.BTM[:n, g * 128:(g + 1) * 128], dtw[:n, h:h + 1], ALU.mult)
                for j in range(8):
                    h = 8 * g + j
                    self.mm(bank[:, j * 64:(j + 1) * 64], Bs[j][:n, :], xsTM[:n, h * 64:(h + 1) * 64])
                if g % 2 == 0:
                    for j in range(8):
                        c0_ = 512 * g + 64 * j
                        self.act(self.STf[:, c0_:c0_ + 64], self.STf[:, c0_:c0_ + 64], AF.Copy, scale=decS[:, 8 * g + j:8 * g + j + 1])
                else:
                    sv = self.STf[:, 512 * g:512 * g + 512].rearrange('p (h d) -> p d h', d=64)
                    self.tt(sv, sv, View(decS, decS.t[:, hs].unsqueeze(1).to_broadcast([128, 64, 8])), ALU.mult)
                self.tt(self.STf[:, 512 * g:512 * g + 512], self.STf[:, 512 * g:512 * g + 512], bank[:, 0:512], ALU.add)
            S.alias_phase(self.MTh[1], [self.ynTM])
            self.cp(self.STb.ap(), self.STf.ap(), 'act')
            if ch.final:
                identf = self.cfv('ident')
                for j in range(16):
                    bank = ps[1 + (j // 4) % 2]
                    self.tr(bank[:, (j % 4) * 128:(j % 4 + 1) * 128], self.STf[:, j * 128:(j + 1) * 128], identf)
                    if j % 4 == 3:
                        stg = self.LA[:, 4 * ((j // 4) % 2):4 * ((j // 4) % 2) + 4, :]
                        self.S.alias_phase(self.LAh, [self.LA])
                        self.cp(stg, bank[:, 0:512].rearrange('p (j n) -> p j n', j=4), 'act')
                        q = j // 4
                        self.dma(O['p_ssm'][512 * q:512 * q + 512, :].rearrange('(j p) n -> p j n', p=128), stg)
                self.S.alias_phase([self.LA], self.LAh)

    def ssd_sample_states(self, tl, ch, wend):
        I, O, ps, sm, S = self.I, self.O, self.ps, self.sm, self.S
        n, s = 64, ch.slot
        identf = self.cfv('ident')
        RS, BM = self.cfv('RS', 64), self.cfv('BM', 64)
        CM = self.cbv('CM').rearrange('p (b j) -> p b j', b=16)
        dect = sm[6]
        S.alias_phase(self.grpArena, self.grpArena2)
        S.alias_phase(self.LTh + self.MTh[0] + [self.t1, self.yz], [self.S0b[1]])
        blsb = sm[11]
        self.cp(blsb[:n, 0:32], ps[0][:n, 32:64], 'act')
        bl3 = blsb[:n, 0:32].rearrange('p (j r) -> p j r', r=2)
        for r in range(2):
            self.tt(self.Rr[:n, r, :].rearrange('p (b j) -> p b j', b=16),
                    View(blsb, bl3.ap[:, :, r].unsqueeze(1).to_broadcast([n, 16, 16])),
                    View(self.cf, RS.ap.unsqueeze(2).to_broadcast([n, 16, 16])), ALU.mult, eng='pool')
        self.mm(ps[0][:, 256:512], self.cfv('H0', 64), self.Rr[:n, 0, :], start=True, stop=False)
        self.mm(ps[0][:, 256:512], self.cfv('H1', 64), self.Rr[:n, 1, :], start=False, stop=True)
        self.act(self.decP.ap(), ps[0][:, 256:512], AF.Exp)
        wxA = self.ynTM
        self.tt(wxA[:n, :].rearrange('p (h d) -> p d h', d=64), self.xdt[:n, :].rearrange('p (h d) -> p d h', d=64),
                View(wend, wend.t[:n, 0:32].unsqueeze(1).to_broadcast([n, 64, 32])), ALU.mult)
        for b in range(16):
            Sb = self.S0b[b % 2]
            cm = self.CTmb[b % 2]
            self.dma(Sb.ap(), I['s_ssm'][b].rearrange('(j p) n -> p j n', p=128))
            self.tt(cm.ap(), self.xbcT[:, 20:24, ch.col0:ch.col0 + 64], View(self.cb, CM.ap[:, b, :].unsqueeze(1).to_broadcast([128, 4, 64])), ALU.mult)
            self.ts(self.wxm[:n, :], wxA[:n, :], BM[:, b:b + 1], ALU.mult)
            for q in range(4):
                bank = ps[5 + q % 2]
                for i in range(4):
                    self.tr(bank[:, i * 128:(i + 1) * 128], Sb[:, 4 * q + i, :], identf)
                self.cp(self.SbT[:, 512 * q:512 * q + 512], bank[:, 0:512], 'act')
            for g in range(4):
                self.mm(ps[1 + g][:n, 0:512], cm[:, g, :], self.SbT[:, 512 * g:512 * g + 512], start=(b == 0), stop=(b == 15))
            for q in range(4):
                bank = ps[7] if q % 2 == 0 else ps[0]
                for i in range(4):
                    j = 4 * q + i
                    self.mm(bank[:, i * 128:(i + 1) * 128], self.wxm[:n, j * 128:(j + 1) * 128], self.BTM[:n, q * 128:(q + 1) * 128])
                for i in range(4):
                    j = 4 * q + i
                    self.stt(Sb[:, j, :], Sb[:, j, :], self.decP[:, 16 * b + j:16 * b + j + 1], bank[:, i * 128:(i + 1) * 128], ALU.mult, ALU.add)
            self.dma(O['o_ssm'][b].rearrange('(j p) n -> p j n', p=128), Sb.ap())
        for g in range(4):
            self.tt(self.ysi[:n, 512 * g:512 * g + 512].rearrange('p (h d) -> p d h', d=64),
                    ps[1 + g][:n, 0:512].rearrange('p (h d) -> p d h', d=64),
                    View(dect, dect.t[:n, 8 * g:8 * g + 8].unsqueeze(1).to_broadcast([n, 64, 8])), ALU.mult)
        S.alias_phase([self.S0b[1]], self.LTh + self.MTh[0] + [self.t1, self.yz])


_CACHE = {}


def _get_kernel():
    if 'k' not in _CACHE:
        _CACHE['k'] = K()
    return _CACHE['k']


def make_in_maps(kb, inputs):
    f = lambda a: np.ascontiguousarray(np.asarray(a, dtype=np.float32))
    xp, xs = f(inputs['x_prompt']), f(inputs['x_sample'])
    shared = {'meta': f(inputs['meta_tokens']), 'cf': kb.cf_np, 'cb': kb.cb_np,
              'ln0_g': f(inputs['ln0_g']), 'ln0_b': f(inputs['ln0_b']),
              'b_if': f(inputs['b_mlstm_if'])[0], 'w_mconv': f(inputs['w_mlstm_conv'])[0],
              'b_mconv': f(inputs['b_mlstm_conv']), 'mnorm_g': f(inputs['mlstm_norm_g']),
              'w_sconv': f(inputs['w_ssm_conv'])[0], 'b_sconv': f(inputs['b_ssm_conv']),
              'dt_bias': f(inputs['ssm_dt_bias'])[0], 'A_log': f(inputs['ssm_A_log'])[0],
              'ssm_D': f(inputs['ssm_D'])[0], 'snorm_g': f(inputs['ssm_norm_g']),
              'ln1_g': f(inputs['ln1_g'])[0], 'ln1_b': f(inputs['ln1_b'])[0],
              'w_fconv': f(inputs['w_ffn_conv'])[0], 'b_fconv': f(inputs['b_ffn_conv']),
              'ln2_g': f(inputs['ln2_g'])[0], 'ln2_b': f(inputs['ln2_b'])[0],
              'w_in': f(inputs['w_in'])[0], 'w_proj_a': f(inputs['w_proj_a'])[0],
              'w_proj_b': f(inputs['w_proj_b'])[0], 'w_out': f(inputs['w_out'])[0],
              'w_up': f(inputs['w_up'])[0], 'w_down': f(inputs['w_down'])[0]}
    maps = []
    for c in range(8):
        b = slice(16 * c, 16 * c + 16)
        m = dict(shared)
        m['xp'] = xp[c]
        m['xs'] = xs[b].reshape(64, D)
        m['s_mconv'] = f(inputs['state_mlstm_conv'])[0, b].reshape(48, 1024)
        m['s_C'] = f(inputs['state_mlstm_C'])[0, b]
        m['s_n'] = f(inputs['state_mlstm_n'])[0, b].reshape(64, 128)
        m['s_m'] = f(inputs['state_mlstm_m'])[0, b]
        m['s_sconv'] = f(inputs['state_ssm_conv'])[0, b].reshape(48, 3072)
        m['s_ssm'] = f(inputs['state_ssm'])[0, b].reshape(16, 2048, 128)
        m['s_fconv'] = f(inputs['state_ffn_conv'])[0, b].reshape(32, 2 * DFF)
        maps.append(m)
    return maps


def kernel(**inputs):
    kb = _get_kernel()
    maps = make_in_maps(kb, inputs)
    res = run_bass_kernel_spmd(kb.nc, maps, core_ids=list(range(8)))
    R = res.results
    cat = lambda k: np.stack([np.asarray(r[k], dtype=np.float32) for r in R])
    y_p = cat('y_p')
    y_s = cat('y_s').reshape(128, 4, D)
    p_mconv = cat('p_mconv')[None]
    p_C = cat('p_C')[None]
    p_n = cat('p_n')[None]
    p_m = cat('p_m').reshape(8, 4)[None]
    p_sconv = cat('p_sconv')[None]
    p_ssm = cat('p_ssm').reshape(8, 32, 64, 128)[None]
    p_fconv = cat('p_fconv')[None]
    s_mconv = cat('o_mconv').reshape(128, 3, 1024)[None]
    s_C = cat('o_C').reshape(128, 4, 128, 256)[None]
    s_n = cat('o_n').reshape(128, 4, 128)[None]
    s_m = cat('o_m').reshape(128, 4)[None]
    s_sconv = cat('o_sconv').reshape(128, 3, 3072)[None]
    s_ssm = cat('o_ssm').reshape(128, 32, 64, 128)[None]
    s_fconv = cat('o_fconv').reshape(128, 2, 2 * DFF)[None]
    return (y_p, y_s, p_mconv, p_C, p_n, p_m, p_sconv, p_ssm, p_fconv,
            s_mconv, s_C, s_n, s_m, s_sconv, s_ssm, s_fconv)
```

```python
import numpy as np
import ml_dtypes
import concourse.bass as bass
import concourse.mybir as mybir
from concourse.bass_utils import run_bass_kernel_spmd

F32 = mybir.dt.float32
BF16 = mybir.dt.bfloat16
ALU = mybir.AluOpType
AF = mybir.ActivationFunctionType
AX = mybir.AxisListType

D = 1024
DIN = 10280
DFF = 2816
NEG = -30000.0
ALPHA = 2.0 ** 0.25
LN_EPS = 1e-5
RMS_EPS = 1e-5
QSCALE = 128.0 ** -0.5


class Buf:
    def __init__(self, name, t, space):
        self.name = name
        self.t = t
        self.space = space
        self.last_w = None
        self.readers = []
        self.sem_in = None
        self.cnt_in = 0
        self.sem_out = None
        self.cnt_out = 0

    def __getitem__(self, idx):
        return View(self, self.t[idx])

    def ap(self):
        return View(self, self.t[:] if self.space != 'dram' else self.t)


class View:
    def __init__(self, buf, ap):
        self.buf = buf
        self.ap = ap

    def __getitem__(self, idx):
        return View(self.buf, self.ap[idx])

    def rearrange(self, *a, **k):
        return View(self.buf, self.ap.rearrange(*a, **k))

    def bc(self, axis, shape):
        return View(self.buf, self.ap.unsqueeze(axis).to_broadcast(list(shape)))

    def bitcast(self, dt):
        return View(self.buf, self.ap.bitcast(dt))


def _bufs(vs):
    out = []
    for v in vs:
        if v is None or isinstance(v, (int, float)):
            continue
        b = v.buf if isinstance(v, View) else v
        if b not in out:
            out.append(b)
    return out


class Sched:
    ENGS = ('pe', 'act', 'dve', 'pool', 'sp')

    def __init__(self, nc):
        self.nc = nc
        self.sem = {e: nc.alloc_semaphore('sem_' + e) for e in self.ENGS}
        self.cnt = {e: 0 for e in self.ENGS}
        self.ops = {e: [] for e in self.ENGS}
        self.seen = {e: {} for e in self.ENGS}
        self.final_tokens = []
        self.sb_off = 16512
        self.sb_end = 229376
        self.nsem = 5

    def sbuf(self, name, shape, dtype, at=None):
        nbytes = int(np.prod(shape[1:])) * (2 if dtype == BF16 else 4)
        nbytes = (nbytes + 31) // 32 * 32
        if at is None:
            at = self.sb_off
            self.sb_off += nbytes
            assert self.sb_off <= self.sb_end, ('SBUF overflow', name, self.sb_off)
        t = self.nc.alloc_sbuf_tensor_at(name, list(shape), dtype, offset=at)
        b = Buf(name, t, 'sbuf')
        b.off = at
        b.nbytes = nbytes
        return b

    def psum(self, name, shape, dtype=F32):
        t = self.nc.alloc_psum_tensor(name, list(shape), dtype)
        return Buf(name, t, 'psum')

    def dram(self, name, shape, dtype, kind):
        t = self.nc.dram_tensor(name, list(shape), dtype, kind=kind)
        return Buf(name, t.ap(), 'dram')

    def alias_phase(self, old, new):
        toks = []
        for b in old:
            if b.last_w is not None:
                toks.append(b.last_w)
            toks.extend(b.readers)
        for b in new:
            b.readers = list(b.readers) + toks

    def _need(self, eng, waits, tok):
        sem, val, teng = tok
        key = id(sem)
        if self.seen[eng].get(key, 0) >= val:
            return
        if key not in waits or waits[key][1] < val:
            waits[key] = (sem, val)

    def _deps(self, eng, reads, writes):
        waits = {}
        for b in reads:
            tok = b.last_w
            if tok is not None and not (tok[2] == eng and eng == 'pe'):
                self._need(eng, waits, tok)
            if b.space == 'psum':
                for r in b.readers:
                    if r[2] != eng:
                        self._need(eng, waits, r)
        for b in writes:
            tok = b.last_w
            if tok is not None and not (tok[2] == eng and eng == 'pe'):
                self._need(eng, waits, tok)
            for r in b.readers:
                if not (r[2] == eng and eng == 'pe'):
                    self._need(eng, waits, r)
        for key, (sem, val) in waits.items():
            self.seen[eng][key] = val
        return list(waits.values())

    def op(self, eng, fn, reads=(), writes=()):
        reads = _bufs(reads)
        writes = _bufs(writes)
        waits = self._deps(eng, reads, writes)
        self.cnt[eng] += 1
        tok = (self.sem[eng], self.cnt[eng], eng)
        self.ops[eng].append((waits, fn, (self.sem[eng], 1)))
        for b in writes:
            b.last_w = tok
            b.readers = []
        for b in reads:
            if b not in writes:
                b.readers.append(tok)
        return tok

    def dma(self, q, out, in_, **kw):
        ob, ib = out.buf, in_.buf
        waits = self._deps(q, [ib], [ob])
        kind = 'sw' if q == 'pool' else 'hw'
        if ob.space != 'dram':
            tab = ob.__dict__.setdefault('sems_in', {})
            if kind not in tab:
                tab[kind] = [self.nc.alloc_semaphore('din_%s_%s' % (kind, ob.name)), 0]
                self.nsem += 1
            tab[kind][1] += 16
            sem, val = tab[kind]
        else:
            tab = ib.__dict__.setdefault('sems_out', {})
            if kind not in tab:
                tab[kind] = [self.nc.alloc_semaphore('dout_%s_%s' % (kind, ib.name)), 0]
                self.nsem += 1
            tab[kind][1] += 16
            sem, val = tab[kind]
        tok = (sem, val, 'dma')
        oap, iap = out.ap, in_.ap

        def fn(e, oap=oap, iap=iap, kw=kw):
            return e.dma_start(out=oap, in_=iap, **kw)
        self.ops[q].append((waits, fn, (sem, 16)))
        ob.last_w = tok
        ob.readers = []
        ib.readers.append(tok)
        if ob.space == 'dram':
            self.final_tokens.append(tok)
        return tok

    def emit(self):
        nc = self.nc
        last = {}
        for sem, val, _ in self.final_tokens:
            k = id(sem)
            if k not in last or last[k][1] < val:
                last[k] = (sem, val)
        fin = list(last.values())
        eng_obj = {'pe': 'tensor', 'act': 'scalar', 'dve': 'vector', 'pool': 'gpsimd', 'sp': 'sync'}
        with nc.Block() as block:
            def mk(eng):
                def body(e):
                    for waits, fn, inc in self.ops[eng]:
                        for sem, val in waits:
                            e.wait_ge(sem, val)
                        fn(e).then_inc(inc[0], inc[1])
                    if eng == 'sp':
                        for sem, val in fin:
                            e.wait_ge(sem, val)
                return body
            for eng, attr in eng_obj.items():
                getattr(block, attr)(mk(eng))


def _const_tables():
    p = np.arange(128)[:, None]
    j = np.arange(128)[None, :]
    f = {}
    f['ident'] = (p == j)
    f['ones'] = np.ones((128, 128))
    f['U'] = (p <= j)
    f['LS'] = (p > j)
    sb = (p // 4 == j // 4) & (p < 64) & (j < 64)
    f['Us'] = ((p <= j) & sb)[:, :64]
    f['LSs'] = ((p > j) & sb)[:, :64]
    f['BOs'] = sb[:, :64]
    f['SELp'] = np.repeat(p == 127, 128, axis=1)
    f['SELm'] = np.repeat(p == 15, 128, axis=1)
    b16 = np.arange(16)[None, :]
    f['RS'] = (p == 4 * b16 + 3)
    f['BM'] = (p // 4 == b16) & (p < 64)
    f['BMT'] = ((p < 16) & (j // 4 == p))[:, :64]
    f['LSEL'] = ((j == 4 * (p // 4) + 3) & (p < 64))[:, :64]
    f['H0'] = np.repeat(p < 64, 128, axis=1) & (j < 64)
    f['H1'] = np.repeat(p < 64, 128, axis=1) & (j >= 64)
    cf_off, cols = {}, []
    o = 0
    for k, v in f.items():
        cf_off[k] = (o, v.shape[1])
        o += v.shape[1]
        cols.append(v.astype(np.float32))
    cf = np.concatenate(cols, axis=1)
    g = {}
    g['identb'] = (p == j).astype(np.float32)
    g['onesb'] = np.ones((128, 128), np.float32)
    g['M'] = np.where(j <= p, 0.0, NEG)
    g['MT'] = np.where(p <= j, 0.0, NEG)
    g['Ms'] = np.where((j <= p) & sb, 0.0, NEG)[:, :64]
    g['MTs'] = np.where((p <= j) & sb, 0.0, NEG)[:, :64]
    jj = np.arange(64)[None, None, :]
    bb = np.arange(16)[None, :, None]
    g['CM'] = np.broadcast_to((jj // 4 == bb), (128, 16, 64)).reshape(128, 1024).astype(np.float32)
    cb_off, cols = {}, []
    o = 0
    for k, v in g.items():
        cb_off[k] = (o, v.shape[1])
        o += v.shape[1]
        cols.append(np.asarray(v, np.float32))
    cbm = np.concatenate(cols, axis=1).astype(ml_dtypes.bfloat16)
    return cf, cf_off, cbm, cb_off


class Chunk:
    def __init__(self, slot, col0, n, kind, row0=0):
        self.slot, self.col0, self.n, self.kind, self.row0 = slot, col0, n, kind, row0


class Tile:
    def __init__(self, name, T, chunks, segs):
        self.name, self.T, self.chunks, self.segs = name, T, chunks, segs


W_SHAPES = {'w_in': (D, DIN), 'w_proj_a': (D, D), 'w_proj_b': (2 * D, D), 'w_out': (D, D),
            'w_up': (D, 2 * DFF), 'w_down': (DFF, D)}


def tile_blocks():
    bl = []
    for c in range(0, 3072, 256):
        bl.append(('w_in', 0, 8, [(c, 256)]))
    bl.append(('w_in', 0, 8, [(3072, 8), (8200, 32)]))
    for c in range(3080, 5128, 256):
        bl.append(('w_in', 0, 8, [(c, 256)]))
    for c in range(5128, 8200, 256):
        bl.append(('w_in', 0, 8, [(c, 256)]))
    for c in range(8232, 10280, 256):
        bl.append(('w_in', 0, 8, [(c, 256)]))
    for j in range(4):
        bl.append(('w_proj_a', 0, 8, [(256 * j, 256)]))
        bl.append(('w_proj_b', 0, 8, [(256 * j, 256)]))
        bl.append(('w_proj_b', 8, 8, [(256 * j, 256)]))
    for j in range(4):
        bl.append(('w_out', 0, 8, [(256 * j, 256)]))
    for j in range(11):
        bl.append(('w_up', 0, 8, [(256 * j, 256)]))
        bl.append(('w_up', 0, 8, [(DFF + 256 * j, 256)]))
    for j in range(4):
        for k0, nk in ((0, 8), (8, 8), (16, 6)):
            bl.append(('w_down', k0, nk, [(256 * j, 256)]))
    return bl


class K:
    def __init__(self, debug=None, tiles=('T0', 'T1', 'T2', 'T3', 'T4')):
        self.debug = debug or {}
        self.tile_sel = tuple(tiles)
        self.ntiles = len(self.tile_sel)
        nc = bass.Bass('TRN2', target_bir_lowering=False)
        self.nc = nc
        self.S = S = Sched(nc)
        self.dumps = {}
        cf, self.cfo, cbm, self.cbo = _const_tables()
        self.cf_np, self.cb_np = cf, cbm
        din = lambda n, s, dt=F32: S.dram(n, s, dt, 'ExternalInput')
        dout = lambda n, s: S.dram(n, s, F32, 'ExternalOutput')
        I = self.I = {}
        I['xp'] = din('xp', [2048, D]); I['xs'] = din('xs', [64, D]); I['meta'] = din('meta', [16, D])
        I['s_mconv'] = din('s_mconv', [48, 1024]); I['s_C'] = din('s_C', [16, 4, 128, 256])
        I['s_n'] = din('s_n', [64, 128]); I['s_m'] = din('s_m', [16, 4])
        I['s_sconv'] = din('s_sconv', [48, 3072]); I['s_ssm'] = din('s_ssm', [16, 2048, 128])
        I['s_fconv'] = din('s_fconv', [32, 2 * DFF])
        I['cf'] = din('cf', list(cf.shape)); I['cb'] = din('cb', list(cbm.shape), BF16)
        for n, s in (('ln0_g', [D]), ('ln0_b', [D]), ('b_if', [8]), ('w_mconv', [4, 1024]), ('b_mconv', [1, 1024]),
                     ('mnorm_g', [1, 1024]), ('w_sconv', [4, 3072]), ('b_sconv', [1, 3072]), ('dt_bias', [32]),
                     ('A_log', [32]), ('ssm_D', [32]), ('snorm_g', [1, 2048]), ('ln1_g', [D]), ('ln1_b', [D]),
                     ('w_fconv', [3, 2 * DFF]), ('b_fconv', [1, 2 * DFF]), ('ln2_g', [D]), ('ln2_b', [D])):
            I[n] = din(n, s)
        for n, s in W_SHAPES.items():
            I[n] = din(n, list(s))
        O = self.O = {}
        O['y_p'] = dout('y_p', [2048, D]); O['y_s'] = dout('y_s', [64, D])
        O['p_mconv'] = dout('p_mconv', [3, 1024]); O['p_C'] = dout('p_C', [4, 128, 256])
        O['p_n'] = dout('p_n', [4, 128]); O['p_m'] = dout('p_m', [1, 4])
        O['p_sconv'] = dout('p_sconv', [3, 3072]); O['p_ssm'] = dout('p_ssm', [2048, 128])
        O['p_fconv'] = dout('p_fconv', [2, 2 * DFF])
        O['o_mconv'] = dout('o_mconv', [48, 1024]); O['o_C'] = dout('o_C', [16, 4, 128, 256])
        O['o_n'] = dout('o_n', [64, 128]); O['o_m'] = dout('o_m', [16, 4])
        O['o_sconv'] = dout('o_sconv', [48, 3072]); O['o_ssm'] = dout('o_ssm', [16, 2048, 128])
        O['o_fconv'] = dout('o_fconv', [32, 2 * DFF])
        self.rr = {}
        self.build()
        S.emit()

    def rot(self, key, n):
        i = self.rr.get(key, 0)
        self.rr[key] = i + 1
        return i % n

    def mm(self, out, lhsT, rhs, start=True, stop=True):
        self.S.op('pe', lambda e: e.matmul(out.ap, lhsT=lhsT.ap, rhs=rhs.ap, start=start, stop=stop),
                  reads=[lhsT, rhs], writes=[out])

    def tr(self, out, in_, ident):
        self.S.op('pe', lambda e: e.transpose(out=out.ap, in_=in_.ap, identity=ident.ap),
                  reads=[in_, ident], writes=[out])

    def act(self, out, in_, func=AF.Copy, bias=None, scale=None, accum=None):
        kw = {}
        if bias is not None:
            kw['bias'] = bias.ap if isinstance(bias, View) else bias
        if scale is not None:
            kw['scale'] = scale.ap if isinstance(scale, View) else scale
        if accum is not None:
            kw['accum_out'] = accum.ap
        self.S.op('act', lambda e: e.activation(out=out.ap, in_=in_.ap, func=func, **kw),
                  reads=[in_, bias, scale], writes=[out, accum])

    def tt(self, out, a, b, op, eng='dve'):
        self.S.op(eng, lambda e: e.tensor_tensor(out=out.ap, in0=a.ap, in1=b.ap, op=op),
                  reads=[a, b], writes=[out])

    def ts(self, out, a, s1, op0, s2=None, op1=None, eng='dve', accum=None):
        v1 = s1.ap if isinstance(s1, View) else s1
        v2 = s2.ap if isinstance(s2, View) else s2
        kw = {}
        if op1 is not None:
            kw['op1'] = op1
        if accum is not None:
            kw['accum_out'] = accum.ap
        self.S.op(eng, lambda e: e.tensor_scalar(out=out.ap, in0=a.ap, scalar1=v1, scalar2=v2, op0=op0, **kw),
                  reads=[a, s1, s2], writes=[out, accum])

    def stt(self, out, a, s, b, op0, op1, eng='dve'):
        v = s.ap if isinstance(s, View) else s
        self.S.op(eng, lambda e: e.scalar_tensor_tensor(out=out.ap, in0=a.ap, scalar=v, in1=b.ap, op0=op0, op1=op1),
                  reads=[a, s, b], writes=[out])

    def cp(self, out, in_, eng='dve'):
        if eng == 'act':
            return self.act(out, in_)
        self.S.op(eng, lambda e: e.tensor_copy(out=out.ap, in_=in_.ap), reads=[in_], writes=[out])

    def memset(self, out, val, eng='dve'):
        self.S.op(eng, lambda e: e.memset(out.ap, val), writes=[out])

    def rmax(self, out, in_, eng='dve'):
        self.S.op(eng, lambda e: e.tensor_reduce(out=out.ap, in_=in_.ap, axis=AX.X, op=ALU.max),
                  reads=[in_], writes=[out])

    def dma(self, out, in_, q='sp'):
        if out.buf.space == 'dram' and q == 'sp' and not getattr(self, 'pass0', False):
            q = 'pool'
        self.S.dma(q, out, in_)

    def dump(self, name, view, shape):
        if name not in self.debug:
            return
        d = self.S.dram('dbg_' + name, list(shape), view.ap.dtype, 'ExternalOutput')
        self.dumps[name] = d
        self.dma(d.ap(), view)

    def cfv(self, name, rows=128, cols=None):
        o, w = self.cfo[name]
        cols = w if cols is None else cols
        return self.cf[:rows, o:o + cols]

    def cbv(self, name, rows=128, cols=None):
        o, w = self.cbo[name]
        cols = w if cols is None else cols
        return self.cb[:rows, o:o + cols]

    def ws_init(self):
        S = self.S
        self.wlist = tile_blocks()
        self.nbt = len(self.wlist)
        self.wblocks = self.wlist * self.ntiles
        self.wring = [S.sbuf(f'wring{i}', [128, 8, 256], BF16) for i in range(6)]
        self.wscr = [S.dram(f'wscr{j}', [128, 8, 256], BF16, 'Internal') for j in range(self.nbt)] if self.ntiles > 1 else None
        self.w_loaded = 0
        self.w_next = 0

    def _w_load0(self, i):
        name, k0, nk, parts = self.wblocks[i]
        dst = self.wring[i % 6]
        W = self.I[name]
        c = 0
        for (c0, n) in parts:
            src = View(W, W.t[k0 * 128:(k0 + nk) * 128, c0:c0 + n].rearrange('(k p) c -> p k c', p=128))
            self.S.dma('pool', dst[:, 0:nk, c:c + n], src)
            c += n
        n = c
        if name in ('w_proj_a', 'w_proj_b'):
            for k in range(nk):
                gcol = self.cwm[:, k0 + k, 5:6] if name == 'w_proj_a' else self.sng[:, k0 + k:k0 + k + 1]
                self.ts(dst[:, k, 0:n], dst[:, k, 0:n], gcol, ALU.mult)
        if self.wscr is not None:
            self.S.dma('sp', self.wscr[i][:, 0:nk, 0:n], dst[:, 0:nk, 0:n])

    def _w_ringload(self, i):
        name, k0, nk, parts = self.wblocks[i]
        n = sum(p[1] for p in parts)
        dst = self.wring[i % 6]
        self.dma(dst[:, 0:nk, 0:n], self.wscr[i % self.nbt][:, 0:nk, 0:n], q='sp')

    def wnext(self):
        i = self.w_next
        nb = len(self.wblocks)
        self.w_next += 1
        while self.w_loaded < min(nb, i + 6):
            if self.w_loaded < self.nbt:
                self._w_load0(self.w_loaded)
            else:
                self._w_ringload(self.w_loaded)
            self.w_loaded += 1
        return self.wring[i % 6], self.wblocks[i]

    def build(self):
        S = self.S
        sb = S.sbuf
        ncf, ncb = self.cf_np.shape[1], self.cb_np.shape[1]
        self.cf = sb('cf', [128, ncf], F32)
        self.cb = sb('cb', [128, ncb], BF16)
        self.lnc = sb('lnc', [128, 2, D], F32)
        self.bif_b = sb('bif_b', [128, 8], F32)
        self.dtb_b = sb('dtb_b', [128, 32], F32)
        self.A_b = sb('A_b', [128, 32], F32)
        self.D_b = sb('D_b', [128, 32], F32)
        self.Dfm = sb('Dfm', [128, 16], F32)
        self.cwm = sb('cwm', [128, 8, 6], F32)
        self.cws = sb('cws', [128, 24, 5], F32)
        self.sng = sb('sng', [128, 16], F32)
        self.cwf = sb('cwf', [128, 44, 4], F32)
        self.ws_init()
        self.xr = [sb(f'xr{i}', [128, D], F32) for i in range(4)]
        self.zs = [None] * 4
        self.xnT = sb('xnT', [128, 8, 512], BF16)
        self.hgT = sb('hgT', [128, 8, 512], BF16)
        self.ygT = sb('ygT', [128, 16, 512], BF16)
        self.Cf = sb('Cf', [128, 4, 256], F32); self.Cb = sb('Cb', [128, 4, 256], BF16)
        self.nf = sb('nf', [128, 4], F32); self.nb = sb('nb', [128, 4], BF16)
        self.m_b = sb('m_b', [128, 4], F32)
        self.STf = sb('STf', [128, 2048], F32); self.STb = sb('STb', [128, 2048], BF16)
        self.cq = sb('cq', [128, 8, 3], F32); self.cx = sb('cx', [128, 24, 3], F32)
        self.cff = sb('cff', [128, 44, 2], F32)
        self.scar = sb('scar', [128, 44 * 16 * 2], F32)
        self.gat = sb('gat', [128, 4, 8], F32)
        self.dta = sb('dta', [128, 4, 64], F32)
        self.ifdt = sb('ifdt', [128, 4, 40], F32)
        self.sm = [sb(f'sm{i}', [128, 32], F32) for i in range(16)]
        self.xb16 = sb('xb16', [128, D], BF16)
        self.lnsc = [sb(f'lnsc{i}', [128, 16], F32) for i in range(4)]
        self.cst = sb('cst', [128, 8], F32)
        self.pn_st = sb('pn_st', [128, 128], F32)
        R0 = S.sb_off
        o = R0
        def at(name, shape, dt):
            nonlocal o
            b = sb(name, shape, dt, at=o)
            o += b.nbytes
            return b
        self.cE = [at(f'cE{i}', [128, 520], F32) for i in range(2)]
        self.cacc = [at(f'cacc{i}', [128, 512], F32) for i in range(3)]
        self.cth = [at(f'cth{i}', [128, 512], F32) for i in range(2)]
        self.cacc2 = [at(f'cacc2_{i}', [128, 512], F32) for i in range(2)]
        e1 = o
        o = R0
        self.R1 = at('R1', [128, 4, 128], F32); self.R2 = at('R2', [128, 4, 128], F32)
        self.R3 = at('R3', [128, 4, 128], F32); self.wT = at('wT', [128, 4, 128], F32)
        self.ST = at('ST', [128, 4, 128], BF16); self.kTM = at('kTM', [128, 4, 128], BF16)
        self.hh = at('hh', [128, 4, 256], F32); self.vw = at('vw', [128, 4, 256], BF16)
        self.hgTM = at('hgTM', [128, D], BF16)
        e2 = o
        o = R0
        self.xdt = at('xdt', [128, 2048], BF16); self.xsD = at('xsD', [128, 2048], BF16)
        self.wx = at('wx', [128, 512], BF16); self.BTM = at('BTM', [128, 512], BF16)
        self.LT = at('LT', [128, 8, 128], BF16); self.MTt = at('MTt', [128, 8, 128], BF16)
        self.t1 = at('t1', [128, 512], F32)
        self.yz = at('yz', [128, 2048], BF16)
        self.ynTM = at('ynTM', [128, 2048], BF16)
        e3 = o
        F0 = max(e1, e2, e3)
        conv_end = F0
        o = F0
        self.qkT = at('qkT', [128, 8, 512], BF16)
        self.v = [at(f'v{i}', [128, D], BF16) for i in range(4)]
        self.oth = [at(f'oth{i}', [128, D], BF16) for i in range(4)]
        a1_end = o
        o = F0
        self.xbcT = at('xbcT', [128, 24, 512], BF16)
        for i in range(2):
            self.zs[i] = at(f'zs{i}', [128, 2048], BF16)
        self.LA = at('LA', [128, 8, 128], F32)
        a2_end = o
        o = F0
        self.gth = at('gth', [128, 16, 512], BF16)
        self.mixT = at('mixT', [128, 8, 512], BF16)
        self.hffT = at('hffT', [128, 22, 512], BF16)
        b_end = o
        S.sb_off = max(a1_end, a2_end, b_end)
        for i in range(2, 4):
            self.zs[i] = sb(f'zs{i}', [128, 2048], BF16)
        self.arenas = [(self.xr[2].off, 2 * self.xr[2].nbytes), (self.zs[2].off, 2 * self.zs[2].nbytes)]
        a0, a1 = self.arenas[0][0], self.arenas[1][0]
        self.C0b = [sb(f'C0b{i}', [128, 4, 256], F32, at=a0 + 4096 * i) for i in range(2)]
        self.C0b16 = [sb(f'C0b16_{i}', [128, 4, 258], BF16, at=a1 + 2080 * i) for i in range(2)]
        self.qmb = [sb(f'qmb{i}', [128, 4, 64], BF16, at=a1 + 4160 + 512 * i) for i in range(2)]
        self.kTMm = [sb(f'kTMm{i}', [128, 4, 128], BF16, at=a1 + 5184 + 1024 * i) for i in range(2)]
        self.n0T = sb('n0T', [128, 64], F32, at=a1 + 7232)
        self.n16 = sb('n16', [128, 64], BF16, at=a1 + 7488)
        self.decS = sb('decS', [128, 64], F32, at=a1 + 7616)
        self.Rm = sb('Rm', [128, 64], F32, at=a1 + 7872)
        self.S0b = [sb('S0b0', [128, 16, 128], F32, at=a0), sb('S0b1', [128, 16, 128], F32, at=self.LT.off)]
        assert self.LT.off + 8192 <= self.ynTM.off
        self.SbT = sb('SbT', [128, 2048], BF16, at=a1)
        self.wxm = sb('wxm', [128, 2048], BF16, at=a1 + 4096)
        self.ysi = sb('ysi', [128, 2048], BF16)
        self.decP = sb('decP', [128, 256], F32)
        self.Rr = sb('Rr', [128, 2, 256], F32)
        self.CTmb = [sb(f'CTmb{i}', [128, 4, 64], BF16) for i in range(2)]
        self.grpArena2 = [self.S0b[0], self.SbT, self.wxm]
        self.xsDT = sb('xsDT', [128, 8, 128], BF16, at=self.ynTM.off)
        self.R1h = [sb(f'R1h{j}', [128, 128], F32, at=self.R1.off + 512 * j) for j in range(4)]
        self.R2h = [sb(f'R2h{j}', [128, 128], F32, at=self.R2.off + 512 * j) for j in range(4)]
        self.R3h = [sb(f'R3h{j}', [128, 128], F32, at=self.R3.off + 512 * j) for j in range(4)]
        self.hhh = [sb(f'hhh{j}', [128, 256], F32, at=self.hh.off + 1024 * j) for j in range(4)]
        self.sinter = [sb(f'sint{j}', [128, 256], F32, at=(self.R1.off if j < 2 else self.R2.off) + 1024 * (j % 2)) for j in range(4)]
        self.LAh = [sb(f'LAh{j}', [128, 128], F32, at=self.LA.off + 512 * j) for j in range(8)]
        self.LTh = [sb(f'LTh{j}', [128, 4, 128], BF16, at=self.LT.off + 1024 * j) for j in range(2)]
        self.MTh = [[sb(f'MTh{b}_{j}', [128, 128], BF16, at=base + 256 * j) for j in range(8)]
                    for b, base in enumerate((self.MTt.off, self.ynTM.off + 2048))]
        self.grpArena = self.C0b + self.C0b16 + self.qmb + self.kTMm + [self.n0T, self.n16, self.decS, self.Rm]
        self.grpA1conv = self.cE + self.cacc + self.cth + self.cacc2
        self.grpA1rec = [self.wT, self.ST, self.kTM, self.vw, self.hgTM] + self.R1h + self.R2h + self.R3h + self.hhh
        self.grpA1fix = [self.qkT] + self.v + self.oth
        self.grpA2fix = [self.xbcT, self.zs[0], self.zs[1]] + self.LAh
        self.grpA2rec = [self.xdt, self.xsD, self.wx, self.BTM, self.t1, self.yz, self.ynTM] + self.LTh + self.MTh[0]
        self.grpB = [self.gth, self.mixT, self.hffT]
        print('SBUF used', S.sb_off, 'of', S.sb_end, 'R', R0, conv_end - R0, a1_end - R0, a2_end - R0, b_end - R0)
        self.ps = [S.psum(f'ps{i}', [128, 512], F32) for i in range(8)]
        self.setup()
        tiles = self.make_tiles()
        first = True
        for tl in tiles:
            if tl.name in self.tile_sel:
                self.pass0 = first
                self.run_tile(tl, last=(tl.name == 'T4'))
                first = False

    def make_tiles(self):
        def pch(slot, col0, c):
            ch = Chunk(slot, col0, 128, 'p', row0=128 * c)
            ch.final = (c == 15)
            return ch
        m = Chunk(0, 0, 16, 'm'); m.final = False
        tiles = [Tile('T0', 400, [m] + [pch(1 + i, 16 + 128 * i, i) for i in range(3)], [(0, 1, 400, 'p')])]
        for t in range(3):
            tiles.append(Tile(f'T{t + 1}', 512, [pch(i, 128 * i, 3 + 4 * t + i) for i in range(4)], [(0, 1, 512, 'p')]))
        sc = Chunk(1, 128, 64, 's'); sc.final = False
        tiles.append(Tile('T4', 192, [pch(0, 0, 15), sc], [(0, 1, 128, 'p'), (128, 16, 4, 's')]))
        return tiles

    def psb(self, i):
        return self.ps[i].ap().bitcast(BF16)

    def setup(self):
        I = self.I
        self.dma(self.cf.ap(), I['cf'].ap())
        self.dma(self.cb.ap(), I['cb'].ap())
        pb = lambda n: View(I[n], I[n].t.partition_broadcast(128))
        self.dma(self.bif_b.ap(), pb('b_if'))
        self.dma(self.dtb_b.ap(), pb('dt_bias'))
        self.dma(self.A_b.ap(), pb('A_log'))
        self.dma(self.D_b.ap(), pb('ssm_D'))
        self.act(self.A_b.ap(), self.A_b.ap(), AF.Exp)
        self.ts(self.A_b.ap(), self.A_b.ap(), -1.0, ALU.mult)
        D3 = self.D_b.ap().rearrange('p (g r) -> p g r', r=2)
        self.cp(self.Dfm[0:64, :], D3[0:64, :, 0], 'dve')
        self.cp(self.Dfm[64:128, :], D3[64:128, :, 1], 'dve')
        identf = self.cfv('ident')
        stg = self.cacc[0]
        def fm_params(dst, rows, G, scale_groups=None):
            R = sum(r for _, r in rows)
            for g0 in range(0, G, 4):
                gn = min(4, G - g0)
                r0 = 0
                for (nm, nr) in rows:
                    self.dma(stg[r0:r0 + nr, 0:gn * 128], I[nm][:, g0 * 128:(g0 + gn) * 128])
                    r0 += nr
                bank = self.ps[self.rot('setup', 2)]
                for g in range(gn):
                    self.tr(bank[:, g * R:(g + 1) * R], stg[0:R, g * 128:(g + 1) * 128], identf[0:R, 0:R])
                self.cp(dst[:, g0:g0 + gn, :], bank[:, 0:gn * R].rearrange('p (g r) -> p g r', r=R), 'act')
        fm_params(self.cwm, [('w_mconv', 4), ('b_mconv', 1), ('mnorm_g', 1)], 8)
        fm_params(self.cws, [('w_sconv', 4), ('b_sconv', 1)], 24)
        fm_params(self.cwf, [('w_fconv', 3), ('b_fconv', 1)], 44)
        sng3 = self.sng.ap().rearrange('p (g r) -> p g r', r=1)
        fm_params(sng3, [('snorm_g', 1)], 16)
        self.ts(self.cwm[:, :, 0:6], self.cwm[:, :, 0:6], 0.5, ALU.mult)
        self.ts(self.cws.ap(), self.cws.ap(), 0.5, ALU.mult)
        self.ts(self.cwf[:, 0:22, :], self.cwf[:, 0:22, :], 0.5, ALU.mult)
        self.memset(self.cst[:, 0:1], LN_EPS)
        self.memset(self.cst[:, 1:2], 0.5 * float(np.log(128.0)))
        self.memset(self.cst[:, 2:3], 1.0)
        self.eps_t = self.cst
        for b in (self.Cf, self.nf, self.m_b, self.STf, self.cq, self.cx, self.cff):
            self.memset(b.ap(), 0.0)
        for b in (self.Cb, self.nb, self.STb):
            self.memset(b.ap(), 0.0, 'pool')

    def kc(self, kind, n):
        if kind == 's':
            return dict(U=self.cfv('Us', 64), LS=self.cfv('LSs', 64), BO=self.cfv('BOs', 64),
                        M=self.cbv('Ms', 64), MT=self.cbv('MTs', 64))
        return dict(U=self.cfv('U', n, n), LS=self.cfv('LS', n, n), BO=self.cfv('ones', n, n),
                    M=self.cbv('M', n, n), MT=self.cbv('MT', n, n))

    def ln_load(self, gname, bname):
        I = self.I
        self.dma(self.lnc[:, 0, :], View(I[gname], I[gname].t.partition_broadcast(128)))
        self.dma(self.lnc[:, 1, :], View(I[bname], I[bname].t.partition_broadcast(128)))

    def ln_rows(self, x, n, slot):
        sc = self.lnsc[slot]
        st, mv, rs = sc[:n, 0:12], sc[:n, 12:14], sc[:n, 14:15]
        for i in range(2):
            self.S.op('dve', lambda e, i=i: e.bn_stats(out=sc.t[:n, i * 6:(i + 1) * 6], in_=x.ap[:, i * 512:(i + 1) * 512]),
                      reads=[x], writes=[sc])
        self.S.op('dve', lambda e: e.bn_aggr(out=sc.t[:n, 12:14], in_=sc.t[:n, 0:12]), reads=[sc], writes=[sc])
        self.act(rs, sc[:n, 13:14], AF.Ln, bias=self.eps_t[:n, 0:1])
        self.act(rs, rs, AF.Exp, scale=-0.5)
        self.ts(x, x, sc[:n, 12:13], ALU.subtract, rs, ALU.mult)
        self.tt(x, x, self.lnc[:n, 0, :], ALU.mult)
        self.tt(x, x, self.lnc[:n, 1, :], ALU.add)

    def to_fm(self, tl, src_of_chunk, dstT):
        identb = self.cbv('identb')
        for ch in tl.chunks:
            n = ch.n
            xb = self.xb16
            self.act(xb[:n, :], src_of_chunk(ch))
            bank = 6 + self.rot('tfm', 2)
            pv = self.psb(bank)
            for k in range(8):
                self.tr(pv[:, k * n:(k + 1) * n], xb[:n, k * 128:(k + 1) * 128], identb[:n, :n])
            self.cp(dstT[:, :, ch.col0:ch.col0 + n], pv[:, 0:8 * n].rearrange('p (k n) -> p k n', n=n), 'dve')

    def _conv_taps(self, tl, psv, W, wtab, g, carry_p, scar_view, E, acc):
        Wm = W - 1
        off = 0
        for (col0, nb, L, kind) in tl.segs:
            Ev = E[:, off:off + nb * (L + Wm)].rearrange('p (b l) -> p b l', b=nb)
            pseg = psv[:, col0:col0 + nb * L].rearrange('p (b l) -> p b l', b=nb)
            if kind == 'p':
                self.cp(Ev[:, :, 0:Wm], carry_p[:, g:g + 1, :], 'act')
            elif kind == 'm':
                self.memset(Ev[:, :, 0:Wm], 0.0, 'dve')
            else:
                self.cp(Ev[:, :, 0:Wm], scar_view[:, g, :, :], 'act')
            self.act(Ev[:, :, Wm:Wm + L], pseg)
            av = acc[:, col0:col0 + nb * L].rearrange('p (b l) -> p b l', b=nb)
            self.act(av, pseg, AF.Identity, scale=wtab[:, g, Wm:W], bias=wtab[:, g, W:W + 1])
            if kind == 's':
                self.cp(scar_view[:, g, :, :], Ev[:, :, L:L + Wm], 'act')
            else:
                self.cp(carry_p[:, g:g + 1, :], Ev[:, :, L:L + Wm], 'act')
            for j in range(Wm):
                self.stt(av, Ev[:, :, j:j + L], wtab[:, g, j:j + 1], av, ALU.mult, ALU.add)
            off += nb * (L + Wm)

    def conv_group(self, tl, psv, W, wtab, g, carry_p, scar_view, dst, final=True):
        E = self.cE[self.rot('cE', 2)]
        acc = self.cacc[self.rot('cacc', 3)]
        self._conv_taps(tl, psv, W, wtab, g, carry_p, scar_view, E, acc)
        T = tl.T
        if not final:
            return acc
        prev = getattr(self, '_conv_pending', None)

        def stage2(acc=acc, dst=dst, T=T):
            th = self.cth[self.rot('cth', 2)]
            self.act(th[:, 0:T], acc[:, 0:T], AF.Tanh)
            self.stt(dst, th[:, 0:T], 1.0, acc[:, 0:T], ALU.add, ALU.mult)
        self._conv_pending = stage2
        if prev is not None:
            prev()
        return acc

    def conv_flush(self):
        prev = getattr(self, '_conv_pending', None)
        self._conv_pending = None
        if prev is not None:
            prev()

    def carry_out(self, src, G, R, dst):
        identf = self.cfv('ident')
        for g0 in range(0, G, 4):
            gn = min(4, G - g0)
            bank = self.ps[self.rot('co', 2)]
            for g in range(gn):
                self.tr(bank[:R, g * 128:(g + 1) * 128], src[:, g0 + g, :], identf)
            stg = self.cacc[self.rot('cacc', 3)]
            self.cp(stg[:R, 0:gn * 128], bank[:R, 0:gn * 128], 'act')
            self.dma(dst[:, g0 * 128:(g0 + gn) * 128], stg[:R, 0:gn * 128])

    def scar_in(self, name, G, R):
        identf = self.cfv('ident')
        rows = 16 * R
        sv = self.scar[:, 0:G * rows].rearrange('p (g b r) -> p g b r', g=G, b=16)
        for g0 in range(0, G, 4):
            gn = min(4, G - g0)
            stg = self.cacc[self.rot('cacc', 3)]
            self.dma(stg[:rows, 0:gn * 128], self.I[name][:, g0 * 128:(g0 + gn) * 128])
            bank = self.ps[self.rot('co', 2)]
            for g in range(gn):
                self.tr(bank[:, g * rows:(g + 1) * rows], stg[:rows, g * 128:(g + 1) * 128], identf[:rows, :rows])
            self.cp(self.scar[:, g0 * rows:(g0 + gn) * rows], bank[:, 0:gn * rows], 'act')
        return sv

    def scar_out(self, name, G, R):
        rows = 16 * R
        src = self.scar[:, 0:G * rows].rearrange('p (g br) -> p g br', g=G)
        self.carry_out(src, G, rows, self.O[name].ap())

    def dense_fm(self, tl, actT, nkt_total, cb_group, kt0=0):
        Wb, (name, k0, nk, parts) = self.wnext()
        ncols = sum(p[1] for p in parts)
        T = tl.T
        for gl in range(ncols // 128):
            bank = self.ps[self.rot('mm', 4)]
            for k in range(nk):
                self.mm(bank[:, 0:T], Wb[:, k, gl * 128:(gl + 1) * 128], actT[:, k0 + k, 0:T],
                        start=(k0 + k == 0), stop=(k0 + k == nkt_total - 1))
            cb_group(gl, bank[:, 0:T])

    def dense_tm(self, tl, actT, cb_chunk):
        Wb, (name, k0, nk, parts) = self.wnext()
        ncols = sum(p[1] for p in parts)
        for ch in tl.chunks:
            bank = self.ps[self.rot('mm', 4)]
            for k in range(nk):
                self.mm(bank[:ch.n, 0:ncols], actT[:, k0 + k, ch.col0:ch.col0 + ch.n], Wb[:, k, 0:ncols],
                        start=(k == 0), stop=(k == nk - 1))
            cb_chunk(ch, bank[:ch.n, 0:ncols])

    def run_tile(self, tl, last):
        S, I, O = self.S, self.I, self.O
        T = tl.T
        isS = any(sg[3] == 's' for sg in tl.segs)
        if isS:
            S.alias_phase([self.xr[2], self.xr[3], self.zs[2], self.zs[3]], self.grpArena + self.grpArena2)
        for ch in tl.chunks:
            src = {'s': I['xs'].ap(), 'm': I['meta'].ap()}.get(ch.kind)
            if src is None:
                src = I['xp'][ch.row0:ch.row0 + ch.n, :]
            self.dma(self.xr[ch.slot][:ch.n, :], src)
        self.ln_load('ln0_g', 'ln0_b')
        for ch in tl.chunks:
            self.ln_rows(self.xr[ch.slot][:ch.n, :], ch.n, ch.slot)
        self.to_fm(tl, lambda ch: self.xr[ch.slot][:ch.n, :], self.xnT)
        for ch in tl.chunks:
            self.ts(self.xr[ch.slot][:ch.n, :], self.xr[ch.slot][:ch.n, :], ALPHA, ALU.mult)
        self.dump('xnT_' + tl.name, self.xnT[:, :, 0:T], [128, 8, T])
        if self.debug.get('stop') == 'p0':
            return
        S.alias_phase(self.grpA2fix + self.grpA2rec + self.grpB + self.grpA1rec, self.grpA1conv + self.grpA1fix)
        sq = self.scar_in('s_mconv', 8, 3) if isS else None
        for blk in range(4):
            def cbq(gl, psv, blk=blk):
                g = 2 * blk + gl
                self.conv_group(tl, psv, 4, self.cwm, g, self.cq, sq, self.qkT[:, g, 0:T])
            self.dense_fm(tl, self.xnT, 8, cbq)
        self.conv_flush()
        if isS:
            self.scar_out('o_mconv', 8, 3)
        if last:
            self.carry_out(self.cq.ap(), 8, 3, O['p_mconv'].ap())
        for blk in range(4):
            self.dense_tm(tl, self.xnT, lambda ch, psv, blk=blk: self.act(self.v[ch.slot][:ch.n, 256 * blk:256 * blk + 256], psv))
        for blk in range(4):
            self.dense_tm(tl, self.xnT, lambda ch, psv, blk=blk: self.act(self.oth[ch.slot][:ch.n, 256 * blk:256 * blk + 256], psv, AF.Tanh, scale=0.5))
        self.dense_tm(tl, self.xnT, lambda ch, psv: self.cp(self.ifdt[:ch.n, ch.slot, :], psv, 'dve'))
        for ch in tl.chunks:
            n, s = ch.n, ch.slot
            gi = self.gat[:n, s, 0:8]
            self.tt(gi, self.ifdt[:n, s, 0:8], self.bif_b[:n, :], ALU.add)
            e1 = self.sm[3]
            self.act(e1[:n, 0:4], self.gat[:n, s, 4:8], AF.Exp, scale=-1.0)
            self.act(e1[:n, 0:4], e1[:n, 0:4], AF.Ln, bias=self.cst[:n, 2:3])
            self.ts(self.gat[:n, s, 4:8], e1[:n, 0:4], -1.0, ALU.mult)
            d1 = self.sm[4]
            self.tt(d1[:n, 0:32], self.ifdt[:n, s, 8:40], self.dtb_b[:n, :], ALU.add)
            self.act(d1[:n, 0:32], d1[:n, 0:32], AF.Exp)
            self.act(self.dta[:n, s, 0:32], d1[:n, 0:32], AF.Ln, bias=self.cst[:n, 2:3])
            self.tt(self.dta[:n, s, 32:64], self.dta[:n, s, 0:32], self.A_b[:n, :], ALU.mult)
        self.dump('qkT_' + tl.name, self.qkT[:, :, 0:T], [128, 8, T])
        self.dump('gat_' + tl.name, self.gat.ap(), [128, 4, 8])
        self.dump('dta_' + tl.name, self.dta.ap(), [128, 4, 64])
        if self.debug.get('stop') == 'a1':
            return
        S.alias_phase(self.grpA1conv, self.grpA1rec)
        for ch in tl.chunks:
            self.mlstm_chunk(tl, ch, last)
        self.dump('hgT_' + tl.name, self.hgT[:, :, 0:T], [128, 8, T])
        if self.debug.get('stop') == 'mlstm':
            return
        S.alias_phase(self.grpA1rec + self.grpA1fix, self.grpA1conv + self.grpA2fix)
        for blk in range(8):
            def cbz(ch, psv, blk=blk):
                n = ch.n
                zc = self.cacc[self.rot('cacc', 3)]
                th = self.cth[self.rot('cth', 2)]
                self.cp(zc[:n, 0:256], psv, 'act')
                self.act(th[:n, 0:256], psv, AF.Tanh, scale=0.5)
                self.stt(self.zs[ch.slot][:n, 256 * blk:256 * blk + 256], th[:n, 0:256], 1.0, zc[:n, 0:256], ALU.add, ALU.mult)
            self.dense_tm(tl, self.xnT, cbz)
        if self.debug.get('stop') == 'a2z':
            return
        sx = self.scar_in('s_sconv', 24, 3) if isS else None
        for blk in range(12):
            def cbx(gl, psv, blk=blk):
                g = 2 * blk + gl
                self.conv_group(tl, psv, 4, self.cws, g, self.cx, sx, self.xbcT[:, g, 0:T])
            self.dense_fm(tl, self.xnT, 8, cbx)
        self.conv_flush()
        if isS:
            self.scar_out('o_sconv', 24, 3)
        if last:
            self.carry_out(self.cx.ap(), 24, 3, O['p_sconv'].ap())
        self.dump('xbcT_' + tl.name, self.xbcT[:, :, 0:T], [128, 24, T])
        if self.debug.get('stop') == 'a2':
            return
        S.alias_phase(self.grpA1conv, self.grpA2rec)
        for ch in tl.chunks:
            self.ssd_chunk(tl, ch, last)
        self.dump('ygT_' + tl.name, self.ygT[:, :, 0:T], [128, 16, T])
        if self.debug.get('stop') == 'ssd':
            return
        S.alias_phase(self.grpA2rec + self.grpA2fix, self.grpA1conv + self.grpB)
        for blk in range(8):
            def cbg(gl, psv, blk=blk):
                self.act(self.gth[:, 2 * blk + gl, 0:T], psv, AF.Tanh, scale=0.5)
            self.dense_fm(tl, self.xnT, 8, cbg)
        for j in range(4):
            Wb, (name, k0, nk, parts) = self.wnext()
            banksA = [self.ps[0], self.ps[1]]
            banksB = [self.ps[2], self.ps[3]]
            for gl in range(2):
                for k in range(8):
                    self.mm(banksA[gl][:, 0:T], Wb[:, k, gl * 128:(gl + 1) * 128], self.hgT[:, k, 0:T], start=(k == 0), stop=(k == 7))
            for half in range(2):
                Wb, (name, k0, nk, parts) = self.wnext()
                for gl in range(2):
                    for k in range(8):
                        kk = 8 * half + k
                        self.mm(banksB[gl][:, 0:T], Wb[:, k, gl * 128:(gl + 1) * 128], self.ygT[:, kk, 0:T], start=(kk == 0), stop=(kk == 15))
            for gl in range(2):
                g = 2 * j + gl
                m1 = self.cacc[self.rot('cacc', 3)]
                m2 = self.cacc2[self.rot('cacc2', 2)]
                self.stt(m1[:, 0:T], self.gth[:, g, 0:T], 1.0, banksA[gl][:, 0:T], ALU.add, ALU.mult)
                self.stt(m2[:, 0:T], self.gth[:, 8 + g, 0:T], 1.0, banksB[gl][:, 0:T], ALU.add, ALU.mult)
                self.tt(self.mixT[:, g, 0:T], m1[:, 0:T], m2[:, 0:T], ALU.add)
        self.rr['mm'] = 0
        for blk in range(4):
            def cbo(ch, psv, blk=blk):
                xv = self.xr[ch.slot][:ch.n, 256 * blk:256 * blk + 256]
                self.stt(xv, psv, 0.5, xv, ALU.mult, ALU.add)
            self.dense_tm(tl, self.mixT, cbo)
        self.ln_load('ln1_g', 'ln1_b')
        for ch in tl.chunks:
            self.ln_rows(self.xr[ch.slot][:ch.n, :], ch.n, ch.slot)
        self.dump('x1_' + tl.name, self.xr[0].ap(), [128, D])
        self.to_fm(tl, lambda ch: self.xr[ch.slot][:ch.n, :], self.xnT)
        for ch in tl.chunks:
            self.ts(self.xr[ch.slot][:ch.n, :], self.xr[ch.slot][:ch.n, :], ALPHA, ALU.mult)
        sf = self.scar_in('s_fconv', 44, 2) if isS else None
        for j in range(11):
            accs = {}
            def cbua(gl, psv, j=j):
                g = 2 * j + gl
                accs[gl] = self.conv_group(tl, psv, 3, self.cwf, g, self.cff, sf, None, final=False)
            self.dense_fm(tl, self.xnT, 8, cbua)
            def cbub(gl, psv, j=j):
                g = 2 * j + gl
                E = self.cE[self.rot('cE', 2)]
                accb = self.cacc2[self.rot('cacc2', 2)]
                self._conv_taps(tl, psv, 3, self.cwf, 22 + g, self.cff, sf, E, accb)
                th = self.cth[self.rot('cth', 2)]
                acca = accs[gl]
                self.act(th[:, 0:T], acca[:, 0:T], AF.Tanh)
                self.stt(th[:, 0:T], th[:, 0:T], 1.0, acca[:, 0:T], ALU.add, ALU.mult)
                self.tt(self.hffT[:, g, 0:T], th[:, 0:T], accb[:, 0:T], ALU.mult)
            self.dense_fm(tl, self.xnT, 8, cbub)
        if isS:
            self.scar_out('o_fconv', 44, 2)
        if last:
            self.carry_out(self.cff.ap(), 44, 2, O['p_fconv'].ap())
        self.dump('hffT_' + tl.name, self.hffT[:, :, 0:T], [128, 22, T])
        for blk in range(4):
            banks = {ch.slot: self.ps[ch.slot] for ch in tl.chunks}
            for (k0, nk) in ((0, 8), (8, 8), (16, 6)):
                Wb, meta = self.wnext()
                for ch in tl.chunks:
                    for k in range(nk):
                        self.mm(banks[ch.slot][:ch.n, 0:256], self.hffT[:, k0 + k, ch.col0:ch.col0 + ch.n], Wb[:, k, 0:256],
                                start=(k0 + k == 0), stop=(k0 + k == 21))
            for ch in tl.chunks:
                xv = self.xr[ch.slot][:ch.n, 256 * blk:256 * blk + 256]
                self.tt(xv, banks[ch.slot][:ch.n, 0:256], xv, ALU.add)
        self.ln_load('ln2_g', 'ln2_b')
        for ch in tl.chunks:
            self.ln_rows(self.xr[ch.slot][:ch.n, :], ch.n, ch.slot)
            if ch.kind == 'p':
                self.dma(O['y_p'][ch.row0:ch.row0 + ch.n, :], self.xr[ch.slot][:ch.n, :])
            elif ch.kind == 's':
                self.dma(O['y_s'].ap(), self.xr[ch.slot][:ch.n, :])

    def mlstm_chunk(self, tl, ch, last):
        I, O = self.I, self.O
        n, s, c0, kind = ch.n, ch.slot, ch.col0, ch.kind
        kc = self.kc(kind, n)
        U, LS, M, MT = kc['U'], kc['LS'], kc['M'], kc['MT']
        identf, onesf = self.cfv('ident', n, n), self.cfv('ones', n, n)
        identb, onesb = self.cbv('identb', n, n), self.cbv('onesb', n, n)
        ps = self.ps
        cols = slice(c0, c0 + n)
        ig = self.gat[:n, s, 0:4]
        lf = self.gat[:n, s, 4:8]
        sm = self.sm
        bt, mi, bm, mt, wi, emt, negm, den, rden, wi2 = (sm[i] for i in range(5, 15))
        R1, R2, R3, wT, ST, kTM, hh, vw = self.R1, self.R2, self.R3, self.wT, self.ST, self.kTM, self.hh, self.vw
        self.mm(ps[0][:n, 0:4], U, lf)
        for h in range(4):
            self.ts(self.R1h[h][:n, :n], LS, self.gat[:n, s, 4 + h:5 + h], ALU.mult)
            self.act(self.R2h[h][:n, :n], identf, AF.Copy, scale=self.gat[:n, s, h:h + 1])
        for h in range(4):
            self.mm(ps[1][:n, h * 128:h * 128 + n], U, self.R1h[h][:n, :n], start=True, stop=False)
            self.mm(ps[1][:n, h * 128:h * 128 + n], onesf, self.R2h[h][:n, :n], start=False, stop=False)
            self.mm(ps[1][:n, h * 128:h * 128 + n], identb, M, start=False, stop=True)
        self.rmax(mi[:n, 0:4], ps[1][:n, :].rearrange('p (h t) -> p h t', h=4)[:, :, 0:n])
        if kind == 's':
            m0s = sm[15]
            self.dma(m0s[:16, 0:4], I['s_m'].ap())
            self.mm(ps[0][:n, 4:8], self.cfv('BMT', 16), m0s[:16, 0:4])
            m0v = ps[0][:n, 4:8]
        else:
            m0v = self.m_b[:n, :]
        self.cp(bt[:n, 0:4], ps[0][:n, 0:4], 'act')
        self.tt(bm[:n, 0:4], bt[:n, 0:4], m0v, ALU.add)
        self.tt(mt[:n, 0:4], bm[:n, 0:4], mi[:n, 0:4], ALU.max)
        self.tt(bm[:n, 0:4], bm[:n, 0:4], mt[:n, 0:4], ALU.subtract)
        self.act(wi[:n, 0:4], bm[:n, 0:4], AF.Exp)
        self.act(emt[:n, 0:4], mt[:n, 0:4], AF.Exp, scale=-1.0, bias=self.cst[:n, 1:2])
        self.ts(negm[:n, 0:4], mt[:n, 0:4], -1.0, ALU.mult)
        for h in range(4):
            self.act(self.R3h[h][:n, :n], identf, AF.Copy, scale=negm[:n, h:h + 1])
        for h in range(4):
            o = ps[2][:n, h * 128:h * 128 + n]
            self.mm(o, self.R1h[h][:n, :n], U, start=True, stop=False)
            self.mm(o, self.R2h[h][:n, :n], onesf, start=False, stop=False)
            self.mm(o, onesf, self.R3h[h][:n, :n], start=False, stop=False)
            self.mm(o, identb, MT, start=False, stop=True)
        ps2v = ps[2][:n, :].rearrange('p (h t) -> p h t', h=4)[:, :, 0:n]
        self.act(wT[:n, :, :n], ps2v, AF.Exp)
        for h in range(4):
            self.mm(ps[1][:n, h * 128:h * 128 + n], self.qkT[:, 4 + h, cols], self.qkT[:, h, cols])
        ps1v = ps[1][:n, :].rearrange('p (h t) -> p h t', h=4)[:, :, 0:n]
        self.tt(ST[:n, :, :n], ps1v, wT[:n, :, :n], ALU.mult)
        pb7 = self.psb(7)
        for h in range(4):
            self.tr(pb7[:n, h * 128:(h + 1) * 128], self.qkT[:, 4 + h, cols], self.cbv('identb'))
        self.cp(kTM[:n, :, :], pb7[:n, 0:512].rearrange('p (h d) -> p h d', h=4), 'act')
        if kind == 's':
            self.mlstm_sample_states(tl, ch, wi, wT, kTM, mt)
        for h in range(4):
            self.mm(ps[3 + h // 2][:n, (h % 2) * 256:(h % 2) * 256 + 256], ST[:n, h, :n], self.v[s][:n, h * 256:(h + 1) * 256])
        for h in range(4):
            self.mm(ps[0][:n, 8 + h:9 + h], ST[:n, h, :n], onesb[:, 0:1])
        if kind != 's':
            for h in range(4):
                self.mm(ps[5 + h // 2][:n, (h % 2) * 256:(h % 2) * 256 + 256], self.qkT[:, h, cols], self.Cb[:, h, :])
            for h in range(4):
                self.mm(ps[0][:n, 12 + h:13 + h], self.qkT[:, h, cols], self.nb[:, h:h + 1])
        dint = ps[0][:n, 12:16] if kind != 's' else sm[12][:n, 0:4]
        self.tt(den[:n, 0:4], dint, wi[:n, 0:4], ALU.mult)
        self.tt(den[:n, 0:4], den[:n, 0:4], ps[0][:n, 8:12], ALU.add)
        self.ts(wi2[:n, 0:4], den[:n, 0:4], -1.0, ALU.mult)
        self.tt(den[:n, 0:4], den[:n, 0:4], wi2[:n, 0:4], ALU.max)
        self.tt(den[:n, 0:4], den[:n, 0:4], emt[:n, 0:4], ALU.max)
        self.S.op('dve', lambda e: e.reciprocal(out=rden.t[:n, 0:4], in_=den.t[:n, 0:4]), reads=[den], writes=[rden])
        self.tt(wi2[:n, 0:4], wi[:n, 0:4], rden[:n, 0:4], ALU.mult)
        for h in range(4):
            self.act(self.hhh[h][:n, :], ps[3 + h // 2][:n, (h % 2) * 256:(h % 2) * 256 + 256], AF.Copy, scale=rden[:n, h:h + 1])
            if kind != 's':
                iv = ps[5 + h // 2][:n, (h % 2) * 256:(h % 2) * 256 + 256]
            else:
                iv = self.sinter[h][:n, :]
            self.stt(self.hhh[h][:n, :], iv, wi2[:n, h:h + 1], self.hhh[h][:n, :], ALU.mult, ALU.add)
        st, mv, rs = sm[0], sm[1], sm[2]
        for h in range(4):
            self.S.op('dve', lambda e, h=h: e.bn_stats(out=st.t[:n, h * 6:(h + 1) * 6], in_=self.hhh[h].t[:n, :]), reads=[self.hhh[h]], writes=[st])
        for h in range(4):
            self.S.op('dve', lambda e, h=h: e.bn_aggr(out=mv.t[:n, 2 * h:2 * h + 2], in_=st.t[:n, h * 6:(h + 1) * 6]), reads=[st], writes=[mv])
        mvv = mv[:n, 0:8].rearrange('p (h t) -> p h t', t=2)
        self.act(rs[:n, 0:4], mvv[:, :, 1], AF.Ln, bias=self.cst[:n, 0:1])
        self.act(rs[:n, 0:4], rs[:n, 0:4], AF.Exp, scale=-0.5)
        for h in range(4):
            self.ts(self.hhh[h][:n, :], self.hhh[h][:n, :], mv[:n, 2 * h:2 * h + 1], ALU.subtract, rs[:n, h:h + 1], ALU.mult)
            self.stt(self.hgTM[:n, h * 256:(h + 1) * 256], self.oth[s][:n, h * 256:(h + 1) * 256], 1.0, self.hhh[h][:n, :], ALU.add, ALU.mult)
        for k in range(8):
            self.tr(pb7[:, k * n:(k + 1) * n], self.hgTM[:n, k * 128:(k + 1) * 128], identb)
        self.cp(self.hgT[:, :, cols], pb7[:, 0:8 * n].rearrange('p (k t) -> p k t', k=8), 'dve')
        if kind != 's':
            wl16 = sm[15]
            self.cp(wl16.ap().bitcast(BF16)[:n, 0:4], wT[:n, :, n - 1], 'act')
            for h in range(4):
                self.ts(vw[:n, h, :], self.v[s][:n, h * 256:(h + 1) * 256], wT[:n, h, n - 1:n], ALU.mult)
            for h in range(4):
                self.mm(ps[5 + h // 2][:, (h % 2) * 256:(h % 2) * 256 + 256], kTM[:n, h, :], vw[:n, h, :])
            for h in range(4):
                self.mm(ps[0][:, 16 + h:17 + h], kTM[:n, h, :], wl16.ap().bitcast(BF16)[:n, h:h + 1])
            SEL = self.cfv('SELp' if n == 128 else 'SELm', n)
            self.mm(ps[0][:, 32:36], SEL, wi[:n, 0:4])
            self.mm(ps[0][:, 36:40], SEL, mt[:n, 0:4])
            dec = sm[3]
            self.cp(dec[:, 0:8], ps[0][:, 32:40], 'act')
            for h in range(4):
                self.stt(self.Cf[:, h, :], self.Cf[:, h, :], dec[:, h:h + 1], ps[5 + h // 2][:, (h % 2) * 256:(h % 2) * 256 + 256], ALU.mult, ALU.add)
            self.tt(self.nf.ap(), self.nf.ap(), dec[:, 0:4], ALU.mult)
            self.tt(self.nf.ap(), self.nf.ap(), ps[0][:, 16:20], ALU.add)
            self.cp(self.m_b.ap(), dec[:, 4:8], 'dve')
            self.cp(self.Cb.ap(), self.Cf.ap(), 'act')
            self.cp(self.nb.ap(), self.nf.ap(), 'act')
            if ch.final:
                self.dma(O['p_C'].ap().rearrange('h d e -> d h e'), self.Cf.ap())
                identf128 = self.cfv('ident')
                self.tr(ps[0][:4, 128:256], self.nf.ap(), identf128)
                self.cp(self.pn_st[:4, :], ps[0][:4, 128:256], 'act')
                self.dma(O['p_n'].ap(), self.pn_st[:4, :])
                self.dma(O['p_m'].ap(), self.m_b[0:1, :])

    def mlstm_sample_states(self, tl, ch, wi, wT, kTM, mt):
        I, O, ps, sm = self.I, self.O, self.ps, self.sm
        n, s = 64, ch.slot
        identf = self.cfv('ident')
        RS, BM = self.cfv('RS', 64), self.cfv('BM', 64)
        CM = self.cbv('CM').rearrange('p (b j) -> p b j', b=16)
        vw = self.vw
        wl = sm[3]
        tmp = self.R3
        self.S.alias_phase(self.R3h, [self.R3])
        self.S.alias_phase(self.R1h + self.R2h, self.sinter)
        self.tt(tmp[:n, :, 0:64], wT[:n, :, 0:64], View(self.cf, self.cfv('LSEL', 64).ap.unsqueeze(1).to_broadcast([64, 4, 64])), ALU.mult)
        self.S.op('dve', lambda e: e.tensor_reduce(out=wl.t[:n, 0:4], in_=tmp.t[:n, :, 0:64], axis=AX.X, op=ALU.add), reads=[tmp], writes=[wl])
        wl16 = sm[4].ap().bitcast(BF16)
        self.cp(wl16[:n, 0:4], wl[:n, 0:4], 'act')
        for h in range(4):
            self.ts(vw[:n, h, :], self.v[s][:n, h * 256:(h + 1) * 256], wl[:n, h:h + 1], ALU.mult)
        Rm3 = self.Rm[:n, :].rearrange('p (b h) -> p b h', h=4)
        self.tt(Rm3, View(wi, wi.t[:n, 0:4].unsqueeze(1).to_broadcast([n, 16, 4])),
                View(self.cf, RS.ap.unsqueeze(2).to_broadcast([n, 16, 4])), ALU.mult)
        self.mm(ps[0][:, 64:128], self.cfv('ones', 64), self.Rm[:n, :])
        self.cp(self.decS.ap(), ps[0][:, 64:128], 'act')
        self.mm(ps[0][:16, 40:44], RS, mt[:n, 0:4])
        mo = sm[15]
        self.cp(mo[:16, 8:12], ps[0][:16, 40:44], 'act')
        self.dma(O['o_m'].ap(), mo[:16, 8:12])
        stg = self.pn_st
        self.dma(stg[:64, :], I['s_n'].ap())
        self.tr(ps[0][:, 192:256], stg[:64, :], identf[:64, :64])
        self.cp(self.n0T.ap(), ps[0][:, 192:256], 'act')
        self.cp(self.n16.ap(), self.n0T.ap(), 'pool')
        kTMflat = kTM[:n, :, :].rearrange('p h d -> p (h d)')
        for b in range(16):
            i = b % 2
            C0, C16, qm, km = self.C0b[i], self.C0b16[i], self.qmb[i], self.kTMm[i]
            self.dma(C0.ap(), I['s_C'][b].rearrange('h d e -> d h e'))
            self.cp(C16[:, :, 0:256], C0.ap(), 'act')
            self.cp(C16[:, :, 256:257], self.n16[:, 4 * b:4 * b + 4].rearrange('p (h o) -> p h o', o=1), 'dve')
            self.tt(qm.ap(), self.qkT[:, 0:4, ch.col0:ch.col0 + 64], View(self.cb, CM.ap[:, b, :].unsqueeze(1).to_broadcast([128, 4, 64])), ALU.mult)
            self.ts(km[:n, :, :].rearrange('p h d -> p (h d)'), kTMflat, BM[:, b:b + 1], ALU.mult)
            for h in range(4):
                self.mm(ps[3 + h][:n, 0:257], qm[:, h, :], C16[:, h, 0:257], start=(b == 0), stop=(b == 15))
            for h in range(4):
                self.mm(ps[1 + h // 2][:, (h % 2) * 256:(h % 2) * 256 + 256], km[:n, h, :], vw[:n, h, :])
            for h in range(4):
                self.mm(ps[0][:, 128 + 4 * b + h:129 + 4 * b + h], km[:n, h, :], wl16[:n, h:h + 1])
            for h in range(4):
                self.stt(C0[:, h, :], C0[:, h, :], self.decS[:, 4 * b + h:4 * b + h + 1],
                         ps[1 + h // 2][:, (h % 2) * 256:(h % 2) * 256 + 256], ALU.mult, ALU.add)
            self.dma(O['o_C'][b].rearrange('h d e -> d h e'), C0.ap())
        for h in range(4):
            self.cp(self.sinter[h][:n, :], ps[3 + h][:n, 0:256], 'act')
            self.cp(sm[12][:n, h:h + 1], ps[3 + h][:n, 256:257], 'act')
        self.tt(self.n0T.ap(), self.n0T.ap(), self.decS.ap(), ALU.mult)
        self.tt(self.n0T.ap(), self.n0T.ap(), ps[0][:, 128:192], ALU.add)
        self.tr(ps[0][:64, 256:384], self.n0T.ap(), identf)
        self.cp(stg[:64, :], ps[0][:64, 256:384], 'act')
        self.dma(O['o_n'].ap(), stg[:64, :])

    def ssd_chunk(self, tl, ch, last):
        I, O, ps, sm = self.I, self.O, self.ps, self.sm
        n, s, c0, kind = ch.n, ch.slot, ch.col0, ch.kind
        kc = self.kc(kind, n)
        U, LS, BO, MT = kc['U'], kc['LS'], kc['BO'], kc['MT']
        identb = self.cbv('identb')
        cols = slice(c0, c0 + n)
        dt = self.dta[:n, s, 0:32]
        a = self.dta[:n, s, 32:64]
        btsb, dect, wend, decS = sm[5], sm[6], sm[7], sm[8]
        self.mm(ps[0][:n, 0:32], U, a)
        self.mm(ps[0][:n, 32:64], BO, a)
        self.cp(btsb[:n, 0:32], ps[0][:n, 0:32], 'act')
        self.act(dect[:n, 0:32], ps[0][:n, 0:32], AF.Exp)
        self.tt(wend[:n, 0:32], ps[0][:n, 32:64], btsb[:n, 0:32], ALU.subtract)
        self.act(wend[:n, 0:32], wend[:n, 0:32], AF.Exp)
        if kind != 's':
            self.mm(ps[0][:, 64:96], self.cfv('ones', n), a)
            self.act(decS[:, 0:32], ps[0][:, 64:96], AF.Exp)
        S = self.S
        pb6, pb7 = self.psb(6), self.psb(7)
        xsTM = self.xdt
        for g in range(16):
            pv = pb6 if g < 8 else pb7
            self.tr(pv[:n, (g % 8) * 128:(g % 8 + 1) * 128], self.xbcT[:, g, cols], identb)
        self.act(xsTM[:n, 0:1024], pb6[:n, 0:1024])
        self.act(xsTM[:n, 1024:2048], pb7[:n, 0:1024])
        S.alias_phase([self.ynTM], [self.xsDT])
        for half in range(2):
            for g in range(8):
                self.ts(self.xsDT[:, g, 0:n], self.xbcT[:, 8 * half + g, cols], self.Dfm[:, 8 * half + g:8 * half + g + 1], ALU.mult)
            pv = pb6 if half == 0 else pb7
            for g in range(8):
                self.tr(pv[:n, g * 128:(g + 1) * 128], self.xsDT[:, g, 0:n], identb)
            self.cp(self.xsD[:n, 1024 * half:1024 * half + 1024], pv[:n, 0:1024], 'dve')
        for g in range(4):
            self.tr(pb6[:n, g * 128:(g + 1) * 128], self.xbcT[:, 16 + g, cols], identb)
        self.act(self.BTM[:n, :], pb6[:n, 0:512])
        dtw = sm[13]
        self.tt(dtw[:n, 0:32], dt, wend[:n, 0:32], ALU.mult)
        S.alias_phase([self.xsDT], [self.ynTM])
        if kind == 's':
            self.ssd_sample_states(tl, ch, dtw)
        S.alias_phase([self.ynTM], self.MTh[1])
        ssq = sm[9]
        self.memset(ssq[:n, 0:4], 0.0)
        def stageA(g):
            MTb = self.MTh[g % 2]
            for j in range(8):
                self.act(self.LAh[j][:n, :n], LS, AF.Copy, scale=self.dta[:n, s, 32 + 8 * g + j:33 + 8 * g + j])
            for j in range(8):
                o = ps[1 + j // 4][:n, (j % 4) * 128:(j % 4) * 128 + n]
                self.mm(o, self.LAh[j][:n, :n], U, start=True, stop=False)
                self.mm(o, identb[:n, :n], MT, start=False, stop=True)
            for half in range(2):
                self.act(self.LTh[half][:n, :, :n],
                         ps[1 + half][:n, :].rearrange('p (h t) -> p h t', h=4)[:, :, 0:n], AF.Exp)
            self.mm(ps[3][:n, 0:n], self.xbcT[:, 16 + g, cols], self.xbcT[:, 20 + g, cols])
            for j in range(8):
                self.stt(MTb[j][:n, :n], self.LTh[j // 4][:n, j % 4, :n], self.dta[:n, s, 8 * g + j:8 * g + j + 1], ps[3][:n, 0:n], ALU.mult, ALU.mult)

        def stageB(g):
            MTb = self.MTh[g % 2]
            for j in range(8):
                h = 8 * g + j
                self.mm(ps[4][:n, j * 64:(j + 1) * 64], MTb[j][:n, :n], xsTM[:n, h * 64:(h + 1) * 64])
            t1 = self.t1
            if kind != 's':
                self.mm(ps[5][:n, 0:512], self.xbcT[:, 20 + g, cols], self.STb[:, 512 * g:512 * g + 512])
                for j in range(4):
                    self.act(t1[:n, j * 64:(j + 1) * 64], ps[5][:n, j * 64:(j + 1) * 64], AF.Copy, scale=dect[:n, 8 * g + j:8 * g + j + 1])
                self.tt(t1[:n, 256:512].rearrange('p (h d) -> p d h', d=64), ps[5][:n, 256:512].rearrange('p (h d) -> p d h', d=64),
                        View(dect, dect.t[:n, 8 * g + 4:8 * g + 8].unsqueeze(1).to_broadcast([n, 64, 4])), ALU.mult)
            else:
                self.cp(t1[:n, :], self.ysi[:n, 512 * g:512 * g + 512], 'dve')
            self.tt(t1[:n, :], t1[:n, :], ps[4][:n, 0:512], ALU.add)
            self.tt(t1[:n, :], t1[:n, :], self.xsD[:n, 512 * g:512 * g + 512], ALU.add)
            self.tt(self.yz[:n, 512 * g:512 * g + 512], t1[:n, :], self.zs[s][:n, 512 * g:512 * g + 512], ALU.mult)

        stageA(0)
        for g in range(4):
            if g + 1 < 4:
                stageA(g + 1)
            stageB(g)
        S.alias_phase(self.MTh[1], [self.ynTM])
        for g in range(4):
            self.act(self.ynTM[:n, 512 * g:512 * g + 512], self.yz[:n, 512 * g:512 * g + 512], AF.Square, accum=ssq[:n, g:g + 1])
        rs = sm[10]
        self.act(rs[:n, 0:4], ssq[:n, 0:4], AF.Ln, scale=0.25 / 512.0, bias=self.cst[:n, 0:1])
        self.act(rs[:n, 0:4], rs[:n, 0:4], AF.Exp, scale=-0.5)
        self.ts(rs[:n, 0:4], rs[:n, 0:4], 0.5, ALU.mult)
        for g in range(4):
            self.ts(self.ynTM[:n, 512 * g:512 * g + 512], self.yz[:n, 512 * g:512 * g + 512], rs[:n, g:g + 1], ALU.mult)
        for k in range(16):
            pv = pb6 if k < 8 else pb7
            self.tr(pv[:, (k % 8) * n:(k % 8 + 1) * n], self.ynTM[:n, k * 128:(k + 1) * 128], identb[:n, :n])
        for half in range(2):
            pv = (pb6 if half == 0 else pb7)[:, 0:8 * n].rearrange('p (k t) -> p k t', k=8)
            self.cp(self.ygT[:, 8 * half:8 * half + 8, cols], pv, 'dve' if half == 0 else 'act')
        if kind != 's':
            S.alias_phase([self.ynTM], self.MTh[1])
            for g in range(4):
                hs = slice(8 * g, 8 * g + 8)
                bank = ps[3 + 2 * (g % 2)]
                Bs = self.MTh[g % 2]
                for j in range(8):
                    h = 8 * g + j
                    self.ts(Bs[j][:n, :], self.BTM[:n, g * 128:(g + 1) * 128], dtw[:n, h:h + 1], ALU.mult)
                for j in range(8):
                    h = 8 * g + j
                    self.mm(bank[:, j * 64:(j + 1) * 64], Bs[j][:n, :], xsTM[:n, h * 64:(h + 1) * 64])
                if g % 2 == 0:
                    for j in range(8):
                        c0_ = 512 * g + 64 * j
                        self.act(self.STf[:, c0_:c0_ + 64], self.STf[:, c0_:c0_ + 64], AF.Copy, scale=decS[:, 8 * g + j:8 * g + j + 1])
                else:
                    sv = self.STf[:, 512 * g:512 * g + 512].rearrange('p (h d) -> p d h', d=64)
                    self.tt(sv, sv, View(decS, decS.t[:, hs].unsqueeze(1).to_broadcast([128, 64, 8])), ALU.mult)
                self.tt(self.STf[:, 512 * g:512 * g + 512], self.STf[:, 512 * g:512 * g + 512], bank[:, 0:512], ALU.add)
            S.alias_phase(self.MTh[1], [self.ynTM])
            self.cp(self.STb.ap(), self.STf.ap(), 'act')
            if ch.final:
                identf = self.cfv('ident')
                for j in range(16):
                    bank = ps[1 + (j // 4) % 2]
                    self.tr(bank[:, (j % 4) * 128:(j % 4 + 1) * 128], self.STf[:, j * 128:(j + 1) * 128], identf)
                    if j % 4 == 3:
                        stg = self.LA[:, 4 * ((j // 4) % 2):4 * ((j // 4) % 2) + 4, :]
                        self.S.alias_phase(self.LAh, [self.LA])
                        self.cp(stg, bank[:, 0:512].rearrange('p (j n) -> p j n', j=4), 'act')
                        q = j // 4
                        self.dma(O['p_ssm'][512 * q:512 * q + 512, :].rearrange('(j p) n -> p j n', p=128), stg)
                self.S.alias_phase([self.LA], self.LAh)

    def ssd_sample_states(self, tl, ch, wend):
        I, O, ps, sm, S = self.I, self.O, self.ps, self.sm, self.S
        n, s = 64, ch.slot
        identf = self.cfv('ident')
        RS, BM = self.cfv('RS', 64), self.cfv('BM', 64)
        CM = self.cbv('CM').rearrange('p (b j) -> p b j', b=16)
        dect = sm[6]
        S.alias_phase(self.grpArena, self.grpArena2)
        S.alias_phase(self.LTh + self.MTh[0] + [self.t1, self.yz], [self.S0b[1]])
        blsb = sm[11]
        self.cp(blsb[:n, 0:32], ps[0][:n, 32:64], 'act')
        bl3 = blsb[:n, 0:32].rearrange('p (j r) -> p j r', r=2)
        for r in range(2):
            self.tt(self.Rr[:n, r, :].rearrange('p (b j) -> p b j', b=16),
                    View(blsb, bl3.ap[:, :, r].unsqueeze(1).to_broadcast([n, 16, 16])),
                    View(self.cf, RS.ap.unsqueeze(2).to_broadcast([n, 16, 16])), ALU.mult, eng='pool')
        self.mm(ps[0][:, 256:512], self.cfv('H0', 64), self.Rr[:n, 0, :], start=True, stop=False)
        self.mm(ps[0][:, 256:512], self.cfv('H1', 64), self.Rr[:n, 1, :], start=False, stop=True)
        self.act(self.decP.ap(), ps[0][:, 256:512], AF.Exp)
        wxA = self.ynTM
        self.tt(wxA[:n, :].rearrange('p (h d) -> p d h', d=64), self.xdt[:n, :].rearrange('p (h d) -> p d h', d=64),
                View(wend, wend.t[:n, 0:32].unsqueeze(1).to_broadcast([n, 64, 32])), ALU.mult)
        for b in range(16):
            Sb = self.S0b[b % 2]
            cm = self.CTmb[b % 2]
            self.dma(Sb.ap(), I['s_ssm'][b].rearrange('(j p) n -> p j n', p=128))
            self.tt(cm.ap(), self.xbcT[:, 20:24, ch.col0:ch.col0 + 64], View(self.cb, CM.ap[:, b, :].unsqueeze(1).to_broadcast([128, 4, 64])), ALU.mult)
            self.ts(self.wxm[:n, :], wxA[:n, :], BM[:, b:b + 1], ALU.mult)
            for q in range(4):
                bank = ps[5 + q % 2]
                for i in range(4):
                    self.tr(bank[:, i * 128:(i + 1) * 128], Sb[:, 4 * q + i, :], identf)
                self.cp(self.SbT[:, 512 * q:512 * q + 512], bank[:, 0:512], 'act')
            for g in range(4):
                self.mm(ps[1 + g][:n, 0:512], cm[:, g, :], self.SbT[:, 512 * g:512 * g + 512], start=(b == 0), stop=(b == 15))
            for q in range(4):
                bank = ps[7] if q % 2 == 0 else ps[0]
                for i in range(4):
                    j = 4 * q + i
                    self.mm(bank[:, i * 128:(i + 1) * 128], self.wxm[:n, j * 128:(j + 1) * 128], self.BTM[:n, q * 128:(q + 1) * 128])
                for i in range(4):
                    j = 4 * q + i
                    self.stt(Sb[:, j, :], Sb[:, j, :], self.decP[:, 16 * b + j:16 * b + j + 1], bank[:, i * 128:(i + 1) * 128], ALU.mult, ALU.add)
            self.dma(O['o_ssm'][b].rearrange('(j p) n -> p j n', p=128), Sb.ap())
        for g in range(4):
            self.tt(self.ysi[:n, 512 * g:512 * g + 512].rearrange('p (h d) -> p d h', d=64),
                    ps[1 + g][:n, 0:512].rearrange('p (h d) -> p d h', d=64),
                    View(dect, dect.t[:n, 8 * g:8 * g + 8].unsqueeze(1).to_broadcast([n, 64, 8])), ALU.mult)
        S.alias_phase([self.S0b[1]], self.LTh + self.MTh[0] + [self.t1, self.yz])


_CACHE = {}


def _get_kernel():
    if 'k' not in _CACHE:
        _CACHE['k'] = K()
    return _CACHE['k']


def make_in_maps(kb, inputs):
    f = lambda a: np.ascontiguousarray(np.asarray(a, dtype=np.float32))
    xp, xs = f(inputs['x_prompt']), f(inputs['x_sample'])
    shared = {'meta': f(inputs['meta_tokens']), 'cf': kb.cf_np, 'cb': kb.cb_np,
              'ln0_g': f(inputs['ln0_g']), 'ln0_b': f(inputs['ln0_b']),
              'b_if': f(inputs['b_mlstm_if'])[0], 'w_mconv': f(inputs['w_mlstm_conv'])[0],
              'b_mconv': f(inputs['b_mlstm_conv']), 'mnorm_g': f(inputs['mlstm_norm_g']),
              'w_sconv': f(inputs['w_ssm_conv'])[0], 'b_sconv': f(inputs['b_ssm_conv']),
              'dt_bias': f(inputs['ssm_dt_bias'])[0], 'A_log': f(inputs['ssm_A_log'])[0],
              'ssm_D': f(inputs['ssm_D'])[0], 'snorm_g': f(inputs['ssm_norm_g']),
              'ln1_g': f(inputs['ln1_g'])[0], 'ln1_b': f(inputs['ln1_b'])[0],
              'w_fconv': f(inputs['w_ffn_conv'])[0], 'b_fconv': f(inputs['b_ffn_conv']),
              'ln2_g': f(inputs['ln2_g'])[0], 'ln2_b': f(inputs['ln2_b'])[0],
              'w_in': f(inputs['w_in'])[0], 'w_proj_a': f(inputs['w_proj_a'])[0],
              'w_proj_b': f(inputs['w_proj_b'])[0], 'w_out': f(inputs['w_out'])[0],
              'w_up': f(inputs['w_up'])[0], 'w_down': f(inputs['w_down'])[0]}
    maps = []
    for c in range(8):
        b = slice(16 * c, 16 * c + 16)
        m = dict(shared)
        m['xp'] = xp[c]
        m['xs'] = xs[b].reshape(64, D)
        m['s_mconv'] = f(inputs['state_mlstm_conv'])[0, b].reshape(48, 1024)
        m['s_C'] = f(inputs['state_mlstm_C'])[0, b]
        m['s_n'] = f(inputs['state_mlstm_n'])[0, b].reshape(64, 128)
        m['s_m'] = f(inputs['state_mlstm_m'])[0, b]
        m['s_sconv'] = f(inputs['state_ssm_conv'])[0, b].reshape(48, 3072)
        m['s_ssm'] = f(inputs['state_ssm'])[0, b].reshape(16, 2048, 128)
        m['s_fconv'] = f(inputs['state_ffn_conv'])[0, b].reshape(32, 2 * DFF)
        maps.append(m)
    return maps


def kernel(**inputs):
    kb = _get_kernel()
    maps = make_in_maps(kb, inputs)
    res = run_bass_kernel_spmd(kb.nc, maps, core_ids=list(range(8)))
    R = res.results
    cat = lambda k: np.stack([np.asarray(r[k], dtype=np.float32) for r in R])
    y_p = cat('y_p')
    y_s = cat('y_s').reshape(128, 4, D)
    p_mconv = cat('p_mconv')[None]
    p_C = cat('p_C')[None]
    p_n = cat('p_n')[None]
    p_m = cat('p_m').reshape(8, 4)[None]
    p_sconv = cat('p_sconv')[None]
    p_ssm = cat('p_ssm').reshape(8, 32, 64, 128)[None]
    p_fconv = cat('p_fconv')[None]
    s_mconv = cat('o_mconv').reshape(128, 3, 1024)[None]
    s_C = cat('o_C').reshape(128, 4, 128, 256)[None]
    s_n = cat('o_n').reshape(128, 4, 128)[None]
    s_m = cat('o_m').reshape(128, 4)[None]
    s_sconv = cat('o_sconv').reshape(128, 3, 3072)[None]
    s_ssm = cat('o_ssm').reshape(128, 32, 64, 128)[None]
    s_fconv = cat('o_fconv').reshape(128, 2, 2 * DFF)[None]
    return (y_p, y_s, p_mconv, p_C, p_n, p_m, p_sconv, p_ssm, p_fconv,
            s_mconv, s_C, s_n, s_m, s_sconv, s_ssm, s_fconv)
```

```python
import numpy as np
import ml_dtypes
import concourse.bass as bass
import concourse.mybir as mybir
from concourse.bass_utils import run_bass_kernel_spmd

F32 = mybir.dt.float32
BF16 = mybir.dt.bfloat16
ALU = mybir.AluOpType
AF = mybir.ActivationFunctionType
AX = mybir.AxisListType

D = 1024
DIN = 10280
DFF = 2816
NEG = -30000.0
ALPHA = 2.0 ** 0.25
LN_EPS = 1e-5
RMS_EPS = 1e-5
QSCALE = 128.0 ** -0.5


class Buf:
    def __init__(self, name, t, space):
        self.name = name
        self.t = t
        self.space = space
        self.last_w = None
        self.readers = []
        self.sem_in = None
        self.cnt_in = 0
        self.sem_out = None
        self.cnt_out = 0

    def __getitem__(self, idx):
        return View(self, self.t[idx])

    def ap(self):
        return View(self, self.t[:] if self.space != 'dram' else self.t)


class View:
    def __init__(self, buf, ap):
        self.buf = buf
        self.ap = ap

    def __getitem__(self, idx):
        return View(self.buf, self.ap[idx])

    def rearrange(self, *a, **k):
        return View(self.buf, self.ap.rearrange(*a, **k))

    def bc(self, axis, shape):
        return View(self.buf, self.ap.unsqueeze(axis).to_broadcast(list(shape)))

    def bitcast(self, dt):
        return View(self.buf, self.ap.bitcast(dt))


def _bufs(vs):
    out = []
    for v in vs:
        if v is None or isinstance(v, (int, float)):
            continue
        b = v.buf if isinstance(v, View) else v
        if b not in out:
            out.append(b)
    return out


class Sched:
    ENGS = ('pe', 'act', 'dve', 'pool', 'sp')

    def __init__(self, nc):
        self.nc = nc
        self.sem = {e: nc.alloc_semaphore('sem_' + e) for e in self.ENGS}
        self.cnt = {e: 0 for e in self.ENGS}
        self.ops = {e: [] for e in self.ENGS}
        self.seen = {e: {} for e in self.ENGS}
        self.final_tokens = []
        self.sb_off = 16512
        self.sb_end = 229376
        self.nsem = 5

    def sbuf(self, name, shape, dtype, at=None):
        nbytes = int(np.prod(shape[1:])) * (2 if dtype == BF16 else 4)
        nbytes = (nbytes + 31) // 32 * 32
        if at is None:
            at = self.sb_off
            self.sb_off += nbytes
            assert self.sb_off <= self.sb_end, ('SBUF overflow', name, self.sb_off)
        t = self.nc.alloc_sbuf_tensor_at(name, list(shape), dtype, offset=at)
        b = Buf(name, t, 'sbuf')
        b.off = at
        b.nbytes = nbytes
        return b

    def psum(self, name, shape, dtype=F32):
        t = self.nc.alloc_psum_tensor(name, list(shape), dtype)
        return Buf(name, t, 'psum')

    def dram(self, name, shape, dtype, kind):
        t = self.nc.dram_tensor(name, list(shape), dtype, kind=kind)
        return Buf(name, t.ap(), 'dram')

    def alias_phase(self, old, new):
        toks = []
        for b in old:
            if b.last_w is not None:
                toks.append(b.last_w)
            toks.extend(b.readers)
        for b in new:
            b.readers = list(b.readers) + toks

    def _need(self, eng, waits, tok):
        sem, val, teng = tok
        key = id(sem)
        if self.seen[eng].get(key, 0) >= val:
            return
        if key not in waits or waits[key][1] < val:
            waits[key] = (sem, val)

    def _deps(self, eng, reads, writes):
        waits = {}
        for b in reads:
            tok = b.last_w
            if tok is not None and not (tok[2] == eng and eng == 'pe'):
                self._need(eng, waits, tok)
            if b.space == 'psum':
                for r in b.readers:
                    if r[2] != eng:
                        self._need(eng, waits, r)
        for b in writes:
            tok = b.last_w
            if tok is not None and not (tok[2] == eng and eng == 'pe'):
                self._need(eng, waits, tok)
            for r in b.readers:
                if not (r[2] == eng and eng == 'pe'):
                    self._need(eng, waits, r)
        for key, (sem, val) in waits.items():
            self.seen[eng][key] = val
        return list(waits.values())

    def op(self, eng, fn, reads=(), writes=()):
        reads = _bufs(reads)
        writes = _bufs(writes)
        waits = self._deps(eng, reads, writes)
        self.cnt[eng] += 1
        tok = (self.sem[eng], self.cnt[eng], eng)
        self.ops[eng].append((waits, fn, (self.sem[eng], 1)))
        for b in writes:
            b.last_w = tok
            b.readers = []
        for b in reads:
            if b not in writes:
                b.readers.append(tok)
        return tok

    def dma(self, q, out, in_, **kw):
        ob, ib = out.buf, in_.buf
        waits = self._deps(q, [ib], [ob])
        kind = 'sw' if q == 'pool' else 'hw'
        if ob.space != 'dram':
            tab = ob.__dict__.setdefault('sems_in', {})
            if kind not in tab:
                tab[kind] = [self.nc.alloc_semaphore('din_%s_%s' % (kind, ob.name)), 0]
                self.nsem += 1
            tab[kind][1] += 16
            sem, val = tab[kind]
        else:
            tab = ib.__dict__.setdefault('sems_out', {})
            if kind not in tab:
                tab[kind] = [self.nc.alloc_semaphore('dout_%s_%s' % (kind, ib.name)), 0]
                self.nsem += 1
            tab[kind][1] += 16
            sem, val = tab[kind]
        tok = (sem, val, 'dma')
        oap, iap = out.ap, in_.ap

        def fn(e, oap=oap, iap=iap, kw=kw):
            return e.dma_start(out=oap, in_=iap, **kw)
        self.ops[q].append((waits, fn, (sem, 16)))
        ob.last_w = tok
        ob.readers = []
        ib.readers.append(tok)
        if ob.space == 'dram':
            self.final_tokens.append(tok)
        return tok

    def emit(self):
        nc = self.nc
        last = {}
        for sem, val, _ in self.final_tokens:
            k = id(sem)
            if k not in last or last[k][1] < val:
                last[k] = (sem, val)
        fin = list(last.values())
        eng_obj = {'pe': 'tensor', 'act': 'scalar', 'dve': 'vector', 'pool': 'gpsimd', 'sp': 'sync'}
        with nc.Block() as block:
            def mk(eng):
                def body(e):
                    for waits, fn, inc in self.ops[eng]:
                        for sem, val in waits:
                            e.wait_ge(sem, val)
                        fn(e).then_inc(inc[0], inc[1])
                    if eng == 'sp':
                        for sem, val in fin:
                            e.wait_ge(sem, val)
                return body
            for eng, attr in eng_obj.items():
                getattr(block, attr)(mk(eng))


def _const_tables():
    p = np.arange(128)[:, None]
    j = np.arange(128)[None, :]
    f = {}
    f['ident'] = (p == j)
    f['ones'] = np.ones((128, 128))
    f['U'] = (p <= j)
    f['LS'] = (p > j)
    sb = (p // 4 == j // 4) & (p < 64) & (j < 64)
    f['Us'] = ((p <= j) & sb)[:, :64]
    f['LSs'] = ((p > j) & sb)[:, :64]
    f['BOs'] = sb[:, :64]
    f['SELp'] = np.repeat(p == 127, 128, axis=1)
    f['SELm'] = np.repeat(p == 15, 128, axis=1)
    b16 = np.arange(16)[None, :]
    f['RS'] = (p == 4 * b16 + 3)
    f['BM'] = (p // 4 == b16) & (p < 64)
    f['BMT'] = ((p < 16) & (j // 4 == p))[:, :64]
    f['LSEL'] = ((j == 4 * (p // 4) + 3) & (p < 64))[:, :64]
    f['H0'] = np.repeat(p < 64, 128, axis=1) & (j < 64)
    f['H1'] = np.repeat(p < 64, 128, axis=1) & (j >= 64)
    cf_off, cols = {}, []
    o = 0
    for k, v in f.items():
        cf_off[k] = (o, v.shape[1])
        o += v.shape[1]
        cols.append(v.astype(np.float32))
    cf = np.concatenate(cols, axis=1)
    g = {}
    g['identb'] = (p == j).astype(np.float32)
    g['onesb'] = np.ones((128, 128), np.float32)
    g['M'] = np.where(j <= p, 0.0, NEG)
    g['MT'] = np.where(p <= j, 0.0, NEG)
    g['Ms'] = np.where((j <= p) & sb, 0.0, NEG)[:, :64]
    g['MTs'] = np.where((p <= j) & sb, 0.0, NEG)[:, :64]
    jj = np.arange(64)[None, None, :]
    bb = np.arange(16)[None, :, None]
    g['CM'] = np.broadcast_to((jj // 4 == bb), (128, 16, 64)).reshape(128, 1024).astype(np.float32)
    cb_off, cols = {}, []
    o = 0
    for k, v in g.items():
        cb_off[k] = (o, v.shape[1])
        o += v.shape[1]
        cols.append(np.asarray(v, np.float32))
    cbm = np.concatenate(cols, axis=1).astype(ml_dtypes.bfloat16)
    return cf, cf_off, cbm, cb_off


class Chunk:
    def __init__(self, slot, col0, n, kind, row0=0):
        self.slot, self.col0, self.n, self.kind, self.row0 = slot, col0, n, kind, row0


class Tile:
    def __init__(self, name, T, chunks, segs):
        self.name, self.T, self.chunks, self.segs = name, T, chunks, segs


W_SHAPES = {'w_in': (D, DIN), 'w_proj_a': (D, D), 'w_proj_b': (2 * D, D), 'w_out': (D, D),
            'w_up': (D, 2 * DFF), 'w_down': (DFF, D)}


def tile_blocks():
    bl = []
    for c in range(0, 3072, 256):
        bl.append(('w_in', 0, 8, [(c, 256)]))
    bl.append(('w_in', 0, 8, [(3072, 8), (8200, 32)]))
    for c in range(3080, 5128, 256):
        bl.append(('w_in', 0, 8, [(c, 256)]))
    for c in range(5128, 8200, 256):
        bl.append(('w_in', 0, 8, [(c, 256)]))
    for c in range(8232, 10280, 256):
        bl.append(('w_in', 0, 8, [(c, 256)]))
    for j in range(4):
        bl.append(('w_proj_a', 0, 8, [(256 * j, 256)]))
        bl.append(('w_proj_b', 0, 8, [(256 * j, 256)]))
        bl.append(('w_proj_b', 8, 8, [(256 * j, 256)]))
    for j in range(4):
        bl.append(('w_out', 0, 8, [(256 * j, 256)]))
    for j in range(11):
        bl.append(('w_up', 0, 8, [(256 * j, 256)]))
        bl.append(('w_up', 0, 8, [(DFF + 256 * j, 256)]))
    for j in range(4):
        for k0, nk in ((0, 8), (8, 8), (16, 6)):
            bl.append(('w_down', k0, nk, [(256 * j, 256)]))
    return bl


class K:
    def __init__(self, debug=None, tiles=('T0', 'T1', 'T2', 'T3', 'T4')):
        self.debug = debug or {}
        self.tile_sel = tuple(tiles)
        self.ntiles = len(self.tile_sel)
        nc = bass.Bass('TRN2', target_bir_lowering=False)
        self.nc = nc
        self.S = S = Sched(nc)
        self.dumps = {}
        cf, self.cfo, cbm, self.cbo = _const_tables()
        self.cf_np, self.cb_np = cf, cbm
        din = lambda n, s, dt=F32: S.dram(n, s, dt, 'ExternalInput')
        dout = lambda n, s: S.dram(n, s, F32, 'ExternalOutput')
        I = self.I = {}
        I['xp'] = din('xp', [2048, D]); I['xs'] = din('xs', [64, D]); I['meta'] = din('meta', [16, D])
        I['s_mconv'] = din('s_mconv', [48, 1024]); I['s_C'] = din('s_C', [16, 4, 128, 256])
        I['s_n'] = din('s_n', [64, 128]); I['s_m'] = din('s_m', [16, 4])
        I['s_sconv'] = din('s_sconv', [48, 3072]); I['s_ssm'] = din('s_ssm', [16, 2048, 128])
        I['s_fconv'] = din('s_fconv', [32, 2 * DFF])
        I['cf'] = din('cf', list(cf.shape)); I['cb'] = din('cb', list(cbm.shape), BF16)
        for n, s in (('ln0_g', [D]), ('ln0_b', [D]), ('b_if', [8]), ('w_mconv', [4, 1024]), ('b_mconv', [1, 1024]),
                     ('mnorm_g', [1, 1024]), ('w_sconv', [4, 3072]), ('b_sconv', [1, 3072]), ('dt_bias', [32]),
                     ('A_log', [32]), ('ssm_D', [32]), ('snorm_g', [1, 2048]), ('ln1_g', [D]), ('ln1_b', [D]),
                     ('w_fconv', [3, 2 * DFF]), ('b_fconv', [1, 2 * DFF]), ('ln2_g', [D]), ('ln2_b', [D])):
            I[n] = din(n, s)
        for n, s in W_SHAPES.items():
            I[n] = din(n, list(s))
        O = self.O = {}
        O['y_p'] = dout('y_p', [2048, D]); O['y_s'] = dout('y_s', [64, D])
        O['p_mconv'] = dout('p_mconv', [3, 1024]); O['p_C'] = dout('p_C', [4, 128, 256])
        O['p_n'] = dout('p_n', [4, 128]); O['p_m'] = dout('p_m', [1, 4])
        O['p_sconv'] = dout('p_sconv', [3, 3072]); O['p_ssm'] = dout('p_ssm', [2048, 128])
        O['p_fconv'] = dout('p_fconv', [2, 2 * DFF])
        O['o_mconv'] = dout('o_mconv', [48, 1024]); O['o_C'] = dout('o_C', [16, 4, 128, 256])
        O['o_n'] = dout('o_n', [64, 128]); O['o_m'] = dout('o_m', [16, 4])
        O['o_sconv'] = dout('o_sconv', [48, 3072]); O['o_ssm'] = dout('o_ssm', [16, 2048, 128])
        O['o_fconv'] = dout('o_fconv', [32, 2 * DFF])
        self.rr = {}
        self.build()
        S.emit()

    def rot(self, key, n):
        i = self.rr.get(key, 0)
        self.rr[key] = i + 1
        return i % n

    def mm(self, out, lhsT, rhs, start=True, stop=True):
        self.S.op('pe', lambda e: e.matmul(out.ap, lhsT=lhsT.ap, rhs=rhs.ap, start=start, stop=stop),
                  reads=[lhsT, rhs], writes=[out])

    def tr(self, out, in_, ident):
        self.S.op('pe', lambda e: e.transpose(out=out.ap, in_=in_.ap, identity=ident.ap),
                  reads=[in_, ident], writes=[out])

    def act(self, out, in_, func=AF.Copy, bias=None, scale=None, accum=None):
        kw = {}
        if bias is not None:
            kw['bias'] = bias.ap if isinstance(bias, View) else bias
        if scale is not None:
            kw['scale'] = scale.ap if isinstance(scale, View) else scale
        if accum is not None:
            kw['accum_out'] = accum.ap
        self.S.op('act', lambda e: e.activation(out=out.ap, in_=in_.ap, func=func, **kw),
                  reads=[in_, bias, scale], writes=[out, accum])

    def tt(self, out, a, b, op, eng='dve'):
        self.S.op(eng, lambda e: e.tensor_tensor(out=out.ap, in0=a.ap, in1=b.ap, op=op),
                  reads=[a, b], writes=[out])

    def ts(self, out, a, s1, op0, s2=None, op1=None, eng='dve', accum=None):
        v1 = s1.ap if isinstance(s1, View) else s1
        v2 = s2.ap if isinstance(s2, View) else s2
        kw = {}
        if op1 is not None:
            kw['op1'] = op1
        if accum is not None:
            kw['accum_out'] = accum.ap
        self.S.op(eng, lambda e: e.tensor_scalar(out=out.ap, in0=a.ap, scalar1=v1, scalar2=v2, op0=op0, **kw),
                  reads=[a, s1, s2], writes=[out, accum])

    def stt(self, out, a, s, b, op0, op1, eng='dve'):
        v = s.ap if isinstance(s, View) else s
        self.S.op(eng, lambda e: e.scalar_tensor_tensor(out=out.ap, in0=a.ap, scalar=v, in1=b.ap, op0=op0, op1=op1),
                  reads=[a, s, b], writes=[out])

    def cp(self, out, in_, eng='dve'):
        if eng == 'act':
            return self.act(out, in_)
        self.S.op(eng, lambda e: e.tensor_copy(out=out.ap, in_=in_.ap), reads=[in_], writes=[out])

    def memset(self, out, val, eng='dve'):
        self.S.op(eng, lambda e: e.memset(out.ap, val), writes=[out])

    def rmax(self, out, in_, eng='dve'):
        self.S.op(eng, lambda e: e.tensor_reduce(out=out.ap, in_=in_.ap, axis=AX.X, op=ALU.max),
                  reads=[in_], writes=[out])

    def dma(self, out, in_, q='sp'):
        if out.buf.space == 'dram' and q == 'sp' and not getattr(self, 'pass0', False):
            q = 'pool'
        self.S.dma(q, out, in_)

    def dump(self, name, view, shape):
        if name not in self.debug:
            return
        d = self.S.dram('dbg_' + name, list(shape), view.ap.dtype, 'ExternalOutput')
        self.dumps[name] = d
        self.dma(d.ap(), view)

    def cfv(self, name, rows=128, cols=None):
        o, w = self.cfo[name]
        cols = w if cols is None else cols
        return self.cf[:rows, o:o + cols]

    def cbv(self, name, rows=128, cols=None):
        o, w = self.cbo[name]
        cols = w if cols is None else cols
        return self.cb[:rows, o:o + cols]

    def ws_init(self):
        S = self.S
        self.wlist = tile_blocks()
        self.nbt = len(self.wlist)
        self.wblocks = self.wlist * self.ntiles
        self.wring = [S.sbuf(f'wring{i}', [128, 8, 256], BF16) for i in range(6)]
        self.wscr = [S.dram(f'wscr{j}', [128, 8, 256], BF16, 'Internal') for j in range(self.nbt)] if self.ntiles > 1 else None
        self.w_loaded = 0
        self.w_next = 0

    def _w_load0(self, i):
        name, k0, nk, parts = self.wblocks[i]
        dst = self.wring[i % 6]
        W = self.I[name]
        c = 0
        for (c0, n) in parts:
            src = View(W, W.t[k0 * 128:(k0 + nk) * 128, c0:c0 + n].rearrange('(k p) c -> p k c', p=128))
            self.S.dma('pool', dst[:, 0:nk, c:c + n], src)
            c += n
        n = c
        if name in ('w_proj_a', 'w_proj_b'):
            for k in range(nk):
                gcol = self.cwm[:, k0 + k, 5:6] if name == 'w_proj_a' else self.sng[:, k0 + k:k0 + k + 1]
                self.ts(dst[:, k, 0:n], dst[:, k, 0:n], gcol, ALU.mult)
        if self.wscr is not None:
            self.S.dma('sp', self.wscr[i][:, 0:nk, 0:n], dst[:, 0:nk, 0:n])

    def _w_ringload(self, i):
        name, k0, nk, parts = self.wblocks[i]
        n = sum(p[1] for p in parts)
        dst = self.wring[i % 6]
        self.dma(dst[:, 0:nk, 0:n], self.wscr[i % self.nbt][:, 0:nk, 0:n], q='sp')

    def wnext(self):
        i = self.w_next
        nb = len(self.wblocks)
        self.w_next += 1
        while self.w_loaded < min(nb, i + 6):
            if self.w_loaded < self.nbt:
                self._w_load0(self.w_loaded)
            else:
                self._w_ringload(self.w_loaded)
            self.w_loaded += 1
        return self.wring[i % 6], self.wblocks[i]

    def build(self):
        S = self.S
        sb = S.sbuf
        ncf, ncb = self.cf_np.shape[1], self.cb_np.shape[1]
        self.cf = sb('cf', [128, ncf], F32)
        self.cb = sb('cb', [128, ncb], BF16)
        self.lnc = sb('lnc', [128, 2, D], F32)
        self.bif_b = sb('bif_b', [128, 8], F32)
        self.dtb_b = sb('dtb_b', [128, 32], F32)
        self.A_b = sb('A_b', [128, 32], F32)
        self.D_b = sb('D_b', [128, 32], F32)
        self.Dfm = sb('Dfm', [128, 16], F32)
        self.cwm = sb('cwm', [128, 8, 6], F32)
        self.cws = sb('cws', [128, 24, 5], F32)
        self.sng = sb('sng', [128, 16], F32)
        self.cwf = sb('cwf', [128, 44, 4], F32)
        self.ws_init()
        self.xr = [sb(f'xr{i}', [128, D], F32) for i in range(4)]
        self.zs = [None] * 4
        self.xnT = sb('xnT', [128, 8, 512], BF16)
        self.hgT = sb('hgT', [128, 8, 512], BF16)
        self.ygT = sb('ygT', [128, 16, 512], BF16)
        self.Cf = sb('Cf', [128, 4, 256], F32); self.Cb = sb('Cb', [128, 4, 256], BF16)
        self.nf = sb('nf', [128, 4], F32); self.nb = sb('nb', [128, 4], BF16)
        self.m_b = sb('m_b', [128, 4], F32)
        self.STf = sb('STf', [128, 2048], F32); self.STb = sb('STb', [128, 2048], BF16)
        self.cq = sb('cq', [128, 8, 3], F32); self.cx = sb('cx', [128, 24, 3], F32)
        self.cff = sb('cff', [128, 44, 2], F32)
        self.scar = sb('scar', [128, 44 * 16 * 2], F32)
        self.gat = sb('gat', [128, 4, 8], F32)
        self.dta = sb('dta', [128, 4, 64], F32)
        self.ifdt = sb('ifdt', [128, 4, 40], F32)
        self.sm = [sb(f'sm{i}', [128, 32], F32) for i in range(16)]
        self.xb16 = sb('xb16', [128, D], BF16)
        self.lnsc = [sb(f'lnsc{i}', [128, 16], F32) for i in range(4)]
        self.xb16h = [sb(f'xb16h{i}', [128, 512], BF16, at=self.xb16.off + 1024 * i) for i in range(2)]
        self.cst = sb('cst', [128, 8], F32)
        self.pn_st = sb('pn_st', [128, 128], F32)
        R0 = S.sb_off
        o = R0
        def at(name, shape, dt):
            nonlocal o
            b = sb(name, shape, dt, at=o)
            o += b.nbytes
            return b
        self.cE = [at(f'cE{i}', [128, 520], F32) for i in range(2)]
        self.cacc = [at(f'cacc{i}', [128, 512], F32) for i in range(3)]
        self.cth = [at(f'cth{i}', [128, 512], F32) for i in range(2)]
        self.cacc2 = [at(f'cacc2_{i}', [128, 512], F32) for i in range(2)]
        e1 = o
        o = R0
        self.R1 = at('R1', [128, 4, 128], F32); self.R2 = at('R2', [128, 4, 128], F32)
        self.R3 = at('R3', [128, 4, 128], F32); self.wT = at('wT', [128, 4, 128], F32)
        self.ST = at('ST', [128, 4, 128], BF16); self.kTM = at('kTM', [128, 4, 128], BF16)
        self.hh = at('hh', [128, 4, 256], F32); self.vw = at('vw', [128, 4, 256], BF16)
        self.hgTM = at('hgTM', [128, D], BF16)
        e2 = o
        o = R0
        self.xdt = at('xdt', [128, 2048], BF16); self.xsD = at('xsD', [128, 2048], BF16)
        self.wx = at('wx', [128, 512], BF16); self.BTM = at('BTM', [128, 512], BF16)
        self.LT = at('LT', [128, 8, 128], BF16); self.MTt = at('MTt', [128, 8, 128], BF16)
        self.t1 = at('t1', [128, 512], F32)
        self.yz = at('yz', [128, 2048], BF16)
        self.ynTM = at('ynTM', [128, 2048], BF16)
        e3 = o
        F0 = max(e1, e2, e3)
        conv_end = F0
        o = F0
        self.qkT = at('qkT', [128, 8, 512], BF16)
        self.v = [at(f'v{i}', [128, D], BF16) for i in range(4)]
        self.oth = [at(f'oth{i}', [128, D], BF16) for i in range(4)]
        a1_end = o
        o = F0
        self.xbcT = at('xbcT', [128, 24, 512], BF16)
        for i in range(2):
            self.zs[i] = at(f'zs{i}', [128, 2048], BF16)
        self.LA = at('LA', [128, 8, 128], F32)
        a2_end = o
        o = F0
        self.gth = at('gth', [128, 16, 512], BF16)
        self.mixT = at('mixT', [128, 8, 512], BF16)
        self.hffT = at('hffT', [128, 22, 512], BF16)
        b_end = o
        S.sb_off = max(a1_end, a2_end, b_end)
        for i in range(2, 4):
            self.zs[i] = sb(f'zs{i}', [128, 2048], BF16)
        self.arenas = [(self.xr[2].off, 2 * self.xr[2].nbytes), (self.zs[2].off, 2 * self.zs[2].nbytes)]
        a0, a1 = self.arenas[0][0], self.arenas[1][0]
        self.C0b = [sb(f'C0b{i}', [128, 4, 256], F32, at=a0 + 4096 * i) for i in range(2)]
        self.C0b16 = [sb(f'C0b16_{i}', [128, 4, 258], BF16, at=a1 + 2080 * i) for i in range(2)]
        self.qmb = [sb(f'qmb{i}', [128, 4, 64], BF16, at=a1 + 4160 + 512 * i) for i in range(2)]
        self.kTMm = [sb(f'kTMm{i}', [128, 4, 128], BF16, at=a1 + 5184 + 1024 * i) for i in range(2)]
        self.n0T = sb('n0T', [128, 64], F32, at=a1 + 7232)
        self.n16 = sb('n16', [128, 64], BF16, at=a1 + 7488)
        self.decS = sb('decS', [128, 64], F32, at=a1 + 7616)
        self.Rm = sb('Rm', [128, 64], F32, at=a1 + 7872)
        self.S0b = [sb('S0b0', [128, 16, 128], F32, at=a0), sb('S0b1', [128, 16, 128], F32, at=self.LT.off)]
        assert self.LT.off + 8192 <= self.ynTM.off
        self.SbT = sb('SbT', [128, 2048], BF16, at=a1)
        self.wxm = sb('wxm', [128, 2048], BF16, at=a1 + 4096)
        self.ysi = sb('ysi', [128, 2048], BF16)
        self.decP = sb('decP', [128, 256], F32)
        self.Rr = sb('Rr', [128, 2, 256], F32)
        self.CTmb = [sb(f'CTmb{i}', [128, 4, 64], BF16) for i in range(2)]
        self.grpArena2 = [self.S0b[0], self.SbT, self.wxm]
        self.xsDT = sb('xsDT', [128, 8, 128], BF16, at=self.ynTM.off)
        self.R1h = [sb(f'R1h{j}', [128, 128], F32, at=self.R1.off + 512 * j) for j in range(4)]
        self.R2h = [sb(f'R2h{j}', [128, 128], F32, at=self.R2.off + 512 * j) for j in range(4)]
        self.R3h = [sb(f'R3h{j}', [128, 128], F32, at=self.R3.off + 512 * j) for j in range(4)]
        self.hhh = [sb(f'hhh{j}', [128, 256], F32, at=self.hh.off + 1024 * j) for j in range(4)]
        self.hgTMh = [sb(f'hgTMh{j}', [128, 256], BF16, at=self.hgTM.off + 512 * j) for j in range(4)]
        self.sinter = [sb(f'sint{j}', [128, 256], F32, at=(self.R1.off if j < 2 else self.R2.off) + 1024 * (j % 2)) for j in range(4)]
        self.LAh = [sb(f'LAh{j}', [128, 128], F32, at=self.LA.off + 512 * j) for j in range(8)]
        self.LTh = [sb(f'LTh{j}', [128, 4, 128], BF16, at=self.LT.off + 1024 * j) for j in range(2)]
        self.MTh = [[sb(f'MTh{b}_{j}', [128, 128], BF16, at=base + 256 * j) for j in range(8)]
                    for b, base in enumerate((self.MTt.off, self.ynTM.off + 2048))]
        self.grpArena = self.C0b + self.C0b16 + self.qmb + self.kTMm + [self.n0T, self.n16, self.decS, self.Rm]
        self.grpA1conv = self.cE + self.cacc + self.cth + self.cacc2
        self.grpA1rec = [self.wT, self.ST, self.kTM, self.vw] + self.hgTMh + self.R1h + self.R2h + self.R3h + self.hhh
        self.grpA1fix = [self.qkT] + self.v + self.oth
        self.grpA2fix = [self.xbcT, self.zs[0], self.zs[1]] + self.LAh
        self.grpA2rec = [self.xdt, self.xsD, self.wx, self.BTM, self.t1, self.yz, self.ynTM] + self.LTh + self.MTh[0]
        self.grpB = [self.gth, self.mixT, self.hffT]
        print('SBUF used', S.sb_off, 'of', S.sb_end, 'R', R0, conv_end - R0, a1_end - R0, a2_end - R0, b_end - R0)
        self.ps = [S.psum(f'ps{i}', [128, 512], F32) for i in range(8)]
        self.setup()
        tiles = self.make_tiles()
        first = True
        for tl in tiles:
            if tl.name in self.tile_sel:
                self.pass0 = first
                self.run_tile(tl, last=(tl.name == 'T4'))
                first = False

    def make_tiles(self):
        def pch(slot, col0, c):
            ch = Chunk(slot, col0, 128, 'p', row0=128 * c)
            ch.final = (c == 15)
            return ch
        m = Chunk(0, 0, 16, 'm'); m.final = False
        tiles = [Tile('T0', 400, [m] + [pch(1 + i, 16 + 128 * i, i) for i in range(3)], [(0, 1, 400, 'p')])]
        for t in range(3):
            tiles.append(Tile(f'T{t + 1}', 512, [pch(i, 128 * i, 3 + 4 * t + i) for i in range(4)], [(0, 1, 512, 'p')]))
        sc = Chunk(1, 128, 64, 's'); sc.final = False
        tiles.append(Tile('T4', 192, [pch(0, 0, 15), sc], [(0, 1, 128, 'p'), (128, 16, 4, 's')]))
        return tiles

    def psb(self, i):
        return self.ps[i].ap().bitcast(BF16)

    def setup(self):
        I = self.I
        self.dma(self.cf.ap(), I['cf'].ap())
        self.dma(self.cb.ap(), I['cb'].ap())
        pb = lambda n: View(I[n], I[n].t.partition_broadcast(128))
        self.dma(self.bif_b.ap(), pb('b_if'))
        self.dma(self.dtb_b.ap(), pb('dt_bias'))
        self.dma(self.A_b.ap(), pb('A_log'))
        self.dma(self.D_b.ap(), pb('ssm_D'))
        self.act(self.A_b.ap(), self.A_b.ap(), AF.Exp)
        self.ts(self.A_b.ap(), self.A_b.ap(), -1.0, ALU.mult)
        D3 = self.D_b.ap().rearrange('p (g r) -> p g r', r=2)
        self.cp(self.Dfm[0:64, :], D3[0:64, :, 0], 'dve')
        self.cp(self.Dfm[64:128, :], D3[64:128, :, 1], 'dve')
        identf = self.cfv('ident')
        stg = self.cacc[0]
        def fm_params(dst, rows, G, scale_groups=None):
            R = sum(r for _, r in rows)
            for g0 in range(0, G, 4):
                gn = min(4, G - g0)
                r0 = 0
                for (nm, nr) in rows:
                    self.dma(stg[r0:r0 + nr, 0:gn * 128], I[nm][:, g0 * 128:(g0 + gn) * 128])
                    r0 += nr
                bank = self.ps[self.rot('setup', 2)]
                for g in range(gn):
                    self.tr(bank[:, g * R:(g + 1) * R], stg[0:R, g * 128:(g + 1) * 128], identf[0:R, 0:R])
                self.cp(dst[:, g0:g0 + gn, :], bank[:, 0:gn * R].rearrange('p (g r) -> p g r', r=R), 'act')
        fm_params(self.cwm, [('w_mconv', 4), ('b_mconv', 1), ('mnorm_g', 1)], 8)
        fm_params(self.cws, [('w_sconv', 4), ('b_sconv', 1)], 24)
        fm_params(self.cwf, [('w_fconv', 3), ('b_fconv', 1)], 44)
        sng3 = self.sng.ap().rearrange('p (g r) -> p g r', r=1)
        fm_params(sng3, [('snorm_g', 1)], 16)
        self.ts(self.cwm[:, :, 0:6], self.cwm[:, :, 0:6], 0.5, ALU.mult)
        self.ts(self.cws.ap(), self.cws.ap(), 0.5, ALU.mult)
        self.ts(self.cwf[:, 0:22, :], self.cwf[:, 0:22, :], 0.5, ALU.mult)
        self.memset(self.cst[:, 0:1], LN_EPS)
        self.memset(self.cst[:, 1:2], 0.5 * float(np.log(128.0)))
        self.memset(self.cst[:, 2:3], 1.0)
        self.eps_t = self.cst
        for b in (self.Cf, self.nf, self.m_b, self.STf, self.cq, self.cx, self.cff):
            self.memset(b.ap(), 0.0)
        for b in (self.Cb, self.nb, self.STb):
            self.memset(b.ap(), 0.0, 'pool')

    def kc(self, kind, n):
        if kind == 's':
            return dict(U=self.cfv('Us', 64), LS=self.cfv('LSs', 64), BO=self.cfv('BOs', 64),
                        M=self.cbv('Ms', 64), MT=self.cbv('MTs', 64))
        return dict(U=self.cfv('U', n, n), LS=self.cfv('LS', n, n), BO=self.cfv('ones', n, n),
                    M=self.cbv('M', n, n), MT=self.cbv('MT', n, n))

    def ln_load(self, gname, bname):
        I = self.I
        self.dma(self.lnc[:, 0, :], View(I[gname], I[gname].t.partition_broadcast(128)))
        self.dma(self.lnc[:, 1, :], View(I[bname], I[bname].t.partition_broadcast(128)))

    def ln_rows(self, x, n, slot):
        sc = self.lnsc[slot]
        st, mv, rs = sc[:n, 0:12], sc[:n, 12:14], sc[:n, 14:15]
        for i in range(2):
            self.S.op('dve', lambda e, i=i: e.bn_stats(out=sc.t[:n, i * 6:(i + 1) * 6], in_=x.ap[:, i * 512:(i + 1) * 512]),
                      reads=[x], writes=[sc])
        self.S.op('dve', lambda e: e.bn_aggr(out=sc.t[:n, 12:14], in_=sc.t[:n, 0:12]), reads=[sc], writes=[sc])
        self.act(rs, sc[:n, 13:14], AF.Ln, bias=self.eps_t[:n, 0:1])
        self.act(rs, rs, AF.Exp, scale=-0.5)
        self.ts(x, x, sc[:n, 12:13], ALU.subtract, rs, ALU.mult)
        self.tt(x, x, self.lnc[:n, 0, :], ALU.mult)
        self.tt(x, x, self.lnc[:n, 1, :], ALU.add)

    def to_fm(self, tl, src_of_chunk, dstT):
        identb = self.cbv('identb')
        for ch in tl.chunks:
            n = ch.n
            srcv = src_of_chunk(ch)
            for half in range(2):
                self.act(self.xb16h[half][:n, :], srcv[:, 512 * half:512 * half + 512])
            bank = 6 + self.rot('tfm', 2)
            pv = self.psb(bank)
            for k in range(8):
                self.tr(pv[:, k * n:(k + 1) * n], self.xb16h[k // 4][:n, (k % 4) * 128:(k % 4 + 1) * 128], identb[:n, :n])
            self.cp(dstT[:, :, ch.col0:ch.col0 + n], pv[:, 0:8 * n].rearrange('p (k n) -> p k n', n=n), 'dve')

    def _conv_taps(self, tl, psv, W, wtab, g, carry_p, scar_view, E, acc):
        Wm = W - 1
        off = 0
        for (col0, nb, L, kind) in tl.segs:
            Ev = E[:, off:off + nb * (L + Wm)].rearrange('p (b l) -> p b l', b=nb)
            pseg = psv[:, col0:col0 + nb * L].rearrange('p (b l) -> p b l', b=nb)
            if kind == 'p':
                self.cp(Ev[:, :, 0:Wm], carry_p[:, g:g + 1, :], 'act')
            elif kind == 'm':
                self.memset(Ev[:, :, 0:Wm], 0.0, 'dve')
            else:
                self.cp(Ev[:, :, 0:Wm], scar_view[:, g, :, :], 'act')
            self.act(Ev[:, :, Wm:Wm + L], pseg)
            av = acc[:, col0:col0 + nb * L].rearrange('p (b l) -> p b l', b=nb)
            self.act(av, pseg, AF.Identity, scale=wtab[:, g, Wm:W], bias=wtab[:, g, W:W + 1])
            if kind == 's':
                self.cp(scar_view[:, g, :, :], Ev[:, :, L:L + Wm], 'act')
            else:
                self.cp(carry_p[:, g:g + 1, :], Ev[:, :, L:L + Wm], 'act')
            for j in range(Wm):
                self.stt(av, Ev[:, :, j:j + L], wtab[:, g, j:j + 1], av, ALU.mult, ALU.add)
            off += nb * (L + Wm)

    def conv_group(self, tl, psv, W, wtab, g, carry_p, scar_view, dst, final=True):
        E = self.cE[self.rot('cE', 2)]
        acc = self.cacc[self.rot('cacc', 3)]
        self._conv_taps(tl, psv, W, wtab, g, carry_p, scar_view, E, acc)
        T = tl.T
        if not final:
            return acc
        prev = getattr(self, '_conv_pending', None)

        def stage2(acc=acc, dst=dst, T=T):
            th = self.cth[self.rot('cth', 2)]
            self.act(th[:, 0:T], acc[:, 0:T], AF.Tanh)
            self.stt(dst, th[:, 0:T], 1.0, acc[:, 0:T], ALU.add, ALU.mult)
        self._conv_pending = stage2
        if prev is not None:
            prev()
        return acc

    def conv_flush(self):
        prev = getattr(self, '_conv_pending', None)
        self._conv_pending = None
        if prev is not None:
            prev()

    def carry_out(self, src, G, R, dst):
        identf = self.cfv('ident')
        for g0 in range(0, G, 4):
            gn = min(4, G - g0)
            bank = self.ps[self.rot('co', 2)]
            for g in range(gn):
                self.tr(bank[:R, g * 128:(g + 1) * 128], src[:, g0 + g, :], identf)
            stg = self.cacc[self.rot('cacc', 3)]
            self.cp(stg[:R, 0:gn * 128], bank[:R, 0:gn * 128], 'act')
            self.dma(dst[:, g0 * 128:(g0 + gn) * 128], stg[:R, 0:gn * 128])

    def scar_in(self, name, G, R):
        identf = self.cfv('ident')
        rows = 16 * R
        sv = self.scar[:, 0:G * rows].rearrange('p (g b r) -> p g b r', g=G, b=16)
        for g0 in range(0, G, 4):
            gn = min(4, G - g0)
            stg = self.cacc[self.rot('cacc', 3)]
            self.dma(stg[:rows, 0:gn * 128], self.I[name][:, g0 * 128:(g0 + gn) * 128])
            bank = self.ps[self.rot('co', 2)]
            for g in range(gn):
                self.tr(bank[:, g * rows:(g + 1) * rows], stg[:rows, g * 128:(g + 1) * 128], identf[:rows, :rows])
            self.cp(self.scar[:, g0 * rows:(g0 + gn) * rows], bank[:, 0:gn * rows], 'act')
        return sv

    def scar_out(self, name, G, R):
        rows = 16 * R
        src = self.scar[:, 0:G * rows].rearrange('p (g br) -> p g br', g=G)
        self.carry_out(src, G, rows, self.O[name].ap())

    def dense_fm(self, tl, actT, nkt_total, cb_group, kt0=0):
        Wb, (name, k0, nk, parts) = self.wnext()
        ncols = sum(p[1] for p in parts)
        T = tl.T
        for gl in range(ncols // 128):
            bank = self.ps[self.rot('mm', 4)]
            for k in range(nk):
                self.mm(bank[:, 0:T], Wb[:, k, gl * 128:(gl + 1) * 128], actT[:, k0 + k, 0:T],
                        start=(k0 + k == 0), stop=(k0 + k == nkt_total - 1))
            cb_group(gl, bank[:, 0:T])

    def dense_tm(self, tl, actT, cb_chunk):
        Wb, (name, k0, nk, parts) = self.wnext()
        ncols = sum(p[1] for p in parts)
        for ch in tl.chunks:
            bank = self.ps[self.rot('mm', 4)]
            for k in range(nk):
                self.mm(bank[:ch.n, 0:ncols], actT[:, k0 + k, ch.col0:ch.col0 + ch.n], Wb[:, k, 0:ncols],
                        start=(k == 0), stop=(k == nk - 1))
            cb_chunk(ch, bank[:ch.n, 0:ncols])

    def run_tile(self, tl, last):
        S, I, O = self.S, self.I, self.O
        T = tl.T
        isS = any(sg[3] == 's' for sg in tl.segs)
        if isS:
            S.alias_phase([self.xr[2], self.xr[3], self.zs[2], self.zs[3]], self.grpArena + self.grpArena2)
        for ch in tl.chunks:
            src = {'s': I['xs'].ap(), 'm': I['meta'].ap()}.get(ch.kind)
            if src is None:
                src = I['xp'][ch.row0:ch.row0 + ch.n, :]
            self.dma(self.xr[ch.slot][:ch.n, :], src)
        self.ln_load('ln0_g', 'ln0_b')
        for ch in tl.chunks:
            self.ln_rows(self.xr[ch.slot][:ch.n, :], ch.n, ch.slot)
        self.to_fm(tl, lambda ch: self.xr[ch.slot][:ch.n, :], self.xnT)
        for ch in tl.chunks:
            self.ts(self.xr[ch.slot][:ch.n, :], self.xr[ch.slot][:ch.n, :], ALPHA, ALU.mult)
        self.dump('xnT_' + tl.name, self.xnT[:, :, 0:T], [128, 8, T])
        if self.debug.get('stop') == 'p0':
            return
        S.alias_phase(self.grpA2fix + self.grpA2rec + self.grpB + self.grpA1rec, self.grpA1conv + self.grpA1fix)
        sq = self.scar_in('s_mconv', 8, 3) if isS else None
        for blk in range(4):
            def cbq(gl, psv, blk=blk):
                g = 2 * blk + gl
                self.conv_group(tl, psv, 4, self.cwm, g, self.cq, sq, self.qkT[:, g, 0:T])
            self.dense_fm(tl, self.xnT, 8, cbq)
        self.conv_flush()
        if isS:
            self.scar_out('o_mconv', 8, 3)
        if last:
            self.carry_out(self.cq.ap(), 8, 3, O['p_mconv'].ap())
        for blk in range(4):
            self.dense_tm(tl, self.xnT, lambda ch, psv, blk=blk: self.act(self.v[ch.slot][:ch.n, 256 * blk:256 * blk + 256], psv))
        for blk in range(4):
            self.dense_tm(tl, self.xnT, lambda ch, psv, blk=blk: self.act(self.oth[ch.slot][:ch.n, 256 * blk:256 * blk + 256], psv, AF.Tanh, scale=0.5))
        self.dense_tm(tl, self.xnT, lambda ch, psv: self.cp(self.ifdt[:ch.n, ch.slot, :], psv, 'dve'))
        for ch in tl.chunks:
            n, s = ch.n, ch.slot
            gi = self.gat[:n, s, 0:8]
            self.tt(gi, self.ifdt[:n, s, 0:8], self.bif_b[:n, :], ALU.add)
            e1 = self.sm[3]
            self.act(e1[:n, 0:4], self.gat[:n, s, 4:8], AF.Exp, scale=-1.0)
            self.act(e1[:n, 0:4], e1[:n, 0:4], AF.Ln, bias=self.cst[:n, 2:3])
            self.ts(self.gat[:n, s, 4:8], e1[:n, 0:4], -1.0, ALU.mult)
            d1 = self.sm[4]
            self.tt(d1[:n, 0:32], self.ifdt[:n, s, 8:40], self.dtb_b[:n, :], ALU.add)
            self.act(d1[:n, 0:32], d1[:n, 0:32], AF.Exp)
            self.act(self.dta[:n, s, 0:32], d1[:n, 0:32], AF.Ln, bias=self.cst[:n, 2:3])
            self.tt(self.dta[:n, s, 32:64], self.dta[:n, s, 0:32], self.A_b[:n, :], ALU.mult)
        self.dump('qkT_' + tl.name, self.qkT[:, :, 0:T], [128, 8, T])
        self.dump('gat_' + tl.name, self.gat.ap(), [128, 4, 8])
        self.dump('dta_' + tl.name, self.dta.ap(), [128, 4, 64])
        if self.debug.get('stop') == 'a1':
            return
        S.alias_phase(self.grpA1conv, self.grpA1rec)
        for ch in tl.chunks:
            self.mlstm_chunk(tl, ch, last)
        self.dump('hgT_' + tl.name, self.hgT[:, :, 0:T], [128, 8, T])
        if self.debug.get('stop') == 'mlstm':
            return
        S.alias_phase(self.grpA1rec + self.grpA1fix, self.grpA1conv + self.grpA2fix)
        for blk in range(8):
            def cbz(ch, psv, blk=blk):
                n = ch.n
                zc = self.cacc[self.rot('cacc', 3)]
                th = self.cth[self.rot('cth', 2)]
                self.cp(zc[:n, 0:256], psv, 'act')
                self.act(th[:n, 0:256], psv, AF.Tanh, scale=0.5)
                self.stt(self.zs[ch.slot][:n, 256 * blk:256 * blk + 256], th[:n, 0:256], 1.0, zc[:n, 0:256], ALU.add, ALU.mult)
            self.dense_tm(tl, self.xnT, cbz)
        if self.debug.get('stop') == 'a2z':
            return
        sx = self.scar_in('s_sconv', 24, 3) if isS else None
        for blk in range(12):
            def cbx(gl, psv, blk=blk):
                g = 2 * blk + gl
                self.conv_group(tl, psv, 4, self.cws, g, self.cx, sx, self.xbcT[:, g, 0:T])
            self.dense_fm(tl, self.xnT, 8, cbx)
        self.conv_flush()
        if isS:
            self.scar_out('o_sconv', 24, 3)
        if last:
            self.carry_out(self.cx.ap(), 24, 3, O['p_sconv'].ap())
        self.dump('xbcT_' + tl.name, self.xbcT[:, :, 0:T], [128, 24, T])
        if self.debug.get('stop') == 'a2':
            return
        S.alias_phase(self.grpA1conv, self.grpA2rec)
        for ch in tl.chunks:
            self.ssd_chunk(tl, ch, last)
        self.dump('ygT_' + tl.name, self.ygT[:, :, 0:T], [128, 16, T])
        if self.debug.get('stop') == 'ssd':
            return
        S.alias_phase(self.grpA2rec + self.grpA2fix, self.grpA1conv + self.grpB)
        for blk in range(8):
            def cbg(gl, psv, blk=blk):
                self.act(self.gth[:, 2 * blk + gl, 0:T], psv, AF.Tanh, scale=0.5)
            self.dense_fm(tl, self.xnT, 8, cbg)
        for j in range(4):
            Wb, (name, k0, nk, parts) = self.wnext()
            banksA = [self.ps[0], self.ps[1]]
            banksB = [self.ps[2], self.ps[3]]
            for gl in range(2):
                for k in range(8):
                    self.mm(banksA[gl][:, 0:T], Wb[:, k, gl * 128:(gl + 1) * 128], self.hgT[:, k, 0:T], start=(k == 0), stop=(k == 7))
            for half in range(2):
                Wb, (name, k0, nk, parts) = self.wnext()
                for gl in range(2):
                    for k in range(8):
                        kk = 8 * half + k
                        self.mm(banksB[gl][:, 0:T], Wb[:, k, gl * 128:(gl + 1) * 128], self.ygT[:, kk, 0:T], start=(kk == 0), stop=(kk == 15))
            for gl in range(2):
                g = 2 * j + gl
                m1 = self.cacc[self.rot('cacc', 3)]
                m2 = self.cacc2[self.rot('cacc2', 2)]
                self.stt(m1[:, 0:T], self.gth[:, g, 0:T], 1.0, banksA[gl][:, 0:T], ALU.add, ALU.mult)
                self.stt(m2[:, 0:T], self.gth[:, 8 + g, 0:T], 1.0, banksB[gl][:, 0:T], ALU.add, ALU.mult)
                self.tt(self.mixT[:, g, 0:T], m1[:, 0:T], m2[:, 0:T], ALU.add)
        self.rr['mm'] = 0
        for blk in range(4):
            def cbo(ch, psv, blk=blk):
                xv = self.xr[ch.slot][:ch.n, 256 * blk:256 * blk + 256]
                self.stt(xv, psv, 0.5, xv, ALU.mult, ALU.add)
            self.dense_tm(tl, self.mixT, cbo)
        self.ln_load('ln1_g', 'ln1_b')
        for ch in tl.chunks:
            self.ln_rows(self.xr[ch.slot][:ch.n, :], ch.n, ch.slot)
        self.dump('x1_' + tl.name, self.xr[0].ap(), [128, D])
        self.to_fm(tl, lambda ch: self.xr[ch.slot][:ch.n, :], self.xnT)
        for ch in tl.chunks:
            self.ts(self.xr[ch.slot][:ch.n, :], self.xr[ch.slot][:ch.n, :], ALPHA, ALU.mult)
        sf = self.scar_in('s_fconv', 44, 2) if isS else None
        for j in range(11):
            accs = {}
            def cbua(gl, psv, j=j):
                g = 2 * j + gl
                accs[gl] = self.conv_group(tl, psv, 3, self.cwf, g, self.cff, sf, None, final=False)
            self.dense_fm(tl, self.xnT, 8, cbua)
            def cbub(gl, psv, j=j):
                g = 2 * j + gl
                E = self.cE[self.rot('cE', 2)]
                accb = self.cacc2[self.rot('cacc2', 2)]
                self._conv_taps(tl, psv, 3, self.cwf, 22 + g, self.cff, sf, E, accb)
                th = self.cth[self.rot('cth', 2)]
                acca = accs[gl]
                self.act(th[:, 0:T], acca[:, 0:T], AF.Tanh)
                self.stt(th[:, 0:T], th[:, 0:T], 1.0, acca[:, 0:T], ALU.add, ALU.mult)
                self.tt(self.hffT[:, g, 0:T], th[:, 0:T], accb[:, 0:T], ALU.mult)
            self.dense_fm(tl, self.xnT, 8, cbub)
        if isS:
            self.scar_out('o_fconv', 44, 2)
        if last:
            self.carry_out(self.cff.ap(), 44, 2, O['p_fconv'].ap())
        self.dump('hffT_' + tl.name, self.hffT[:, :, 0:T], [128, 22, T])
        for blk in range(4):
            banks = {ch.slot: self.ps[ch.slot] for ch in tl.chunks}
            for (k0, nk) in ((0, 8), (8, 8), (16, 6)):
                Wb, meta = self.wnext()
                for ch in tl.chunks:
                    for k in range(nk):
                        self.mm(banks[ch.slot][:ch.n, 0:256], self.hffT[:, k0 + k, ch.col0:ch.col0 + ch.n], Wb[:, k, 0:256],
                                start=(k0 + k == 0), stop=(k0 + k == 21))
            for ch in tl.chunks:
                xv = self.xr[ch.slot][:ch.n, 256 * blk:256 * blk + 256]
                self.tt(xv, banks[ch.slot][:ch.n, 0:256], xv, ALU.add)
        self.ln_load('ln2_g', 'ln2_b')
        for ch in tl.chunks:
            self.ln_rows(self.xr[ch.slot][:ch.n, :], ch.n, ch.slot)
            if ch.kind == 'p':
                self.dma(O['y_p'][ch.row0:ch.row0 + ch.n, :], self.xr[ch.slot][:ch.n, :])
            elif ch.kind == 's':
                self.dma(O['y_s'].ap(), self.xr[ch.slot][:ch.n, :])

    def mlstm_chunk(self, tl, ch, last):
        I, O = self.I, self.O
        n, s, c0, kind = ch.n, ch.slot, ch.col0, ch.kind
        kc = self.kc(kind, n)
        U, LS, M, MT = kc['U'], kc['LS'], kc['M'], kc['MT']
        identf, onesf = self.cfv('ident', n, n), self.cfv('ones', n, n)
        identb, onesb = self.cbv('identb', n, n), self.cbv('onesb', n, n)
        ps = self.ps
        cols = slice(c0, c0 + n)
        ig = self.gat[:n, s, 0:4]
        lf = self.gat[:n, s, 4:8]
        sm = self.sm
        bt, mi, bm, mt, wi, emt, negm, den, rden, wi2 = (sm[i] for i in range(5, 15))
        R1, R2, R3, wT, ST, kTM, hh, vw = self.R1, self.R2, self.R3, self.wT, self.ST, self.kTM, self.hh, self.vw
        self.mm(ps[0][:n, 0:4], U, lf)
        for h in range(4):
            self.ts(self.R1h[h][:n, :n], LS, self.gat[:n, s, 4 + h:5 + h], ALU.mult)
            self.act(self.R2h[h][:n, :n], identf, AF.Copy, scale=self.gat[:n, s, h:h + 1])
        for h in range(4):
            self.mm(ps[1][:n, h * 128:h * 128 + n], U, self.R1h[h][:n, :n], start=True, stop=False)
            self.mm(ps[1][:n, h * 128:h * 128 + n], onesf, self.R2h[h][:n, :n], start=False, stop=False)
            self.mm(ps[1][:n, h * 128:h * 128 + n], identb, M, start=False, stop=True)
        self.rmax(mi[:n, 0:4], ps[1][:n, :].rearrange('p (h t) -> p h t', h=4)[:, :, 0:n])
        if kind == 's':
            m0s = sm[15]
            self.dma(m0s[:16, 0:4], I['s_m'].ap())
            self.mm(ps[0][:n, 4:8], self.cfv('BMT', 16), m0s[:16, 0:4])
            m0v = ps[0][:n, 4:8]
        else:
            m0v = self.m_b[:n, :]
        self.cp(bt[:n, 0:4], ps[0][:n, 0:4], 'act')
        self.tt(bm[:n, 0:4], bt[:n, 0:4], m0v, ALU.add)
        self.tt(mt[:n, 0:4], bm[:n, 0:4], mi[:n, 0:4], ALU.max)
        self.tt(bm[:n, 0:4], bm[:n, 0:4], mt[:n, 0:4], ALU.subtract)
        self.act(wi[:n, 0:4], bm[:n, 0:4], AF.Exp)
        self.act(emt[:n, 0:4], mt[:n, 0:4], AF.Exp, scale=-1.0, bias=self.cst[:n, 1:2])
        self.ts(negm[:n, 0:4], mt[:n, 0:4], -1.0, ALU.mult)
        for h in range(4):
            self.act(self.R3h[h][:n, :n], identf, AF.Copy, scale=negm[:n, h:h + 1])
        for h in range(4):
            o = ps[2][:n, h * 128:h * 128 + n]
            self.mm(o, self.R1h[h][:n, :n], U, start=True, stop=False)
            self.mm(o, self.R2h[h][:n, :n], onesf, start=False, stop=False)
            self.mm(o, onesf, self.R3h[h][:n, :n], start=False, stop=False)
            self.mm(o, identb, MT, start=False, stop=True)
        ps2v = ps[2][:n, :].rearrange('p (h t) -> p h t', h=4)[:, :, 0:n]
        self.act(wT[:n, :, :n], ps2v, AF.Exp)
        for h in range(4):
            self.mm(ps[1][:n, h * 128:h * 128 + n], self.qkT[:, 4 + h, cols], self.qkT[:, h, cols])
        ps1v = ps[1][:n, :].rearrange('p (h t) -> p h t', h=4)[:, :, 0:n]
        self.tt(ST[:n, :, :n], ps1v, wT[:n, :, :n], ALU.mult)
        pb7 = self.psb(7)
        for h in range(4):
            self.tr(pb7[:n, h * 128:(h + 1) * 128], self.qkT[:, 4 + h, cols], self.cbv('identb'))
        self.cp(kTM[:n, :, :], pb7[:n, 0:512].rearrange('p (h d) -> p h d', h=4), 'act')
        if kind == 's':
            self.mlstm_sample_states(tl, ch, wi, wT, kTM, mt)
        for h in range(4):
            self.mm(ps[3 + h // 2][:n, (h % 2) * 256:(h % 2) * 256 + 256], ST[:n, h, :n], self.v[s][:n, h * 256:(h + 1) * 256])
        for h in range(4):
            self.mm(ps[0][:n, 8 + h:9 + h], ST[:n, h, :n], onesb[:, 0:1])
        if kind != 's':
            for h in range(4):
                self.mm(ps[5 + h // 2][:n, (h % 2) * 256:(h % 2) * 256 + 256], self.qkT[:, h, cols], self.Cb[:, h, :])
            for h in range(4):
                self.mm(ps[0][:n, 12 + h:13 + h], self.qkT[:, h, cols], self.nb[:, h:h + 1])
        dint = ps[0][:n, 12:16] if kind != 's' else sm[12][:n, 0:4]
        self.tt(den[:n, 0:4], dint, wi[:n, 0:4], ALU.mult)
        self.tt(den[:n, 0:4], den[:n, 0:4], ps[0][:n, 8:12], ALU.add)
        self.ts(wi2[:n, 0:4], den[:n, 0:4], -1.0, ALU.mult)
        self.tt(den[:n, 0:4], den[:n, 0:4], wi2[:n, 0:4], ALU.max)
        self.tt(den[:n, 0:4], den[:n, 0:4], emt[:n, 0:4], ALU.max)
        self.S.op('dve', lambda e: e.reciprocal(out=rden.t[:n, 0:4], in_=den.t[:n, 0:4]), reads=[den], writes=[rden])
        self.tt(wi2[:n, 0:4], wi[:n, 0:4], rden[:n, 0:4], ALU.mult)
        for h in range(4):
            self.act(self.hhh[h][:n, :], ps[3 + h // 2][:n, (h % 2) * 256:(h % 2) * 256 + 256], AF.Copy, scale=rden[:n, h:h + 1])
            if kind != 's':
                iv = ps[5 + h // 2][:n, (h % 2) * 256:(h % 2) * 256 + 256]
            else:
                iv = self.sinter[h][:n, :]
            self.stt(self.hhh[h][:n, :], iv, wi2[:n, h:h + 1], self.hhh[h][:n, :], ALU.mult, ALU.add)
        st, mv, rs = sm[0], sm[1], sm[2]
        for h in range(4):
            self.S.op('dve', lambda e, h=h: e.bn_stats(out=st.t[:n, h * 6:(h + 1) * 6], in_=self.hhh[h].t[:n, :]), reads=[self.hhh[h]], writes=[st])
        for h in range(4):
            self.S.op('dve', lambda e, h=h: e.bn_aggr(out=mv.t[:n, 2 * h:2 * h + 2], in_=st.t[:n, h * 6:(h + 1) * 6]), reads=[st], writes=[mv])
        mvv = mv[:n, 0:8].rearrange('p (h t) -> p h t', t=2)
        self.act(rs[:n, 0:4], mvv[:, :, 1], AF.Ln, bias=self.cst[:n, 0:1])
        self.act(rs[:n, 0:4], rs[:n, 0:4], AF.Exp, scale=-0.5)
        for h in range(4):
            self.ts(self.hhh[h][:n, :], self.hhh[h][:n, :], mv[:n, 2 * h:2 * h + 1], ALU.subtract, rs[:n, h:h + 1], ALU.mult)
            self.stt(self.hgTMh[h][:n, :], self.oth[s][:n, h * 256:(h + 1) * 256], 1.0, self.hhh[h][:n, :], ALU.add, ALU.mult)
        for k in range(8):
            self.tr(pb7[:, k * n:(k + 1) * n], self.hgTMh[k // 2][:n, (k % 2) * 128:(k % 2 + 1) * 128], identb)
        self.cp(self.hgT[:, :, cols], pb7[:, 0:8 * n].rearrange('p (k t) -> p k t', k=8), 'dve')
        if kind != 's':
            wl16 = sm[15]
            self.cp(wl16.ap().bitcast(BF16)[:n, 0:4], wT[:n, :, n - 1], 'act')
            for h in range(4):
                self.ts(vw[:n, h, :], self.v[s][:n, h * 256:(h + 1) * 256], wT[:n, h, n - 1:n], ALU.mult)
            for h in range(4):
                self.mm(ps[5 + h // 2][:, (h % 2) * 256:(h % 2) * 256 + 256], kTM[:n, h, :], vw[:n, h, :])
            for h in range(4):
                self.mm(ps[0][:, 16 + h:17 + h], kTM[:n, h, :], wl16.ap().bitcast(BF16)[:n, h:h + 1])
            SEL = self.cfv('SELp' if n == 128 else 'SELm', n)
            self.mm(ps[0][:, 32:36], SEL, wi[:n, 0:4])
            self.mm(ps[0][:, 36:40], SEL, mt[:n, 0:4])
            dec = sm[3]
            self.cp(dec[:, 0:8], ps[0][:, 32:40], 'act')
            for h in range(4):
                self.stt(self.Cf[:, h, :], self.Cf[:, h, :], dec[:, h:h + 1], ps[5 + h // 2][:, (h % 2) * 256:(h % 2) * 256 + 256], ALU.mult, ALU.add)
            self.tt(self.nf.ap(), self.nf.ap(), dec[:, 0:4], ALU.mult)
            self.tt(self.nf.ap(), self.nf.ap(), ps[0][:, 16:20], ALU.add)
            self.cp(self.m_b.ap(), dec[:, 4:8], 'dve')
            self.cp(self.Cb.ap(), self.Cf.ap(), 'act')
            self.cp(self.nb.ap(), self.nf.ap(), 'act')
            if ch.final:
                self.dma(O['p_C'].ap().rearrange('h d e -> d h e'), self.Cf.ap())
                identf128 = self.cfv('ident')
                self.tr(ps[0][:4, 128:256], self.nf.ap(), identf128)
                self.cp(self.pn_st[:4, :], ps[0][:4, 128:256], 'act')
                self.dma(O['p_n'].ap(), self.pn_st[:4, :])
                self.dma(O['p_m'].ap(), self.m_b[0:1, :])

    def mlstm_sample_states(self, tl, ch, wi, wT, kTM, mt):
        I, O, ps, sm = self.I, self.O, self.ps, self.sm
        n, s = 64, ch.slot
        identf = self.cfv('ident')
        RS, BM = self.cfv('RS', 64), self.cfv('BM', 64)
        CM = self.cbv('CM').rearrange('p (b j) -> p b j', b=16)
        vw = self.vw
        wl = sm[3]
        tmp = self.R3
        self.S.alias_phase(self.R3h, [self.R3])
        self.S.alias_phase(self.R1h + self.R2h, self.sinter)
        self.tt(tmp[:n, :, 0:64], wT[:n, :, 0:64], View(self.cf, self.cfv('LSEL', 64).ap.unsqueeze(1).to_broadcast([64, 4, 64])), ALU.mult)
        self.S.op('dve', lambda e: e.tensor_reduce(out=wl.t[:n, 0:4], in_=tmp.t[:n, :, 0:64], axis=AX.X, op=ALU.add), reads=[tmp], writes=[wl])
        wl16 = sm[4].ap().bitcast(BF16)
        self.cp(wl16[:n, 0:4], wl[:n, 0:4], 'act')
        for h in range(4):
            self.ts(vw[:n, h, :], self.v[s][:n, h * 256:(h + 1) * 256], wl[:n, h:h + 1], ALU.mult)
        Rm3 = self.Rm[:n, :].rearrange('p (b h) -> p b h', h=4)
        self.tt(Rm3, View(wi, wi.t[:n, 0:4].unsqueeze(1).to_broadcast([n, 16, 4])),
                View(self.cf, RS.ap.unsqueeze(2).to_broadcast([n, 16, 4])), ALU.mult)
        self.mm(ps[0][:, 64:128], self.cfv('ones', 64), self.Rm[:n, :])
        self.cp(self.decS.ap(), ps[0][:, 64:128], 'act')
        self.mm(ps[0][:16, 40:44], RS, mt[:n, 0:4])
        mo = sm[15]
        self.cp(mo[:16, 8:12], ps[0][:16, 40:44], 'act')
        self.dma(O['o_m'].ap(), mo[:16, 8:12])
        stg = self.pn_st
        self.dma(stg[:64, :], I['s_n'].ap())
        self.tr(ps[0][:, 192:256], stg[:64, :], identf[:64, :64])
        self.cp(self.n0T.ap(), ps[0][:, 192:256], 'act')
        self.cp(self.n16.ap(), self.n0T.ap(), 'pool')
        kTMflat = kTM[:n, :, :].rearrange('p h d -> p (h d)')
        for b in range(16):
            i = b % 2
            C0, C16, qm, km = self.C0b[i], self.C0b16[i], self.qmb[i], self.kTMm[i]
            self.dma(C0.ap(), I['s_C'][b].rearrange('h d e -> d h e'))
            self.cp(C16[:, :, 0:256], C0.ap(), 'act')
            self.cp(C16[:, :, 256:257], self.n16[:, 4 * b:4 * b + 4].rearrange('p (h o) -> p h o', o=1), 'dve')
            self.tt(qm.ap(), self.qkT[:, 0:4, ch.col0:ch.col0 + 64], View(self.cb, CM.ap[:, b, :].unsqueeze(1).to_broadcast([128, 4, 64])), ALU.mult)
            self.ts(km[:n, :, :].rearrange('p h d -> p (h d)'), kTMflat, BM[:, b:b + 1], ALU.mult)
            for h in range(4):
                self.mm(ps[3 + h][:n, 0:257], qm[:, h, :], C16[:, h, 0:257], start=(b == 0), stop=(b == 15))
            for h in range(4):
                self.mm(ps[1 + h // 2][:, (h % 2) * 256:(h % 2) * 256 + 256], km[:n, h, :], vw[:n, h, :])
            for h in range(4):
                self.mm(ps[0][:, 128 + 4 * b + h:129 + 4 * b + h], km[:n, h, :], wl16[:n, h:h + 1])
            for h in range(4):
                self.stt(C0[:, h, :], C0[:, h, :], self.decS[:, 4 * b + h:4 * b + h + 1],
                         ps[1 + h // 2][:, (h % 2) * 256:(h % 2) * 256 + 256], ALU.mult, ALU.add)
            self.dma(O['o_C'][b].rearrange('h d e -> d h e'), C0.ap())
        for h in range(4):
            self.cp(self.sinter[h][:n, :], ps[3 + h][:n, 0:256], 'act')
            self.cp(sm[12][:n, h:h + 1], ps[3 + h][:n, 256:257], 'act')
        self.tt(self.n0T.ap(), self.n0T.ap(), self.decS.ap(), ALU.mult)
        self.tt(self.n0T.ap(), self.n0T.ap(), ps[0][:, 128:192], ALU.add)
        self.tr(ps[0][:64, 256:384], self.n0T.ap(), identf)
        self.cp(stg[:64, :], ps[0][:64, 256:384], 'act')
        self.dma(O['o_n'].ap(), stg[:64, :])

    def ssd_chunk(self, tl, ch, last):
        I, O, ps, sm = self.I, self.O, self.ps, self.sm
        n, s, c0, kind = ch.n, ch.slot, ch.col0, ch.kind
        kc = self.kc(kind, n)
        U, LS, BO, MT = kc['U'], kc['LS'], kc['BO'], kc['MT']
        identb = self.cbv('identb')
        cols = slice(c0, c0 + n)
        dt = self.dta[:n, s, 0:32]
        a = self.dta[:n, s, 32:64]
        btsb, dect, wend, decS = sm[5], sm[6], sm[7], sm[8]
        self.mm(ps[0][:n, 0:32], U, a)
        self.mm(ps[0][:n, 32:64], BO, a)
        self.cp(btsb[:n, 0:32], ps[0][:n, 0:32], 'act')
        self.act(dect[:n, 0:32], ps[0][:n, 0:32], AF.Exp)
        self.tt(wend[:n, 0:32], ps[0][:n, 32:64], btsb[:n, 0:32], ALU.subtract)
        self.act(wend[:n, 0:32], wend[:n, 0:32], AF.Exp)
        if kind != 's':
            self.mm(ps[0][:, 64:96], self.cfv('ones', n), a)
            self.act(decS[:, 0:32], ps[0][:, 64:96], AF.Exp)
        S = self.S
        pb6, pb7 = self.psb(6), self.psb(7)
        xsTM = self.xdt
        for g in range(16):
            pv = pb6 if g < 8 else pb7
            self.tr(pv[:n, (g % 8) * 128:(g % 8 + 1) * 128], self.xbcT[:, g, cols], identb)
        self.act(xsTM[:n, 0:1024], pb6[:n, 0:1024])
        self.act(xsTM[:n, 1024:2048], pb7[:n, 0:1024])
        S.alias_phase([self.ynTM], [self.xsDT])
        for half in range(2):
            for g in range(8):
                self.ts(self.xsDT[:, g, 0:n], self.xbcT[:, 8 * half + g, cols], self.Dfm[:, 8 * half + g:8 * half + g + 1], ALU.mult)
            pv = pb6 if half == 0 else pb7
            for g in range(8):
                self.tr(pv[:n, g * 128:(g + 1) * 128], self.xsDT[:, g, 0:n], identb)
            self.cp(self.xsD[:n, 1024 * half:1024 * half + 1024], pv[:n, 0:1024], 'dve')
        for g in range(4):
            self.tr(pb6[:n, g * 128:(g + 1) * 128], self.xbcT[:, 16 + g, cols], identb)
        self.act(self.BTM[:n, :], pb6[:n, 0:512])
        dtw = sm[13]
        self.tt(dtw[:n, 0:32], dt, wend[:n, 0:32], ALU.mult)
        S.alias_phase([self.xsDT], [self.ynTM])
        if kind == 's':
            self.ssd_sample_states(tl, ch, dtw)
        S.alias_phase([self.ynTM], self.MTh[1])
        ssq = sm[9]
        self.memset(ssq[:n, 0:4], 0.0)
        def stageA(g):
            MTb = self.MTh[g % 2]
            for j in range(8):
                self.act(self.LAh[j][:n, :n], LS, AF.Copy, scale=self.dta[:n, s, 32 + 8 * g + j:33 + 8 * g + j])
            for j in range(8):
                o = ps[1 + j // 4][:n, (j % 4) * 128:(j % 4) * 128 + n]
                self.mm(o, self.LAh[j][:n, :n], U, start=True, stop=False)
                self.mm(o, identb[:n, :n], MT, start=False, stop=True)
            for half in range(2):
                self.act(self.LTh[half][:n, :, :n],
                         ps[1 + half][:n, :].rearrange('p (h t) -> p h t', h=4)[:, :, 0:n], AF.Exp)
            self.mm(ps[3][:n, 0:n], self.xbcT[:, 16 + g, cols], self.xbcT[:, 20 + g, cols])
            for j in range(8):
                self.stt(MTb[j][:n, :n], self.LTh[j // 4][:n, j % 4, :n], self.dta[:n, s, 8 * g + j:8 * g + j + 1], ps[3][:n, 0:n], ALU.mult, ALU.mult)

        def stageB(g):
            MTb = self.MTh[g % 2]
            for j in range(8):
                h = 8 * g + j
                self.mm(ps[4][:n, j * 64:(j + 1) * 64], MTb[j][:n, :n], xsTM[:n, h * 64:(h + 1) * 64])
            t1 = self.t1
            if kind != 's':
                self.mm(ps[5][:n, 0:512], self.xbcT[:, 20 + g, cols], self.STb[:, 512 * g:512 * g + 512])
                for j in range(4):
                    self.act(t1[:n, j * 64:(j + 1) * 64], ps[5][:n, j * 64:(j + 1) * 64], AF.Copy, scale=dect[:n, 8 * g + j:8 * g + j + 1])
                self.tt(t1[:n, 256:512].rearrange('p (h d) -> p d h', d=64), ps[5][:n, 256:512].rearrange('p (h d) -> p d h', d=64),
                        View(dect, dect.t[:n, 8 * g + 4:8 * g + 8].unsqueeze(1).to_broadcast([n, 64, 4])), ALU.mult)
            else:
                self.cp(t1[:n, :], self.ysi[:n, 512 * g:512 * g + 512], 'dve')
            self.tt(t1[:n, :], t1[:n, :], ps[4][:n, 0:512], ALU.add)
            self.tt(t1[:n, :], t1[:n, :], self.xsD[:n, 512 * g:512 * g + 512], ALU.add)
            self.tt(self.yz[:n, 512 * g:512 * g + 512], t1[:n, :], self.zs[s][:n, 512 * g:512 * g + 512], ALU.mult)

        stageA(0)
        for g in range(4):
            if g + 1 < 4:
                stageA(g + 1)
            stageB(g)
        S.alias_phase(self.MTh[1], [self.ynTM])
        for g in range(4):
            self.act(self.ynTM[:n, 512 * g:512 * g + 512], self.yz[:n, 512 * g:512 * g + 512], AF.Square, accum=ssq[:n, g:g + 1])
        rs = sm[10]
        self.act(rs[:n, 0:4], ssq[:n, 0:4], AF.Ln, scale=0.25 / 512.0, bias=self.cst[:n, 0:1])
        self.act(rs[:n, 0:4], rs[:n, 0:4], AF.Exp, scale=-0.5)
        self.ts(rs[:n, 0:4], rs[:n, 0:4], 0.5, ALU.mult)
        for g in range(4):
            self.ts(self.ynTM[:n, 512 * g:512 * g + 512], self.yz[:n, 512 * g:512 * g + 512], rs[:n, g:g + 1], ALU.mult)
        for k in range(16):
            pv = pb6 if k < 8 else pb7
            self.tr(pv[:, (k % 8) * n:(k % 8 + 1) * n], self.ynTM[:n, k * 128:(k + 1) * 128], identb[:n, :n])
        for half in range(2):
            pv = (pb6 if half == 0 else pb7)[:, 0:8 * n].rearrange('p (k t) -> p k t', k=8)
            self.cp(self.ygT[:, 8 * half:8 * half + 8, cols], pv, 'dve' if half == 0 else 'act')
        if kind != 's':
            S.alias_phase([self.ynTM], self.MTh[1])
            for g in range(4):
                hs = slice(8 * g, 8 * g + 8)
                bank = ps[3 + 2 * (g % 2)]
                Bs = self.MTh[g % 2]
                for j in range(8):
                    h = 8 * g + j
                    self.ts(Bs[j][:n, :], self.BTM[:n, g * 128:(g + 1) * 128], dtw[:n, h:h + 1], ALU.mult)
                for j in range(8):
                    h = 8 * g + j
                    self.mm(bank[:, j * 64:(j + 1) * 64], Bs[j][:n, :], xsTM[:n, h * 64:(h + 1) * 64])
                if g % 2 == 0:
                    for j in range(8):
                        c0_ = 512 * g + 64 * j
                        self.act(self.STf[:, c0_:c0_ + 64], self.STf[:, c0_:c0_ + 64], AF.Copy, scale=decS[:, 8 * g + j:8 * g + j + 1])
                else:
                    sv = self.STf[:, 512 * g:512 * g + 512].rearrange('p (h d) -> p d h', d=64)
                    self.tt(sv, sv, View(decS, decS.t[:, hs].unsqueeze(1).to_broadcast([128, 64, 8])), ALU.mult)
                self.tt(self.STf[:, 512 * g:512 * g + 512], self.STf[:, 512 * g:512 * g + 512], bank[:, 0:512], ALU.add)
            S.alias_phase(self.MTh[1], [self.ynTM])
            self.cp(self.STb.ap(), self.STf.ap(), 'act')
            if ch.final:
                identf = self.cfv('ident')
                for j in range(16):
                    bank = ps[1 + (j // 4) % 2]
                    self.tr(bank[:, (j % 4) * 128:(j % 4 + 1) * 128], self.STf[:, j * 128:(j + 1) * 128], identf)
                    if j % 4 == 3:
                        stg = self.LA[:, 4 * ((j // 4) % 2):4 * ((j // 4) % 2) + 4, :]
                        self.S.alias_phase(self.LAh, [self.LA])
                        self.cp(stg, bank[:, 0:512].rearrange('p (j n) -> p j n', j=4), 'act')
                        q = j // 4
                        self.dma(O['p_ssm'][512 * q:512 * q + 512, :].rearrange('(j p) n -> p j n', p=128), stg)
                self.S.alias_phase([self.LA], self.LAh)

    def ssd_sample_states(self, tl, ch, wend):
        I, O, ps, sm, S = self.I, self.O, self.ps, self.sm, self.S
        n, s = 64, ch.slot
        identf = self.cfv('ident')
        RS, BM = self.cfv('RS', 64), self.cfv('BM', 64)
        CM = self.cbv('CM').rearrange('p (b j) -> p b j', b=16)
        dect = sm[6]
        S.alias_phase(self.grpArena, self.grpArena2)
        S.alias_phase(self.LTh + self.MTh[0] + [self.t1, self.yz], [self.S0b[1]])
        blsb = sm[11]
        self.cp(blsb[:n, 0:32], ps[0][:n, 32:64], 'act')
        bl3 = blsb[:n, 0:32].rearrange('p (j r) -> p j r', r=2)
        for r in range(2):
            self.tt(self.Rr[:n, r, :].rearrange('p (b j) -> p b j', b=16),
                    View(blsb, bl3.ap[:, :, r].unsqueeze(1).to_broadcast([n, 16, 16])),
                    View(self.cf, RS.ap.unsqueeze(2).to_broadcast([n, 16, 16])), ALU.mult, eng='pool')
        self.mm(ps[0][:, 256:512], self.cfv('H0', 64), self.Rr[:n, 0, :], start=True, stop=False)
        self.mm(ps[0][:, 256:512], self.cfv('H1', 64), self.Rr[:n, 1, :], start=False, stop=True)
        self.act(self.decP.ap(), ps[0][:, 256:512], AF.Exp)
        wxA = self.ynTM
        self.tt(wxA[:n, :].rearrange('p (h d) -> p d h', d=64), self.xdt[:n, :].rearrange('p (h d) -> p d h', d=64),
                View(wend, wend.t[:n, 0:32].unsqueeze(1).to_broadcast([n, 64, 32])), ALU.mult)
        for b in range(16):
            Sb = self.S0b[b % 2]
            cm = self.CTmb[b % 2]
            self.dma(Sb.ap(), I['s_ssm'][b].rearrange('(j p) n -> p j n', p=128))
            self.tt(cm.ap(), self.xbcT[:, 20:24, ch.col0:ch.col0 + 64], View(self.cb, CM.ap[:, b, :].unsqueeze(1).to_broadcast([128, 4, 64])), ALU.mult)
            self.ts(self.wxm[:n, :], wxA[:n, :], BM[:, b:b + 1], ALU.mult)
            for q in range(4):
                bank = ps[5 + q % 2]
                for i in range(4):
                    self.tr(bank[:, i * 128:(i + 1) * 128], Sb[:, 4 * q + i, :], identf)
                self.cp(self.SbT[:, 512 * q:512 * q + 512], bank[:, 0:512], 'act')
            for g in range(4):
                self.mm(ps[1 + g][:n, 0:512], cm[:, g, :], self.SbT[:, 512 * g:512 * g + 512], start=(b == 0), stop=(b == 15))
            for q in range(4):
                bank = ps[7] if q % 2 == 0 else ps[0]
                for i in range(4):
                    j = 4 * q + i
                    self.mm(bank[:, i * 128:(i + 1) * 128], self.wxm[:n, j * 128:(j + 1) * 128], self.BTM[:n, q * 128:(q + 1) * 128])
                for i in range(4):
                    j = 4 * q + i
                    self.stt(Sb[:, j, :], Sb[:, j, :], self.decP[:, 16 * b + j:16 * b + j + 1], bank[:, i * 128:(i + 1) * 128], ALU.mult, ALU.add)
            self.dma(O['o_ssm'][b].rearrange('(j p) n -> p j n', p=128), Sb.ap())
        for g in range(4):
            self.tt(self.ysi[:n, 512 * g:512 * g + 512].rearrange('p (h d) -> p d h', d=64),
                    ps[1 + g][:n, 0:512].rearrange('p (h d) -> p d h', d=64),
                    View(dect, dect.t[:n, 8 * g:8 * g + 8].unsqueeze(1).to_broadcast([n, 64, 8])), ALU.mult)
        S.alias_phase([self.S0b[1]], self.LTh + self.MTh[0] + [self.t1, self.yz])


_CACHE = {}


def _get_kernel():
    if 'k' not in _CACHE:
        _CACHE['k'] = K()
    return _CACHE['k']


def make_in_maps(kb, inputs):
    f = lambda a: np.ascontiguousarray(np.asarray(a, dtype=np.float32))
    xp, xs = f(inputs['x_prompt']), f(inputs['x_sample'])
    shared = {'meta': f(inputs['meta_tokens']), 'cf': kb.cf_np, 'cb': kb.cb_np,
              'ln0_g': f(inputs['ln0_g']), 'ln0_b': f(inputs['ln0_b']),
              'b_if': f(inputs['b_mlstm_if'])[0], 'w_mconv': f(inputs['w_mlstm_conv'])[0],
              'b_mconv': f(inputs['b_mlstm_conv']), 'mnorm_g': f(inputs['mlstm_norm_g']),
              'w_sconv': f(inputs['w_ssm_conv'])[0], 'b_sconv': f(inputs['b_ssm_conv']),
              'dt_bias': f(inputs['ssm_dt_bias'])[0], 'A_log': f(inputs['ssm_A_log'])[0],
              'ssm_D': f(inputs['ssm_D'])[0], 'snorm_g': f(inputs['ssm_norm_g']),
              'ln1_g': f(inputs['ln1_g'])[0], 'ln1_b': f(inputs['ln1_b'])[0],
              'w_fconv': f(inputs['w_ffn_conv'])[0], 'b_fconv': f(inputs['b_ffn_conv']),
              'ln2_g': f(inputs['ln2_g'])[0], 'ln2_b': f(inputs['ln2_b'])[0],
              'w_in': f(inputs['w_in'])[0], 'w_proj_a': f(inputs['w_proj_a'])[0],
              'w_proj_b': f(inputs['w_proj_b'])[0], 'w_out': f(inputs['w_out'])[0],
              'w_up': f(inputs['w_up'])[0], 'w_down': f(inputs['w_down'])[0]}
    maps = []
    for c in range(8):
        b = slice(16 * c, 16 * c + 16)
        m = dict(shared)
        m['xp'] = xp[c]
        m['xs'] = xs[b].reshape(64, D)
        m['s_mconv'] = f(inputs['state_mlstm_conv'])[0, b].reshape(48, 1024)
        m['s_C'] = f(inputs['state_mlstm_C'])[0, b]
        m['s_n'] = f(inputs['state_mlstm_n'])[0, b].reshape(64, 128)
        m['s_m'] = f(inputs['state_mlstm_m'])[0, b]
        m['s_sconv'] = f(inputs['state_ssm_conv'])[0, b].reshape(48, 3072)
        m['s_ssm'] = f(inputs['state_ssm'])[0, b].reshape(16, 2048, 128)
        m['s_fconv'] = f(inputs['state_ffn_conv'])[0, b].reshape(32, 2 * DFF)
        maps.append(m)
    return maps


def kernel(**inputs):
    kb = _get_kernel()
    maps = make_in_maps(kb, inputs)
    res = run_bass_kernel_spmd(kb.nc, maps, core_ids=list(range(8)))
    R = res.results
    cat = lambda k: np.stack([np.asarray(r[k], dtype=np.float32) for r in R])
    y_p = cat('y_p')
    y_s = cat('y_s').reshape(128, 4, D)
    p_mconv = cat('p_mconv')[None]
    p_C = cat('p_C')[None]
    p_n = cat('p_n')[None]
    p_m = cat('p_m').reshape(8, 4)[None]
    p_sconv = cat('p_sconv')[None]
    p_ssm = cat('p_ssm').reshape(8, 32, 64, 128)[None]
    p_fconv = cat('p_fconv')[None]
    s_mconv = cat('o_mconv').reshape(128, 3, 1024)[None]
    s_C = cat('o_C').reshape(128, 4, 128, 256)[None]
    s_n = cat('o_n').reshape(128, 4, 128)[None]
    s_m = cat('o_m').reshape(128, 4)[None]
    s_sconv = cat('o_sconv').reshape(128, 3, 3072)[None]
    s_ssm = cat('o_ssm').reshape(128, 32, 64, 128)[None]
    s_fconv = cat('o_fconv').reshape(128, 2, 2 * DFF)[None]
    return (y_p, y_s, p_mconv, p_C, p_n, p_m, p_sconv, p_ssm, p_fconv,
            s_mconv, s_C, s_n, s_m, s_sconv, s_ssm, s_fconv)
```
